# Optimizing a Trainium2 kernel written in Bass

```python
import math
import jax, jax.numpy as jnp
from jax import lax
import numpy as np

D_MODEL = 1024
BATCH = 8
SEQ = 2048
DEPTH = 1
DEC_BATCH = 128
DEC_SEQ = 4
PAST_LEN = 16384
PAGE_SIZE = 128

MIX_WIDTH = D_MODEL
SSM_WIDTH = MIX_WIDTH // 2
SSM_GROUP = 16
SSM_GROUPS = SSM_WIDTH // SSM_GROUP
SSM_STATE = 64
RET_WIDTH = MIX_WIDTH - SSM_WIDTH
RET_HEADS = 4
RET_HEAD_DIM = RET_WIDTH // RET_HEADS
RET_CHUNK = 128
ROPE_BASE = 10000.0
MEM_LEN = 256
MEM_HEADS = 4
MEM_HEAD_DIM = D_MODEL // MEM_HEADS
D_FF = 4 * D_MODEL
PROJ_WIDTH = SSM_WIDTH + 4 * RET_WIDTH
EPS = 1e-6
DT_MIN = 1e-3
DT_MAX = 1e-1

kernel_name = "hymba_s5_retnet_memxattn_step"

F32 = jnp.float32


def rmsnorm(x, g):
    xf = x.astype(F32)
    y = xf * lax.rsqrt(jnp.mean(xf * xf, axis=-1, keepdims=True) + EPS)
    return (y * g.astype(F32)).astype(x.dtype)


def s5_discretize(lam_re, lam_im, log_dt, b_re, b_im):
    lr = lam_re.astype(F32)
    li = lam_im.astype(F32)
    dt = jnp.exp(log_dt.astype(F32))[:, None]
    mag = jnp.exp(lr * dt)
    ab_re = mag * jnp.cos(li * dt)
    ab_im = mag * jnp.sin(li * dt)
    den = lr * lr + li * li
    f_re = ((ab_re - 1.0) * lr + ab_im * li) / den
    f_im = (ab_im * lr - (ab_re - 1.0) * li) / den
    br = b_re.astype(F32)
    bi = b_im.astype(F32)
    bb_re = f_re[..., None] * br - f_im[..., None] * bi
    bb_im = f_re[..., None] * bi + f_im[..., None] * br
    return ab_re, ab_im, bb_re, bb_im


def _cplx_scan_op(e1, e2):
    a1r, a1i, b1r, b1i = e1
    a2r, a2i, b2r, b2i = e2
    ar = a2r * a1r - a2i * a1i
    ai = a2r * a1i + a2i * a1r
    br = a2r * b1r - a2i * b1i + b2r
    bi = a2r * b1i + a2i * b1r + b2i
    return ar, ai, br, bi


def s5_mixer(u, h0_re, h0_im, lam_re, lam_im, log_dt, b_re, b_im, c_re, c_im, d_skip, w_glu):
    bsz, L, _ = u.shape
    uf = u.astype(F32)
    ug = uf.reshape(bsz, L, SSM_GROUPS, SSM_GROUP)
    ab_re, ab_im, bb_re, bb_im = s5_discretize(lam_re, lam_im, log_dt, b_re, b_im)
    bu_re = jnp.einsum('blgh,gph->blgp', ug, bb_re)
    bu_im = jnp.einsum('blgh,gph->blgp', ug, bb_im)
    a_re = jnp.broadcast_to(ab_re, bu_re.shape)
    a_im = jnp.broadcast_to(ab_im, bu_im.shape)
    ac_re, ac_im, s_re, s_im = lax.associative_scan(
        _cplx_scan_op, (a_re, a_im, bu_re, bu_im), axis=1)
    h0r = h0_re.astype(F32)[:, None]
    h0i = h0_im.astype(F32)[:, None]
    x_re = s_re + ac_re * h0r - ac_im * h0i
    x_im = s_im + ac_re * h0i + ac_im * h0r
    y = (jnp.einsum('blgp,ghp->blgh', x_re, c_re.astype(F32))
         - jnp.einsum('blgp,ghp->blgh', x_im, c_im.astype(F32)))
    y = y.reshape(bsz, L, SSM_WIDTH) + d_skip.astype(F32) * uf
    z = jax.nn.gelu(y)
    out = z * jax.nn.sigmoid(z @ w_glu.astype(F32))
    return out.astype(u.dtype), x_re[:, -1], x_im[:, -1]


def rope(x, pos):
    half = x.shape[-1] // 2
    inv = ROPE_BASE ** (-jnp.arange(half, dtype=F32) / half)
    ang = pos[:, None] * inv[None, :]
    cos = jnp.cos(ang)[None, :, None, :]
    sin = jnp.sin(ang)[None, :, None, :]
    x1 = x[..., :half]
    x2 = x[..., half:]
    return jnp.concatenate([x1 * cos - x2 * sin, x1 * sin + x2 * cos], axis=-1)


def retention_mixer(q, k, v, g, s0, pos0, gn_gain):
    bsz, L, _ = q.shape
    H, dh = RET_HEADS, RET_HEAD_DIM
    C = RET_CHUNK if L % RET_CHUNK == 0 else L
    n = L // C
    pos = pos0 + jnp.arange(L, dtype=F32)
    qh = rope(q.astype(F32).reshape(bsz, L, H, dh), pos) * (dh ** -0.5)
    kh = rope(k.astype(F32).reshape(bsz, L, H, dh), pos)
    vh = v.astype(F32).reshape(bsz, L, H, dh)
    qc = qh.reshape(bsz, n, C, H, dh)
    kc = kh.reshape(bsz, n, C, H, dh)
    vc = vh.reshape(bsz, n, C, H, dh)
    log_gamma = jnp.log(1.0 - 2.0 ** (-5.0 - jnp.arange(H, dtype=F32)))
    idx = jnp.arange(C, dtype=F32)
    diff = idx[:, None] - idx[None, :]
    decay_mask = jnp.where(diff[None] >= 0,
                           jnp.exp(jnp.maximum(diff, 0.0)[None] * log_gamma[:, None, None]),
                           0.0)
    scores = jnp.einsum('bnqhd,bnkhd->bnhqk', qc, kc) * decay_mask[None, None]
    inner = jnp.einsum('bnhqk,bnkhe->bnqhe', scores, vc)
    zeta = jnp.exp((C - 1.0 - idx)[:, None] * log_gamma[None, :])
    kv = jnp.einsum('bnkhd,bnkhe->nbhde', kc * zeta[None, None, :, :, None], vc)
    gamma_c = jnp.exp(C * log_gamma)[None, :, None, None]

    def step(s, kv_i):
        return s * gamma_c + kv_i, s

    s_final, s_prev = lax.scan(step, s0.astype(F32), kv)
    xi = jnp.exp((idx + 1.0)[:, None] * log_gamma[None, :])
    cross = jnp.einsum('bnqhd,nbhde->bnqhe', qc, s_prev) * xi[None, None, :, :, None]
    o = (inner + cross).reshape(bsz, L, H, dh)
    mu = jnp.mean(o, axis=-1, keepdims=True)
    var = jnp.mean(jnp.square(o - mu), axis=-1, keepdims=True)
    o = ((o - mu) * lax.rsqrt(var + EPS)).reshape(bsz, L, RET_WIDTH) * gn_gain.astype(F32)
    out = jax.nn.silu(g.astype(F32)) * o
    return out.astype(q.dtype), s_final


def memory_kv(mem, g_mem, w_mk, w_mv):
    bsz = mem.shape[0]
    m = rmsnorm(mem, g_mem)
    mk = (m @ w_mk).reshape(bsz, MEM_LEN, MEM_HEADS, MEM_HEAD_DIM)
    mv = (m @ w_mv).reshape(bsz, MEM_LEN, MEM_HEADS, MEM_HEAD_DIM)
    return mk, mv


def memory_attend(h, mk, mv, w_mq, w_mo):
    bsz, L, _ = h.shape
    q = (h @ w_mq).reshape(bsz, L, MEM_HEADS, MEM_HEAD_DIM).astype(F32)
    s = jnp.einsum('blhd,bmhd->bhlm', q, mk.astype(F32)) * (MEM_HEAD_DIM ** -0.5)
    p = jax.nn.softmax(s, axis=-1)
    o = jnp.einsum('bhlm,bmhd->blhd', p, mv.astype(F32)).reshape(bsz, L, D_MODEL)
    return o.astype(h.dtype) @ w_mo


def decoder_layer(x, mk, mv, s5_re, s5_im, ret_s, pos0, p):
    h = rmsnorm(x, p['g_mix'])
    proj = h @ p['w_in']
    u = proj[..., :SSM_WIDTH]
    q = proj[..., SSM_WIDTH:SSM_WIDTH + RET_WIDTH]
    k = proj[..., SSM_WIDTH + RET_WIDTH:SSM_WIDTH + 2 * RET_WIDTH]
    v = proj[..., SSM_WIDTH + 2 * RET_WIDTH:SSM_WIDTH + 3 * RET_WIDTH]
    g = proj[..., SSM_WIDTH + 3 * RET_WIDTH:]
    ssm_out, s5_re_new, s5_im_new = s5_mixer(
        u, s5_re, s5_im, p['lam_re'], p['lam_im'], p['log_dt'], p['b_re'], p['b_im'],
        p['c_re'], p['c_im'], p['d_skip'], p['w_glu'])
    ret_out, ret_new = retention_mixer(q, k, v, g, ret_s, pos0, p['ret_gn'])
    x = x + jnp.concatenate([ssm_out, ret_out], axis=-1) @ p['w_out']
    x = x + memory_attend(rmsnorm(x, p['g_xattn']), mk, mv, p['w_mq'], p['w_mo'])
    h = rmsnorm(x, p['g_mlp'])
    x = x + jnp.square(jax.nn.relu(h @ p['w_up'])) @ p['w_down']
    return x, s5_re_new, s5_im_new, ret_new


def setup_inputs(seed: int = 0) -> dict:
    key = jax.random.key(seed)
    ks = jax.random.split(key, 32)
    nrm = jax.random.normal
    G, P, Hg = SSM_GROUPS, SSM_STATE, SSM_GROUP
    lam_im_base = jnp.pi * jnp.arange(P, dtype=F32)
    return {
        'x_prompt': nrm(ks[0], (BATCH, SEQ, D_MODEL), F32),
        'x_sample': nrm(ks[1], (DEC_BATCH, DEC_SEQ, D_MODEL), F32),
        'mem_prompt': nrm(ks[2], (BATCH, MEM_LEN, D_MODEL), F32),
        'state_s5_re': 0.3 * nrm(ks[3], (DEPTH, DEC_BATCH, G, P), F32),
        'state_s5_im': 0.3 * nrm(ks[4], (DEPTH, DEC_BATCH, G, P), F32),
        'state_ret': 3.0 * nrm(ks[5], (DEPTH, DEC_BATCH, RET_HEADS, RET_HEAD_DIM, RET_HEAD_DIM), F32),
        'cache_mem_k': nrm(ks[6], (DEPTH, DEC_BATCH, MEM_LEN, MEM_HEADS, MEM_HEAD_DIM), F32),
        'cache_mem_v': nrm(ks[7], (DEPTH, DEC_BATCH, MEM_LEN, MEM_HEADS, MEM_HEAD_DIM), F32),
        'g_mix': 1.0 + 0.02 * nrm(ks[8], (DEPTH, D_MODEL), F32),
        'w_in': nrm(ks[9], (DEPTH, D_MODEL, PROJ_WIDTH), F32) * D_MODEL ** -0.5,
        'lam_re': -0.5 + 0.01 * nrm(ks[10], (DEPTH, G, P), F32),
        'lam_im': lam_im_base + 0.01 * nrm(ks[11], (DEPTH, G, P), F32),
        'log_dt': jax.random.uniform(ks[12], (DEPTH, G), F32, math.log(DT_MIN), math.log(DT_MAX)),
        'b_re': nrm(ks[13], (DEPTH, G, P, Hg), F32) * (2 * Hg) ** -0.5,
        'b_im': nrm(ks[14], (DEPTH, G, P, Hg), F32) * (2 * Hg) ** -0.5,
        'c_re': nrm(ks[15], (DEPTH, G, Hg, P), F32) * P ** -0.5,
        'c_im': nrm(ks[16], (DEPTH, G, Hg, P), F32) * P ** -0.5,
        'd_skip': nrm(ks[17], (DEPTH, SSM_WIDTH), F32),
        'w_glu': nrm(ks[18], (DEPTH, SSM_WIDTH, SSM_WIDTH), F32) * SSM_WIDTH ** -0.5,
        'ret_gn': 1.0 + 0.02 * nrm(ks[19], (DEPTH, RET_WIDTH), F32),
        'w_out': nrm(ks[20], (DEPTH, MIX_WIDTH, D_MODEL), F32) * MIX_WIDTH ** -0.5,
        'g_xattn': 1.0 + 0.02 * nrm(ks[21], (DEPTH, D_MODEL), F32),
        'g_mem': 1.0 + 0.02 * nrm(ks[22], (DEPTH, D_MODEL), F32),
        'w_mq': nrm(ks[23], (DEPTH, D_MODEL, D_MODEL), F32) * D_MODEL ** -0.5,
        'w_mk': nrm(ks[24], (DEPTH, D_MODEL, D_MODEL), F32) * D_MODEL ** -0.5,
        'w_mv': nrm(ks[25], (DEPTH, D_MODEL, D_MODEL), F32) * D_MODEL ** -0.5,
        'w_mo': nrm(ks[26], (DEPTH, D_MODEL, D_MODEL), F32) * D_MODEL ** -0.5,
        'g_mlp': 1.0 + 0.02 * nrm(ks[27], (DEPTH, D_MODEL), F32),
        'w_up': nrm(ks[28], (DEPTH, D_MODEL, D_FF), F32) * D_MODEL ** -0.5,
        'w_down': nrm(ks[29], (DEPTH, D_FF, D_MODEL), F32) * D_FF ** -0.5,
        'g_final': 1.0 + 0.02 * nrm(ks[30], (D_MODEL,), F32),
    }


def reference(x_prompt, x_sample, mem_prompt, state_s5_re, state_s5_im, state_ret,
              cache_mem_k, cache_mem_v, g_mix, w_in, lam_re, lam_im, log_dt, b_re, b_im,
              c_re, c_im, d_skip, w_glu, ret_gn, w_out, g_xattn, g_mem, w_mq, w_mk, w_mv,
              w_mo, g_mlp, w_up, w_down, g_final):
    bp = x_prompt.shape[0]
    bs = x_sample.shape[0]
    xp = x_prompt
    xs = x_sample
    s5r_p_l, s5i_p_l, ret_p_l, mk_p_l, mv_p_l = [], [], [], [], []
    s5r_s_l, s5i_s_l, ret_s_l = [], [], []
    for l in range(DEPTH):
        p = dict(g_mix=g_mix[l], w_in=w_in[l], lam_re=lam_re[l], lam_im=lam_im[l],
                 log_dt=log_dt[l], b_re=b_re[l], b_im=b_im[l], c_re=c_re[l], c_im=c_im[l],
                 d_skip=d_skip[l], w_glu=w_glu[l], ret_gn=ret_gn[l], w_out=w_out[l],
                 g_xattn=g_xattn[l], w_mq=w_mq[l], w_mo=w_mo[l], g_mlp=g_mlp[l],
                 w_up=w_up[l], w_down=w_down[l])
        mk_p, mv_p = memory_kv(mem_prompt, g_mem[l], w_mk[l], w_mv[l])
        zs = jnp.zeros((bp, SSM_GROUPS, SSM_STATE), F32)
        zr = jnp.zeros((bp, RET_HEADS, RET_HEAD_DIM, RET_HEAD_DIM), F32)
        xp, s5r_p, s5i_p, ret_p = decoder_layer(xp, mk_p, mv_p, zs, zs, zr, 0.0, p)
        xs, s5r_s, s5i_s, ret_s = decoder_layer(
            xs, cache_mem_k[l], cache_mem_v[l], state_s5_re[l], state_s5_im[l],
            state_ret[l], float(PAST_LEN), p)
        s5r_p_l.append(s5r_p)
        s5i_p_l.append(s5i_p)
        ret_p_l.append(ret_p)
        mk_p_l.append(mk_p)
        mv_p_l.append(mv_p)
        s5r_s_l.append(s5r_s)
        s5i_s_l.append(s5i_s)
        ret_s_l.append(ret_s)
    y_prompt = rmsnorm(xp, g_final)
    y_sample = rmsnorm(xs, g_final)
    new_s5_re_prompt = jnp.stack(s5r_p_l, axis=0)
    new_s5_im_prompt = jnp.stack(s5i_p_l, axis=0)
    new_ret_prompt = jnp.stack(ret_p_l, axis=0)
    new_mem_k_prompt = jnp.stack(mk_p_l, axis=0)
    new_mem_v_prompt = jnp.stack(mv_p_l, axis=0)
    new_s5_re_sample = jnp.stack(s5r_s_l, axis=0)
    new_s5_im_sample = jnp.stack(s5i_s_l, axis=0)
    new_ret_sample = jnp.stack(ret_s_l, axis=0)
    return (y_prompt, y_sample, new_s5_re_prompt, new_s5_im_prompt, new_ret_prompt,
            new_mem_k_prompt, new_mem_v_prompt, new_s5_re_sample, new_s5_im_sample,
            new_ret_sample)
```

```python
import numpy as np
import concourse.bass as bass
import concourse.mybir as mybir
from concourse.bass_utils import run_bass_kernel_spmd
from contextlib import ExitStack

F32 = mybir.dt.float32
BF16 = mybir.dt.bfloat16
AF = mybir.ActivationFunctionType
ALU = mybir.AluOpType

D = 1024
SEQ = 2048
NTP = 16
TS = 64
NT = 17
NTOK = SEQ + TS
G = 32
DFF = 4096
MEM = 256
EPS = 1e-6
PAST = 16384.0
MAGIC = 12582912.0
TWO_PI = float(2.0 * np.pi)
ML = [7, 6, 5, 4, 3, 2, 1, 0, 1, 2, 3, 4, 5, 6, 7, 8, -4, 0.5]
K1 = len(ML)
I_A1, I_A8, I_A4, I_AM4, I_HALF = 8, 15, 3, 16, 17
GAM = [1.0 - 2.0 ** (-5.0 - h) for h in range(4)]


class Grp:
    __slots__ = ("sem", "cnt", "sealed")


class Buf:
    __slots__ = ("w", "r", "name", "grp", "ps")

    def __init__(self, name="", grp=None, ps=False):
        self.w = None
        self.r = []
        self.name = name
        self.grp = grp
        self.ps = ps


class _Rec:
    def __init__(self):
        self.call = None

    def __getattr__(self, name):
        def f(*a, **kw):
            self.call = (name, a, kw)
            return self
        return f


class Sched:
    ENG = ("pe", "dve", "act", "pool", "sp")

    def __init__(self, nc, stack, self_sync=("dve", "act", "pool")):
        self.nc = nc
        self.stack = stack
        self.prog = {k: [] for k in self.ENG}
        self.cnt = {k: 0 for k in self.ENG}
        self.waited = {k: {} for k in self.ENG}
        self.sem = {}
        self.nsem = 0
        for k in ("pe", "dve", "act", "pool"):
            self.sem[k] = self.new_sem("c_" + k)
        self.self_sync = set(self_sync)
        self.groups = []
        self.GC = self.group("gc")
        self.GP = self.group("gp")
        self.GW = [self.group("gw%d" % i) for i in range(4)]
        self.GX = self.group("gx")
        self.GL = [self.group("gl%d" % i) for i in range(2)]
        self.GS = [self.group("gs%d" % i) for i in range(3)]

    def group(self, name):
        g = Grp()
        g.sem = self.new_sem(name)
        g.cnt = 0
        g.sealed = False
        self.groups.append(g)
        return g

    def new_sem(self, name):
        self.nsem += 1
        assert self.nsem < 98, "too many semaphores"
        return self.stack.enter_context(self.nc.semaphore(name + "_%d" % self.nsem))

    def _waits(self, eng, deps):
        w = self.waited[eng]
        need = {}
        dd = []
        for d in deps:
            if isinstance(d, Grp):
                d.sealed = True
                dd.append((d.sem, d.cnt))
            else:
                dd.append(d)
        deps = dd
        for (s, v) in deps:
            if eng in self.sem and s is self.sem[eng] and eng not in self.self_sync:
                continue
            k = id(s)
            if w.get(k, 0) >= v:
                continue
            if k not in need or need[k][1] < v:
                need[k] = (s, v)
        for k, (s, v) in need.items():
            w[k] = v
            self.prog[eng].append(lambda e, s=s, v=v: e.wait_ge(s, v))

    def op(self, eng, fn, reads=(), writes=()):
        deps = []
        for b in reads:
            if b.w is not None:
                deps.append(b.w)
            if b.ps:
                mys = self.sem[eng]
                deps.extend(d for d in b.r if not (isinstance(d, tuple) and d[0] is mys))
        for b in writes:
            if b.w is not None:
                deps.append(b.w)
            deps.extend(b.r)
        self._waits(eng, deps)
        self.cnt[eng] += 1
        c = self.cnt[eng]
        s = self.sem[eng]
        rec = _Rec()
        fn(rec)
        name, a, kw = rec.call
        self.prog[eng].append(lambda e, name=name, a=a, kw=kw, s=s: getattr(e, name)(*a, **kw).then_inc(s, 1))
        for b in reads:
            b.r.append((s, c))
        for b in writes:
            b.w = (s, c)
            b.r = []

    def dma(self, q, out, in_, reads=(), writes=(), **kw):
        tb = writes[0] if writes else reads[0]
        g = tb.grp
        if g is None:
            g = self.GP if q == "pool" else (self.GC if writes else self.GS[0])
        deps = []
        for b in reads:
            if b.w is not None:
                deps.append(b.w)
        for b in writes:
            if b.w is not None and b.w is not g:
                deps.append(b.w)
            deps.extend(b.r)
        self._waits(q, deps)
        if g.sealed and g.cnt > 0:
            self._waits(q, [(g.sem, g.cnt)])
        g.sealed = False
        g.cnt += 16
        s = g.sem
        self.prog[q].append(
            lambda e, out=out, in_=in_, s=s, kw=kw: e.dma_start(out=out, in_=in_, **kw).then_inc(s, 16))
        for b in reads:
            b.r.append(g)
        for b in writes:
            b.w = g
            b.r = []

    def barrier(self, engines=None):
        deps = [(self.sem[k], self.cnt[k]) for k in ("pe", "dve", "act", "pool") if self.cnt[k] > 0]
        deps += [g for g in self.groups if g.cnt > 0]
        for e in (engines or self.ENG):
            self._waits(e, deps)

    def run_block(self):
        nc = self.nc
        with nc.Block() as block:
            @block.sync
            def _(e):
                for t in self.prog["sp"]:
                    t(e)

            @block.tensor
            def _(e):
                for t in self.prog["pe"]:
                    t(e)

            @block.vector
            def _(e):
                for t in self.prog["dve"]:
                    t(e)

            @block.scalar
            def _(e):
                for t in self.prog["act"]:
                    t(e)

            @block.gpsimd
            def _(e):
                for t in self.prog["pool"]:
                    t(e)


_CONSTS = None


def _consts():
    global _CONSTS
    if _CONSTS is not None:
        return _CONSTS
    f = np.float32
    c = {}
    c["c_ident"] = np.eye(128, dtype=f)
    m = np.zeros((8, 128, 240), f)
    for a in range(8):
        for i in range(16):
            m[a, 16 * a + i, 112 + i] = 1.0
    c["c_masters"] = m
    ml = np.array(ML, np.float64)
    rows = np.concatenate([ml / (2 * np.pi), ml, 8.0 * (np.arange(64) + 1) / (2 * np.pi)])
    c["c_rows"] = rows.astype(f)[None, :]
    sg = np.zeros((128, 2), f)
    sg[:64, 0] = 1.0
    sg[64:, 0] = -1.0
    sg[:64, 1] = -1.0
    sg[64:, 1] = 1.0
    c["c_sgn"] = sg
    inv = (f(10000.0) ** (-(np.arange(64, dtype=f) / f(64.0)))).astype(f)
    pos = np.zeros((128, NT), f)
    for n in range(NTP):
        pos[:, n] = 128 * n + np.arange(128)
    pos[:64, 16] = PAST + (np.arange(64) % 4)
    ang = (pos[:, :, None] * inv[None, None, :]).astype(f).astype(np.float64)
    c["c_rope"] = np.stack([np.cos(ang), np.sin(ang), -np.sin(ang)]).astype(f)
    lg = np.log(np.array(GAM, np.float64))
    sc = 128.0 ** -0.5
    idx = np.arange(128)
    dm = np.zeros((128, 4, 128), np.float64)
    diff = idx[None, :] - idx[:, None]
    for h in range(4):
        dm[:, h, :] = np.where(diff >= 0, np.exp(np.maximum(diff, 0) * lg[h]), 0.0) * sc
    c["c_dmask_p"] = dm.reshape(128, 512).astype(f)
    ds_ = np.zeros((64, 4, 64), np.float64)
    r = np.arange(64)
    bb = r // 4
    tt = r % 4
    same = bb[:, None] == bb[None, :]
    dts = tt[None, :] - tt[:, None]
    for h in range(4):
        ds_[:, h, :] = np.where(same & (dts >= 0), np.exp(np.maximum(dts, 0) * lg[h]), 0.0) * sc
    c["c_dmask_s"] = ds_.reshape(64, 256).astype(f)
    xi_p = np.stack([np.exp((idx + 1.0) * lg[h]) * sc for h in range(4)])
    xi_s = np.stack([np.exp((tt + 1.0) * lg[h]) * sc for h in range(4)])
    c["c_xi"] = np.concatenate([xi_p.reshape(-1), xi_s.reshape(-1)]).astype(f)[None, :]
    zp = np.stack([np.exp((127.0 - idx) * lg[h]) for h in range(4)], axis=1)
    c["c_zeta_p"] = zp.astype(f)
    zs = np.zeros((64, 16, 4), np.float64)
    for h in range(4):
        for b in range(16):
            zs[:, b, h] = np.where(bb == b, np.exp((3.0 - tt) * lg[h]), 0.0)
    c["c_zs"] = zs.reshape(64, 64).astype(f)
    cm = np.zeros((16, 64), f)
    for b in range(16):
        cm[b, 4 * b:4 * b + 4] = 1.0
    c["c_cmask"] = cm.reshape(1, -1)
    _CONSTS = c
    return c


W_NAMES = ["g_mix", "w_in", "lam_re", "lam_im", "log_dt", "b_re", "b_im", "c_re", "c_im", "d_skip", "w_glu",
           "ret_gn", "w_out", "g_xattn", "g_mem", "w_mq", "w_mk", "w_mv", "w_mo", "g_mlp", "w_up", "w_down",
           "g_final"]
W_SHAPES = {"g_mix": [D], "w_in": [D, 2560], "lam_re": [G, 64], "lam_im": [G, 64], "log_dt": [G],
            "b_re": [G, 64, 16], "b_im": [G, 64, 16], "c_re": [G * 16, 64], "c_im": [G * 16, 64], "d_skip": [512],
            "w_glu": [512, 512], "ret_gn": [512], "w_out": [D, D], "g_xattn": [D], "g_mem": [D], "w_mq": [D, D],
            "w_mk": [D, D], "w_mv": [D, D], "w_mo": [D, D], "g_mlp": [D], "w_up": [D, DFF], "w_down": [DFF, D],
            "g_final": [D]}
IN_SHAPES = {"xp": [SEQ, D], "xs": [TS, D], "memp": [MEM, D], "s5r": [512, 64], "s5i": [512, 64],
             "sret": [16, 4, 128, 128], "ck": [16, MEM, D], "cv": [16, MEM, D]}
OUT_SHAPES = {"yp": [SEQ, D], "ys": [TS, D], "o_s5r_p": [G, 64], "o_s5i_p": [G, 64], "o_ret_p": [4, 128, 128],
              "o_mk": [MEM, D], "o_mv": [MEM, D], "o_s5r_s": [512, 64], "o_s5i_s": [512, 64],
              "o_ret_s": [16, 4, 128, 128]}


def build(stage=99, dbg=False):
    nc = bass.Bass("TRN2", target_bir_lowering=False)
    cst = _consts()
    I = {}
    for k, shp in list(IN_SHAPES.items()) + list(W_SHAPES.items()):
        I[k] = nc.dram_tensor(k, shp, F32, kind="ExternalInput").ap()
    for k, v in cst.items():
        I[k] = nc.dram_tensor(k, list(v.shape), F32, kind="ExternalInput").ap()
    O = {}
    for k, shp in OUT_SHAPES.items():
        O[k] = nc.dram_tensor(k, shp, F32, kind="ExternalOutput").ap()
    if dbg:
        O["dbg_ssm"] = nc.dram_tensor("dbg_ssm", [128, 4, NTOK], F32, kind="ExternalOutput").ap()
        O["dbg_x"] = nc.dram_tensor("dbg_x", [128, NT, D], F32, kind="ExternalOutput").ap()

    with ExitStack() as st:
        S = Sched(nc, st)

        def alloc(stack, name, shape, dt=F32):
            return stack.enter_context(nc.sbuf_tensor(name, shape, dt))

        def palloc(stack, name, shape, dt=F32):
            return stack.enter_context(nc.psum_tensor(name, shape, dt))

        def V(fn, r=(), w=()):
            S.op("dve", fn, reads=r, writes=w)

        def A(fn, r=(), w=()):
            S.op("act", fn, reads=r, writes=w)

        import os as _os0
        _nopool = _os0.environ.get("K_NOPOOL") == "1"

        def PL(fn, r=(), w=()):
            S.op("dve" if _nopool else "pool", fn, reads=r, writes=w)

        def T(fn, r=(), w=()):
            S.op("pe", fn, reads=r, writes=w)

        nck = nc.allow_non_contiguous_dma(reason="small param layout loads")
        nck.__enter__()

        identb = alloc(st, "identb", [128, 128], BF16)
        identf = alloc(st, "identf", [128, 128], F32)
        sgn = alloc(st, "sgn", [128, 2])
        epsc = alloc(st, "epsc", [128, 1])
        ssmT = alloc(st, "ssmT", [128, 4, NTOK], BF16)
        b_const = Buf("const")
        b_ssmT = [Buf("ssmT%d" % i) for i in range(5)]
        b_constp = Buf("constp")
        S.dma("pool", identb[:], I["c_ident"][:, :], writes=[b_constp])
        S.dma("sp", identf[:], I["c_ident"][:, :], writes=[b_const])
        S.dma("sp", sgn[:], I["c_sgn"][:, :], writes=[b_const])
        V(lambda e: e.memset(epsc[:], EPS), r=[b_constp], w=[b_const])
        PS = [palloc(st, "ps%d" % i, [128, 512], F32) for i in range(8)]
        bPS = [Buf("ps%d" % i, ps=True) for i in range(8)]

        def ps_bf(i):
            return PS[i][:].bitcast(BF16)

        def rmsnorm_hT(xt_ap, bx, npart, gcol, hT_ap, bhT, scr, col0, ph, ln=False):
            sq, ss, rstd, hb, bscr, pbank = scr
            lim = ph if ph is not None else 99
            if lim == 0:
                V(lambda e: e.tensor_tensor(out=sq[:npart, :], in0=xt_ap, in1=xt_ap, op=ALU.mult), r=[bx], w=[bscr])
                return
            if lim == -1:
                A(lambda e: e.activation(out=sq[:npart, :], in_=xt_ap, func=AF.Square), r=[bx], w=[bscr])
                return
            A(lambda e: e.activation(out=sq[:npart, :], in_=xt_ap, func=AF.Square, accum_out=ss[:npart, :]),
              r=[bx], w=[bscr])
            if lim <= 1:
                return
            if ln:
                A(lambda e: e.activation(out=rstd[:npart, :], in_=ss[:npart, :], func=AF.Ln, scale=1.0 / D,
                                         bias=epsc[:npart, :]), r=[bscr, b_const], w=[bscr])
                A(lambda e: e.activation(out=rstd[:npart, :], in_=rstd[:npart, :], func=AF.Exp, scale=-0.5),
                  r=[bscr], w=[bscr])
            else:
                A(lambda e: e.activation(out=rstd[:npart, :], in_=ss[:npart, :], func=AF.Sqrt, scale=1.0 / D,
                                         bias=epsc[:npart, :]), r=[bscr, b_const], w=[bscr])
                V(lambda e: e.reciprocal(out=rstd[:npart, :], in_=rstd[:npart, :]), r=[bscr], w=[bscr])
            if lim <= 2:
                return
            V(lambda e: e.tensor_scalar(out=hb[:npart, :], in0=xt_ap, scalar1=rstd[:npart, :], scalar2=None,
                                        op0=ALU.mult), r=[bx, bscr], w=[bscr])
            if lim <= 3:
                return
            pv = ps_bf(pbank)
            for kt in range(8):
                T(lambda e, kt=kt: e.transpose(out=pv[:, kt * 128:kt * 128 + npart],
                                               in_=hb[:npart, kt * 128:(kt + 1) * 128],
                                               identity=identb[:npart, :npart]),
                  r=[bscr, b_const], w=[bPS[pbank]])
            V(lambda e: e.tensor_tensor(
                out=hT_ap[:, :, col0:col0 + npart],
                in0=pv.rearrange("p (k t) -> p k t", k=8)[:, :, 0:npart],
                in1=gcol.unsqueeze(2).to_broadcast([128, 8, npart]), op=ALU.mult),
              r=[bPS[pbank], b_const], w=[bhT])

        def load_w_bf16(dst, bdst, src, kt_n, ncols, c0=0):
            for kt in range(kt_n):
                for cc in range(0, ncols, 1024):
                    w_ = min(1024, ncols - cc)
                    S.dma("pool", dst[:, kt, cc:cc + w_], src[kt * 128:(kt + 1) * 128, c0 + cc:c0 + cc + w_],
                          writes=[bdst])

        with ExitStack() as sa:
            Wt = alloc(sa, "Wt", [128, G, 128], BF16)
            Wst = alloc(sa, "Wst", [128, G, 128], BF16)
            Tt = alloc(sa, "Tt", [128, G, 128], BF16)
            Vt = alloc(sa, "Vt", [128, G, 128], BF16)
            COSR = alloc(sa, "COSR", [128, G, 64])
            SINR = alloc(sa, "SINR", [128, G, 64])
            masters = alloc(sa, "masters", [128, 8, 240], BF16)
            AR = alloc(sa, "AR", [128, G, K1])
            AI = alloc(sa, "AI", [128, G, K1])
            MAGJ = alloc(sa, "MAGJ", [128, G, K1])
            DS = alloc(sa, "DS", [128, G])
            gm = alloc(sa, "gm", [128, 8])
            winu = alloc(sa, "winu", [128, 8, 512], BF16)
            wglu = alloc(sa, "wglu", [128, 4, 512], BF16)
            b_tab = Buf("s5tab")
            b_winu = Buf("winu", S.GW[0])
            b_wglu = Buf("wglu", S.GW[1])
            b_tabp = Buf("s5tabp")
            S.dma("pool", masters[:], I["c_masters"].rearrange("a k j -> k a j"), writes=[b_tabp])
            S.dma("sp", gm[:], I["g_mix"].rearrange("(k p) -> p k", p=128), writes=[b_tab])
            for tau in range(8):
                S.dma("sp", DS[16 * tau:16 * tau + 16, :], I["d_skip"].rearrange("(g h) -> h g", h=16),
                      writes=[b_tab])
            load_w_bf16(winu, b_winu, I["w_in"], 8, 512, 0)
            load_w_bf16(wglu, b_wglu, I["w_glu"], 4, 512, 0)

            with ExitStack() as s0:
                rows = alloc(s0, "rows", [128, 2 * K1 + 64])
                LR = alloc(s0, "LR", [128, G])
                LI = alloc(s0, "LI", [128, G])
                DT = alloc(s0, "DT", [128, G])
                LRDT = alloc(s0, "LRDT", [128, G])
                LIDT = alloc(s0, "LIDT", [128, G])
                tA = alloc(s0, "tA", [128, G, 64])
                tB = alloc(s0, "tB", [128, G, 64])
                tC = alloc(s0, "tC", [128, G, 64])
                COSJ = alloc(s0, "COSJ", [128, G, K1])
                SINJ = alloc(s0, "SINJ", [128, G, K1])
                sm = alloc(s0, "sm", [128, 12, G])
                Br1 = alloc(s0, "Br1", [128, G, 16])
                Br2 = alloc(s0, "Br2", [128, G, 16])
                BB1 = alloc(s0, "BB1", [128, G, 16])
                BB2 = alloc(s0, "BB2", [128, G, 16])
                tb1 = alloc(s0, "tb1", [128, G, 16])
                big1 = alloc(s0, "big1", [128, G, 128])
                big2 = alloc(s0, "big2", [128, G, 128])
                WTpad = alloc(s0, "WTpad", [128, G, 256], BF16)
                WTs = alloc(s0, "WTs", [128, G, 128], BF16)
                CN1 = alloc(s0, "CN1", [128, 4, 128])
                CN2 = alloc(s0, "CN2", [128, 4, 128])
                CMa = alloc(s0, "CMa", [128, G, 16])
                CMb = alloc(s0, "CMb", [128, G, 16])
                CMab = alloc(s0, "CMab", [128, G, 16], BF16)
                b0 = Buf("p0in")
                bt = Buf("p0tmp")
                S.dma("sp", rows[:], I["c_rows"][0:1, :].partition_broadcast(128), writes=[b0])
                for hf in range(2):
                    S.dma("sp", LR[64 * hf:64 * hf + 64, :], I["lam_re"].rearrange("g p -> p g"), writes=[b0])
                    S.dma("sp", LI[64 * hf:64 * hf + 64, :], I["lam_im"].rearrange("g p -> p g"), writes=[b0])
                S.dma("sp", DT[:], I["log_dt"].rearrange("(o g) -> o g", o=1).partition_broadcast(128), writes=[b0])
                S.dma("sp", Br1[0:64], I["b_re"].rearrange("g p h -> p g h"), writes=[b0])
                S.dma("sp", Br1[64:128], I["b_im"].rearrange("g p h -> p g h"), writes=[b0])
                S.dma("sp", Br2[0:64], I["b_im"].rearrange("g p h -> p g h"), writes=[b0])
                S.dma("sp", Br2[64:128], I["b_re"].rearrange("g p h -> p g h"), writes=[b0])
                S.dma("sp", CN1[:, :, 0:64], I["c_re"].rearrange("(c r) p -> r c p", r=128), writes=[b0])
                S.dma("sp", CN1[:, :, 64:128], I["c_im"].rearrange("(c r) p -> r c p", r=128), writes=[b0])
                S.dma("sp", CN2[:, :, 0:64], I["c_im"].rearrange("(c r) p -> r c p", r=128), writes=[b0])
                S.dma("sp", CN2[:, :, 64:128], I["c_re"].rearrange("(c r) p -> r c p", r=128), writes=[b0])
                MT1 = rows[:, 0:K1]
                MLr = rows[:, K1:2 * K1]
                MRT = rows[:, 2 * K1:2 * K1 + 64]
                A(lambda e: e.activation(out=DT[:], in_=DT[:], func=AF.Exp), r=[b0], w=[b0])
                V(lambda e: e.tensor_tensor(out=LRDT[:], in0=LR[:], in1=DT[:], op=ALU.mult), r=[b0], w=[bt])
                V(lambda e: e.tensor_tensor(out=LIDT[:], in0=LI[:], in1=DT[:], op=ALU.mult), r=[b0], w=[bt])

                def trig(mt_ap, K, cos_out, sin_out):
                    shp = [128, G, K]
                    a_, b_, c_ = tA[:, :, 0:K], tB[:, :, 0:K], tC[:, :, 0:K]
                    V(lambda e: e.tensor_tensor(out=a_, in0=LIDT[:].unsqueeze(2).to_broadcast(shp),
                                                in1=mt_ap.unsqueeze(1).to_broadcast(shp), op=ALU.mult),
                      r=[bt, b0], w=[bt])
                    for (outp, off) in ((sin_out, 0.0), (cos_out, 0.25)):
                        if outp is None:
                            continue
                        V(lambda e, off=off: e.tensor_scalar(out=c_, in0=a_, scalar1=off, scalar2=None,
                                                             op0=ALU.add), r=[bt], w=[bt])
                        V(lambda e: e.tensor_scalar(out=b_, in0=c_, scalar1=MAGIC, scalar2=None, op0=ALU.add),
                          r=[bt], w=[bt])
                        V(lambda e: e.tensor_scalar(out=b_, in0=b_, scalar1=MAGIC, scalar2=None, op0=ALU.subtract),
                          r=[bt], w=[bt])
                        V(lambda e: e.tensor_tensor(out=c_, in0=c_, in1=b_, op=ALU.subtract), r=[bt], w=[bt])
                        A(lambda e, outp=outp: e.activation(out=outp, in_=c_, func=AF.Sin, scale=TWO_PI),
                          r=[bt], w=[b_tab])

                trig(MT1, K1, COSJ[:], SINJ[:])
                trig(MRT, 64, COSR[:], SINR[:])
                shpj = [128, G, K1]
                V(lambda e: e.tensor_tensor(out=MAGJ[:], in0=LRDT[:].unsqueeze(2).to_broadcast(shpj),
                                            in1=MLr.unsqueeze(1).to_broadcast(shpj), op=ALU.mult),
                  r=[bt, b0], w=[b_tab])
                A(lambda e: e.activation(out=MAGJ[:], in_=MAGJ[:], func=AF.Exp), r=[b_tab], w=[b_tab])
                V(lambda e: e.tensor_tensor(out=AR[:], in0=MAGJ[:], in1=COSJ[:], op=ALU.mult), r=[b_tab], w=[b_tab])
                V(lambda e: e.tensor_tensor(out=AI[:], in0=MAGJ[:], in1=SINJ[:], op=ALU.mult), r=[b_tab], w=[b_tab])
                em1, shalf, cm1, am1r, ai1, den, fr, fi, t0_, t1_ = [sm[:, i, :] for i in range(10)]
                x_ = LRDT[:]
                V(lambda e: e.tensor_scalar(out=em1, in0=x_, scalar1=0.2, scalar2=1.0, op0=ALU.mult, op1=ALU.add),
                  r=[bt], w=[bt])
                for cf in (0.25, 1.0 / 3.0, 0.5):
                    V(lambda e: e.tensor_tensor(out=em1, in0=em1, in1=x_, op=ALU.mult), r=[bt], w=[bt])
                    V(lambda e, cf=cf: e.tensor_scalar(out=em1, in0=em1, scalar1=cf, scalar2=1.0, op0=ALU.mult,
                                                       op1=ALU.add), r=[bt], w=[bt])
                V(lambda e: e.tensor_tensor(out=em1, in0=em1, in1=x_, op=ALU.mult), r=[bt], w=[bt])
                V(lambda e: e.tensor_copy(out=shalf, in_=SINJ[:, :, I_HALF]), r=[b_tab], w=[bt])
                V(lambda e: e.scalar_tensor_tensor(out=cm1, in0=shalf, scalar=-2.0, op0=ALU.mult, in1=shalf,
                                                   op1=ALU.mult), r=[bt], w=[bt])
                V(lambda e: e.tensor_tensor(out=am1r, in0=em1, in1=COSJ[:, :, I_A1], op=ALU.mult), r=[bt, b_tab], w=[bt])
                V(lambda e: e.tensor_tensor(out=am1r, in0=am1r, in1=cm1, op=ALU.add), r=[bt], w=[bt])
                V(lambda e: e.tensor_copy(out=ai1, in_=AI[:, :, I_A1]), r=[b_tab], w=[bt])
                V(lambda e: e.tensor_tensor(out=den, in0=LR[:], in1=LR[:], op=ALU.mult), r=[b0], w=[bt])
                V(lambda e: e.tensor_tensor(out=t0_, in0=LI[:], in1=LI[:], op=ALU.mult), r=[b0], w=[bt])
                V(lambda e: e.tensor_tensor(out=den, in0=den, in1=t0_, op=ALU.add), r=[bt], w=[bt])
                V(lambda e: e.reciprocal(out=den, in_=den), r=[bt], w=[bt])
                V(lambda e: e.tensor_tensor(out=fr, in0=am1r, in1=LR[:], op=ALU.mult), r=[bt, b0], w=[bt])
                V(lambda e: e.tensor_tensor(out=t0_, in0=ai1, in1=LI[:], op=ALU.mult), r=[bt, b0], w=[bt])
                V(lambda e: e.tensor_tensor(out=fr, in0=fr, in1=t0_, op=ALU.add), r=[bt], w=[bt])
                V(lambda e: e.tensor_tensor(out=fr, in0=fr, in1=den, op=ALU.mult), r=[bt], w=[bt])
                V(lambda e: e.tensor_tensor(out=fi, in0=ai1, in1=LR[:], op=ALU.mult), r=[bt, b0], w=[bt])
                V(lambda e: e.tensor_tensor(out=t0_, in0=am1r, in1=LI[:], op=ALU.mult), r=[bt, b0], w=[bt])
                V(lambda e: e.tensor_tensor(out=fi, in0=fi, in1=t0_, op=ALU.subtract), r=[bt], w=[bt])
                V(lambda e: e.tensor_tensor(out=fi, in0=fi, in1=den, op=ALU.mult), r=[bt], w=[bt])
                V(lambda e: e.tensor_scalar(out=Br2[:], in0=Br2[:], scalar1=sgn[:, 1:2], scalar2=None, op0=ALU.mult),
                  r=[b0, b_const], w=[b0])
                shb = [128, G, 16]
                frb = fr.unsqueeze(2).to_broadcast(shb)
                fib = fi.unsqueeze(2).to_broadcast(shb)
                V(lambda e: e.tensor_tensor(out=BB1[:], in0=Br1[:], in1=frb, op=ALU.mult), r=[b0, bt], w=[bt])
                V(lambda e: e.tensor_tensor(out=tb1[:], in0=Br2[:], in1=fib, op=ALU.mult), r=[b0, bt], w=[bt])
                V(lambda e: e.tensor_tensor(out=BB1[:], in0=BB1[:], in1=tb1[:], op=ALU.add), r=[bt], w=[bt])
                V(lambda e: e.tensor_tensor(out=BB2[:], in0=Br2[:], in1=frb, op=ALU.mult), r=[b0, bt], w=[bt])
                V(lambda e: e.tensor_tensor(out=tb1[:], in0=Br1[:], in1=fib, op=ALU.mult), r=[b0, bt], w=[bt])
                V(lambda e: e.tensor_tensor(out=BB2[:], in0=BB2[:], in1=tb1[:], op=ALU.subtract), r=[bt], w=[bt])
                sh4 = [128, G, 8, 16]
                arv = AR[:, :, 0:8].unsqueeze(3).to_broadcast(sh4)
                aiv = AI[:, :, 0:8].unsqueeze(3).to_broadcast(sh4)
                bb1 = BB1[:].unsqueeze(2).to_broadcast(sh4)
                bb2 = BB2[:].unsqueeze(2).to_broadcast(sh4)
                g1 = big1[:].rearrange("p g (s h) -> p g s h", s=8)
                g2 = big2[:].rearrange("p g (s h) -> p g s h", s=8)
                V(lambda e: e.memset(WTpad[:], 0.0), w=[bt])
                V(lambda e: e.tensor_tensor(out=g1, in0=arv, in1=bb1, op=ALU.mult), r=[b_tab, bt], w=[bt])
                V(lambda e: e.tensor_tensor(out=g2, in0=aiv, in1=bb2, op=ALU.mult), r=[b_tab, bt], w=[bt])
                V(lambda e: e.tensor_tensor(out=WTpad[:, :, 0:128], in0=big1[:], in1=big2[:], op=ALU.add),
                  r=[bt], w=[bt])
                V(lambda e: e.tensor_tensor(out=g1, in0=arv, in1=bb2, op=ALU.mult), r=[b_tab, bt], w=[bt])
                V(lambda e: e.tensor_tensor(out=g2, in0=aiv, in1=bb1, op=ALU.mult), r=[b_tab, bt], w=[bt])
                V(lambda e: e.tensor_tensor(out=WTs[:], in0=big1[:], in1=big2[:], op=ALU.subtract), r=[bt], w=[bt])
                for (src_fn, dstt) in ((lambda g: WTpad[:, g, 0:128], Wt), (lambda g: WTs[:, g, :], Wst)):
                    for gq in range(8):
                        bank = gq % 2
                        pv = ps_bf(bank)
                        for j in range(4):
                            g = gq * 4 + j
                            T(lambda e, g=g, j=j, pv=pv, src_fn=src_fn: e.transpose(
                                out=pv[:, j * 128:(j + 1) * 128], in_=src_fn(g), identity=identb[:]),
                              r=[bt, b_const], w=[bPS[bank]])
                        A(lambda e, gq=gq, pv=pv, dstt=dstt: e.copy(
                            out=dstt[:, gq * 4:gq * 4 + 4, :], in_=pv[:, 0:512].rearrange("p (j c) -> p j c", j=4)),
                          r=[bPS[bank]], w=[b_tab])
                for (CN, CM, col) in ((CN1, CMa, 0), (CN2, CMb, None)):
                    for c4 in range(4):
                        bank = 2 + (c4 % 2)
                        T(lambda e, CN=CN, c4=c4, bank=bank: e.transpose(out=PS[bank][:, 0:128], in_=CN[:, c4, :],
                                                                         identity=identf[:]),
                          r=[b0, b_const], w=[bPS[bank]])
                        if col is not None:
                            V(lambda e, CM=CM, c4=c4, bank=bank: e.tensor_scalar(
                                out=CM[:, c4 * 8:(c4 + 1) * 8, :],
                                in0=PS[bank][:, 0:128].rearrange("p (g h) -> p g h", g=8),
                                scalar1=sgn[:, 0:1], scalar2=None, op0=ALU.mult),
                              r=[bPS[bank], b_const], w=[bt])
                        else:
                            V(lambda e, CM=CM, c4=c4, bank=bank: e.tensor_scalar(
                                out=CM[:, c4 * 8:(c4 + 1) * 8, :],
                                in0=PS[bank][:, 0:128].rearrange("p (g h) -> p g h", g=8),
                                scalar1=-1.0, scalar2=None, op0=ALU.mult),
                              r=[bPS[bank]], w=[bt])
                V(lambda e: e.tensor_copy(out=CMab[:], in_=CMa[:]), r=[bt], w=[bt])
                afw = AR[:, :, 8:16].unsqueeze(3).to_broadcast(sh4)
                aifw = AI[:, :, 8:16].unsqueeze(3).to_broadcast(sh4)
                cma = CMa[:].unsqueeze(2).to_broadcast(sh4)
                cmb = CMb[:].unsqueeze(2).to_broadcast(sh4)
                V(lambda e: e.tensor_tensor(out=g1, in0=afw, in1=cma, op=ALU.mult), r=[b_tab, bt], w=[bt])
                V(lambda e: e.tensor_tensor(out=g2, in0=aifw, in1=cmb, op=ALU.mult), r=[b_tab, bt], w=[bt])
                V(lambda e: e.tensor_tensor(out=Vt[:], in0=big1[:], in1=big2[:], op=ALU.add), r=[bt], w=[b_tab])
                for gq in range(8):
                    bank = 4 + (gq % 2)
                    for j in range(4):
                        g = gq * 4 + j
                        for tau in range(8):
                            c0 = (7 - tau) * 16
                            T(lambda e, g=g, j=j, tau=tau, c0=c0, bank=bank: e.matmul(
                                PS[bank][:, j * 128 + tau * 16:j * 128 + tau * 16 + 16],
                                lhsT=WTpad[:, g, c0:c0 + 128], rhs=CMab[:, g, :], start=True, stop=True),
                              r=[bt], w=[bPS[bank]])
                    A(lambda e, gq=gq, bank=bank: e.copy(
                        out=Tt[:, gq * 4:gq * 4 + 4, :], in_=PS[bank][:].rearrange("p (j c) -> p j c", j=4)),
                      r=[bPS[bank]], w=[b_tab])
                S.barrier()
            xst = [alloc(sa, "xst%d" % i, [128, D]) for i in range(2)]
            bxst = [Buf("xst%d" % i, S.GL[i]) for i in range(2)]
            sq = alloc(sa, "sq", [128, D])
            ss = alloc(sa, "ss", [128, 1])
            rstd = alloc(sa, "rstd", [128, 1])
            hb = alloc(sa, "hb", [128, D], BF16)
            bscr = Buf("scrA")
            hT = alloc(sa, "hT", [128, 8, 512], BF16)
            bhT = Buf("hT")
            uT = alloc(sa, "uT", [128, 4, 512], BF16)
            buT = Buf("uT")
            U = alloc(sa, "U", [128, G, 64], BF16)
            bU = Buf("U")
            rr = alloc(sa, "rr", [128, G, 64])
            rs = alloc(sa, "rs", [128, G, 64])
            ww = alloc(sa, "ww", [128, G, 64])
            ws = alloc(sa, "ws", [128, G, 64])
            tmpr = alloc(sa, "tmpr", [128, 16, 64])
            b_r, b_rs, b_w, b_ws, b_tmpr = Buf("r"), Buf("rs"), Buf("w"), Buf("ws"), Buf("tmpr")
            Xb = alloc(sa, "Xb", [128, G, 65], BF16)
            bXb = Buf("Xb")
            Xc = alloc(sa, "Xc", [128, G])
            Xsc = alloc(sa, "Xsc", [128, G])
            ctmp = alloc(sa, "ctmp", [128, 2, G])
            bXc = Buf("Xc", S.GS[0])
            ytmp = alloc(sa, "ytmp", [128, 8, 64])
            bytmp = Buf("ytmp")
            Zt = alloc(sa, "Zt", [128, G, 64], BF16)
            bZ = Buf("Z")
            zT = alloc(sa, "zT", [128, 4, 512], BF16)
            bzT = Buf("zT")
            sig = alloc(sa, "sig", [128, 4, 512])
            bsig = Buf("sig")
            H0 = alloc(sa, "H0", [128, 512])
            H0s = alloc(sa, "H0s", [128, 512])
            hn = alloc(sa, "hn", [128, 4, 128])
            hn2 = alloc(sa, "hn2", [128, 4, 128])
            Hp = alloc(sa, "Hp", [128, G, 16])
            Xf = alloc(sa, "Xf", [128, G, 16])
            xo = alloc(sa, "xo", [128, 4, 128])
            bH = Buf("H0")
            bxo = Buf("xo", S.GS[1])
            V(lambda e: e.memset(Xc[:], 0.0), r=[b_tabp], w=[bXc, b_tab])
            V(lambda e: e.memset(Xsc[:], 0.0), w=[bXc])
            V(lambda e: e.memset(Xb[:], 0.0), w=[bXb])

            blocks = [(i * 512, 512, False) for i in range(4)] + [(SEQ, TS, True)]
            if _os0.environ.get("K1A") == "0":
                blocks = []
            for bi, (t0, n, is_s) in enumerate(blocks):
                nch = n // 8 if not is_s else 16
                ntile = (n + 127) // 128
                for ti in range(ntile):
                    npart = min(128, n - ti * 128)
                    slot = (bi * 4 + ti) % 2
                    src = I["xs"][:, :] if is_s else I["xp"][t0 + ti * 128:t0 + ti * 128 + 128, :]
                    S.dma("sp", xst[slot][:npart, :], src, writes=[bxst[slot]])
                    rmsnorm_hT(xst[slot][:npart, :], bxst[slot], npart, gm[:], hT, bhT,
                               (sq, ss, rstd, hb, bscr, 7), ti * 128, None)
                for ct in range(4):
                    bank = ct
                    for kt in range(8):
                        T(lambda e, ct=ct, kt=kt, bank=bank: e.matmul(
                            PS[bank][:, 0:n], lhsT=winu[:, kt, ct * 128:(ct + 1) * 128], rhs=hT[:, kt, 0:n],
                            start=(kt == 0), stop=(kt == 7)), r=[b_winu, bhT], w=[bPS[bank]])
                    A(lambda e, ct=ct, bank=bank: e.copy(out=uT[:, ct, 0:n], in_=PS[bank][:, 0:n]),
                      r=[bPS[bank]], w=[buT])
                for gq in range(4):
                    bank = 4 + (gq % 2)
                    for j in range(8):
                        g = gq * 8 + j
                        ct, gl = g // 8, g % 8
                        if not is_s:
                            uv = uT[:, ct, 0:n].rearrange("p (c s) -> p s c", s=8)
                            sig_list = list(range(8))
                        else:
                            uv = uT[:, ct, 0:n].rearrange("p (b t) -> p t b", t=4)
                            sig_list = [4, 5, 6, 7]
                        for si, sg_ in enumerate(sig_list):
                            rhs = uv[:, sg_ if not is_s else si, :]
                            T(lambda e, j=j, gl=gl, sg_=sg_, rhs=rhs, si=si, bank=bank, L=len(sig_list): e.matmul(
                                PS[bank][:, j * 64:j * 64 + nch],
                                lhsT=masters[:, gl, 112 - 16 * sg_:240 - 16 * sg_], rhs=rhs,
                                start=(si == 0), stop=(si == L - 1)),
                              r=[b_tab, buT], w=[bPS[bank]])
                    A(lambda e, gq=gq, bank=bank: e.copy(
                        out=U[:, gq * 8:gq * 8 + 8, 0:nch],
                        in_=PS[bank][:].rearrange("p (j c) -> p j c", j=8)[:, :, 0:nch]),
                      r=[bPS[bank]], w=[bU])
                if not is_s:
                    for hf in range(2):
                        for j in range(16):
                            g = hf * 16 + j
                            for (wt, bk) in ((Wt, 0), (Wst, 2)):
                                bank = bk + j // 8
                                T(lambda e, g=g, j=j, wt=wt, bank=bank: e.matmul(
                                    PS[bank][:, (j % 8) * 64:(j % 8) * 64 + 64], lhsT=wt[:, g, :], rhs=U[:, g, :],
                                    start=True, stop=True), r=[b_tab, bU], w=[bPS[bank]])
                        for q in range(2):
                            gs = slice(hf * 16 + q * 8, hf * 16 + q * 8 + 8)
                            Sv = PS[q][:].rearrange("p (j c) -> p j c", j=8)
                            Ssv = PS[2 + q][:].rearrange("p (j c) -> p j c", j=8)
                            tm = tmpr[:, q * 8:q * 8 + 8, :]
                            V(lambda e, gs=gs, Sv=Sv: e.tensor_tensor(out=rr[:, gs, :], in0=Sv, in1=COSR[:, gs, :],
                                                                     op=ALU.mult), r=[bPS[q], b_tab], w=[b_r])
                            V(lambda e, gs=gs, Ssv=Ssv, tm=tm: e.tensor_tensor(out=tm, in0=Ssv, in1=SINR[:, gs, :],
                                                                              op=ALU.mult),
                              r=[bPS[2 + q], b_tab], w=[b_tmpr])
                            V(lambda e, gs=gs, tm=tm: e.tensor_tensor(out=rr[:, gs, :], in0=rr[:, gs, :], in1=tm,
                                                                     op=ALU.subtract), r=[b_r, b_tmpr], w=[b_r])
                            V(lambda e, gs=gs, Ssv=Ssv: e.tensor_tensor(out=rs[:, gs, :], in0=Ssv, in1=COSR[:, gs, :],
                                                                       op=ALU.mult), r=[bPS[2 + q], b_tab], w=[b_rs])
                            V(lambda e, gs=gs, Sv=Sv, tm=tm: e.tensor_tensor(out=tm, in0=Sv, in1=SINR[:, gs, :],
                                                                            op=ALU.mult),
                              r=[bPS[q], b_tab], w=[b_tmpr])
                            V(lambda e, gs=gs, tm=tm: e.tensor_tensor(out=rs[:, gs, :], in0=rs[:, gs, :], in1=tm,
                                                                     op=ALU.add), r=[b_rs, b_tmpr], w=[b_rs])
                    for g in range(G):
                        rho = MAGJ[:, g, I_A8:I_A8 + 1].to_broadcast([128, 64])
                        V(lambda e, g=g, rho=rho: e.tensor_tensor_scan(
                            out=ww[:, g, :], data0=rho, data1=rr[:, g, :], initial=Xc[:, g:g + 1], op0=ALU.mult,
                            op1=ALU.add), r=[b_r, b_tab, bXc], w=[b_w])
                        V(lambda e, g=g, rho=rho: e.tensor_tensor_scan(
                            out=ws[:, g, :], data0=rho, data1=rs[:, g, :], initial=Xsc[:, g:g + 1], op0=ALU.mult,
                            op1=ALU.add), r=[b_rs, b_tab, bXc], w=[b_ws])
                    ce, se_ = COSR[:, :, 63], SINR[:, :, 63]
                    we, wse = ww[:, :, 63], ws[:, :, 63]
                    V(lambda e: e.tensor_tensor(out=ctmp[:, 0, :], in0=ce, in1=we, op=ALU.mult), r=[b_w, b_tab], w=[bscr])
                    V(lambda e: e.tensor_tensor(out=ctmp[:, 1, :], in0=se_, in1=wse, op=ALU.mult), r=[b_ws, b_tab], w=[bscr])
                    V(lambda e: e.tensor_tensor(out=Xc[:], in0=ctmp[:, 0, :], in1=ctmp[:, 1, :], op=ALU.add),
                      r=[bscr], w=[bXc])
                    V(lambda e: e.tensor_tensor(out=ctmp[:, 0, :], in0=ce, in1=wse, op=ALU.mult), r=[b_ws, b_tab], w=[bscr])
                    V(lambda e: e.tensor_tensor(out=ctmp[:, 1, :], in0=se_, in1=we, op=ALU.mult), r=[b_w, b_tab], w=[bscr])
                    V(lambda e: e.tensor_tensor(out=Xsc[:], in0=ctmp[:, 0, :], in1=ctmp[:, 1, :], op=ALU.subtract),
                      r=[bscr], w=[bXc])
                    if bi > 0:
                        V(lambda e: e.tensor_copy(out=Xb[:, :, 0], in_=Xb[:, :, 64]), r=[bXb], w=[bXb])
                    PL(lambda e: e.tensor_tensor(out=ww[:], in0=ww[:], in1=COSR[:], op=ALU.mult), r=[b_w, b_tab, bXc],
                       w=[b_w])
                    PL(lambda e: e.tensor_tensor(out=ws[:], in0=ws[:], in1=SINR[:], op=ALU.mult), r=[b_ws, b_tab, bXc],
                       w=[b_ws])
                    PL(lambda e: e.tensor_tensor(out=Xb[:, :, 1:65], in0=ww[:], in1=ws[:], op=ALU.add),
                       r=[b_w, b_ws], w=[bXb])
                    xprev = lambda g: Xb[:, g, 0:64]
                    bXprev = bXb
                    if bi == 3:
                        S.dma("sp", O["o_s5r_p"].rearrange("g p -> p g"), Xc[0:64, :], reads=[bXc])
                        S.dma("sp", O["o_s5i_p"].rearrange("g p -> p g"), Xc[64:128, :], reads=[bXc])
                else:
                    S.dma("sp", hn[:, :, 0:64], I["s5r"].rearrange("(j r) p -> r j p", r=128), writes=[bH])
                    S.dma("sp", hn[:, :, 64:128], I["s5i"].rearrange("(j r) p -> r j p", r=128), writes=[bH])
                    S.dma("sp", hn2[:, :, 0:64], I["s5i"].rearrange("(j r) p -> r j p", r=128), writes=[bH])
                    S.dma("sp", hn2[:, :, 64:128], I["s5r"].rearrange("(j r) p -> r j p", r=128), writes=[bH])
                    for (src_, dst_, bank) in ((hn, H0, 0), (hn2, H0s, 1)):
                        for j in range(4):
                            T(lambda e, src_=src_, j=j, bank=bank: e.transpose(
                                out=PS[bank][:, j * 128:(j + 1) * 128], in_=src_[:, j, :], identity=identf[:]),
                              r=[bH, b_const], w=[bPS[bank]])
                        V(lambda e, dst_=dst_, bank=bank: e.tensor_copy(out=dst_[:], in_=PS[bank][:]),
                          r=[bPS[bank]], w=[bH])
                    V(lambda e: e.tensor_scalar(out=H0s[0:64, :], in0=H0s[0:64, :], scalar1=-1.0, scalar2=None,
                                                op0=ALU.mult), r=[bH], w=[bH])
                    shs = [128, G, 16]
                    h0v = H0[:].rearrange("p (b g) -> p g b", g=G)
                    h0sv = H0s[:].rearrange("p (b g) -> p g b", g=G)

                    def abc(tab, idx):
                        return tab[:, :, idx].unsqueeze(2).to_broadcast(shs)
                    V(lambda e: e.tensor_tensor(out=Xf[:], in0=h0v, in1=abc(AR, I_AM4), op=ALU.mult), r=[bH, b_tab], w=[bxo])
                    V(lambda e: e.tensor_tensor(out=Hp[:], in0=h0sv, in1=abc(AI, I_AM4), op=ALU.mult), r=[bH, b_tab], w=[bxo])
                    V(lambda e: e.tensor_tensor(out=Xb[:, :, 0:16], in0=Xf[:], in1=Hp[:], op=ALU.add), r=[bxo], w=[bXb])
                    V(lambda e: e.tensor_tensor(out=Xf[:], in0=h0v, in1=abc(AR, I_A4), op=ALU.mult), r=[bH, b_tab], w=[bxo])
                    V(lambda e: e.tensor_tensor(out=Hp[:], in0=h0sv, in1=abc(AI, I_A4), op=ALU.mult), r=[bH, b_tab], w=[bxo])
                    V(lambda e: e.tensor_tensor(out=Xf[:], in0=Xf[:], in1=Hp[:], op=ALU.add), r=[bxo], w=[bxo])
                    for q in range(4):
                        bank = q % 2
                        for j in range(8):
                            g = q * 8 + j
                            T(lambda e, g=g, j=j, bank=bank: e.matmul(
                                PS[bank][:, j * 64:j * 64 + 16], lhsT=Wt[:, g, :], rhs=U[:, g, 0:16],
                                start=True, stop=True), r=[b_tab, bU], w=[bPS[bank]])
                        V(lambda e, q=q, bank=bank: e.tensor_tensor(
                            out=Xf[:, q * 8:q * 8 + 8, :], in0=Xf[:, q * 8:q * 8 + 8, :],
                            in1=PS[bank][:].rearrange("p (j c) -> p j c", j=8)[:, :, 0:16], op=ALU.add),
                          r=[bxo, bPS[bank]], w=[bxo])
                    Xf2 = Xf[:].rearrange("p g b -> p (g b)")
                    for j in range(4):
                        T(lambda e, j=j: e.transpose(out=PS[2][:, j * 128:(j + 1) * 128],
                                                     in_=Xf2[:, j * 128:(j + 1) * 128], identity=identf[:]),
                          r=[bxo, b_const], w=[bPS[2]])
                    V(lambda e: e.tensor_copy(out=xo[:], in_=PS[2][:].rearrange("p (j c) -> p j c", j=4)),
                      r=[bPS[2]], w=[bxo])
                    for j in range(4):
                        for gl in range(8):
                            for (nm, c0) in (("o_s5r_s", 0), ("o_s5i_s", 64)):
                                S.dma("sp", O[nm].rearrange("(b g) p -> g b p", g=G)[8 * j + gl],
                                      xo[gl * 16:gl * 16 + 16, j, c0:c0 + 64], reads=[bxo])
                    xprev = lambda g: Xb[:, g, 0:16]
                    bXprev = bXb
                for gq in range(4):
                    bank = 6 + (gq % 2)
                    for j in range(8):
                        g = gq * 8 + j
                        T(lambda e, g=g, j=j, bank=bank: e.matmul(
                            PS[bank][:, j * 64:j * 64 + nch], lhsT=Tt[:, g, :], rhs=U[:, g, 0:nch],
                            start=True, stop=False), r=[b_tab, bU], w=[bPS[bank]])
                        T(lambda e, g=g, j=j, bank=bank: e.matmul(
                            PS[bank][:, j * 64:j * 64 + nch], lhsT=Vt[:, g, :], rhs=xprev(g)[:, 0:nch],
                            start=False, stop=True), r=[b_tab, bXprev], w=[bPS[bank]])
                    gs = slice(gq * 8, gq * 8 + 8)
                    yv = PS[bank][:].rearrange("p (j c) -> p j c", j=8)[:, :, 0:nch]
                    V(lambda e, gs=gs: e.tensor_tensor(out=ytmp[:, :, 0:nch], in0=U[:, gs, 0:nch],
                                                       in1=DS[:, gs].unsqueeze(2).to_broadcast([128, 8, nch]),
                                                       op=ALU.mult), r=[bU, b_tab], w=[bytmp])
                    V(lambda e, yv=yv: e.tensor_tensor(out=ytmp[:, :, 0:nch], in0=yv, in1=ytmp[:, :, 0:nch],
                                                       op=ALU.add), r=[bPS[bank], bytmp], w=[bytmp])
                    A(lambda e, gs=gs: e.activation(out=Zt[:, gs, 0:nch], in_=ytmp[:, :, 0:nch],
                                                    func=AF.Gelu_apprx_tanh), r=[bytmp], w=[bZ])
                for ct in range(4):
                    bank = ct % 2
                    taus = list(range(8)) if not is_s else [4, 5, 6, 7]
                    for ti_, tau in enumerate(taus):
                        for gl in range(8):
                            g = ct * 8 + gl
                            T(lambda e, g=g, gl=gl, tau=tau, ti_=ti_, bank=bank: e.matmul(
                                PS[bank][:, ti_ * 64:ti_ * 64 + nch],
                                lhsT=masters[:, tau, 112 - 16 * gl:240 - 16 * gl], rhs=Zt[:, g, 0:nch],
                                start=(gl == 0), stop=(gl == 7)), r=[b_tab, bZ], w=[bPS[bank]])
                    if not is_s:
                        A(lambda e, ct=ct, bank=bank: e.copy(
                            out=zT[:, ct, 0:n].rearrange("p (c t) -> p t c", t=8),
                            in_=PS[bank][:].rearrange("p (t c) -> p t c", t=8)), r=[bPS[bank]], w=[bzT])
                    else:
                        A(lambda e, ct=ct, bank=bank: e.copy(
                            out=zT[:, ct, 0:n].rearrange("p (b t) -> p t b", t=4),
                            in_=PS[bank][:].rearrange("p (t c) -> p t c", t=8)[:, 0:4, 0:16]),
                          r=[bPS[bank]], w=[bzT])
                for ct in range(4):
                    bank = 2 + (ct % 2)
                    for kt in range(4):
                        T(lambda e, ct=ct, kt=kt, bank=bank: e.matmul(
                            PS[bank][:, 0:n], lhsT=wglu[:, kt, ct * 128:(ct + 1) * 128], rhs=zT[:, kt, 0:n],
                            start=(kt == 0), stop=(kt == 3)), r=[b_wglu, bzT], w=[bPS[bank]])
                    A(lambda e, ct=ct, bank=bank: e.activation(out=sig[:, ct, 0:n], in_=PS[bank][:, 0:n],
                                                               func=AF.Sigmoid), r=[bPS[bank]], w=[bsig])
                V(lambda e: e.tensor_tensor(out=ssmT[:, :, t0:t0 + n], in0=zT[:, :, 0:n], in1=sig[:, :, 0:n],
                                            op=ALU.mult), r=[bzT, bsig], w=[b_ssmT[bi]])
            S.barrier()
        if dbg:
            with ExitStack() as sd:
                dtmp = alloc(sd, "dtmp", [128, 4, NTOK])
                bd = Buf("dtmp", S.GS[2])
                V(lambda e: e.tensor_copy(out=dtmp[:], in_=ssmT[:]), r=b_ssmT, w=[bd])
                S.dma("sp", O["dbg_ssm"][:, :, :], dtmp[:], reads=[bd])
                S.barrier()
        if stage <= 1:
            S.barrier()
            S.run_block()
            nck.__exit__(None, None, None)
            return nc

        with ExitStack() as sbx:
            x = alloc(sbx, "x", [128, NT, D])
            bx = [Buf("x%d" % n, S.GX) for n in range(NT)]
            for n in range(NTP):
                S.dma("sp", x[:, n, :], I["xp"][n * 128:(n + 1) * 128, :], writes=[bx[n]])
            S.dma("sp", x[0:TS, 16, :], I["xs"][:, :], writes=[bx[16]])
            sq = alloc(sbx, "sqB", [128, D])
            ss = alloc(sbx, "ssB", [128, 1])
            rstd = alloc(sbx, "rstdB", [128, 1])
            hb = alloc(sbx, "hbB", [128, D], BF16)
            bscr = Buf("scrB")
            hT1 = alloc(sbx, "hT1", [128, 8, 128], BF16)
            bhT1 = Buf("hT1")
            scrB = (sq, ss, rstd, hb, bscr, 7)

            def resid_add(n, npart, half, bank):
                V(lambda e: e.tensor_tensor(out=x[:npart, n, half * 512:(half + 1) * 512], in0=PS[bank][:npart, :],
                                            in1=x[:npart, n, half * 512:(half + 1) * 512], op=ALU.add),
                  r=[bPS[bank], bx[n]], w=[bx[n]])

            with ExitStack() as s1:
                wq = alloc(s1, "wqkvg", [128, 8, 2048], BF16)
                wout = alloc(s1, "wout", [128, 8, D], BF16)
                b_wq, b_wout = Buf("wq", S.GW[2]), Buf("wout", S.GW[3])
                load_w_bf16(wq, b_wq, I["w_in"], 8, 2048, 512)
                load_w_bf16(wout, b_wout, I["w_out"], 8, D, 0)
                gm2 = alloc(s1, "gm2", [128, 8])
                gn = alloc(s1, "gn", [128, 4])
                rope = alloc(s1, "rope", [128, 3, NT, 64])
                dmp = alloc(s1, "dmp", [128, 512])
                dms = alloc(s1, "dms", [64, 256])
                xi = alloc(s1, "xi", [128, 768])
                zetap = alloc(s1, "zetap", [128, 4])
                zs = alloc(s1, "zs", [64, 64])
                cmask = alloc(s1, "cmask", [128, 16 * 64])
                b_t1 = Buf("tab1")
                S.dma("sp", gm2[:], I["g_mix"].rearrange("(k p) -> p k", p=128), writes=[b_t1])
                S.dma("sp", gn[:], I["ret_gn"].rearrange("(k p) -> p k", p=128), writes=[b_t1])
                for a_ in range(3):
                    S.dma("sp", rope[:, a_, :, :], I["c_rope"][a_], writes=[b_t1])
                S.dma("sp", dmp[:], I["c_dmask_p"][:, :], writes=[b_t1])
                S.dma("sp", dms[:], I["c_dmask_s"][:, :], writes=[b_t1])
                S.dma("sp", xi[:], I["c_xi"][0:1, :].partition_broadcast(128), writes=[b_t1])
                S.dma("sp", zetap[:], I["c_zeta_p"][:, :], writes=[b_t1])
                S.dma("sp", zs[:], I["c_zs"][:, :], writes=[b_t1])
                S.dma("sp", cmask[:], I["c_cmask"][0:1, :].partition_broadcast(128), writes=[b_t1])
                for k in range(4):
                    V(lambda e: e.tensor_scalar(out=wout[:, 4 + k, :], in0=wout[:, 4 + k, :], scalar1=gn[:, k:k + 1],
                                                scalar2=None, op0=ALU.mult), r=[b_wout, b_t1], w=[b_wout])
                t1q = alloc(s1, "t1q", [128, 512])
                t2q = alloc(s1, "t2q", [128, 512])
                t1k = alloc(s1, "t1k", [128, 512])
                t2k = alloc(s1, "t2k", [128, 512])
                qr = alloc(s1, "qr", [128, 512], BF16)
                kr = alloc(s1, "kr", [128, 512], BF16)
                qT = alloc(s1, "qT", [128, 4, 128], BF16)
                qxT = alloc(s1, "qxT", [128, 4, 128], BF16)
                kT = alloc(s1, "kT", [128, 4, 128], BF16)
                vb = alloc(s1, "vb", [128, 512], BF16)
                vz = alloc(s1, "vz", [128, 512], BF16)
                sg_ = alloc(s1, "sgl", [128, 512])
                sT = alloc(s1, "sT", [128, 4, 128], BF16)
                Sst = alloc(s1, "Sst", [128, 4, 128])
                Sbf = alloc(s1, "Sbf", [128, 4, 128], BF16)
                stats = alloc(s1, "stats", [128, 4, 6])
                mv = alloc(s1, "mv", [128, 4, 2])
                rs4 = alloc(s1, "rs4", [128, 4])
                nb4 = alloc(s1, "nb4", [128, 4])
                on = alloc(s1, "on", [128, 512])
                ret = alloc(s1, "ret", [128, 512], BF16)
                retT = alloc(s1, "retT", [128, 4, 128], BF16)
                S0 = [alloc(s1, "S0_%d" % i, [128, 4, 128]) for i in range(2)]
                S0b = [alloc(s1, "S0b_%d" % i, [128, 4, 128], BF16) for i in range(2)]
                qxm = [alloc(s1, "qxm_%d" % i, [128, 4, 64], BF16) for i in range(2)]
                vzb = [alloc(s1, "vzb_%d" % i, [64, 512], BF16) for i in range(2)]
                Sn = [alloc(s1, "Sn_%d" % i, [128, 4, 128]) for i in range(2)]
                bS0 = [Buf("S0_%d" % i, S.GL[i]) for i in range(2)]
                bS0b = [Buf("S0b_%d" % i) for i in range(2)]
                bqxm = [Buf("qxm%d" % i) for i in range(2)]
                bvzb = [Buf("vzb%d" % i) for i in range(2)]
                bSn = [Buf("Sn%d" % i, S.GS[i]) for i in range(2)]
                (b_t1q, b_t2q, b_t1k, b_t2k, b_qr, b_kr, b_qT, b_qxT, b_kT, b_vb, b_vz, b_sg, b_sT, b_Sst, b_Sbf,
                 b_st, b_on, b_ret, b_retT) = [Buf("p1b%d" % i) for i in range(19)]
                b_Sst.grp = S.GS[2]
                V(lambda e: e.memset(Sst[:], 0.0), w=[b_Sst])
                GC_P = [float(g ** 128) for g in GAM]
                GC_S = [float(g ** 4) for g in GAM]

                import os as _os
                _tl = _os.environ.get("K_TILES")
                _tiles = [int(v) for v in _tl.split(",") if int(v) >= 0] if _tl else list(range(NT))
                _step = int(_os.environ.get("K_STEP", "99"))
                for n in _tiles:
                    is_s = (n == 16)
                    npt = TS if is_s else 128
                    tok0 = n * 128
                    pob = [4, 6, 7, 1] if is_s else [4, 4, 4, 4]

                    def po(h):
                        if is_s:
                            return PS[pob[h]][:npt, 0:128]
                        return PS[4][:npt, h * 128:(h + 1) * 128]
                    rmsnorm_hT(x[:npt, n, :], bx[n], npt, gm2[:], hT1, bhT1, scrB, 0, None)
                    for c in range(4):
                        for kt in range(8):
                            T(lambda e: e.matmul(PS[c][:npt, :], lhsT=hT1[:, kt, 0:npt],
                                                 rhs=wq[:, kt, c * 512:(c + 1) * 512], start=(kt == 0), stop=(kt == 7)),
                              r=[bhT1, b_wq], w=[bPS[c]])
                    if _step <= 1:
                        continue
                    for (bank, t1_, t2_, out_, bt1, bt2, bo) in ((0, t1q, t2q, qr, b_t1q, b_t2q, b_qr),
                                                               (1, t1k, t2k, kr, b_t1k, b_t2k, b_kr)):
                        pv4 = PS[bank][:npt, :].rearrange("p (h a j) -> p h a j", h=4, a=2)
                        t1v = t1_[:npt, :].rearrange("p (h a j) -> p h a j", h=4, a=2)
                        t2v = t2_[:npt, :].rearrange("p (h a j) -> p h a j", h=4, a=2)
                        cosb = rope[:npt, 0, n, :].unsqueeze(1).unsqueeze(1).to_broadcast([npt, 4, 2, 64])
                        sinb = rope[:npt, 1, n, :].unsqueeze(1).to_broadcast([npt, 4, 64])
                        nsinb = rope[:npt, 2, n, :].unsqueeze(1).to_broadcast([npt, 4, 64])
                        V(lambda e: e.tensor_tensor(out=t1v, in0=pv4, in1=cosb, op=ALU.mult), r=[bPS[bank], b_t1], w=[bt1])
                        V(lambda e: e.tensor_tensor(out=t2v[:, :, 0, :], in0=pv4[:, :, 1, :], in1=nsinb, op=ALU.mult),
                          r=[bPS[bank], b_t1], w=[bt2])
                        V(lambda e: e.tensor_tensor(out=t2v[:, :, 1, :], in0=pv4[:, :, 0, :], in1=sinb, op=ALU.mult),
                          r=[bPS[bank], b_t1], w=[bt2])
                        V(lambda e: e.tensor_tensor(out=out_[:npt, :], in0=t1_[:npt, :], in1=t2_[:npt, :], op=ALU.add),
                           r=[bt1, bt2], w=[bo])
                    if _step <= 2:
                        continue
                    A(lambda e: e.copy(out=vb[:npt, :], in_=PS[2][:npt, :]), r=[bPS[2]], w=[b_vb])
                    if not is_s:
                        V(lambda e: e.tensor_tensor(
                            out=vz[:, :].rearrange("p (h e) -> p h e", h=4),
                            in0=PS[2][:, :].rearrange("p (h e) -> p h e", h=4),
                            in1=zetap[:, :].unsqueeze(2).to_broadcast([128, 4, 128]), op=ALU.mult),
                          r=[bPS[2], b_t1], w=[b_vz])
                    A(lambda e: e.activation(out=sg_[:npt, :], in_=PS[3][:npt, :], func=AF.Silu), r=[bPS[3]], w=[b_sg])
                    pv4b = ps_bf(4)
                    pv5b = ps_bf(5)
                    for h in range(4):
                        T(lambda e: e.transpose(out=pv4b[:, h * 128:h * 128 + npt], in_=qr[:npt, h * 128:(h + 1) * 128],
                                                identity=identb[:npt, :npt]), r=[b_qr, b_const], w=[bPS[4]])
                    for h in range(4):
                        T(lambda e: e.transpose(out=pv5b[:, h * 128:h * 128 + npt], in_=kr[:npt, h * 128:(h + 1) * 128],
                                                identity=identb[:npt, :npt]), r=[b_kr, b_const], w=[bPS[5]])
                    q4 = pv4b[:, 0:512].rearrange("p (h t) -> p h t", h=4)[:, :, 0:npt]
                    k4 = pv5b[:, 0:512].rearrange("p (h t) -> p h t", h=4)[:, :, 0:npt]
                    A(lambda e: e.copy(out=qT[:, :, 0:npt], in_=q4), r=[bPS[4]], w=[b_qT])
                    xiv = (xi[:, 0:512].rearrange("p (h t) -> p h t", h=4) if not is_s
                           else xi[:, 512:768].rearrange("p (h t) -> p h t", h=4))
                    V(lambda e: e.tensor_tensor(out=qxT[:, :, 0:npt], in0=q4, in1=xiv, op=ALU.mult),
                      r=[bPS[4], b_t1], w=[b_qxT])
                    A(lambda e: e.copy(out=kT[:, :, 0:npt], in_=k4), r=[bPS[5]], w=[b_kT])
                    if _step <= 3:
                        continue
                    for h in range(4):
                        T(lambda e: e.matmul(PS[6][:npt, h * 128:h * 128 + npt], lhsT=kT[:, h, 0:npt], rhs=qT[:, h, 0:npt],
                                             start=True, stop=True), r=[b_kT, b_qT], w=[bPS[6]])
                    dmv = (dmp[:, :].rearrange("p (h t) -> p h t", h=4) if not is_s
                           else dms[:, :].rearrange("p (h t) -> p h t", h=4))
                    V(lambda e: e.tensor_tensor(out=sT[:npt, :, 0:npt],
                                                in0=PS[6][:npt, :].rearrange("p (h t) -> p h t", h=4)[:, :, 0:npt],
                                                in1=dmv, op=ALU.mult), r=[bPS[6], b_t1], w=[b_sT])
                    if _step <= 4:
                        continue
                    for h in range(4):
                        only = (n == 0)
                        T(lambda e: e.matmul(po(h), lhsT=sT[:npt, h, 0:npt],
                                             rhs=vb[:npt, h * 128:(h + 1) * 128], start=True, stop=only),
                          r=[b_sT, b_vb], w=[bPS[pob[h]]])
                        if (not is_s) and n > 0:
                            T(lambda e: e.matmul(po(h), lhsT=qxT[:, h, 0:npt],
                                                 rhs=Sbf[:, h, :], start=False, stop=True),
                              r=[b_qxT, b_Sbf], w=[bPS[4]])
                    if not is_s:
                        for h in range(4):
                            T(lambda e: e.matmul(PS[5][:, h * 128:(h + 1) * 128], lhsT=kr[:, h * 128:(h + 1) * 128],
                                                 rhs=vz[:, h * 128:(h + 1) * 128], start=True, stop=True),
                              r=[b_kr, b_vz], w=[bPS[5]])
                        for h in range(4):
                            V(lambda e: e.scalar_tensor_tensor(out=Sst[:, h, :], in0=Sst[:, h, :], scalar=GC_P[h],
                                                               op0=ALU.mult, in1=PS[5][:, h * 128:(h + 1) * 128],
                                                               op1=ALU.add), r=[b_Sst, bPS[5]], w=[b_Sst])
                        A(lambda e: e.copy(out=Sbf[:], in_=Sst[:]), r=[b_Sst], w=[b_Sbf])
                        if n == NTP - 1:
                            S.dma("sp", O["o_ret_p"].rearrange("h d e -> d h e"), Sst[:], reads=[b_Sst])
                    else:
                        for b in range(16):
                            sl = b % 2
                            S.dma("sp", S0[sl][:], I["sret"][b].rearrange("h d e -> d h e"), writes=[bS0[sl]])
                            A(lambda e: e.copy(out=S0b[sl][:], in_=S0[sl][:]), r=[bS0[sl]], w=[bS0b[sl]])
                            V(lambda e: e.tensor_tensor(
                                out=qxm[sl][:], in0=qxT[:, :, 0:64],
                                in1=cmask[:, b * 64:(b + 1) * 64].unsqueeze(1).to_broadcast([128, 4, 64]), op=ALU.mult),
                              r=[b_qxT, b_t1], w=[bqxm[sl]])
                            for h in range(4):
                                T(lambda e: e.matmul(po(h), lhsT=qxm[sl][:, h, :],
                                                     rhs=S0b[sl][:, h, :], start=False, stop=(b == 15)),
                                  r=[bqxm[sl], bS0b[sl]], w=[bPS[pob[h]]])
                            V(lambda e: e.tensor_tensor(
                                out=vzb[sl][:, :].rearrange("p (h e) -> p h e", h=4),
                                in0=PS[2][:64, :].rearrange("p (h e) -> p h e", h=4),
                                in1=zs[:, b * 4:(b + 1) * 4].unsqueeze(2).to_broadcast([64, 4, 128]), op=ALU.mult),
                              r=[bPS[2], b_t1], w=[bvzb[sl]])
                            kvb = 5 if sl == 0 else 0
                            for h in range(4):
                                T(lambda e: e.matmul(PS[kvb][:, h * 128:(h + 1) * 128], lhsT=kr[:64, h * 128:(h + 1) * 128],
                                                     rhs=vzb[sl][:, h * 128:(h + 1) * 128], start=True, stop=True),
                                  r=[b_kr, bvzb[sl]], w=[bPS[kvb]])
                            for h in range(4):
                                V(lambda e: e.scalar_tensor_tensor(out=Sn[sl][:, h, :], in0=S0[sl][:, h, :], scalar=GC_S[h],
                                                                   op0=ALU.mult, in1=PS[kvb][:, h * 128:(h + 1) * 128],
                                                                   op1=ALU.add), r=[bS0[sl], bPS[kvb]], w=[bSn[sl]])
                            S.dma("sp", O["o_ret_s"][b].rearrange("h d e -> d h e"), Sn[sl][:], reads=[bSn[sl]])
                    if _step <= 5:
                        continue
                    for h in range(4):
                        V(lambda e: e.bn_stats(out=stats[:npt, h, :], in_=po(h)),
                          r=[bPS[pob[h]]], w=[b_st])
                    for h in range(4):
                        V(lambda e: e.bn_aggr(out=mv[:npt, h, :], in_=stats[:npt, h, :]), r=[b_st], w=[b_st])
                    A(lambda e: e.activation(out=rs4[:npt, :], in_=mv[:npt, :, 1], func=AF.Sqrt, scale=1.0,
                                             bias=epsc[:npt, :]), r=[b_st, b_const], w=[b_st])
                    V(lambda e: e.reciprocal(out=rs4[:npt, :], in_=rs4[:npt, :]), r=[b_st], w=[b_st])
                    V(lambda e: e.scalar_tensor_tensor(out=nb4[:npt, :], in0=mv[:npt, :, 0], scalar=-1.0, op0=ALU.mult,
                                                       in1=rs4[:npt, :], op1=ALU.mult), r=[b_st], w=[b_st])
                    for h in range(4):
                        A(lambda e: e.activation(out=on[:npt, h * 128:(h + 1) * 128], in_=po(h),
                                                 func=AF.Identity, scale=rs4[:npt, h:h + 1], bias=nb4[:npt, h:h + 1]),
                          r=[bPS[pob[h]], b_st], w=[b_on])
                    V(lambda e: e.tensor_tensor(out=ret[:npt, :], in0=on[:npt, :], in1=sg_[:npt, :], op=ALU.mult),
                       r=[b_on, b_sg], w=[b_ret])
                    if _step <= 6:
                        continue
                    pv6b = ps_bf(6)
                    for h in range(4):
                        T(lambda e: e.transpose(out=pv6b[:, h * 128:h * 128 + npt], in_=ret[:npt, h * 128:(h + 1) * 128],
                                                identity=identb[:npt, :npt]), r=[b_ret, b_const], w=[bPS[6]])
                    A(lambda e: e.copy(out=retT[:, :, 0:npt],
                                       in_=pv6b[:, 0:512].rearrange("p (h t) -> p h t", h=4)[:, :, 0:npt]),
                      r=[bPS[6]], w=[b_retT])
                    if _step <= 7:
                        continue
                    bi_ = min(n // 4, 4)
                    for half in range(2):
                        bank = 2 + half
                        for kt in range(8):
                            lh = ssmT[:, kt, tok0:tok0 + npt] if kt < 4 else retT[:, kt - 4, 0:npt]
                            T(lambda e: e.matmul(PS[bank][:npt, :], lhsT=lh, rhs=wout[:, kt, half * 512:(half + 1) * 512],
                                                 start=(kt == 0), stop=(kt == 7)),
                              r=[b_ssmT[bi_], b_retT, b_wout], w=[bPS[bank]])
                        resid_add(n, npt, half, bank)
                S.barrier()
            if dbg:
                for n in range(NT):
                    S.dma("sp", O["dbg_x"][:, n, :], x[:, n, :], reads=[bx[n]])
            if stage <= 2:
                S.barrier()
                S.run_block()
                nck.__exit__(None, None, None)
                return nc

            with ExitStack() as s2:
                gx = alloc(s2, "gx", [128, 8])
                gmem = alloc(s2, "gmem", [128, 8])
                ones = alloc(s2, "ones", [128, 128], BF16)
                b_t2 = Buf("tab2")
                S.dma("sp", gx[:], I["g_xattn"].rearrange("(k p) -> p k", p=128), writes=[b_t2])
                S.dma("sp", gmem[:], I["g_mem"].rearrange("(k p) -> p k", p=128), writes=[b_t2])
                V(lambda e: e.memset(ones[:], 1.0), w=[b_t2])
                KT = alloc(s2, "KT", [128, 8, MEM], BF16)
                Vm = alloc(s2, "Vm", [128, 2, D], BF16)
                b_KT, b_Vm = Buf("KT"), Buf("Vm")
                wmq = alloc(s2, "wmq", [128, 8, D], BF16)
                b_wmq, b_wmo = Buf("wmq", S.GW[2]), Buf("wmo", S.GW[3])
                with ExitStack() as s2a:
                    wmk = alloc(s2a, "wmk", [128, 8, D], BF16)
                    wmv = alloc(s2a, "wmv", [128, 8, D], BF16)
                    b_wmk, b_wmv = Buf("wmk", S.GW[0]), Buf("wmv", S.GW[1])
                    load_w_bf16(wmk, b_wmk, I["w_mk"], 8, D, 0)
                    load_w_bf16(wmv, b_wmv, I["w_mv"], 8, D, 0)
                    load_w_bf16(wmq, b_wmq, I["w_mq"], 8, D, 0)
                    mx = [alloc(s2a, "mx%d" % i, [128, D]) for i in range(2)]
                    bmx = [Buf("mx%d" % i, S.GL[i]) for i in range(2)]
                    mhT = alloc(s2a, "mhT", [128, 8, MEM], BF16)
                    b_mhT = Buf("mhT")
                    mo = [alloc(s2a, "mo%d" % i, [128, D]) for i in range(2)]
                    bmo = [Buf("mo%d" % i, S.GS[i]) for i in range(2)]
                    _k2a = int(_os.environ.get("K2A", "9"))
                    for mt in range(2):
                        S.dma("sp", mx[mt][:], I["memp"][mt * 128:(mt + 1) * 128, :], writes=[bmx[mt]])
                        if _k2a >= 1:
                            if _os.environ.get("K_MXX") == "1":
                                rmsnorm_hT(x[:, mt, :], bx[mt], 128, gmem[:], mhT, b_mhT, scrB, mt * 128,
                                           int(_os.environ.get("K_RMS", "99")))
                            else:
                                rmsnorm_hT(mx[mt][:, :], bmx[mt], 128, gmem[:], mhT, b_mhT, scrB, mt * 128,
                                           int(_os.environ.get("K_RMS", "99")))
                    oi = 0
                    for (wm, bwm, oname, isv) in ((wmk, b_wmk, "o_mk", False), (wmv, b_wmv, "o_mv", True)) if _k2a >= 2 else ():
                        for mt in range(2):
                            sl = oi % 2
                            oi += 1
                            for half in range(2):
                                bank = half
                                for kt in range(8):
                                    T(lambda e: e.matmul(PS[bank][:, :], lhsT=mhT[:, kt, mt * 128:(mt + 1) * 128],
                                                         rhs=wm[:, kt, half * 512:(half + 1) * 512], start=(kt == 0),
                                                         stop=(kt == 7)), r=[b_mhT, bwm], w=[bPS[bank]])
                                A(lambda e: e.copy(out=mo[sl][:, half * 512:(half + 1) * 512], in_=PS[bank][:, :]),
                                  r=[bPS[bank]], w=[bmo[sl]])
                                if isv:
                                    V(lambda e: e.tensor_copy(out=Vm[:, mt, half * 512:(half + 1) * 512], in_=PS[bank][:, :]),
                                      r=[bPS[bank]], w=[b_Vm])
                            S.dma("sp", O[oname][mt * 128:(mt + 1) * 128, :], mo[sl][:], reads=[bmo[sl]])
                    for j in range(8 if _k2a >= 3 else 0):
                        bank = 2 + (j % 2)
                        for kt in range(8):
                            T(lambda e: e.matmul(PS[bank][:, 0:MEM], lhsT=wmk[:, kt, j * 128:(j + 1) * 128],
                                                 rhs=mhT[:, kt, :], start=(kt == 0), stop=(kt == 7)),
                              r=[b_mhT, b_wmk], w=[bPS[bank]])
                        A(lambda e: e.copy(out=KT[:, j, :], in_=PS[bank][:, 0:MEM]), r=[bPS[bank]], w=[b_KT])
                    S.barrier()
                wmo = alloc(s2, "wmo", [128, 8, D], BF16)
                load_w_bf16(wmo, b_wmo, I["w_mo"], 8, D, 0)
                hT4 = alloc(s2, "hT4", [128, 8, 512], BF16)
                qm4 = alloc(s2, "qm4", [128, 8, 512], BF16)
                oT4 = alloc(s2, "oT4", [128, 8, 512], BF16)
                eT4 = [alloc(s2, "eT4_%d" % i, [128, 2, 512], BF16) for i in range(2)]
                rdn4 = [alloc(s2, "rdn4_%d" % i, [128, 512]) for i in range(2)]
                b_hT4, b_qm4, b_oT4 = Buf("hT4"), Buf("qm4"), Buf("oT4")
                b_eT4 = [Buf("eT4_%d" % i) for i in range(2)]
                b_rdn4 = [Buf("rdn4_%d" % i) for i in range(2)]
                Kb = [alloc(s2, "Kb%d" % i, [128, 2, D]) for i in range(2)]
                bKb = [Buf("Kb%d" % i, S.GL[i]) for i in range(2)]
                KbT = [alloc(s2, "KbT%d" % i, [128, 8, MEM], BF16) for i in range(2)]
                bKbT = [Buf("KbT%d" % i) for i in range(2)]
                Vb = [alloc(s2, "Vb%d" % i, [128, 2, D], BF16) for i in range(2)]
                bVb = [Buf("Vb%d" % i, S.GW[i]) for i in range(2)]
                eTs = alloc(s2, "eTs", [128, 2, 4, 64], BF16)
                b_eTs = Buf("eTs")
                qrot = [0]

                def q_proj(nc_):
                    for j in range(8):
                        bank = 5 + (qrot[0] % 3)
                        qrot[0] += 1
                        for kt in range(8):
                            T(lambda e: e.matmul(PS[bank][:, 0:nc_], lhsT=wmq[:, kt, j * 128:(j + 1) * 128],
                                                 rhs=hT4[:, kt, 0:nc_], start=(kt == 0), stop=(kt == 7)),
                              r=[b_wmq, b_hT4], w=[bPS[bank]])
                        A(lambda e: e.activation(out=qm4[:, j, 0:nc_], in_=PS[bank][:, 0:nc_], func=AF.Copy,
                                                 scale=1.0 / 16.0), r=[bPS[bank]], w=[b_qm4])

                def w_mo_resid(n, npt, c0):
                    for half in range(2):
                        bank = 5 + (qrot[0] % 3)
                        qrot[0] += 1
                        for j in range(8):
                            T(lambda e: e.matmul(PS[bank][:npt, :], lhsT=oT4[:, j, c0:c0 + npt],
                                                 rhs=wmo[:, j, half * 512:(half + 1) * 512], start=(j == 0), stop=(j == 7)),
                              r=[b_oT4, b_wmo], w=[bPS[bank]])
                        resid_add(n, npt, half, bank)

                for bi in range(4):
                    for ti in range(4):
                        n = bi * 4 + ti
                        rmsnorm_hT(x[:, n, :], bx[n], 128, gx[:], hT4, b_hT4, scrB, ti * 128, None, ln=True)
                    q_proj(512)
                    for h in range(4):
                        par = h % 2
                        for mt in range(2):
                            bank = mt
                            for dt_ in range(2):
                                T(lambda e: e.matmul(PS[bank][:, :], lhsT=KT[:, h * 2 + dt_, mt * 128:(mt + 1) * 128],
                                                     rhs=qm4[:, h * 2 + dt_, :], start=(dt_ == 0), stop=(dt_ == 1)),
                                  r=[b_KT, b_qm4], w=[bPS[bank]])
                            A(lambda e: e.activation(out=eT4[par][:, mt, :], in_=PS[bank][:, :], func=AF.Exp),
                              r=[bPS[bank]], w=[b_eT4[par]])
                        for mt in range(2):
                            T(lambda e: e.matmul(PS[2][:, :], lhsT=ones[:, :], rhs=eT4[par][:, mt, :], start=(mt == 0),
                                                 stop=(mt == 1)), r=[b_t2, b_eT4[par]], w=[bPS[2]])
                        A(lambda e: e.activation(out=rdn4[par][:, :], in_=PS[2][:, :], func=AF.Ln), r=[bPS[2]], w=[b_rdn4[par]])
                        A(lambda e: e.activation(out=rdn4[par][:, :], in_=rdn4[par][:, :], func=AF.Exp, scale=-1.0),
                          r=[b_rdn4[par]], w=[b_rdn4[par]])
                        for dt_ in range(2):
                            bank = 3 + dt_
                            j = h * 2 + dt_
                            for mt in range(2):
                                T(lambda e: e.matmul(PS[bank][:, :], lhsT=Vm[:, mt, j * 128:(j + 1) * 128],
                                                     rhs=eT4[par][:, mt, :], start=(mt == 0), stop=(mt == 1)),
                                  r=[b_Vm, b_eT4[par]], w=[bPS[bank]])
                            V(lambda e: e.tensor_tensor(out=oT4[:, j, :], in0=PS[bank][:, :], in1=rdn4[par][:, :], op=ALU.mult),
                              r=[bPS[bank], b_rdn4[par]], w=[b_oT4])
                    for ti in range(4):
                        w_mo_resid(bi * 4 + ti, 128, ti * 128)
                n = 16
                rmsnorm_hT(x[:TS, n, :], bx[n], TS, gx[:], hT4, b_hT4, scrB, 0, None, ln=True)
                q_proj(TS)
                rden_s = rdn4[0][:, 0:256].rearrange("p (h t) -> p h t", h=4)
                for b in range(16):
                    sl = b % 2
                    S.dma("sp", Kb[sl][:], I["ck"][b].rearrange("(mt p) d -> p mt d", p=128), writes=[bKb[sl]])
                    for q4 in range(4):
                        bank = 2 + (q4 % 2)
                        for i4 in range(4):
                            idx = q4 * 4 + i4
                            j, mt = idx // 2, idx % 2
                            T(lambda e: e.transpose(out=PS[bank][:, i4 * 128:(i4 + 1) * 128],
                                                    in_=Kb[sl][:, mt, j * 128:(j + 1) * 128], identity=identf[:]),
                              r=[bKb[sl], b_const], w=[bPS[bank]])
                        A(lambda e: e.copy(
                            out=KbT[sl][:, 2 * q4:2 * q4 + 2, :].rearrange("p j (m t) -> p j m t", m=2),
                            in_=PS[bank][:, :].rearrange("p (j m t) -> p j m t", j=2, m=2)),
                          r=[bPS[bank]], w=[bKbT[sl]])
                    for h in range(4):
                        for mt in range(2):
                            c0 = mt * 256 + h * 64 + 4 * b
                            for dt_ in range(2):
                                T(lambda e: e.matmul(PS[4][:, c0:c0 + 4],
                                                     lhsT=KbT[sl][:, h * 2 + dt_, mt * 128:(mt + 1) * 128],
                                                     rhs=qm4[:, h * 2 + dt_, 4 * b:4 * b + 4], start=(dt_ == 0),
                                                     stop=(dt_ == 1)), r=[bKbT[sl], b_qm4], w=[bPS[4]])
                A(lambda e: e.activation(out=eTs[:].rearrange("p m h t -> p (m h t)"), in_=PS[4][:, :], func=AF.Exp),
                  r=[bPS[4]], w=[b_eTs])
                for h in range(4):
                    for mt in range(2):
                        T(lambda e: e.matmul(PS[0][:, h * 64:(h + 1) * 64], lhsT=ones[:, :], rhs=eTs[:, mt, h, :],
                                             start=(mt == 0), stop=(mt == 1)), r=[b_t2, b_eTs], w=[bPS[0]])
                V(lambda e: e.reciprocal(out=rden_s, in_=PS[0][:, 0:256].rearrange("p (h t) -> p h t", h=4)),
                  r=[bPS[0]], w=[b_rdn4[0]])
                for b in range(16):
                    sl = b % 2
                    for mt in range(2):
                        S.dma("pool", Vb[sl][:, mt, :], I["cv"][b, mt * 128:(mt + 1) * 128, :], writes=[bVb[sl]])
                    for j in range(8):
                        h = j // 2
                        for mt in range(2):
                            T(lambda e: e.matmul(PS[1][:, j * 64 + 4 * b:j * 64 + 4 * b + 4],
                                                 lhsT=Vb[sl][:, mt, j * 128:(j + 1) * 128],
                                                 rhs=eTs[:, mt, h, 4 * b:4 * b + 4], start=(mt == 0), stop=(mt == 1)),
                              r=[bVb[sl], b_eTs], w=[bPS[1]])
                V(lambda e: e.tensor_tensor(
                    out=oT4[:, :, 0:64].rearrange("p (h a) t -> p h a t", a=2),
                    in0=PS[1][:, :].rearrange("p (h a t) -> p h a t", h=4, a=2),
                    in1=rden_s.unsqueeze(2).to_broadcast([128, 4, 2, 64]), op=ALU.mult),
                  r=[bPS[1], b_rdn4[0]], w=[b_oT4])
                w_mo_resid(16, TS, 0)
                S.barrier()
            if stage <= 3:
                if dbg:
                    for n in range(NT):
                        S.dma("sp", O["dbg_x"][:, n, :], x[:, n, :], reads=[bx[n]])
                S.barrier()
                S.run_block()
                nck.__exit__(None, None, None)
                return nc

            with ExitStack() as s3:
                gml = alloc(s3, "gml", [128, 8])
                b_t3 = Buf("tab3")
                S.dma("sp", gml[:], I["g_mlp"].rearrange("(k p) -> p k", p=128), writes=[b_t3])
                hTa = alloc(s3, "hTa", [128, 8, NTOK], BF16)
                b_hTa = [Buf("hTa%d" % n) for n in range(NT)]
                wup = [alloc(s3, "wup%d" % i, [128, 8, 512], BF16) for i in range(2)]
                wdn = [alloc(s3, "wdn%d" % i, [128, 4, D], BF16) for i in range(2)]
                bwup = [Buf("wup%d" % i, S.GW[i]) for i in range(2)]
                bwdn = [Buf("wdn%d" % i, S.GW[2 + i]) for i in range(2)]
                rl = [alloc(s3, "rl%d" % i, [128, 512]) for i in range(2)]
                brl = [Buf("rl%d" % i) for i in range(2)]
                aT = [alloc(s3, "aT%d" % i, [128, 4, 512], BF16) for i in range(2)]
                baT = [Buf("aT%d" % i) for i in range(2)]

                def load_fc(fc):
                    sl = fc % 2
                    for kt in range(8):
                        S.dma("pool", wup[sl][:, kt, :], I["w_up"][kt * 128:(kt + 1) * 128, fc * 512:(fc + 1) * 512],
                              writes=[bwup[sl]])
                    for ft in range(4):
                        S.dma("pool", wdn[sl][:, ft, :], I["w_down"][fc * 512 + ft * 128:fc * 512 + (ft + 1) * 128, :],
                              writes=[bwdn[sl]])
                load_fc(0)
                for n in range(NT):
                    npt = TS if n == 16 else 128
                    rmsnorm_hT(x[:npt, n, :], bx[n], npt, gml[:], hTa, b_hTa[n], scrB, n * 128, None)
                blocks3 = [(i * 512, 512) for i in range(4)] + [(SEQ, TS)]
                ai = 0
                ri = 0
                di = 0
                for fc in range(8):
                    sl = fc % 2
                    if fc + 1 < 8:
                        load_fc(fc + 1)
                    for (t0, nn) in blocks3:
                        tiles = list(range(t0 // 128, t0 // 128 + (nn + 127) // 128))
                        asl = ai % 2
                        ai += 1
                        for ft in range(4):
                            bank = ft
                            for kt in range(8):
                                T(lambda e: e.matmul(PS[bank][:, 0:nn], lhsT=wup[sl][:, kt, ft * 128:(ft + 1) * 128],
                                                     rhs=hTa[:, kt, t0:t0 + nn], start=(kt == 0), stop=(kt == 7)),
                                  r=[bwup[sl]] + [b_hTa[t] for t in tiles], w=[bPS[bank]])
                            rsl = ri % 2
                            ri += 1
                            A(lambda e: e.activation(out=rl[rsl][:, 0:nn], in_=PS[bank][:, 0:nn], func=AF.Relu),
                              r=[bPS[bank]], w=[brl[rsl]])
                            V(lambda e: e.tensor_tensor(out=aT[asl][:, ft, 0:nn], in0=rl[rsl][:, 0:nn], in1=rl[rsl][:, 0:nn],
                                                        op=ALU.mult), r=[brl[rsl]], w=[baT[asl]])
                        for ti, tl in enumerate(tiles):
                            npt = TS if tl == 16 else 128
                            for half in range(2):
                                bank = 4 + (di % 4)
                                di += 1
                                for ft in range(4):
                                    T(lambda e: e.matmul(PS[bank][:npt, :], lhsT=aT[asl][:, ft, ti * 128:ti * 128 + npt],
                                                         rhs=wdn[sl][:, ft, half * 512:(half + 1) * 512], start=(ft == 0),
                                                         stop=(ft == 3)), r=[baT[asl], bwdn[sl]], w=[bPS[bank]])
                                resid_add(tl, npt, half, bank)
                S.barrier()
            if dbg:
                for n in range(NT):
                    S.dma("sp", O["dbg_x"][:, n, :], x[:, n, :], reads=[bx[n]])
            with ExitStack() as s4:
                gf = alloc(s4, "gf", [128, D])
                b_gf = Buf("gf")
                S.dma("sp", gf[:], I["g_final"].rearrange("(o d) -> o d", o=1).partition_broadcast(128), writes=[b_gf])
                yst = [alloc(s4, "yst%d" % i, [128, D]) for i in range(3)]
                byst = [Buf("yst%d" % i, S.GS[i]) for i in range(3)]
                for n in range(NT):
                    npt = TS if n == 16 else 128
                    sl = n % 3
                    A(lambda e: e.activation(out=sq[:npt, :], in_=x[:npt, n, :], func=AF.Square, accum_out=ss[:npt, :]),
                      r=[bx[n]], w=[bscr])
                    A(lambda e: e.activation(out=rstd[:npt, :], in_=ss[:npt, :], func=AF.Sqrt, scale=1.0 / D,
                                             bias=epsc[:npt, :]), r=[bscr, b_const], w=[bscr])
                    V(lambda e: e.reciprocal(out=rstd[:npt, :], in_=rstd[:npt, :]), r=[bscr], w=[bscr])
                    V(lambda e: e.scalar_tensor_tensor(out=yst[sl][:npt, :], in0=x[:npt, n, :], scalar=rstd[:npt, :],
                                                       op0=ALU.mult, in1=gf[:npt, :], op1=ALU.mult),
                      r=[bx[n], bscr, b_gf], w=[byst[sl]])
                    if n < 16:
                        S.dma("sp", O["yp"][n * 128:(n + 1) * 128, :], yst[sl][:, :], reads=[byst[sl]])
                    else:
                        S.dma("sp", O["ys"][:, :], yst[sl][:TS, :], reads=[byst[sl]])
                S.barrier()
            S.barrier()
            S.run_block()
            nck.__exit__(None, None, None)
    return nc


_NC = None


def kernel(**inputs):
    global _NC
    if _NC is None:
        _NC = build()
    maps = _in_maps(inputs)
    res = run_bass_kernel_spmd(_NC, maps, core_ids=list(range(8)))
    R = res.results
    f = np.float32

    def cat(name, shape=None):
        return np.stack([np.asarray(R[c][name], f) for c in range(8)])
    y_prompt = cat("yp")
    y_sample = cat("ys").reshape(128, 4, D)
    s5r_p = cat("o_s5r_p")[None]
    s5i_p = cat("o_s5i_p")[None]
    ret_p = cat("o_ret_p")[None]
    mk_p = cat("o_mk").reshape(8, MEM, 4, 256)[None]
    mv_p = cat("o_mv").reshape(8, MEM, 4, 256)[None]
    s5r_s = cat("o_s5r_s").reshape(128, G, 64)[None]
    s5i_s = cat("o_s5i_s").reshape(128, G, 64)[None]
    ret_s = cat("o_ret_s").reshape(128, 4, 128, 128)[None]
    return (y_prompt, y_sample, s5r_p, s5i_p, ret_p, mk_p, mv_p, s5r_s, s5i_s, ret_s)


def _in_maps(inputs):
    cst = _consts()
    f = np.float32
    maps = []
    w = {}
    for k in W_NAMES:
        a = np.asarray(inputs[k], f)
        if k != "g_final":
            a = a[0]
        w[k] = np.ascontiguousarray(a.reshape(W_SHAPES[k]))
    for c in range(8):
        m = dict(w)
        m.update(cst)
        b0 = 16 * c
        m["xp"] = np.ascontiguousarray(np.asarray(inputs["x_prompt"], f)[c])
        m["xs"] = np.ascontiguousarray(np.asarray(inputs["x_sample"], f)[b0:b0 + 16].reshape(TS, D))
        m["memp"] = np.ascontiguousarray(np.asarray(inputs["mem_prompt"], f)[c])
        m["s5r"] = np.ascontiguousarray(np.asarray(inputs["state_s5_re"], f)[0, b0:b0 + 16].reshape(512, 64))
        m["s5i"] = np.ascontiguousarray(np.asarray(inputs["state_s5_im"], f)[0, b0:b0 + 16].reshape(512, 64))
        m["sret"] = np.ascontiguousarray(np.asarray(inputs["state_ret"], f)[0, b0:b0 + 16])
        m["ck"] = np.ascontiguousarray(np.asarray(inputs["cache_mem_k"], f)[0, b0:b0 + 16].reshape(16, MEM, D))
        m["cv"] = np.ascontiguousarray(np.asarray(inputs["cache_mem_v"], f)[0, b0:b0 + 16].reshape(16, MEM, D))
        maps.append(m)
    return maps
```

```python
import numpy as np
import concourse.bass as bass
import concourse.mybir as mybir
from concourse.bass_utils import run_bass_kernel_spmd
from contextlib import ExitStack

F32 = mybir.dt.float32
BF16 = mybir.dt.bfloat16
AF = mybir.ActivationFunctionType
ALU = mybir.AluOpType

D = 1024
SEQ = 2048
NTP = 16
TS = 64
NT = 17
NTOK = SEQ + TS
G = 32
DFF = 4096
MEM = 256
EPS = 1e-6
PAST = 16384.0
MAGIC = 12582912.0
TWO_PI = float(2.0 * np.pi)
ML = [7, 6, 5, 4, 3, 2, 1, 0, 1, 2, 3, 4, 5, 6, 7, 8, -4, 0.5]
K1 = len(ML)
I_A1, I_A8, I_A4, I_AM4, I_HALF = 8, 15, 3, 16, 17
GAM = [1.0 - 2.0 ** (-5.0 - h) for h in range(4)]


class Grp:
    __slots__ = ("sem", "cnt", "sealed")


class Buf:
    __slots__ = ("w", "r", "name", "grp", "ps")

    def __init__(self, name="", grp=None, ps=False):
        self.w = None
        self.r = []
        self.name = name
        self.grp = grp
        self.ps = ps


class _Rec:
    def __init__(self):
        self.call = None

    def __getattr__(self, name):
        def f(*a, **kw):
            self.call = (name, a, kw)
            return self
        return f


class Sched:
    ENG = ("pe", "dve", "act", "pool", "sp")

    def __init__(self, nc, stack, self_sync=("dve", "act", "pool")):
        self.nc = nc
        self.stack = stack
        self.prog = {k: [] for k in self.ENG}
        self.cnt = {k: 0 for k in self.ENG}
        self.waited = {k: {} for k in self.ENG}
        self.sem = {}
        self.nsem = 0
        for k in ("pe", "dve", "act", "pool"):
            self.sem[k] = self.new_sem("c_" + k)
        self.self_sync = set(self_sync)
        self.groups = []
        self.GC = self.group("gc")
        self.GP = self.group("gp")
        self.GW = [self.group("gw%d" % i) for i in range(4)]
        self.GX = self.group("gx")
        self.GL = [self.group("gl%d" % i) for i in range(2)]
        self.GS = [self.group("gs%d" % i) for i in range(3)]

    def group(self, name):
        g = Grp()
        g.sem = self.new_sem(name)
        g.cnt = 0
        g.sealed = False
        self.groups.append(g)
        return g

    def new_sem(self, name):
        self.nsem += 1
        assert self.nsem < 98, "too many semaphores"
        return self.stack.enter_context(self.nc.semaphore(name + "_%d" % self.nsem))

    def _waits(self, eng, deps):
        w = self.waited[eng]
        need = {}
        dd = []
        for d in deps:
            if isinstance(d, Grp):
                d.sealed = True
                dd.append((d.sem, d.cnt))
            else:
                dd.append(d)
        deps = dd
        for (s, v) in deps:
            if eng in self.sem and s is self.sem[eng] and eng not in self.self_sync:
                continue
            k = id(s)
            if w.get(k, 0) >= v:
                continue
            if k not in need or need[k][1] < v:
                need[k] = (s, v)
        for k, (s, v) in need.items():
            w[k] = v
            self.prog[eng].append(lambda e, s=s, v=v: e.wait_ge(s, v))

    def op(self, eng, fn, reads=(), writes=()):
        deps = []
        for b in reads:
            if b.w is not None:
                deps.append(b.w)
            if b.ps:
                mys = self.sem[eng]
                deps.extend(d for d in b.r if not (isinstance(d, tuple) and d[0] is mys))
        for b in writes:
            if b.w is not None:
                deps.append(b.w)
            deps.extend(b.r)
        self._waits(eng, deps)
        self.cnt[eng] += 1
        c = self.cnt[eng]
        s = self.sem[eng]
        rec = _Rec()
        fn(rec)
        name, a, kw = rec.call
        self.prog[eng].append(lambda e, name=name, a=a, kw=kw, s=s: getattr(e, name)(*a, **kw).then_inc(s, 1))
        for b in reads:
            b.r.append((s, c))
        for b in writes:
            b.w = (s, c)
            b.r = []

    def dma(self, q, out, in_, reads=(), writes=(), **kw):
        tb = writes[0] if writes else reads[0]
        g = tb.grp
        if g is None:
            g = self.GP if q == "pool" else (self.GC if writes else self.GS[0])
        deps = []
        for b in reads:
            if b.w is not None:
                deps.append(b.w)
        for b in writes:
            if b.w is not None and b.w is not g:
                deps.append(b.w)
            deps.extend(b.r)
        self._waits(q, deps)
        if g.sealed and g.cnt > 0:
            self._waits(q, [(g.sem, g.cnt)])
        g.sealed = False
        g.cnt += 16
        s = g.sem
        self.prog[q].append(
            lambda e, out=out, in_=in_, s=s, kw=kw: e.dma_start(out=out, in_=in_, **kw).then_inc(s, 16))
        for b in reads:
            b.r.append(g)
        for b in writes:
            b.w = g
            b.r = []

    def barrier(self, engines=None):
        deps = [(self.sem[k], self.cnt[k]) for k in ("pe", "dve", "act", "pool") if self.cnt[k] > 0]
        deps += [g for g in self.groups if g.cnt > 0]
        for e in (engines or self.ENG):
            self._waits(e, deps)

    def run_block(self):
        nc = self.nc
        with nc.Block() as block:
            @block.sync
            def _(e):
                for t in self.prog["sp"]:
                    t(e)

            @block.tensor
            def _(e):
                for t in self.prog["pe"]:
                    t(e)

            @block.vector
            def _(e):
                for t in self.prog["dve"]:
                    t(e)

            @block.scalar
            def _(e):
                for t in self.prog["act"]:
                    t(e)

            @block.gpsimd
            def _(e):
                for t in self.prog["pool"]:
                    t(e)


_CONSTS = None


def _consts():
    global _CONSTS
    if _CONSTS is not None:
        return _CONSTS
    f = np.float32
    c = {}
    c["c_ident"] = np.eye(128, dtype=f)
    m = np.zeros((8, 128, 240), f)
    for a in range(8):
        for i in range(16):
            m[a, 16 * a + i, 112 + i] = 1.0
    c["c_masters"] = m
    ml = np.array(ML, np.float64)
    rows = np.concatenate([ml / (2 * np.pi), ml, 8.0 * (np.arange(64) + 1) / (2 * np.pi)])
    c["c_rows"] = rows.astype(f)[None, :]
    sg = np.zeros((128, 2), f)
    sg[:64, 0] = 1.0
    sg[64:, 0] = -1.0
    sg[:64, 1] = -1.0
    sg[64:, 1] = 1.0
    c["c_sgn"] = sg
    inv = (f(10000.0) ** (-(np.arange(64, dtype=f) / f(64.0)))).astype(f)
    pos = np.zeros((128, NT), f)
    for n in range(NTP):
        pos[:, n] = 128 * n + np.arange(128)
    pos[:64, 16] = PAST + (np.arange(64) % 4)
    ang = (pos[:, :, None] * inv[None, None, :]).astype(f).astype(np.float64)
    c["c_rope"] = np.stack([np.cos(ang), np.sin(ang), -np.sin(ang)]).astype(f)
    lg = np.log(np.array(GAM, np.float64))
    sc = 128.0 ** -0.5
    idx = np.arange(128)
    dm = np.zeros((128, 4, 128), np.float64)
    diff = idx[None, :] - idx[:, None]
    for h in range(4):
        dm[:, h, :] = np.where(diff >= 0, np.exp(np.maximum(diff, 0) * lg[h]), 0.0) * sc
    c["c_dmask_p"] = dm.reshape(128, 512).astype(f)
    ds_ = np.zeros((64, 4, 64), np.float64)
    r = np.arange(64)
    bb = r // 4
    tt = r % 4
    same = bb[:, None] == bb[None, :]
    dts = tt[None, :] - tt[:, None]
    for h in range(4):
        ds_[:, h, :] = np.where(same & (dts >= 0), np.exp(np.maximum(dts, 0) * lg[h]), 0.0) * sc
    c["c_dmask_s"] = ds_.reshape(64, 256).astype(f)
    xi_p = np.stack([np.exp((idx + 1.0) * lg[h]) * sc for h in range(4)])
    xi_s = np.stack([np.exp((tt + 1.0) * lg[h]) * sc for h in range(4)])
    c["c_xi"] = np.concatenate([xi_p.reshape(-1), xi_s.reshape(-1)]).astype(f)[None, :]
    zp = np.stack([np.exp((127.0 - idx) * lg[h]) for h in range(4)], axis=1)
    c["c_zeta_p"] = zp.astype(f)
    zs = np.zeros((64, 16, 4), np.float64)
    for h in range(4):
        for b in range(16):
            zs[:, b, h] = np.where(bb == b, np.exp((3.0 - tt) * lg[h]), 0.0)
    c["c_zs"] = zs.reshape(64, 64).astype(f)
    cm = np.zeros((16, 64), f)
    for b in range(16):
        cm[b, 4 * b:4 * b + 4] = 1.0
    c["c_cmask"] = cm.reshape(1, -1)
    _CONSTS = c
    return c


W_NAMES = ["g_mix", "w_in", "lam_re", "lam_im", "log_dt", "b_re", "b_im", "c_re", "c_im", "d_skip", "w_glu",
           "ret_gn", "w_out", "g_xattn", "g_mem", "w_mq", "w_mk", "w_mv", "w_mo", "g_mlp", "w_up", "w_down",
           "g_final"]
W_SHAPES = {"g_mix": [D], "w_in": [D, 2560], "lam_re": [G, 64], "lam_im": [G, 64], "log_dt": [G],
            "b_re": [G, 64, 16], "b_im": [G, 64, 16], "c_re": [G * 16, 64], "c_im": [G * 16, 64], "d_skip": [512],
            "w_glu": [512, 512], "ret_gn": [512], "w_out": [D, D], "g_xattn": [D], "g_mem": [D], "w_mq": [D, D],
            "w_mk": [D, D], "w_mv": [D, D], "w_mo": [D, D], "g_mlp": [D], "w_up": [D, DFF], "w_down": [DFF, D],
            "g_final": [D]}
IN_SHAPES = {"xp": [SEQ, D], "xs": [TS, D], "memp": [MEM, D], "s5r": [512, 64], "s5i": [512, 64],
             "sret": [16, 4, 128, 128], "ck": [16, MEM, D], "cv": [16, MEM, D]}
OUT_SHAPES = {"yp": [SEQ, D], "ys": [TS, D], "o_s5r_p": [G, 64], "o_s5i_p": [G, 64], "o_ret_p": [4, 128, 128],
              "o_mk": [MEM, D], "o_mv": [MEM, D], "o_s5r_s": [512, 64], "o_s5i_s": [512, 64],
              "o_ret_s": [16, 4, 128, 128]}


def build(stage=99, dbg=False):
    nc = bass.Bass("TRN2", target_bir_lowering=False)
    cst = _consts()
    I = {}
    for k, shp in list(IN_SHAPES.items()) + list(W_SHAPES.items()):
        I[k] = nc.dram_tensor(k, shp, F32, kind="ExternalInput").ap()
    for k, v in cst.items():
        I[k] = nc.dram_tensor(k, list(v.shape), F32, kind="ExternalInput").ap()
    O = {}
    for k, shp in OUT_SHAPES.items():
        O[k] = nc.dram_tensor(k, shp, F32, kind="ExternalOutput").ap()
    if dbg:
        O["dbg_ssm"] = nc.dram_tensor("dbg_ssm", [128, 4, NTOK], F32, kind="ExternalOutput").ap()
        O["dbg_x"] = nc.dram_tensor("dbg_x", [128, NT, D], F32, kind="ExternalOutput").ap()

    with ExitStack() as st:
        S = Sched(nc, st)

        def alloc(stack, name, shape, dt=F32):
            return stack.enter_context(nc.sbuf_tensor(name, shape, dt))

        def palloc(stack, name, shape, dt=F32):
            return stack.enter_context(nc.psum_tensor(name, shape, dt))

        def V(fn, r=(), w=()):
            S.op("dve", fn, reads=r, writes=w)

        def A(fn, r=(), w=()):
            S.op("act", fn, reads=r, writes=w)

        import os as _os0
        _nopool = _os0.environ.get("K_NOPOOL") == "1"

        def PL(fn, r=(), w=()):
            S.op("dve" if _nopool else "pool", fn, reads=r, writes=w)

        def T(fn, r=(), w=()):
            S.op("pe", fn, reads=r, writes=w)

        nck = nc.allow_non_contiguous_dma(reason="small param layout loads")
        nck.__enter__()

        identb = alloc(st, "identb", [128, 128], BF16)
        identf = alloc(st, "identf", [128, 128], F32)
        sgn = alloc(st, "sgn", [128, 2])
        epsc = alloc(st, "epsc", [128, 1])
        ssmT = alloc(st, "ssmT", [128, 4, NTOK], BF16)
        b_const = Buf("const")
        b_ssmT = [Buf("ssmT%d" % i) for i in range(5)]
        b_constp = Buf("constp")
        S.dma("pool", identb[:], I["c_ident"][:, :], writes=[b_constp])
        S.dma("sp", identf[:], I["c_ident"][:, :], writes=[b_const])
        S.dma("sp", sgn[:], I["c_sgn"][:, :], writes=[b_const])
        V(lambda e: e.memset(epsc[:], EPS), r=[b_constp], w=[b_const])
        PS = [palloc(st, "ps%d" % i, [128, 512], F32) for i in range(8)]
        bPS = [Buf("ps%d" % i, ps=True) for i in range(8)]

        def ps_bf(i):
            return PS[i][:].bitcast(BF16)

        def rmsnorm_hT(xt_ap, bx, npart, gcol, hT_ap, bhT, scr, col0, ph, ln=False, bg=None):
            sq, ss, rstd, hb, bscr, pbank = scr
            lim = ph if ph is not None else 99
            if lim == 0:
                V(lambda e: e.tensor_tensor(out=sq[:npart, :], in0=xt_ap, in1=xt_ap, op=ALU.mult), r=[bx], w=[bscr])
                return
            if lim == -1:
                A(lambda e: e.activation(out=sq[:npart, :], in_=xt_ap, func=AF.Square), r=[bx], w=[bscr])
                return
            A(lambda e: e.activation(out=sq[:npart, :], in_=xt_ap, func=AF.Square, accum_out=ss[:npart, :]),
              r=[bx], w=[bscr])
            if lim <= 1:
                return
            if ln:
                A(lambda e: e.activation(out=rstd[:npart, :], in_=ss[:npart, :], func=AF.Ln, scale=1.0 / D,
                                         bias=epsc[:npart, :]), r=[bscr, b_const], w=[bscr])
                A(lambda e: e.activation(out=rstd[:npart, :], in_=rstd[:npart, :], func=AF.Exp, scale=-0.5),
                  r=[bscr], w=[bscr])
            else:
                A(lambda e: e.activation(out=rstd[:npart, :], in_=ss[:npart, :], func=AF.Sqrt, scale=1.0 / D,
                                         bias=epsc[:npart, :]), r=[bscr, b_const], w=[bscr])
                V(lambda e: e.reciprocal(out=rstd[:npart, :], in_=rstd[:npart, :]), r=[bscr], w=[bscr])
            if lim <= 2:
                return
            V(lambda e: e.tensor_scalar(out=hb[:npart, :], in0=xt_ap, scalar1=rstd[:npart, :], scalar2=None,
                                        op0=ALU.mult), r=[bx, bscr], w=[bscr])
            if lim <= 3:
                return
            pv = ps_bf(pbank)
            for kt in range(8):
                T(lambda e, kt=kt: e.transpose(out=pv[:, kt * 128:kt * 128 + npart],
                                               in_=hb[:npart, kt * 128:(kt + 1) * 128],
                                               identity=identb[:npart, :npart]),
                  r=[bscr, b_const], w=[bPS[pbank]])
            V(lambda e: e.tensor_tensor(
                out=hT_ap[:, :, col0:col0 + npart],
                in0=pv.rearrange("p (k t) -> p k t", k=8)[:, :, 0:npart],
                in1=gcol.unsqueeze(2).to_broadcast([128, 8, npart]), op=ALU.mult),
              r=[bPS[pbank], b_const] + ([bg] if bg is not None else []), w=[bhT])

        def load_w_bf16(dst, bdst, src, kt_n, ncols, c0=0):
            for kt in range(kt_n):
                for cc in range(0, ncols, 1024):
                    w_ = min(1024, ncols - cc)
                    S.dma("pool", dst[:, kt, cc:cc + w_], src[kt * 128:(kt + 1) * 128, c0 + cc:c0 + cc + w_],
                          writes=[bdst])

        with ExitStack() as sa:
            Wt = alloc(sa, "Wt", [128, G, 128], BF16)
            Wst = alloc(sa, "Wst", [128, G, 128], BF16)
            Tt = alloc(sa, "Tt", [128, G, 128], BF16)
            Vt = alloc(sa, "Vt", [128, G, 128], BF16)
            COSR = alloc(sa, "COSR", [128, G, 64])
            SINR = alloc(sa, "SINR", [128, G, 64])
            masters = alloc(sa, "masters", [128, 8, 240], BF16)
            AR = alloc(sa, "AR", [128, G, K1])
            AI = alloc(sa, "AI", [128, G, K1])
            MAGJ = alloc(sa, "MAGJ", [128, G, K1])
            DS = alloc(sa, "DS", [128, G])
            gm = alloc(sa, "gm", [128, 8])
            winu = alloc(sa, "winu", [128, 8, 512], BF16)
            wglu = alloc(sa, "wglu", [128, 4, 512], BF16)
            b_tab = Buf("s5tab")
            b_winu = Buf("winu", S.GW[0])
            b_wglu = Buf("wglu", S.GW[1])
            b_tabp = Buf("s5tabp")
            S.dma("pool", masters[:], I["c_masters"].rearrange("a k j -> k a j"), writes=[b_tabp])
            S.dma("sp", gm[:], I["g_mix"].rearrange("(k p) -> p k", p=128), writes=[b_tab])
            for tau in range(8):
                S.dma("sp", DS[16 * tau:16 * tau + 16, :], I["d_skip"].rearrange("(g h) -> h g", h=16),
                      writes=[b_tab])
            load_w_bf16(winu, b_winu, I["w_in"], 8, 512, 0)
            load_w_bf16(wglu, b_wglu, I["w_glu"], 4, 512, 0)

            with ExitStack() as s0:
                rows = alloc(s0, "rows", [128, 2 * K1 + 64])
                LR = alloc(s0, "LR", [128, G])
                LI = alloc(s0, "LI", [128, G])
                DT = alloc(s0, "DT", [128, G])
                LRDT = alloc(s0, "LRDT", [128, G])
                LIDT = alloc(s0, "LIDT", [128, G])
                tA = alloc(s0, "tA", [128, G, 64])
                tB = alloc(s0, "tB", [128, G, 64])
                tC = alloc(s0, "tC", [128, G, 64])
                COSJ = alloc(s0, "COSJ", [128, G, K1])
                SINJ = alloc(s0, "SINJ", [128, G, K1])
                sm = alloc(s0, "sm", [128, 12, G])
                Br1 = alloc(s0, "Br1", [128, G, 16])
                Br2 = alloc(s0, "Br2", [128, G, 16])
                BB1 = alloc(s0, "BB1", [128, G, 16])
                BB2 = alloc(s0, "BB2", [128, G, 16])
                tb1 = alloc(s0, "tb1", [128, G, 16])
                big1 = alloc(s0, "big1", [128, G, 128])
                big2 = alloc(s0, "big2", [128, G, 128])
                WTpad = alloc(s0, "WTpad", [128, G, 256], BF16)
                WTs = alloc(s0, "WTs", [128, G, 128], BF16)
                CN1 = alloc(s0, "CN1", [128, 4, 128])
                CN2 = alloc(s0, "CN2", [128, 4, 128])
                CMa = alloc(s0, "CMa", [128, G, 16])
                CMb = alloc(s0, "CMb", [128, G, 16])
                CMab = alloc(s0, "CMab", [128, G, 16], BF16)
                b0 = Buf("p0in")
                bt = Buf("p0tmp")
                S.dma("sp", rows[:], I["c_rows"][0:1, :].partition_broadcast(128), writes=[b0])
                for hf in range(2):
                    S.dma("sp", LR[64 * hf:64 * hf + 64, :], I["lam_re"].rearrange("g p -> p g"), writes=[b0])
                    S.dma("sp", LI[64 * hf:64 * hf + 64, :], I["lam_im"].rearrange("g p -> p g"), writes=[b0])
                S.dma("sp", DT[:], I["log_dt"].rearrange("(o g) -> o g", o=1).partition_broadcast(128), writes=[b0])
                S.dma("sp", Br1[0:64], I["b_re"].rearrange("g p h -> p g h"), writes=[b0])
                S.dma("sp", Br1[64:128], I["b_im"].rearrange("g p h -> p g h"), writes=[b0])
                S.dma("sp", Br2[0:64], I["b_im"].rearrange("g p h -> p g h"), writes=[b0])
                S.dma("sp", Br2[64:128], I["b_re"].rearrange("g p h -> p g h"), writes=[b0])
                S.dma("sp", CN1[:, :, 0:64], I["c_re"].rearrange("(c r) p -> r c p", r=128), writes=[b0])
                S.dma("sp", CN1[:, :, 64:128], I["c_im"].rearrange("(c r) p -> r c p", r=128), writes=[b0])
                S.dma("sp", CN2[:, :, 0:64], I["c_im"].rearrange("(c r) p -> r c p", r=128), writes=[b0])
                S.dma("sp", CN2[:, :, 64:128], I["c_re"].rearrange("(c r) p -> r c p", r=128), writes=[b0])
                MT1 = rows[:, 0:K1]
                MLr = rows[:, K1:2 * K1]
                MRT = rows[:, 2 * K1:2 * K1 + 64]
                A(lambda e: e.activation(out=DT[:], in_=DT[:], func=AF.Exp), r=[b0], w=[b0])
                V(lambda e: e.tensor_tensor(out=LRDT[:], in0=LR[:], in1=DT[:], op=ALU.mult), r=[b0], w=[bt])
                V(lambda e: e.tensor_tensor(out=LIDT[:], in0=LI[:], in1=DT[:], op=ALU.mult), r=[b0], w=[bt])

                def trig(mt_ap, K, cos_out, sin_out):
                    shp = [128, G, K]
                    a_, b_, c_ = tA[:, :, 0:K], tB[:, :, 0:K], tC[:, :, 0:K]
                    V(lambda e: e.tensor_tensor(out=a_, in0=LIDT[:].unsqueeze(2).to_broadcast(shp),
                                                in1=mt_ap.unsqueeze(1).to_broadcast(shp), op=ALU.mult),
                      r=[bt, b0], w=[bt])
                    for (outp, off) in ((sin_out, 0.0), (cos_out, 0.25)):
                        if outp is None:
                            continue
                        V(lambda e, off=off: e.tensor_scalar(out=c_, in0=a_, scalar1=off, scalar2=None,
                                                             op0=ALU.add), r=[bt], w=[bt])
                        V(lambda e: e.tensor_scalar(out=b_, in0=c_, scalar1=MAGIC, scalar2=None, op0=ALU.add),
                          r=[bt], w=[bt])
                        V(lambda e: e.tensor_scalar(out=b_, in0=b_, scalar1=MAGIC, scalar2=None, op0=ALU.subtract),
                          r=[bt], w=[bt])
                        V(lambda e: e.tensor_tensor(out=c_, in0=c_, in1=b_, op=ALU.subtract), r=[bt], w=[bt])
                        A(lambda e, outp=outp: e.activation(out=outp, in_=c_, func=AF.Sin, scale=TWO_PI),
                          r=[bt], w=[b_tab])

                trig(MT1, K1, COSJ[:], SINJ[:])
                trig(MRT, 64, COSR[:], SINR[:])
                shpj = [128, G, K1]
                V(lambda e: e.tensor_tensor(out=MAGJ[:], in0=LRDT[:].unsqueeze(2).to_broadcast(shpj),
                                            in1=MLr.unsqueeze(1).to_broadcast(shpj), op=ALU.mult),
                  r=[bt, b0], w=[b_tab])
                A(lambda e: e.activation(out=MAGJ[:], in_=MAGJ[:], func=AF.Exp), r=[b_tab], w=[b_tab])
                V(lambda e: e.tensor_tensor(out=AR[:], in0=MAGJ[:], in1=COSJ[:], op=ALU.mult), r=[b_tab], w=[b_tab])
                V(lambda e: e.tensor_tensor(out=AI[:], in0=MAGJ[:], in1=SINJ[:], op=ALU.mult), r=[b_tab], w=[b_tab])
                em1, shalf, cm1, am1r, ai1, den, fr, fi, t0_, t1_ = [sm[:, i, :] for i in range(10)]
                x_ = LRDT[:]
                V(lambda e: e.tensor_scalar(out=em1, in0=x_, scalar1=0.2, scalar2=1.0, op0=ALU.mult, op1=ALU.add),
                  r=[bt], w=[bt])
                for cf in (0.25, 1.0 / 3.0, 0.5):
                    V(lambda e: e.tensor_tensor(out=em1, in0=em1, in1=x_, op=ALU.mult), r=[bt], w=[bt])
                    V(lambda e, cf=cf: e.tensor_scalar(out=em1, in0=em1, scalar1=cf, scalar2=1.0, op0=ALU.mult,
                                                       op1=ALU.add), r=[bt], w=[bt])
                V(lambda e: e.tensor_tensor(out=em1, in0=em1, in1=x_, op=ALU.mult), r=[bt], w=[bt])
                V(lambda e: e.tensor_copy(out=shalf, in_=SINJ[:, :, I_HALF]), r=[b_tab], w=[bt])
                V(lambda e: e.scalar_tensor_tensor(out=cm1, in0=shalf, scalar=-2.0, op0=ALU.mult, in1=shalf,
                                                   op1=ALU.mult), r=[bt], w=[bt])
                V(lambda e: e.tensor_tensor(out=am1r, in0=em1, in1=COSJ[:, :, I_A1], op=ALU.mult), r=[bt, b_tab], w=[bt])
                V(lambda e: e.tensor_tensor(out=am1r, in0=am1r, in1=cm1, op=ALU.add), r=[bt], w=[bt])
                V(lambda e: e.tensor_copy(out=ai1, in_=AI[:, :, I_A1]), r=[b_tab], w=[bt])
                V(lambda e: e.tensor_tensor(out=den, in0=LR[:], in1=LR[:], op=ALU.mult), r=[b0], w=[bt])
                V(lambda e: e.tensor_tensor(out=t0_, in0=LI[:], in1=LI[:], op=ALU.mult), r=[b0], w=[bt])
                V(lambda e: e.tensor_tensor(out=den, in0=den, in1=t0_, op=ALU.add), r=[bt], w=[bt])
                V(lambda e: e.reciprocal(out=den, in_=den), r=[bt], w=[bt])
                V(lambda e: e.tensor_tensor(out=fr, in0=am1r, in1=LR[:], op=ALU.mult), r=[bt, b0], w=[bt])
                V(lambda e: e.tensor_tensor(out=t0_, in0=ai1, in1=LI[:], op=ALU.mult), r=[bt, b0], w=[bt])
                V(lambda e: e.tensor_tensor(out=fr, in0=fr, in1=t0_, op=ALU.add), r=[bt], w=[bt])
                V(lambda e: e.tensor_tensor(out=fr, in0=fr, in1=den, op=ALU.mult), r=[bt], w=[bt])
                V(lambda e: e.tensor_tensor(out=fi, in0=ai1, in1=LR[:], op=ALU.mult), r=[bt, b0], w=[bt])
                V(lambda e: e.tensor_tensor(out=t0_, in0=am1r, in1=LI[:], op=ALU.mult), r=[bt, b0], w=[bt])
                V(lambda e: e.tensor_tensor(out=fi, in0=fi, in1=t0_, op=ALU.subtract), r=[bt], w=[bt])
                V(lambda e: e.tensor_tensor(out=fi, in0=fi, in1=den, op=ALU.mult), r=[bt], w=[bt])
                V(lambda e: e.tensor_scalar(out=Br2[:], in0=Br2[:], scalar1=sgn[:, 1:2], scalar2=None, op0=ALU.mult),
                  r=[b0, b_const], w=[b0])
                shb = [128, G, 16]
                frb = fr.unsqueeze(2).to_broadcast(shb)
                fib = fi.unsqueeze(2).to_broadcast(shb)
                V(lambda e: e.tensor_tensor(out=BB1[:], in0=Br1[:], in1=frb, op=ALU.mult), r=[b0, bt], w=[bt])
                V(lambda e: e.tensor_tensor(out=tb1[:], in0=Br2[:], in1=fib, op=ALU.mult), r=[b0, bt], w=[bt])
                V(lambda e: e.tensor_tensor(out=BB1[:], in0=BB1[:], in1=tb1[:], op=ALU.add), r=[bt], w=[bt])
                V(lambda e: e.tensor_tensor(out=BB2[:], in0=Br2[:], in1=frb, op=ALU.mult), r=[b0, bt], w=[bt])
                V(lambda e: e.tensor_tensor(out=tb1[:], in0=Br1[:], in1=fib, op=ALU.mult), r=[b0, bt], w=[bt])
                V(lambda e: e.tensor_tensor(out=BB2[:], in0=BB2[:], in1=tb1[:], op=ALU.subtract), r=[bt], w=[bt])
                sh4 = [128, G, 8, 16]
                arv = AR[:, :, 0:8].unsqueeze(3).to_broadcast(sh4)
                aiv = AI[:, :, 0:8].unsqueeze(3).to_broadcast(sh4)
                bb1 = BB1[:].unsqueeze(2).to_broadcast(sh4)
                bb2 = BB2[:].unsqueeze(2).to_broadcast(sh4)
                g1 = big1[:].rearrange("p g (s h) -> p g s h", s=8)
                g2 = big2[:].rearrange("p g (s h) -> p g s h", s=8)
                V(lambda e: e.memset(WTpad[:], 0.0), w=[bt])
                V(lambda e: e.tensor_tensor(out=g1, in0=arv, in1=bb1, op=ALU.mult), r=[b_tab, bt], w=[bt])
                V(lambda e: e.tensor_tensor(out=g2, in0=aiv, in1=bb2, op=ALU.mult), r=[b_tab, bt], w=[bt])
                V(lambda e: e.tensor_tensor(out=WTpad[:, :, 0:128], in0=big1[:], in1=big2[:], op=ALU.add),
                  r=[bt], w=[bt])
                V(lambda e: e.tensor_tensor(out=g1, in0=arv, in1=bb2, op=ALU.mult), r=[b_tab, bt], w=[bt])
                V(lambda e: e.tensor_tensor(out=g2, in0=aiv, in1=bb1, op=ALU.mult), r=[b_tab, bt], w=[bt])
                V(lambda e: e.tensor_tensor(out=WTs[:], in0=big1[:], in1=big2[:], op=ALU.subtract), r=[bt], w=[bt])
                for (src_fn, dstt) in ((lambda g: WTpad[:, g, 0:128], Wt), (lambda g: WTs[:, g, :], Wst)):
                    for gq in range(8):
                        bank = gq % 2
                        pv = ps_bf(bank)
                        for j in range(4):
                            g = gq * 4 + j
                            T(lambda e, g=g, j=j, pv=pv, src_fn=src_fn: e.transpose(
                                out=pv[:, j * 128:(j + 1) * 128], in_=src_fn(g), identity=identb[:]),
                              r=[bt, b_const], w=[bPS[bank]])
                        A(lambda e, gq=gq, pv=pv, dstt=dstt: e.copy(
                            out=dstt[:, gq * 4:gq * 4 + 4, :], in_=pv[:, 0:512].rearrange("p (j c) -> p j c", j=4)),
                          r=[bPS[bank]], w=[b_tab])
                for (CN, CM, col) in ((CN1, CMa, 0), (CN2, CMb, None)):
                    for c4 in range(4):
                        bank = 2 + (c4 % 2)
                        T(lambda e, CN=CN, c4=c4, bank=bank: e.transpose(out=PS[bank][:, 0:128], in_=CN[:, c4, :],
                                                                         identity=identf[:]),
                          r=[b0, b_const], w=[bPS[bank]])
                        if col is not None:
                            V(lambda e, CM=CM, c4=c4, bank=bank: e.tensor_scalar(
                                out=CM[:, c4 * 8:(c4 + 1) * 8, :],
                                in0=PS[bank][:, 0:128].rearrange("p (g h) -> p g h", g=8),
                                scalar1=sgn[:, 0:1], scalar2=None, op0=ALU.mult),
                              r=[bPS[bank], b_const], w=[bt])
                        else:
                            V(lambda e, CM=CM, c4=c4, bank=bank: e.tensor_scalar(
                                out=CM[:, c4 * 8:(c4 + 1) * 8, :],
                                in0=PS[bank][:, 0:128].rearrange("p (g h) -> p g h", g=8),
                                scalar1=-1.0, scalar2=None, op0=ALU.mult),
                              r=[bPS[bank]], w=[bt])
                V(lambda e: e.tensor_copy(out=CMab[:], in_=CMa[:]), r=[bt], w=[bt])
                afw = AR[:, :, 8:16].unsqueeze(3).to_broadcast(sh4)
                aifw = AI[:, :, 8:16].unsqueeze(3).to_broadcast(sh4)
                cma = CMa[:].unsqueeze(2).to_broadcast(sh4)
                cmb = CMb[:].unsqueeze(2).to_broadcast(sh4)
                V(lambda e: e.tensor_tensor(out=g1, in0=afw, in1=cma, op=ALU.mult), r=[b_tab, bt], w=[bt])
                V(lambda e: e.tensor_tensor(out=g2, in0=aifw, in1=cmb, op=ALU.mult), r=[b_tab, bt], w=[bt])
                V(lambda e: e.tensor_tensor(out=Vt[:], in0=big1[:], in1=big2[:], op=ALU.add), r=[bt], w=[b_tab])
                for gq in range(8):
                    bank = 4 + (gq % 2)
                    for j in range(4):
                        g = gq * 4 + j
                        for tau in range(8):
                            c0 = (7 - tau) * 16
                            T(lambda e, g=g, j=j, tau=tau, c0=c0, bank=bank: e.matmul(
                                PS[bank][:, j * 128 + tau * 16:j * 128 + tau * 16 + 16],
                                lhsT=WTpad[:, g, c0:c0 + 128], rhs=CMab[:, g, :], start=True, stop=True),
                              r=[bt], w=[bPS[bank]])
                    A(lambda e, gq=gq, bank=bank: e.copy(
                        out=Tt[:, gq * 4:gq * 4 + 4, :], in_=PS[bank][:].rearrange("p (j c) -> p j c", j=4)),
                      r=[bPS[bank]], w=[b_tab])
                S.barrier()
            xst = [alloc(sa, "xst%d" % i, [128, D]) for i in range(2)]
            bxst = [Buf("xst%d" % i, S.GL[i]) for i in range(2)]
            sq = alloc(sa, "sq", [128, D])
            ss = alloc(sa, "ss", [128, 1])
            rstd = alloc(sa, "rstd", [128, 1])
            hb = alloc(sa, "hb", [128, D], BF16)
            bscr = Buf("scrA")
            hT = alloc(sa, "hT", [128, 8, 512], BF16)
            bhT = Buf("hT")
            uT = alloc(sa, "uT", [128, 4, 512], BF16)
            buT = Buf("uT")
            U = alloc(sa, "U", [128, G, 64], BF16)
            bU = Buf("U")
            rr = alloc(sa, "rr", [128, G, 64])
            rs = alloc(sa, "rs", [128, G, 64])
            ww = alloc(sa, "ww", [128, G, 64])
            ws = alloc(sa, "ws", [128, G, 64])
            tmpr = alloc(sa, "tmpr", [128, 16, 64])
            b_r, b_rs, b_w, b_ws, b_tmpr = Buf("r"), Buf("rs"), Buf("w"), Buf("ws"), Buf("tmpr")
            Xb = alloc(sa, "Xb", [128, G, 65], BF16)
            bXb = Buf("Xb")
            Xc = alloc(sa, "Xc", [128, G])
            Xsc = alloc(sa, "Xsc", [128, G])
            ctmp = alloc(sa, "ctmp", [128, 2, G])
            bXc = Buf("Xc", S.GS[0])
            ytmp = alloc(sa, "ytmp", [128, 8, 64])
            bytmp = Buf("ytmp")
            Zt = alloc(sa, "Zt", [128, G, 64], BF16)
            bZ = Buf("Z")
            zT = alloc(sa, "zT", [128, 4, 512], BF16)
            bzT = Buf("zT")
            sig = alloc(sa, "sig", [128, 4, 512])
            bsig = Buf("sig")
            H0 = alloc(sa, "H0", [128, 512])
            H0s = alloc(sa, "H0s", [128, 512])
            hn = alloc(sa, "hn", [128, 4, 128])
            hn2 = alloc(sa, "hn2", [128, 4, 128])
            Hp = alloc(sa, "Hp", [128, G, 16])
            Xf = alloc(sa, "Xf", [128, G, 16])
            xo = alloc(sa, "xo", [128, 4, 128])
            bH = Buf("H0")
            bxo = Buf("xo", S.GS[1])
            V(lambda e: e.memset(Xc[:], 0.0), r=[b_tabp], w=[bXc, b_tab])
            V(lambda e: e.memset(Xsc[:], 0.0), w=[bXc])
            V(lambda e: e.memset(Xb[:], 0.0), w=[bXb])

            blocks = [(i * 512, 512, False) for i in range(4)] + [(SEQ, TS, True)]
            if _os0.environ.get("K1A") == "0":
                blocks = []
            for bi, (t0, n, is_s) in enumerate(blocks):
                nch = n // 8 if not is_s else 16
                ntile = (n + 127) // 128
                for ti in range(ntile):
                    npart = min(128, n - ti * 128)
                    slot = (bi * 4 + ti) % 2
                    src = I["xs"][:, :] if is_s else I["xp"][t0 + ti * 128:t0 + ti * 128 + 128, :]
                    S.dma("sp", xst[slot][:npart, :], src, writes=[bxst[slot]])
                    rmsnorm_hT(xst[slot][:npart, :], bxst[slot], npart, gm[:], hT, bhT,
                               (sq, ss, rstd, hb, bscr, 7), ti * 128, None, bg=b_tab)
                for ct in range(4):
                    bank = ct
                    for kt in range(8):
                        T(lambda e, ct=ct, kt=kt, bank=bank: e.matmul(
                            PS[bank][:, 0:n], lhsT=winu[:, kt, ct * 128:(ct + 1) * 128], rhs=hT[:, kt, 0:n],
                            start=(kt == 0), stop=(kt == 7)), r=[b_winu, bhT], w=[bPS[bank]])
                    A(lambda e, ct=ct, bank=bank: e.copy(out=uT[:, ct, 0:n], in_=PS[bank][:, 0:n]),
                      r=[bPS[bank]], w=[buT])
                for gq in range(4):
                    bank = 4 + (gq % 2)
                    for j in range(8):
                        g = gq * 8 + j
                        ct, gl = g // 8, g % 8
                        if not is_s:
                            uv = uT[:, ct, 0:n].rearrange("p (c s) -> p s c", s=8)
                            sig_list = list(range(8))
                        else:
                            uv = uT[:, ct, 0:n].rearrange("p (b t) -> p t b", t=4)
                            sig_list = [4, 5, 6, 7]
                        for si, sg_ in enumerate(sig_list):
                            rhs = uv[:, sg_ if not is_s else si, :]
                            T(lambda e, j=j, gl=gl, sg_=sg_, rhs=rhs, si=si, bank=bank, L=len(sig_list): e.matmul(
                                PS[bank][:, j * 64:j * 64 + nch],
                                lhsT=masters[:, gl, 112 - 16 * sg_:240 - 16 * sg_], rhs=rhs,
                                start=(si == 0), stop=(si == L - 1)),
                              r=[b_tab, buT], w=[bPS[bank]])
                    A(lambda e, gq=gq, bank=bank: e.copy(
                        out=U[:, gq * 8:gq * 8 + 8, 0:nch],
                        in_=PS[bank][:].rearrange("p (j c) -> p j c", j=8)[:, :, 0:nch]),
                      r=[bPS[bank]], w=[bU])
                if not is_s:
                    for hf in range(2):
                        for j in range(16):
                            g = hf * 16 + j
                            for (wt, bk) in ((Wt, 0), (Wst, 2)):
                                bank = bk + j // 8
                                T(lambda e, g=g, j=j, wt=wt, bank=bank: e.matmul(
                                    PS[bank][:, (j % 8) * 64:(j % 8) * 64 + 64], lhsT=wt[:, g, :], rhs=U[:, g, :],
                                    start=True, stop=True), r=[b_tab, bU], w=[bPS[bank]])
                        for q in range(2):
                            gs = slice(hf * 16 + q * 8, hf * 16 + q * 8 + 8)
                            Sv = PS[q][:].rearrange("p (j c) -> p j c", j=8)
                            Ssv = PS[2 + q][:].rearrange("p (j c) -> p j c", j=8)
                            tm = tmpr[:, q * 8:q * 8 + 8, :]
                            V(lambda e, gs=gs, Sv=Sv: e.tensor_tensor(out=rr[:, gs, :], in0=Sv, in1=COSR[:, gs, :],
                                                                     op=ALU.mult), r=[bPS[q], b_tab], w=[b_r])
                            V(lambda e, gs=gs, Ssv=Ssv, tm=tm: e.tensor_tensor(out=tm, in0=Ssv, in1=SINR[:, gs, :],
                                                                              op=ALU.mult),
                              r=[bPS[2 + q], b_tab], w=[b_tmpr])
                            V(lambda e, gs=gs, tm=tm: e.tensor_tensor(out=rr[:, gs, :], in0=rr[:, gs, :], in1=tm,
                                                                     op=ALU.subtract), r=[b_r, b_tmpr], w=[b_r])
                            V(lambda e, gs=gs, Ssv=Ssv: e.tensor_tensor(out=rs[:, gs, :], in0=Ssv, in1=COSR[:, gs, :],
                                                                       op=ALU.mult), r=[bPS[2 + q], b_tab], w=[b_rs])
                            V(lambda e, gs=gs, Sv=Sv, tm=tm: e.tensor_tensor(out=tm, in0=Sv, in1=SINR[:, gs, :],
                                                                            op=ALU.mult),
                              r=[bPS[q], b_tab], w=[b_tmpr])
                            V(lambda e, gs=gs, tm=tm: e.tensor_tensor(out=rs[:, gs, :], in0=rs[:, gs, :], in1=tm,
                                                                     op=ALU.add), r=[b_rs, b_tmpr], w=[b_rs])
                    for g in range(G):
                        rho = MAGJ[:, g, I_A8:I_A8 + 1].to_broadcast([128, 64])
                        V(lambda e, g=g, rho=rho: e.tensor_tensor_scan(
                            out=ww[:, g, :], data0=rho, data1=rr[:, g, :], initial=Xc[:, g:g + 1], op0=ALU.mult,
                            op1=ALU.add), r=[b_r, b_tab, bXc], w=[b_w])
                        V(lambda e, g=g, rho=rho: e.tensor_tensor_scan(
                            out=ws[:, g, :], data0=rho, data1=rs[:, g, :], initial=Xsc[:, g:g + 1], op0=ALU.mult,
                            op1=ALU.add), r=[b_rs, b_tab, bXc], w=[b_ws])
                    ce, se_ = COSR[:, :, 63], SINR[:, :, 63]
                    we, wse = ww[:, :, 63], ws[:, :, 63]
                    V(lambda e: e.tensor_tensor(out=ctmp[:, 0, :], in0=ce, in1=we, op=ALU.mult), r=[b_w, b_tab], w=[bscr])
                    V(lambda e: e.tensor_tensor(out=ctmp[:, 1, :], in0=se_, in1=wse, op=ALU.mult), r=[b_ws, b_tab], w=[bscr])
                    V(lambda e: e.tensor_tensor(out=Xc[:], in0=ctmp[:, 0, :], in1=ctmp[:, 1, :], op=ALU.add),
                      r=[bscr], w=[bXc])
                    V(lambda e: e.tensor_tensor(out=ctmp[:, 0, :], in0=ce, in1=wse, op=ALU.mult), r=[b_ws, b_tab], w=[bscr])
                    V(lambda e: e.tensor_tensor(out=ctmp[:, 1, :], in0=se_, in1=we, op=ALU.mult), r=[b_w, b_tab], w=[bscr])
                    V(lambda e: e.tensor_tensor(out=Xsc[:], in0=ctmp[:, 0, :], in1=ctmp[:, 1, :], op=ALU.subtract),
                      r=[bscr], w=[bXc])
                    if bi > 0:
                        V(lambda e: e.tensor_copy(out=Xb[:, :, 0], in_=Xb[:, :, 64]), r=[bXb], w=[bXb])
                    PL(lambda e: e.tensor_tensor(out=ww[:], in0=ww[:], in1=COSR[:], op=ALU.mult), r=[b_w, b_tab, bXc],
                       w=[b_w])
                    PL(lambda e: e.tensor_tensor(out=ws[:], in0=ws[:], in1=SINR[:], op=ALU.mult), r=[b_ws, b_tab, bXc],
                       w=[b_ws])
                    PL(lambda e: e.tensor_tensor(out=Xb[:, :, 1:65], in0=ww[:], in1=ws[:], op=ALU.add),
                       r=[b_w, b_ws], w=[bXb])
                    xprev = lambda g: Xb[:, g, 0:64]
                    bXprev = bXb
                    if bi == 3:
                        S.dma("sp", O["o_s5r_p"].rearrange("g p -> p g"), Xc[0:64, :], reads=[bXc])
                        S.dma("sp", O["o_s5i_p"].rearrange("g p -> p g"), Xc[64:128, :], reads=[bXc])
                else:
                    S.dma("sp", hn[:, :, 0:64], I["s5r"].rearrange("(j r) p -> r j p", r=128), writes=[bH])
                    S.dma("sp", hn[:, :, 64:128], I["s5i"].rearrange("(j r) p -> r j p", r=128), writes=[bH])
                    S.dma("sp", hn2[:, :, 0:64], I["s5i"].rearrange("(j r) p -> r j p", r=128), writes=[bH])
                    S.dma("sp", hn2[:, :, 64:128], I["s5r"].rearrange("(j r) p -> r j p", r=128), writes=[bH])
                    for (src_, dst_, bank) in ((hn, H0, 0), (hn2, H0s, 1)):
                        for j in range(4):
                            T(lambda e, src_=src_, j=j, bank=bank: e.transpose(
                                out=PS[bank][:, j * 128:(j + 1) * 128], in_=src_[:, j, :], identity=identf[:]),
                              r=[bH, b_const], w=[bPS[bank]])
                        V(lambda e, dst_=dst_, bank=bank: e.tensor_copy(out=dst_[:], in_=PS[bank][:]),
                          r=[bPS[bank]], w=[bH])
                    V(lambda e: e.tensor_scalar(out=H0s[0:64, :], in0=H0s[0:64, :], scalar1=-1.0, scalar2=None,
                                                op0=ALU.mult), r=[bH], w=[bH])
                    shs = [128, G, 16]
                    h0v = H0[:].rearrange("p (b g) -> p g b", g=G)
                    h0sv = H0s[:].rearrange("p (b g) -> p g b", g=G)

                    def abc(tab, idx):
                        return tab[:, :, idx].unsqueeze(2).to_broadcast(shs)
                    V(lambda e: e.tensor_tensor(out=Xf[:], in0=h0v, in1=abc(AR, I_AM4), op=ALU.mult), r=[bH, b_tab], w=[bxo])
                    V(lambda e: e.tensor_tensor(out=Hp[:], in0=h0sv, in1=abc(AI, I_AM4), op=ALU.mult), r=[bH, b_tab], w=[bxo])
                    V(lambda e: e.tensor_tensor(out=Xb[:, :, 0:16], in0=Xf[:], in1=Hp[:], op=ALU.add), r=[bxo], w=[bXb])
                    V(lambda e: e.tensor_tensor(out=Xf[:], in0=h0v, in1=abc(AR, I_A4), op=ALU.mult), r=[bH, b_tab], w=[bxo])
                    V(lambda e: e.tensor_tensor(out=Hp[:], in0=h0sv, in1=abc(AI, I_A4), op=ALU.mult), r=[bH, b_tab], w=[bxo])
                    V(lambda e: e.tensor_tensor(out=Xf[:], in0=Xf[:], in1=Hp[:], op=ALU.add), r=[bxo], w=[bxo])
                    for q in range(4):
                        bank = q % 2
                        for j in range(8):
                            g = q * 8 + j
                            T(lambda e, g=g, j=j, bank=bank: e.matmul(
                                PS[bank][:, j * 64:j * 64 + 16], lhsT=Wt[:, g, :], rhs=U[:, g, 0:16],
                                start=True, stop=True), r=[b_tab, bU], w=[bPS[bank]])
                        V(lambda e, q=q, bank=bank: e.tensor_tensor(
                            out=Xf[:, q * 8:q * 8 + 8, :], in0=Xf[:, q * 8:q * 8 + 8, :],
                            in1=PS[bank][:].rearrange("p (j c) -> p j c", j=8)[:, :, 0:16], op=ALU.add),
                          r=[bxo, bPS[bank]], w=[bxo])
                    Xf2 = Xf[:].rearrange("p g b -> p (g b)")
                    for j in range(4):
                        T(lambda e, j=j: e.transpose(out=PS[2][:, j * 128:(j + 1) * 128],
                                                     in_=Xf2[:, j * 128:(j + 1) * 128], identity=identf[:]),
                          r=[bxo, b_const], w=[bPS[2]])
                    V(lambda e: e.tensor_copy(out=xo[:], in_=PS[2][:].rearrange("p (j c) -> p j c", j=4)),
                      r=[bPS[2]], w=[bxo])
                    for j in range(4):
                        for gl in range(8):
                            for (nm, c0) in (("o_s5r_s", 0), ("o_s5i_s", 64)):
                                S.dma("sp", O[nm].rearrange("(b g) p -> g b p", g=G)[8 * j + gl],
                                      xo[gl * 16:gl * 16 + 16, j, c0:c0 + 64], reads=[bxo])
                    xprev = lambda g: Xb[:, g, 0:16]
                    bXprev = bXb
                for gq in range(4):
                    bank = 6 + (gq % 2)
                    for j in range(8):
                        g = gq * 8 + j
                        T(lambda e, g=g, j=j, bank=bank: e.matmul(
                            PS[bank][:, j * 64:j * 64 + nch], lhsT=Tt[:, g, :], rhs=U[:, g, 0:nch],
                            start=True, stop=False), r=[b_tab, bU], w=[bPS[bank]])
                        T(lambda e, g=g, j=j, bank=bank: e.matmul(
                            PS[bank][:, j * 64:j * 64 + nch], lhsT=Vt[:, g, :], rhs=xprev(g)[:, 0:nch],
                            start=False, stop=True), r=[b_tab, bXprev], w=[bPS[bank]])
                    gs = slice(gq * 8, gq * 8 + 8)
                    yv = PS[bank][:].rearrange("p (j c) -> p j c", j=8)[:, :, 0:nch]
                    V(lambda e, gs=gs: e.tensor_tensor(out=ytmp[:, :, 0:nch], in0=U[:, gs, 0:nch],
                                                       in1=DS[:, gs].unsqueeze(2).to_broadcast([128, 8, nch]),
                                                       op=ALU.mult), r=[bU, b_tab], w=[bytmp])
                    V(lambda e, yv=yv: e.tensor_tensor(out=ytmp[:, :, 0:nch], in0=yv, in1=ytmp[:, :, 0:nch],
                                                       op=ALU.add), r=[bPS[bank], bytmp], w=[bytmp])
                    A(lambda e, gs=gs: e.activation(out=Zt[:, gs, 0:nch], in_=ytmp[:, :, 0:nch],
                                                    func=AF.Gelu_apprx_tanh), r=[bytmp], w=[bZ])
                for ct in range(4):
                    bank = ct % 2
                    taus = list(range(8)) if not is_s else [4, 5, 6, 7]
                    for ti_, tau in enumerate(taus):
                        for gl in range(8):
                            g = ct * 8 + gl
                            T(lambda e, g=g, gl=gl, tau=tau, ti_=ti_, bank=bank: e.matmul(
                                PS[bank][:, ti_ * 64:ti_ * 64 + nch],
                                lhsT=masters[:, tau, 112 - 16 * gl:240 - 16 * gl], rhs=Zt[:, g, 0:nch],
                                start=(gl == 0), stop=(gl == 7)), r=[b_tab, bZ], w=[bPS[bank]])
                    if not is_s:
                        A(lambda e, ct=ct, bank=bank: e.copy(
                            out=zT[:, ct, 0:n].rearrange("p (c t) -> p t c", t=8),
                            in_=PS[bank][:].rearrange("p (t c) -> p t c", t=8)), r=[bPS[bank]], w=[bzT])
                    else:
                        A(lambda e, ct=ct, bank=bank: e.copy(
                            out=zT[:, ct, 0:n].rearrange("p (b t) -> p t b", t=4),
                            in_=PS[bank][:].rearrange("p (t c) -> p t c", t=8)[:, 0:4, 0:16]),
                          r=[bPS[bank]], w=[bzT])
                for ct in range(4):
                    bank = 2 + (ct % 2)
                    for kt in range(4):
                        T(lambda e, ct=ct, kt=kt, bank=bank: e.matmul(
                            PS[bank][:, 0:n], lhsT=wglu[:, kt, ct * 128:(ct + 1) * 128], rhs=zT[:, kt, 0:n],
                            start=(kt == 0), stop=(kt == 3)), r=[b_wglu, bzT], w=[bPS[bank]])
                    A(lambda e, ct=ct, bank=bank: e.activation(out=sig[:, ct, 0:n], in_=PS[bank][:, 0:n],
                                                               func=AF.Sigmoid), r=[bPS[bank]], w=[bsig])
                V(lambda e: e.tensor_tensor(out=ssmT[:, :, t0:t0 + n], in0=zT[:, :, 0:n], in1=sig[:, :, 0:n],
                                            op=ALU.mult), r=[bzT, bsig], w=[b_ssmT[bi]])
            S.barrier()
        if dbg:
            with ExitStack() as sd:
                dtmp = alloc(sd, "dtmp", [128, 4, NTOK])
                bd = Buf("dtmp", S.GS[2])
                V(lambda e: e.tensor_copy(out=dtmp[:], in_=ssmT[:]), r=b_ssmT, w=[bd])
                S.dma("sp", O["dbg_ssm"][:, :, :], dtmp[:], reads=[bd])
                S.barrier()
        if stage <= 1:
            S.barrier()
            S.run_block()
            nck.__exit__(None, None, None)
            return nc

        with ExitStack() as sbx:
            x = alloc(sbx, "x", [128, NT, D])
            bx = [Buf("x%d" % n, S.GX) for n in range(NT)]
            for n in range(NTP):
                S.dma("sp", x[:, n, :], I["xp"][n * 128:(n + 1) * 128, :], writes=[bx[n]])
            S.dma("sp", x[0:TS, 16, :], I["xs"][:, :], writes=[bx[16]])
            sq = alloc(sbx, "sqB", [128, D])
            ss = alloc(sbx, "ssB", [128, 1])
            rstd = alloc(sbx, "rstdB", [128, 1])
            hb = alloc(sbx, "hbB", [128, D], BF16)
            bscr = Buf("scrB")
            hT1 = alloc(sbx, "hT1", [128, 8, 128], BF16)
            bhT1 = Buf("hT1")
            scrB = (sq, ss, rstd, hb, bscr, 7)

            def resid_add(n, npart, half, bank):
                V(lambda e: e.tensor_tensor(out=x[:npart, n, half * 512:(half + 1) * 512], in0=PS[bank][:npart, :],
                                            in1=x[:npart, n, half * 512:(half + 1) * 512], op=ALU.add),
                  r=[bPS[bank], bx[n]], w=[bx[n]])

            with ExitStack() as s1:
                wq = alloc(s1, "wqkvg", [128, 8, 2048], BF16)
                wout = alloc(s1, "wout", [128, 8, D], BF16)
                b_wq, b_wout = Buf("wq", S.GW[2]), Buf("wout", S.GW[3])
                load_w_bf16(wq, b_wq, I["w_in"], 8, 2048, 512)
                load_w_bf16(wout, b_wout, I["w_out"], 8, D, 0)
                gm2 = alloc(s1, "gm2", [128, 8])
                gn = alloc(s1, "gn", [128, 4])
                rope = alloc(s1, "rope", [128, 3, NT, 64])
                dmp = alloc(s1, "dmp", [128, 512])
                dms = alloc(s1, "dms", [64, 256])
                xi = alloc(s1, "xi", [128, 768])
                zetap = alloc(s1, "zetap", [128, 4])
                zs = alloc(s1, "zs", [64, 64])
                cmask = alloc(s1, "cmask", [128, 16 * 64])
                b_t1 = Buf("tab1")
                S.dma("sp", gm2[:], I["g_mix"].rearrange("(k p) -> p k", p=128), writes=[b_t1])
                S.dma("sp", gn[:], I["ret_gn"].rearrange("(k p) -> p k", p=128), writes=[b_t1])
                for a_ in range(3):
                    S.dma("sp", rope[:, a_, :, :], I["c_rope"][a_], writes=[b_t1])
                S.dma("sp", dmp[:], I["c_dmask_p"][:, :], writes=[b_t1])
                S.dma("sp", dms[:], I["c_dmask_s"][:, :], writes=[b_t1])
                S.dma("sp", xi[:], I["c_xi"][0:1, :].partition_broadcast(128), writes=[b_t1])
                S.dma("sp", zetap[:], I["c_zeta_p"][:, :], writes=[b_t1])
                S.dma("sp", zs[:], I["c_zs"][:, :], writes=[b_t1])
                S.dma("sp", cmask[:], I["c_cmask"][0:1, :].partition_broadcast(128), writes=[b_t1])
                for k in range(4):
                    V(lambda e: e.tensor_scalar(out=wout[:, 4 + k, :], in0=wout[:, 4 + k, :], scalar1=gn[:, k:k + 1],
                                                scalar2=None, op0=ALU.mult), r=[b_wout, b_t1], w=[b_wout])
                t1q = alloc(s1, "t1q", [128, 512])
                t2q = alloc(s1, "t2q", [128, 512])
                t1k = alloc(s1, "t1k", [128, 512])
                t2k = alloc(s1, "t2k", [128, 512])
                qr = alloc(s1, "qr", [128, 512], BF16)
                kr = alloc(s1, "kr", [128, 512], BF16)
                qT = alloc(s1, "qT", [128, 4, 128], BF16)
                qxT = alloc(s1, "qxT", [128, 4, 128], BF16)
                kT = alloc(s1, "kT", [128, 4, 128], BF16)
                vb = alloc(s1, "vb", [128, 512], BF16)
                vz = alloc(s1, "vz", [128, 512], BF16)
                sg_ = alloc(s1, "sgl", [128, 512])
                sT = alloc(s1, "sT", [128, 4, 128], BF16)
                Sst = alloc(s1, "Sst", [128, 4, 128])
                Sbf = alloc(s1, "Sbf", [128, 4, 128], BF16)
                stats = alloc(s1, "stats", [128, 4, 6])
                mv = alloc(s1, "mv", [128, 4, 2])
                rs4 = alloc(s1, "rs4", [128, 4])
                nb4 = alloc(s1, "nb4", [128, 4])
                on = alloc(s1, "on", [128, 512])
                ret = alloc(s1, "ret", [128, 512], BF16)
                retT = alloc(s1, "retT", [128, 4, 128], BF16)
                S0 = [alloc(s1, "S0_%d" % i, [128, 4, 128]) for i in range(2)]
                S0b = [alloc(s1, "S0b_%d" % i, [128, 4, 128], BF16) for i in range(2)]
                qxm = [alloc(s1, "qxm_%d" % i, [128, 4, 64], BF16) for i in range(2)]
                vzb = [alloc(s1, "vzb_%d" % i, [64, 512], BF16) for i in range(2)]
                Sn = [alloc(s1, "Sn_%d" % i, [128, 4, 128]) for i in range(2)]
                bS0 = [Buf("S0_%d" % i, S.GL[i]) for i in range(2)]
                bS0b = [Buf("S0b_%d" % i) for i in range(2)]
                bqxm = [Buf("qxm%d" % i) for i in range(2)]
                bvzb = [Buf("vzb%d" % i) for i in range(2)]
                bSn = [Buf("Sn%d" % i, S.GS[i]) for i in range(2)]
                (b_t1q, b_t2q, b_t1k, b_t2k, b_qr, b_kr, b_qT, b_qxT, b_kT, b_vb, b_vz, b_sg, b_sT, b_Sst, b_Sbf,
                 b_st, b_on, b_ret, b_retT) = [Buf("p1b%d" % i) for i in range(19)]
                b_Sst.grp = S.GS[2]
                V(lambda e: e.memset(Sst[:], 0.0), w=[b_Sst])
                GC_P = [float(g ** 128) for g in GAM]
                GC_S = [float(g ** 4) for g in GAM]

                import os as _os
                _tl = _os.environ.get("K_TILES")
                _tiles = [int(v) for v in _tl.split(",") if int(v) >= 0] if _tl else list(range(NT))
                _step = int(_os.environ.get("K_STEP", "99"))
                for n in _tiles:
                    is_s = (n == 16)
                    npt = TS if is_s else 128
                    tok0 = n * 128
                    pob = [4, 6, 7, 1] if is_s else [4, 4, 4, 4]

                    def po(h):
                        if is_s:
                            return PS[pob[h]][:npt, 0:128]
                        return PS[4][:npt, h * 128:(h + 1) * 128]
                    rmsnorm_hT(x[:npt, n, :], bx[n], npt, gm2[:], hT1, bhT1, scrB, 0, None, bg=b_t1)
                    for c in range(4):
                        for kt in range(8):
                            T(lambda e: e.matmul(PS[c][:npt, :], lhsT=hT1[:, kt, 0:npt],
                                                 rhs=wq[:, kt, c * 512:(c + 1) * 512], start=(kt == 0), stop=(kt == 7)),
                              r=[bhT1, b_wq], w=[bPS[c]])
                    if _step <= 1:
                        continue
                    for (bank, t1_, t2_, out_, bt1, bt2, bo) in ((0, t1q, t2q, qr, b_t1q, b_t2q, b_qr),
                                                               (1, t1k, t2k, kr, b_t1k, b_t2k, b_kr)):
                        pv4 = PS[bank][:npt, :].rearrange("p (h a j) -> p h a j", h=4, a=2)
                        t1v = t1_[:npt, :].rearrange("p (h a j) -> p h a j", h=4, a=2)
                        t2v = t2_[:npt, :].rearrange("p (h a j) -> p h a j", h=4, a=2)
                        cosb = rope[:npt, 0, n, :].unsqueeze(1).unsqueeze(1).to_broadcast([npt, 4, 2, 64])
                        sinb = rope[:npt, 1, n, :].unsqueeze(1).to_broadcast([npt, 4, 64])
                        nsinb = rope[:npt, 2, n, :].unsqueeze(1).to_broadcast([npt, 4, 64])
                        V(lambda e: e.tensor_tensor(out=t1v, in0=pv4, in1=cosb, op=ALU.mult), r=[bPS[bank], b_t1], w=[bt1])
                        V(lambda e: e.tensor_tensor(out=t2v[:, :, 0, :], in0=pv4[:, :, 1, :], in1=nsinb, op=ALU.mult),
                          r=[bPS[bank], b_t1], w=[bt2])
                        V(lambda e: e.tensor_tensor(out=t2v[:, :, 1, :], in0=pv4[:, :, 0, :], in1=sinb, op=ALU.mult),
                          r=[bPS[bank], b_t1], w=[bt2])
                        V(lambda e: e.tensor_tensor(out=out_[:npt, :], in0=t1_[:npt, :], in1=t2_[:npt, :], op=ALU.add),
                           r=[bt1, bt2], w=[bo])
                    if _step <= 2:
                        continue
                    A(lambda e: e.copy(out=vb[:npt, :], in_=PS[2][:npt, :]), r=[bPS[2]], w=[b_vb])
                    if not is_s:
                        V(lambda e: e.tensor_tensor(
                            out=vz[:, :].rearrange("p (h e) -> p h e", h=4),
                            in0=PS[2][:, :].rearrange("p (h e) -> p h e", h=4),
                            in1=zetap[:, :].unsqueeze(2).to_broadcast([128, 4, 128]), op=ALU.mult),
                          r=[bPS[2], b_t1], w=[b_vz])
                    A(lambda e: e.activation(out=sg_[:npt, :], in_=PS[3][:npt, :], func=AF.Silu), r=[bPS[3]], w=[b_sg])
                    pv4b = ps_bf(4)
                    pv5b = ps_bf(5)
                    for h in range(4):
                        T(lambda e: e.transpose(out=pv4b[:, h * 128:h * 128 + npt], in_=qr[:npt, h * 128:(h + 1) * 128],
                                                identity=identb[:npt, :npt]), r=[b_qr, b_const], w=[bPS[4]])
                    for h in range(4):
                        T(lambda e: e.transpose(out=pv5b[:, h * 128:h * 128 + npt], in_=kr[:npt, h * 128:(h + 1) * 128],
                                                identity=identb[:npt, :npt]), r=[b_kr, b_const], w=[bPS[5]])
                    q4 = pv4b[:, 0:512].rearrange("p (h t) -> p h t", h=4)[:, :, 0:npt]
                    k4 = pv5b[:, 0:512].rearrange("p (h t) -> p h t", h=4)[:, :, 0:npt]
                    A(lambda e: e.copy(out=qT[:, :, 0:npt], in_=q4), r=[bPS[4]], w=[b_qT])
                    xiv = (xi[:, 0:512].rearrange("p (h t) -> p h t", h=4) if not is_s
                           else xi[:, 512:768].rearrange("p (h t) -> p h t", h=4))
                    V(lambda e: e.tensor_tensor(out=qxT[:, :, 0:npt], in0=q4, in1=xiv, op=ALU.mult),
                      r=[bPS[4], b_t1], w=[b_qxT])
                    A(lambda e: e.copy(out=kT[:, :, 0:npt], in_=k4), r=[bPS[5]], w=[b_kT])
                    if _step <= 3:
                        continue
                    for h in range(4):
                        T(lambda e: e.matmul(PS[6][:npt, h * 128:h * 128 + npt], lhsT=kT[:, h, 0:npt], rhs=qT[:, h, 0:npt],
                                             start=True, stop=True), r=[b_kT, b_qT], w=[bPS[6]])
                    dmv = (dmp[:, :].rearrange("p (h t) -> p h t", h=4) if not is_s
                           else dms[:, :].rearrange("p (h t) -> p h t", h=4))
                    V(lambda e: e.tensor_tensor(out=sT[:npt, :, 0:npt],
                                                in0=PS[6][:npt, :].rearrange("p (h t) -> p h t", h=4)[:, :, 0:npt],
                                                in1=dmv, op=ALU.mult), r=[bPS[6], b_t1], w=[b_sT])
                    if _step <= 4:
                        continue
                    for h in range(4):
                        only = (n == 0)
                        T(lambda e: e.matmul(po(h), lhsT=sT[:npt, h, 0:npt],
                                             rhs=vb[:npt, h * 128:(h + 1) * 128], start=True, stop=only),
                          r=[b_sT, b_vb], w=[bPS[pob[h]]])
                        if (not is_s) and n > 0:
                            T(lambda e: e.matmul(po(h), lhsT=qxT[:, h, 0:npt],
                                                 rhs=Sbf[:, h, :], start=False, stop=True),
                              r=[b_qxT, b_Sbf], w=[bPS[4]])
                    if not is_s:
                        for h in range(4):
                            T(lambda e: e.matmul(PS[5][:, h * 128:(h + 1) * 128], lhsT=kr[:, h * 128:(h + 1) * 128],
                                                 rhs=vz[:, h * 128:(h + 1) * 128], start=True, stop=True),
                              r=[b_kr, b_vz], w=[bPS[5]])
                        for h in range(4):
                            V(lambda e: e.scalar_tensor_tensor(out=Sst[:, h, :], in0=Sst[:, h, :], scalar=GC_P[h],
                                                               op0=ALU.mult, in1=PS[5][:, h * 128:(h + 1) * 128],
                                                               op1=ALU.add), r=[b_Sst, bPS[5]], w=[b_Sst])
                        A(lambda e: e.copy(out=Sbf[:], in_=Sst[:]), r=[b_Sst], w=[b_Sbf])
                        if n == NTP - 1:
                            S.dma("sp", O["o_ret_p"].rearrange("h d e -> d h e"), Sst[:], reads=[b_Sst])
                    else:
                        for b in range(16):
                            sl = b % 2
                            S.dma("sp", S0[sl][:], I["sret"][b].rearrange("h d e -> d h e"), writes=[bS0[sl]])
                            A(lambda e: e.copy(out=S0b[sl][:], in_=S0[sl][:]), r=[bS0[sl]], w=[bS0b[sl]])
                            V(lambda e: e.tensor_tensor(
                                out=qxm[sl][:], in0=qxT[:, :, 0:64],
                                in1=cmask[:, b * 64:(b + 1) * 64].unsqueeze(1).to_broadcast([128, 4, 64]), op=ALU.mult),
                              r=[b_qxT, b_t1], w=[bqxm[sl]])
                            for h in range(4):
                                T(lambda e: e.matmul(po(h), lhsT=qxm[sl][:, h, :],
                                                     rhs=S0b[sl][:, h, :], start=False, stop=(b == 15)),
                                  r=[bqxm[sl], bS0b[sl]], w=[bPS[pob[h]]])
                            V(lambda e: e.tensor_tensor(
                                out=vzb[sl][:, :].rearrange("p (h e) -> p h e", h=4),
                                in0=PS[2][:64, :].rearrange("p (h e) -> p h e", h=4),
                                in1=zs[:, b * 4:(b + 1) * 4].unsqueeze(2).to_broadcast([64, 4, 128]), op=ALU.mult),
                              r=[bPS[2], b_t1], w=[bvzb[sl]])
                            kvb = 5 if sl == 0 else 0
                            for h in range(4):
                                T(lambda e: e.matmul(PS[kvb][:, h * 128:(h + 1) * 128], lhsT=kr[:64, h * 128:(h + 1) * 128],
                                                     rhs=vzb[sl][:, h * 128:(h + 1) * 128], start=True, stop=True),
                                  r=[b_kr, bvzb[sl]], w=[bPS[kvb]])
                            for h in range(4):
                                V(lambda e: e.scalar_tensor_tensor(out=Sn[sl][:, h, :], in0=S0[sl][:, h, :], scalar=GC_S[h],
                                                                   op0=ALU.mult, in1=PS[kvb][:, h * 128:(h + 1) * 128],
                                                                   op1=ALU.add), r=[bS0[sl], bPS[kvb]], w=[bSn[sl]])
                            S.dma("sp", O["o_ret_s"][b].rearrange("h d e -> d h e"), Sn[sl][:], reads=[bSn[sl]])
                    if _step <= 5:
                        continue
                    for h in range(4):
                        V(lambda e: e.bn_stats(out=stats[:npt, h, :], in_=po(h)),
                          r=[bPS[pob[h]]], w=[b_st])
                    for h in range(4):
                        V(lambda e: e.bn_aggr(out=mv[:npt, h, :], in_=stats[:npt, h, :]), r=[b_st], w=[b_st])
                    A(lambda e: e.activation(out=rs4[:npt, :], in_=mv[:npt, :, 1], func=AF.Sqrt, scale=1.0,
                                             bias=epsc[:npt, :]), r=[b_st, b_const], w=[b_st])
                    V(lambda e: e.reciprocal(out=rs4[:npt, :], in_=rs4[:npt, :]), r=[b_st], w=[b_st])
                    V(lambda e: e.scalar_tensor_tensor(out=nb4[:npt, :], in0=mv[:npt, :, 0], scalar=-1.0, op0=ALU.mult,
                                                       in1=rs4[:npt, :], op1=ALU.mult), r=[b_st], w=[b_st])
                    for h in range(4):
                        A(lambda e: e.activation(out=on[:npt, h * 128:(h + 1) * 128], in_=po(h),
                                                 func=AF.Identity, scale=rs4[:npt, h:h + 1], bias=nb4[:npt, h:h + 1]),
                          r=[bPS[pob[h]], b_st], w=[b_on])
                    V(lambda e: e.tensor_tensor(out=ret[:npt, :], in0=on[:npt, :], in1=sg_[:npt, :], op=ALU.mult),
                       r=[b_on, b_sg], w=[b_ret])
                    if _step <= 6:
                        continue
                    pv6b = ps_bf(6)
                    for h in range(4):
                        T(lambda e: e.transpose(out=pv6b[:, h * 128:h * 128 + npt], in_=ret[:npt, h * 128:(h + 1) * 128],
                                                identity=identb[:npt, :npt]), r=[b_ret, b_const], w=[bPS[6]])
                    A(lambda e: e.copy(out=retT[:, :, 0:npt],
                                       in_=pv6b[:, 0:512].rearrange("p (h t) -> p h t", h=4)[:, :, 0:npt]),
                      r=[bPS[6]], w=[b_retT])
                    if _step <= 7:
                        continue
                    bi_ = min(n // 4, 4)
                    for half in range(2):
                        bank = 2 + half
                        for kt in range(8):
                            lh = ssmT[:, kt, tok0:tok0 + npt] if kt < 4 else retT[:, kt - 4, 0:npt]
                            T(lambda e: e.matmul(PS[bank][:npt, :], lhsT=lh, rhs=wout[:, kt, half * 512:(half + 1) * 512],
                                                 start=(kt == 0), stop=(kt == 7)),
                              r=[b_ssmT[bi_], b_retT, b_wout], w=[bPS[bank]])
                        resid_add(n, npt, half, bank)
                S.barrier()
            if dbg:
                for n in range(NT):
                    S.dma("sp", O["dbg_x"][:, n, :], x[:, n, :], reads=[bx[n]])
            if stage <= 2:
                S.barrier()
                S.run_block()
                nck.__exit__(None, None, None)
                return nc

            with ExitStack() as s2:
                gx = alloc(s2, "gx", [128, 8])
                gmem = alloc(s2, "gmem", [128, 8])
                ones = alloc(s2, "ones", [128, 128], BF16)
                b_t2 = Buf("tab2")
                S.dma("sp", gx[:], I["g_xattn"].rearrange("(k p) -> p k", p=128), writes=[b_t2])
                S.dma("sp", gmem[:], I["g_mem"].rearrange("(k p) -> p k", p=128), writes=[b_t2])
                V(lambda e: e.memset(ones[:], 1.0), w=[b_t2])
                KT = alloc(s2, "KT", [128, 8, MEM], BF16)
                Vm = alloc(s2, "Vm", [128, 2, D], BF16)
                b_KT, b_Vm = Buf("KT"), Buf("Vm")
                wmq = alloc(s2, "wmq", [128, 8, D], BF16)
                b_wmq, b_wmo = Buf("wmq", S.GW[2]), Buf("wmo", S.GW[3])
                with ExitStack() as s2a:
                    wmk = alloc(s2a, "wmk", [128, 8, D], BF16)
                    wmv = alloc(s2a, "wmv", [128, 8, D], BF16)
                    b_wmk, b_wmv = Buf("wmk", S.GW[0]), Buf("wmv", S.GW[1])
                    load_w_bf16(wmk, b_wmk, I["w_mk"], 8, D, 0)
                    load_w_bf16(wmv, b_wmv, I["w_mv"], 8, D, 0)
                    load_w_bf16(wmq, b_wmq, I["w_mq"], 8, D, 0)
                    mx = [alloc(s2a, "mx%d" % i, [128, D]) for i in range(2)]
                    bmx = [Buf("mx%d" % i, S.GL[i]) for i in range(2)]
                    mhT = alloc(s2a, "mhT", [128, 8, MEM], BF16)
                    b_mhT = Buf("mhT")
                    mo = [alloc(s2a, "mo%d" % i, [128, D]) for i in range(2)]
                    bmo = [Buf("mo%d" % i, S.GS[i]) for i in range(2)]
                    _k2a = int(_os.environ.get("K2A", "9"))
                    for mt in range(2):
                        S.dma("sp", mx[mt][:], I["memp"][mt * 128:(mt + 1) * 128, :], writes=[bmx[mt]])
                        if _k2a >= 1:
                            if _os.environ.get("K_MXX") == "1":
                                rmsnorm_hT(x[:, mt, :], bx[mt], 128, gmem[:], mhT, b_mhT, scrB, mt * 128,
                                           int(_os.environ.get("K_RMS", "99")), bg=b_t2)
                            else:
                                rmsnorm_hT(mx[mt][:, :], bmx[mt], 128, gmem[:], mhT, b_mhT, scrB, mt * 128,
                                           int(_os.environ.get("K_RMS", "99")), bg=b_t2)
                    oi = 0
                    for (wm, bwm, oname, isv) in ((wmk, b_wmk, "o_mk", False), (wmv, b_wmv, "o_mv", True)) if _k2a >= 2 else ():
                        for mt in range(2):
                            sl = oi % 2
                            oi += 1
                            for half in range(2):
                                bank = half
                                for kt in range(8):
                                    T(lambda e: e.matmul(PS[bank][:, :], lhsT=mhT[:, kt, mt * 128:(mt + 1) * 128],
                                                         rhs=wm[:, kt, half * 512:(half + 1) * 512], start=(kt == 0),
                                                         stop=(kt == 7)), r=[b_mhT, bwm], w=[bPS[bank]])
                                A(lambda e: e.copy(out=mo[sl][:, half * 512:(half + 1) * 512], in_=PS[bank][:, :]),
                                  r=[bPS[bank]], w=[bmo[sl]])
                                if isv:
                                    V(lambda e: e.tensor_copy(out=Vm[:, mt, half * 512:(half + 1) * 512], in_=PS[bank][:, :]),
                                      r=[bPS[bank]], w=[b_Vm])
                            S.dma("sp", O[oname][mt * 128:(mt + 1) * 128, :], mo[sl][:], reads=[bmo[sl]])
                    for j in range(8 if _k2a >= 3 else 0):
                        bank = 2 + (j % 2)
                        for kt in range(8):
                            T(lambda e: e.matmul(PS[bank][:, 0:MEM], lhsT=wmk[:, kt, j * 128:(j + 1) * 128],
                                                 rhs=mhT[:, kt, :], start=(kt == 0), stop=(kt == 7)),
                              r=[b_mhT, b_wmk], w=[bPS[bank]])
                        A(lambda e: e.copy(out=KT[:, j, :], in_=PS[bank][:, 0:MEM]), r=[bPS[bank]], w=[b_KT])
                    S.barrier()
                wmo = alloc(s2, "wmo", [128, 8, D], BF16)
                load_w_bf16(wmo, b_wmo, I["w_mo"], 8, D, 0)
                hT4 = alloc(s2, "hT4", [128, 8, 512], BF16)
                qm4 = alloc(s2, "qm4", [128, 8, 512], BF16)
                oT4 = alloc(s2, "oT4", [128, 8, 512], BF16)
                eT4 = [alloc(s2, "eT4_%d" % i, [128, 2, 512], BF16) for i in range(2)]
                rdn4 = [alloc(s2, "rdn4_%d" % i, [128, 512]) for i in range(2)]
                b_hT4, b_qm4, b_oT4 = Buf("hT4"), Buf("qm4"), Buf("oT4")
                b_eT4 = [Buf("eT4_%d" % i) for i in range(2)]
                b_rdn4 = [Buf("rdn4_%d" % i) for i in range(2)]
                Kb = [alloc(s2, "Kb%d" % i, [128, 2, D]) for i in range(2)]
                bKb = [Buf("Kb%d" % i, S.GL[i]) for i in range(2)]
                KbT = [alloc(s2, "KbT%d" % i, [128, 8, MEM], BF16) for i in range(2)]
                bKbT = [Buf("KbT%d" % i) for i in range(2)]
                Vb = [alloc(s2, "Vb%d" % i, [128, 2, D], BF16) for i in range(2)]
                bVb = [Buf("Vb%d" % i, S.GW[i]) for i in range(2)]
                eTs = alloc(s2, "eTs", [128, 2, 4, 64], BF16)
                b_eTs = Buf("eTs")
                qrot = [0]

                def q_proj(nc_):
                    for j in range(8):
                        bank = 5 + (qrot[0] % 3)
                        qrot[0] += 1
                        for kt in range(8):
                            T(lambda e: e.matmul(PS[bank][:, 0:nc_], lhsT=wmq[:, kt, j * 128:(j + 1) * 128],
                                                 rhs=hT4[:, kt, 0:nc_], start=(kt == 0), stop=(kt == 7)),
                              r=[b_wmq, b_hT4], w=[bPS[bank]])
                        A(lambda e: e.activation(out=qm4[:, j, 0:nc_], in_=PS[bank][:, 0:nc_], func=AF.Copy,
                                                 scale=1.0 / 16.0), r=[bPS[bank]], w=[b_qm4])

                def w_mo_resid(n, npt, c0):
                    for half in range(2):
                        bank = 5 + (qrot[0] % 3)
                        qrot[0] += 1
                        for j in range(8):
                            T(lambda e: e.matmul(PS[bank][:npt, :], lhsT=oT4[:, j, c0:c0 + npt],
                                                 rhs=wmo[:, j, half * 512:(half + 1) * 512], start=(j == 0), stop=(j == 7)),
                              r=[b_oT4, b_wmo], w=[bPS[bank]])
                        resid_add(n, npt, half, bank)

                for bi in range(4):
                    for ti in range(4):
                        n = bi * 4 + ti
                        rmsnorm_hT(x[:, n, :], bx[n], 128, gx[:], hT4, b_hT4, scrB, ti * 128, None, ln=True, bg=b_t2)
                    q_proj(512)
                    for h in range(4):
                        par = h % 2
                        for mt in range(2):
                            bank = mt
                            for dt_ in range(2):
                                T(lambda e: e.matmul(PS[bank][:, :], lhsT=KT[:, h * 2 + dt_, mt * 128:(mt + 1) * 128],
                                                     rhs=qm4[:, h * 2 + dt_, :], start=(dt_ == 0), stop=(dt_ == 1)),
                                  r=[b_KT, b_qm4], w=[bPS[bank]])
                            A(lambda e: e.activation(out=eT4[par][:, mt, :], in_=PS[bank][:, :], func=AF.Exp),
                              r=[bPS[bank]], w=[b_eT4[par]])
                        for mt in range(2):
                            T(lambda e: e.matmul(PS[2][:, :], lhsT=ones[:, :], rhs=eT4[par][:, mt, :], start=(mt == 0),
                                                 stop=(mt == 1)), r=[b_t2, b_eT4[par]], w=[bPS[2]])
                        A(lambda e: e.activation(out=rdn4[par][:, :], in_=PS[2][:, :], func=AF.Ln), r=[bPS[2]], w=[b_rdn4[par]])
                        A(lambda e: e.activation(out=rdn4[par][:, :], in_=rdn4[par][:, :], func=AF.Exp, scale=-1.0),
                          r=[b_rdn4[par]], w=[b_rdn4[par]])
                        for dt_ in range(2):
                            bank = 3 + dt_
                            j = h * 2 + dt_
                            for mt in range(2):
                                T(lambda e: e.matmul(PS[bank][:, :], lhsT=Vm[:, mt, j * 128:(j + 1) * 128],
                                                     rhs=eT4[par][:, mt, :], start=(mt == 0), stop=(mt == 1)),
                                  r=[b_Vm, b_eT4[par]], w=[bPS[bank]])
                            V(lambda e: e.tensor_tensor(out=oT4[:, j, :], in0=PS[bank][:, :], in1=rdn4[par][:, :], op=ALU.mult),
                              r=[bPS[bank], b_rdn4[par]], w=[b_oT4])
                    for ti in range(4):
                        w_mo_resid(bi * 4 + ti, 128, ti * 128)
                n = 16
                rmsnorm_hT(x[:TS, n, :], bx[n], TS, gx[:], hT4, b_hT4, scrB, 0, None, ln=True, bg=b_t2)
                q_proj(TS)
                rden_s = rdn4[0][:, 0:256].rearrange("p (h t) -> p h t", h=4)
                for b in range(16):
                    sl = b % 2
                    S.dma("sp", Kb[sl][:], I["ck"][b].rearrange("(mt p) d -> p mt d", p=128), writes=[bKb[sl]])
                    for q4 in range(4):
                        bank = 2 + (q4 % 2)
                        for i4 in range(4):
                            idx = q4 * 4 + i4
                            j, mt = idx // 2, idx % 2
                            T(lambda e: e.transpose(out=PS[bank][:, i4 * 128:(i4 + 1) * 128],
                                                    in_=Kb[sl][:, mt, j * 128:(j + 1) * 128], identity=identf[:]),
                              r=[bKb[sl], b_const], w=[bPS[bank]])
                        A(lambda e: e.copy(
                            out=KbT[sl][:, 2 * q4:2 * q4 + 2, :].rearrange("p j (m t) -> p j m t", m=2),
                            in_=PS[bank][:, :].rearrange("p (j m t) -> p j m t", j=2, m=2)),
                          r=[bPS[bank]], w=[bKbT[sl]])
                    for h in range(4):
                        for mt in range(2):
                            c0 = mt * 256 + h * 64 + 4 * b
                            for dt_ in range(2):
                                T(lambda e: e.matmul(PS[4][:, c0:c0 + 4],
                                                     lhsT=KbT[sl][:, h * 2 + dt_, mt * 128:(mt + 1) * 128],
                                                     rhs=qm4[:, h * 2 + dt_, 4 * b:4 * b + 4], start=(dt_ == 0),
                                                     stop=(dt_ == 1)), r=[bKbT[sl], b_qm4], w=[bPS[4]])
                A(lambda e: e.activation(out=eTs[:].rearrange("p m h t -> p (m h t)"), in_=PS[4][:, :], func=AF.Exp),
                  r=[bPS[4]], w=[b_eTs])
                for h in range(4):
                    for mt in range(2):
                        T(lambda e: e.matmul(PS[0][:, h * 64:(h + 1) * 64], lhsT=ones[:, :], rhs=eTs[:, mt, h, :],
                                             start=(mt == 0), stop=(mt == 1)), r=[b_t2, b_eTs], w=[bPS[0]])
                V(lambda e: e.reciprocal(out=rden_s, in_=PS[0][:, 0:256].rearrange("p (h t) -> p h t", h=4)),
                  r=[bPS[0]], w=[b_rdn4[0]])
                for b in range(16):
                    sl = b % 2
                    for mt in range(2):
                        S.dma("pool", Vb[sl][:, mt, :], I["cv"][b, mt * 128:(mt + 1) * 128, :], writes=[bVb[sl]])
                    for j in range(8):
                        h = j // 2
                        for mt in range(2):
                            T(lambda e: e.matmul(PS[1][:, j * 64 + 4 * b:j * 64 + 4 * b + 4],
                                                 lhsT=Vb[sl][:, mt, j * 128:(j + 1) * 128],
                                                 rhs=eTs[:, mt, h, 4 * b:4 * b + 4], start=(mt == 0), stop=(mt == 1)),
                              r=[bVb[sl], b_eTs], w=[bPS[1]])
                V(lambda e: e.tensor_tensor(
                    out=oT4[:, :, 0:64].rearrange("p (h a) t -> p h a t", a=2),
                    in0=PS[1][:, :].rearrange("p (h a t) -> p h a t", h=4, a=2),
                    in1=rden_s.unsqueeze(2).to_broadcast([128, 4, 2, 64]), op=ALU.mult),
                  r=[bPS[1], b_rdn4[0]], w=[b_oT4])
                w_mo_resid(16, TS, 0)
                S.barrier()
            if stage <= 3:
                if dbg:
                    for n in range(NT):
                        S.dma("sp", O["dbg_x"][:, n, :], x[:, n, :], reads=[bx[n]])
                S.barrier()
                S.run_block()
                nck.__exit__(None, None, None)
                return nc

            with ExitStack() as s3:
                gml = alloc(s3, "gml", [128, 8])
                b_t3 = Buf("tab3")
                S.dma("sp", gml[:], I["g_mlp"].rearrange("(k p) -> p k", p=128), writes=[b_t3])
                hTa = alloc(s3, "hTa", [128, 8, NTOK], BF16)
                b_hTa = [Buf("hTa%d" % n) for n in range(NT)]
                wup = [alloc(s3, "wup%d" % i, [128, 8, 512], BF16) for i in range(2)]
                wdn = [alloc(s3, "wdn%d" % i, [128, 4, D], BF16) for i in range(2)]
                bwup = [Buf("wup%d" % i, S.GW[i]) for i in range(2)]
                bwdn = [Buf("wdn%d" % i, S.GW[2 + i]) for i in range(2)]
                rl = [alloc(s3, "rl%d" % i, [128, 512]) for i in range(2)]
                brl = [Buf("rl%d" % i) for i in range(2)]
                aT = [alloc(s3, "aT%d" % i, [128, 4, 512], BF16) for i in range(2)]
                baT = [Buf("aT%d" % i) for i in range(2)]

                def load_fc(fc):
                    sl = fc % 2
                    for kt in range(8):
                        S.dma("pool", wup[sl][:, kt, :], I["w_up"][kt * 128:(kt + 1) * 128, fc * 512:(fc + 1) * 512],
                              writes=[bwup[sl]])
                    for ft in range(4):
                        S.dma("pool", wdn[sl][:, ft, :], I["w_down"][fc * 512 + ft * 128:fc * 512 + (ft + 1) * 128, :],
                              writes=[bwdn[sl]])
                load_fc(0)
                for n in range(NT):
                    npt = TS if n == 16 else 128
                    rmsnorm_hT(x[:npt, n, :], bx[n], npt, gml[:], hTa, b_hTa[n], scrB, n * 128, None, bg=b_t3)
                blocks3 = [(i * 512, 512) for i in range(4)] + [(SEQ, TS)]
                ai = 0
                ri = 0
                di = 0
                for fc in range(8):
                    sl = fc % 2
                    if fc + 1 < 8:
                        load_fc(fc + 1)
                    for (t0, nn) in blocks3:
                        tiles = list(range(t0 // 128, t0 // 128 + (nn + 127) // 128))
                        asl = ai % 2
                        ai += 1
                        for ft in range(4):
                            bank = ft
                            for kt in range(8):
                                T(lambda e: e.matmul(PS[bank][:, 0:nn], lhsT=wup[sl][:, kt, ft * 128:(ft + 1) * 128],
                                                     rhs=hTa[:, kt, t0:t0 + nn], start=(kt == 0), stop=(kt == 7)),
                                  r=[bwup[sl]] + [b_hTa[t] for t in tiles], w=[bPS[bank]])
                            rsl = ri % 2
                            ri += 1
                            A(lambda e: e.activation(out=rl[rsl][:, 0:nn], in_=PS[bank][:, 0:nn], func=AF.Relu),
                              r=[bPS[bank]], w=[brl[rsl]])
                            V(lambda e: e.tensor_tensor(out=aT[asl][:, ft, 0:nn], in0=rl[rsl][:, 0:nn], in1=rl[rsl][:, 0:nn],
                                                        op=ALU.mult), r=[brl[rsl]], w=[baT[asl]])
                        for ti, tl in enumerate(tiles):
                            npt = TS if tl == 16 else 128
                            for half in range(2):
                                bank = 4 + (di % 4)
                                di += 1
                                for ft in range(4):
                                    T(lambda e: e.matmul(PS[bank][:npt, :], lhsT=aT[asl][:, ft, ti * 128:ti * 128 + npt],
                                                         rhs=wdn[sl][:, ft, half * 512:(half + 1) * 512], start=(ft == 0),
                                                         stop=(ft == 3)), r=[baT[asl], bwdn[sl]], w=[bPS[bank]])
                                resid_add(tl, npt, half, bank)
                S.barrier()
            if dbg:
                for n in range(NT):
                    S.dma("sp", O["dbg_x"][:, n, :], x[:, n, :], reads=[bx[n]])
            with ExitStack() as s4:
                gf = alloc(s4, "gf", [128, D])
                b_gf = Buf("gf")
                S.dma("sp", gf[:], I["g_final"].rearrange("(o d) -> o d", o=1).partition_broadcast(128), writes=[b_gf])
                yst = [alloc(s4, "yst%d" % i, [128, D]) for i in range(3)]
                byst = [Buf("yst%d" % i, S.GS[i]) for i in range(3)]
                for n in range(NT):
                    npt = TS if n == 16 else 128
                    sl = n % 3
                    A(lambda e: e.activation(out=sq[:npt, :], in_=x[:npt, n, :], func=AF.Square, accum_out=ss[:npt, :]),
                      r=[bx[n]], w=[bscr])
                    A(lambda e: e.activation(out=rstd[:npt, :], in_=ss[:npt, :], func=AF.Sqrt, scale=1.0 / D,
                                             bias=epsc[:npt, :]), r=[bscr, b_const], w=[bscr])
                    V(lambda e: e.reciprocal(out=rstd[:npt, :], in_=rstd[:npt, :]), r=[bscr], w=[bscr])
                    V(lambda e: e.scalar_tensor_tensor(out=yst[sl][:npt, :], in0=x[:npt, n, :], scalar=rstd[:npt, :],
                                                       op0=ALU.mult, in1=gf[:npt, :], op1=ALU.mult),
                      r=[bx[n], bscr, b_gf], w=[byst[sl]])
                    if n < 16:
                        S.dma("sp", O["yp"][n * 128:(n + 1) * 128, :], yst[sl][:, :], reads=[byst[sl]])
                    else:
                        S.dma("sp", O["ys"][:, :], yst[sl][:TS, :], reads=[byst[sl]])
                S.barrier()
            S.barrier()
            S.run_block()
            nck.__exit__(None, None, None)
    return nc


_NC = None


def kernel(**inputs):
    global _NC
    if _NC is None:
        _NC = build()
    maps = _in_maps(inputs)
    res = run_bass_kernel_spmd(_NC, maps, core_ids=list(range(8)))
    R = res.results
    f = np.float32

    def cat(name, shape=None):
        return np.stack([np.asarray(R[c][name], f) for c in range(8)])
    y_prompt = cat("yp")
    y_sample = cat("ys").reshape(128, 4, D)
    s5r_p = cat("o_s5r_p")[None]
    s5i_p = cat("o_s5i_p")[None]
    ret_p = cat("o_ret_p")[None]
    mk_p = cat("o_mk").reshape(8, MEM, 4, 256)[None]
    mv_p = cat("o_mv").reshape(8, MEM, 4, 256)[None]
    s5r_s = cat("o_s5r_s").reshape(128, G, 64)[None]
    s5i_s = cat("o_s5i_s").reshape(128, G, 64)[None]
    ret_s = cat("o_ret_s").reshape(128, 4, 128, 128)[None]
    return (y_prompt, y_sample, s5r_p, s5i_p, ret_p, mk_p, mv_p, s5r_s, s5i_s, ret_s)


def _in_maps(inputs):
    cst = _consts()
    f = np.float32
    maps = []
    w = {}
    for k in W_NAMES:
        a = np.asarray(inputs[k], f)
        if k != "g_final":
            a = a[0]
        w[k] = np.ascontiguousarray(a.reshape(W_SHAPES[k]))
    for c in range(8):
        m = dict(w)
        m.update(cst)
        b0 = 16 * c
        m["xp"] = np.ascontiguousarray(np.asarray(inputs["x_prompt"], f)[c])
        m["xs"] = np.ascontiguousarray(np.asarray(inputs["x_sample"], f)[b0:b0 + 16].reshape(TS, D))
        m["memp"] = np.ascontiguousarray(np.asarray(inputs["mem_prompt"], f)[c])
        m["s5r"] = np.ascontiguousarray(np.asarray(inputs["state_s5_re"], f)[0, b0:b0 + 16].reshape(512, 64))
        m["s5i"] = np.ascontiguousarray(np.asarray(inputs["state_s5_im"], f)[0, b0:b0 + 16].reshape(512, 64))
        m["sret"] = np.ascontiguousarray(np.asarray(inputs["state_ret"], f)[0, b0:b0 + 16])
        m["ck"] = np.ascontiguousarray(np.asarray(inputs["cache_mem_k"], f)[0, b0:b0 + 16].reshape(16, MEM, D))
        m["cv"] = np.ascontiguousarray(np.asarray(inputs["cache_mem_v"], f)[0, b0:b0 + 16].reshape(16, MEM, D))
        maps.append(m)
    return maps
```

```python
import numpy as np
import concourse.bass as bass
import concourse.mybir as mybir
from concourse.bass_utils import run_bass_kernel_spmd
from contextlib import ExitStack

F32 = mybir.dt.float32
BF16 = mybir.dt.bfloat16
AF = mybir.ActivationFunctionType
ALU = mybir.AluOpType

D = 1024
SEQ = 2048
NTP = 16
TS = 64
NT = 17
NTOK = SEQ + TS
G = 32
DFF = 4096
MEM = 256
EPS = 1e-6
PAST = 16384.0
MAGIC = 12582912.0
TWO_PI = float(2.0 * np.pi)
ML = [7, 6, 5, 4, 3, 2, 1, 0, 1, 2, 3, 4, 5, 6, 7, 8, -4, 0.5]
K1 = len(ML)
I_A1, I_A8, I_A4, I_AM4, I_HALF = 8, 15, 3, 16, 17
GAM = [1.0 - 2.0 ** (-5.0 - h) for h in range(4)]


class Grp:
    __slots__ = ("sem", "cnt", "sealed")


class Buf:
    __slots__ = ("w", "r", "name", "grp", "ps")

    def __init__(self, name="", grp=None, ps=False):
        self.w = None
        self.r = []
        self.name = name
        self.grp = grp
        self.ps = ps


class _Rec:
    def __init__(self):
        self.call = None

    def __getattr__(self, name):
        def f(*a, **kw):
            self.call = (name, a, kw)
            return self
        return f


class Sched:
    ENG = ("pe", "dve", "act", "pool", "sp")

    def __init__(self, nc, stack, self_sync=("dve", "act", "pool")):
        self.nc = nc
        self.stack = stack
        self.prog = {k: [] for k in self.ENG}
        self.cnt = {k: 0 for k in self.ENG}
        self.waited = {k: {} for k in self.ENG}
        self.sem = {}
        self.nsem = 0
        for k in ("pe", "dve", "act", "pool"):
            self.sem[k] = self.new_sem("c_" + k)
        self.self_sync = set(self_sync)
        self.groups = []
        self.GC = self.group("gc")
        self.GP = self.group("gp")
        self.GW = [self.group("gw%d" % i) for i in range(4)]
        self.GX = self.group("gx")
        self.GL = [self.group("gl%d" % i) for i in range(2)]
        self.GS = [self.group("gs%d" % i) for i in range(3)]

    def group(self, name):
        g = Grp()
        g.sem = self.new_sem(name)
        g.cnt = 0
        g.sealed = False
        self.groups.append(g)
        return g

    def new_sem(self, name):
        self.nsem += 1
        assert self.nsem < 98, "too many semaphores"
        return self.stack.enter_context(self.nc.semaphore(name + "_%d" % self.nsem))

    def _waits(self, eng, deps):
        w = self.waited[eng]
        need = {}
        dd = []
        for d in deps:
            if isinstance(d, Grp):
                d.sealed = True
                dd.append((d.sem, d.cnt))
            else:
                dd.append(d)
        deps = dd
        for (s, v) in deps:
            if eng in self.sem and s is self.sem[eng] and eng not in self.self_sync:
                continue
            k = id(s)
            if w.get(k, 0) >= v:
                continue
            if k not in need or need[k][1] < v:
                need[k] = (s, v)
        for k, (s, v) in need.items():
            w[k] = v
            self.prog[eng].append(lambda e, s=s, v=v: e.wait_ge(s, v))

    def op(self, eng, fn, reads=(), writes=()):
        deps = []
        for b in reads:
            if b.w is not None:
                deps.append(b.w)
            if b.ps:
                mys = self.sem[eng]
                deps.extend(d for d in b.r if not (isinstance(d, tuple) and d[0] is mys))
        for b in writes:
            if b.w is not None:
                deps.append(b.w)
            deps.extend(b.r)
        self._waits(eng, deps)
        self.cnt[eng] += 1
        c = self.cnt[eng]
        s = self.sem[eng]
        rec = _Rec()
        fn(rec)
        name, a, kw = rec.call
        self.prog[eng].append(lambda e, name=name, a=a, kw=kw, s=s: getattr(e, name)(*a, **kw).then_inc(s, 1))
        for b in reads:
            b.r.append((s, c))
        for b in writes:
            b.w = (s, c)
            b.r = []

    def dma(self, q, out, in_, reads=(), writes=(), **kw):
        tb = writes[0] if writes else reads[0]
        g = tb.grp
        if g is None:
            g = self.GP if q == "pool" else (self.GC if writes else self.GS[0])
        deps = []
        for b in reads:
            if b.w is not None:
                deps.append(b.w)
        for b in writes:
            if b.w is not None and b.w is not g:
                deps.append(b.w)
            deps.extend(b.r)
        self._waits(q, deps)
        if g.sealed and g.cnt > 0:
            self._waits(q, [(g.sem, g.cnt)])
        g.sealed = False
        g.cnt += 16
        s = g.sem
        self.prog[q].append(
            lambda e, out=out, in_=in_, s=s, kw=kw: e.dma_start(out=out, in_=in_, **kw).then_inc(s, 16))
        for b in reads:
            b.r.append(g)
        for b in writes:
            b.w = g
            b.r = []

    def barrier(self, engines=None):
        deps = [(self.sem[k], self.cnt[k]) for k in ("pe", "dve", "act", "pool") if self.cnt[k] > 0]
        deps += [g for g in self.groups if g.cnt > 0]
        for e in (engines or self.ENG):
            self._waits(e, deps)

    def run_block(self):
        nc = self.nc
        with nc.Block() as block:
            @block.sync
            def _(e):
                for t in self.prog["sp"]:
                    t(e)

            @block.tensor
            def _(e):
                for t in self.prog["pe"]:
                    t(e)

            @block.vector
            def _(e):
                for t in self.prog["dve"]:
                    t(e)

            @block.scalar
            def _(e):
                for t in self.prog["act"]:
                    t(e)

            @block.gpsimd
            def _(e):
                for t in self.prog["pool"]:
                    t(e)


_CONSTS = None


def _consts():
    global _CONSTS
    if _CONSTS is not None:
        return _CONSTS
    f = np.float32
    c = {}
    c["c_ident"] = np.eye(128, dtype=f)
    m = np.zeros((8, 128, 240), f)
    for a in range(8):
        for i in range(16):
            m[a, 16 * a + i, 112 + i] = 1.0
    c["c_masters"] = m
    ml = np.array(ML, np.float64)
    rows = np.concatenate([ml / (2 * np.pi), ml, 8.0 * (np.arange(64) + 1) / (2 * np.pi)])
    c["c_rows"] = rows.astype(f)[None, :]
    sg = np.zeros((128, 2), f)
    sg[:64, 0] = 1.0
    sg[64:, 0] = -1.0
    sg[:64, 1] = -1.0
    sg[64:, 1] = 1.0
    c["c_sgn"] = sg
    inv = (f(10000.0) ** (-(np.arange(64, dtype=f) / f(64.0)))).astype(f)
    pos = np.zeros((128, NT), f)
    for n in range(NTP):
        pos[:, n] = 128 * n + np.arange(128)
    pos[:64, 16] = PAST + (np.arange(64) % 4)
    ang = (pos[:, :, None] * inv[None, None, :]).astype(f).astype(np.float64)
    c["c_rope"] = np.stack([np.cos(ang), np.sin(ang), -np.sin(ang)]).astype(f)
    lg = np.log(np.array(GAM, np.float64))
    sc = 128.0 ** -0.5
    idx = np.arange(128)
    dm = np.zeros((128, 4, 128), np.float64)
    diff = idx[None, :] - idx[:, None]
    for h in range(4):
        dm[:, h, :] = np.where(diff >= 0, np.exp(np.maximum(diff, 0) * lg[h]), 0.0) * sc
    c["c_dmask_p"] = dm.reshape(128, 512).astype(f)
    ds_ = np.zeros((64, 4, 64), np.float64)
    r = np.arange(64)
    bb = r // 4
    tt = r % 4
    same = bb[:, None] == bb[None, :]
    dts = tt[None, :] - tt[:, None]
    for h in range(4):
        ds_[:, h, :] = np.where(same & (dts >= 0), np.exp(np.maximum(dts, 0) * lg[h]), 0.0) * sc
    c["c_dmask_s"] = ds_.reshape(64, 256).astype(f)
    xi_p = np.stack([np.exp((idx + 1.0) * lg[h]) * sc for h in range(4)])
    xi_s = np.stack([np.exp((tt + 1.0) * lg[h]) * sc for h in range(4)])
    c["c_xi"] = np.concatenate([xi_p.reshape(-1), xi_s.reshape(-1)]).astype(f)[None, :]
    zp = np.stack([np.exp((127.0 - idx) * lg[h]) for h in range(4)], axis=1)
    c["c_zeta_p"] = zp.astype(f)
    zs = np.zeros((64, 16, 4), np.float64)
    for h in range(4):
        for b in range(16):
            zs[:, b, h] = np.where(bb == b, np.exp((3.0 - tt) * lg[h]), 0.0)
    c["c_zs"] = zs.reshape(64, 64).astype(f)
    cm = np.zeros((16, 64), f)
    for b in range(16):
        cm[b, 4 * b:4 * b + 4] = 1.0
    c["c_cmask"] = cm.reshape(1, -1)
    _CONSTS = c
    return c


W_NAMES = ["g_mix", "w_in", "lam_re", "lam_im", "log_dt", "b_re", "b_im", "c_re", "c_im", "d_skip", "w_glu",
           "ret_gn", "w_out", "g_xattn", "g_mem", "w_mq", "w_mk", "w_mv", "w_mo", "g_mlp", "w_up", "w_down",
           "g_final"]
W_SHAPES = {"g_mix": [D], "w_in": [D, 2560], "lam_re": [G, 64], "lam_im": [G, 64], "log_dt": [G],
            "b_re": [G, 64, 16], "b_im": [G, 64, 16], "c_re": [G * 16, 64], "c_im": [G * 16, 64], "d_skip": [512],
            "w_glu": [512, 512], "ret_gn": [512], "w_out": [D, D], "g_xattn": [D], "g_mem": [D], "w_mq": [D, D],
            "w_mk": [D, D], "w_mv": [D, D], "w_mo": [D, D], "g_mlp": [D], "w_up": [D, DFF], "w_down": [DFF, D],
            "g_final": [D]}
IN_SHAPES = {"xp": [SEQ, D], "xs": [TS, D], "memp": [MEM, D], "s5r": [512, 64], "s5i": [512, 64],
             "sret": [16, 4, 128, 128], "ck": [16, MEM, D], "cv": [16, MEM, D]}
OUT_SHAPES = {"yp": [SEQ, D], "ys": [TS, D], "o_s5r_p": [G, 64], "o_s5i_p": [G, 64], "o_ret_p": [4, 128, 128],
              "o_mk": [MEM, D], "o_mv": [MEM, D], "o_s5r_s": [512, 64], "o_s5i_s": [512, 64],
              "o_ret_s": [16, 4, 128, 128]}


def build(stage=99, dbg=False):
    nc = bass.Bass("TRN2", target_bir_lowering=False)
    cst = _consts()
    I = {}
    for k, shp in list(IN_SHAPES.items()) + list(W_SHAPES.items()):
        I[k] = nc.dram_tensor(k, shp, F32, kind="ExternalInput").ap()
    for k, v in cst.items():
        I[k] = nc.dram_tensor(k, list(v.shape), F32, kind="ExternalInput").ap()
    O = {}
    for k, shp in OUT_SHAPES.items():
        O[k] = nc.dram_tensor(k, shp, F32, kind="ExternalOutput").ap()
    if dbg:
        O["dbg_ssm"] = nc.dram_tensor("dbg_ssm", [128, 4, NTOK], F32, kind="ExternalOutput").ap()
        O["dbg_x"] = nc.dram_tensor("dbg_x", [128, NT, D], F32, kind="ExternalOutput").ap()

    with ExitStack() as st:
        S = Sched(nc, st)

        def alloc(stack, name, shape, dt=F32):
            return stack.enter_context(nc.sbuf_tensor(name, shape, dt))

        def palloc(stack, name, shape, dt=F32):
            return stack.enter_context(nc.psum_tensor(name, shape, dt))

        def V(fn, r=(), w=()):
            S.op("dve", fn, reads=r, writes=w)

        def A(fn, r=(), w=()):
            S.op("act", fn, reads=r, writes=w)

        import os as _os0
        _nopool = _os0.environ.get("K_NOPOOL") == "1"

        def PL(fn, r=(), w=()):
            S.op("dve" if _nopool else "pool", fn, reads=r, writes=w)

        def T(fn, r=(), w=()):
            S.op("pe", fn, reads=r, writes=w)

        nck = nc.allow_non_contiguous_dma(reason="small param layout loads")
        nck.__enter__()

        identb = alloc(st, "identb", [128, 128], BF16)
        identf = alloc(st, "identf", [128, 128], F32)
        sgn = alloc(st, "sgn", [128, 2])
        epsc = alloc(st, "epsc", [128, 1])
        ssmT = alloc(st, "ssmT", [128, 4, NTOK], BF16)
        b_const = Buf("const")
        b_ssmT = [Buf("ssmT%d" % i) for i in range(5)]
        b_constp = Buf("constp")
        S.dma("pool", identb[:], I["c_ident"][:, :], writes=[b_constp])
        S.dma("sp", identf[:], I["c_ident"][:, :], writes=[b_const])
        S.dma("sp", sgn[:], I["c_sgn"][:, :], writes=[b_const])
        V(lambda e: e.memset(epsc[:], EPS), r=[b_constp], w=[b_const])
        PS = [palloc(st, "ps%d" % i, [128, 512], F32) for i in range(8)]
        bPS = [Buf("ps%d" % i, ps=True) for i in range(8)]

        def ps_bf(i):
            return PS[i][:].bitcast(BF16)

        def rmsnorm_hT(xt_ap, bx, npart, gcol, hT_ap, bhT, scr, col0, ph, ln=False, bg=None):
            sq, ss, rstd, hb, bscr, pbank = scr
            lim = ph if ph is not None else 99
            if lim == 0:
                V(lambda e: e.tensor_tensor(out=sq[:npart, :], in0=xt_ap, in1=xt_ap, op=ALU.mult), r=[bx], w=[bscr])
                return
            if lim == -1:
                A(lambda e: e.activation(out=sq[:npart, :], in_=xt_ap, func=AF.Square), r=[bx], w=[bscr])
                return
            A(lambda e: e.activation(out=sq[:npart, :], in_=xt_ap, func=AF.Square, accum_out=ss[:npart, :]),
              r=[bx], w=[bscr])
            if lim <= 1:
                return
            if ln:
                A(lambda e: e.activation(out=rstd[:npart, :], in_=ss[:npart, :], func=AF.Ln, scale=1.0 / D,
                                         bias=epsc[:npart, :]), r=[bscr, b_const], w=[bscr])
                A(lambda e: e.activation(out=rstd[:npart, :], in_=rstd[:npart, :], func=AF.Exp, scale=-0.5),
                  r=[bscr], w=[bscr])
            else:
                A(lambda e: e.activation(out=rstd[:npart, :], in_=ss[:npart, :], func=AF.Sqrt, scale=1.0 / D,
                                         bias=epsc[:npart, :]), r=[bscr, b_const], w=[bscr])
                V(lambda e: e.reciprocal(out=rstd[:npart, :], in_=rstd[:npart, :]), r=[bscr], w=[bscr])
            if lim <= 2:
                return
            V(lambda e: e.tensor_scalar(out=hb[:npart, :], in0=xt_ap, scalar1=rstd[:npart, :], scalar2=None,
                                        op0=ALU.mult), r=[bx, bscr], w=[bscr])
            if lim <= 3:
                return
            pv = ps_bf(pbank)
            for kt in range(8):
                T(lambda e, kt=kt: e.transpose(out=pv[:, kt * 128:kt * 128 + npart],
                                               in_=hb[:npart, kt * 128:(kt + 1) * 128],
                                               identity=identb[:npart, :npart]),
                  r=[bscr, b_const], w=[bPS[pbank]])
            V(lambda e: e.tensor_tensor(
                out=hT_ap[:, :, col0:col0 + npart],
                in0=pv.rearrange("p (k t) -> p k t", k=8)[:, :, 0:npart],
                in1=gcol.unsqueeze(2).to_broadcast([128, 8, npart]), op=ALU.mult),
              r=[bPS[pbank], b_const] + ([bg] if bg is not None else []), w=[bhT])

        def load_w_bf16(dst, bdst, src, kt_n, ncols, c0=0):
            for kt in range(kt_n):
                for cc in range(0, ncols, 1024):
                    w_ = min(1024, ncols - cc)
                    S.dma("pool", dst[:, kt, cc:cc + w_], src[kt * 128:(kt + 1) * 128, c0 + cc:c0 + cc + w_],
                          writes=[bdst])

        with ExitStack() as sa:
            Wt = alloc(sa, "Wt", [128, G, 128], BF16)
            Wst = alloc(sa, "Wst", [128, G, 128], BF16)
            Tt = alloc(sa, "Tt", [128, G, 128], BF16)
            Vt = alloc(sa, "Vt", [128, G, 128], BF16)
            COSR = alloc(sa, "COSR", [128, G, 64])
            SINR = alloc(sa, "SINR", [128, G, 64])
            masters = alloc(sa, "masters", [128, 8, 240], BF16)
            AR = alloc(sa, "AR", [128, G, K1])
            AI = alloc(sa, "AI", [128, G, K1])
            MAGJ = alloc(sa, "MAGJ", [128, G, K1])
            DS = alloc(sa, "DS", [128, G])
            gm = alloc(sa, "gm", [128, 8])
            winu = alloc(sa, "winu", [128, 8, 512], BF16)
            wglu = alloc(sa, "wglu", [128, 4, 512], BF16)
            b_tab = Buf("s5tab")
            b_winu = Buf("winu", S.GW[0])
            b_wglu = Buf("wglu", S.GW[1])
            b_tabp = Buf("s5tabp")
            S.dma("pool", masters[:], I["c_masters"].rearrange("a k j -> k a j"), writes=[b_tabp])
            S.dma("sp", gm[:], I["g_mix"].rearrange("(k p) -> p k", p=128), writes=[b_tab])
            for tau in range(8):
                S.dma("sp", DS[16 * tau:16 * tau + 16, :], I["d_skip"].rearrange("(g h) -> h g", h=16),
                      writes=[b_tab])
            load_w_bf16(winu, b_winu, I["w_in"], 8, 512, 0)
            load_w_bf16(wglu, b_wglu, I["w_glu"], 4, 512, 0)

            with ExitStack() as s0:
                rows = alloc(s0, "rows", [128, 2 * K1 + 64])
                LR = alloc(s0, "LR", [128, G])
                LI = alloc(s0, "LI", [128, G])
                DT = alloc(s0, "DT", [128, G])
                LRDT = alloc(s0, "LRDT", [128, G])
                LIDT = alloc(s0, "LIDT", [128, G])
                tA = alloc(s0, "tA", [128, G, 64])
                tB = alloc(s0, "tB", [128, G, 64])
                tC = alloc(s0, "tC", [128, G, 64])
                COSJ = alloc(s0, "COSJ", [128, G, K1])
                SINJ = alloc(s0, "SINJ", [128, G, K1])
                sm = alloc(s0, "sm", [128, 12, G])
                Br1 = alloc(s0, "Br1", [128, G, 16])
                Br2 = alloc(s0, "Br2", [128, G, 16])
                BB1 = alloc(s0, "BB1", [128, G, 16])
                BB2 = alloc(s0, "BB2", [128, G, 16])
                tb1 = alloc(s0, "tb1", [128, G, 16])
                big1 = alloc(s0, "big1", [128, G, 128])
                big2 = alloc(s0, "big2", [128, G, 128])
                WTpad = alloc(s0, "WTpad", [128, G, 256], BF16)
                WTs = alloc(s0, "WTs", [128, G, 128], BF16)
                CN1 = alloc(s0, "CN1", [128, 4, 128])
                CN2 = alloc(s0, "CN2", [128, 4, 128])
                CMa = alloc(s0, "CMa", [128, G, 16])
                CMb = alloc(s0, "CMb", [128, G, 16])
                CMab = alloc(s0, "CMab", [128, G, 16], BF16)
                b0 = Buf("p0in")
                bt = Buf("p0tmp")
                S.dma("sp", rows[:], I["c_rows"][0:1, :].partition_broadcast(128), writes=[b0])
                for hf in range(2):
                    S.dma("sp", LR[64 * hf:64 * hf + 64, :], I["lam_re"].rearrange("g p -> p g"), writes=[b0])
                    S.dma("sp", LI[64 * hf:64 * hf + 64, :], I["lam_im"].rearrange("g p -> p g"), writes=[b0])
                S.dma("sp", DT[:], I["log_dt"].rearrange("(o g) -> o g", o=1).partition_broadcast(128), writes=[b0])
                S.dma("sp", Br1[0:64], I["b_re"].rearrange("g p h -> p g h"), writes=[b0])
                S.dma("sp", Br1[64:128], I["b_im"].rearrange("g p h -> p g h"), writes=[b0])
                S.dma("sp", Br2[0:64], I["b_im"].rearrange("g p h -> p g h"), writes=[b0])
                S.dma("sp", Br2[64:128], I["b_re"].rearrange("g p h -> p g h"), writes=[b0])
                S.dma("sp", CN1[:, :, 0:64], I["c_re"].rearrange("(c r) p -> r c p", r=128), writes=[b0])
                S.dma("sp", CN1[:, :, 64:128], I["c_im"].rearrange("(c r) p -> r c p", r=128), writes=[b0])
                S.dma("sp", CN2[:, :, 0:64], I["c_im"].rearrange("(c r) p -> r c p", r=128), writes=[b0])
                S.dma("sp", CN2[:, :, 64:128], I["c_re"].rearrange("(c r) p -> r c p", r=128), writes=[b0])
                MT1 = rows[:, 0:K1]
                MLr = rows[:, K1:2 * K1]
                MRT = rows[:, 2 * K1:2 * K1 + 64]
                A(lambda e: e.activation(out=DT[:], in_=DT[:], func=AF.Exp), r=[b0], w=[b0])
                V(lambda e: e.tensor_tensor(out=LRDT[:], in0=LR[:], in1=DT[:], op=ALU.mult), r=[b0], w=[bt])
                V(lambda e: e.tensor_tensor(out=LIDT[:], in0=LI[:], in1=DT[:], op=ALU.mult), r=[b0], w=[bt])

                def trig(mt_ap, K, cos_out, sin_out):
                    shp = [128, G, K]
                    a_, b_, c_ = tA[:, :, 0:K], tB[:, :, 0:K], tC[:, :, 0:K]
                    V(lambda e: e.tensor_tensor(out=a_, in0=LIDT[:].unsqueeze(2).to_broadcast(shp),
                                                in1=mt_ap.unsqueeze(1).to_broadcast(shp), op=ALU.mult),
                      r=[bt, b0], w=[bt])
                    for (outp, off) in ((sin_out, 0.0), (cos_out, 0.25)):
                        if outp is None:
                            continue
                        V(lambda e, off=off: e.tensor_scalar(out=c_, in0=a_, scalar1=off, scalar2=None,
                                                             op0=ALU.add), r=[bt], w=[bt])
                        V(lambda e: e.tensor_scalar(out=b_, in0=c_, scalar1=MAGIC, scalar2=None, op0=ALU.add),
                          r=[bt], w=[bt])
                        V(lambda e: e.tensor_scalar(out=b_, in0=b_, scalar1=MAGIC, scalar2=None, op0=ALU.subtract),
                          r=[bt], w=[bt])
                        V(lambda e: e.tensor_tensor(out=c_, in0=c_, in1=b_, op=ALU.subtract), r=[bt], w=[bt])
                        A(lambda e, outp=outp: e.activation(out=outp, in_=c_, func=AF.Sin, scale=TWO_PI),
                          r=[bt], w=[b_tab])

                trig(MT1, K1, COSJ[:], SINJ[:])
                trig(MRT, 64, COSR[:], SINR[:])
                shpj = [128, G, K1]
                V(lambda e: e.tensor_tensor(out=MAGJ[:], in0=LRDT[:].unsqueeze(2).to_broadcast(shpj),
                                            in1=MLr.unsqueeze(1).to_broadcast(shpj), op=ALU.mult),
                  r=[bt, b0], w=[b_tab])
                A(lambda e: e.activation(out=MAGJ[:], in_=MAGJ[:], func=AF.Exp), r=[b_tab], w=[b_tab])
                V(lambda e: e.tensor_tensor(out=AR[:], in0=MAGJ[:], in1=COSJ[:], op=ALU.mult), r=[b_tab], w=[b_tab])
                V(lambda e: e.tensor_tensor(out=AI[:], in0=MAGJ[:], in1=SINJ[:], op=ALU.mult), r=[b_tab], w=[b_tab])
                em1, shalf, cm1, am1r, ai1, den, fr, fi, t0_, t1_ = [sm[:, i, :] for i in range(10)]
                x_ = LRDT[:]
                V(lambda e: e.tensor_scalar(out=em1, in0=x_, scalar1=0.2, scalar2=1.0, op0=ALU.mult, op1=ALU.add),
                  r=[bt], w=[bt])
                for cf in (0.25, 1.0 / 3.0, 0.5):
                    V(lambda e: e.tensor_tensor(out=em1, in0=em1, in1=x_, op=ALU.mult), r=[bt], w=[bt])
                    V(lambda e, cf=cf: e.tensor_scalar(out=em1, in0=em1, scalar1=cf, scalar2=1.0, op0=ALU.mult,
                                                       op1=ALU.add), r=[bt], w=[bt])
                V(lambda e: e.tensor_tensor(out=em1, in0=em1, in1=x_, op=ALU.mult), r=[bt], w=[bt])
                V(lambda e: e.tensor_copy(out=shalf, in_=SINJ[:, :, I_HALF]), r=[b_tab], w=[bt])
                V(lambda e: e.scalar_tensor_tensor(out=cm1, in0=shalf, scalar=-2.0, op0=ALU.mult, in1=shalf,
                                                   op1=ALU.mult), r=[bt], w=[bt])
                V(lambda e: e.tensor_tensor(out=am1r, in0=em1, in1=COSJ[:, :, I_A1], op=ALU.mult), r=[bt, b_tab], w=[bt])
                V(lambda e: e.tensor_tensor(out=am1r, in0=am1r, in1=cm1, op=ALU.add), r=[bt], w=[bt])
                V(lambda e: e.tensor_copy(out=ai1, in_=AI[:, :, I_A1]), r=[b_tab], w=[bt])
                V(lambda e: e.tensor_tensor(out=den, in0=LR[:], in1=LR[:], op=ALU.mult), r=[b0], w=[bt])
                V(lambda e: e.tensor_tensor(out=t0_, in0=LI[:], in1=LI[:], op=ALU.mult), r=[b0], w=[bt])
                V(lambda e: e.tensor_tensor(out=den, in0=den, in1=t0_, op=ALU.add), r=[bt], w=[bt])
                V(lambda e: e.reciprocal(out=den, in_=den), r=[bt], w=[bt])
                V(lambda e: e.tensor_tensor(out=fr, in0=am1r, in1=LR[:], op=ALU.mult), r=[bt, b0], w=[bt])
                V(lambda e: e.tensor_tensor(out=t0_, in0=ai1, in1=LI[:], op=ALU.mult), r=[bt, b0], w=[bt])
                V(lambda e: e.tensor_tensor(out=fr, in0=fr, in1=t0_, op=ALU.add), r=[bt], w=[bt])
                V(lambda e: e.tensor_tensor(out=fr, in0=fr, in1=den, op=ALU.mult), r=[bt], w=[bt])
                V(lambda e: e.tensor_tensor(out=fi, in0=ai1, in1=LR[:], op=ALU.mult), r=[bt, b0], w=[bt])
                V(lambda e: e.tensor_tensor(out=t0_, in0=am1r, in1=LI[:], op=ALU.mult), r=[bt, b0], w=[bt])
                V(lambda e: e.tensor_tensor(out=fi, in0=fi, in1=t0_, op=ALU.subtract), r=[bt], w=[bt])
                V(lambda e: e.tensor_tensor(out=fi, in0=fi, in1=den, op=ALU.mult), r=[bt], w=[bt])
                V(lambda e: e.tensor_scalar(out=Br2[:], in0=Br2[:], scalar1=sgn[:, 1:2], scalar2=None, op0=ALU.mult),
                  r=[b0, b_const], w=[b0])
                shb = [128, G, 16]
                frb = fr.unsqueeze(2).to_broadcast(shb)
                fib = fi.unsqueeze(2).to_broadcast(shb)
                V(lambda e: e.tensor_tensor(out=BB1[:], in0=Br1[:], in1=frb, op=ALU.mult), r=[b0, bt], w=[bt])
                V(lambda e: e.tensor_tensor(out=tb1[:], in0=Br2[:], in1=fib, op=ALU.mult), r=[b0, bt], w=[bt])
                V(lambda e: e.tensor_tensor(out=BB1[:], in0=BB1[:], in1=tb1[:], op=ALU.add), r=[bt], w=[bt])
                V(lambda e: e.tensor_tensor(out=BB2[:], in0=Br2[:], in1=frb, op=ALU.mult), r=[b0, bt], w=[bt])
                V(lambda e: e.tensor_tensor(out=tb1[:], in0=Br1[:], in1=fib, op=ALU.mult), r=[b0, bt], w=[bt])
                V(lambda e: e.tensor_tensor(out=BB2[:], in0=BB2[:], in1=tb1[:], op=ALU.subtract), r=[bt], w=[bt])
                sh4 = [128, G, 8, 16]
                arv = AR[:, :, 0:8].unsqueeze(3).to_broadcast(sh4)
                aiv = AI[:, :, 0:8].unsqueeze(3).to_broadcast(sh4)
                bb1 = BB1[:].unsqueeze(2).to_broadcast(sh4)
                bb2 = BB2[:].unsqueeze(2).to_broadcast(sh4)
                g1 = big1[:].rearrange("p g (s h) -> p g s h", s=8)
                g2 = big2[:].rearrange("p g (s h) -> p g s h", s=8)
                V(lambda e: e.memset(WTpad[:], 0.0), w=[bt])
                V(lambda e: e.tensor_tensor(out=g1, in0=arv, in1=bb1, op=ALU.mult), r=[b_tab, bt], w=[bt])
                V(lambda e: e.tensor_tensor(out=g2, in0=aiv, in1=bb2, op=ALU.mult), r=[b_tab, bt], w=[bt])
                V(lambda e: e.tensor_tensor(out=WTpad[:, :, 0:128], in0=big1[:], in1=big2[:], op=ALU.add),
                  r=[bt], w=[bt])
                V(lambda e: e.tensor_tensor(out=g1, in0=arv, in1=bb2, op=ALU.mult), r=[b_tab, bt], w=[bt])
                V(lambda e: e.tensor_tensor(out=g2, in0=aiv, in1=bb1, op=ALU.mult), r=[b_tab, bt], w=[bt])
                V(lambda e: e.tensor_tensor(out=WTs[:], in0=big1[:], in1=big2[:], op=ALU.subtract), r=[bt], w=[bt])
                for (src_fn, dstt) in ((lambda g: WTpad[:, g, 0:128], Wt), (lambda g: WTs[:, g, :], Wst)):
                    for gq in range(8):
                        bank = gq % 2
                        pv = ps_bf(bank)
                        for j in range(4):
                            g = gq * 4 + j
                            T(lambda e, g=g, j=j, pv=pv, src_fn=src_fn: e.transpose(
                                out=pv[:, j * 128:(j + 1) * 128], in_=src_fn(g), identity=identb[:]),
                              r=[bt, b_const], w=[bPS[bank]])
                        A(lambda e, gq=gq, pv=pv, dstt=dstt: e.copy(
                            out=dstt[:, gq * 4:gq * 4 + 4, :], in_=pv[:, 0:512].rearrange("p (j c) -> p j c", j=4)),
                          r=[bPS[bank]], w=[b_tab])
                for (CN, CM, col) in ((CN1, CMa, 0), (CN2, CMb, None)):
                    for c4 in range(4):
                        bank = 2 + (c4 % 2)
                        T(lambda e, CN=CN, c4=c4, bank=bank: e.transpose(out=PS[bank][:, 0:128], in_=CN[:, c4, :],
                                                                         identity=identf[:]),
                          r=[b0, b_const], w=[bPS[bank]])
                        if col is not None:
                            V(lambda e, CM=CM, c4=c4, bank=bank: e.tensor_scalar(
                                out=CM[:, c4 * 8:(c4 + 1) * 8, :],
                                in0=PS[bank][:, 0:128].rearrange("p (g h) -> p g h", g=8),
                                scalar1=sgn[:, 0:1], scalar2=None, op0=ALU.mult),
                              r=[bPS[bank], b_const], w=[bt])
                        else:
                            V(lambda e, CM=CM, c4=c4, bank=bank: e.tensor_scalar(
                                out=CM[:, c4 * 8:(c4 + 1) * 8, :],
                                in0=PS[bank][:, 0:128].rearrange("p (g h) -> p g h", g=8),
                                scalar1=-1.0, scalar2=None, op0=ALU.mult),
                              r=[bPS[bank]], w=[bt])
                V(lambda e: e.tensor_copy(out=CMab[:], in_=CMa[:]), r=[bt], w=[bt])
                afw = AR[:, :, 8:16].unsqueeze(3).to_broadcast(sh4)
                aifw = AI[:, :, 8:16].unsqueeze(3).to_broadcast(sh4)
                cma = CMa[:].unsqueeze(2).to_broadcast(sh4)
                cmb = CMb[:].unsqueeze(2).to_broadcast(sh4)
                V(lambda e: e.tensor_tensor(out=g1, in0=afw, in1=cma, op=ALU.mult), r=[b_tab, bt], w=[bt])
                V(lambda e: e.tensor_tensor(out=g2, in0=aifw, in1=cmb, op=ALU.mult), r=[b_tab, bt], w=[bt])
                V(lambda e: e.tensor_tensor(out=Vt[:], in0=big1[:], in1=big2[:], op=ALU.add), r=[bt], w=[b_tab])
                for gq in range(8):
                    bank = 4 + (gq % 2)
                    for j in range(4):
                        g = gq * 4 + j
                        for tau in range(8):
                            c0 = (7 - tau) * 16
                            T(lambda e, g=g, j=j, tau=tau, c0=c0, bank=bank: e.matmul(
                                PS[bank][:, j * 128 + tau * 16:j * 128 + tau * 16 + 16],
                                lhsT=WTpad[:, g, c0:c0 + 128], rhs=CMab[:, g, :], start=True, stop=True),
                              r=[bt], w=[bPS[bank]])
                    A(lambda e, gq=gq, bank=bank: e.copy(
                        out=Tt[:, gq * 4:gq * 4 + 4, :], in_=PS[bank][:].rearrange("p (j c) -> p j c", j=4)),
                      r=[bPS[bank]], w=[b_tab])
                S.barrier()
            xst = [alloc(sa, "xst%d" % i, [128, D]) for i in range(2)]
            bxst = [Buf("xst%d" % i, S.GL[i]) for i in range(2)]
            sq = alloc(sa, "sq", [128, D])
            ss = alloc(sa, "ss", [128, 1])
            rstd = alloc(sa, "rstd", [128, 1])
            hb = alloc(sa, "hb", [128, D], BF16)
            bscr = Buf("scrA")
            hT = alloc(sa, "hT", [128, 8, 512], BF16)
            bhT = Buf("hT")
            uT = alloc(sa, "uT", [128, 4, 512], BF16)
            buT = Buf("uT")
            U = alloc(sa, "U", [128, G, 64], BF16)
            bU = Buf("U")
            rr = alloc(sa, "rr", [128, G, 64])
            rs = alloc(sa, "rs", [128, G, 64])
            ww = alloc(sa, "ww", [128, G, 64])
            ws = alloc(sa, "ws", [128, G, 64])
            tmpr = alloc(sa, "tmpr", [128, 16, 64])
            b_r, b_rs, b_w, b_ws, b_tmpr = Buf("r"), Buf("rs"), Buf("w"), Buf("ws"), Buf("tmpr")
            Xb = alloc(sa, "Xb", [128, G, 65], BF16)
            bXb = Buf("Xb")
            Xc = alloc(sa, "Xc", [128, G])
            Xsc = alloc(sa, "Xsc", [128, G])
            ctmp = alloc(sa, "ctmp", [128, 2, G])
            bXc = Buf("Xc", S.GS[0])
            ytmp = alloc(sa, "ytmp", [128, 8, 64])
            bytmp = Buf("ytmp")
            Zt = alloc(sa, "Zt", [128, G, 64], BF16)
            bZ = Buf("Z")
            zT = alloc(sa, "zT", [128, 4, 512], BF16)
            bzT = Buf("zT")
            sig = alloc(sa, "sig", [128, 4, 512])
            bsig = Buf("sig")
            H0 = alloc(sa, "H0", [128, 512])
            H0s = alloc(sa, "H0s", [128, 512])
            hn = alloc(sa, "hn", [128, 4, 128])
            hn2 = alloc(sa, "hn2", [128, 4, 128])
            Hp = alloc(sa, "Hp", [128, G, 16])
            Xf = alloc(sa, "Xf", [128, G, 16])
            xo = alloc(sa, "xo", [128, 4, 128])
            bH = Buf("H0")
            bxo = Buf("xo", S.GS[1])
            V(lambda e: e.memset(Xc[:], 0.0), r=[b_tabp], w=[bXc, b_tab])
            V(lambda e: e.memset(Xsc[:], 0.0), w=[bXc])
            V(lambda e: e.memset(Xb[:], 0.0), w=[bXb])

            blocks = [(i * 512, 512, False) for i in range(4)] + [(SEQ, TS, True)]
            if _os0.environ.get("K1A") == "0":
                blocks = []
            for bi, (t0, n, is_s) in enumerate(blocks):
                nch = n // 8 if not is_s else 16
                ntile = (n + 127) // 128
                for ti in range(ntile):
                    npart = min(128, n - ti * 128)
                    slot = (bi * 4 + ti) % 2
                    src = I["xs"][:, :] if is_s else I["xp"][t0 + ti * 128:t0 + ti * 128 + 128, :]
                    S.dma("sp", xst[slot][:npart, :], src, writes=[bxst[slot]])
                    rmsnorm_hT(xst[slot][:npart, :], bxst[slot], npart, gm[:], hT, bhT,
                               (sq, ss, rstd, hb, bscr, 7), ti * 128, None, bg=b_tab)
                for ct in range(4):
                    bank = ct
                    for kt in range(8):
                        T(lambda e, ct=ct, kt=kt, bank=bank: e.matmul(
                            PS[bank][:, 0:n], lhsT=winu[:, kt, ct * 128:(ct + 1) * 128], rhs=hT[:, kt, 0:n],
                            start=(kt == 0), stop=(kt == 7)), r=[b_winu, bhT], w=[bPS[bank]])
                    A(lambda e, ct=ct, bank=bank: e.copy(out=uT[:, ct, 0:n], in_=PS[bank][:, 0:n]),
                      r=[bPS[bank]], w=[buT])
                for gq in range(4):
                    bank = 4 + (gq % 2)
                    for j in range(8):
                        g = gq * 8 + j
                        ct, gl = g // 8, g % 8
                        if not is_s:
                            uv = uT[:, ct, 0:n].rearrange("p (c s) -> p s c", s=8)
                            sig_list = list(range(8))
                        else:
                            uv = uT[:, ct, 0:n].rearrange("p (b t) -> p t b", t=4)
                            sig_list = [4, 5, 6, 7]
                        for si, sg_ in enumerate(sig_list):
                            rhs = uv[:, sg_ if not is_s else si, :]
                            T(lambda e, j=j, gl=gl, sg_=sg_, rhs=rhs, si=si, bank=bank, L=len(sig_list): e.matmul(
                                PS[bank][:, j * 64:j * 64 + nch],
                                lhsT=masters[:, gl, 112 - 16 * sg_:240 - 16 * sg_], rhs=rhs,
                                start=(si == 0), stop=(si == L - 1)),
                              r=[b_tab, buT], w=[bPS[bank]])
                    A(lambda e, gq=gq, bank=bank: e.copy(
                        out=U[:, gq * 8:gq * 8 + 8, 0:nch],
                        in_=PS[bank][:].rearrange("p (j c) -> p j c", j=8)[:, :, 0:nch]),
                      r=[bPS[bank]], w=[bU])
                if not is_s:
                    for hf in range(2):
                        for j in range(16):
                            g = hf * 16 + j
                            for (wt, bk) in ((Wt, 0), (Wst, 2)):
                                bank = bk + j // 8
                                T(lambda e, g=g, j=j, wt=wt, bank=bank: e.matmul(
                                    PS[bank][:, (j % 8) * 64:(j % 8) * 64 + 64], lhsT=wt[:, g, :], rhs=U[:, g, :],
                                    start=True, stop=True), r=[b_tab, bU], w=[bPS[bank]])
                        for q in range(2):
                            gs = slice(hf * 16 + q * 8, hf * 16 + q * 8 + 8)
                            Sv = PS[q][:].rearrange("p (j c) -> p j c", j=8)
                            Ssv = PS[2 + q][:].rearrange("p (j c) -> p j c", j=8)
                            tm = tmpr[:, q * 8:q * 8 + 8, :]
                            V(lambda e, gs=gs, Sv=Sv: e.tensor_tensor(out=rr[:, gs, :], in0=Sv, in1=COSR[:, gs, :],
                                                                     op=ALU.mult), r=[bPS[q], b_tab], w=[b_r])
                            V(lambda e, gs=gs, Ssv=Ssv, tm=tm: e.tensor_tensor(out=tm, in0=Ssv, in1=SINR[:, gs, :],
                                                                              op=ALU.mult),
                              r=[bPS[2 + q], b_tab], w=[b_tmpr])
                            V(lambda e, gs=gs, tm=tm: e.tensor_tensor(out=rr[:, gs, :], in0=rr[:, gs, :], in1=tm,
                                                                     op=ALU.subtract), r=[b_r, b_tmpr], w=[b_r])
                            V(lambda e, gs=gs, Ssv=Ssv: e.tensor_tensor(out=rs[:, gs, :], in0=Ssv, in1=COSR[:, gs, :],
                                                                       op=ALU.mult), r=[bPS[2 + q], b_tab], w=[b_rs])
                            V(lambda e, gs=gs, Sv=Sv, tm=tm: e.tensor_tensor(out=tm, in0=Sv, in1=SINR[:, gs, :],
                                                                            op=ALU.mult),
                              r=[bPS[q], b_tab], w=[b_tmpr])
                            V(lambda e, gs=gs, tm=tm: e.tensor_tensor(out=rs[:, gs, :], in0=rs[:, gs, :], in1=tm,
                                                                     op=ALU.add), r=[b_rs, b_tmpr], w=[b_rs])
                    for g in range(G):
                        rho = MAGJ[:, g, I_A8:I_A8 + 1].to_broadcast([128, 64])
                        V(lambda e, g=g, rho=rho: e.tensor_tensor_scan(
                            out=ww[:, g, :], data0=rho, data1=rr[:, g, :], initial=Xc[:, g:g + 1], op0=ALU.mult,
                            op1=ALU.add), r=[b_r, b_tab, bXc], w=[b_w])
                        V(lambda e, g=g, rho=rho: e.tensor_tensor_scan(
                            out=ws[:, g, :], data0=rho, data1=rs[:, g, :], initial=Xsc[:, g:g + 1], op0=ALU.mult,
                            op1=ALU.add), r=[b_rs, b_tab, bXc], w=[b_ws])
                    ce, se_ = COSR[:, :, 63], SINR[:, :, 63]
                    we, wse = ww[:, :, 63], ws[:, :, 63]
                    V(lambda e: e.tensor_tensor(out=ctmp[:, 0, :], in0=ce, in1=we, op=ALU.mult), r=[b_w, b_tab], w=[bscr])
                    V(lambda e: e.tensor_tensor(out=ctmp[:, 1, :], in0=se_, in1=wse, op=ALU.mult), r=[b_ws, b_tab], w=[bscr])
                    V(lambda e: e.tensor_tensor(out=Xc[:], in0=ctmp[:, 0, :], in1=ctmp[:, 1, :], op=ALU.add),
                      r=[bscr], w=[bXc])
                    V(lambda e: e.tensor_tensor(out=ctmp[:, 0, :], in0=ce, in1=wse, op=ALU.mult), r=[b_ws, b_tab], w=[bscr])
                    V(lambda e: e.tensor_tensor(out=ctmp[:, 1, :], in0=se_, in1=we, op=ALU.mult), r=[b_w, b_tab], w=[bscr])
                    V(lambda e: e.tensor_tensor(out=Xsc[:], in0=ctmp[:, 0, :], in1=ctmp[:, 1, :], op=ALU.subtract),
                      r=[bscr], w=[bXc])
                    if bi > 0:
                        V(lambda e: e.tensor_copy(out=Xb[:, :, 0], in_=Xb[:, :, 64]), r=[bXb], w=[bXb])
                    PL(lambda e: e.tensor_tensor(out=ww[:], in0=ww[:], in1=COSR[:], op=ALU.mult), r=[b_w, b_tab, bXc],
                       w=[b_w])
                    PL(lambda e: e.tensor_tensor(out=ws[:], in0=ws[:], in1=SINR[:], op=ALU.mult), r=[b_ws, b_tab, bXc],
                       w=[b_ws])
                    PL(lambda e: e.tensor_tensor(out=Xb[:, :, 1:65], in0=ww[:], in1=ws[:], op=ALU.add),
                       r=[b_w, b_ws], w=[bXb])
                    xprev = lambda g: Xb[:, g, 0:64]
                    bXprev = bXb
                    if bi == 3:
                        S.dma("sp", O["o_s5r_p"].rearrange("g p -> p g"), Xc[0:64, :], reads=[bXc])
                        S.dma("sp", O["o_s5i_p"].rearrange("g p -> p g"), Xc[64:128, :], reads=[bXc])
                else:
                    S.dma("sp", hn[:, :, 0:64], I["s5r"].rearrange("(j r) p -> r j p", r=128), writes=[bH])
                    S.dma("sp", hn[:, :, 64:128], I["s5i"].rearrange("(j r) p -> r j p", r=128), writes=[bH])
                    S.dma("sp", hn2[:, :, 0:64], I["s5i"].rearrange("(j r) p -> r j p", r=128), writes=[bH])
                    S.dma("sp", hn2[:, :, 64:128], I["s5r"].rearrange("(j r) p -> r j p", r=128), writes=[bH])
                    for (src_, dst_, bank) in ((hn, H0, 0), (hn2, H0s, 1)):
                        for j in range(4):
                            T(lambda e, src_=src_, j=j, bank=bank: e.transpose(
                                out=PS[bank][:, j * 128:(j + 1) * 128], in_=src_[:, j, :], identity=identf[:]),
                              r=[bH, b_const], w=[bPS[bank]])
                        V(lambda e, dst_=dst_, bank=bank: e.tensor_copy(out=dst_[:], in_=PS[bank][:]),
                          r=[bPS[bank]], w=[bH])
                    V(lambda e: e.tensor_scalar(out=H0s[0:64, :], in0=H0s[0:64, :], scalar1=-1.0, scalar2=None,
                                                op0=ALU.mult), r=[bH], w=[bH])
                    shs = [128, G, 16]
                    h0v = H0[:].rearrange("p (b g) -> p g b", g=G)
                    h0sv = H0s[:].rearrange("p (b g) -> p g b", g=G)

                    def abc(tab, idx):
                        return tab[:, :, idx].unsqueeze(2).to_broadcast(shs)
                    V(lambda e: e.tensor_tensor(out=Xf[:], in0=h0v, in1=abc(AR, I_AM4), op=ALU.mult), r=[bH, b_tab], w=[bxo])
                    V(lambda e: e.tensor_tensor(out=Hp[:], in0=h0sv, in1=abc(AI, I_AM4), op=ALU.mult), r=[bH, b_tab], w=[bxo])
                    V(lambda e: e.tensor_tensor(out=Xb[:, :, 0:16], in0=Xf[:], in1=Hp[:], op=ALU.add), r=[bxo], w=[bXb])
                    V(lambda e: e.tensor_tensor(out=Xf[:], in0=h0v, in1=abc(AR, I_A4), op=ALU.mult), r=[bH, b_tab], w=[bxo])
                    V(lambda e: e.tensor_tensor(out=Hp[:], in0=h0sv, in1=abc(AI, I_A4), op=ALU.mult), r=[bH, b_tab], w=[bxo])
                    V(lambda e: e.tensor_tensor(out=Xf[:], in0=Xf[:], in1=Hp[:], op=ALU.add), r=[bxo], w=[bxo])
                    for q in range(4):
                        bank = q % 2
                        for j in range(8):
                            g = q * 8 + j
                            T(lambda e, g=g, j=j, bank=bank: e.matmul(
                                PS[bank][:, j * 64:j * 64 + 16], lhsT=Wt[:, g, :], rhs=U[:, g, 0:16],
                                start=True, stop=True), r=[b_tab, bU], w=[bPS[bank]])
                        V(lambda e, q=q, bank=bank: e.tensor_tensor(
                            out=Xf[:, q * 8:q * 8 + 8, :], in0=Xf[:, q * 8:q * 8 + 8, :],
                            in1=PS[bank][:].rearrange("p (j c) -> p j c", j=8)[:, :, 0:16], op=ALU.add),
                          r=[bxo, bPS[bank]], w=[bxo])
                    Xf2 = Xf[:].rearrange("p g b -> p (g b)")
                    for j in range(4):
                        T(lambda e, j=j: e.transpose(out=PS[2][:, j * 128:(j + 1) * 128],
                                                     in_=Xf2[:, j * 128:(j + 1) * 128], identity=identf[:]),
                          r=[bxo, b_const], w=[bPS[2]])
                    V(lambda e: e.tensor_copy(out=xo[:], in_=PS[2][:].rearrange("p (j c) -> p j c", j=4)),
                      r=[bPS[2]], w=[bxo])
                    for j in range(4):
                        for gl in range(8):
                            for (nm, c0) in (("o_s5r_s", 0), ("o_s5i_s", 64)):
                                S.dma("sp", O[nm].rearrange("(b g) p -> g b p", g=G)[8 * j + gl],
                                      xo[gl * 16:gl * 16 + 16, j, c0:c0 + 64], reads=[bxo])
                    xprev = lambda g: Xb[:, g, 0:16]
                    bXprev = bXb
                for gq in range(4):
                    bank = 6 + (gq % 2)
                    for j in range(8):
                        g = gq * 8 + j
                        T(lambda e, g=g, j=j, bank=bank: e.matmul(
                            PS[bank][:, j * 64:j * 64 + nch], lhsT=Tt[:, g, :], rhs=U[:, g, 0:nch],
                            start=True, stop=False), r=[b_tab, bU], w=[bPS[bank]])
                        T(lambda e, g=g, j=j, bank=bank: e.matmul(
                            PS[bank][:, j * 64:j * 64 + nch], lhsT=Vt[:, g, :], rhs=xprev(g)[:, 0:nch],
                            start=False, stop=True), r=[b_tab, bXprev], w=[bPS[bank]])
                    gs = slice(gq * 8, gq * 8 + 8)
                    yv = PS[bank][:].rearrange("p (j c) -> p j c", j=8)[:, :, 0:nch]
                    V(lambda e, gs=gs: e.tensor_tensor(out=ytmp[:, :, 0:nch], in0=U[:, gs, 0:nch],
                                                       in1=DS[:, gs].unsqueeze(2).to_broadcast([128, 8, nch]),
                                                       op=ALU.mult), r=[bU, b_tab], w=[bytmp])
                    V(lambda e, yv=yv: e.tensor_tensor(out=ytmp[:, :, 0:nch], in0=yv, in1=ytmp[:, :, 0:nch],
                                                       op=ALU.add), r=[bPS[bank], bytmp], w=[bytmp])
                    A(lambda e, gs=gs: e.activation(out=Zt[:, gs, 0:nch], in_=ytmp[:, :, 0:nch],
                                                    func=AF.Gelu_apprx_tanh), r=[bytmp], w=[bZ])
                for ct in range(4):
                    bank = ct % 2
                    taus = list(range(8)) if not is_s else [4, 5, 6, 7]
                    for ti_, tau in enumerate(taus):
                        for gl in range(8):
                            g = ct * 8 + gl
                            T(lambda e, g=g, gl=gl, tau=tau, ti_=ti_, bank=bank: e.matmul(
                                PS[bank][:, ti_ * 64:ti_ * 64 + nch],
                                lhsT=masters[:, tau, 112 - 16 * gl:240 - 16 * gl], rhs=Zt[:, g, 0:nch],
                                start=(gl == 0), stop=(gl == 7)), r=[b_tab, bZ], w=[bPS[bank]])
                    if not is_s:
                        A(lambda e, ct=ct, bank=bank: e.copy(
                            out=zT[:, ct, 0:n].rearrange("p (c t) -> p t c", t=8),
                            in_=PS[bank][:].rearrange("p (t c) -> p t c", t=8)), r=[bPS[bank]], w=[bzT])
                    else:
                        A(lambda e, ct=ct, bank=bank: e.copy(
                            out=zT[:, ct, 0:n].rearrange("p (b t) -> p t b", t=4),
                            in_=PS[bank][:].rearrange("p (t c) -> p t c", t=8)[:, 0:4, 0:16]),
                          r=[bPS[bank]], w=[bzT])
                for ct in range(4):
                    bank = 2 + (ct % 2)
                    for kt in range(4):
                        T(lambda e, ct=ct, kt=kt, bank=bank: e.matmul(
                            PS[bank][:, 0:n], lhsT=wglu[:, kt, ct * 128:(ct + 1) * 128], rhs=zT[:, kt, 0:n],
                            start=(kt == 0), stop=(kt == 3)), r=[b_wglu, bzT], w=[bPS[bank]])
                    A(lambda e, ct=ct, bank=bank: e.activation(out=sig[:, ct, 0:n], in_=PS[bank][:, 0:n],
                                                               func=AF.Sigmoid), r=[bPS[bank]], w=[bsig])
                V(lambda e: e.tensor_tensor(out=ssmT[:, :, t0:t0 + n], in0=zT[:, :, 0:n], in1=sig[:, :, 0:n],
                                            op=ALU.mult), r=[bzT, bsig], w=[b_ssmT[bi]])
            S.barrier()
        if dbg:
            with ExitStack() as sd:
                dtmp = alloc(sd, "dtmp", [128, 4, NTOK])
                bd = Buf("dtmp", S.GS[2])
                V(lambda e: e.tensor_copy(out=dtmp[:], in_=ssmT[:]), r=b_ssmT, w=[bd])
                S.dma("sp", O["dbg_ssm"][:, :, :], dtmp[:], reads=[bd])
                S.barrier()
        if stage <= 1:
            S.barrier()
            S.run_block()
            nck.__exit__(None, None, None)
            return nc

        with ExitStack() as sbx:
            x = alloc(sbx, "x", [128, NT, D])
            bx = [Buf("x%d" % n, S.GX) for n in range(NT)]
            for n in range(NTP):
                S.dma("sp", x[:, n, :], I["xp"][n * 128:(n + 1) * 128, :], writes=[bx[n]])
            S.dma("sp", x[0:TS, 16, :], I["xs"][:, :], writes=[bx[16]])
            sq = alloc(sbx, "sqB", [128, D])
            ss = alloc(sbx, "ssB", [128, 1])
            rstd = alloc(sbx, "rstdB", [128, 1])
            hb = alloc(sbx, "hbB", [128, D], BF16)
            bscr = Buf("scrB")
            hT1 = alloc(sbx, "hT1", [128, 8, 128], BF16)
            bhT1 = Buf("hT1")
            scrB = (sq, ss, rstd, hb, bscr, 7)

            def resid_add(n, npart, half, bank):
                V(lambda e: e.tensor_tensor(out=x[:npart, n, half * 512:(half + 1) * 512], in0=PS[bank][:npart, :],
                                            in1=x[:npart, n, half * 512:(half + 1) * 512], op=ALU.add),
                  r=[bPS[bank], bx[n]], w=[bx[n]])

            with ExitStack() as s1:
                wq = alloc(s1, "wqkvg", [128, 8, 2048], BF16)
                wout = alloc(s1, "wout", [128, 8, D], BF16)
                b_wq, b_wout = Buf("wq", S.GW[2]), Buf("wout", S.GW[3])
                load_w_bf16(wq, b_wq, I["w_in"], 8, 2048, 512)
                load_w_bf16(wout, b_wout, I["w_out"], 8, D, 0)
                gm2 = alloc(s1, "gm2", [128, 8])
                gn = alloc(s1, "gn", [128, 4])
                rope = alloc(s1, "rope", [128, 3, NT, 64])
                dmp = alloc(s1, "dmp", [128, 512])
                dms = alloc(s1, "dms", [64, 256])
                xi = alloc(s1, "xi", [128, 768])
                zetap = alloc(s1, "zetap", [128, 4])
                zs = alloc(s1, "zs", [64, 64])
                cmask = alloc(s1, "cmask", [128, 16 * 64])
                b_t1 = Buf("tab1")
                S.dma("sp", gm2[:], I["g_mix"].rearrange("(k p) -> p k", p=128), writes=[b_t1])
                S.dma("sp", gn[:], I["ret_gn"].rearrange("(k p) -> p k", p=128), writes=[b_t1])
                for a_ in range(3):
                    S.dma("sp", rope[:, a_, :, :], I["c_rope"][a_], writes=[b_t1])
                S.dma("sp", dmp[:], I["c_dmask_p"][:, :], writes=[b_t1])
                S.dma("sp", dms[:], I["c_dmask_s"][:, :], writes=[b_t1])
                S.dma("sp", xi[:], I["c_xi"][0:1, :].partition_broadcast(128), writes=[b_t1])
                S.dma("sp", zetap[:], I["c_zeta_p"][:, :], writes=[b_t1])
                S.dma("sp", zs[:], I["c_zs"][:, :], writes=[b_t1])
                S.dma("sp", cmask[:], I["c_cmask"][0:1, :].partition_broadcast(128), writes=[b_t1])
                for k in range(4):
                    V(lambda e: e.tensor_scalar(out=wout[:, 4 + k, :], in0=wout[:, 4 + k, :], scalar1=gn[:, k:k + 1],
                                                scalar2=None, op0=ALU.mult), r=[b_wout, b_t1], w=[b_wout])
                t1q = alloc(s1, "t1q", [128, 512])
                t2q = alloc(s1, "t2q", [128, 512])
                t1k = alloc(s1, "t1k", [128, 512])
                t2k = alloc(s1, "t2k", [128, 512])
                qr = alloc(s1, "qr", [128, 512], BF16)
                kr = alloc(s1, "kr", [128, 512], BF16)
                qT = alloc(s1, "qT", [128, 4, 128], BF16)
                qxT = alloc(s1, "qxT", [128, 4, 128], BF16)
                kT = alloc(s1, "kT", [128, 4, 128], BF16)
                vb = alloc(s1, "vb", [128, 512], BF16)
                vz = alloc(s1, "vz", [128, 512], BF16)
                sg_ = alloc(s1, "sgl", [128, 512])
                sT = alloc(s1, "sT", [128, 4, 128], BF16)
                Sst = alloc(s1, "Sst", [128, 4, 128])
                Sbf = alloc(s1, "Sbf", [128, 4, 128], BF16)
                stats = alloc(s1, "stats", [128, 4, 6])
                mv = alloc(s1, "mv", [128, 4, 2])
                rs4 = alloc(s1, "rs4", [128, 4])
                nb4 = alloc(s1, "nb4", [128, 4])
                on = alloc(s1, "on", [128, 512])
                ret = alloc(s1, "ret", [128, 512], BF16)
                retT = alloc(s1, "retT", [128, 4, 128], BF16)
                S0 = [alloc(s1, "S0_%d" % i, [128, 4, 128]) for i in range(2)]
                S0b = [alloc(s1, "S0b_%d" % i, [128, 4, 128], BF16) for i in range(2)]
                qxm = [alloc(s1, "qxm_%d" % i, [128, 4, 64], BF16) for i in range(2)]
                vzb = [alloc(s1, "vzb_%d" % i, [64, 512], BF16) for i in range(2)]
                Sn = [alloc(s1, "Sn_%d" % i, [128, 4, 128]) for i in range(2)]
                bS0 = [Buf("S0_%d" % i, S.GL[i]) for i in range(2)]
                bS0b = [Buf("S0b_%d" % i) for i in range(2)]
                bqxm = [Buf("qxm%d" % i) for i in range(2)]
                bvzb = [Buf("vzb%d" % i) for i in range(2)]
                bSn = [Buf("Sn%d" % i, S.GS[i]) for i in range(2)]
                (b_t1q, b_t2q, b_t1k, b_t2k, b_qr, b_kr, b_qT, b_qxT, b_kT, b_vb, b_vz, b_sg, b_sT, b_Sst, b_Sbf,
                 b_st, b_on, b_ret, b_retT) = [Buf("p1b%d" % i) for i in range(19)]
                b_Sst.grp = S.GS[2]
                V(lambda e: e.memset(Sst[:], 0.0), w=[b_Sst])
                GC_P = [float(g ** 128) for g in GAM]
                GC_S = [float(g ** 4) for g in GAM]

                import os as _os
                _tl = _os.environ.get("K_TILES")
                _tiles = [int(v) for v in _tl.split(",") if int(v) >= 0] if _tl else list(range(NT))
                _step = int(_os.environ.get("K_STEP", "99"))
                hT1s = [hT1, alloc(s1, "hT1c", [128, 8, 128], BF16)]
                bhT1s = [bhT1, Buf("hT1c")]

                def p1b_norm(n):
                    npt_ = TS if n == 16 else 128
                    rmsnorm_hT(x[:npt_, n, :], bx[n], npt_, gm2[:], hT1s[n % 2], bhT1s[n % 2], scrB, 0, None, bg=b_t1)
                if _tiles:
                    p1b_norm(_tiles[0])
                for ti_, n in enumerate(_tiles):
                    is_s = (n == 16)
                    npt = TS if is_s else 128
                    tok0 = n * 128
                    hT1, bhT1 = hT1s[n % 2], bhT1s[n % 2]
                    pob = [4, 6, 7, 1] if is_s else [4, 4, 4, 4]

                    def po(h):
                        if is_s:
                            return PS[pob[h]][:npt, 0:128]
                        return PS[4][:npt, h * 128:(h + 1) * 128]
                    for c in range(4):
                        for kt in range(8):
                            T(lambda e: e.matmul(PS[c][:npt, :], lhsT=hT1[:, kt, 0:npt],
                                                 rhs=wq[:, kt, c * 512:(c + 1) * 512], start=(kt == 0), stop=(kt == 7)),
                              r=[bhT1, b_wq], w=[bPS[c]])
                    if _step <= 1:
                        continue
                    for (bank, t1_, t2_, out_, bt1, bt2, bo) in ((0, t1q, t2q, qr, b_t1q, b_t2q, b_qr),
                                                               (1, t1k, t2k, kr, b_t1k, b_t2k, b_kr)):
                        pv4 = PS[bank][:npt, :].rearrange("p (h a j) -> p h a j", h=4, a=2)
                        t1v = t1_[:npt, :].rearrange("p (h a j) -> p h a j", h=4, a=2)
                        t2v = t2_[:npt, :].rearrange("p (h a j) -> p h a j", h=4, a=2)
                        cosb = rope[:npt, 0, n, :].unsqueeze(1).unsqueeze(1).to_broadcast([npt, 4, 2, 64])
                        sinb = rope[:npt, 1, n, :].unsqueeze(1).to_broadcast([npt, 4, 64])
                        nsinb = rope[:npt, 2, n, :].unsqueeze(1).to_broadcast([npt, 4, 64])
                        V(lambda e: e.tensor_tensor(out=t1v, in0=pv4, in1=cosb, op=ALU.mult), r=[bPS[bank], b_t1], w=[bt1])
                        V(lambda e: e.tensor_tensor(out=t2v[:, :, 0, :], in0=pv4[:, :, 1, :], in1=nsinb, op=ALU.mult),
                          r=[bPS[bank], b_t1], w=[bt2])
                        V(lambda e: e.tensor_tensor(out=t2v[:, :, 1, :], in0=pv4[:, :, 0, :], in1=sinb, op=ALU.mult),
                          r=[bPS[bank], b_t1], w=[bt2])
                        V(lambda e: e.tensor_tensor(out=out_[:npt, :], in0=t1_[:npt, :], in1=t2_[:npt, :], op=ALU.add),
                           r=[bt1, bt2], w=[bo])
                    if _step <= 2:
                        continue
                    A(lambda e: e.copy(out=vb[:npt, :], in_=PS[2][:npt, :]), r=[bPS[2]], w=[b_vb])
                    if not is_s:
                        V(lambda e: e.tensor_tensor(
                            out=vz[:, :].rearrange("p (h e) -> p h e", h=4),
                            in0=PS[2][:, :].rearrange("p (h e) -> p h e", h=4),
                            in1=zetap[:, :].unsqueeze(2).to_broadcast([128, 4, 128]), op=ALU.mult),
                          r=[bPS[2], b_t1], w=[b_vz])
                    A(lambda e: e.activation(out=sg_[:npt, :], in_=PS[3][:npt, :], func=AF.Silu), r=[bPS[3]], w=[b_sg])
                    pv4b = ps_bf(4)
                    pv5b = ps_bf(5)
                    for h in range(4):
                        T(lambda e: e.transpose(out=pv4b[:, h * 128:h * 128 + npt], in_=qr[:npt, h * 128:(h + 1) * 128],
                                                identity=identb[:npt, :npt]), r=[b_qr, b_const], w=[bPS[4]])
                    for h in range(4):
                        T(lambda e: e.transpose(out=pv5b[:, h * 128:h * 128 + npt], in_=kr[:npt, h * 128:(h + 1) * 128],
                                                identity=identb[:npt, :npt]), r=[b_kr, b_const], w=[bPS[5]])
                    q4 = pv4b[:, 0:512].rearrange("p (h t) -> p h t", h=4)[:, :, 0:npt]
                    k4 = pv5b[:, 0:512].rearrange("p (h t) -> p h t", h=4)[:, :, 0:npt]
                    A(lambda e: e.copy(out=qT[:, :, 0:npt], in_=q4), r=[bPS[4]], w=[b_qT])
                    xiv = (xi[:, 0:512].rearrange("p (h t) -> p h t", h=4) if not is_s
                           else xi[:, 512:768].rearrange("p (h t) -> p h t", h=4))
                    V(lambda e: e.tensor_tensor(out=qxT[:, :, 0:npt], in0=q4, in1=xiv, op=ALU.mult),
                      r=[bPS[4], b_t1], w=[b_qxT])
                    A(lambda e: e.copy(out=kT[:, :, 0:npt], in_=k4), r=[bPS[5]], w=[b_kT])
                    if _step <= 3:
                        continue
                    for h in range(4):
                        T(lambda e: e.matmul(PS[6][:npt, h * 128:h * 128 + npt], lhsT=kT[:, h, 0:npt], rhs=qT[:, h, 0:npt],
                                             start=True, stop=True), r=[b_kT, b_qT], w=[bPS[6]])
                    dmv = (dmp[:, :].rearrange("p (h t) -> p h t", h=4) if not is_s
                           else dms[:, :].rearrange("p (h t) -> p h t", h=4))
                    V(lambda e: e.tensor_tensor(out=sT[:npt, :, 0:npt],
                                                in0=PS[6][:npt, :].rearrange("p (h t) -> p h t", h=4)[:, :, 0:npt],
                                                in1=dmv, op=ALU.mult), r=[bPS[6], b_t1], w=[b_sT])
                    if _step <= 4:
                        continue
                    if ti_ + 1 < len(_tiles):
                        p1b_norm(_tiles[ti_ + 1])
                    for h in range(4):
                        only = (n == 0)
                        T(lambda e: e.matmul(po(h), lhsT=sT[:npt, h, 0:npt],
                                             rhs=vb[:npt, h * 128:(h + 1) * 128], start=True, stop=only),
                          r=[b_sT, b_vb], w=[bPS[pob[h]]])
                        if (not is_s) and n > 0:
                            T(lambda e: e.matmul(po(h), lhsT=qxT[:, h, 0:npt],
                                                 rhs=Sbf[:, h, :], start=False, stop=True),
                              r=[b_qxT, b_Sbf], w=[bPS[4]])
                    if not is_s:
                        for h in range(4):
                            T(lambda e: e.matmul(PS[5][:, h * 128:(h + 1) * 128], lhsT=kr[:, h * 128:(h + 1) * 128],
                                                 rhs=vz[:, h * 128:(h + 1) * 128], start=True, stop=True),
                              r=[b_kr, b_vz], w=[bPS[5]])
                        for h in range(4):
                            V(lambda e: e.scalar_tensor_tensor(out=Sst[:, h, :], in0=Sst[:, h, :], scalar=GC_P[h],
                                                               op0=ALU.mult, in1=PS[5][:, h * 128:(h + 1) * 128],
                                                               op1=ALU.add), r=[b_Sst, bPS[5]], w=[b_Sst])
                        A(lambda e: e.copy(out=Sbf[:], in_=Sst[:]), r=[b_Sst], w=[b_Sbf])
                        if n == NTP - 1:
                            S.dma("sp", O["o_ret_p"].rearrange("h d e -> d h e"), Sst[:], reads=[b_Sst])
                    else:
                        for b in range(16):
                            sl = b % 2
                            S.dma("sp", S0[sl][:], I["sret"][b].rearrange("h d e -> d h e"), writes=[bS0[sl]])
                            A(lambda e: e.copy(out=S0b[sl][:], in_=S0[sl][:]), r=[bS0[sl]], w=[bS0b[sl]])
                            V(lambda e: e.tensor_tensor(
                                out=qxm[sl][:], in0=qxT[:, :, 0:64],
                                in1=cmask[:, b * 64:(b + 1) * 64].unsqueeze(1).to_broadcast([128, 4, 64]), op=ALU.mult),
                              r=[b_qxT, b_t1], w=[bqxm[sl]])
                            for h in range(4):
                                T(lambda e: e.matmul(po(h), lhsT=qxm[sl][:, h, :],
                                                     rhs=S0b[sl][:, h, :], start=False, stop=(b == 15)),
                                  r=[bqxm[sl], bS0b[sl]], w=[bPS[pob[h]]])
                            V(lambda e: e.tensor_tensor(
                                out=vzb[sl][:, :].rearrange("p (h e) -> p h e", h=4),
                                in0=PS[2][:64, :].rearrange("p (h e) -> p h e", h=4),
                                in1=zs[:, b * 4:(b + 1) * 4].unsqueeze(2).to_broadcast([64, 4, 128]), op=ALU.mult),
                              r=[bPS[2], b_t1], w=[bvzb[sl]])
                            kvb = 5 if sl == 0 else 0
                            for h in range(4):
                                T(lambda e: e.matmul(PS[kvb][:, h * 128:(h + 1) * 128], lhsT=kr[:64, h * 128:(h + 1) * 128],
                                                     rhs=vzb[sl][:, h * 128:(h + 1) * 128], start=True, stop=True),
                                  r=[b_kr, bvzb[sl]], w=[bPS[kvb]])
                            for h in range(4):
                                V(lambda e: e.scalar_tensor_tensor(out=Sn[sl][:, h, :], in0=S0[sl][:, h, :], scalar=GC_S[h],
                                                                   op0=ALU.mult, in1=PS[kvb][:, h * 128:(h + 1) * 128],
                                                                   op1=ALU.add), r=[bS0[sl], bPS[kvb]], w=[bSn[sl]])
                            S.dma("sp", O["o_ret_s"][b].rearrange("h d e -> d h e"), Sn[sl][:], reads=[bSn[sl]])
                    if _step <= 5:
                        continue
                    for h in range(4):
                        V(lambda e: e.bn_stats(out=stats[:npt, h, :], in_=po(h)),
                          r=[bPS[pob[h]]], w=[b_st])
                    for h in range(4):
                        V(lambda e: e.bn_aggr(out=mv[:npt, h, :], in_=stats[:npt, h, :]), r=[b_st], w=[b_st])
                    A(lambda e: e.activation(out=rs4[:npt, :], in_=mv[:npt, :, 1], func=AF.Sqrt, scale=1.0,
                                             bias=epsc[:npt, :]), r=[b_st, b_const], w=[b_st])
                    V(lambda e: e.reciprocal(out=rs4[:npt, :], in_=rs4[:npt, :]), r=[b_st], w=[b_st])
                    V(lambda e: e.scalar_tensor_tensor(out=nb4[:npt, :], in0=mv[:npt, :, 0], scalar=-1.0, op0=ALU.mult,
                                                       in1=rs4[:npt, :], op1=ALU.mult), r=[b_st], w=[b_st])
                    for h in range(4):
                        A(lambda e: e.activation(out=on[:npt, h * 128:(h + 1) * 128], in_=po(h),
                                                 func=AF.Identity, scale=rs4[:npt, h:h + 1], bias=nb4[:npt, h:h + 1]),
                          r=[bPS[pob[h]], b_st], w=[b_on])
                    V(lambda e: e.tensor_tensor(out=ret[:npt, :], in0=on[:npt, :], in1=sg_[:npt, :], op=ALU.mult),
                       r=[b_on, b_sg], w=[b_ret])
                    if _step <= 6:
                        continue
                    pv6b = ps_bf(6)
                    for h in range(4):
                        T(lambda e: e.transpose(out=pv6b[:, h * 128:h * 128 + npt], in_=ret[:npt, h * 128:(h + 1) * 128],
                                                identity=identb[:npt, :npt]), r=[b_ret, b_const], w=[bPS[6]])
                    A(lambda e: e.copy(out=retT[:, :, 0:npt],
                                       in_=pv6b[:, 0:512].rearrange("p (h t) -> p h t", h=4)[:, :, 0:npt]),
                      r=[bPS[6]], w=[b_retT])
                    if _step <= 7:
                        continue
                    bi_ = min(n // 4, 4)
                    for half in range(2):
                        bank = 2 + half
                        for kt in range(8):
                            lh = ssmT[:, kt, tok0:tok0 + npt] if kt < 4 else retT[:, kt - 4, 0:npt]
                            T(lambda e: e.matmul(PS[bank][:npt, :], lhsT=lh, rhs=wout[:, kt, half * 512:(half + 1) * 512],
                                                 start=(kt == 0), stop=(kt == 7)),
                              r=[b_ssmT[bi_], b_retT, b_wout], w=[bPS[bank]])
                        resid_add(n, npt, half, bank)
                S.barrier()
            if dbg:
                for n in range(NT):
                    S.dma("sp", O["dbg_x"][:, n, :], x[:, n, :], reads=[bx[n]])
            if stage <= 2:
                S.barrier()
                S.run_block()
                nck.__exit__(None, None, None)
                return nc

            with ExitStack() as s2:
                gx = alloc(s2, "gx", [128, 8])
                gmem = alloc(s2, "gmem", [128, 8])
                ones = alloc(s2, "ones", [128, 128], BF16)
                b_t2 = Buf("tab2")
                S.dma("sp", gx[:], I["g_xattn"].rearrange("(k p) -> p k", p=128), writes=[b_t2])
                S.dma("sp", gmem[:], I["g_mem"].rearrange("(k p) -> p k", p=128), writes=[b_t2])
                V(lambda e: e.memset(ones[:], 1.0), w=[b_t2])
                KT = alloc(s2, "KT", [128, 8, MEM], BF16)
                Vm = alloc(s2, "Vm", [128, 2, D], BF16)
                b_KT, b_Vm = Buf("KT"), Buf("Vm")
                wmq = alloc(s2, "wmq", [128, 8, D], BF16)
                b_wmq, b_wmo = Buf("wmq", S.GW[2]), Buf("wmo", S.GW[3])
                with ExitStack() as s2a:
                    wmk = alloc(s2a, "wmk", [128, 8, D], BF16)
                    wmv = alloc(s2a, "wmv", [128, 8, D], BF16)
                    b_wmk, b_wmv = Buf("wmk", S.GW[0]), Buf("wmv", S.GW[1])
                    load_w_bf16(wmk, b_wmk, I["w_mk"], 8, D, 0)
                    load_w_bf16(wmv, b_wmv, I["w_mv"], 8, D, 0)
                    load_w_bf16(wmq, b_wmq, I["w_mq"], 8, D, 0)
                    mx = [alloc(s2a, "mx%d" % i, [128, D]) for i in range(2)]
                    bmx = [Buf("mx%d" % i, S.GL[i]) for i in range(2)]
                    mhT = alloc(s2a, "mhT", [128, 8, MEM], BF16)
                    b_mhT = Buf("mhT")
                    mo = [alloc(s2a, "mo%d" % i, [128, D]) for i in range(2)]
                    bmo = [Buf("mo%d" % i, S.GS[i]) for i in range(2)]
                    _k2a = int(_os.environ.get("K2A", "9"))
                    for mt in range(2):
                        S.dma("sp", mx[mt][:], I["memp"][mt * 128:(mt + 1) * 128, :], writes=[bmx[mt]])
                        if _k2a >= 1:
                            if _os.environ.get("K_MXX") == "1":
                                rmsnorm_hT(x[:, mt, :], bx[mt], 128, gmem[:], mhT, b_mhT, scrB, mt * 128,
                                           int(_os.environ.get("K_RMS", "99")), bg=b_t2)
                            else:
                                rmsnorm_hT(mx[mt][:, :], bmx[mt], 128, gmem[:], mhT, b_mhT, scrB, mt * 128,
                                           int(_os.environ.get("K_RMS", "99")), bg=b_t2)
                    oi = 0
                    for (wm, bwm, oname, isv) in ((wmk, b_wmk, "o_mk", False), (wmv, b_wmv, "o_mv", True)) if _k2a >= 2 else ():
                        for mt in range(2):
                            sl = oi % 2
                            oi += 1
                            for half in range(2):
                                bank = half
                                for kt in range(8):
                                    T(lambda e: e.matmul(PS[bank][:, :], lhsT=mhT[:, kt, mt * 128:(mt + 1) * 128],
                                                         rhs=wm[:, kt, half * 512:(half + 1) * 512], start=(kt == 0),
                                                         stop=(kt == 7)), r=[b_mhT, bwm], w=[bPS[bank]])
                                A(lambda e: e.copy(out=mo[sl][:, half * 512:(half + 1) * 512], in_=PS[bank][:, :]),
                                  r=[bPS[bank]], w=[bmo[sl]])
                                if isv:
                                    V(lambda e: e.tensor_copy(out=Vm[:, mt, half * 512:(half + 1) * 512], in_=PS[bank][:, :]),
                                      r=[bPS[bank]], w=[b_Vm])
                            S.dma("sp", O[oname][mt * 128:(mt + 1) * 128, :], mo[sl][:], reads=[bmo[sl]])
                    for j in range(8 if _k2a >= 3 else 0):
                        bank = 2 + (j % 2)
                        for kt in range(8):
                            T(lambda e: e.matmul(PS[bank][:, 0:MEM], lhsT=wmk[:, kt, j * 128:(j + 1) * 128],
                                                 rhs=mhT[:, kt, :], start=(kt == 0), stop=(kt == 7)),
                              r=[b_mhT, b_wmk], w=[bPS[bank]])
                        A(lambda e: e.copy(out=KT[:, j, :], in_=PS[bank][:, 0:MEM]), r=[bPS[bank]], w=[b_KT])
                    S.barrier()
                wmo = alloc(s2, "wmo", [128, 8, D], BF16)
                load_w_bf16(wmo, b_wmo, I["w_mo"], 8, D, 0)
                hT4 = alloc(s2, "hT4", [128, 8, 512], BF16)
                qm4 = alloc(s2, "qm4", [128, 8, 512], BF16)
                oT4 = alloc(s2, "oT4", [128, 8, 512], BF16)
                eT4 = [alloc(s2, "eT4_%d" % i, [128, 2, 512], BF16) for i in range(2)]
                rdn4 = [alloc(s2, "rdn4_%d" % i, [128, 512]) for i in range(2)]
                b_hT4, b_qm4, b_oT4 = Buf("hT4"), Buf("qm4"), Buf("oT4")
                b_eT4 = [Buf("eT4_%d" % i) for i in range(2)]
                b_rdn4 = [Buf("rdn4_%d" % i) for i in range(2)]
                Kb = [alloc(s2, "Kb%d" % i, [128, 2, D]) for i in range(2)]
                bKb = [Buf("Kb%d" % i, S.GL[i]) for i in range(2)]
                KbT = [alloc(s2, "KbT%d" % i, [128, 8, MEM], BF16) for i in range(2)]
                bKbT = [Buf("KbT%d" % i) for i in range(2)]
                Vb = [alloc(s2, "Vb%d" % i, [128, 2, D], BF16) for i in range(2)]
                bVb = [Buf("Vb%d" % i, S.GW[i]) for i in range(2)]
                eTs = alloc(s2, "eTs", [128, 2, 4, 64], BF16)
                b_eTs = Buf("eTs")
                qrot = [0]

                def q_proj(nc_):
                    for j in range(8):
                        bank = 5 + (qrot[0] % 3)
                        qrot[0] += 1
                        for kt in range(8):
                            T(lambda e: e.matmul(PS[bank][:, 0:nc_], lhsT=wmq[:, kt, j * 128:(j + 1) * 128],
                                                 rhs=hT4[:, kt, 0:nc_], start=(kt == 0), stop=(kt == 7)),
                              r=[b_wmq, b_hT4], w=[bPS[bank]])
                        A(lambda e: e.activation(out=qm4[:, j, 0:nc_], in_=PS[bank][:, 0:nc_], func=AF.Copy,
                                                 scale=1.0 / 16.0), r=[bPS[bank]], w=[b_qm4])

                def w_mo_resid(n, npt, c0):
                    for half in range(2):
                        bank = 5 + (qrot[0] % 3)
                        qrot[0] += 1
                        for j in range(8):
                            T(lambda e: e.matmul(PS[bank][:npt, :], lhsT=oT4[:, j, c0:c0 + npt],
                                                 rhs=wmo[:, j, half * 512:(half + 1) * 512], start=(j == 0), stop=(j == 7)),
                              r=[b_oT4, b_wmo], w=[bPS[bank]])
                        resid_add(n, npt, half, bank)

                for bi in range(4):
                    for ti in range(4):
                        n = bi * 4 + ti
                        rmsnorm_hT(x[:, n, :], bx[n], 128, gx[:], hT4, b_hT4, scrB, ti * 128, None, ln=True, bg=b_t2)
                    q_proj(512)
                    for h in range(4):
                        par = h % 2
                        for mt in range(2):
                            bank = mt
                            for dt_ in range(2):
                                T(lambda e: e.matmul(PS[bank][:, :], lhsT=KT[:, h * 2 + dt_, mt * 128:(mt + 1) * 128],
                                                     rhs=qm4[:, h * 2 + dt_, :], start=(dt_ == 0), stop=(dt_ == 1)),
                                  r=[b_KT, b_qm4], w=[bPS[bank]])
                            A(lambda e: e.activation(out=eT4[par][:, mt, :], in_=PS[bank][:, :], func=AF.Exp),
                              r=[bPS[bank]], w=[b_eT4[par]])
                        for mt in range(2):
                            T(lambda e: e.matmul(PS[2][:, :], lhsT=ones[:, :], rhs=eT4[par][:, mt, :], start=(mt == 0),
                                                 stop=(mt == 1)), r=[b_t2, b_eT4[par]], w=[bPS[2]])
                        A(lambda e: e.activation(out=rdn4[par][:, :], in_=PS[2][:, :], func=AF.Ln), r=[bPS[2]], w=[b_rdn4[par]])
                        A(lambda e: e.activation(out=rdn4[par][:, :], in_=rdn4[par][:, :], func=AF.Exp, scale=-1.0),
                          r=[b_rdn4[par]], w=[b_rdn4[par]])
                        for dt_ in range(2):
                            bank = 3 + dt_
                            j = h * 2 + dt_
                            for mt in range(2):
                                T(lambda e: e.matmul(PS[bank][:, :], lhsT=Vm[:, mt, j * 128:(j + 1) * 128],
                                                     rhs=eT4[par][:, mt, :], start=(mt == 0), stop=(mt == 1)),
                                  r=[b_Vm, b_eT4[par]], w=[bPS[bank]])
                            V(lambda e: e.tensor_tensor(out=oT4[:, j, :], in0=PS[bank][:, :], in1=rdn4[par][:, :], op=ALU.mult),
                              r=[bPS[bank], b_rdn4[par]], w=[b_oT4])
                    for ti in range(4):
                        w_mo_resid(bi * 4 + ti, 128, ti * 128)
                n = 16
                rmsnorm_hT(x[:TS, n, :], bx[n], TS, gx[:], hT4, b_hT4, scrB, 0, None, ln=True, bg=b_t2)
                q_proj(TS)
                rden_s = rdn4[0][:, 0:256].rearrange("p (h t) -> p h t", h=4)
                for b in range(16):
                    sl = b % 2
                    S.dma("sp", Kb[sl][:], I["ck"][b].rearrange("(mt p) d -> p mt d", p=128), writes=[bKb[sl]])
                    for q4 in range(4):
                        bank = 2 + (q4 % 2)
                        for i4 in range(4):
                            idx = q4 * 4 + i4
                            j, mt = idx // 2, idx % 2
                            T(lambda e: e.transpose(out=PS[bank][:, i4 * 128:(i4 + 1) * 128],
                                                    in_=Kb[sl][:, mt, j * 128:(j + 1) * 128], identity=identf[:]),
                              r=[bKb[sl], b_const], w=[bPS[bank]])
                        A(lambda e: e.copy(
                            out=KbT[sl][:, 2 * q4:2 * q4 + 2, :].rearrange("p j (m t) -> p j m t", m=2),
                            in_=PS[bank][:, :].rearrange("p (j m t) -> p j m t", j=2, m=2)),
                          r=[bPS[bank]], w=[bKbT[sl]])
                    for h in range(4):
                        for mt in range(2):
                            c0 = mt * 256 + h * 64 + 4 * b
                            for dt_ in range(2):
                                T(lambda e: e.matmul(PS[4][:, c0:c0 + 4],
                                                     lhsT=KbT[sl][:, h * 2 + dt_, mt * 128:(mt + 1) * 128],
                                                     rhs=qm4[:, h * 2 + dt_, 4 * b:4 * b + 4], start=(dt_ == 0),
                                                     stop=(dt_ == 1)), r=[bKbT[sl], b_qm4], w=[bPS[4]])
                A(lambda e: e.activation(out=eTs[:].rearrange("p m h t -> p (m h t)"), in_=PS[4][:, :], func=AF.Exp),
                  r=[bPS[4]], w=[b_eTs])
                for h in range(4):
                    for mt in range(2):
                        T(lambda e: e.matmul(PS[0][:, h * 64:(h + 1) * 64], lhsT=ones[:, :], rhs=eTs[:, mt, h, :],
                                             start=(mt == 0), stop=(mt == 1)), r=[b_t2, b_eTs], w=[bPS[0]])
                V(lambda e: e.reciprocal(out=rden_s, in_=PS[0][:, 0:256].rearrange("p (h t) -> p h t", h=4)),
                  r=[bPS[0]], w=[b_rdn4[0]])
                for b in range(16):
                    sl = b % 2
                    for mt in range(2):
                        S.dma("pool", Vb[sl][:, mt, :], I["cv"][b, mt * 128:(mt + 1) * 128, :], writes=[bVb[sl]])
                    for j in range(8):
                        h = j // 2
                        for mt in range(2):
                            T(lambda e: e.matmul(PS[1][:, j * 64 + 4 * b:j * 64 + 4 * b + 4],
                                                 lhsT=Vb[sl][:, mt, j * 128:(j + 1) * 128],
                                                 rhs=eTs[:, mt, h, 4 * b:4 * b + 4], start=(mt == 0), stop=(mt == 1)),
                              r=[bVb[sl], b_eTs], w=[bPS[1]])
                V(lambda e: e.tensor_tensor(
                    out=oT4[:, :, 0:64].rearrange("p (h a) t -> p h a t", a=2),
                    in0=PS[1][:, :].rearrange("p (h a t) -> p h a t", h=4, a=2),
                    in1=rden_s.unsqueeze(2).to_broadcast([128, 4, 2, 64]), op=ALU.mult),
                  r=[bPS[1], b_rdn4[0]], w=[b_oT4])
                w_mo_resid(16, TS, 0)
                S.barrier()
            if stage <= 3:
                if dbg:
                    for n in range(NT):
                        S.dma("sp", O["dbg_x"][:, n, :], x[:, n, :], reads=[bx[n]])
                S.barrier()
                S.run_block()
                nck.__exit__(None, None, None)
                return nc

            with ExitStack() as s3:
                gml = alloc(s3, "gml", [128, 8])
                b_t3 = Buf("tab3")
                S.dma("sp", gml[:], I["g_mlp"].rearrange("(k p) -> p k", p=128), writes=[b_t3])
                hTa = alloc(s3, "hTa", [128, 8, NTOK], BF16)
                b_hTa = [Buf("hTa%d" % n) for n in range(NT)]
                wup = [alloc(s3, "wup%d" % i, [128, 8, 512], BF16) for i in range(2)]
                wdn = [alloc(s3, "wdn%d" % i, [128, 4, D], BF16) for i in range(2)]
                bwup = [Buf("wup%d" % i, S.GW[i]) for i in range(2)]
                bwdn = [Buf("wdn%d" % i, S.GW[2 + i]) for i in range(2)]
                rl = [alloc(s3, "rl%d" % i, [128, 512]) for i in range(2)]
                brl = [Buf("rl%d" % i) for i in range(2)]
                aT = [alloc(s3, "aT%d" % i, [128, 4, 512], BF16) for i in range(2)]
                baT = [Buf("aT%d" % i) for i in range(2)]

                def load_fc(fc):
                    sl = fc % 2
                    for kt in range(8):
                        S.dma("pool", wup[sl][:, kt, :], I["w_up"][kt * 128:(kt + 1) * 128, fc * 512:(fc + 1) * 512],
                              writes=[bwup[sl]])
                    for ft in range(4):
                        S.dma("pool", wdn[sl][:, ft, :], I["w_down"][fc * 512 + ft * 128:fc * 512 + (ft + 1) * 128, :],
                              writes=[bwdn[sl]])
                load_fc(0)
                for n in range(NT):
                    npt = TS if n == 16 else 128
                    rmsnorm_hT(x[:npt, n, :], bx[n], npt, gml[:], hTa, b_hTa[n], scrB, n * 128, None, bg=b_t3)
                blocks3 = [(i * 512, 512) for i in range(4)] + [(SEQ, TS)]
                ai = 0
                ri = 0
                di = 0
                for fc in range(8):
                    sl = fc % 2
                    if fc + 1 < 8:
                        load_fc(fc + 1)
                    for (t0, nn) in blocks3:
                        tiles = list(range(t0 // 128, t0 // 128 + (nn + 127) // 128))
                        asl = ai % 2
                        ai += 1
                        for ft in range(4):
                            bank = ft
                            for kt in range(8):
                                T(lambda e: e.matmul(PS[bank][:, 0:nn], lhsT=wup[sl][:, kt, ft * 128:(ft + 1) * 128],
                                                     rhs=hTa[:, kt, t0:t0 + nn], start=(kt == 0), stop=(kt == 7)),
                                  r=[bwup[sl]] + [b_hTa[t] for t in tiles], w=[bPS[bank]])
                            rsl = ri % 2
                            ri += 1
                            A(lambda e: e.activation(out=rl[rsl][:, 0:nn], in_=PS[bank][:, 0:nn], func=AF.Relu),
                              r=[bPS[bank]], w=[brl[rsl]])
                            V(lambda e: e.tensor_tensor(out=aT[asl][:, ft, 0:nn], in0=rl[rsl][:, 0:nn], in1=rl[rsl][:, 0:nn],
                                                        op=ALU.mult), r=[brl[rsl]], w=[baT[asl]])
                        for ti, tl in enumerate(tiles):
                            npt = TS if tl == 16 else 128
                            for half in range(2):
                                bank = 4 + (di % 4)
                                di += 1
                                for ft in range(4):
                                    T(lambda e: e.matmul(PS[bank][:npt, :], lhsT=aT[asl][:, ft, ti * 128:ti * 128 + npt],
                                                         rhs=wdn[sl][:, ft, half * 512:(half + 1) * 512], start=(ft == 0),
                                                         stop=(ft == 3)), r=[baT[asl], bwdn[sl]], w=[bPS[bank]])
                                resid_add(tl, npt, half, bank)
                S.barrier()
            if dbg:
                for n in range(NT):
                    S.dma("sp", O["dbg_x"][:, n, :], x[:, n, :], reads=[bx[n]])
            with ExitStack() as s4:
                gf = alloc(s4, "gf", [128, D])
                b_gf = Buf("gf")
                S.dma("sp", gf[:], I["g_final"].rearrange("(o d) -> o d", o=1).partition_broadcast(128), writes=[b_gf])
                yst = [alloc(s4, "yst%d" % i, [128, D]) for i in range(3)]
                byst = [Buf("yst%d" % i, S.GS[i]) for i in range(3)]
                for n in range(NT):
                    npt = TS if n == 16 else 128
                    sl = n % 3
                    A(lambda e: e.activation(out=sq[:npt, :], in_=x[:npt, n, :], func=AF.Square, accum_out=ss[:npt, :]),
                      r=[bx[n]], w=[bscr])
                    A(lambda e: e.activation(out=rstd[:npt, :], in_=ss[:npt, :], func=AF.Sqrt, scale=1.0 / D,
                                             bias=epsc[:npt, :]), r=[bscr, b_const], w=[bscr])
                    V(lambda e: e.reciprocal(out=rstd[:npt, :], in_=rstd[:npt, :]), r=[bscr], w=[bscr])
                    V(lambda e: e.scalar_tensor_tensor(out=yst[sl][:npt, :], in0=x[:npt, n, :], scalar=rstd[:npt, :],
                                                       op0=ALU.mult, in1=gf[:npt, :], op1=ALU.mult),
                      r=[bx[n], bscr, b_gf], w=[byst[sl]])
                    if n < 16:
                        S.dma("sp", O["yp"][n * 128:(n + 1) * 128, :], yst[sl][:, :], reads=[byst[sl]])
                    else:
                        S.dma("sp", O["ys"][:, :], yst[sl][:TS, :], reads=[byst[sl]])
                S.barrier()
            S.barrier()
            S.run_block()
            nck.__exit__(None, None, None)
    return nc


_NC = None


def kernel(**inputs):
    global _NC
    if _NC is None:
        _NC = build()
    maps = _in_maps(inputs)
    res = run_bass_kernel_spmd(_NC, maps, core_ids=list(range(8)))
    R = res.results
    f = np.float32

    def cat(name, shape=None):
        return np.stack([np.asarray(R[c][name], f) for c in range(8)])
    y_prompt = cat("yp")
    y_sample = cat("ys").reshape(128, 4, D)
    s5r_p = cat("o_s5r_p")[None]
    s5i_p = cat("o_s5i_p")[None]
    ret_p = cat("o_ret_p")[None]
    mk_p = cat("o_mk").reshape(8, MEM, 4, 256)[None]
    mv_p = cat("o_mv").reshape(8, MEM, 4, 256)[None]
    s5r_s = cat("o_s5r_s").reshape(128, G, 64)[None]
    s5i_s = cat("o_s5i_s").reshape(128, G, 64)[None]
    ret_s = cat("o_ret_s").reshape(128, 4, 128, 128)[None]
    return (y_prompt, y_sample, s5r_p, s5i_p, ret_p, mk_p, mv_p, s5r_s, s5i_s, ret_s)


def _in_maps(inputs):
    cst = _consts()
    f = np.float32
    maps = []
    w = {}
    for k in W_NAMES:
        a = np.asarray(inputs[k], f)
        if k != "g_final":
            a = a[0]
        w[k] = np.ascontiguousarray(a.reshape(W_SHAPES[k]))
    for c in range(8):
        m = dict(w)
        m.update(cst)
        b0 = 16 * c
        m["xp"] = np.ascontiguousarray(np.asarray(inputs["x_prompt"], f)[c])
        m["xs"] = np.ascontiguousarray(np.asarray(inputs["x_sample"], f)[b0:b0 + 16].reshape(TS, D))
        m["memp"] = np.ascontiguousarray(np.asarray(inputs["mem_prompt"], f)[c])
        m["s5r"] = np.ascontiguousarray(np.asarray(inputs["state_s5_re"], f)[0, b0:b0 + 16].reshape(512, 64))
        m["s5i"] = np.ascontiguousarray(np.asarray(inputs["state_s5_im"], f)[0, b0:b0 + 16].reshape(512, 64))
        m["sret"] = np.ascontiguousarray(np.asarray(inputs["state_ret"], f)[0, b0:b0 + 16])
        m["ck"] = np.ascontiguousarray(np.asarray(inputs["cache_mem_k"], f)[0, b0:b0 + 16].reshape(16, MEM, D))
        m["cv"] = np.ascontiguousarray(np.asarray(inputs["cache_mem_v"], f)[0, b0:b0 + 16].reshape(16, MEM, D))
        maps.append(m)
    return maps
```

```python
import numpy as np
import concourse.bass as bass
import concourse.mybir as mybir
from concourse.bass_utils import run_bass_kernel_spmd
from contextlib import ExitStack

F32 = mybir.dt.float32
BF16 = mybir.dt.bfloat16
AF = mybir.ActivationFunctionType
ALU = mybir.AluOpType

D = 1024
SEQ = 2048
NTP = 16
TS = 64
NT = 17
NTOK = SEQ + TS
G = 32
DFF = 4096
MEM = 256
EPS = 1e-6
PAST = 16384.0
MAGIC = 12582912.0
TWO_PI = float(2.0 * np.pi)
ML = [7, 6, 5, 4, 3, 2, 1, 0, 1, 2, 3, 4, 5, 6, 7, 8, -4, 0.5]
K1 = len(ML)
I_A1, I_A8, I_A4, I_AM4, I_HALF = 8, 15, 3, 16, 17
GAM = [1.0 - 2.0 ** (-5.0 - h) for h in range(4)]


class Grp:
    __slots__ = ("sem", "cnt", "sealed")


class Buf:
    __slots__ = ("w", "r", "name", "grp", "ps")

    def __init__(self, name="", grp=None, ps=False):
        self.w = None
        self.r = []
        self.name = name
        self.grp = grp
        self.ps = ps


class _Rec:
    def __init__(self):
        self.call = None

    def __getattr__(self, name):
        def f(*a, **kw):
            self.call = (name, a, kw)
            return self
        return f


class Sched:
    ENG = ("pe", "dve", "act", "pool", "sp")

    def __init__(self, nc, stack, self_sync=("dve", "act", "pool")):
        self.nc = nc
        self.stack = stack
        self.prog = {k: [] for k in self.ENG}
        self.cnt = {k: 0 for k in self.ENG}
        self.waited = {k: {} for k in self.ENG}
        self.sem = {}
        self.nsem = 0
        for k in ("pe", "dve", "act", "pool"):
            self.sem[k] = self.new_sem("c_" + k)
        self.self_sync = set(self_sync)
        self.groups = []
        self.GC = self.group("gc")
        self.GP = self.group("gp")
        self.GW = [self.group("gw%d" % i) for i in range(4)]
        self.GX = self.group("gx")
        self.GL = [self.group("gl%d" % i) for i in range(2)]
        self.GS = [self.group("gs%d" % i) for i in range(3)]

    def group(self, name):
        g = Grp()
        g.sem = self.new_sem(name)
        g.cnt = 0
        g.sealed = False
        self.groups.append(g)
        return g

    def new_sem(self, name):
        self.nsem += 1
        assert self.nsem < 98, "too many semaphores"
        return self.stack.enter_context(self.nc.semaphore(name + "_%d" % self.nsem))

    def _waits(self, eng, deps):
        w = self.waited[eng]
        need = {}
        dd = []
        for d in deps:
            if isinstance(d, Grp):
                d.sealed = True
                dd.append((d.sem, d.cnt))
            else:
                dd.append(d)
        deps = dd
        for (s, v) in deps:
            if eng in self.sem and s is self.sem[eng] and eng not in self.self_sync:
                continue
            k = id(s)
            if w.get(k, 0) >= v:
                continue
            if k not in need or need[k][1] < v:
                need[k] = (s, v)
        for k, (s, v) in need.items():
            w[k] = v
            self.prog[eng].append(lambda e, s=s, v=v: e.wait_ge(s, v))

    def op(self, eng, fn, reads=(), writes=()):
        deps = []
        for b in reads:
            if b.w is not None:
                deps.append(b.w)
            if b.ps:
                mys = self.sem[eng]
                deps.extend(d for d in b.r if not (isinstance(d, tuple) and d[0] is mys))
        for b in writes:
            if b.w is not None:
                deps.append(b.w)
            deps.extend(b.r)
        self._waits(eng, deps)
        self.cnt[eng] += 1
        c = self.cnt[eng]
        s = self.sem[eng]
        rec = _Rec()
        fn(rec)
        name, a, kw = rec.call
        self.prog[eng].append(lambda e, name=name, a=a, kw=kw, s=s: getattr(e, name)(*a, **kw).then_inc(s, 1))
        for b in reads:
            b.r.append((s, c))
        for b in writes:
            b.w = (s, c)
            b.r = []

    def dma(self, q, out, in_, reads=(), writes=(), **kw):
        tb = writes[0] if writes else reads[0]
        g = tb.grp
        if g is None:
            g = self.GP if q == "pool" else (self.GC if writes else self.GS[0])
        deps = []
        for b in reads:
            if b.w is not None:
                deps.append(b.w)
        for b in writes:
            if b.w is not None and b.w is not g:
                deps.append(b.w)
            deps.extend(b.r)
        self._waits(q, deps)
        if g.sealed and g.cnt > 0:
            self._waits(q, [(g.sem, g.cnt)])
        g.sealed = False
        g.cnt += 16
        s = g.sem
        self.prog[q].append(
            lambda e, out=out, in_=in_, s=s, kw=kw: e.dma_start(out=out, in_=in_, **kw).then_inc(s, 16))
        for b in reads:
            b.r.append(g)
        for b in writes:
            b.w = g
            b.r = []

    def barrier(self, engines=None):
        deps = [(self.sem[k], self.cnt[k]) for k in ("pe", "dve", "act", "pool") if self.cnt[k] > 0]
        deps += [g for g in self.groups if g.cnt > 0]
        for e in (engines or self.ENG):
            self._waits(e, deps)

    def run_block(self):
        nc = self.nc
        with nc.Block() as block:
            @block.sync
            def _(e):
                for t in self.prog["sp"]:
                    t(e)

            @block.tensor
            def _(e):
                for t in self.prog["pe"]:
                    t(e)

            @block.vector
            def _(e):
                for t in self.prog["dve"]:
                    t(e)

            @block.scalar
            def _(e):
                for t in self.prog["act"]:
                    t(e)

            @block.gpsimd
            def _(e):
                for t in self.prog["pool"]:
                    t(e)


_CONSTS = None


def _consts():
    global _CONSTS
    if _CONSTS is not None:
        return _CONSTS
    f = np.float32
    c = {}
    c["c_ident"] = np.eye(128, dtype=f)
    m = np.zeros((8, 128, 240), f)
    for a in range(8):
        for i in range(16):
            m[a, 16 * a + i, 112 + i] = 1.0
    c["c_masters"] = m
    ml = np.array(ML, np.float64)
    rows = np.concatenate([ml / (2 * np.pi), ml, 8.0 * (np.arange(64) + 1) / (2 * np.pi)])
    c["c_rows"] = rows.astype(f)[None, :]
    sg = np.zeros((128, 2), f)
    sg[:64, 0] = 1.0
    sg[64:, 0] = -1.0
    sg[:64, 1] = -1.0
    sg[64:, 1] = 1.0
    c["c_sgn"] = sg
    inv = (f(10000.0) ** (-(np.arange(64, dtype=f) / f(64.0)))).astype(f)
    pos = np.zeros((128, NT), f)
    for n in range(NTP):
        pos[:, n] = 128 * n + np.arange(128)
    pos[:64, 16] = PAST + (np.arange(64) % 4)
    ang = (pos[:, :, None] * inv[None, None, :]).astype(f).astype(np.float64)
    c["c_rope"] = np.stack([np.cos(ang), np.sin(ang), -np.sin(ang)]).astype(f)
    lg = np.log(np.array(GAM, np.float64))
    sc = 128.0 ** -0.5
    idx = np.arange(128)
    dm = np.zeros((128, 4, 128), np.float64)
    diff = idx[None, :] - idx[:, None]
    for h in range(4):
        dm[:, h, :] = np.where(diff >= 0, np.exp(np.maximum(diff, 0) * lg[h]), 0.0) * sc
    c["c_dmask_p"] = dm.reshape(128, 512).astype(f)
    ds_ = np.zeros((64, 4, 64), np.float64)
    r = np.arange(64)
    bb = r // 4
    tt = r % 4
    same = bb[:, None] == bb[None, :]
    dts = tt[None, :] - tt[:, None]
    for h in range(4):
        ds_[:, h, :] = np.where(same & (dts >= 0), np.exp(np.maximum(dts, 0) * lg[h]), 0.0) * sc
    c["c_dmask_s"] = ds_.reshape(64, 256).astype(f)
    xi_p = np.stack([np.exp((idx + 1.0) * lg[h]) * sc for h in range(4)])
    xi_s = np.stack([np.exp((tt + 1.0) * lg[h]) * sc for h in range(4)])
    c["c_xi"] = np.concatenate([xi_p.reshape(-1), xi_s.reshape(-1)]).astype(f)[None, :]
    zp = np.stack([np.exp((127.0 - idx) * lg[h]) for h in range(4)], axis=1)
    c["c_zeta_p"] = zp.astype(f)
    zs = np.zeros((64, 16, 4), np.float64)
    for h in range(4):
        for b in range(16):
            zs[:, b, h] = np.where(bb == b, np.exp((3.0 - tt) * lg[h]), 0.0)
    c["c_zs"] = zs.reshape(64, 64).astype(f)
    cm = np.zeros((16, 64), f)
    for b in range(16):
        cm[b, 4 * b:4 * b + 4] = 1.0
    c["c_cmask"] = cm.reshape(1, -1)
    _CONSTS = c
    return c


W_NAMES = ["g_mix", "w_in", "lam_re", "lam_im", "log_dt", "b_re", "b_im", "c_re", "c_im", "d_skip", "w_glu",
           "ret_gn", "w_out", "g_xattn", "g_mem", "w_mq", "w_mk", "w_mv", "w_mo", "g_mlp", "w_up", "w_down",
           "g_final"]
W_SHAPES = {"g_mix": [D], "w_in": [D, 2560], "lam_re": [G, 64], "lam_im": [G, 64], "log_dt": [G],
            "b_re": [G, 64, 16], "b_im": [G, 64, 16], "c_re": [G * 16, 64], "c_im": [G * 16, 64], "d_skip": [512],
            "w_glu": [512, 512], "ret_gn": [512], "w_out": [D, D], "g_xattn": [D], "g_mem": [D], "w_mq": [D, D],
            "w_mk": [D, D], "w_mv": [D, D], "w_mo": [D, D], "g_mlp": [D], "w_up": [D, DFF], "w_down": [DFF, D],
            "g_final": [D]}
IN_SHAPES = {"xp": [SEQ, D], "xs": [TS, D], "memp": [MEM, D], "s5r": [512, 64], "s5i": [512, 64],
             "sret": [16, 4, 128, 128], "ck": [16, MEM, D], "cv": [16, MEM, D]}
OUT_SHAPES = {"yp": [SEQ, D], "ys": [TS, D], "o_s5r_p": [G, 64], "o_s5i_p": [G, 64], "o_ret_p": [4, 128, 128],
              "o_mk": [MEM, D], "o_mv": [MEM, D], "o_s5r_s": [512, 64], "o_s5i_s": [512, 64],
              "o_ret_s": [16, 4, 128, 128]}


def build(stage=99, dbg=False):
    nc = bass.Bass("TRN2", target_bir_lowering=False)
    cst = _consts()
    I = {}
    for k, shp in list(IN_SHAPES.items()) + list(W_SHAPES.items()):
        I[k] = nc.dram_tensor(k, shp, F32, kind="ExternalInput").ap()
    for k, v in cst.items():
        I[k] = nc.dram_tensor(k, list(v.shape), F32, kind="ExternalInput").ap()
    O = {}
    for k, shp in OUT_SHAPES.items():
        O[k] = nc.dram_tensor(k, shp, F32, kind="ExternalOutput").ap()
    if dbg:
        O["dbg_ssm"] = nc.dram_tensor("dbg_ssm", [128, 4, NTOK], F32, kind="ExternalOutput").ap()
        O["dbg_x"] = nc.dram_tensor("dbg_x", [128, NT, D], F32, kind="ExternalOutput").ap()

    with ExitStack() as st:
        S = Sched(nc, st)

        def alloc(stack, name, shape, dt=F32):
            return stack.enter_context(nc.sbuf_tensor(name, shape, dt))

        def palloc(stack, name, shape, dt=F32):
            return stack.enter_context(nc.psum_tensor(name, shape, dt))

        def V(fn, r=(), w=()):
            S.op("dve", fn, reads=r, writes=w)

        def A(fn, r=(), w=()):
            S.op("act", fn, reads=r, writes=w)

        import os as _os0
        _nopool = _os0.environ.get("K_NOPOOL") == "1"

        def PL(fn, r=(), w=()):
            S.op("dve" if _nopool else "pool", fn, reads=r, writes=w)

        def T(fn, r=(), w=()):
            S.op("pe", fn, reads=r, writes=w)

        nck = nc.allow_non_contiguous_dma(reason="small param layout loads")
        nck.__enter__()

        identb = alloc(st, "identb", [128, 128], BF16)
        identf = alloc(st, "identf", [128, 128], F32)
        sgn = alloc(st, "sgn", [128, 2])
        epsc = alloc(st, "epsc", [128, 1])
        ssmT = alloc(st, "ssmT", [128, 4, NTOK], BF16)
        b_const = Buf("const")
        b_ssmT = [Buf("ssmT%d" % i) for i in range(5)]
        b_constp = Buf("constp")
        S.dma("pool", identb[:], I["c_ident"][:, :], writes=[b_constp])
        S.dma("sp", identf[:], I["c_ident"][:, :], writes=[b_const])
        S.dma("sp", sgn[:], I["c_sgn"][:, :], writes=[b_const])
        V(lambda e: e.memset(epsc[:], EPS), r=[b_constp], w=[b_const])
        PS = [palloc(st, "ps%d" % i, [128, 512], F32) for i in range(8)]
        bPS = [Buf("ps%d" % i, ps=True) for i in range(8)]

        def ps_bf(i):
            return PS[i][:].bitcast(BF16)

        def make_scr(stack, tag, pbanks):
            d = {"i": 0, "pb": list(pbanks)}
            d["sq"] = [alloc(stack, "sq%s%d" % (tag, i), [128, D], BF16) for i in range(2)]
            d["ss"] = [alloc(stack, "ss%s%d" % (tag, i), [128, 1]) for i in range(2)]
            d["rstd"] = [alloc(stack, "rstd%s%d" % (tag, i), [128, 1]) for i in range(2)]
            d["hb"] = [alloc(stack, "hb%s%d" % (tag, i), [128, D], BF16) for i in range(2)]
            d["ba"] = [Buf("ba%s%d" % (tag, i)) for i in range(2)]
            d["bh"] = [Buf("bh%s%d" % (tag, i)) for i in range(2)]
            return d

        def rmsnorm_hT(xt_ap, bx, npart, gcol, hT_ap, bhT, scr, col0, ph, ln=False, bg=None):
            k = scr["i"] % 2
            pbank = scr["pb"][scr["i"] % len(scr["pb"])]
            scr["i"] += 1
            sq, ss, rstd, hb = scr["sq"][k], scr["ss"][k], scr["rstd"][k], scr["hb"][k]
            ba, bh = scr["ba"][k], scr["bh"][k]
            A(lambda e: e.activation(out=sq[:npart, :], in_=xt_ap, func=AF.Square, accum_out=ss[:npart, :]),
              r=[bx], w=[ba])
            if ln:
                A(lambda e: e.activation(out=rstd[:npart, :], in_=ss[:npart, :], func=AF.Ln, scale=1.0 / D,
                                         bias=epsc[:npart, :]), r=[ba, b_const], w=[ba])
                A(lambda e: e.activation(out=rstd[:npart, :], in_=rstd[:npart, :], func=AF.Exp, scale=-0.5),
                  r=[ba], w=[ba])
            else:
                A(lambda e: e.activation(out=rstd[:npart, :], in_=ss[:npart, :], func=AF.Sqrt, scale=1.0 / D,
                                         bias=epsc[:npart, :]), r=[ba, b_const], w=[ba])
                V(lambda e: e.reciprocal(out=rstd[:npart, :], in_=rstd[:npart, :]), r=[ba], w=[ba])
            V(lambda e: e.tensor_scalar(out=hb[:npart, :], in0=xt_ap, scalar1=rstd[:npart, :], scalar2=None,
                                        op0=ALU.mult), r=[bx, ba], w=[bh])
            pv = ps_bf(pbank)
            for kt in range(8):
                T(lambda e, kt=kt: e.transpose(out=pv[:, kt * 128:kt * 128 + npart],
                                               in_=hb[:npart, kt * 128:(kt + 1) * 128],
                                               identity=identb[:npart, :npart]),
                  r=[bh, b_const], w=[bPS[pbank]])
            V(lambda e: e.tensor_tensor(
                out=hT_ap[:, :, col0:col0 + npart],
                in0=pv.rearrange("p (k t) -> p k t", k=8)[:, :, 0:npart],
                in1=gcol.unsqueeze(2).to_broadcast([128, 8, npart]), op=ALU.mult),
              r=[bPS[pbank], b_const] + ([bg] if bg is not None else []), w=[bhT])

        def load_w_bf16(dst, bdst, src, kt_n, ncols, c0=0):
            for kt in range(kt_n):
                for cc in range(0, ncols, 1024):
                    w_ = min(1024, ncols - cc)
                    S.dma("pool", dst[:, kt, cc:cc + w_], src[kt * 128:(kt + 1) * 128, c0 + cc:c0 + cc + w_],
                          writes=[bdst])

        with ExitStack() as sa:
            Wt = alloc(sa, "Wt", [128, G, 128], BF16)
            Wst = alloc(sa, "Wst", [128, G, 128], BF16)
            Tt = alloc(sa, "Tt", [128, G, 128], BF16)
            Vt = alloc(sa, "Vt", [128, G, 128], BF16)
            COSR = alloc(sa, "COSR", [128, G, 64])
            SINR = alloc(sa, "SINR", [128, G, 64])
            masters = alloc(sa, "masters", [128, 8, 240], BF16)
            AR = alloc(sa, "AR", [128, G, K1])
            AI = alloc(sa, "AI", [128, G, K1])
            MAGJ = alloc(sa, "MAGJ", [128, G, K1])
            DS = alloc(sa, "DS", [128, G])
            gm = alloc(sa, "gm", [128, 8])
            winu = alloc(sa, "winu", [128, 8, 512], BF16)
            wglu = alloc(sa, "wglu", [128, 4, 512], BF16)
            b_tab = Buf("s5tab")
            b_winu = Buf("winu", S.GW[0])
            b_wglu = Buf("wglu", S.GW[1])
            b_tabp = Buf("s5tabp")
            S.dma("pool", masters[:], I["c_masters"].rearrange("a k j -> k a j"), writes=[b_tabp])
            S.dma("sp", gm[:], I["g_mix"].rearrange("(k p) -> p k", p=128), writes=[b_tab])
            for tau in range(8):
                S.dma("sp", DS[16 * tau:16 * tau + 16, :], I["d_skip"].rearrange("(g h) -> h g", h=16),
                      writes=[b_tab])
            load_w_bf16(winu, b_winu, I["w_in"], 8, 512, 0)
            load_w_bf16(wglu, b_wglu, I["w_glu"], 4, 512, 0)

            with ExitStack() as s0:
                rows = alloc(s0, "rows", [128, 2 * K1 + 64])
                LR = alloc(s0, "LR", [128, G])
                LI = alloc(s0, "LI", [128, G])
                DT = alloc(s0, "DT", [128, G])
                LRDT = alloc(s0, "LRDT", [128, G])
                LIDT = alloc(s0, "LIDT", [128, G])
                tA = alloc(s0, "tA", [128, G, 64])
                tB = alloc(s0, "tB", [128, G, 64])
                tC = alloc(s0, "tC", [128, G, 64])
                COSJ = alloc(s0, "COSJ", [128, G, K1])
                SINJ = alloc(s0, "SINJ", [128, G, K1])
                sm = alloc(s0, "sm", [128, 12, G])
                Br1 = alloc(s0, "Br1", [128, G, 16])
                Br2 = alloc(s0, "Br2", [128, G, 16])
                BB1 = alloc(s0, "BB1", [128, G, 16])
                BB2 = alloc(s0, "BB2", [128, G, 16])
                tb1 = alloc(s0, "tb1", [128, G, 16])
                big1 = alloc(s0, "big1", [128, G, 128])
                big2 = alloc(s0, "big2", [128, G, 128])
                WTpad = alloc(s0, "WTpad", [128, G, 256], BF16)
                WTs = alloc(s0, "WTs", [128, G, 128], BF16)
                CN1 = alloc(s0, "CN1", [128, 4, 128])
                CN2 = alloc(s0, "CN2", [128, 4, 128])
                CMa = alloc(s0, "CMa", [128, G, 16])
                CMb = alloc(s0, "CMb", [128, G, 16])
                CMab = alloc(s0, "CMab", [128, G, 16], BF16)
                b0 = Buf("p0in")
                bt = Buf("p0tmp")
                S.dma("sp", rows[:], I["c_rows"][0:1, :].partition_broadcast(128), writes=[b0])
                for hf in range(2):
                    S.dma("sp", LR[64 * hf:64 * hf + 64, :], I["lam_re"].rearrange("g p -> p g"), writes=[b0])
                    S.dma("sp", LI[64 * hf:64 * hf + 64, :], I["lam_im"].rearrange("g p -> p g"), writes=[b0])
                S.dma("sp", DT[:], I["log_dt"].rearrange("(o g) -> o g", o=1).partition_broadcast(128), writes=[b0])
                S.dma("sp", Br1[0:64], I["b_re"].rearrange("g p h -> p g h"), writes=[b0])
                S.dma("sp", Br1[64:128], I["b_im"].rearrange("g p h -> p g h"), writes=[b0])
                S.dma("sp", Br2[0:64], I["b_im"].rearrange("g p h -> p g h"), writes=[b0])
                S.dma("sp", Br2[64:128], I["b_re"].rearrange("g p h -> p g h"), writes=[b0])
                S.dma("sp", CN1[:, :, 0:64], I["c_re"].rearrange("(c r) p -> r c p", r=128), writes=[b0])
                S.dma("sp", CN1[:, :, 64:128], I["c_im"].rearrange("(c r) p -> r c p", r=128), writes=[b0])
                S.dma("sp", CN2[:, :, 0:64], I["c_im"].rearrange("(c r) p -> r c p", r=128), writes=[b0])
                S.dma("sp", CN2[:, :, 64:128], I["c_re"].rearrange("(c r) p -> r c p", r=128), writes=[b0])
                MT1 = rows[:, 0:K1]
                MLr = rows[:, K1:2 * K1]
                MRT = rows[:, 2 * K1:2 * K1 + 64]
                A(lambda e: e.activation(out=DT[:], in_=DT[:], func=AF.Exp), r=[b0], w=[b0])
                V(lambda e: e.tensor_tensor(out=LRDT[:], in0=LR[:], in1=DT[:], op=ALU.mult), r=[b0], w=[bt])
                V(lambda e: e.tensor_tensor(out=LIDT[:], in0=LI[:], in1=DT[:], op=ALU.mult), r=[b0], w=[bt])

                def trig(mt_ap, K, cos_out, sin_out):
                    shp = [128, G, K]
                    a_, b_, c_ = tA[:, :, 0:K], tB[:, :, 0:K], tC[:, :, 0:K]
                    V(lambda e: e.tensor_tensor(out=a_, in0=LIDT[:].unsqueeze(2).to_broadcast(shp),
                                                in1=mt_ap.unsqueeze(1).to_broadcast(shp), op=ALU.mult),
                      r=[bt, b0], w=[bt])
                    for (outp, off) in ((sin_out, 0.0), (cos_out, 0.25)):
                        if outp is None:
                            continue
                        V(lambda e, off=off: e.tensor_scalar(out=c_, in0=a_, scalar1=off, scalar2=None,
                                                             op0=ALU.add), r=[bt], w=[bt])
                        V(lambda e: e.tensor_scalar(out=b_, in0=c_, scalar1=MAGIC, scalar2=None, op0=ALU.add),
                          r=[bt], w=[bt])
                        V(lambda e: e.tensor_scalar(out=b_, in0=b_, scalar1=MAGIC, scalar2=None, op0=ALU.subtract),
                          r=[bt], w=[bt])
                        V(lambda e: e.tensor_tensor(out=c_, in0=c_, in1=b_, op=ALU.subtract), r=[bt], w=[bt])
                        A(lambda e, outp=outp: e.activation(out=outp, in_=c_, func=AF.Sin, scale=TWO_PI),
                          r=[bt], w=[b_tab])

                trig(MT1, K1, COSJ[:], SINJ[:])
                trig(MRT, 64, COSR[:], SINR[:])
                shpj = [128, G, K1]
                V(lambda e: e.tensor_tensor(out=MAGJ[:], in0=LRDT[:].unsqueeze(2).to_broadcast(shpj),
                                            in1=MLr.unsqueeze(1).to_broadcast(shpj), op=ALU.mult),
                  r=[bt, b0], w=[b_tab])
                A(lambda e: e.activation(out=MAGJ[:], in_=MAGJ[:], func=AF.Exp), r=[b_tab], w=[b_tab])
                V(lambda e: e.tensor_tensor(out=AR[:], in0=MAGJ[:], in1=COSJ[:], op=ALU.mult), r=[b_tab], w=[b_tab])
                V(lambda e: e.tensor_tensor(out=AI[:], in0=MAGJ[:], in1=SINJ[:], op=ALU.mult), r=[b_tab], w=[b_tab])
                em1, shalf, cm1, am1r, ai1, den, fr, fi, t0_, t1_ = [sm[:, i, :] for i in range(10)]
                x_ = LRDT[:]
                V(lambda e: e.tensor_scalar(out=em1, in0=x_, scalar1=0.2, scalar2=1.0, op0=ALU.mult, op1=ALU.add),
                  r=[bt], w=[bt])
                for cf in (0.25, 1.0 / 3.0, 0.5):
                    V(lambda e: e.tensor_tensor(out=em1, in0=em1, in1=x_, op=ALU.mult), r=[bt], w=[bt])
                    V(lambda e, cf=cf: e.tensor_scalar(out=em1, in0=em1, scalar1=cf, scalar2=1.0, op0=ALU.mult,
                                                       op1=ALU.add), r=[bt], w=[bt])
                V(lambda e: e.tensor_tensor(out=em1, in0=em1, in1=x_, op=ALU.mult), r=[bt], w=[bt])
                V(lambda e: e.tensor_copy(out=shalf, in_=SINJ[:, :, I_HALF]), r=[b_tab], w=[bt])
                V(lambda e: e.scalar_tensor_tensor(out=cm1, in0=shalf, scalar=-2.0, op0=ALU.mult, in1=shalf,
                                                   op1=ALU.mult), r=[bt], w=[bt])
                V(lambda e: e.tensor_tensor(out=am1r, in0=em1, in1=COSJ[:, :, I_A1], op=ALU.mult), r=[bt, b_tab], w=[bt])
                V(lambda e: e.tensor_tensor(out=am1r, in0=am1r, in1=cm1, op=ALU.add), r=[bt], w=[bt])
                V(lambda e: e.tensor_copy(out=ai1, in_=AI[:, :, I_A1]), r=[b_tab], w=[bt])
                V(lambda e: e.tensor_tensor(out=den, in0=LR[:], in1=LR[:], op=ALU.mult), r=[b0], w=[bt])
                V(lambda e: e.tensor_tensor(out=t0_, in0=LI[:], in1=LI[:], op=ALU.mult), r=[b0], w=[bt])
                V(lambda e: e.tensor_tensor(out=den, in0=den, in1=t0_, op=ALU.add), r=[bt], w=[bt])
                V(lambda e: e.reciprocal(out=den, in_=den), r=[bt], w=[bt])
                V(lambda e: e.tensor_tensor(out=fr, in0=am1r, in1=LR[:], op=ALU.mult), r=[bt, b0], w=[bt])
                V(lambda e: e.tensor_tensor(out=t0_, in0=ai1, in1=LI[:], op=ALU.mult), r=[bt, b0], w=[bt])
                V(lambda e: e.tensor_tensor(out=fr, in0=fr, in1=t0_, op=ALU.add), r=[bt], w=[bt])
                V(lambda e: e.tensor_tensor(out=fr, in0=fr, in1=den, op=ALU.mult), r=[bt], w=[bt])
                V(lambda e: e.tensor_tensor(out=fi, in0=ai1, in1=LR[:], op=ALU.mult), r=[bt, b0], w=[bt])
                V(lambda e: e.tensor_tensor(out=t0_, in0=am1r, in1=LI[:], op=ALU.mult), r=[bt, b0], w=[bt])
                V(lambda e: e.tensor_tensor(out=fi, in0=fi, in1=t0_, op=ALU.subtract), r=[bt], w=[bt])
                V(lambda e: e.tensor_tensor(out=fi, in0=fi, in1=den, op=ALU.mult), r=[bt], w=[bt])
                V(lambda e: e.tensor_scalar(out=Br2[:], in0=Br2[:], scalar1=sgn[:, 1:2], scalar2=None, op0=ALU.mult),
                  r=[b0, b_const], w=[b0])
                shb = [128, G, 16]
                frb = fr.unsqueeze(2).to_broadcast(shb)
                fib = fi.unsqueeze(2).to_broadcast(shb)
                V(lambda e: e.tensor_tensor(out=BB1[:], in0=Br1[:], in1=frb, op=ALU.mult), r=[b0, bt], w=[bt])
                V(lambda e: e.tensor_tensor(out=tb1[:], in0=Br2[:], in1=fib, op=ALU.mult), r=[b0, bt], w=[bt])
                V(lambda e: e.tensor_tensor(out=BB1[:], in0=BB1[:], in1=tb1[:], op=ALU.add), r=[bt], w=[bt])
                V(lambda e: e.tensor_tensor(out=BB2[:], in0=Br2[:], in1=frb, op=ALU.mult), r=[b0, bt], w=[bt])
                V(lambda e: e.tensor_tensor(out=tb1[:], in0=Br1[:], in1=fib, op=ALU.mult), r=[b0, bt], w=[bt])
                V(lambda e: e.tensor_tensor(out=BB2[:], in0=BB2[:], in1=tb1[:], op=ALU.subtract), r=[bt], w=[bt])
                sh4 = [128, G, 8, 16]
                arv = AR[:, :, 0:8].unsqueeze(3).to_broadcast(sh4)
                aiv = AI[:, :, 0:8].unsqueeze(3).to_broadcast(sh4)
                bb1 = BB1[:].unsqueeze(2).to_broadcast(sh4)
                bb2 = BB2[:].unsqueeze(2).to_broadcast(sh4)
                g1 = big1[:].rearrange("p g (s h) -> p g s h", s=8)
                g2 = big2[:].rearrange("p g (s h) -> p g s h", s=8)
                V(lambda e: e.memset(WTpad[:], 0.0), w=[bt])
                V(lambda e: e.tensor_tensor(out=g1, in0=arv, in1=bb1, op=ALU.mult), r=[b_tab, bt], w=[bt])
                V(lambda e: e.tensor_tensor(out=g2, in0=aiv, in1=bb2, op=ALU.mult), r=[b_tab, bt], w=[bt])
                V(lambda e: e.tensor_tensor(out=WTpad[:, :, 0:128], in0=big1[:], in1=big2[:], op=ALU.add),
                  r=[bt], w=[bt])
                V(lambda e: e.tensor_tensor(out=g1, in0=arv, in1=bb2, op=ALU.mult), r=[b_tab, bt], w=[bt])
                V(lambda e: e.tensor_tensor(out=g2, in0=aiv, in1=bb1, op=ALU.mult), r=[b_tab, bt], w=[bt])
                V(lambda e: e.tensor_tensor(out=WTs[:], in0=big1[:], in1=big2[:], op=ALU.subtract), r=[bt], w=[bt])
                for (src_fn, dstt) in ((lambda g: WTpad[:, g, 0:128], Wt), (lambda g: WTs[:, g, :], Wst)):
                    for gq in range(8):
                        bank = gq % 2
                        pv = ps_bf(bank)
                        for j in range(4):
                            g = gq * 4 + j
                            T(lambda e, g=g, j=j, pv=pv, src_fn=src_fn: e.transpose(
                                out=pv[:, j * 128:(j + 1) * 128], in_=src_fn(g), identity=identb[:]),
                              r=[bt, b_const], w=[bPS[bank]])
                        A(lambda e, gq=gq, pv=pv, dstt=dstt: e.copy(
                            out=dstt[:, gq * 4:gq * 4 + 4, :], in_=pv[:, 0:512].rearrange("p (j c) -> p j c", j=4)),
                          r=[bPS[bank]], w=[b_tab])
                for (CN, CM, col) in ((CN1, CMa, 0), (CN2, CMb, None)):
                    for c4 in range(4):
                        bank = 2 + (c4 % 2)
                        T(lambda e, CN=CN, c4=c4, bank=bank: e.transpose(out=PS[bank][:, 0:128], in_=CN[:, c4, :],
                                                                         identity=identf[:]),
                          r=[b0, b_const], w=[bPS[bank]])
                        if col is not None:
                            V(lambda e, CM=CM, c4=c4, bank=bank: e.tensor_scalar(
                                out=CM[:, c4 * 8:(c4 + 1) * 8, :],
                                in0=PS[bank][:, 0:128].rearrange("p (g h) -> p g h", g=8),
                                scalar1=sgn[:, 0:1], scalar2=None, op0=ALU.mult),
                              r=[bPS[bank], b_const], w=[bt])
                        else:
                            V(lambda e, CM=CM, c4=c4, bank=bank: e.tensor_scalar(
                                out=CM[:, c4 * 8:(c4 + 1) * 8, :],
                                in0=PS[bank][:, 0:128].rearrange("p (g h) -> p g h", g=8),
                                scalar1=-1.0, scalar2=None, op0=ALU.mult),
                              r=[bPS[bank]], w=[bt])
                V(lambda e: e.tensor_copy(out=CMab[:], in_=CMa[:]), r=[bt], w=[bt])
                afw = AR[:, :, 8:16].unsqueeze(3).to_broadcast(sh4)
                aifw = AI[:, :, 8:16].unsqueeze(3).to_broadcast(sh4)
                cma = CMa[:].unsqueeze(2).to_broadcast(sh4)
                cmb = CMb[:].unsqueeze(2).to_broadcast(sh4)
                V(lambda e: e.tensor_tensor(out=g1, in0=afw, in1=cma, op=ALU.mult), r=[b_tab, bt], w=[bt])
                V(lambda e: e.tensor_tensor(out=g2, in0=aifw, in1=cmb, op=ALU.mult), r=[b_tab, bt], w=[bt])
                V(lambda e: e.tensor_tensor(out=Vt[:], in0=big1[:], in1=big2[:], op=ALU.add), r=[bt], w=[b_tab])
                for gq in range(8):
                    bank = 4 + (gq % 2)
                    for j in range(4):
                        g = gq * 4 + j
                        for tau in range(8):
                            c0 = (7 - tau) * 16
                            T(lambda e, g=g, j=j, tau=tau, c0=c0, bank=bank: e.matmul(
                                PS[bank][:, j * 128 + tau * 16:j * 128 + tau * 16 + 16],
                                lhsT=WTpad[:, g, c0:c0 + 128], rhs=CMab[:, g, :], start=True, stop=True),
                              r=[bt], w=[bPS[bank]])
                    A(lambda e, gq=gq, bank=bank: e.copy(
                        out=Tt[:, gq * 4:gq * 4 + 4, :], in_=PS[bank][:].rearrange("p (j c) -> p j c", j=4)),
                      r=[bPS[bank]], w=[b_tab])
                S.barrier()
            xst = [alloc(sa, "xst%d" % i, [128, D]) for i in range(2)]
            bxst = [Buf("xst%d" % i, S.GL[i]) for i in range(2)]
            scrA = make_scr(sa, "A", [7])
            bscr = Buf("scrA")
            hT = alloc(sa, "hT", [128, 8, 512], BF16)
            bhT = Buf("hT")
            uT = alloc(sa, "uT", [128, 4, 512], BF16)
            buT = Buf("uT")
            U = alloc(sa, "U", [128, G, 64], BF16)
            bU = Buf("U")
            rr = alloc(sa, "rr", [128, G, 64])
            rs = alloc(sa, "rs", [128, G, 64])
            ww = alloc(sa, "ww", [128, G, 64])
            ws = alloc(sa, "ws", [128, G, 64])
            tmpr = alloc(sa, "tmpr", [128, 16, 64])
            b_r, b_rs, b_w, b_ws, b_tmpr = Buf("r"), Buf("rs"), Buf("w"), Buf("ws"), Buf("tmpr")
            Xb = alloc(sa, "Xb", [128, G, 65], BF16)
            bXb = Buf("Xb")
            Xc = alloc(sa, "Xc", [128, G])
            Xsc = alloc(sa, "Xsc", [128, G])
            ctmp = alloc(sa, "ctmp", [128, 2, G])
            bXc = Buf("Xc", S.GS[0])
            ytmp = alloc(sa, "ytmp", [128, 8, 64])
            bytmp = Buf("ytmp")
            Zt = alloc(sa, "Zt", [128, G, 64], BF16)
            bZ = Buf("Z")
            zT = alloc(sa, "zT", [128, 4, 512], BF16)
            bzT = Buf("zT")
            sig = alloc(sa, "sig", [128, 4, 512])
            bsig = Buf("sig")
            H0 = alloc(sa, "H0", [128, 512])
            H0s = alloc(sa, "H0s", [128, 512])
            hn = alloc(sa, "hn", [128, 4, 128])
            hn2 = alloc(sa, "hn2", [128, 4, 128])
            Hp = alloc(sa, "Hp", [128, G, 16])
            Xf = alloc(sa, "Xf", [128, G, 16])
            xo = alloc(sa, "xo", [128, 4, 128])
            bH = Buf("H0")
            bxo = Buf("xo", S.GS[1])
            V(lambda e: e.memset(Xc[:], 0.0), r=[b_tabp], w=[bXc, b_tab])
            V(lambda e: e.memset(Xsc[:], 0.0), w=[bXc])
            V(lambda e: e.memset(Xb[:], 0.0), w=[bXb])

            blocks = [(i * 512, 512, False) for i in range(4)] + [(SEQ, TS, True)]
            if _os0.environ.get("K1A") == "0":
                blocks = []
            for bi, (t0, n, is_s) in enumerate(blocks):
                nch = n // 8 if not is_s else 16
                ntile = (n + 127) // 128
                for ti in range(ntile):
                    npart = min(128, n - ti * 128)
                    slot = (bi * 4 + ti) % 2
                    src = I["xs"][:, :] if is_s else I["xp"][t0 + ti * 128:t0 + ti * 128 + 128, :]
                    S.dma("sp", xst[slot][:npart, :], src, writes=[bxst[slot]])
                    rmsnorm_hT(xst[slot][:npart, :], bxst[slot], npart, gm[:], hT, bhT,
                               scrA, ti * 128, None, bg=b_tab)
                for ct in range(4):
                    bank = ct
                    for kt in range(8):
                        T(lambda e, ct=ct, kt=kt, bank=bank: e.matmul(
                            PS[bank][:, 0:n], lhsT=winu[:, kt, ct * 128:(ct + 1) * 128], rhs=hT[:, kt, 0:n],
                            start=(kt == 0), stop=(kt == 7)), r=[b_winu, bhT], w=[bPS[bank]])
                    A(lambda e, ct=ct, bank=bank: e.copy(out=uT[:, ct, 0:n], in_=PS[bank][:, 0:n]),
                      r=[bPS[bank]], w=[buT])
                for gq in range(4):
                    bank = 4 + (gq % 2)
                    for j in range(8):
                        g = gq * 8 + j
                        ct, gl = g // 8, g % 8
                        if not is_s:
                            uv = uT[:, ct, 0:n].rearrange("p (c s) -> p s c", s=8)
                            sig_list = list(range(8))
                        else:
                            uv = uT[:, ct, 0:n].rearrange("p (b t) -> p t b", t=4)
                            sig_list = [4, 5, 6, 7]
                        for si, sg_ in enumerate(sig_list):
                            rhs = uv[:, sg_ if not is_s else si, :]
                            T(lambda e, j=j, gl=gl, sg_=sg_, rhs=rhs, si=si, bank=bank, L=len(sig_list): e.matmul(
                                PS[bank][:, j * 64:j * 64 + nch],
                                lhsT=masters[:, gl, 112 - 16 * sg_:240 - 16 * sg_], rhs=rhs,
                                start=(si == 0), stop=(si == L - 1)),
                              r=[b_tab, buT], w=[bPS[bank]])
                    A(lambda e, gq=gq, bank=bank: e.copy(
                        out=U[:, gq * 8:gq * 8 + 8, 0:nch],
                        in_=PS[bank][:].rearrange("p (j c) -> p j c", j=8)[:, :, 0:nch]),
                      r=[bPS[bank]], w=[bU])
                if not is_s:
                    for hf in range(2):
                        for j in range(16):
                            g = hf * 16 + j
                            for (wt, bk) in ((Wt, 0), (Wst, 2)):
                                bank = bk + j // 8
                                T(lambda e, g=g, j=j, wt=wt, bank=bank: e.matmul(
                                    PS[bank][:, (j % 8) * 64:(j % 8) * 64 + 64], lhsT=wt[:, g, :], rhs=U[:, g, :],
                                    start=True, stop=True), r=[b_tab, bU], w=[bPS[bank]])
                        for q in range(2):
                            gs = slice(hf * 16 + q * 8, hf * 16 + q * 8 + 8)
                            Sv = PS[q][:].rearrange("p (j c) -> p j c", j=8)
                            Ssv = PS[2 + q][:].rearrange("p (j c) -> p j c", j=8)
                            tm = tmpr[:, q * 8:q * 8 + 8, :]
                            V(lambda e, gs=gs, Sv=Sv: e.tensor_tensor(out=rr[:, gs, :], in0=Sv, in1=COSR[:, gs, :],
                                                                     op=ALU.mult), r=[bPS[q], b_tab], w=[b_r])
                            V(lambda e, gs=gs, Ssv=Ssv, tm=tm: e.tensor_tensor(out=tm, in0=Ssv, in1=SINR[:, gs, :],
                                                                              op=ALU.mult),
                              r=[bPS[2 + q], b_tab], w=[b_tmpr])
                            V(lambda e, gs=gs, tm=tm: e.tensor_tensor(out=rr[:, gs, :], in0=rr[:, gs, :], in1=tm,
                                                                     op=ALU.subtract), r=[b_r, b_tmpr], w=[b_r])
                            V(lambda e, gs=gs, Ssv=Ssv: e.tensor_tensor(out=rs[:, gs, :], in0=Ssv, in1=COSR[:, gs, :],
                                                                       op=ALU.mult), r=[bPS[2 + q], b_tab], w=[b_rs])
                            V(lambda e, gs=gs, Sv=Sv, tm=tm: e.tensor_tensor(out=tm, in0=Sv, in1=SINR[:, gs, :],
                                                                            op=ALU.mult),
                              r=[bPS[q], b_tab], w=[b_tmpr])
                            V(lambda e, gs=gs, tm=tm: e.tensor_tensor(out=rs[:, gs, :], in0=rs[:, gs, :], in1=tm,
                                                                     op=ALU.add), r=[b_rs, b_tmpr], w=[b_rs])
                    for g in range(G):
                        rho = MAGJ[:, g, I_A8:I_A8 + 1].to_broadcast([128, 64])
                        V(lambda e, g=g, rho=rho: e.tensor_tensor_scan(
                            out=ww[:, g, :], data0=rho, data1=rr[:, g, :], initial=Xc[:, g:g + 1], op0=ALU.mult,
                            op1=ALU.add), r=[b_r, b_tab, bXc], w=[b_w])
                        V(lambda e, g=g, rho=rho: e.tensor_tensor_scan(
                            out=ws[:, g, :], data0=rho, data1=rs[:, g, :], initial=Xsc[:, g:g + 1], op0=ALU.mult,
                            op1=ALU.add), r=[b_rs, b_tab, bXc], w=[b_ws])
                    ce, se_ = COSR[:, :, 63], SINR[:, :, 63]
                    we, wse = ww[:, :, 63], ws[:, :, 63]
                    V(lambda e: e.tensor_tensor(out=ctmp[:, 0, :], in0=ce, in1=we, op=ALU.mult), r=[b_w, b_tab], w=[bscr])
                    V(lambda e: e.tensor_tensor(out=ctmp[:, 1, :], in0=se_, in1=wse, op=ALU.mult), r=[b_ws, b_tab], w=[bscr])
                    V(lambda e: e.tensor_tensor(out=Xc[:], in0=ctmp[:, 0, :], in1=ctmp[:, 1, :], op=ALU.add),
                      r=[bscr], w=[bXc])
                    V(lambda e: e.tensor_tensor(out=ctmp[:, 0, :], in0=ce, in1=wse, op=ALU.mult), r=[b_ws, b_tab], w=[bscr])
                    V(lambda e: e.tensor_tensor(out=ctmp[:, 1, :], in0=se_, in1=we, op=ALU.mult), r=[b_w, b_tab], w=[bscr])
                    V(lambda e: e.tensor_tensor(out=Xsc[:], in0=ctmp[:, 0, :], in1=ctmp[:, 1, :], op=ALU.subtract),
                      r=[bscr], w=[bXc])
                    if bi > 0:
                        V(lambda e: e.tensor_copy(out=Xb[:, :, 0], in_=Xb[:, :, 64]), r=[bXb], w=[bXb])
                    PL(lambda e: e.tensor_tensor(out=ww[:], in0=ww[:], in1=COSR[:], op=ALU.mult), r=[b_w, b_tab, bXc],
                       w=[b_w])
                    PL(lambda e: e.tensor_tensor(out=ws[:], in0=ws[:], in1=SINR[:], op=ALU.mult), r=[b_ws, b_tab, bXc],
                       w=[b_ws])
                    PL(lambda e: e.tensor_tensor(out=Xb[:, :, 1:65], in0=ww[:], in1=ws[:], op=ALU.add),
                       r=[b_w, b_ws], w=[bXb])
                    xprev = lambda g: Xb[:, g, 0:64]
                    bXprev = bXb
                    if bi == 3:
                        S.dma("sp", O["o_s5r_p"].rearrange("g p -> p g"), Xc[0:64, :], reads=[bXc])
                        S.dma("sp", O["o_s5i_p"].rearrange("g p -> p g"), Xc[64:128, :], reads=[bXc])
                else:
                    S.dma("sp", hn[:, :, 0:64], I["s5r"].rearrange("(j r) p -> r j p", r=128), writes=[bH])
                    S.dma("sp", hn[:, :, 64:128], I["s5i"].rearrange("(j r) p -> r j p", r=128), writes=[bH])
                    S.dma("sp", hn2[:, :, 0:64], I["s5i"].rearrange("(j r) p -> r j p", r=128), writes=[bH])
                    S.dma("sp", hn2[:, :, 64:128], I["s5r"].rearrange("(j r) p -> r j p", r=128), writes=[bH])
                    for (src_, dst_, bank) in ((hn, H0, 0), (hn2, H0s, 1)):
                        for j in range(4):
                            T(lambda e, src_=src_, j=j, bank=bank: e.transpose(
                                out=PS[bank][:, j * 128:(j + 1) * 128], in_=src_[:, j, :], identity=identf[:]),
                              r=[bH, b_const], w=[bPS[bank]])
                        V(lambda e, dst_=dst_, bank=bank: e.tensor_copy(out=dst_[:], in_=PS[bank][:]),
                          r=[bPS[bank]], w=[bH])
                    V(lambda e: e.tensor_scalar(out=H0s[0:64, :], in0=H0s[0:64, :], scalar1=-1.0, scalar2=None,
                                                op0=ALU.mult), r=[bH], w=[bH])
                    shs = [128, G, 16]
                    h0v = H0[:].rearrange("p (b g) -> p g b", g=G)
                    h0sv = H0s[:].rearrange("p (b g) -> p g b", g=G)

                    def abc(tab, idx):
                        return tab[:, :, idx].unsqueeze(2).to_broadcast(shs)
                    V(lambda e: e.tensor_tensor(out=Xf[:], in0=h0v, in1=abc(AR, I_AM4), op=ALU.mult), r=[bH, b_tab], w=[bxo])
                    V(lambda e: e.tensor_tensor(out=Hp[:], in0=h0sv, in1=abc(AI, I_AM4), op=ALU.mult), r=[bH, b_tab], w=[bxo])
                    V(lambda e: e.tensor_tensor(out=Xb[:, :, 0:16], in0=Xf[:], in1=Hp[:], op=ALU.add), r=[bxo], w=[bXb])
                    V(lambda e: e.tensor_tensor(out=Xf[:], in0=h0v, in1=abc(AR, I_A4), op=ALU.mult), r=[bH, b_tab], w=[bxo])
                    V(lambda e: e.tensor_tensor(out=Hp[:], in0=h0sv, in1=abc(AI, I_A4), op=ALU.mult), r=[bH, b_tab], w=[bxo])
                    V(lambda e: e.tensor_tensor(out=Xf[:], in0=Xf[:], in1=Hp[:], op=ALU.add), r=[bxo], w=[bxo])
                    for q in range(4):
                        bank = q % 2
                        for j in range(8):
                            g = q * 8 + j
                            T(lambda e, g=g, j=j, bank=bank: e.matmul(
                                PS[bank][:, j * 64:j * 64 + 16], lhsT=Wt[:, g, :], rhs=U[:, g, 0:16],
                                start=True, stop=True), r=[b_tab, bU], w=[bPS[bank]])
                        V(lambda e, q=q, bank=bank: e.tensor_tensor(
                            out=Xf[:, q * 8:q * 8 + 8, :], in0=Xf[:, q * 8:q * 8 + 8, :],
                            in1=PS[bank][:].rearrange("p (j c) -> p j c", j=8)[:, :, 0:16], op=ALU.add),
                          r=[bxo, bPS[bank]], w=[bxo])
                    Xf2 = Xf[:].rearrange("p g b -> p (g b)")
                    for j in range(4):
                        T(lambda e, j=j: e.transpose(out=PS[2][:, j * 128:(j + 1) * 128],
                                                     in_=Xf2[:, j * 128:(j + 1) * 128], identity=identf[:]),
                          r=[bxo, b_const], w=[bPS[2]])
                    V(lambda e: e.tensor_copy(out=xo[:], in_=PS[2][:].rearrange("p (j c) -> p j c", j=4)),
                      r=[bPS[2]], w=[bxo])
                    for j in range(4):
                        for gl in range(8):
                            for (nm, c0) in (("o_s5r_s", 0), ("o_s5i_s", 64)):
                                S.dma("sp", O[nm].rearrange("(b g) p -> g b p", g=G)[8 * j + gl],
                                      xo[gl * 16:gl * 16 + 16, j, c0:c0 + 64], reads=[bxo])
                    xprev = lambda g: Xb[:, g, 0:16]
                    bXprev = bXb
                for gq in range(4):
                    bank = 6 + (gq % 2)
                    for j in range(8):
                        g = gq * 8 + j
                        T(lambda e, g=g, j=j, bank=bank: e.matmul(
                            PS[bank][:, j * 64:j * 64 + nch], lhsT=Tt[:, g, :], rhs=U[:, g, 0:nch],
                            start=True, stop=False), r=[b_tab, bU], w=[bPS[bank]])
                        T(lambda e, g=g, j=j, bank=bank: e.matmul(
                            PS[bank][:, j * 64:j * 64 + nch], lhsT=Vt[:, g, :], rhs=xprev(g)[:, 0:nch],
                            start=False, stop=True), r=[b_tab, bXprev], w=[bPS[bank]])
                    gs = slice(gq * 8, gq * 8 + 8)
                    yv = PS[bank][:].rearrange("p (j c) -> p j c", j=8)[:, :, 0:nch]
                    V(lambda e, gs=gs: e.tensor_tensor(out=ytmp[:, :, 0:nch], in0=U[:, gs, 0:nch],
                                                       in1=DS[:, gs].unsqueeze(2).to_broadcast([128, 8, nch]),
                                                       op=ALU.mult), r=[bU, b_tab], w=[bytmp])
                    V(lambda e, yv=yv: e.tensor_tensor(out=ytmp[:, :, 0:nch], in0=yv, in1=ytmp[:, :, 0:nch],
                                                       op=ALU.add), r=[bPS[bank], bytmp], w=[bytmp])
                    A(lambda e, gs=gs: e.activation(out=Zt[:, gs, 0:nch], in_=ytmp[:, :, 0:nch],
                                                    func=AF.Gelu_apprx_tanh), r=[bytmp], w=[bZ])
                for ct in range(4):
                    bank = ct % 2
                    taus = list(range(8)) if not is_s else [4, 5, 6, 7]
                    for ti_, tau in enumerate(taus):
                        for gl in range(8):
                            g = ct * 8 + gl
                            T(lambda e, g=g, gl=gl, tau=tau, ti_=ti_, bank=bank: e.matmul(
                                PS[bank][:, ti_ * 64:ti_ * 64 + nch],
                                lhsT=masters[:, tau, 112 - 16 * gl:240 - 16 * gl], rhs=Zt[:, g, 0:nch],
                                start=(gl == 0), stop=(gl == 7)), r=[b_tab, bZ], w=[bPS[bank]])
                    if not is_s:
                        A(lambda e, ct=ct, bank=bank: e.copy(
                            out=zT[:, ct, 0:n].rearrange("p (c t) -> p t c", t=8),
                            in_=PS[bank][:].rearrange("p (t c) -> p t c", t=8)), r=[bPS[bank]], w=[bzT])
                    else:
                        A(lambda e, ct=ct, bank=bank: e.copy(
                            out=zT[:, ct, 0:n].rearrange("p (b t) -> p t b", t=4),
                            in_=PS[bank][:].rearrange("p (t c) -> p t c", t=8)[:, 0:4, 0:16]),
                          r=[bPS[bank]], w=[bzT])
                for ct in range(4):
                    bank = 2 + (ct % 2)
                    for kt in range(4):
                        T(lambda e, ct=ct, kt=kt, bank=bank: e.matmul(
                            PS[bank][:, 0:n], lhsT=wglu[:, kt, ct * 128:(ct + 1) * 128], rhs=zT[:, kt, 0:n],
                            start=(kt == 0), stop=(kt == 3)), r=[b_wglu, bzT], w=[bPS[bank]])
                    A(lambda e, ct=ct, bank=bank: e.activation(out=sig[:, ct, 0:n], in_=PS[bank][:, 0:n],
                                                               func=AF.Sigmoid), r=[bPS[bank]], w=[bsig])
                V(lambda e: e.tensor_tensor(out=ssmT[:, :, t0:t0 + n], in0=zT[:, :, 0:n], in1=sig[:, :, 0:n],
                                            op=ALU.mult), r=[bzT, bsig], w=[b_ssmT[bi]])
            S.barrier()
        if dbg:
            with ExitStack() as sd:
                dtmp = alloc(sd, "dtmp", [128, 4, NTOK])
                bd = Buf("dtmp", S.GS[2])
                V(lambda e: e.tensor_copy(out=dtmp[:], in_=ssmT[:]), r=b_ssmT, w=[bd])
                S.dma("sp", O["dbg_ssm"][:, :, :], dtmp[:], reads=[bd])
                S.barrier()
        if stage <= 1:
            S.barrier()
            S.run_block()
            nck.__exit__(None, None, None)
            return nc

        with ExitStack() as sbx:
            x = alloc(sbx, "x", [128, NT, D])
            bx = [Buf("x%d" % n, S.GX) for n in range(NT)]
            for n in range(NTP):
                S.dma("sp", x[:, n, :], I["xp"][n * 128:(n + 1) * 128, :], writes=[bx[n]])
            S.dma("sp", x[0:TS, 16, :], I["xs"][:, :], writes=[bx[16]])
            scrB = make_scr(sbx, "B", [7])
            hT1 = alloc(sbx, "hT1", [128, 8, 128], BF16)
            bhT1 = Buf("hT1")

            def resid_add(n, npart, half, bank):
                V(lambda e: e.tensor_tensor(out=x[:npart, n, half * 512:(half + 1) * 512], in0=PS[bank][:npart, :],
                                            in1=x[:npart, n, half * 512:(half + 1) * 512], op=ALU.add),
                  r=[bPS[bank], bx[n]], w=[bx[n]])

            with ExitStack() as s1:
                wq = alloc(s1, "wqkvg", [128, 8, 2048], BF16)
                wout = alloc(s1, "wout", [128, 8, D], BF16)
                b_wq, b_wout = Buf("wq", S.GW[2]), Buf("wout", S.GW[3])
                load_w_bf16(wq, b_wq, I["w_in"], 8, 2048, 512)
                load_w_bf16(wout, b_wout, I["w_out"], 8, D, 0)
                gm2 = alloc(s1, "gm2", [128, 8])
                gn = alloc(s1, "gn", [128, 4])
                rope = alloc(s1, "rope", [128, 3, NT, 64])
                dmp = alloc(s1, "dmp", [128, 512])
                dms = alloc(s1, "dms", [64, 256])
                xi = alloc(s1, "xi", [128, 768])
                zetap = alloc(s1, "zetap", [128, 4])
                zs = alloc(s1, "zs", [64, 64])
                cmask = alloc(s1, "cmask", [128, 16 * 64])
                b_t1 = Buf("tab1")
                S.dma("sp", gm2[:], I["g_mix"].rearrange("(k p) -> p k", p=128), writes=[b_t1])
                S.dma("sp", gn[:], I["ret_gn"].rearrange("(k p) -> p k", p=128), writes=[b_t1])
                for a_ in range(3):
                    S.dma("sp", rope[:, a_, :, :], I["c_rope"][a_], writes=[b_t1])
                S.dma("sp", dmp[:], I["c_dmask_p"][:, :], writes=[b_t1])
                S.dma("sp", dms[:], I["c_dmask_s"][:, :], writes=[b_t1])
                S.dma("sp", xi[:], I["c_xi"][0:1, :].partition_broadcast(128), writes=[b_t1])
                S.dma("sp", zetap[:], I["c_zeta_p"][:, :], writes=[b_t1])
                S.dma("sp", zs[:], I["c_zs"][:, :], writes=[b_t1])
                S.dma("sp", cmask[:], I["c_cmask"][0:1, :].partition_broadcast(128), writes=[b_t1])
                for k in range(4):
                    V(lambda e: e.tensor_scalar(out=wout[:, 4 + k, :], in0=wout[:, 4 + k, :], scalar1=gn[:, k:k + 1],
                                                scalar2=None, op0=ALU.mult), r=[b_wout, b_t1], w=[b_wout])
                t1q = alloc(s1, "t1q", [128, 512])
                t2q = alloc(s1, "t2q", [128, 512])
                t1k = alloc(s1, "t1k", [128, 512])
                t2k = alloc(s1, "t2k", [128, 512])
                qr = alloc(s1, "qr", [128, 512], BF16)
                kr = alloc(s1, "kr", [128, 512], BF16)
                qT = alloc(s1, "qT", [128, 4, 128], BF16)
                qxT = alloc(s1, "qxT", [128, 4, 128], BF16)
                kT = alloc(s1, "kT", [128, 4, 128], BF16)
                vb = alloc(s1, "vb", [128, 512], BF16)
                vz = alloc(s1, "vz", [128, 512], BF16)
                sg_ = alloc(s1, "sgl", [128, 512])
                sT = alloc(s1, "sT", [128, 4, 128], BF16)
                Sst = alloc(s1, "Sst", [128, 4, 128])
                Sbf = alloc(s1, "Sbf", [128, 4, 128], BF16)
                stats = alloc(s1, "stats", [128, 4, 6])
                mv = alloc(s1, "mv", [128, 4, 2])
                rs4 = alloc(s1, "rs4", [128, 4])
                nb4 = alloc(s1, "nb4", [128, 4])
                on = alloc(s1, "on", [128, 512])
                ret = alloc(s1, "ret", [128, 512], BF16)
                retT = alloc(s1, "retT", [128, 4, 128], BF16)
                S0 = [alloc(s1, "S0_%d" % i, [128, 4, 128]) for i in range(2)]
                S0b = [alloc(s1, "S0b_%d" % i, [128, 4, 128], BF16) for i in range(2)]
                qxm = [alloc(s1, "qxm_%d" % i, [128, 4, 64], BF16) for i in range(2)]
                vzb = [alloc(s1, "vzb_%d" % i, [64, 512], BF16) for i in range(2)]
                Sn = [alloc(s1, "Sn_%d" % i, [128, 4, 128]) for i in range(2)]
                bS0 = [Buf("S0_%d" % i, S.GL[i]) for i in range(2)]
                bS0b = [Buf("S0b_%d" % i) for i in range(2)]
                bqxm = [Buf("qxm%d" % i) for i in range(2)]
                bvzb = [Buf("vzb%d" % i) for i in range(2)]
                bSn = [Buf("Sn%d" % i, S.GS[i]) for i in range(2)]
                (b_t1q, b_t2q, b_t1k, b_t2k, b_qr, b_kr, b_qT, b_qxT, b_kT, b_vb, b_vz, b_sg, b_sT, b_Sst, b_Sbf,
                 b_st, b_on, b_ret, b_retT) = [Buf("p1b%d" % i) for i in range(19)]
                b_Sst.grp = S.GS[2]
                V(lambda e: e.memset(Sst[:], 0.0), w=[b_Sst])
                GC_P = [float(g ** 128) for g in GAM]
                GC_S = [float(g ** 4) for g in GAM]

                import os as _os
                _tl = _os.environ.get("K_TILES")
                _tiles = [int(v) for v in _tl.split(",") if int(v) >= 0] if _tl else list(range(NT))
                _step = int(_os.environ.get("K_STEP", "99"))
                hT1s = [hT1, alloc(s1, "hT1c", [128, 8, 128], BF16)]
                bhT1s = [bhT1, Buf("hT1c")]

                def p1b_norm(n):
                    npt_ = TS if n == 16 else 128
                    rmsnorm_hT(x[:npt_, n, :], bx[n], npt_, gm2[:], hT1s[n % 2], bhT1s[n % 2], scrB, 0, None, bg=b_t1)
                if _tiles:
                    p1b_norm(_tiles[0])
                for ti_, n in enumerate(_tiles):
                    is_s = (n == 16)
                    npt = TS if is_s else 128
                    tok0 = n * 128
                    hT1, bhT1 = hT1s[n % 2], bhT1s[n % 2]
                    pob = [4, 6, 7, 1] if is_s else [4, 4, 4, 4]

                    def po(h):
                        if is_s:
                            return PS[pob[h]][:npt, 0:128]
                        return PS[4][:npt, h * 128:(h + 1) * 128]
                    for c in range(4):
                        for kt in range(8):
                            T(lambda e: e.matmul(PS[c][:npt, :], lhsT=hT1[:, kt, 0:npt],
                                                 rhs=wq[:, kt, c * 512:(c + 1) * 512], start=(kt == 0), stop=(kt == 7)),
                              r=[bhT1, b_wq], w=[bPS[c]])
                    if _step <= 1:
                        continue
                    for (bank, t1_, t2_, out_, bt1, bt2, bo) in ((0, t1q, t2q, qr, b_t1q, b_t2q, b_qr),
                                                               (1, t1k, t2k, kr, b_t1k, b_t2k, b_kr)):
                        pv4 = PS[bank][:npt, :].rearrange("p (h a j) -> p h a j", h=4, a=2)
                        t1v = t1_[:npt, :].rearrange("p (h a j) -> p h a j", h=4, a=2)
                        t2v = t2_[:npt, :].rearrange("p (h a j) -> p h a j", h=4, a=2)
                        cosb = rope[:npt, 0, n, :].unsqueeze(1).unsqueeze(1).to_broadcast([npt, 4, 2, 64])
                        sinb = rope[:npt, 1, n, :].unsqueeze(1).to_broadcast([npt, 4, 64])
                        nsinb = rope[:npt, 2, n, :].unsqueeze(1).to_broadcast([npt, 4, 64])
                        V(lambda e: e.tensor_tensor(out=t1v, in0=pv4, in1=cosb, op=ALU.mult), r=[bPS[bank], b_t1], w=[bt1])
                        V(lambda e: e.tensor_tensor(out=t2v[:, :, 0, :], in0=pv4[:, :, 1, :], in1=nsinb, op=ALU.mult),
                          r=[bPS[bank], b_t1], w=[bt2])
                        V(lambda e: e.tensor_tensor(out=t2v[:, :, 1, :], in0=pv4[:, :, 0, :], in1=sinb, op=ALU.mult),
                          r=[bPS[bank], b_t1], w=[bt2])
                        V(lambda e: e.tensor_tensor(out=out_[:npt, :], in0=t1_[:npt, :], in1=t2_[:npt, :], op=ALU.add),
                           r=[bt1, bt2], w=[bo])
                    if _step <= 2:
                        continue
                    A(lambda e: e.copy(out=vb[:npt, :], in_=PS[2][:npt, :]), r=[bPS[2]], w=[b_vb])
                    if not is_s:
                        V(lambda e: e.tensor_tensor(
                            out=vz[:, :].rearrange("p (h e) -> p h e", h=4),
                            in0=PS[2][:, :].rearrange("p (h e) -> p h e", h=4),
                            in1=zetap[:, :].unsqueeze(2).to_broadcast([128, 4, 128]), op=ALU.mult),
                          r=[bPS[2], b_t1], w=[b_vz])
                    A(lambda e: e.activation(out=sg_[:npt, :], in_=PS[3][:npt, :], func=AF.Silu), r=[bPS[3]], w=[b_sg])
                    pv4b = ps_bf(4)
                    pv5b = ps_bf(5)
                    for h in range(4):
                        T(lambda e: e.transpose(out=pv4b[:, h * 128:h * 128 + npt], in_=qr[:npt, h * 128:(h + 1) * 128],
                                                identity=identb[:npt, :npt]), r=[b_qr, b_const], w=[bPS[4]])
                    for h in range(4):
                        T(lambda e: e.transpose(out=pv5b[:, h * 128:h * 128 + npt], in_=kr[:npt, h * 128:(h + 1) * 128],
                                                identity=identb[:npt, :npt]), r=[b_kr, b_const], w=[bPS[5]])
                    q4 = pv4b[:, 0:512].rearrange("p (h t) -> p h t", h=4)[:, :, 0:npt]
                    k4 = pv5b[:, 0:512].rearrange("p (h t) -> p h t", h=4)[:, :, 0:npt]
                    A(lambda e: e.copy(out=qT[:, :, 0:npt], in_=q4), r=[bPS[4]], w=[b_qT])
                    xiv = (xi[:, 0:512].rearrange("p (h t) -> p h t", h=4) if not is_s
                           else xi[:, 512:768].rearrange("p (h t) -> p h t", h=4))
                    V(lambda e: e.tensor_tensor(out=qxT[:, :, 0:npt], in0=q4, in1=xiv, op=ALU.mult),
                      r=[bPS[4], b_t1], w=[b_qxT])
                    A(lambda e: e.copy(out=kT[:, :, 0:npt], in_=k4), r=[bPS[5]], w=[b_kT])
                    if _step <= 3:
                        continue
                    for h in range(4):
                        T(lambda e: e.matmul(PS[6][:npt, h * 128:h * 128 + npt], lhsT=kT[:, h, 0:npt], rhs=qT[:, h, 0:npt],
                                             start=True, stop=True), r=[b_kT, b_qT], w=[bPS[6]])
                    dmv = (dmp[:, :].rearrange("p (h t) -> p h t", h=4) if not is_s
                           else dms[:, :].rearrange("p (h t) -> p h t", h=4))
                    V(lambda e: e.tensor_tensor(out=sT[:npt, :, 0:npt],
                                                in0=PS[6][:npt, :].rearrange("p (h t) -> p h t", h=4)[:, :, 0:npt],
                                                in1=dmv, op=ALU.mult), r=[bPS[6], b_t1], w=[b_sT])
                    if _step <= 4:
                        continue
                    if ti_ + 1 < len(_tiles):
                        p1b_norm(_tiles[ti_ + 1])
                    for h in range(4):
                        only = (n == 0)
                        T(lambda e: e.matmul(po(h), lhsT=sT[:npt, h, 0:npt],
                                             rhs=vb[:npt, h * 128:(h + 1) * 128], start=True, stop=only),
                          r=[b_sT, b_vb], w=[bPS[pob[h]]])
                        if (not is_s) and n > 0:
                            T(lambda e: e.matmul(po(h), lhsT=qxT[:, h, 0:npt],
                                                 rhs=Sbf[:, h, :], start=False, stop=True),
                              r=[b_qxT, b_Sbf], w=[bPS[4]])
                    if not is_s:
                        for h in range(4):
                            T(lambda e: e.matmul(PS[5][:, h * 128:(h + 1) * 128], lhsT=kr[:, h * 128:(h + 1) * 128],
                                                 rhs=vz[:, h * 128:(h + 1) * 128], start=True, stop=True),
                              r=[b_kr, b_vz], w=[bPS[5]])
                        for h in range(4):
                            V(lambda e: e.scalar_tensor_tensor(out=Sst[:, h, :], in0=Sst[:, h, :], scalar=GC_P[h],
                                                               op0=ALU.mult, in1=PS[5][:, h * 128:(h + 1) * 128],
                                                               op1=ALU.add), r=[b_Sst, bPS[5]], w=[b_Sst])
                        A(lambda e: e.copy(out=Sbf[:], in_=Sst[:]), r=[b_Sst], w=[b_Sbf])
                        if n == NTP - 1:
                            S.dma("sp", O["o_ret_p"].rearrange("h d e -> d h e"), Sst[:], reads=[b_Sst])
                    else:
                        for b in range(16):
                            sl = b % 2
                            S.dma("sp", S0[sl][:], I["sret"][b].rearrange("h d e -> d h e"), writes=[bS0[sl]])
                            A(lambda e: e.copy(out=S0b[sl][:], in_=S0[sl][:]), r=[bS0[sl]], w=[bS0b[sl]])
                            V(lambda e: e.tensor_tensor(
                                out=qxm[sl][:], in0=qxT[:, :, 0:64],
                                in1=cmask[:, b * 64:(b + 1) * 64].unsqueeze(1).to_broadcast([128, 4, 64]), op=ALU.mult),
                              r=[b_qxT, b_t1], w=[bqxm[sl]])
                            for h in range(4):
                                T(lambda e: e.matmul(po(h), lhsT=qxm[sl][:, h, :],
                                                     rhs=S0b[sl][:, h, :], start=False, stop=(b == 15)),
                                  r=[bqxm[sl], bS0b[sl]], w=[bPS[pob[h]]])
                            V(lambda e: e.tensor_tensor(
                                out=vzb[sl][:, :].rearrange("p (h e) -> p h e", h=4),
                                in0=PS[2][:64, :].rearrange("p (h e) -> p h e", h=4),
                                in1=zs[:, b * 4:(b + 1) * 4].unsqueeze(2).to_broadcast([64, 4, 128]), op=ALU.mult),
                              r=[bPS[2], b_t1], w=[bvzb[sl]])
                            kvb = 5 if sl == 0 else 0
                            for h in range(4):
                                T(lambda e: e.matmul(PS[kvb][:, h * 128:(h + 1) * 128], lhsT=kr[:64, h * 128:(h + 1) * 128],
                                                     rhs=vzb[sl][:, h * 128:(h + 1) * 128], start=True, stop=True),
                                  r=[b_kr, bvzb[sl]], w=[bPS[kvb]])
                            for h in range(4):
                                V(lambda e: e.scalar_tensor_tensor(out=Sn[sl][:, h, :], in0=S0[sl][:, h, :], scalar=GC_S[h],
                                                                   op0=ALU.mult, in1=PS[kvb][:, h * 128:(h + 1) * 128],
                                                                   op1=ALU.add), r=[bS0[sl], bPS[kvb]], w=[bSn[sl]])
                            S.dma("sp", O["o_ret_s"][b].rearrange("h d e -> d h e"), Sn[sl][:], reads=[bSn[sl]])
                    if _step <= 5:
                        continue
                    for h in range(4):
                        V(lambda e: e.bn_stats(out=stats[:npt, h, :], in_=po(h)),
                          r=[bPS[pob[h]]], w=[b_st])
                    for h in range(4):
                        V(lambda e: e.bn_aggr(out=mv[:npt, h, :], in_=stats[:npt, h, :]), r=[b_st], w=[b_st])
                    A(lambda e: e.activation(out=rs4[:npt, :], in_=mv[:npt, :, 1], func=AF.Sqrt, scale=1.0,
                                             bias=epsc[:npt, :]), r=[b_st, b_const], w=[b_st])
                    V(lambda e: e.reciprocal(out=rs4[:npt, :], in_=rs4[:npt, :]), r=[b_st], w=[b_st])
                    V(lambda e: e.scalar_tensor_tensor(out=nb4[:npt, :], in0=mv[:npt, :, 0], scalar=-1.0, op0=ALU.mult,
                                                       in1=rs4[:npt, :], op1=ALU.mult), r=[b_st], w=[b_st])
                    for h in range(4):
                        A(lambda e: e.activation(out=on[:npt, h * 128:(h + 1) * 128], in_=po(h),
                                                 func=AF.Identity, scale=rs4[:npt, h:h + 1], bias=nb4[:npt, h:h + 1]),
                          r=[bPS[pob[h]], b_st], w=[b_on])
                    V(lambda e: e.tensor_tensor(out=ret[:npt, :], in0=on[:npt, :], in1=sg_[:npt, :], op=ALU.mult),
                       r=[b_on, b_sg], w=[b_ret])
                    if _step <= 6:
                        continue
                    pv6b = ps_bf(6)
                    for h in range(4):
                        T(lambda e: e.transpose(out=pv6b[:, h * 128:h * 128 + npt], in_=ret[:npt, h * 128:(h + 1) * 128],
                                                identity=identb[:npt, :npt]), r=[b_ret, b_const], w=[bPS[6]])
                    A(lambda e: e.copy(out=retT[:, :, 0:npt],
                                       in_=pv6b[:, 0:512].rearrange("p (h t) -> p h t", h=4)[:, :, 0:npt]),
                      r=[bPS[6]], w=[b_retT])
                    if _step <= 7:
                        continue
                    bi_ = min(n // 4, 4)
                    for half in range(2):
                        bank = 2 + half
                        for kt in range(8):
                            lh = ssmT[:, kt, tok0:tok0 + npt] if kt < 4 else retT[:, kt - 4, 0:npt]
                            T(lambda e: e.matmul(PS[bank][:npt, :], lhsT=lh, rhs=wout[:, kt, half * 512:(half + 1) * 512],
                                                 start=(kt == 0), stop=(kt == 7)),
                              r=[b_ssmT[bi_], b_retT, b_wout], w=[bPS[bank]])
                        resid_add(n, npt, half, bank)
                S.barrier()
            if dbg:
                for n in range(NT):
                    S.dma("sp", O["dbg_x"][:, n, :], x[:, n, :], reads=[bx[n]])
            if stage <= 2:
                S.barrier()
                S.run_block()
                nck.__exit__(None, None, None)
                return nc

            with ExitStack() as s2:
                gx = alloc(s2, "gx", [128, 8])
                gmem = alloc(s2, "gmem", [128, 8])
                ones = alloc(s2, "ones", [128, 128], BF16)
                b_t2 = Buf("tab2")
                S.dma("sp", gx[:], I["g_xattn"].rearrange("(k p) -> p k", p=128), writes=[b_t2])
                S.dma("sp", gmem[:], I["g_mem"].rearrange("(k p) -> p k", p=128), writes=[b_t2])
                V(lambda e: e.memset(ones[:], 1.0), w=[b_t2])
                KT = alloc(s2, "KT", [128, 8, MEM], BF16)
                Vm = alloc(s2, "Vm", [128, 2, D], BF16)
                b_KT, b_Vm = Buf("KT"), Buf("Vm")
                wmq = alloc(s2, "wmq", [128, 8, D], BF16)
                b_wmq, b_wmo = Buf("wmq", S.GW[2]), Buf("wmo", S.GW[3])
                with ExitStack() as s2a:
                    wmk = alloc(s2a, "wmk", [128, 8, D], BF16)
                    wmv = alloc(s2a, "wmv", [128, 8, D], BF16)
                    b_wmk, b_wmv = Buf("wmk", S.GW[0]), Buf("wmv", S.GW[1])
                    load_w_bf16(wmk, b_wmk, I["w_mk"], 8, D, 0)
                    load_w_bf16(wmv, b_wmv, I["w_mv"], 8, D, 0)
                    load_w_bf16(wmq, b_wmq, I["w_mq"], 8, D, 0)
                    mx = [alloc(s2a, "mx%d" % i, [128, D]) for i in range(2)]
                    bmx = [Buf("mx%d" % i, S.GL[i]) for i in range(2)]
                    mhT = alloc(s2a, "mhT", [128, 8, MEM], BF16)
                    b_mhT = Buf("mhT")
                    mo = [alloc(s2a, "mo%d" % i, [128, D]) for i in range(2)]
                    bmo = [Buf("mo%d" % i, S.GS[i]) for i in range(2)]
                    _k2a = int(_os.environ.get("K2A", "9"))
                    for mt in range(2):
                        S.dma("sp", mx[mt][:], I["memp"][mt * 128:(mt + 1) * 128, :], writes=[bmx[mt]])
                        if _k2a >= 1:
                            rmsnorm_hT(mx[mt][:, :], bmx[mt], 128, gmem[:], mhT, b_mhT, scrB, mt * 128, None,
                                       ln=True, bg=b_t2)
                    oi = 0
                    for (wm, bwm, oname, isv) in ((wmk, b_wmk, "o_mk", False), (wmv, b_wmv, "o_mv", True)) if _k2a >= 2 else ():
                        for mt in range(2):
                            sl = oi % 2
                            oi += 1
                            for half in range(2):
                                bank = half
                                for kt in range(8):
                                    T(lambda e: e.matmul(PS[bank][:, :], lhsT=mhT[:, kt, mt * 128:(mt + 1) * 128],
                                                         rhs=wm[:, kt, half * 512:(half + 1) * 512], start=(kt == 0),
                                                         stop=(kt == 7)), r=[b_mhT, bwm], w=[bPS[bank]])
                                A(lambda e: e.copy(out=mo[sl][:, half * 512:(half + 1) * 512], in_=PS[bank][:, :]),
                                  r=[bPS[bank]], w=[bmo[sl]])
                                if isv:
                                    V(lambda e: e.tensor_copy(out=Vm[:, mt, half * 512:(half + 1) * 512], in_=PS[bank][:, :]),
                                      r=[bPS[bank]], w=[b_Vm])
                            S.dma("sp", O[oname][mt * 128:(mt + 1) * 128, :], mo[sl][:], reads=[bmo[sl]])
                    for j in range(8 if _k2a >= 3 else 0):
                        bank = 2 + (j % 2)
                        for kt in range(8):
                            T(lambda e: e.matmul(PS[bank][:, 0:MEM], lhsT=wmk[:, kt, j * 128:(j + 1) * 128],
                                                 rhs=mhT[:, kt, :], start=(kt == 0), stop=(kt == 7)),
                              r=[b_mhT, b_wmk], w=[bPS[bank]])
                        A(lambda e: e.copy(out=KT[:, j, :], in_=PS[bank][:, 0:MEM]), r=[bPS[bank]], w=[b_KT])
                    S.barrier()
                wmo = alloc(s2, "wmo", [128, 8, D], BF16)
                load_w_bf16(wmo, b_wmo, I["w_mo"], 8, D, 0)
                hT4 = alloc(s2, "hT4", [128, 8, 512], BF16)
                qm4 = alloc(s2, "qm4", [128, 8, 512], BF16)
                oT4 = alloc(s2, "oT4", [128, 8, 512], BF16)
                eT4 = [alloc(s2, "eT4_%d" % i, [128, 2, 512], BF16) for i in range(2)]
                rdn4 = [alloc(s2, "rdn4_%d" % i, [128, 512]) for i in range(2)]
                b_hT4, b_qm4, b_oT4 = Buf("hT4"), Buf("qm4"), Buf("oT4")
                b_eT4 = [Buf("eT4_%d" % i) for i in range(2)]
                b_rdn4 = [Buf("rdn4_%d" % i) for i in range(2)]
                Kb = [alloc(s2, "Kb%d" % i, [128, 2, D]) for i in range(2)]
                bKb = [Buf("Kb%d" % i, S.GL[i]) for i in range(2)]
                KbT = [alloc(s2, "KbT%d" % i, [128, 8, MEM], BF16) for i in range(2)]
                bKbT = [Buf("KbT%d" % i) for i in range(2)]
                Vb = [alloc(s2, "Vb%d" % i, [128, 2, D], BF16) for i in range(2)]
                bVb = [Buf("Vb%d" % i, S.GW[i]) for i in range(2)]
                eTs = alloc(s2, "eTs", [128, 2, 4, 64], BF16)
                b_eTs = Buf("eTs")
                qrot = [0]

                def q_proj(nc_):
                    for j in range(8):
                        bank = 5 + (qrot[0] % 3)
                        qrot[0] += 1
                        for kt in range(8):
                            T(lambda e: e.matmul(PS[bank][:, 0:nc_], lhsT=wmq[:, kt, j * 128:(j + 1) * 128],
                                                 rhs=hT4[:, kt, 0:nc_], start=(kt == 0), stop=(kt == 7)),
                              r=[b_wmq, b_hT4], w=[bPS[bank]])
                        A(lambda e: e.activation(out=qm4[:, j, 0:nc_], in_=PS[bank][:, 0:nc_], func=AF.Copy,
                                                 scale=1.0 / 16.0), r=[bPS[bank]], w=[b_qm4])

                def w_mo_resid(n, npt, c0):
                    for half in range(2):
                        bank = 5 + (qrot[0] % 3)
                        qrot[0] += 1
                        for j in range(8):
                            T(lambda e: e.matmul(PS[bank][:npt, :], lhsT=oT4[:, j, c0:c0 + npt],
                                                 rhs=wmo[:, j, half * 512:(half + 1) * 512], start=(j == 0), stop=(j == 7)),
                              r=[b_oT4, b_wmo], w=[bPS[bank]])
                        resid_add(n, npt, half, bank)

                for bi in range(4):
                    for ti in range(4):
                        n = bi * 4 + ti
                        rmsnorm_hT(x[:, n, :], bx[n], 128, gx[:], hT4, b_hT4, scrB, ti * 128, None, ln=True, bg=b_t2)
                    q_proj(512)
                    for h in range(4):
                        par = h % 2
                        for mt in range(2):
                            bank = mt
                            for dt_ in range(2):
                                T(lambda e: e.matmul(PS[bank][:, :], lhsT=KT[:, h * 2 + dt_, mt * 128:(mt + 1) * 128],
                                                     rhs=qm4[:, h * 2 + dt_, :], start=(dt_ == 0), stop=(dt_ == 1)),
                                  r=[b_KT, b_qm4], w=[bPS[bank]])
                            A(lambda e: e.activation(out=eT4[par][:, mt, :], in_=PS[bank][:, :], func=AF.Exp),
                              r=[bPS[bank]], w=[b_eT4[par]])
                        for mt in range(2):
                            T(lambda e: e.matmul(PS[2][:, :], lhsT=ones[:, :], rhs=eT4[par][:, mt, :], start=(mt == 0),
                                                 stop=(mt == 1)), r=[b_t2, b_eT4[par]], w=[bPS[2]])
                        A(lambda e: e.activation(out=rdn4[par][:, :], in_=PS[2][:, :], func=AF.Ln), r=[bPS[2]], w=[b_rdn4[par]])
                        A(lambda e: e.activation(out=rdn4[par][:, :], in_=rdn4[par][:, :], func=AF.Exp, scale=-1.0),
                          r=[b_rdn4[par]], w=[b_rdn4[par]])
                        for dt_ in range(2):
                            bank = 3 + dt_
                            j = h * 2 + dt_
                            for mt in range(2):
                                T(lambda e: e.matmul(PS[bank][:, :], lhsT=Vm[:, mt, j * 128:(j + 1) * 128],
                                                     rhs=eT4[par][:, mt, :], start=(mt == 0), stop=(mt == 1)),
                                  r=[b_Vm, b_eT4[par]], w=[bPS[bank]])
                            V(lambda e: e.tensor_tensor(out=oT4[:, j, :], in0=PS[bank][:, :], in1=rdn4[par][:, :], op=ALU.mult),
                              r=[bPS[bank], b_rdn4[par]], w=[b_oT4])
                    for ti in range(4):
                        w_mo_resid(bi * 4 + ti, 128, ti * 128)
                n = 16
                rmsnorm_hT(x[:TS, n, :], bx[n], TS, gx[:], hT4, b_hT4, scrB, 0, None, ln=True, bg=b_t2)
                q_proj(TS)
                rden_s = rdn4[0][:, 0:256].rearrange("p (h t) -> p h t", h=4)
                for b in range(16):
                    sl = b % 2
                    S.dma("sp", Kb[sl][:], I["ck"][b].rearrange("(mt p) d -> p mt d", p=128), writes=[bKb[sl]])
                    for q4 in range(4):
                        bank = 2 + (q4 % 2)
                        for i4 in range(4):
                            idx = q4 * 4 + i4
                            j, mt = idx // 2, idx % 2
                            T(lambda e: e.transpose(out=PS[bank][:, i4 * 128:(i4 + 1) * 128],
                                                    in_=Kb[sl][:, mt, j * 128:(j + 1) * 128], identity=identf[:]),
                              r=[bKb[sl], b_const], w=[bPS[bank]])
                        A(lambda e: e.copy(
                            out=KbT[sl][:, 2 * q4:2 * q4 + 2, :].rearrange("p j (m t) -> p j m t", m=2),
                            in_=PS[bank][:, :].rearrange("p (j m t) -> p j m t", j=2, m=2)),
                          r=[bPS[bank]], w=[bKbT[sl]])
                    for h in range(4):
                        for mt in range(2):
                            c0 = mt * 256 + h * 64 + 4 * b
                            for dt_ in range(2):
                                T(lambda e: e.matmul(PS[4][:, c0:c0 + 4],
                                                     lhsT=KbT[sl][:, h * 2 + dt_, mt * 128:(mt + 1) * 128],
                                                     rhs=qm4[:, h * 2 + dt_, 4 * b:4 * b + 4], start=(dt_ == 0),
                                                     stop=(dt_ == 1)), r=[bKbT[sl], b_qm4], w=[bPS[4]])
                A(lambda e: e.activation(out=eTs[:].rearrange("p m h t -> p (m h t)"), in_=PS[4][:, :], func=AF.Exp),
                  r=[bPS[4]], w=[b_eTs])
                for h in range(4):
                    for mt in range(2):
                        T(lambda e: e.matmul(PS[0][:, h * 64:(h + 1) * 64], lhsT=ones[:, :], rhs=eTs[:, mt, h, :],
                                             start=(mt == 0), stop=(mt == 1)), r=[b_t2, b_eTs], w=[bPS[0]])
                V(lambda e: e.reciprocal(out=rden_s, in_=PS[0][:, 0:256].rearrange("p (h t) -> p h t", h=4)),
                  r=[bPS[0]], w=[b_rdn4[0]])
                for b in range(16):
                    sl = b % 2
                    for mt in range(2):
                        S.dma("pool", Vb[sl][:, mt, :], I["cv"][b, mt * 128:(mt + 1) * 128, :], writes=[bVb[sl]])
                    for j in range(8):
                        h = j // 2
                        for mt in range(2):
                            T(lambda e: e.matmul(PS[1][:, j * 64 + 4 * b:j * 64 + 4 * b + 4],
                                                 lhsT=Vb[sl][:, mt, j * 128:(j + 1) * 128],
                                                 rhs=eTs[:, mt, h, 4 * b:4 * b + 4], start=(mt == 0), stop=(mt == 1)),
                              r=[bVb[sl], b_eTs], w=[bPS[1]])
                V(lambda e: e.tensor_tensor(
                    out=oT4[:, :, 0:64].rearrange("p (h a) t -> p h a t", a=2),
                    in0=PS[1][:, :].rearrange("p (h a t) -> p h a t", h=4, a=2),
                    in1=rden_s.unsqueeze(2).to_broadcast([128, 4, 2, 64]), op=ALU.mult),
                  r=[bPS[1], b_rdn4[0]], w=[b_oT4])
                w_mo_resid(16, TS, 0)
                S.barrier()
            if stage <= 3:
                if dbg:
                    for n in range(NT):
                        S.dma("sp", O["dbg_x"][:, n, :], x[:, n, :], reads=[bx[n]])
                S.barrier()
                S.run_block()
                nck.__exit__(None, None, None)
                return nc

            with ExitStack() as s3:
                gml = alloc(s3, "gml", [128, 8])
                b_t3 = Buf("tab3")
                S.dma("sp", gml[:], I["g_mlp"].rearrange("(k p) -> p k", p=128), writes=[b_t3])
                hTa = alloc(s3, "hTa", [128, 8, NTOK], BF16)
                b_hTa = [Buf("hTa%d" % n) for n in range(NT)]
                wup = [alloc(s3, "wup%d" % i, [128, 8, 512], BF16) for i in range(2)]
                wdn = [alloc(s3, "wdn%d" % i, [128, 4, D], BF16) for i in range(2)]
                bwup = [Buf("wup%d" % i, S.GW[i]) for i in range(2)]
                bwdn = [Buf("wdn%d" % i, S.GW[2 + i]) for i in range(2)]
                rl = [alloc(s3, "rl%d" % i, [128, 512]) for i in range(2)]
                brl = [Buf("rl%d" % i) for i in range(2)]
                aT = [alloc(s3, "aT%d" % i, [128, 4, 512], BF16) for i in range(2)]
                baT = [Buf("aT%d" % i) for i in range(2)]

                def load_fc(fc):
                    sl = fc % 2
                    for kt in range(8):
                        S.dma("pool", wup[sl][:, kt, :], I["w_up"][kt * 128:(kt + 1) * 128, fc * 512:(fc + 1) * 512],
                              writes=[bwup[sl]])
                    for ft in range(4):
                        S.dma("pool", wdn[sl][:, ft, :], I["w_down"][fc * 512 + ft * 128:fc * 512 + (ft + 1) * 128, :],
                              writes=[bwdn[sl]])
                load_fc(0)
                scrB["pb"] = [7, 6]
                for n in range(NT):
                    npt = TS if n == 16 else 128
                    rmsnorm_hT(x[:npt, n, :], bx[n], npt, gml[:], hTa, b_hTa[n], scrB, n * 128, None, ln=True, bg=b_t3)
                blocks3 = [(i * 512, 512) for i in range(4)] + [(SEQ, TS)]
                ai = 0
                ri = 0
                di = 0
                for fc in range(8):
                    sl = fc % 2
                    if fc + 1 < 8:
                        load_fc(fc + 1)
                    for (t0, nn) in blocks3:
                        tiles = list(range(t0 // 128, t0 // 128 + (nn + 127) // 128))
                        asl = ai % 2
                        ai += 1
                        for ft in range(4):
                            bank = ft
                            for kt in range(8):
                                T(lambda e: e.matmul(PS[bank][:, 0:nn], lhsT=wup[sl][:, kt, ft * 128:(ft + 1) * 128],
                                                     rhs=hTa[:, kt, t0:t0 + nn], start=(kt == 0), stop=(kt == 7)),
                                  r=[bwup[sl]] + [b_hTa[t] for t in tiles], w=[bPS[bank]])
                            rsl = ri % 2
                            ri += 1
                            A(lambda e: e.activation(out=rl[rsl][:, 0:nn], in_=PS[bank][:, 0:nn], func=AF.Relu),
                              r=[bPS[bank]], w=[brl[rsl]])
                            V(lambda e: e.tensor_tensor(out=aT[asl][:, ft, 0:nn], in0=rl[rsl][:, 0:nn], in1=rl[rsl][:, 0:nn],
                                                        op=ALU.mult), r=[brl[rsl]], w=[baT[asl]])
                        for ti, tl in enumerate(tiles):
                            npt = TS if tl == 16 else 128
                            for half in range(2):
                                bank = 4 + (di % 4)
                                di += 1
                                for ft in range(4):
                                    T(lambda e: e.matmul(PS[bank][:npt, :], lhsT=aT[asl][:, ft, ti * 128:ti * 128 + npt],
                                                         rhs=wdn[sl][:, ft, half * 512:(half + 1) * 512], start=(ft == 0),
                                                         stop=(ft == 3)), r=[baT[asl], bwdn[sl]], w=[bPS[bank]])
                                resid_add(tl, npt, half, bank)
                S.barrier()
            if dbg:
                for n in range(NT):
                    S.dma("sp", O["dbg_x"][:, n, :], x[:, n, :], reads=[bx[n]])
            with ExitStack() as s4:
                gf = alloc(s4, "gf", [128, D])
                b_gf = Buf("gf")
                S.dma("sp", gf[:], I["g_final"].rearrange("(o d) -> o d", o=1).partition_broadcast(128), writes=[b_gf])
                yst = [alloc(s4, "yst%d" % i, [128, D]) for i in range(3)]
                byst = [Buf("yst%d" % i, S.GS[i]) for i in range(3)]
                for n in range(NT):
                    npt = TS if n == 16 else 128
                    sl = n % 3
                    k4 = n % 2
                    sq, ss, rstd, bscr = scrB["sq"][k4], scrB["ss"][k4], scrB["rstd"][k4], scrB["ba"][k4]
                    A(lambda e: e.activation(out=sq[:npt, :], in_=x[:npt, n, :], func=AF.Square, accum_out=ss[:npt, :]),
                      r=[bx[n]], w=[bscr])
                    A(lambda e: e.activation(out=rstd[:npt, :], in_=ss[:npt, :], func=AF.Ln, scale=1.0 / D,
                                             bias=epsc[:npt, :]), r=[bscr, b_const], w=[bscr])
                    A(lambda e: e.activation(out=rstd[:npt, :], in_=rstd[:npt, :], func=AF.Exp, scale=-0.5),
                      r=[bscr], w=[bscr])
                    V(lambda e: e.scalar_tensor_tensor(out=yst[sl][:npt, :], in0=x[:npt, n, :], scalar=rstd[:npt, :],
                                                       op0=ALU.mult, in1=gf[:npt, :], op1=ALU.mult),
                      r=[bx[n], bscr, b_gf], w=[byst[sl]])
                    if n < 16:
                        S.dma("sp", O["yp"][n * 128:(n + 1) * 128, :], yst[sl][:, :], reads=[byst[sl]])
                    else:
                        S.dma("sp", O["ys"][:, :], yst[sl][:TS, :], reads=[byst[sl]])
                S.barrier()
            S.barrier()
            S.run_block()
            nck.__exit__(None, None, None)
    return nc


_NC = None


def kernel(**inputs):
    global _NC
    if _NC is None:
        _NC = build()
    maps = _in_maps(inputs)
    res = run_bass_kernel_spmd(_NC, maps, core_ids=list(range(8)))
    R = res.results
    f = np.float32

    def cat(name, shape=None):
        return np.stack([np.asarray(R[c][name], f) for c in range(8)])
    y_prompt = cat("yp")
    y_sample = cat("ys").reshape(128, 4, D)
    s5r_p = cat("o_s5r_p")[None]
    s5i_p = cat("o_s5i_p")[None]
    ret_p = cat("o_ret_p")[None]
    mk_p = cat("o_mk").reshape(8, MEM, 4, 256)[None]
    mv_p = cat("o_mv").reshape(8, MEM, 4, 256)[None]
    s5r_s = cat("o_s5r_s").reshape(128, G, 64)[None]
    s5i_s = cat("o_s5i_s").reshape(128, G, 64)[None]
    ret_s = cat("o_ret_s").reshape(128, 4, 128, 128)[None]
    return (y_prompt, y_sample, s5r_p, s5i_p, ret_p, mk_p, mv_p, s5r_s, s5i_s, ret_s)


def _in_maps(inputs):
    cst = _consts()
    f = np.float32
    maps = []
    w = {}
    for k in W_NAMES:
        a = np.asarray(inputs[k], f)
        if k != "g_final":
            a = a[0]
        w[k] = np.ascontiguousarray(a.reshape(W_SHAPES[k]))
    for c in range(8):
        m = dict(w)
        m.update(cst)
        b0 = 16 * c
        m["xp"] = np.ascontiguousarray(np.asarray(inputs["x_prompt"], f)[c])
        m["xs"] = np.ascontiguousarray(np.asarray(inputs["x_sample"], f)[b0:b0 + 16].reshape(TS, D))
        m["memp"] = np.ascontiguousarray(np.asarray(inputs["mem_prompt"], f)[c])
        m["s5r"] = np.ascontiguousarray(np.asarray(inputs["state_s5_re"], f)[0, b0:b0 + 16].reshape(512, 64))
        m["s5i"] = np.ascontiguousarray(np.asarray(inputs["state_s5_im"], f)[0, b0:b0 + 16].reshape(512, 64))
        m["sret"] = np.ascontiguousarray(np.asarray(inputs["state_ret"], f)[0, b0:b0 + 16])
        m["ck"] = np.ascontiguousarray(np.asarray(inputs["cache_mem_k"], f)[0, b0:b0 + 16].reshape(16, MEM, D))
        m["cv"] = np.ascontiguousarray(np.asarray(inputs["cache_mem_v"], f)[0, b0:b0 + 16].reshape(16, MEM, D))
        maps.append(m)
    return maps
```

```python
import numpy as np
import concourse.bass as bass
import concourse.mybir as mybir
from concourse.bass_utils import run_bass_kernel_spmd
from contextlib import ExitStack

F32 = mybir.dt.float32
BF16 = mybir.dt.bfloat16
AF = mybir.ActivationFunctionType
ALU = mybir.AluOpType

D = 1024
SEQ = 2048
NTP = 16
TS = 64
NT = 17
NTOK = SEQ + TS
G = 32
DFF = 4096
MEM = 256
EPS = 1e-6
PAST = 16384.0
MAGIC = 12582912.0
TWO_PI = float(2.0 * np.pi)
ML = [7, 6, 5, 4, 3, 2, 1, 0, 1, 2, 3, 4, 5, 6, 7, 8, -4, 0.5]
K1 = len(ML)
I_A1, I_A8, I_A4, I_AM4, I_HALF = 8, 15, 3, 16, 17
GAM = [1.0 - 2.0 ** (-5.0 - h) for h in range(4)]


class Grp:
    __slots__ = ("sem", "cnt", "sealed")


class Buf:
    __slots__ = ("w", "r", "name", "grp", "ps")

    def __init__(self, name="", grp=None, ps=False):
        self.w = None
        self.r = []
        self.name = name
        self.grp = grp
        self.ps = ps


class _Rec:
    def __init__(self):
        self.call = None

    def __getattr__(self, name):
        def f(*a, **kw):
            self.call = (name, a, kw)
            return self
        return f


class Sched:
    ENG = ("pe", "dve", "act", "pool", "sp")

    def __init__(self, nc, stack, self_sync=("dve", "act", "pool")):
        self.nc = nc
        self.stack = stack
        self.prog = {k: [] for k in self.ENG}
        self.cnt = {k: 0 for k in self.ENG}
        self.waited = {k: {} for k in self.ENG}
        self.sem = {}
        self.nsem = 0
        for k in ("pe", "dve", "act", "pool"):
            self.sem[k] = self.new_sem("c_" + k)
        self.self_sync = set(self_sync)
        self.groups = []
        self.GC = self.group("gc")
        self.GP = self.group("gp")
        self.GW = [self.group("gw%d" % i) for i in range(4)]
        self.GX = self.group("gx")
        self.GL = [self.group("gl%d" % i) for i in range(2)]
        self.GS = [self.group("gs%d" % i) for i in range(3)]

    def group(self, name):
        g = Grp()
        g.sem = self.new_sem(name)
        g.cnt = 0
        g.sealed = False
        self.groups.append(g)
        return g

    def new_sem(self, name):
        self.nsem += 1
        assert self.nsem < 98, "too many semaphores"
        return self.stack.enter_context(self.nc.semaphore(name + "_%d" % self.nsem))

    def _waits(self, eng, deps):
        w = self.waited[eng]
        need = {}
        dd = []
        for d in deps:
            if isinstance(d, Grp):
                d.sealed = True
                dd.append((d.sem, d.cnt))
            else:
                dd.append(d)
        deps = dd
        for (s, v) in deps:
            if eng in self.sem and s is self.sem[eng] and eng not in self.self_sync:
                continue
            k = id(s)
            if w.get(k, 0) >= v:
                continue
            if k not in need or need[k][1] < v:
                need[k] = (s, v)
        for k, (s, v) in need.items():
            w[k] = v
            self.prog[eng].append(lambda e, s=s, v=v: e.wait_ge(s, v))

    def op(self, eng, fn, reads=(), writes=()):
        deps = []
        for b in reads:
            if b.w is not None:
                deps.append(b.w)
            if b.ps:
                mys = self.sem[eng]
                deps.extend(d for d in b.r if not (isinstance(d, tuple) and d[0] is mys))
        for b in writes:
            if b.w is not None:
                deps.append(b.w)
            deps.extend(b.r)
        self._waits(eng, deps)
        self.cnt[eng] += 1
        c = self.cnt[eng]
        s = self.sem[eng]
        rec = _Rec()
        fn(rec)
        name, a, kw = rec.call
        self.prog[eng].append(lambda e, name=name, a=a, kw=kw, s=s: getattr(e, name)(*a, **kw).then_inc(s, 1))
        for b in reads:
            b.r.append((s, c))
        for b in writes:
            b.w = (s, c)
            b.r = []

    def dma(self, q, out, in_, reads=(), writes=(), **kw):
        tb = writes[0] if writes else reads[0]
        g = tb.grp
        if g is None:
            g = self.GP if q == "pool" else (self.GC if writes else self.GS[0])
        deps = []
        for b in reads:
            if b.w is not None:
                deps.append(b.w)
        for b in writes:
            if b.w is not None and b.w is not g:
                deps.append(b.w)
            deps.extend(b.r)
        self._waits(q, deps)
        if g.sealed and g.cnt > 0:
            self._waits(q, [(g.sem, g.cnt)])
        g.sealed = False
        g.cnt += 16
        s = g.sem
        self.prog[q].append(
            lambda e, out=out, in_=in_, s=s, kw=kw: e.dma_start(out=out, in_=in_, **kw).then_inc(s, 16))
        for b in reads:
            b.r.append(g)
        for b in writes:
            b.w = g
            b.r = []

    def barrier(self, engines=None):
        deps = [(self.sem[k], self.cnt[k]) for k in ("pe", "dve", "act", "pool") if self.cnt[k] > 0]
        deps += [g for g in self.groups if g.cnt > 0]
        for e in (engines or self.ENG):
            self._waits(e, deps)

    def run_block(self):
        nc = self.nc
        with nc.Block() as block:
            @block.sync
            def _(e):
                for t in self.prog["sp"]:
                    t(e)

            @block.tensor
            def _(e):
                for t in self.prog["pe"]:
                    t(e)

            @block.vector
            def _(e):
                for t in self.prog["dve"]:
                    t(e)

            @block.scalar
            def _(e):
                for t in self.prog["act"]:
                    t(e)

            @block.gpsimd
            def _(e):
                for t in self.prog["pool"]:
                    t(e)


_CONSTS = None


def _consts():
    global _CONSTS
    if _CONSTS is not None:
        return _CONSTS
    f = np.float32
    c = {}
    c["c_ident"] = np.eye(128, dtype=f)
    m = np.zeros((8, 128, 240), f)
    for a in range(8):
        for i in range(16):
            m[a, 16 * a + i, 112 + i] = 1.0
    c["c_masters"] = m
    ml = np.array(ML, np.float64)
    rows = np.concatenate([ml / (2 * np.pi), ml, 8.0 * (np.arange(64) + 1) / (2 * np.pi)])
    c["c_rows"] = rows.astype(f)[None, :]
    sg = np.zeros((128, 2), f)
    sg[:64, 0] = 1.0
    sg[64:, 0] = -1.0
    sg[:64, 1] = -1.0
    sg[64:, 1] = 1.0
    c["c_sgn"] = sg
    inv = (f(10000.0) ** (-(np.arange(64, dtype=f) / f(64.0)))).astype(f)
    pos = np.zeros((128, NT), f)
    for n in range(NTP):
        pos[:, n] = 128 * n + np.arange(128)
    pos[:64, 16] = PAST + (np.arange(64) % 4)
    ang = (pos[:, :, None] * inv[None, None, :]).astype(f).astype(np.float64)
    c["c_rope"] = np.stack([np.cos(ang), np.sin(ang), -np.sin(ang)]).astype(f)
    lg = np.log(np.array(GAM, np.float64))
    sc = 128.0 ** -0.5
    idx = np.arange(128)
    dm = np.zeros((128, 4, 128), np.float64)
    diff = idx[None, :] - idx[:, None]
    for h in range(4):
        dm[:, h, :] = np.where(diff >= 0, np.exp(np.maximum(diff, 0) * lg[h]), 0.0) * sc
    c["c_dmask_p"] = dm.reshape(128, 512).astype(f)
    ds_ = np.zeros((64, 4, 64), np.float64)
    r = np.arange(64)
    bb = r // 4
    tt = r % 4
    same = bb[:, None] == bb[None, :]
    dts = tt[None, :] - tt[:, None]
    for h in range(4):
        ds_[:, h, :] = np.where(same & (dts >= 0), np.exp(np.maximum(dts, 0) * lg[h]), 0.0) * sc
    c["c_dmask_s"] = ds_.reshape(64, 256).astype(f)
    xi_p = np.stack([np.exp((idx + 1.0) * lg[h]) * sc for h in range(4)])
    xi_s = np.stack([np.exp((tt + 1.0) * lg[h]) * sc for h in range(4)])
    c["c_xi"] = np.concatenate([xi_p.reshape(-1), xi_s.reshape(-1)]).astype(f)[None, :]
    zp = np.stack([np.exp((127.0 - idx) * lg[h]) for h in range(4)], axis=1)
    c["c_zeta_p"] = zp.astype(f)
    zs = np.zeros((64, 16, 4), np.float64)
    for h in range(4):
        for b in range(16):
            zs[:, b, h] = np.where(bb == b, np.exp((3.0 - tt) * lg[h]), 0.0)
    c["c_zs"] = zs.reshape(64, 64).astype(f)
    cm = np.zeros((16, 64), f)
    for b in range(16):
        cm[b, 4 * b:4 * b + 4] = 1.0
    c["c_cmask"] = cm.reshape(1, -1)
    _CONSTS = c
    return c


W_NAMES = ["g_mix", "w_in", "lam_re", "lam_im", "log_dt", "b_re", "b_im", "c_re", "c_im", "d_skip", "w_glu",
           "ret_gn", "w_out", "g_xattn", "g_mem", "w_mq", "w_mk", "w_mv", "w_mo", "g_mlp", "w_up", "w_down",
           "g_final"]
W_SHAPES = {"g_mix": [D], "w_in": [D, 2560], "lam_re": [G, 64], "lam_im": [G, 64], "log_dt": [G],
            "b_re": [G, 64, 16], "b_im": [G, 64, 16], "c_re": [G * 16, 64], "c_im": [G * 16, 64], "d_skip": [512],
            "w_glu": [512, 512], "ret_gn": [512], "w_out": [D, D], "g_xattn": [D], "g_mem": [D], "w_mq": [D, D],
            "w_mk": [D, D], "w_mv": [D, D], "w_mo": [D, D], "g_mlp": [D], "w_up": [D, DFF], "w_down": [DFF, D],
            "g_final": [D]}
IN_SHAPES = {"xp": [SEQ, D], "xs": [TS, D], "memp": [MEM, D], "s5r": [512, 64], "s5i": [512, 64],
             "sret": [16, 4, 128, 128], "ck": [16, MEM, D], "cv": [16, MEM, D]}
OUT_SHAPES = {"yp": [SEQ, D], "ys": [TS, D], "o_s5r_p": [G, 64], "o_s5i_p": [G, 64], "o_ret_p": [4, 128, 128],
              "o_mk": [MEM, D], "o_mv": [MEM, D], "o_s5r_s": [512, 64], "o_s5i_s": [512, 64],
              "o_ret_s": [16, 4, 128, 128]}


def build(stage=99, dbg=False):
    nc = bass.Bass("TRN2", target_bir_lowering=False)
    cst = _consts()
    I = {}
    for k, shp in list(IN_SHAPES.items()) + list(W_SHAPES.items()):
        I[k] = nc.dram_tensor(k, shp, F32, kind="ExternalInput").ap()
    for k, v in cst.items():
        I[k] = nc.dram_tensor(k, list(v.shape), F32, kind="ExternalInput").ap()
    O = {}
    for k, shp in OUT_SHAPES.items():
        O[k] = nc.dram_tensor(k, shp, F32, kind="ExternalOutput").ap()
    if dbg:
        O["dbg_ssm"] = nc.dram_tensor("dbg_ssm", [128, 4, NTOK], F32, kind="ExternalOutput").ap()
        O["dbg_x"] = nc.dram_tensor("dbg_x", [128, NT, D], F32, kind="ExternalOutput").ap()

    with ExitStack() as st:
        S = Sched(nc, st)

        def alloc(stack, name, shape, dt=F32):
            return stack.enter_context(nc.sbuf_tensor(name, shape, dt))

        def palloc(stack, name, shape, dt=F32):
            return stack.enter_context(nc.psum_tensor(name, shape, dt))

        def V(fn, r=(), w=()):
            S.op("dve", fn, reads=r, writes=w)

        def A(fn, r=(), w=()):
            S.op("act", fn, reads=r, writes=w)

        import os as _os0
        _nopool = _os0.environ.get("K_NOPOOL") == "1"

        def PL(fn, r=(), w=()):
            S.op("dve" if _nopool else "pool", fn, reads=r, writes=w)

        def T(fn, r=(), w=()):
            S.op("pe", fn, reads=r, writes=w)

        nck = nc.allow_non_contiguous_dma(reason="small param layout loads")
        nck.__enter__()

        identb = alloc(st, "identb", [128, 128], BF16)
        identf = alloc(st, "identf", [128, 128], F32)
        sgn = alloc(st, "sgn", [128, 2])
        epsc = alloc(st, "epsc", [128, 1])
        ssmT = alloc(st, "ssmT", [128, 4, NTOK], BF16)
        b_const = Buf("const")
        b_ssmT = [Buf("ssmT%d" % i) for i in range(5)]
        b_constp = Buf("constp")
        S.dma("pool", identb[:], I["c_ident"][:, :], writes=[b_constp])
        S.dma("sp", identf[:], I["c_ident"][:, :], writes=[b_const])
        S.dma("sp", sgn[:], I["c_sgn"][:, :], writes=[b_const])
        V(lambda e: e.memset(epsc[:], EPS), r=[b_constp], w=[b_const])
        PS = [palloc(st, "ps%d" % i, [128, 512], F32) for i in range(8)]
        bPS = [Buf("ps%d" % i, ps=True) for i in range(8)]

        def ps_bf(i):
            return PS[i][:].bitcast(BF16)

        def make_scr(stack, tag, pbanks):
            d = {"i": 0, "pb": list(pbanks)}
            d["sq"] = [alloc(stack, "sq%s%d" % (tag, i), [128, D], BF16) for i in range(2)]
            d["ss"] = [alloc(stack, "ss%s%d" % (tag, i), [128, 1]) for i in range(2)]
            d["rstd"] = [alloc(stack, "rstd%s%d" % (tag, i), [128, 1]) for i in range(2)]
            d["hb"] = [alloc(stack, "hb%s%d" % (tag, i), [128, D], BF16) for i in range(2)]
            d["ba"] = [Buf("ba%s%d" % (tag, i)) for i in range(2)]
            d["bh"] = [Buf("bh%s%d" % (tag, i)) for i in range(2)]
            return d

        def rmsnorm_hT(xt_ap, bx, npart, gcol, hT_ap, bhT, scr, col0, ph, ln=False, bg=None, out4=None):
            k = scr["i"] % 2
            pbank = scr["pb"][scr["i"] % len(scr["pb"])]
            scr["i"] += 1
            sq, ss, rstd, hb = scr["sq"][k], scr["ss"][k], scr["rstd"][k], scr["hb"][k]
            ba, bh = scr["ba"][k], scr["bh"][k]
            A(lambda e: e.activation(out=sq[:npart, :], in_=xt_ap, func=AF.Square, accum_out=ss[:npart, :]),
              r=[bx], w=[ba])
            if ln:
                A(lambda e: e.activation(out=rstd[:npart, :], in_=ss[:npart, :], func=AF.Ln, scale=1.0 / D,
                                         bias=epsc[:npart, :]), r=[ba, b_const], w=[ba])
                A(lambda e: e.activation(out=rstd[:npart, :], in_=rstd[:npart, :], func=AF.Exp, scale=-0.5),
                  r=[ba], w=[ba])
            else:
                A(lambda e: e.activation(out=rstd[:npart, :], in_=ss[:npart, :], func=AF.Sqrt, scale=1.0 / D,
                                         bias=epsc[:npart, :]), r=[ba, b_const], w=[ba])
                V(lambda e: e.reciprocal(out=rstd[:npart, :], in_=rstd[:npart, :]), r=[ba], w=[ba])
            V(lambda e: e.tensor_scalar(out=hb[:npart, :], in0=xt_ap, scalar1=rstd[:npart, :], scalar2=None,
                                        op0=ALU.mult), r=[bx, ba], w=[bh])
            pv = ps_bf(pbank)
            for kt in range(8):
                T(lambda e, kt=kt: e.transpose(out=pv[:, kt * 128:kt * 128 + npart],
                                               in_=hb[:npart, kt * 128:(kt + 1) * 128],
                                               identity=identb[:npart, :npart]),
                  r=[bh, b_const], w=[bPS[pbank]])
            if out4 is not None:
                V(lambda e: e.tensor_tensor(
                    out=out4, in0=pv.rearrange("p (k c s) -> p k c s", k=8, s=8),
                    in1=gcol.unsqueeze(2).unsqueeze(3).to_broadcast([128, 8, 16, 8]), op=ALU.mult),
                  r=[bPS[pbank], b_const] + ([bg] if bg is not None else []), w=[bhT])
                return
            V(lambda e: e.tensor_tensor(
                out=hT_ap[:, :, col0:col0 + npart],
                in0=pv.rearrange("p (k t) -> p k t", k=8)[:, :, 0:npart],
                in1=gcol.unsqueeze(2).to_broadcast([128, 8, npart]), op=ALU.mult),
              r=[bPS[pbank], b_const] + ([bg] if bg is not None else []), w=[bhT])

        def load_w_bf16(dst, bdst, src, kt_n, ncols, c0=0):
            for kt in range(kt_n):
                for cc in range(0, ncols, 1024):
                    w_ = min(1024, ncols - cc)
                    S.dma("pool", dst[:, kt, cc:cc + w_], src[kt * 128:(kt + 1) * 128, c0 + cc:c0 + cc + w_],
                          writes=[bdst])

        with ExitStack() as sa:
            Wt = alloc(sa, "Wt", [128, G, 128], BF16)
            Wst = alloc(sa, "Wst", [128, G, 128], BF16)
            Tt = alloc(sa, "Tt", [128, G, 128], BF16)
            Vt = alloc(sa, "Vt", [128, G, 128], BF16)
            COSR = alloc(sa, "COSR", [128, G, 64])
            SINR = alloc(sa, "SINR", [128, G, 64])
            masters = alloc(sa, "masters", [128, 8, 240], BF16)
            AR = alloc(sa, "AR", [128, G, K1])
            AI = alloc(sa, "AI", [128, G, K1])
            MAGJ = alloc(sa, "MAGJ", [128, G, K1])
            DS = alloc(sa, "DS", [128, G])
            gm = alloc(sa, "gm", [128, 8])
            winu = alloc(sa, "winu", [128, 8, 512], BF16)
            wglu = alloc(sa, "wglu", [128, 4, 512], BF16)
            b_tab = Buf("s5tab")
            b_winu = Buf("winu", S.GW[0])
            b_wglu = Buf("wglu", S.GW[1])
            b_tabp = Buf("s5tabp")
            S.dma("pool", masters[:], I["c_masters"].rearrange("a k j -> k a j"), writes=[b_tabp])
            S.dma("sp", gm[:], I["g_mix"].rearrange("(k p) -> p k", p=128), writes=[b_tab])
            for tau in range(8):
                S.dma("sp", DS[16 * tau:16 * tau + 16, :], I["d_skip"].rearrange("(g h) -> h g", h=16),
                      writes=[b_tab])
            load_w_bf16(winu, b_winu, I["w_in"], 8, 512, 0)
            load_w_bf16(wglu, b_wglu, I["w_glu"], 4, 512, 0)

            with ExitStack() as s0:
                rows = alloc(s0, "rows", [128, 2 * K1 + 64])
                LR = alloc(s0, "LR", [128, G])
                LI = alloc(s0, "LI", [128, G])
                DT = alloc(s0, "DT", [128, G])
                LRDT = alloc(s0, "LRDT", [128, G])
                LIDT = alloc(s0, "LIDT", [128, G])
                tA = alloc(s0, "tA", [128, G, 64])
                tB = alloc(s0, "tB", [128, G, 64])
                tC = alloc(s0, "tC", [128, G, 64])
                COSJ = alloc(s0, "COSJ", [128, G, K1])
                SINJ = alloc(s0, "SINJ", [128, G, K1])
                sm = alloc(s0, "sm", [128, 12, G])
                Br1 = alloc(s0, "Br1", [128, G, 16])
                Br2 = alloc(s0, "Br2", [128, G, 16])
                BB1 = alloc(s0, "BB1", [128, G, 16])
                BB2 = alloc(s0, "BB2", [128, G, 16])
                tb1 = alloc(s0, "tb1", [128, G, 16])
                big1 = alloc(s0, "big1", [128, G, 128])
                big2 = alloc(s0, "big2", [128, G, 128])
                WTpad = alloc(s0, "WTpad", [128, G, 256], BF16)
                WTs = alloc(s0, "WTs", [128, G, 128], BF16)
                CN1 = alloc(s0, "CN1", [128, 4, 128])
                CN2 = alloc(s0, "CN2", [128, 4, 128])
                CMa = alloc(s0, "CMa", [128, G, 16])
                CMb = alloc(s0, "CMb", [128, G, 16])
                CMab = alloc(s0, "CMab", [128, G, 16], BF16)
                b0 = Buf("p0in")
                bt = Buf("p0tmp")
                S.dma("sp", rows[:], I["c_rows"][0:1, :].partition_broadcast(128), writes=[b0])
                for hf in range(2):
                    S.dma("sp", LR[64 * hf:64 * hf + 64, :], I["lam_re"].rearrange("g p -> p g"), writes=[b0])
                    S.dma("sp", LI[64 * hf:64 * hf + 64, :], I["lam_im"].rearrange("g p -> p g"), writes=[b0])
                S.dma("sp", DT[:], I["log_dt"].rearrange("(o g) -> o g", o=1).partition_broadcast(128), writes=[b0])
                S.dma("sp", Br1[0:64], I["b_re"].rearrange("g p h -> p g h"), writes=[b0])
                S.dma("sp", Br1[64:128], I["b_im"].rearrange("g p h -> p g h"), writes=[b0])
                S.dma("sp", Br2[0:64], I["b_im"].rearrange("g p h -> p g h"), writes=[b0])
                S.dma("sp", Br2[64:128], I["b_re"].rearrange("g p h -> p g h"), writes=[b0])
                S.dma("sp", CN1[:, :, 0:64], I["c_re"].rearrange("(c r) p -> r c p", r=128), writes=[b0])
                S.dma("sp", CN1[:, :, 64:128], I["c_im"].rearrange("(c r) p -> r c p", r=128), writes=[b0])
                S.dma("sp", CN2[:, :, 0:64], I["c_im"].rearrange("(c r) p -> r c p", r=128), writes=[b0])
                S.dma("sp", CN2[:, :, 64:128], I["c_re"].rearrange("(c r) p -> r c p", r=128), writes=[b0])
                MT1 = rows[:, 0:K1]
                MLr = rows[:, K1:2 * K1]
                MRT = rows[:, 2 * K1:2 * K1 + 64]
                A(lambda e: e.activation(out=DT[:], in_=DT[:], func=AF.Exp), r=[b0], w=[b0])
                V(lambda e: e.tensor_tensor(out=LRDT[:], in0=LR[:], in1=DT[:], op=ALU.mult), r=[b0], w=[bt])
                V(lambda e: e.tensor_tensor(out=LIDT[:], in0=LI[:], in1=DT[:], op=ALU.mult), r=[b0], w=[bt])

                def trig(mt_ap, K, cos_out, sin_out):
                    shp = [128, G, K]
                    a_, b_, c_ = tA[:, :, 0:K], tB[:, :, 0:K], tC[:, :, 0:K]
                    V(lambda e: e.tensor_tensor(out=a_, in0=LIDT[:].unsqueeze(2).to_broadcast(shp),
                                                in1=mt_ap.unsqueeze(1).to_broadcast(shp), op=ALU.mult),
                      r=[bt, b0], w=[bt])
                    for (outp, off) in ((sin_out, 0.0), (cos_out, 0.25)):
                        if outp is None:
                            continue
                        V(lambda e, off=off: e.tensor_scalar(out=c_, in0=a_, scalar1=off, scalar2=None,
                                                             op0=ALU.add), r=[bt], w=[bt])
                        V(lambda e: e.tensor_scalar(out=b_, in0=c_, scalar1=MAGIC, scalar2=None, op0=ALU.add),
                          r=[bt], w=[bt])
                        V(lambda e: e.tensor_scalar(out=b_, in0=b_, scalar1=MAGIC, scalar2=None, op0=ALU.subtract),
                          r=[bt], w=[bt])
                        V(lambda e: e.tensor_tensor(out=c_, in0=c_, in1=b_, op=ALU.subtract), r=[bt], w=[bt])
                        A(lambda e, outp=outp: e.activation(out=outp, in_=c_, func=AF.Sin, scale=TWO_PI),
                          r=[bt], w=[b_tab])

                trig(MT1, K1, COSJ[:], SINJ[:])
                trig(MRT, 64, COSR[:], SINR[:])
                shpj = [128, G, K1]
                V(lambda e: e.tensor_tensor(out=MAGJ[:], in0=LRDT[:].unsqueeze(2).to_broadcast(shpj),
                                            in1=MLr.unsqueeze(1).to_broadcast(shpj), op=ALU.mult),
                  r=[bt, b0], w=[b_tab])
                A(lambda e: e.activation(out=MAGJ[:], in_=MAGJ[:], func=AF.Exp), r=[b_tab], w=[b_tab])
                V(lambda e: e.tensor_tensor(out=AR[:], in0=MAGJ[:], in1=COSJ[:], op=ALU.mult), r=[b_tab], w=[b_tab])
                V(lambda e: e.tensor_tensor(out=AI[:], in0=MAGJ[:], in1=SINJ[:], op=ALU.mult), r=[b_tab], w=[b_tab])
                em1, shalf, cm1, am1r, ai1, den, fr, fi, t0_, t1_ = [sm[:, i, :] for i in range(10)]
                x_ = LRDT[:]
                V(lambda e: e.tensor_scalar(out=em1, in0=x_, scalar1=0.2, scalar2=1.0, op0=ALU.mult, op1=ALU.add),
                  r=[bt], w=[bt])
                for cf in (0.25, 1.0 / 3.0, 0.5):
                    V(lambda e: e.tensor_tensor(out=em1, in0=em1, in1=x_, op=ALU.mult), r=[bt], w=[bt])
                    V(lambda e, cf=cf: e.tensor_scalar(out=em1, in0=em1, scalar1=cf, scalar2=1.0, op0=ALU.mult,
                                                       op1=ALU.add), r=[bt], w=[bt])
                V(lambda e: e.tensor_tensor(out=em1, in0=em1, in1=x_, op=ALU.mult), r=[bt], w=[bt])
                V(lambda e: e.tensor_copy(out=shalf, in_=SINJ[:, :, I_HALF]), r=[b_tab], w=[bt])
                V(lambda e: e.scalar_tensor_tensor(out=cm1, in0=shalf, scalar=-2.0, op0=ALU.mult, in1=shalf,
                                                   op1=ALU.mult), r=[bt], w=[bt])
                V(lambda e: e.tensor_tensor(out=am1r, in0=em1, in1=COSJ[:, :, I_A1], op=ALU.mult), r=[bt, b_tab], w=[bt])
                V(lambda e: e.tensor_tensor(out=am1r, in0=am1r, in1=cm1, op=ALU.add), r=[bt], w=[bt])
                V(lambda e: e.tensor_copy(out=ai1, in_=AI[:, :, I_A1]), r=[b_tab], w=[bt])
                V(lambda e: e.tensor_tensor(out=den, in0=LR[:], in1=LR[:], op=ALU.mult), r=[b0], w=[bt])
                V(lambda e: e.tensor_tensor(out=t0_, in0=LI[:], in1=LI[:], op=ALU.mult), r=[b0], w=[bt])
                V(lambda e: e.tensor_tensor(out=den, in0=den, in1=t0_, op=ALU.add), r=[bt], w=[bt])
                V(lambda e: e.reciprocal(out=den, in_=den), r=[bt], w=[bt])
                V(lambda e: e.tensor_tensor(out=fr, in0=am1r, in1=LR[:], op=ALU.mult), r=[bt, b0], w=[bt])
                V(lambda e: e.tensor_tensor(out=t0_, in0=ai1, in1=LI[:], op=ALU.mult), r=[bt, b0], w=[bt])
                V(lambda e: e.tensor_tensor(out=fr, in0=fr, in1=t0_, op=ALU.add), r=[bt], w=[bt])
                V(lambda e: e.tensor_tensor(out=fr, in0=fr, in1=den, op=ALU.mult), r=[bt], w=[bt])
                V(lambda e: e.tensor_tensor(out=fi, in0=ai1, in1=LR[:], op=ALU.mult), r=[bt, b0], w=[bt])
                V(lambda e: e.tensor_tensor(out=t0_, in0=am1r, in1=LI[:], op=ALU.mult), r=[bt, b0], w=[bt])
                V(lambda e: e.tensor_tensor(out=fi, in0=fi, in1=t0_, op=ALU.subtract), r=[bt], w=[bt])
                V(lambda e: e.tensor_tensor(out=fi, in0=fi, in1=den, op=ALU.mult), r=[bt], w=[bt])
                V(lambda e: e.tensor_scalar(out=Br2[:], in0=Br2[:], scalar1=sgn[:, 1:2], scalar2=None, op0=ALU.mult),
                  r=[b0, b_const], w=[b0])
                shb = [128, G, 16]
                frb = fr.unsqueeze(2).to_broadcast(shb)
                fib = fi.unsqueeze(2).to_broadcast(shb)
                V(lambda e: e.tensor_tensor(out=BB1[:], in0=Br1[:], in1=frb, op=ALU.mult), r=[b0, bt], w=[bt])
                V(lambda e: e.tensor_tensor(out=tb1[:], in0=Br2[:], in1=fib, op=ALU.mult), r=[b0, bt], w=[bt])
                V(lambda e: e.tensor_tensor(out=BB1[:], in0=BB1[:], in1=tb1[:], op=ALU.add), r=[bt], w=[bt])
                V(lambda e: e.tensor_tensor(out=BB2[:], in0=Br2[:], in1=frb, op=ALU.mult), r=[b0, bt], w=[bt])
                V(lambda e: e.tensor_tensor(out=tb1[:], in0=Br1[:], in1=fib, op=ALU.mult), r=[b0, bt], w=[bt])
                V(lambda e: e.tensor_tensor(out=BB2[:], in0=BB2[:], in1=tb1[:], op=ALU.subtract), r=[bt], w=[bt])
                sh4 = [128, G, 8, 16]
                arv = AR[:, :, 0:8].unsqueeze(3).to_broadcast(sh4)
                aiv = AI[:, :, 0:8].unsqueeze(3).to_broadcast(sh4)
                bb1 = BB1[:].unsqueeze(2).to_broadcast(sh4)
                bb2 = BB2[:].unsqueeze(2).to_broadcast(sh4)
                g1 = big1[:].rearrange("p g (s h) -> p g s h", s=8)
                g2 = big2[:].rearrange("p g (s h) -> p g s h", s=8)
                V(lambda e: e.memset(WTpad[:], 0.0), w=[bt])
                V(lambda e: e.tensor_tensor(out=g1, in0=arv, in1=bb1, op=ALU.mult), r=[b_tab, bt], w=[bt])
                V(lambda e: e.tensor_tensor(out=g2, in0=aiv, in1=bb2, op=ALU.mult), r=[b_tab, bt], w=[bt])
                V(lambda e: e.tensor_tensor(out=WTpad[:, :, 0:128], in0=big1[:], in1=big2[:], op=ALU.add),
                  r=[bt], w=[bt])
                V(lambda e: e.tensor_tensor(out=g1, in0=arv, in1=bb2, op=ALU.mult), r=[b_tab, bt], w=[bt])
                V(lambda e: e.tensor_tensor(out=g2, in0=aiv, in1=bb1, op=ALU.mult), r=[b_tab, bt], w=[bt])
                V(lambda e: e.tensor_tensor(out=WTs[:], in0=big1[:], in1=big2[:], op=ALU.subtract), r=[bt], w=[bt])
                for (src_fn, dstt) in ((lambda g: WTpad[:, g, 0:128], Wt), (lambda g: WTs[:, g, :], Wst)):
                    for gq in range(8):
                        bank = gq % 2
                        pv = ps_bf(bank)
                        for j in range(4):
                            g = gq * 4 + j
                            T(lambda e, g=g, j=j, pv=pv, src_fn=src_fn: e.transpose(
                                out=pv[:, j * 128:(j + 1) * 128], in_=src_fn(g), identity=identb[:]),
                              r=[bt, b_const], w=[bPS[bank]])
                        A(lambda e, gq=gq, pv=pv, dstt=dstt: e.copy(
                            out=dstt[:, gq * 4:gq * 4 + 4, :], in_=pv[:, 0:512].rearrange("p (j c) -> p j c", j=4)),
                          r=[bPS[bank]], w=[b_tab])
                for (CN, CM, col) in ((CN1, CMa, 0), (CN2, CMb, None)):
                    for c4 in range(4):
                        bank = 2 + (c4 % 2)
                        T(lambda e, CN=CN, c4=c4, bank=bank: e.transpose(out=PS[bank][:, 0:128], in_=CN[:, c4, :],
                                                                         identity=identf[:]),
                          r=[b0, b_const], w=[bPS[bank]])
                        if col is not None:
                            V(lambda e, CM=CM, c4=c4, bank=bank: e.tensor_scalar(
                                out=CM[:, c4 * 8:(c4 + 1) * 8, :],
                                in0=PS[bank][:, 0:128].rearrange("p (g h) -> p g h", g=8),
                                scalar1=sgn[:, 0:1], scalar2=None, op0=ALU.mult),
                              r=[bPS[bank], b_const], w=[bt])
                        else:
                            V(lambda e, CM=CM, c4=c4, bank=bank: e.tensor_scalar(
                                out=CM[:, c4 * 8:(c4 + 1) * 8, :],
                                in0=PS[bank][:, 0:128].rearrange("p (g h) -> p g h", g=8),
                                scalar1=-1.0, scalar2=None, op0=ALU.mult),
                              r=[bPS[bank]], w=[bt])
                V(lambda e: e.tensor_copy(out=CMab[:], in_=CMa[:]), r=[bt], w=[bt])
                afw = AR[:, :, 8:16].unsqueeze(3).to_broadcast(sh4)
                aifw = AI[:, :, 8:16].unsqueeze(3).to_broadcast(sh4)
                cma = CMa[:].unsqueeze(2).to_broadcast(sh4)
                cmb = CMb[:].unsqueeze(2).to_broadcast(sh4)
                V(lambda e: e.tensor_tensor(out=g1, in0=afw, in1=cma, op=ALU.mult), r=[b_tab, bt], w=[bt])
                V(lambda e: e.tensor_tensor(out=g2, in0=aifw, in1=cmb, op=ALU.mult), r=[b_tab, bt], w=[bt])
                V(lambda e: e.tensor_tensor(out=Vt[:], in0=big1[:], in1=big2[:], op=ALU.add), r=[bt], w=[b_tab])
                for gq in range(8):
                    bank = 4 + (gq % 2)
                    for j in range(4):
                        g = gq * 4 + j
                        for tau in range(8):
                            c0 = (7 - tau) * 16
                            T(lambda e, g=g, j=j, tau=tau, c0=c0, bank=bank: e.matmul(
                                PS[bank][:, j * 128 + tau * 16:j * 128 + tau * 16 + 16],
                                lhsT=WTpad[:, g, c0:c0 + 128], rhs=CMab[:, g, :], start=True, stop=True),
                              r=[bt], w=[bPS[bank]])
                    A(lambda e, gq=gq, bank=bank: e.copy(
                        out=Tt[:, gq * 4:gq * 4 + 4, :], in_=PS[bank][:].rearrange("p (j c) -> p j c", j=4)),
                      r=[bPS[bank]], w=[b_tab])
                S.barrier()
            xst = [alloc(sa, "xst%d" % i, [128, D]) for i in range(2)]
            bxst = [Buf("xst%d" % i, S.GL[i]) for i in range(2)]
            scrA = make_scr(sa, "A", [7])
            bscr = Buf("scrA")
            hT2 = [alloc(sa, "hT_%d" % i, [128, 8, 512], BF16) for i in range(2)]
            bhT2 = [Buf("hT_%d" % i) for i in range(2)]
            uT2 = [alloc(sa, "uT_%d" % i, [128, 4, 512], BF16) for i in range(2)]
            buT2 = [Buf("uT_%d" % i) for i in range(2)]
            U = alloc(sa, "U", [128, G, 64], BF16)
            bU = Buf("U")
            rr = alloc(sa, "rr", [128, G, 64])
            rs = alloc(sa, "rs", [128, G, 64])
            ww = alloc(sa, "ww", [128, G, 64])
            ws = alloc(sa, "ws", [128, G, 64])
            tmpr = alloc(sa, "tmpr", [128, 16, 64])
            b_r, b_rs, b_w, b_ws, b_tmpr = Buf("r"), Buf("rs"), Buf("w"), Buf("ws"), Buf("tmpr")
            Xb = alloc(sa, "Xb", [128, G, 65], BF16)
            bXb = Buf("Xb")
            Xc = alloc(sa, "Xc", [128, G])
            Xsc = alloc(sa, "Xsc", [128, G])
            ctmp = alloc(sa, "ctmp", [128, 2, G])
            bXc = Buf("Xc", S.GS[0])
            ytmp = alloc(sa, "ytmp", [128, 8, 64])
            bytmp = Buf("ytmp")
            Zt = alloc(sa, "Zt", [128, G, 64], BF16)
            bZ = Buf("Z")
            zT = alloc(sa, "zT", [128, 4, 512], BF16)
            bzT = Buf("zT")
            sig = alloc(sa, "sig", [128, 4, 512])
            bsig = Buf("sig")
            H0 = alloc(sa, "H0", [128, 512])
            H0s = alloc(sa, "H0s", [128, 512])
            hn = alloc(sa, "hn", [128, 4, 128])
            hn2 = alloc(sa, "hn2", [128, 4, 128])
            Hp = alloc(sa, "Hp", [128, G, 16])
            Xf = alloc(sa, "Xf", [128, G, 16])
            xo = alloc(sa, "xo", [128, 4, 128])
            bH = Buf("H0")
            bxo = Buf("xo", S.GS[1])
            V(lambda e: e.memset(Xc[:], 0.0), r=[b_tabp], w=[bXc, b_tab])
            V(lambda e: e.memset(Xsc[:], 0.0), w=[bXc])
            V(lambda e: e.memset(Xb[:], 0.0), w=[bXb])

            blocks = [(i * 512, 512, False) for i in range(4)] + [(SEQ, TS, True)]
            if _os0.environ.get("K1A") == "0":
                blocks = []
            def p1a_stageA(bi):
                t0, n, is_s = blocks[bi]
                hT, bhT = hT2[bi % 2], bhT2[bi % 2]
                uT, buT = uT2[bi % 2], buT2[bi % 2]
                ntile = (n + 127) // 128
                for ti in range(ntile):
                    npart = min(128, n - ti * 128)
                    slot = (bi * 4 + ti) % 2
                    src = I["xs"][:, :] if is_s else I["xp"][t0 + ti * 128:t0 + ti * 128 + 128, :]
                    S.dma("sp", xst[slot][:npart, :], src, writes=[bxst[slot]])
                    o4 = None if is_s else hT[:, :, :].rearrange("p k (s c) -> p k c s", s=8)[:, :, ti * 16:(ti + 1) * 16, :]
                    rmsnorm_hT(xst[slot][:npart, :], bxst[slot], npart, gm[:], hT, bhT,
                               scrA, ti * 128, None, bg=b_tab, out4=o4)
                for ct in range(4):
                    bank = ct
                    for kt in range(8):
                        T(lambda e, ct=ct, kt=kt, bank=bank: e.matmul(
                            PS[bank][:, 0:n], lhsT=winu[:, kt, ct * 128:(ct + 1) * 128], rhs=hT[:, kt, 0:n],
                            start=(kt == 0), stop=(kt == 7)), r=[b_winu, bhT], w=[bPS[bank]])
                    A(lambda e, ct=ct, bank=bank: e.copy(out=uT[:, ct, 0:n], in_=PS[bank][:, 0:n]),
                      r=[bPS[bank]], w=[buT])

            if blocks:
                p1a_stageA(0)
            for bi, (t0, n, is_s) in enumerate(blocks):
                nch = n // 8 if not is_s else 16
                uT, buT = uT2[bi % 2], buT2[bi % 2]
                for gq in range(4):
                    bank = 4 + (gq % 2)
                    for j in range(8):
                        g = gq * 8 + j
                        ct, gl = g // 8, g % 8
                        if not is_s:
                            uv = uT[:, ct, 0:n].rearrange("p (s c) -> p s c", s=8)
                            sig_list = list(range(8))
                        else:
                            uv = uT[:, ct, 0:n].rearrange("p (b t) -> p t b", t=4)
                            sig_list = [4, 5, 6, 7]
                        for si, sg_ in enumerate(sig_list):
                            rhs = uv[:, sg_ if not is_s else si, :]
                            T(lambda e, j=j, gl=gl, sg_=sg_, rhs=rhs, si=si, bank=bank, L=len(sig_list): e.matmul(
                                PS[bank][:, j * 64:j * 64 + nch],
                                lhsT=masters[:, gl, 112 - 16 * sg_:240 - 16 * sg_], rhs=rhs,
                                start=(si == 0), stop=(si == L - 1)),
                              r=[b_tab, buT], w=[bPS[bank]])
                    A(lambda e, gq=gq, bank=bank: e.copy(
                        out=U[:, gq * 8:gq * 8 + 8, 0:nch],
                        in_=PS[bank][:].rearrange("p (j c) -> p j c", j=8)[:, :, 0:nch]),
                      r=[bPS[bank]], w=[bU])
                if not is_s:
                    for hf in range(2):
                        for j in range(16):
                            g = hf * 16 + j
                            for (wt, bk) in ((Wt, 0), (Wst, 2)):
                                bank = bk + j // 8
                                T(lambda e, g=g, j=j, wt=wt, bank=bank: e.matmul(
                                    PS[bank][:, (j % 8) * 64:(j % 8) * 64 + 64], lhsT=wt[:, g, :], rhs=U[:, g, :],
                                    start=True, stop=True), r=[b_tab, bU], w=[bPS[bank]])
                        for q in range(2):
                            gs = slice(hf * 16 + q * 8, hf * 16 + q * 8 + 8)
                            Sv = PS[q][:].rearrange("p (j c) -> p j c", j=8)
                            Ssv = PS[2 + q][:].rearrange("p (j c) -> p j c", j=8)
                            tm = tmpr[:, q * 8:q * 8 + 8, :]
                            V(lambda e, gs=gs, Sv=Sv: e.tensor_tensor(out=rr[:, gs, :], in0=Sv, in1=COSR[:, gs, :],
                                                                     op=ALU.mult), r=[bPS[q], b_tab], w=[b_r])
                            V(lambda e, gs=gs, Ssv=Ssv, tm=tm: e.tensor_tensor(out=tm, in0=Ssv, in1=SINR[:, gs, :],
                                                                              op=ALU.mult),
                              r=[bPS[2 + q], b_tab], w=[b_tmpr])
                            V(lambda e, gs=gs, tm=tm: e.tensor_tensor(out=rr[:, gs, :], in0=rr[:, gs, :], in1=tm,
                                                                     op=ALU.subtract), r=[b_r, b_tmpr], w=[b_r])
                            V(lambda e, gs=gs, Ssv=Ssv: e.tensor_tensor(out=rs[:, gs, :], in0=Ssv, in1=COSR[:, gs, :],
                                                                       op=ALU.mult), r=[bPS[2 + q], b_tab], w=[b_rs])
                            V(lambda e, gs=gs, Sv=Sv, tm=tm: e.tensor_tensor(out=tm, in0=Sv, in1=SINR[:, gs, :],
                                                                            op=ALU.mult),
                              r=[bPS[q], b_tab], w=[b_tmpr])
                            V(lambda e, gs=gs, tm=tm: e.tensor_tensor(out=rs[:, gs, :], in0=rs[:, gs, :], in1=tm,
                                                                     op=ALU.add), r=[b_rs, b_tmpr], w=[b_rs])
                    for g in range(G):
                        rho = MAGJ[:, g, I_A8:I_A8 + 1].to_broadcast([128, 64])
                        V(lambda e, g=g, rho=rho: e.tensor_tensor_scan(
                            out=ww[:, g, :], data0=rho, data1=rr[:, g, :], initial=Xc[:, g:g + 1], op0=ALU.mult,
                            op1=ALU.add), r=[b_r, b_tab, bXc], w=[b_w])
                        V(lambda e, g=g, rho=rho: e.tensor_tensor_scan(
                            out=ws[:, g, :], data0=rho, data1=rs[:, g, :], initial=Xsc[:, g:g + 1], op0=ALU.mult,
                            op1=ALU.add), r=[b_rs, b_tab, bXc], w=[b_ws])
                    if bi + 1 < len(blocks):
                        p1a_stageA(bi + 1)
                    ce, se_ = COSR[:, :, 63], SINR[:, :, 63]
                    we, wse = ww[:, :, 63], ws[:, :, 63]
                    V(lambda e: e.tensor_tensor(out=ctmp[:, 0, :], in0=ce, in1=we, op=ALU.mult), r=[b_w, b_tab], w=[bscr])
                    V(lambda e: e.tensor_tensor(out=ctmp[:, 1, :], in0=se_, in1=wse, op=ALU.mult), r=[b_ws, b_tab], w=[bscr])
                    V(lambda e: e.tensor_tensor(out=Xc[:], in0=ctmp[:, 0, :], in1=ctmp[:, 1, :], op=ALU.add),
                      r=[bscr], w=[bXc])
                    V(lambda e: e.tensor_tensor(out=ctmp[:, 0, :], in0=ce, in1=wse, op=ALU.mult), r=[b_ws, b_tab], w=[bscr])
                    V(lambda e: e.tensor_tensor(out=ctmp[:, 1, :], in0=se_, in1=we, op=ALU.mult), r=[b_w, b_tab], w=[bscr])
                    V(lambda e: e.tensor_tensor(out=Xsc[:], in0=ctmp[:, 0, :], in1=ctmp[:, 1, :], op=ALU.subtract),
                      r=[bscr], w=[bXc])
                    if bi > 0:
                        V(lambda e: e.tensor_copy(out=Xb[:, :, 0], in_=Xb[:, :, 64]), r=[bXb], w=[bXb])
                    V(lambda e: e.tensor_tensor(out=ww[:], in0=ww[:], in1=COSR[:], op=ALU.mult), r=[b_w, b_tab, bXc],
                      w=[b_w])
                    PL(lambda e: e.tensor_tensor(out=ws[:], in0=ws[:], in1=SINR[:], op=ALU.mult), r=[b_ws, b_tab, bXc],
                       w=[b_ws])
                    V(lambda e: e.tensor_tensor(out=Xb[:, :, 1:65], in0=ww[:], in1=ws[:], op=ALU.add),
                      r=[b_w, b_ws], w=[bXb])
                    xprev = lambda g: Xb[:, g, 0:64]
                    bXprev = bXb
                    if bi == 3:
                        S.dma("sp", O["o_s5r_p"].rearrange("g p -> p g"), Xc[0:64, :], reads=[bXc])
                        S.dma("sp", O["o_s5i_p"].rearrange("g p -> p g"), Xc[64:128, :], reads=[bXc])
                else:
                    S.dma("sp", hn[:, :, 0:64], I["s5r"].rearrange("(j r) p -> r j p", r=128), writes=[bH])
                    S.dma("sp", hn[:, :, 64:128], I["s5i"].rearrange("(j r) p -> r j p", r=128), writes=[bH])
                    S.dma("sp", hn2[:, :, 0:64], I["s5i"].rearrange("(j r) p -> r j p", r=128), writes=[bH])
                    S.dma("sp", hn2[:, :, 64:128], I["s5r"].rearrange("(j r) p -> r j p", r=128), writes=[bH])
                    for (src_, dst_, bank) in ((hn, H0, 0), (hn2, H0s, 1)):
                        for j in range(4):
                            T(lambda e, src_=src_, j=j, bank=bank: e.transpose(
                                out=PS[bank][:, j * 128:(j + 1) * 128], in_=src_[:, j, :], identity=identf[:]),
                              r=[bH, b_const], w=[bPS[bank]])
                        V(lambda e, dst_=dst_, bank=bank: e.tensor_copy(out=dst_[:], in_=PS[bank][:]),
                          r=[bPS[bank]], w=[bH])
                    V(lambda e: e.tensor_scalar(out=H0s[0:64, :], in0=H0s[0:64, :], scalar1=-1.0, scalar2=None,
                                                op0=ALU.mult), r=[bH], w=[bH])
                    shs = [128, G, 16]
                    h0v = H0[:].rearrange("p (b g) -> p g b", g=G)
                    h0sv = H0s[:].rearrange("p (b g) -> p g b", g=G)

                    def abc(tab, idx):
                        return tab[:, :, idx].unsqueeze(2).to_broadcast(shs)
                    V(lambda e: e.tensor_tensor(out=Xf[:], in0=h0v, in1=abc(AR, I_AM4), op=ALU.mult), r=[bH, b_tab], w=[bxo])
                    V(lambda e: e.tensor_tensor(out=Hp[:], in0=h0sv, in1=abc(AI, I_AM4), op=ALU.mult), r=[bH, b_tab], w=[bxo])
                    V(lambda e: e.tensor_tensor(out=Xb[:, :, 0:16], in0=Xf[:], in1=Hp[:], op=ALU.add), r=[bxo], w=[bXb])
                    V(lambda e: e.tensor_tensor(out=Xf[:], in0=h0v, in1=abc(AR, I_A4), op=ALU.mult), r=[bH, b_tab], w=[bxo])
                    V(lambda e: e.tensor_tensor(out=Hp[:], in0=h0sv, in1=abc(AI, I_A4), op=ALU.mult), r=[bH, b_tab], w=[bxo])
                    V(lambda e: e.tensor_tensor(out=Xf[:], in0=Xf[:], in1=Hp[:], op=ALU.add), r=[bxo], w=[bxo])
                    for q in range(4):
                        bank = q % 2
                        for j in range(8):
                            g = q * 8 + j
                            T(lambda e, g=g, j=j, bank=bank: e.matmul(
                                PS[bank][:, j * 64:j * 64 + 16], lhsT=Wt[:, g, :], rhs=U[:, g, 0:16],
                                start=True, stop=True), r=[b_tab, bU], w=[bPS[bank]])
                        V(lambda e, q=q, bank=bank: e.tensor_tensor(
                            out=Xf[:, q * 8:q * 8 + 8, :], in0=Xf[:, q * 8:q * 8 + 8, :],
                            in1=PS[bank][:].rearrange("p (j c) -> p j c", j=8)[:, :, 0:16], op=ALU.add),
                          r=[bxo, bPS[bank]], w=[bxo])
                    Xf2 = Xf[:].rearrange("p g b -> p (g b)")
                    for j in range(4):
                        T(lambda e, j=j: e.transpose(out=PS[2][:, j * 128:(j + 1) * 128],
                                                     in_=Xf2[:, j * 128:(j + 1) * 128], identity=identf[:]),
                          r=[bxo, b_const], w=[bPS[2]])
                    V(lambda e: e.tensor_copy(out=xo[:], in_=PS[2][:].rearrange("p (j c) -> p j c", j=4)),
                      r=[bPS[2]], w=[bxo])
                    for j in range(4):
                        for gl in range(8):
                            for (nm, c0) in (("o_s5r_s", 0), ("o_s5i_s", 64)):
                                S.dma("sp", O[nm].rearrange("(b g) p -> g b p", g=G)[8 * j + gl],
                                      xo[gl * 16:gl * 16 + 16, j, c0:c0 + 64], reads=[bxo])
                    xprev = lambda g: Xb[:, g, 0:16]
                    bXprev = bXb
                for gq in range(4):
                    bank = 6 + (gq % 2)
                    for j in range(8):
                        g = gq * 8 + j
                        T(lambda e, g=g, j=j, bank=bank: e.matmul(
                            PS[bank][:, j * 64:j * 64 + nch], lhsT=Tt[:, g, :], rhs=U[:, g, 0:nch],
                            start=True, stop=False), r=[b_tab, bU], w=[bPS[bank]])
                        T(lambda e, g=g, j=j, bank=bank: e.matmul(
                            PS[bank][:, j * 64:j * 64 + nch], lhsT=Vt[:, g, :], rhs=xprev(g)[:, 0:nch],
                            start=False, stop=True), r=[b_tab, bXprev], w=[bPS[bank]])
                    gs = slice(gq * 8, gq * 8 + 8)
                    yv = PS[bank][:].rearrange("p (j c) -> p j c", j=8)[:, :, 0:nch]
                    V(lambda e, gs=gs: e.tensor_tensor(out=ytmp[:, :, 0:nch], in0=U[:, gs, 0:nch],
                                                       in1=DS[:, gs].unsqueeze(2).to_broadcast([128, 8, nch]),
                                                       op=ALU.mult), r=[bU, b_tab], w=[bytmp])
                    V(lambda e, yv=yv: e.tensor_tensor(out=ytmp[:, :, 0:nch], in0=yv, in1=ytmp[:, :, 0:nch],
                                                       op=ALU.add), r=[bPS[bank], bytmp], w=[bytmp])
                    A(lambda e, gs=gs: e.activation(out=Zt[:, gs, 0:nch], in_=ytmp[:, :, 0:nch],
                                                    func=AF.Gelu_apprx_tanh), r=[bytmp], w=[bZ])
                for ct in range(4):
                    bank = ct % 2
                    taus = list(range(8)) if not is_s else [4, 5, 6, 7]
                    for ti_, tau in enumerate(taus):
                        for gl in range(8):
                            g = ct * 8 + gl
                            T(lambda e, g=g, gl=gl, tau=tau, ti_=ti_, bank=bank: e.matmul(
                                PS[bank][:, ti_ * 64:ti_ * 64 + nch],
                                lhsT=masters[:, tau, 112 - 16 * gl:240 - 16 * gl], rhs=Zt[:, g, 0:nch],
                                start=(gl == 0), stop=(gl == 7)), r=[b_tab, bZ], w=[bPS[bank]])
                    if not is_s:
                        A(lambda e, ct=ct, bank=bank: e.copy(
                            out=zT[:, ct, 0:n].rearrange("p (c t) -> p t c", t=8),
                            in_=PS[bank][:].rearrange("p (t c) -> p t c", t=8)), r=[bPS[bank]], w=[bzT])
                    else:
                        A(lambda e, ct=ct, bank=bank: e.copy(
                            out=zT[:, ct, 0:n].rearrange("p (b t) -> p t b", t=4),
                            in_=PS[bank][:].rearrange("p (t c) -> p t c", t=8)[:, 0:4, 0:16]),
                          r=[bPS[bank]], w=[bzT])
                for ct in range(4):
                    bank = 2 + (ct % 2)
                    for kt in range(4):
                        T(lambda e, ct=ct, kt=kt, bank=bank: e.matmul(
                            PS[bank][:, 0:n], lhsT=wglu[:, kt, ct * 128:(ct + 1) * 128], rhs=zT[:, kt, 0:n],
                            start=(kt == 0), stop=(kt == 3)), r=[b_wglu, bzT], w=[bPS[bank]])
                    A(lambda e, ct=ct, bank=bank: e.activation(out=sig[:, ct, 0:n], in_=PS[bank][:, 0:n],
                                                               func=AF.Sigmoid), r=[bPS[bank]], w=[bsig])
                V(lambda e: e.tensor_tensor(out=ssmT[:, :, t0:t0 + n], in0=zT[:, :, 0:n], in1=sig[:, :, 0:n],
                                            op=ALU.mult), r=[bzT, bsig], w=[b_ssmT[bi]])
            S.barrier()
        if dbg:
            with ExitStack() as sd:
                dtmp = alloc(sd, "dtmp", [128, 4, NTOK])
                bd = Buf("dtmp", S.GS[2])
                V(lambda e: e.tensor_copy(out=dtmp[:], in_=ssmT[:]), r=b_ssmT, w=[bd])
                S.dma("sp", O["dbg_ssm"][:, :, :], dtmp[:], reads=[bd])
                S.barrier()
        if stage <= 1:
            S.barrier()
            S.run_block()
            nck.__exit__(None, None, None)
            return nc

        with ExitStack() as sbx:
            x = alloc(sbx, "x", [128, NT, D])
            bx = [Buf("x%d" % n, S.GX) for n in range(NT)]
            for n in range(NTP):
                S.dma("sp", x[:, n, :], I["xp"][n * 128:(n + 1) * 128, :], writes=[bx[n]])
            S.dma("sp", x[0:TS, 16, :], I["xs"][:, :], writes=[bx[16]])
            scrB = make_scr(sbx, "B", [7])
            hT1 = alloc(sbx, "hT1", [128, 8, 128], BF16)
            bhT1 = Buf("hT1")

            def resid_add(n, npart, half, bank):
                V(lambda e: e.tensor_tensor(out=x[:npart, n, half * 512:(half + 1) * 512], in0=PS[bank][:npart, :],
                                            in1=x[:npart, n, half * 512:(half + 1) * 512], op=ALU.add),
                  r=[bPS[bank], bx[n]], w=[bx[n]])

            with ExitStack() as s1:
                wq = alloc(s1, "wqkvg", [128, 8, 2048], BF16)
                wout = alloc(s1, "wout", [128, 8, D], BF16)
                b_wqc = [Buf("wq%d" % c, S.GW[c]) for c in range(4)]
                b_wout = Buf("wout", S.GW[0])
                for c in range(4):
                    for kt in range(8):
                        S.dma("pool", wq[:, kt, c * 512:(c + 1) * 512],
                              I["w_in"][kt * 128:(kt + 1) * 128, 512 + c * 512:512 + (c + 1) * 512], writes=[b_wqc[c]])
                wout_loaded = [False]
                gm2 = alloc(s1, "gm2", [128, 8])
                gn = alloc(s1, "gn", [128, 4])
                rope = alloc(s1, "rope", [128, 3, NT, 64])
                dmp = alloc(s1, "dmp", [128, 512])
                dms = alloc(s1, "dms", [64, 256])
                xi = alloc(s1, "xi", [128, 768])
                zetap = alloc(s1, "zetap", [128, 4])
                zs = alloc(s1, "zs", [64, 64])
                cmask = alloc(s1, "cmask", [128, 16 * 64])
                b_t1 = Buf("tab1")
                S.dma("sp", gm2[:], I["g_mix"].rearrange("(k p) -> p k", p=128), writes=[b_t1])
                S.dma("sp", gn[:], I["ret_gn"].rearrange("(k p) -> p k", p=128), writes=[b_t1])
                for a_ in range(3):
                    S.dma("sp", rope[:, a_, :, :], I["c_rope"][a_], writes=[b_t1])
                S.dma("sp", dmp[:], I["c_dmask_p"][:, :], writes=[b_t1])
                S.dma("sp", dms[:], I["c_dmask_s"][:, :], writes=[b_t1])
                S.dma("sp", xi[:], I["c_xi"][0:1, :].partition_broadcast(128), writes=[b_t1])
                S.dma("sp", zetap[:], I["c_zeta_p"][:, :], writes=[b_t1])
                S.dma("sp", zs[:], I["c_zs"][:, :], writes=[b_t1])
                S.dma("sp", cmask[:], I["c_cmask"][0:1, :].partition_broadcast(128), writes=[b_t1])
                def load_wout():
                    load_w_bf16(wout, b_wout, I["w_out"], 8, D, 0)
                    for k in range(4):
                        V(lambda e: e.tensor_scalar(out=wout[:, 4 + k, :], in0=wout[:, 4 + k, :], scalar1=gn[:, k:k + 1],
                                                    scalar2=None, op0=ALU.mult), r=[b_wout, b_t1], w=[b_wout])
                    wout_loaded[0] = True
                t1q = alloc(s1, "t1q", [128, 512])
                t2q = alloc(s1, "t2q", [128, 512])
                t1k = alloc(s1, "t1k", [128, 512])
                t2k = alloc(s1, "t2k", [128, 512])
                qr = alloc(s1, "qr", [128, 512], BF16)
                kr = alloc(s1, "kr", [128, 512], BF16)
                qT = alloc(s1, "qT", [128, 4, 128], BF16)
                qxT = alloc(s1, "qxT", [128, 4, 128], BF16)
                kT = alloc(s1, "kT", [128, 4, 128], BF16)
                vb = alloc(s1, "vb", [128, 512], BF16)
                vz = alloc(s1, "vz", [128, 512], BF16)
                sg_ = alloc(s1, "sgl", [128, 512])
                sT = alloc(s1, "sT", [128, 4, 128], BF16)
                Sst = alloc(s1, "Sst", [128, 4, 128])
                Sbf = alloc(s1, "Sbf", [128, 4, 128], BF16)
                stats = alloc(s1, "stats", [128, 4, 6])
                mv = alloc(s1, "mv", [128, 4, 2])
                rs4 = alloc(s1, "rs4", [128, 4])
                nb4 = alloc(s1, "nb4", [128, 4])
                on = alloc(s1, "on", [128, 512])
                ret = alloc(s1, "ret", [128, 512], BF16)
                retT = alloc(s1, "retT", [128, 4, 128], BF16)
                S0 = [alloc(s1, "S0_%d" % i, [128, 4, 128]) for i in range(2)]
                S0b = [alloc(s1, "S0b_%d" % i, [128, 4, 128], BF16) for i in range(2)]
                qxm = [alloc(s1, "qxm_%d" % i, [128, 4, 64], BF16) for i in range(2)]
                vzb = [alloc(s1, "vzb_%d" % i, [64, 512], BF16) for i in range(2)]
                Sn = [alloc(s1, "Sn_%d" % i, [128, 4, 128]) for i in range(2)]
                bS0 = [Buf("S0_%d" % i, S.GL[i]) for i in range(2)]
                bS0b = [Buf("S0b_%d" % i) for i in range(2)]
                bqxm = [Buf("qxm%d" % i) for i in range(2)]
                bvzb = [Buf("vzb%d" % i) for i in range(2)]
                bSn = [Buf("Sn%d" % i, S.GS[i]) for i in range(2)]
                (b_t1q, b_t2q, b_t1k, b_t2k, b_qr, b_kr, b_qT, b_qxT, b_kT, b_vb, b_vz, b_sg, b_sT, b_Sst, b_Sbf,
                 b_st, b_on, b_ret, b_retT) = [Buf("p1b%d" % i) for i in range(19)]
                b_Sst.grp = S.GS[2]
                V(lambda e: e.memset(Sst[:], 0.0), w=[b_Sst])
                GC_P = [float(g ** 128) for g in GAM]
                GC_S = [float(g ** 4) for g in GAM]

                import os as _os
                _tl = _os.environ.get("K_TILES")
                _tiles = [int(v) for v in _tl.split(",") if int(v) >= 0] if _tl else list(range(NT))
                _step = int(_os.environ.get("K_STEP", "99"))
                hT1s = [hT1, alloc(s1, "hT1c", [128, 8, 128], BF16)]
                bhT1s = [bhT1, Buf("hT1c")]

                def p1b_norm(n):
                    npt_ = TS if n == 16 else 128
                    rmsnorm_hT(x[:npt_, n, :], bx[n], npt_, gm2[:], hT1s[n % 2], bhT1s[n % 2], scrB, 0, None, bg=b_t1)
                if _tiles:
                    p1b_norm(_tiles[0])
                for ti_, n in enumerate(_tiles):
                    is_s = (n == 16)
                    npt = TS if is_s else 128
                    tok0 = n * 128
                    hT1, bhT1 = hT1s[n % 2], bhT1s[n % 2]
                    pob = [4, 6, 7, 1] if is_s else [4, 4, 4, 4]

                    def po(h):
                        if is_s:
                            return PS[pob[h]][:npt, 0:128]
                        return PS[4][:npt, h * 128:(h + 1) * 128]
                    for c in range(4):
                        for kt in range(8):
                            T(lambda e: e.matmul(PS[c][:npt, :], lhsT=hT1[:, kt, 0:npt],
                                                 rhs=wq[:, kt, c * 512:(c + 1) * 512], start=(kt == 0), stop=(kt == 7)),
                              r=[bhT1, b_wqc[c]], w=[bPS[c]])
                    if not wout_loaded[0]:
                        load_wout()
                    if _step <= 1:
                        continue
                    for (bank, t1_, t2_, out_, bt1, bt2, bo) in ((0, t1q, t2q, qr, b_t1q, b_t2q, b_qr),
                                                               (1, t1k, t2k, kr, b_t1k, b_t2k, b_kr)):
                        pv4 = PS[bank][:npt, :].rearrange("p (h a j) -> p h a j", h=4, a=2)
                        t1v = t1_[:npt, :].rearrange("p (h a j) -> p h a j", h=4, a=2)
                        t2v = t2_[:npt, :].rearrange("p (h a j) -> p h a j", h=4, a=2)
                        cosb = rope[:npt, 0, n, :].unsqueeze(1).unsqueeze(1).to_broadcast([npt, 4, 2, 64])
                        sinb = rope[:npt, 1, n, :].unsqueeze(1).to_broadcast([npt, 4, 64])
                        nsinb = rope[:npt, 2, n, :].unsqueeze(1).to_broadcast([npt, 4, 64])
                        V(lambda e: e.tensor_tensor(out=t1v, in0=pv4, in1=cosb, op=ALU.mult), r=[bPS[bank], b_t1], w=[bt1])
                        V(lambda e: e.tensor_tensor(out=t2v[:, :, 0, :], in0=pv4[:, :, 1, :], in1=nsinb, op=ALU.mult),
                          r=[bPS[bank], b_t1], w=[bt2])
                        V(lambda e: e.tensor_tensor(out=t2v[:, :, 1, :], in0=pv4[:, :, 0, :], in1=sinb, op=ALU.mult),
                          r=[bPS[bank], b_t1], w=[bt2])
                        V(lambda e: e.tensor_tensor(out=out_[:npt, :], in0=t1_[:npt, :], in1=t2_[:npt, :], op=ALU.add),
                           r=[bt1, bt2], w=[bo])
                    if _step <= 2:
                        continue
                    A(lambda e: e.copy(out=vb[:npt, :], in_=PS[2][:npt, :]), r=[bPS[2]], w=[b_vb])
                    if not is_s:
                        V(lambda e: e.tensor_tensor(
                            out=vz[:, :].rearrange("p (h e) -> p h e", h=4),
                            in0=PS[2][:, :].rearrange("p (h e) -> p h e", h=4),
                            in1=zetap[:, :].unsqueeze(2).to_broadcast([128, 4, 128]), op=ALU.mult),
                          r=[bPS[2], b_t1], w=[b_vz])
                    A(lambda e: e.activation(out=sg_[:npt, :], in_=PS[3][:npt, :], func=AF.Silu), r=[bPS[3]], w=[b_sg])
                    pv4b = ps_bf(4)
                    pv5b = ps_bf(5)
                    for h in range(4):
                        T(lambda e: e.transpose(out=pv4b[:, h * 128:h * 128 + npt], in_=qr[:npt, h * 128:(h + 1) * 128],
                                                identity=identb[:npt, :npt]), r=[b_qr, b_const], w=[bPS[4]])
                    for h in range(4):
                        T(lambda e: e.transpose(out=pv5b[:, h * 128:h * 128 + npt], in_=kr[:npt, h * 128:(h + 1) * 128],
                                                identity=identb[:npt, :npt]), r=[b_kr, b_const], w=[bPS[5]])
                    q4 = pv4b[:, 0:512].rearrange("p (h t) -> p h t", h=4)[:, :, 0:npt]
                    k4 = pv5b[:, 0:512].rearrange("p (h t) -> p h t", h=4)[:, :, 0:npt]
                    A(lambda e: e.copy(out=qT[:, :, 0:npt], in_=q4), r=[bPS[4]], w=[b_qT])
                    xiv = (xi[:, 0:512].rearrange("p (h t) -> p h t", h=4) if not is_s
                           else xi[:, 512:768].rearrange("p (h t) -> p h t", h=4))
                    V(lambda e: e.tensor_tensor(out=qxT[:, :, 0:npt], in0=q4, in1=xiv, op=ALU.mult),
                      r=[bPS[4], b_t1], w=[b_qxT])
                    A(lambda e: e.copy(out=kT[:, :, 0:npt], in_=k4), r=[bPS[5]], w=[b_kT])
                    if _step <= 3:
                        continue
                    for h in range(4):
                        T(lambda e: e.matmul(PS[6][:npt, h * 128:h * 128 + npt], lhsT=kT[:, h, 0:npt], rhs=qT[:, h, 0:npt],
                                             start=True, stop=True), r=[b_kT, b_qT], w=[bPS[6]])
                    dmv = (dmp[:, :].rearrange("p (h t) -> p h t", h=4) if not is_s
                           else dms[:, :].rearrange("p (h t) -> p h t", h=4))
                    V(lambda e: e.tensor_tensor(out=sT[:npt, :, 0:npt],
                                                in0=PS[6][:npt, :].rearrange("p (h t) -> p h t", h=4)[:, :, 0:npt],
                                                in1=dmv, op=ALU.mult), r=[bPS[6], b_t1], w=[b_sT])
                    if _step <= 4:
                        continue
                    if ti_ + 1 < len(_tiles):
                        p1b_norm(_tiles[ti_ + 1])
                    for h in range(4):
                        only = (n == 0)
                        T(lambda e: e.matmul(po(h), lhsT=sT[:npt, h, 0:npt],
                                             rhs=vb[:npt, h * 128:(h + 1) * 128], start=True, stop=only),
                          r=[b_sT, b_vb], w=[bPS[pob[h]]])
                        if (not is_s) and n > 0:
                            T(lambda e: e.matmul(po(h), lhsT=qxT[:, h, 0:npt],
                                                 rhs=Sbf[:, h, :], start=False, stop=True),
                              r=[b_qxT, b_Sbf], w=[bPS[4]])
                    if not is_s:
                        for h in range(4):
                            T(lambda e: e.matmul(PS[5][:, h * 128:(h + 1) * 128], lhsT=kr[:, h * 128:(h + 1) * 128],
                                                 rhs=vz[:, h * 128:(h + 1) * 128], start=True, stop=True),
                              r=[b_kr, b_vz], w=[bPS[5]])
                        for h in range(4):
                            V(lambda e: e.scalar_tensor_tensor(out=Sst[:, h, :], in0=Sst[:, h, :], scalar=GC_P[h],
                                                               op0=ALU.mult, in1=PS[5][:, h * 128:(h + 1) * 128],
                                                               op1=ALU.add), r=[b_Sst, bPS[5]], w=[b_Sst])
                        A(lambda e: e.copy(out=Sbf[:], in_=Sst[:]), r=[b_Sst], w=[b_Sbf])
                        if n == NTP - 1:
                            S.dma("sp", O["o_ret_p"].rearrange("h d e -> d h e"), Sst[:], reads=[b_Sst])
                    else:
                        for b in range(16):
                            sl = b % 2
                            S.dma("sp", S0[sl][:], I["sret"][b].rearrange("h d e -> d h e"), writes=[bS0[sl]])
                            A(lambda e: e.copy(out=S0b[sl][:], in_=S0[sl][:]), r=[bS0[sl]], w=[bS0b[sl]])
                            V(lambda e: e.tensor_tensor(
                                out=qxm[sl][:], in0=qxT[:, :, 0:64],
                                in1=cmask[:, b * 64:(b + 1) * 64].unsqueeze(1).to_broadcast([128, 4, 64]), op=ALU.mult),
                              r=[b_qxT, b_t1], w=[bqxm[sl]])
                            for h in range(4):
                                T(lambda e: e.matmul(po(h), lhsT=qxm[sl][:, h, :],
                                                     rhs=S0b[sl][:, h, :], start=False, stop=(b == 15)),
                                  r=[bqxm[sl], bS0b[sl]], w=[bPS[pob[h]]])
                            V(lambda e: e.tensor_tensor(
                                out=vzb[sl][:, :].rearrange("p (h e) -> p h e", h=4),
                                in0=PS[2][:64, :].rearrange("p (h e) -> p h e", h=4),
                                in1=zs[:, b * 4:(b + 1) * 4].unsqueeze(2).to_broadcast([64, 4, 128]), op=ALU.mult),
                              r=[bPS[2], b_t1], w=[bvzb[sl]])
                            kvb = 5 if sl == 0 else 0
                            for h in range(4):
                                T(lambda e: e.matmul(PS[kvb][:, h * 128:(h + 1) * 128], lhsT=kr[:64, h * 128:(h + 1) * 128],
                                                     rhs=vzb[sl][:, h * 128:(h + 1) * 128], start=True, stop=True),
                                  r=[b_kr, bvzb[sl]], w=[bPS[kvb]])
                            for h in range(4):
                                V(lambda e: e.scalar_tensor_tensor(out=Sn[sl][:, h, :], in0=S0[sl][:, h, :], scalar=GC_S[h],
                                                                   op0=ALU.mult, in1=PS[kvb][:, h * 128:(h + 1) * 128],
                                                                   op1=ALU.add), r=[bS0[sl], bPS[kvb]], w=[bSn[sl]])
                            S.dma("sp", O["o_ret_s"][b].rearrange("h d e -> d h e"), Sn[sl][:], reads=[bSn[sl]])
                    if _step <= 5:
                        continue
                    for h in range(4):
                        V(lambda e: e.bn_stats(out=stats[:npt, h, :], in_=po(h)),
                          r=[bPS[pob[h]]], w=[b_st])
                    for h in range(4):
                        V(lambda e: e.bn_aggr(out=mv[:npt, h, :], in_=stats[:npt, h, :]), r=[b_st], w=[b_st])
                    A(lambda e: e.activation(out=rs4[:npt, :], in_=mv[:npt, :, 1], func=AF.Sqrt, scale=1.0,
                                             bias=epsc[:npt, :]), r=[b_st, b_const], w=[b_st])
                    V(lambda e: e.reciprocal(out=rs4[:npt, :], in_=rs4[:npt, :]), r=[b_st], w=[b_st])
                    V(lambda e: e.scalar_tensor_tensor(out=nb4[:npt, :], in0=mv[:npt, :, 0], scalar=-1.0, op0=ALU.mult,
                                                       in1=rs4[:npt, :], op1=ALU.mult), r=[b_st], w=[b_st])
                    for h in range(4):
                        A(lambda e: e.activation(out=on[:npt, h * 128:(h + 1) * 128], in_=po(h),
                                                 func=AF.Identity, scale=rs4[:npt, h:h + 1], bias=nb4[:npt, h:h + 1]),
                          r=[bPS[pob[h]], b_st], w=[b_on])
                    V(lambda e: e.tensor_tensor(out=ret[:npt, :], in0=on[:npt, :], in1=sg_[:npt, :], op=ALU.mult),
                       r=[b_on, b_sg], w=[b_ret])
                    if _step <= 6:
                        continue
                    pv6b = ps_bf(6)
                    for h in range(4):
                        T(lambda e: e.transpose(out=pv6b[:, h * 128:h * 128 + npt], in_=ret[:npt, h * 128:(h + 1) * 128],
                                                identity=identb[:npt, :npt]), r=[b_ret, b_const], w=[bPS[6]])
                    A(lambda e: e.copy(out=retT[:, :, 0:npt],
                                       in_=pv6b[:, 0:512].rearrange("p (h t) -> p h t", h=4)[:, :, 0:npt]),
                      r=[bPS[6]], w=[b_retT])
                    if _step <= 7:
                        continue
                    bi_ = min(n // 4, 4)
                    for half in range(2):
                        bank = 2 + half
                        for kt in range(8):
                            lh = ssmT[:, kt, tok0:tok0 + npt] if kt < 4 else retT[:, kt - 4, 0:npt]
                            T(lambda e: e.matmul(PS[bank][:npt, :], lhsT=lh, rhs=wout[:, kt, half * 512:(half + 1) * 512],
                                                 start=(kt == 0), stop=(kt == 7)),
                              r=[b_ssmT[bi_], b_retT, b_wout], w=[bPS[bank]])
                        resid_add(n, npt, half, bank)
                S.barrier()
            if dbg:
                for n in range(NT):
                    S.dma("sp", O["dbg_x"][:, n, :], x[:, n, :], reads=[bx[n]])
            if stage <= 2:
                S.barrier()
                S.run_block()
                nck.__exit__(None, None, None)
                return nc

            with ExitStack() as s2:
                gx = alloc(s2, "gx", [128, 8])
                gmem = alloc(s2, "gmem", [128, 8])
                ones = alloc(s2, "ones", [128, 128], BF16)
                b_t2 = Buf("tab2")
                S.dma("sp", gx[:], I["g_xattn"].rearrange("(k p) -> p k", p=128), writes=[b_t2])
                S.dma("sp", gmem[:], I["g_mem"].rearrange("(k p) -> p k", p=128), writes=[b_t2])
                V(lambda e: e.memset(ones[:], 1.0), w=[b_t2])
                KT = alloc(s2, "KT", [128, 8, MEM], BF16)
                Vm = alloc(s2, "Vm", [128, 2, D], BF16)
                b_KT, b_Vm = Buf("KT"), Buf("Vm")
                wmq = alloc(s2, "wmq", [128, 8, D], BF16)
                b_wmq, b_wmo = Buf("wmq", S.GW[2]), Buf("wmo", S.GW[3])
                with ExitStack() as s2a:
                    wmk = alloc(s2a, "wmk", [128, 8, D], BF16)
                    wmv = alloc(s2a, "wmv", [128, 8, D], BF16)
                    b_wmk, b_wmv = Buf("wmk", S.GW[0]), Buf("wmv", S.GW[1])
                    load_w_bf16(wmk, b_wmk, I["w_mk"], 8, D, 0)
                    load_w_bf16(wmv, b_wmv, I["w_mv"], 8, D, 0)
                    load_w_bf16(wmq, b_wmq, I["w_mq"], 8, D, 0)
                    mx = [alloc(s2a, "mx%d" % i, [128, D]) for i in range(2)]
                    bmx = [Buf("mx%d" % i, S.GL[i]) for i in range(2)]
                    mhT = alloc(s2a, "mhT", [128, 8, MEM], BF16)
                    b_mhT = Buf("mhT")
                    mo = [alloc(s2a, "mo%d" % i, [128, D]) for i in range(2)]
                    bmo = [Buf("mo%d" % i, S.GS[i]) for i in range(2)]
                    _k2a = int(_os.environ.get("K2A", "9"))
                    for mt in range(2):
                        S.dma("sp", mx[mt][:], I["memp"][mt * 128:(mt + 1) * 128, :], writes=[bmx[mt]])
                        if _k2a >= 1:
                            rmsnorm_hT(mx[mt][:, :], bmx[mt], 128, gmem[:], mhT, b_mhT, scrB, mt * 128, None,
                                       ln=True, bg=b_t2)
                    oi = 0
                    for (wm, bwm, oname, isv) in ((wmk, b_wmk, "o_mk", False), (wmv, b_wmv, "o_mv", True)) if _k2a >= 2 else ():
                        for mt in range(2):
                            sl = oi % 2
                            oi += 1
                            for half in range(2):
                                bank = half
                                for kt in range(8):
                                    T(lambda e: e.matmul(PS[bank][:, :], lhsT=mhT[:, kt, mt * 128:(mt + 1) * 128],
                                                         rhs=wm[:, kt, half * 512:(half + 1) * 512], start=(kt == 0),
                                                         stop=(kt == 7)), r=[b_mhT, bwm], w=[bPS[bank]])
                                A(lambda e: e.copy(out=mo[sl][:, half * 512:(half + 1) * 512], in_=PS[bank][:, :]),
                                  r=[bPS[bank]], w=[bmo[sl]])
                                if isv:
                                    V(lambda e: e.tensor_copy(out=Vm[:, mt, half * 512:(half + 1) * 512], in_=PS[bank][:, :]),
                                      r=[bPS[bank]], w=[b_Vm])
                            S.dma("sp", O[oname][mt * 128:(mt + 1) * 128, :], mo[sl][:], reads=[bmo[sl]])
                    for j in range(8 if _k2a >= 3 else 0):
                        bank = 2 + (j % 2)
                        for kt in range(8):
                            T(lambda e: e.matmul(PS[bank][:, 0:MEM], lhsT=wmk[:, kt, j * 128:(j + 1) * 128],
                                                 rhs=mhT[:, kt, :], start=(kt == 0), stop=(kt == 7)),
                              r=[b_mhT, b_wmk], w=[bPS[bank]])
                        A(lambda e: e.copy(out=KT[:, j, :], in_=PS[bank][:, 0:MEM]), r=[bPS[bank]], w=[b_KT])
                    S.barrier()
                wmo = alloc(s2, "wmo", [128, 8, D], BF16)
                load_w_bf16(wmo, b_wmo, I["w_mo"], 8, D, 0)
                hT4 = alloc(s2, "hT4", [128, 8, 512], BF16)
                qm4 = alloc(s2, "qm4", [128, 8, 512], BF16)
                oT4 = alloc(s2, "oT4", [128, 8, 512], BF16)
                eT4 = [alloc(s2, "eT4_%d" % i, [128, 2, 512], BF16) for i in range(2)]
                rdn4 = [alloc(s2, "rdn4_%d" % i, [128, 512]) for i in range(2)]
                b_hT4, b_qm4, b_oT4 = Buf("hT4"), Buf("qm4"), Buf("oT4")
                b_eT4 = [Buf("eT4_%d" % i) for i in range(2)]
                b_rdn4 = [Buf("rdn4_%d" % i) for i in range(2)]
                Kb = [alloc(s2, "Kb%d" % i, [128, 2, D]) for i in range(2)]
                bKb = [Buf("Kb%d" % i, S.GL[i]) for i in range(2)]
                KbT = [alloc(s2, "KbT%d" % i, [128, 8, MEM], BF16) for i in range(2)]
                bKbT = [Buf("KbT%d" % i) for i in range(2)]
                Vb = [alloc(s2, "Vb%d" % i, [128, 2, D], BF16) for i in range(2)]
                bVb = [Buf("Vb%d" % i, S.GW[i]) for i in range(2)]
                eTs = alloc(s2, "eTs", [128, 2, 4, 64], BF16)
                b_eTs = Buf("eTs")
                qrot = [0]

                def q_proj(nc_):
                    for j in range(8):
                        bank = 5 + (qrot[0] % 3)
                        qrot[0] += 1
                        for kt in range(8):
                            T(lambda e: e.matmul(PS[bank][:, 0:nc_], lhsT=wmq[:, kt, j * 128:(j + 1) * 128],
                                                 rhs=hT4[:, kt, 0:nc_], start=(kt == 0), stop=(kt == 7)),
                              r=[b_wmq, b_hT4], w=[bPS[bank]])
                        A(lambda e: e.activation(out=qm4[:, j, 0:nc_], in_=PS[bank][:, 0:nc_], func=AF.Copy,
                                                 scale=1.0 / 16.0), r=[bPS[bank]], w=[b_qm4])

                def w_mo_resid(n, npt, c0):
                    for half in range(2):
                        bank = 5 + (qrot[0] % 3)
                        qrot[0] += 1
                        for j in range(8):
                            T(lambda e: e.matmul(PS[bank][:npt, :], lhsT=oT4[:, j, c0:c0 + npt],
                                                 rhs=wmo[:, j, half * 512:(half + 1) * 512], start=(j == 0), stop=(j == 7)),
                              r=[b_oT4, b_wmo], w=[bPS[bank]])
                        resid_add(n, npt, half, bank)

                for bi in range(4):
                    for ti in range(4):
                        n = bi * 4 + ti
                        rmsnorm_hT(x[:, n, :], bx[n], 128, gx[:], hT4, b_hT4, scrB, ti * 128, None, ln=True, bg=b_t2)
                    q_proj(512)
                    for h in range(4):
                        par = h % 2
                        for mt in range(2):
                            bank = mt
                            for dt_ in range(2):
                                T(lambda e: e.matmul(PS[bank][:, :], lhsT=KT[:, h * 2 + dt_, mt * 128:(mt + 1) * 128],
                                                     rhs=qm4[:, h * 2 + dt_, :], start=(dt_ == 0), stop=(dt_ == 1)),
                                  r=[b_KT, b_qm4], w=[bPS[bank]])
                            A(lambda e: e.activation(out=eT4[par][:, mt, :], in_=PS[bank][:, :], func=AF.Exp),
                              r=[bPS[bank]], w=[b_eT4[par]])
                        for mt in range(2):
                            T(lambda e: e.matmul(PS[2][:, :], lhsT=ones[:, :], rhs=eT4[par][:, mt, :], start=(mt == 0),
                                                 stop=(mt == 1)), r=[b_t2, b_eT4[par]], w=[bPS[2]])
                        A(lambda e: e.activation(out=rdn4[par][:, :], in_=PS[2][:, :], func=AF.Ln), r=[bPS[2]], w=[b_rdn4[par]])
                        A(lambda e: e.activation(out=rdn4[par][:, :], in_=rdn4[par][:, :], func=AF.Exp, scale=-1.0),
                          r=[b_rdn4[par]], w=[b_rdn4[par]])
                        for dt_ in range(2):
                            bank = 3 + dt_
                            j = h * 2 + dt_
                            for mt in range(2):
                                T(lambda e: e.matmul(PS[bank][:, :], lhsT=Vm[:, mt, j * 128:(j + 1) * 128],
                                                     rhs=eT4[par][:, mt, :], start=(mt == 0), stop=(mt == 1)),
                                  r=[b_Vm, b_eT4[par]], w=[bPS[bank]])
                            V(lambda e: e.tensor_tensor(out=oT4[:, j, :], in0=PS[bank][:, :], in1=rdn4[par][:, :], op=ALU.mult),
                              r=[bPS[bank], b_rdn4[par]], w=[b_oT4])
                    for ti in range(4):
                        w_mo_resid(bi * 4 + ti, 128, ti * 128)
                n = 16
                rmsnorm_hT(x[:TS, n, :], bx[n], TS, gx[:], hT4, b_hT4, scrB, 0, None, ln=True, bg=b_t2)
                q_proj(TS)
                rden_s = rdn4[0][:, 0:256].rearrange("p (h t) -> p h t", h=4)
                for b in range(16):
                    sl = b % 2
                    S.dma("sp", Kb[sl][:], I["ck"][b].rearrange("(mt p) d -> p mt d", p=128), writes=[bKb[sl]])
                    for q4 in range(4):
                        bank = 2 + (q4 % 2)
                        for i4 in range(4):
                            idx = q4 * 4 + i4
                            j, mt = idx // 2, idx % 2
                            T(lambda e: e.transpose(out=PS[bank][:, i4 * 128:(i4 + 1) * 128],
                                                    in_=Kb[sl][:, mt, j * 128:(j + 1) * 128], identity=identf[:]),
                              r=[bKb[sl], b_const], w=[bPS[bank]])
                        A(lambda e: e.copy(
                            out=KbT[sl][:, 2 * q4:2 * q4 + 2, :].rearrange("p j (m t) -> p j m t", m=2),
                            in_=PS[bank][:, :].rearrange("p (j m t) -> p j m t", j=2, m=2)),
                          r=[bPS[bank]], w=[bKbT[sl]])
                    for h in range(4):
                        for mt in range(2):
                            c0 = mt * 256 + h * 64 + 4 * b
                            for dt_ in range(2):
                                T(lambda e: e.matmul(PS[4][:, c0:c0 + 4],
                                                     lhsT=KbT[sl][:, h * 2 + dt_, mt * 128:(mt + 1) * 128],
                                                     rhs=qm4[:, h * 2 + dt_, 4 * b:4 * b + 4], start=(dt_ == 0),
                                                     stop=(dt_ == 1)), r=[bKbT[sl], b_qm4], w=[bPS[4]])
                A(lambda e: e.activation(out=eTs[:].rearrange("p m h t -> p (m h t)"), in_=PS[4][:, :], func=AF.Exp),
                  r=[bPS[4]], w=[b_eTs])
                for h in range(4):
                    for mt in range(2):
                        T(lambda e: e.matmul(PS[0][:, h * 64:(h + 1) * 64], lhsT=ones[:, :], rhs=eTs[:, mt, h, :],
                                             start=(mt == 0), stop=(mt == 1)), r=[b_t2, b_eTs], w=[bPS[0]])
                V(lambda e: e.reciprocal(out=rden_s, in_=PS[0][:, 0:256].rearrange("p (h t) -> p h t", h=4)),
                  r=[bPS[0]], w=[b_rdn4[0]])
                for b in range(16):
                    sl = b % 2
                    for mt in range(2):
                        S.dma("pool", Vb[sl][:, mt, :], I["cv"][b, mt * 128:(mt + 1) * 128, :], writes=[bVb[sl]])
                    for j in range(8):
                        h = j // 2
                        for mt in range(2):
                            T(lambda e: e.matmul(PS[1][:, j * 64 + 4 * b:j * 64 + 4 * b + 4],
                                                 lhsT=Vb[sl][:, mt, j * 128:(j + 1) * 128],
                                                 rhs=eTs[:, mt, h, 4 * b:4 * b + 4], start=(mt == 0), stop=(mt == 1)),
                              r=[bVb[sl], b_eTs], w=[bPS[1]])
                V(lambda e: e.tensor_tensor(
                    out=oT4[:, :, 0:64].rearrange("p (h a) t -> p h a t", a=2),
                    in0=PS[1][:, :].rearrange("p (h a t) -> p h a t", h=4, a=2),
                    in1=rden_s.unsqueeze(2).to_broadcast([128, 4, 2, 64]), op=ALU.mult),
                  r=[bPS[1], b_rdn4[0]], w=[b_oT4])
                w_mo_resid(16, TS, 0)
                S.barrier()
            if stage <= 3:
                if dbg:
                    for n in range(NT):
                        S.dma("sp", O["dbg_x"][:, n, :], x[:, n, :], reads=[bx[n]])
                S.barrier()
                S.run_block()
                nck.__exit__(None, None, None)
                return nc

            with ExitStack() as s3:
                gml = alloc(s3, "gml", [128, 8])
                b_t3 = Buf("tab3")
                S.dma("sp", gml[:], I["g_mlp"].rearrange("(k p) -> p k", p=128), writes=[b_t3])
                hTa = alloc(s3, "hTa", [128, 8, NTOK], BF16)
                b_hTa = [Buf("hTa%d" % n) for n in range(NT)]
                wup = [alloc(s3, "wup%d" % i, [128, 8, 512], BF16) for i in range(2)]
                wdn = [alloc(s3, "wdn%d" % i, [128, 4, D], BF16) for i in range(2)]
                bwup = [Buf("wup%d" % i, S.GW[i]) for i in range(2)]
                bwdn = [Buf("wdn%d" % i, S.GW[2 + i]) for i in range(2)]
                rl = [alloc(s3, "rl%d" % i, [128, 512]) for i in range(2)]
                brl = [Buf("rl%d" % i) for i in range(2)]
                aT = [alloc(s3, "aT%d" % i, [128, 4, 512], BF16) for i in range(2)]
                baT = [Buf("aT%d" % i) for i in range(2)]

                def load_fc(fc):
                    sl = fc % 2
                    for kt in range(8):
                        S.dma("pool", wup[sl][:, kt, :], I["w_up"][kt * 128:(kt + 1) * 128, fc * 512:(fc + 1) * 512],
                              writes=[bwup[sl]])
                    for ft in range(4):
                        S.dma("pool", wdn[sl][:, ft, :], I["w_down"][fc * 512 + ft * 128:fc * 512 + (ft + 1) * 128, :],
                              writes=[bwdn[sl]])
                load_fc(0)
                scrB["pb"] = [7, 6]
                for n in range(NT):
                    npt = TS if n == 16 else 128
                    rmsnorm_hT(x[:npt, n, :], bx[n], npt, gml[:], hTa, b_hTa[n], scrB, n * 128, None, ln=True, bg=b_t3)
                blocks3 = [(i * 512, 512) for i in range(4)] + [(SEQ, TS)]
                ai = 0
                ri = 0
                di = 0
                for fc in range(8):
                    sl = fc % 2
                    if fc + 1 < 8:
                        load_fc(fc + 1)
                    for (t0, nn) in blocks3:
                        tiles = list(range(t0 // 128, t0 // 128 + (nn + 127) // 128))
                        asl = ai % 2
                        ai += 1
                        for ft in range(4):
                            bank = ft
                            for kt in range(8):
                                T(lambda e: e.matmul(PS[bank][:, 0:nn], lhsT=wup[sl][:, kt, ft * 128:(ft + 1) * 128],
                                                     rhs=hTa[:, kt, t0:t0 + nn], start=(kt == 0), stop=(kt == 7)),
                                  r=[bwup[sl]] + [b_hTa[t] for t in tiles], w=[bPS[bank]])
                            rsl = ri % 2
                            ri += 1
                            A(lambda e: e.activation(out=rl[rsl][:, 0:nn], in_=PS[bank][:, 0:nn], func=AF.Relu),
                              r=[bPS[bank]], w=[brl[rsl]])
                            V(lambda e: e.tensor_tensor(out=aT[asl][:, ft, 0:nn], in0=rl[rsl][:, 0:nn], in1=rl[rsl][:, 0:nn],
                                                        op=ALU.mult), r=[brl[rsl]], w=[baT[asl]])
                        for ti, tl in enumerate(tiles):
                            npt = TS if tl == 16 else 128
                            for half in range(2):
                                bank = 4 + (di % 4)
                                di += 1
                                for ft in range(4):
                                    T(lambda e: e.matmul(PS[bank][:npt, :], lhsT=aT[asl][:, ft, ti * 128:ti * 128 + npt],
                                                         rhs=wdn[sl][:, ft, half * 512:(half + 1) * 512], start=(ft == 0),
                                                         stop=(ft == 3)), r=[baT[asl], bwdn[sl]], w=[bPS[bank]])
                                resid_add(tl, npt, half, bank)
                S.barrier()
            if dbg:
                for n in range(NT):
                    S.dma("sp", O["dbg_x"][:, n, :], x[:, n, :], reads=[bx[n]])
            with ExitStack() as s4:
                gf = alloc(s4, "gf", [128, D])
                b_gf = Buf("gf")
                S.dma("sp", gf[:], I["g_final"].rearrange("(o d) -> o d", o=1).partition_broadcast(128), writes=[b_gf])
                yst = [alloc(s4, "yst%d" % i, [128, D]) for i in range(3)]
                byst = [Buf("yst%d" % i, S.GS[i]) for i in range(3)]
                for n in range(NT):
                    npt = TS if n == 16 else 128
                    sl = n % 3
                    k4 = n % 2
                    sq, ss, rstd, bscr = scrB["sq"][k4], scrB["ss"][k4], scrB["rstd"][k4], scrB["ba"][k4]
                    A(lambda e: e.activation(out=sq[:npt, :], in_=x[:npt, n, :], func=AF.Square, accum_out=ss[:npt, :]),
                      r=[bx[n]], w=[bscr])
                    A(lambda e: e.activation(out=rstd[:npt, :], in_=ss[:npt, :], func=AF.Ln, scale=1.0 / D,
                                             bias=epsc[:npt, :]), r=[bscr, b_const], w=[bscr])
                    A(lambda e: e.activation(out=rstd[:npt, :], in_=rstd[:npt, :], func=AF.Exp, scale=-0.5),
                      r=[bscr], w=[bscr])
                    V(lambda e: e.scalar_tensor_tensor(out=yst[sl][:npt, :], in0=x[:npt, n, :], scalar=rstd[:npt, :],
                                                       op0=ALU.mult, in1=gf[:npt, :], op1=ALU.mult),
                      r=[bx[n], bscr, b_gf], w=[byst[sl]])
                    if n < 16:
                        S.dma("sp", O["yp"][n * 128:(n + 1) * 128, :], yst[sl][:, :], reads=[byst[sl]])
                    else:
                        S.dma("sp", O["ys"][:, :], yst[sl][:TS, :], reads=[byst[sl]])
                S.barrier()
            S.barrier()
            S.run_block()
            nck.__exit__(None, None, None)
    return nc


_NC = None


def kernel(**inputs):
    global _NC
    if _NC is None:
        _NC = build()
    maps = _in_maps(inputs)
    res = run_bass_kernel_spmd(_NC, maps, core_ids=list(range(8)))
    R = res.results
    f = np.float32

    def cat(name, shape=None):
        return np.stack([np.asarray(R[c][name], f) for c in range(8)])
    y_prompt = cat("yp")
    y_sample = cat("ys").reshape(128, 4, D)
    s5r_p = cat("o_s5r_p")[None]
    s5i_p = cat("o_s5i_p")[None]
    ret_p = cat("o_ret_p")[None]
    mk_p = cat("o_mk").reshape(8, MEM, 4, 256)[None]
    mv_p = cat("o_mv").reshape(8, MEM, 4, 256)[None]
    s5r_s = cat("o_s5r_s").reshape(128, G, 64)[None]
    s5i_s = cat("o_s5i_s").reshape(128, G, 64)[None]
    ret_s = cat("o_ret_s").reshape(128, 4, 128, 128)[None]
    return (y_prompt, y_sample, s5r_p, s5i_p, ret_p, mk_p, mv_p, s5r_s, s5i_s, ret_s)


def _in_maps(inputs):
    cst = _consts()
    f = np.float32
    maps = []
    w = {}
    for k in W_NAMES:
        a = np.asarray(inputs[k], f)
        if k != "g_final":
            a = a[0]
        w[k] = np.ascontiguousarray(a.reshape(W_SHAPES[k]))
    for c in range(8):
        m = dict(w)
        m.update(cst)
        b0 = 16 * c
        m["xp"] = np.ascontiguousarray(np.asarray(inputs["x_prompt"], f)[c])
        m["xs"] = np.ascontiguousarray(np.asarray(inputs["x_sample"], f)[b0:b0 + 16].reshape(TS, D))
        m["memp"] = np.ascontiguousarray(np.asarray(inputs["mem_prompt"], f)[c])
        m["s5r"] = np.ascontiguousarray(np.asarray(inputs["state_s5_re"], f)[0, b0:b0 + 16].reshape(512, 64))
        m["s5i"] = np.ascontiguousarray(np.asarray(inputs["state_s5_im"], f)[0, b0:b0 + 16].reshape(512, 64))
        m["sret"] = np.ascontiguousarray(np.asarray(inputs["state_ret"], f)[0, b0:b0 + 16])
        m["ck"] = np.ascontiguousarray(np.asarray(inputs["cache_mem_k"], f)[0, b0:b0 + 16].reshape(16, MEM, D))
        m["cv"] = np.ascontiguousarray(np.asarray(inputs["cache_mem_v"], f)[0, b0:b0 + 16].reshape(16, MEM, D))
        maps.append(m)
    return maps
```

```python
import numpy as np
import concourse.bass as bass
import concourse.mybir as mybir
from concourse.bass_utils import run_bass_kernel_spmd
from contextlib import ExitStack

F32 = mybir.dt.float32
BF16 = mybir.dt.bfloat16
AF = mybir.ActivationFunctionType
ALU = mybir.AluOpType

D = 1024
SEQ = 2048
NTP = 16
TS = 64
NT = 17
NTOK = SEQ + TS
G = 32
DFF = 4096
MEM = 256
EPS = 1e-6
PAST = 16384.0
MAGIC = 12582912.0
TWO_PI = float(2.0 * np.pi)
ML = [7, 6, 5, 4, 3, 2, 1, 0, 1, 2, 3, 4, 5, 6, 7, 8, -4, 0.5]
K1 = len(ML)
I_A1, I_A8, I_A4, I_AM4, I_HALF = 8, 15, 3, 16, 17
GAM = [1.0 - 2.0 ** (-5.0 - h) for h in range(4)]


class Grp:
    __slots__ = ("sem", "cnt", "sealed")


class Buf:
    __slots__ = ("w", "r", "name", "grp", "ps")

    def __init__(self, name="", grp=None, ps=False):
        self.w = None
        self.r = []
        self.name = name
        self.grp = grp
        self.ps = ps


class _Rec:
    def __init__(self):
        self.call = None

    def __getattr__(self, name):
        def f(*a, **kw):
            self.call = (name, a, kw)
            return self
        return f


class Sched:
    ENG = ("pe", "dve", "act", "pool", "sp")

    def __init__(self, nc, stack, self_sync=("dve", "act", "pool")):
        self.nc = nc
        self.stack = stack
        self.prog = {k: [] for k in self.ENG}
        self.cnt = {k: 0 for k in self.ENG}
        self.waited = {k: {} for k in self.ENG}
        self.sem = {}
        self.nsem = 0
        for k in ("pe", "dve", "act", "pool"):
            self.sem[k] = self.new_sem("c_" + k)
        self.self_sync = set(self_sync)
        self.groups = []
        self.GC = self.group("gc")
        self.GP = self.group("gp")
        self.GW = [self.group("gw%d" % i) for i in range(4)]
        self.GX = self.group("gx")
        self.GL = [self.group("gl%d" % i) for i in range(2)]
        self.GS = [self.group("gs%d" % i) for i in range(3)]

    def group(self, name):
        g = Grp()
        g.sem = self.new_sem(name)
        g.cnt = 0
        g.sealed = False
        self.groups.append(g)
        return g

    def new_sem(self, name):
        self.nsem += 1
        assert self.nsem < 98, "too many semaphores"
        return self.stack.enter_context(self.nc.semaphore(name + "_%d" % self.nsem))

    def _waits(self, eng, deps):
        w = self.waited[eng]
        need = {}
        dd = []
        for d in deps:
            if isinstance(d, Grp):
                d.sealed = True
                dd.append((d.sem, d.cnt))
            else:
                dd.append(d)
        deps = dd
        for (s, v) in deps:
            if eng in self.sem and s is self.sem[eng] and eng not in self.self_sync:
                continue
            k = id(s)
            if w.get(k, 0) >= v:
                continue
            if k not in need or need[k][1] < v:
                need[k] = (s, v)
        for k, (s, v) in need.items():
            w[k] = v
            self.prog[eng].append(lambda e, s=s, v=v: e.wait_ge(s, v))

    def op(self, eng, fn, reads=(), writes=()):
        deps = []
        for b in reads:
            if b.w is not None:
                deps.append(b.w)
            if b.ps:
                mys = self.sem[eng]
                deps.extend(d for d in b.r if not (isinstance(d, tuple) and d[0] is mys))
        for b in writes:
            if b.w is not None:
                deps.append(b.w)
            deps.extend(b.r)
        self._waits(eng, deps)
        self.cnt[eng] += 1
        c = self.cnt[eng]
        s = self.sem[eng]
        rec = _Rec()
        fn(rec)
        name, a, kw = rec.call
        self.prog[eng].append(lambda e, name=name, a=a, kw=kw, s=s: getattr(e, name)(*a, **kw).then_inc(s, 1))
        for b in reads:
            b.r.append((s, c))
        for b in writes:
            b.w = (s, c)
            b.r = []

    def dma(self, q, out, in_, reads=(), writes=(), **kw):
        tb = writes[0] if writes else reads[0]
        g = tb.grp
        if g is None:
            g = self.GP if q == "pool" else (self.GC if writes else self.GS[0])
        deps = []
        for b in reads:
            if b.w is not None:
                deps.append(b.w)
        for b in writes:
            if b.w is not None and b.w is not g:
                deps.append(b.w)
            deps.extend(b.r)
        self._waits(q, deps)
        if g.sealed and g.cnt > 0:
            self._waits(q, [(g.sem, g.cnt)])
        g.sealed = False
        g.cnt += 16
        s = g.sem
        self.prog[q].append(
            lambda e, out=out, in_=in_, s=s, kw=kw: e.dma_start(out=out, in_=in_, **kw).then_inc(s, 16))
        for b in reads:
            b.r.append(g)
        for b in writes:
            b.w = g
            b.r = []

    def barrier(self, engines=None):
        deps = [(self.sem[k], self.cnt[k]) for k in ("pe", "dve", "act", "pool") if self.cnt[k] > 0]
        deps += [g for g in self.groups if g.cnt > 0]
        for e in (engines or self.ENG):
            self._waits(e, deps)

    def run_block(self):
        nc = self.nc
        with nc.Block() as block:
            @block.sync
            def _(e):
                for t in self.prog["sp"]:
                    t(e)

            @block.tensor
            def _(e):
                for t in self.prog["pe"]:
                    t(e)

            @block.vector
            def _(e):
                for t in self.prog["dve"]:
                    t(e)

            @block.scalar
            def _(e):
                for t in self.prog["act"]:
                    t(e)

            @block.gpsimd
            def _(e):
                for t in self.prog["pool"]:
                    t(e)


_CONSTS = None


def _consts():
    global _CONSTS
    if _CONSTS is not None:
        return _CONSTS
    f = np.float32
    c = {}
    c["c_ident"] = np.eye(128, dtype=f)
    m = np.zeros((8, 128, 240), f)
    for a in range(8):
        for i in range(16):
            m[a, 16 * a + i, 112 + i] = 1.0
    c["c_masters"] = m
    ml = np.array(ML, np.float64)
    rows = np.concatenate([ml / (2 * np.pi), ml, 8.0 * (np.arange(64) + 1) / (2 * np.pi)])
    c["c_rows"] = rows.astype(f)[None, :]
    sg = np.zeros((128, 2), f)
    sg[:64, 0] = 1.0
    sg[64:, 0] = -1.0
    sg[:64, 1] = -1.0
    sg[64:, 1] = 1.0
    c["c_sgn"] = sg
    inv = (f(10000.0) ** (-(np.arange(64, dtype=f) / f(64.0)))).astype(f)
    pos = np.zeros((128, NT), f)
    for n in range(NTP):
        pos[:, n] = 128 * n + np.arange(128)
    pos[:64, 16] = PAST + (np.arange(64) % 4)
    ang = (pos[:, :, None] * inv[None, None, :]).astype(f).astype(np.float64)
    c["c_rope"] = np.stack([np.cos(ang), np.sin(ang), -np.sin(ang)]).astype(f)
    lg = np.log(np.array(GAM, np.float64))
    sc = 128.0 ** -0.5
    idx = np.arange(128)
    dm = np.zeros((128, 4, 128), np.float64)
    diff = idx[None, :] - idx[:, None]
    for h in range(4):
        dm[:, h, :] = np.where(diff >= 0, np.exp(np.maximum(diff, 0) * lg[h]), 0.0) * sc
    c["c_dmask_p"] = dm.reshape(128, 512).astype(f)
    ds_ = np.zeros((64, 4, 64), np.float64)
    r = np.arange(64)
    bb = r // 4
    tt = r % 4
    same = bb[:, None] == bb[None, :]
    dts = tt[None, :] - tt[:, None]
    for h in range(4):
        ds_[:, h, :] = np.where(same & (dts >= 0), np.exp(np.maximum(dts, 0) * lg[h]), 0.0) * sc
    c["c_dmask_s"] = ds_.reshape(64, 256).astype(f)
    xi_p = np.stack([np.exp((idx + 1.0) * lg[h]) * sc for h in range(4)])
    xi_s = np.stack([np.exp((tt + 1.0) * lg[h]) * sc for h in range(4)])
    c["c_xi"] = np.concatenate([xi_p.reshape(-1), xi_s.reshape(-1)]).astype(f)[None, :]
    zp = np.stack([np.exp((127.0 - idx) * lg[h]) for h in range(4)], axis=1)
    c["c_zeta_p"] = zp.astype(f)
    zs = np.zeros((64, 16, 4), np.float64)
    for h in range(4):
        for b in range(16):
            zs[:, b, h] = np.where(bb == b, np.exp((3.0 - tt) * lg[h]), 0.0)
    c["c_zs"] = zs.reshape(64, 64).astype(f)
    cm = np.zeros((16, 64), f)
    for b in range(16):
        cm[b, 4 * b:4 * b + 4] = 1.0
    c["c_cmask"] = cm.reshape(1, -1)
    _CONSTS = c
    return c


W_NAMES = ["g_mix", "w_in", "lam_re", "lam_im", "log_dt", "b_re", "b_im", "c_re", "c_im", "d_skip", "w_glu",
           "ret_gn", "w_out", "g_xattn", "g_mem", "w_mq", "w_mk", "w_mv", "w_mo", "g_mlp", "w_up", "w_down",
           "g_final"]
W_SHAPES = {"g_mix": [D], "w_in": [D, 2560], "lam_re": [G, 64], "lam_im": [G, 64], "log_dt": [G],
            "b_re": [G, 64, 16], "b_im": [G, 64, 16], "c_re": [G * 16, 64], "c_im": [G * 16, 64], "d_skip": [512],
            "w_glu": [512, 512], "ret_gn": [512], "w_out": [D, D], "g_xattn": [D], "g_mem": [D], "w_mq": [D, D],
            "w_mk": [D, D], "w_mv": [D, D], "w_mo": [D, D], "g_mlp": [D], "w_up": [D, DFF], "w_down": [DFF, D],
            "g_final": [D]}
IN_SHAPES = {"xp": [SEQ, D], "xs": [TS, D], "memp": [MEM, D], "s5r": [512, 64], "s5i": [512, 64],
             "sret": [16, 4, 128, 128], "ck": [16, MEM, D], "cv": [16, MEM, D]}
OUT_SHAPES = {"yp": [SEQ, D], "ys": [TS, D], "o_s5r_p": [G, 64], "o_s5i_p": [G, 64], "o_ret_p": [4, 128, 128],
              "o_mk": [MEM, D], "o_mv": [MEM, D], "o_s5r_s": [512, 64], "o_s5i_s": [512, 64],
              "o_ret_s": [16, 4, 128, 128]}


def build(stage=99, dbg=False):
    nc = bass.Bass("TRN2", target_bir_lowering=False)
    cst = _consts()
    I = {}
    for k, shp in list(IN_SHAPES.items()) + list(W_SHAPES.items()):
        I[k] = nc.dram_tensor(k, shp, F32, kind="ExternalInput").ap()
    for k, v in cst.items():
        I[k] = nc.dram_tensor(k, list(v.shape), F32, kind="ExternalInput").ap()
    O = {}
    for k, shp in OUT_SHAPES.items():
        O[k] = nc.dram_tensor(k, shp, F32, kind="ExternalOutput").ap()
    if dbg:
        O["dbg_ssm"] = nc.dram_tensor("dbg_ssm", [128, 4, NTOK], F32, kind="ExternalOutput").ap()
        O["dbg_x"] = nc.dram_tensor("dbg_x", [128, NT, D], F32, kind="ExternalOutput").ap()

    with ExitStack() as st:
        S = Sched(nc, st)

        def alloc(stack, name, shape, dt=F32):
            return stack.enter_context(nc.sbuf_tensor(name, shape, dt))

        def palloc(stack, name, shape, dt=F32):
            return stack.enter_context(nc.psum_tensor(name, shape, dt))

        def V(fn, r=(), w=()):
            S.op("dve", fn, reads=r, writes=w)

        def A(fn, r=(), w=()):
            S.op("act", fn, reads=r, writes=w)

        import os as _os0
        _nopool = _os0.environ.get("K_NOPOOL") == "1"

        def PL(fn, r=(), w=()):
            S.op("dve" if _nopool else "pool", fn, reads=r, writes=w)

        def T(fn, r=(), w=()):
            S.op("pe", fn, reads=r, writes=w)

        nck = nc.allow_non_contiguous_dma(reason="small param layout loads")
        nck.__enter__()

        identb = alloc(st, "identb", [128, 128], BF16)
        identf = alloc(st, "identf", [128, 128], F32)
        sgn = alloc(st, "sgn", [128, 2])
        epsc = alloc(st, "epsc", [128, 1])
        ssmT = alloc(st, "ssmT", [128, 4, NTOK], BF16)
        b_const = Buf("const")
        b_ssmT = [Buf("ssmT%d" % i) for i in range(5)]
        b_constp = Buf("constp")
        S.dma("pool", identb[:], I["c_ident"][:, :], writes=[b_constp])
        S.dma("sp", identf[:], I["c_ident"][:, :], writes=[b_const])
        S.dma("sp", sgn[:], I["c_sgn"][:, :], writes=[b_const])
        V(lambda e: e.memset(epsc[:], EPS), r=[b_constp], w=[b_const])
        PS = [palloc(st, "ps%d" % i, [128, 512], F32) for i in range(8)]
        bPS = [Buf("ps%d" % i, ps=True) for i in range(8)]

        def ps_bf(i):
            return PS[i][:].bitcast(BF16)

        def make_scr(stack, tag, pbanks):
            d = {"i": 0, "pb": list(pbanks)}
            d["sq"] = [alloc(stack, "sq%s%d" % (tag, i), [128, D], BF16) for i in range(2)]
            d["ss"] = [alloc(stack, "ss%s%d" % (tag, i), [128, 1]) for i in range(2)]
            d["rstd"] = [alloc(stack, "rstd%s%d" % (tag, i), [128, 1]) for i in range(2)]
            d["hb"] = [alloc(stack, "hb%s%d" % (tag, i), [128, D], BF16) for i in range(2)]
            d["ba"] = [Buf("ba%s%d" % (tag, i)) for i in range(2)]
            d["bh"] = [Buf("bh%s%d" % (tag, i)) for i in range(2)]
            return d

        def rmsnorm_hT(xt_ap, bx, npart, gcol, hT_ap, bhT, scr, col0, ph, ln=False, bg=None, out4=None):
            k = scr["i"] % 2
            pbank = scr["pb"][scr["i"] % len(scr["pb"])]
            scr["i"] += 1
            sq, ss, rstd, hb = scr["sq"][k], scr["ss"][k], scr["rstd"][k], scr["hb"][k]
            ba, bh = scr["ba"][k], scr["bh"][k]
            A(lambda e: e.activation(out=sq[:npart, :], in_=xt_ap, func=AF.Square, accum_out=ss[:npart, :]),
              r=[bx], w=[ba])
            if ln:
                A(lambda e: e.activation(out=rstd[:npart, :], in_=ss[:npart, :], func=AF.Ln, scale=1.0 / D,
                                         bias=epsc[:npart, :]), r=[ba, b_const], w=[ba])
                A(lambda e: e.activation(out=rstd[:npart, :], in_=rstd[:npart, :], func=AF.Exp, scale=-0.5),
                  r=[ba], w=[ba])
            else:
                A(lambda e: e.activation(out=rstd[:npart, :], in_=ss[:npart, :], func=AF.Sqrt, scale=1.0 / D,
                                         bias=epsc[:npart, :]), r=[ba, b_const], w=[ba])
                V(lambda e: e.reciprocal(out=rstd[:npart, :], in_=rstd[:npart, :]), r=[ba], w=[ba])
            V(lambda e: e.tensor_scalar(out=hb[:npart, :], in0=xt_ap, scalar1=rstd[:npart, :], scalar2=None,
                                        op0=ALU.mult), r=[bx, ba], w=[bh])
            pv = ps_bf(pbank)
            for kt in range(8):
                T(lambda e, kt=kt: e.transpose(out=pv[:, kt * 128:kt * 128 + npart],
                                               in_=hb[:npart, kt * 128:(kt + 1) * 128],
                                               identity=identb[:npart, :npart]),
                  r=[bh, b_const], w=[bPS[pbank]])
            if out4 is not None:
                V(lambda e: e.tensor_tensor(
                    out=out4, in0=pv.rearrange("p (k c s) -> p k c s", k=8, s=8),
                    in1=gcol.unsqueeze(2).unsqueeze(3).to_broadcast([128, 8, 16, 8]), op=ALU.mult),
                  r=[bPS[pbank], b_const] + ([bg] if bg is not None else []), w=[bhT])
                return
            V(lambda e: e.tensor_tensor(
                out=hT_ap[:, :, col0:col0 + npart],
                in0=pv.rearrange("p (k t) -> p k t", k=8)[:, :, 0:npart],
                in1=gcol.unsqueeze(2).to_broadcast([128, 8, npart]), op=ALU.mult),
              r=[bPS[pbank], b_const] + ([bg] if bg is not None else []), w=[bhT])

        def load_w_bf16(dst, bdst, src, kt_n, ncols, c0=0):
            for kt in range(kt_n):
                for cc in range(0, ncols, 1024):
                    w_ = min(1024, ncols - cc)
                    S.dma("pool", dst[:, kt, cc:cc + w_], src[kt * 128:(kt + 1) * 128, c0 + cc:c0 + cc + w_],
                          writes=[bdst])

        with ExitStack() as sa:
            Wt = alloc(sa, "Wt", [128, G, 128], BF16)
            Wst = alloc(sa, "Wst", [128, G, 128], BF16)
            Tt = alloc(sa, "Tt", [128, G, 128], BF16)
            Vt = alloc(sa, "Vt", [128, G, 128], BF16)
            COSR = alloc(sa, "COSR", [128, G, 64])
            SINR = alloc(sa, "SINR", [128, G, 64])
            masters = alloc(sa, "masters", [128, 8, 240], BF16)
            AR = alloc(sa, "AR", [128, G, K1])
            AI = alloc(sa, "AI", [128, G, K1])
            MAGJ = alloc(sa, "MAGJ", [128, G, K1])
            DS = alloc(sa, "DS", [128, G])
            gm = alloc(sa, "gm", [128, 8])
            winu = alloc(sa, "winu", [128, 8, 512], BF16)
            wglu = alloc(sa, "wglu", [128, 4, 512], BF16)
            b_tab = Buf("s5tab")
            b_winu = Buf("winu", S.GW[0])
            b_wglu = Buf("wglu", S.GW[1])
            b_tabp = Buf("s5tabp")
            S.dma("pool", masters[:], I["c_masters"].rearrange("a k j -> k a j"), writes=[b_tabp])
            S.dma("sp", gm[:], I["g_mix"].rearrange("(k p) -> p k", p=128), writes=[b_tab])
            for tau in range(8):
                S.dma("sp", DS[16 * tau:16 * tau + 16, :], I["d_skip"].rearrange("(g h) -> h g", h=16),
                      writes=[b_tab])
            load_w_bf16(winu, b_winu, I["w_in"], 8, 512, 0)
            load_w_bf16(wglu, b_wglu, I["w_glu"], 4, 512, 0)

            with ExitStack() as s0:
                rows = alloc(s0, "rows", [128, 2 * K1 + 64])
                LR = alloc(s0, "LR", [128, G])
                LI = alloc(s0, "LI", [128, G])
                DT = alloc(s0, "DT", [128, G])
                LRDT = alloc(s0, "LRDT", [128, G])
                LIDT = alloc(s0, "LIDT", [128, G])
                tA = alloc(s0, "tA", [128, G, 64])
                tB = alloc(s0, "tB", [128, G, 64])
                tC = alloc(s0, "tC", [128, G, 64])
                COSJ = alloc(s0, "COSJ", [128, G, K1])
                SINJ = alloc(s0, "SINJ", [128, G, K1])
                sm = alloc(s0, "sm", [128, 12, G])
                Br1 = alloc(s0, "Br1", [128, G, 16])
                Br2 = alloc(s0, "Br2", [128, G, 16])
                BB1 = alloc(s0, "BB1", [128, G, 16])
                BB2 = alloc(s0, "BB2", [128, G, 16])
                tb1 = alloc(s0, "tb1", [128, G, 16])
                big1 = alloc(s0, "big1", [128, G, 128])
                big2 = alloc(s0, "big2", [128, G, 128])
                WTpad = alloc(s0, "WTpad", [128, G, 256], BF16)
                WTs = alloc(s0, "WTs", [128, G, 128], BF16)
                CN1 = alloc(s0, "CN1", [128, 4, 128])
                CN2 = alloc(s0, "CN2", [128, 4, 128])
                CMa = alloc(s0, "CMa", [128, G, 16])
                CMb = alloc(s0, "CMb", [128, G, 16])
                CMab = alloc(s0, "CMab", [128, G, 16], BF16)
                b0 = Buf("p0in")
                bt = Buf("p0tmp")
                S.dma("sp", rows[:], I["c_rows"][0:1, :].partition_broadcast(128), writes=[b0])
                for hf in range(2):
                    S.dma("sp", LR[64 * hf:64 * hf + 64, :], I["lam_re"].rearrange("g p -> p g"), writes=[b0])
                    S.dma("sp", LI[64 * hf:64 * hf + 64, :], I["lam_im"].rearrange("g p -> p g"), writes=[b0])
                S.dma("sp", DT[:], I["log_dt"].rearrange("(o g) -> o g", o=1).partition_broadcast(128), writes=[b0])
                S.dma("sp", Br1[0:64], I["b_re"].rearrange("g p h -> p g h"), writes=[b0])
                S.dma("sp", Br1[64:128], I["b_im"].rearrange("g p h -> p g h"), writes=[b0])
                S.dma("sp", Br2[0:64], I["b_im"].rearrange("g p h -> p g h"), writes=[b0])
                S.dma("sp", Br2[64:128], I["b_re"].rearrange("g p h -> p g h"), writes=[b0])
                S.dma("sp", CN1[:, :, 0:64], I["c_re"].rearrange("(c r) p -> r c p", r=128), writes=[b0])
                S.dma("sp", CN1[:, :, 64:128], I["c_im"].rearrange("(c r) p -> r c p", r=128), writes=[b0])
                S.dma("sp", CN2[:, :, 0:64], I["c_im"].rearrange("(c r) p -> r c p", r=128), writes=[b0])
                S.dma("sp", CN2[:, :, 64:128], I["c_re"].rearrange("(c r) p -> r c p", r=128), writes=[b0])
                MT1 = rows[:, 0:K1]
                MLr = rows[:, K1:2 * K1]
                MRT = rows[:, 2 * K1:2 * K1 + 64]
                A(lambda e: e.activation(out=DT[:], in_=DT[:], func=AF.Exp), r=[b0], w=[b0])
                V(lambda e: e.tensor_tensor(out=LRDT[:], in0=LR[:], in1=DT[:], op=ALU.mult), r=[b0], w=[bt])
                V(lambda e: e.tensor_tensor(out=LIDT[:], in0=LI[:], in1=DT[:], op=ALU.mult), r=[b0], w=[bt])

                def trig(mt_ap, K, cos_out, sin_out):
                    shp = [128, G, K]
                    a_, b_, c_ = tA[:, :, 0:K], tB[:, :, 0:K], tC[:, :, 0:K]
                    V(lambda e: e.tensor_tensor(out=a_, in0=LIDT[:].unsqueeze(2).to_broadcast(shp),
                                                in1=mt_ap.unsqueeze(1).to_broadcast(shp), op=ALU.mult),
                      r=[bt, b0], w=[bt])
                    for (outp, off) in ((sin_out, 0.0), (cos_out, 0.25)):
                        if outp is None:
                            continue
                        V(lambda e, off=off: e.tensor_scalar(out=c_, in0=a_, scalar1=off, scalar2=None,
                                                             op0=ALU.add), r=[bt], w=[bt])
                        V(lambda e: e.tensor_scalar(out=b_, in0=c_, scalar1=MAGIC, scalar2=None, op0=ALU.add),
                          r=[bt], w=[bt])
                        V(lambda e: e.tensor_scalar(out=b_, in0=b_, scalar1=MAGIC, scalar2=None, op0=ALU.subtract),
                          r=[bt], w=[bt])
                        V(lambda e: e.tensor_tensor(out=c_, in0=c_, in1=b_, op=ALU.subtract), r=[bt], w=[bt])
                        A(lambda e, outp=outp: e.activation(out=outp, in_=c_, func=AF.Sin, scale=TWO_PI),
                          r=[bt], w=[b_tab])

                trig(MT1, K1, COSJ[:], SINJ[:])
                trig(MRT, 64, COSR[:], SINR[:])
                shpj = [128, G, K1]
                V(lambda e: e.tensor_tensor(out=MAGJ[:], in0=LRDT[:].unsqueeze(2).to_broadcast(shpj),
                                            in1=MLr.unsqueeze(1).to_broadcast(shpj), op=ALU.mult),
                  r=[bt, b0], w=[b_tab])
                A(lambda e: e.activation(out=MAGJ[:], in_=MAGJ[:], func=AF.Exp), r=[b_tab], w=[b_tab])
                V(lambda e: e.tensor_tensor(out=AR[:], in0=MAGJ[:], in1=COSJ[:], op=ALU.mult), r=[b_tab], w=[b_tab])
                V(lambda e: e.tensor_tensor(out=AI[:], in0=MAGJ[:], in1=SINJ[:], op=ALU.mult), r=[b_tab], w=[b_tab])
                em1, shalf, cm1, am1r, ai1, den, fr, fi, t0_, t1_ = [sm[:, i, :] for i in range(10)]
                x_ = LRDT[:]
                V(lambda e: e.tensor_scalar(out=em1, in0=x_, scalar1=0.2, scalar2=1.0, op0=ALU.mult, op1=ALU.add),
                  r=[bt], w=[bt])
                for cf in (0.25, 1.0 / 3.0, 0.5):
                    V(lambda e: e.tensor_tensor(out=em1, in0=em1, in1=x_, op=ALU.mult), r=[bt], w=[bt])
                    V(lambda e, cf=cf: e.tensor_scalar(out=em1, in0=em1, scalar1=cf, scalar2=1.0, op0=ALU.mult,
                                                       op1=ALU.add), r=[bt], w=[bt])
                V(lambda e: e.tensor_tensor(out=em1, in0=em1, in1=x_, op=ALU.mult), r=[bt], w=[bt])
                V(lambda e: e.tensor_copy(out=shalf, in_=SINJ[:, :, I_HALF]), r=[b_tab], w=[bt])
                V(lambda e: e.scalar_tensor_tensor(out=cm1, in0=shalf, scalar=-2.0, op0=ALU.mult, in1=shalf,
                                                   op1=ALU.mult), r=[bt], w=[bt])
                V(lambda e: e.tensor_tensor(out=am1r, in0=em1, in1=COSJ[:, :, I_A1], op=ALU.mult), r=[bt, b_tab], w=[bt])
                V(lambda e: e.tensor_tensor(out=am1r, in0=am1r, in1=cm1, op=ALU.add), r=[bt], w=[bt])
                V(lambda e: e.tensor_copy(out=ai1, in_=AI[:, :, I_A1]), r=[b_tab], w=[bt])
                V(lambda e: e.tensor_tensor(out=den, in0=LR[:], in1=LR[:], op=ALU.mult), r=[b0], w=[bt])
                V(lambda e: e.tensor_tensor(out=t0_, in0=LI[:], in1=LI[:], op=ALU.mult), r=[b0], w=[bt])
                V(lambda e: e.tensor_tensor(out=den, in0=den, in1=t0_, op=ALU.add), r=[bt], w=[bt])
                V(lambda e: e.reciprocal(out=den, in_=den), r=[bt], w=[bt])
                V(lambda e: e.tensor_tensor(out=fr, in0=am1r, in1=LR[:], op=ALU.mult), r=[bt, b0], w=[bt])
                V(lambda e: e.tensor_tensor(out=t0_, in0=ai1, in1=LI[:], op=ALU.mult), r=[bt, b0], w=[bt])
                V(lambda e: e.tensor_tensor(out=fr, in0=fr, in1=t0_, op=ALU.add), r=[bt], w=[bt])
                V(lambda e: e.tensor_tensor(out=fr, in0=fr, in1=den, op=ALU.mult), r=[bt], w=[bt])
                V(lambda e: e.tensor_tensor(out=fi, in0=ai1, in1=LR[:], op=ALU.mult), r=[bt, b0], w=[bt])
                V(lambda e: e.tensor_tensor(out=t0_, in0=am1r, in1=LI[:], op=ALU.mult), r=[bt, b0], w=[bt])
                V(lambda e: e.tensor_tensor(out=fi, in0=fi, in1=t0_, op=ALU.subtract), r=[bt], w=[bt])
                V(lambda e: e.tensor_tensor(out=fi, in0=fi, in1=den, op=ALU.mult), r=[bt], w=[bt])
                V(lambda e: e.tensor_scalar(out=Br2[:], in0=Br2[:], scalar1=sgn[:, 1:2], scalar2=None, op0=ALU.mult),
                  r=[b0, b_const], w=[b0])
                shb = [128, G, 16]
                frb = fr.unsqueeze(2).to_broadcast(shb)
                fib = fi.unsqueeze(2).to_broadcast(shb)
                V(lambda e: e.tensor_tensor(out=BB1[:], in0=Br1[:], in1=frb, op=ALU.mult), r=[b0, bt], w=[bt])
                V(lambda e: e.tensor_tensor(out=tb1[:], in0=Br2[:], in1=fib, op=ALU.mult), r=[b0, bt], w=[bt])
                V(lambda e: e.tensor_tensor(out=BB1[:], in0=BB1[:], in1=tb1[:], op=ALU.add), r=[bt], w=[bt])
                V(lambda e: e.tensor_tensor(out=BB2[:], in0=Br2[:], in1=frb, op=ALU.mult), r=[b0, bt], w=[bt])
                V(lambda e: e.tensor_tensor(out=tb1[:], in0=Br1[:], in1=fib, op=ALU.mult), r=[b0, bt], w=[bt])
                V(lambda e: e.tensor_tensor(out=BB2[:], in0=BB2[:], in1=tb1[:], op=ALU.subtract), r=[bt], w=[bt])
                sh4 = [128, G, 8, 16]
                arv = AR[:, :, 0:8].unsqueeze(3).to_broadcast(sh4)
                aiv = AI[:, :, 0:8].unsqueeze(3).to_broadcast(sh4)
                bb1 = BB1[:].unsqueeze(2).to_broadcast(sh4)
                bb2 = BB2[:].unsqueeze(2).to_broadcast(sh4)
                g1 = big1[:].rearrange("p g (s h) -> p g s h", s=8)
                g2 = big2[:].rearrange("p g (s h) -> p g s h", s=8)
                V(lambda e: e.memset(WTpad[:], 0.0), w=[bt])
                V(lambda e: e.tensor_tensor(out=g1, in0=arv, in1=bb1, op=ALU.mult), r=[b_tab, bt], w=[bt])
                V(lambda e: e.tensor_tensor(out=g2, in0=aiv, in1=bb2, op=ALU.mult), r=[b_tab, bt], w=[bt])
                V(lambda e: e.tensor_tensor(out=WTpad[:, :, 0:128], in0=big1[:], in1=big2[:], op=ALU.add),
                  r=[bt], w=[bt])
                V(lambda e: e.tensor_tensor(out=g1, in0=arv, in1=bb2, op=ALU.mult), r=[b_tab, bt], w=[bt])
                V(lambda e: e.tensor_tensor(out=g2, in0=aiv, in1=bb1, op=ALU.mult), r=[b_tab, bt], w=[bt])
                V(lambda e: e.tensor_tensor(out=WTs[:], in0=big1[:], in1=big2[:], op=ALU.subtract), r=[bt], w=[bt])
                for (src_fn, dstt) in ((lambda g: WTpad[:, g, 0:128], Wt), (lambda g: WTs[:, g, :], Wst)):
                    for gq in range(8):
                        bank = gq % 2
                        pv = ps_bf(bank)
                        for j in range(4):
                            g = gq * 4 + j
                            T(lambda e, g=g, j=j, pv=pv, src_fn=src_fn: e.transpose(
                                out=pv[:, j * 128:(j + 1) * 128], in_=src_fn(g), identity=identb[:]),
                              r=[bt, b_const], w=[bPS[bank]])
                        A(lambda e, gq=gq, pv=pv, dstt=dstt: e.copy(
                            out=dstt[:, gq * 4:gq * 4 + 4, :], in_=pv[:, 0:512].rearrange("p (j c) -> p j c", j=4)),
                          r=[bPS[bank]], w=[b_tab])
                for (CN, CM, col) in ((CN1, CMa, 0), (CN2, CMb, None)):
                    for c4 in range(4):
                        bank = 2 + (c4 % 2)
                        T(lambda e, CN=CN, c4=c4, bank=bank: e.transpose(out=PS[bank][:, 0:128], in_=CN[:, c4, :],
                                                                         identity=identf[:]),
                          r=[b0, b_const], w=[bPS[bank]])
                        if col is not None:
                            V(lambda e, CM=CM, c4=c4, bank=bank: e.tensor_scalar(
                                out=CM[:, c4 * 8:(c4 + 1) * 8, :],
                                in0=PS[bank][:, 0:128].rearrange("p (g h) -> p g h", g=8),
                                scalar1=sgn[:, 0:1], scalar2=None, op0=ALU.mult),
                              r=[bPS[bank], b_const], w=[bt])
                        else:
                            V(lambda e, CM=CM, c4=c4, bank=bank: e.tensor_scalar(
                                out=CM[:, c4 * 8:(c4 + 1) * 8, :],
                                in0=PS[bank][:, 0:128].rearrange("p (g h) -> p g h", g=8),
                                scalar1=-1.0, scalar2=None, op0=ALU.mult),
                              r=[bPS[bank]], w=[bt])
                V(lambda e: e.tensor_copy(out=CMab[:], in_=CMa[:]), r=[bt], w=[bt])
                afw = AR[:, :, 8:16].unsqueeze(3).to_broadcast(sh4)
                aifw = AI[:, :, 8:16].unsqueeze(3).to_broadcast(sh4)
                cma = CMa[:].unsqueeze(2).to_broadcast(sh4)
                cmb = CMb[:].unsqueeze(2).to_broadcast(sh4)
                V(lambda e: e.tensor_tensor(out=g1, in0=afw, in1=cma, op=ALU.mult), r=[b_tab, bt], w=[bt])
                V(lambda e: e.tensor_tensor(out=g2, in0=aifw, in1=cmb, op=ALU.mult), r=[b_tab, bt], w=[bt])
                V(lambda e: e.tensor_tensor(out=Vt[:], in0=big1[:], in1=big2[:], op=ALU.add), r=[bt], w=[b_tab])
                for gq in range(8):
                    bank = 4 + (gq % 2)
                    for j in range(4):
                        g = gq * 4 + j
                        for tau in range(8):
                            c0 = (7 - tau) * 16
                            T(lambda e, g=g, j=j, tau=tau, c0=c0, bank=bank: e.matmul(
                                PS[bank][:, j * 128 + tau * 16:j * 128 + tau * 16 + 16],
                                lhsT=WTpad[:, g, c0:c0 + 128], rhs=CMab[:, g, :], start=True, stop=True),
                              r=[bt], w=[bPS[bank]])
                    A(lambda e, gq=gq, bank=bank: e.copy(
                        out=Tt[:, gq * 4:gq * 4 + 4, :], in_=PS[bank][:].rearrange("p (j c) -> p j c", j=4)),
                      r=[bPS[bank]], w=[b_tab])
                S.barrier()
            xst = [alloc(sa, "xst%d" % i, [128, D]) for i in range(2)]
            bxst = [Buf("xst%d" % i, S.GL[i]) for i in range(2)]
            scrA = make_scr(sa, "A", [7])
            bscr = Buf("scrA")
            hT2 = [alloc(sa, "hT_%d" % i, [128, 8, 512], BF16) for i in range(2)]
            bhT2 = [Buf("hT_%d" % i) for i in range(2)]
            uT2 = [alloc(sa, "uT_%d" % i, [128, 4, 512], BF16) for i in range(2)]
            buT2 = [Buf("uT_%d" % i) for i in range(2)]
            U = alloc(sa, "U", [128, G, 64], BF16)
            bU = Buf("U")
            rr = alloc(sa, "rr", [128, G, 64])
            rs = alloc(sa, "rs", [128, G, 64])
            ww = alloc(sa, "ww", [128, G, 64])
            ws = alloc(sa, "ws", [128, G, 64])
            tmpr = alloc(sa, "tmpr", [128, 16, 64])
            b_r, b_rs, b_w, b_ws, b_tmpr = Buf("r"), Buf("rs"), Buf("w"), Buf("ws"), Buf("tmpr")
            Xb = alloc(sa, "Xb", [128, G, 65], BF16)
            bXb = Buf("Xb")
            Xc = alloc(sa, "Xc", [128, G])
            Xsc = alloc(sa, "Xsc", [128, G])
            ctmp = alloc(sa, "ctmp", [128, 2, G])
            bXc = Buf("Xc", S.GS[0])
            ytmp = alloc(sa, "ytmp", [128, 8, 64])
            bytmp = Buf("ytmp")
            Zt = alloc(sa, "Zt", [128, G, 64], BF16)
            bZ = Buf("Z")
            zT = alloc(sa, "zT", [128, 4, 512], BF16)
            bzT = Buf("zT")
            sig = alloc(sa, "sig", [128, 4, 512])
            bsig = Buf("sig")
            H0 = alloc(sa, "H0", [128, 512])
            H0s = alloc(sa, "H0s", [128, 512])
            hn = alloc(sa, "hn", [128, 4, 128])
            hn2 = alloc(sa, "hn2", [128, 4, 128])
            Hp = alloc(sa, "Hp", [128, G, 16])
            Xf = alloc(sa, "Xf", [128, G, 16])
            xo = alloc(sa, "xo", [128, 4, 128])
            bH = Buf("H0")
            bxo = Buf("xo", S.GS[1])
            V(lambda e: e.memset(Xc[:], 0.0), r=[b_tabp], w=[bXc, b_tab])
            V(lambda e: e.memset(Xsc[:], 0.0), w=[bXc])
            V(lambda e: e.memset(Xb[:], 0.0), w=[bXb])

            blocks = [(i * 512, 512, False) for i in range(4)] + [(SEQ, TS, True)]
            if _os0.environ.get("K1A") == "0":
                blocks = []
            def p1a_stageA(bi):
                t0, n, is_s = blocks[bi]
                hT, bhT = hT2[bi % 2], bhT2[bi % 2]
                uT, buT = uT2[bi % 2], buT2[bi % 2]
                ntile = (n + 127) // 128
                for ti in range(ntile):
                    npart = min(128, n - ti * 128)
                    slot = (bi * 4 + ti) % 2
                    src = I["xs"][:, :] if is_s else I["xp"][t0 + ti * 128:t0 + ti * 128 + 128, :]
                    S.dma("sp", xst[slot][:npart, :], src, writes=[bxst[slot]])
                    o4 = None if is_s else hT[:, :, :].rearrange("p k (s c) -> p k c s", s=8)[:, :, ti * 16:(ti + 1) * 16, :]
                    rmsnorm_hT(xst[slot][:npart, :], bxst[slot], npart, gm[:], hT, bhT,
                               scrA, ti * 128, None, bg=b_tab, out4=o4)
                for ct in range(4):
                    bank = ct
                    for kt in range(8):
                        T(lambda e, ct=ct, kt=kt, bank=bank: e.matmul(
                            PS[bank][:, 0:n], lhsT=winu[:, kt, ct * 128:(ct + 1) * 128], rhs=hT[:, kt, 0:n],
                            start=(kt == 0), stop=(kt == 7)), r=[b_winu, bhT], w=[bPS[bank]])
                    A(lambda e, ct=ct, bank=bank: e.copy(out=uT[:, ct, 0:n], in_=PS[bank][:, 0:n]),
                      r=[bPS[bank]], w=[buT])

            if blocks:
                p1a_stageA(0)
            for bi, (t0, n, is_s) in enumerate(blocks):
                nch = n // 8 if not is_s else 16
                uT, buT = uT2[bi % 2], buT2[bi % 2]
                for gq in range(4):
                    bank = 4 + (gq % 2)
                    for j in range(8):
                        g = gq * 8 + j
                        ct, gl = g // 8, g % 8
                        if not is_s:
                            uv = uT[:, ct, 0:n].rearrange("p (s c) -> p s c", s=8)
                            sig_list = list(range(8))
                        else:
                            uv = uT[:, ct, 0:n].rearrange("p (b t) -> p t b", t=4)
                            sig_list = [4, 5, 6, 7]
                        for si, sg_ in enumerate(sig_list):
                            rhs = uv[:, sg_ if not is_s else si, :]
                            T(lambda e, j=j, gl=gl, sg_=sg_, rhs=rhs, si=si, bank=bank, L=len(sig_list): e.matmul(
                                PS[bank][:, j * 64:j * 64 + nch],
                                lhsT=masters[:, gl, 112 - 16 * sg_:240 - 16 * sg_], rhs=rhs,
                                start=(si == 0), stop=(si == L - 1)),
                              r=[b_tab, buT], w=[bPS[bank]])
                    A(lambda e, gq=gq, bank=bank: e.copy(
                        out=U[:, gq * 8:gq * 8 + 8, 0:nch],
                        in_=PS[bank][:].rearrange("p (j c) -> p j c", j=8)[:, :, 0:nch]),
                      r=[bPS[bank]], w=[bU])
                if not is_s:
                    for hf in range(2):
                        for j in range(16):
                            g = hf * 16 + j
                            for (wt, bk) in ((Wt, 0), (Wst, 2)):
                                bank = bk + j // 8
                                T(lambda e, g=g, j=j, wt=wt, bank=bank: e.matmul(
                                    PS[bank][:, (j % 8) * 64:(j % 8) * 64 + 64], lhsT=wt[:, g, :], rhs=U[:, g, :],
                                    start=True, stop=True), r=[b_tab, bU], w=[bPS[bank]])
                        for q in range(2):
                            gs = slice(hf * 16 + q * 8, hf * 16 + q * 8 + 8)
                            Sv = PS[q][:].rearrange("p (j c) -> p j c", j=8)
                            Ssv = PS[2 + q][:].rearrange("p (j c) -> p j c", j=8)
                            tm = tmpr[:, q * 8:q * 8 + 8, :]
                            V(lambda e, gs=gs, Sv=Sv: e.tensor_tensor(out=rr[:, gs, :], in0=Sv, in1=COSR[:, gs, :],
                                                                     op=ALU.mult), r=[bPS[q], b_tab], w=[b_r])
                            V(lambda e, gs=gs, Ssv=Ssv, tm=tm: e.tensor_tensor(out=tm, in0=Ssv, in1=SINR[:, gs, :],
                                                                              op=ALU.mult),
                              r=[bPS[2 + q], b_tab], w=[b_tmpr])
                            V(lambda e, gs=gs, tm=tm: e.tensor_tensor(out=rr[:, gs, :], in0=rr[:, gs, :], in1=tm,
                                                                     op=ALU.subtract), r=[b_r, b_tmpr], w=[b_r])
                            V(lambda e, gs=gs, Ssv=Ssv: e.tensor_tensor(out=rs[:, gs, :], in0=Ssv, in1=COSR[:, gs, :],
                                                                       op=ALU.mult), r=[bPS[2 + q], b_tab], w=[b_rs])
                            V(lambda e, gs=gs, Sv=Sv, tm=tm: e.tensor_tensor(out=tm, in0=Sv, in1=SINR[:, gs, :],
                                                                            op=ALU.mult),
                              r=[bPS[q], b_tab], w=[b_tmpr])
                            V(lambda e, gs=gs, tm=tm: e.tensor_tensor(out=rs[:, gs, :], in0=rs[:, gs, :], in1=tm,
                                                                     op=ALU.add), r=[b_rs, b_tmpr], w=[b_rs])
                    for g in range(G):
                        rho = MAGJ[:, g, I_A8:I_A8 + 1].to_broadcast([128, 64])
                        V(lambda e, g=g, rho=rho: e.tensor_tensor_scan(
                            out=ww[:, g, :], data0=rho, data1=rr[:, g, :], initial=Xc[:, g:g + 1], op0=ALU.mult,
                            op1=ALU.add), r=[b_r, b_tab, bXc], w=[b_w])
                        V(lambda e, g=g, rho=rho: e.tensor_tensor_scan(
                            out=ws[:, g, :], data0=rho, data1=rs[:, g, :], initial=Xsc[:, g:g + 1], op0=ALU.mult,
                            op1=ALU.add), r=[b_rs, b_tab, bXc], w=[b_ws])
                    if bi + 1 < len(blocks):
                        p1a_stageA(bi + 1)
                    ce, se_ = COSR[:, :, 63], SINR[:, :, 63]
                    we, wse = ww[:, :, 63], ws[:, :, 63]
                    V(lambda e: e.tensor_tensor(out=ctmp[:, 0, :], in0=ce, in1=we, op=ALU.mult), r=[b_w, b_tab], w=[bscr])
                    V(lambda e: e.tensor_tensor(out=ctmp[:, 1, :], in0=se_, in1=wse, op=ALU.mult), r=[b_ws, b_tab], w=[bscr])
                    V(lambda e: e.tensor_tensor(out=Xc[:], in0=ctmp[:, 0, :], in1=ctmp[:, 1, :], op=ALU.add),
                      r=[bscr], w=[bXc])
                    V(lambda e: e.tensor_tensor(out=ctmp[:, 0, :], in0=ce, in1=wse, op=ALU.mult), r=[b_ws, b_tab], w=[bscr])
                    V(lambda e: e.tensor_tensor(out=ctmp[:, 1, :], in0=se_, in1=we, op=ALU.mult), r=[b_w, b_tab], w=[bscr])
                    V(lambda e: e.tensor_tensor(out=Xsc[:], in0=ctmp[:, 0, :], in1=ctmp[:, 1, :], op=ALU.subtract),
                      r=[bscr], w=[bXc])
                    if bi > 0:
                        V(lambda e: e.tensor_copy(out=Xb[:, :, 0], in_=Xb[:, :, 64]), r=[bXb], w=[bXb])
                    V(lambda e: e.tensor_tensor(out=ww[:], in0=ww[:], in1=COSR[:], op=ALU.mult), r=[b_w, b_tab, bXc],
                      w=[b_w])
                    PL(lambda e: e.tensor_tensor(out=ws[:], in0=ws[:], in1=SINR[:], op=ALU.mult), r=[b_ws, b_tab, bXc],
                       w=[b_ws])
                    V(lambda e: e.tensor_tensor(out=Xb[:, :, 1:65], in0=ww[:], in1=ws[:], op=ALU.add),
                      r=[b_w, b_ws], w=[bXb])
                    xprev = lambda g: Xb[:, g, 0:64]
                    bXprev = bXb
                    if bi == 3:
                        S.dma("sp", O["o_s5r_p"].rearrange("g p -> p g"), Xc[0:64, :], reads=[bXc])
                        S.dma("sp", O["o_s5i_p"].rearrange("g p -> p g"), Xc[64:128, :], reads=[bXc])
                else:
                    S.dma("sp", hn[:, :, 0:64], I["s5r"].rearrange("(j r) p -> r j p", r=128), writes=[bH])
                    S.dma("sp", hn[:, :, 64:128], I["s5i"].rearrange("(j r) p -> r j p", r=128), writes=[bH])
                    S.dma("sp", hn2[:, :, 0:64], I["s5i"].rearrange("(j r) p -> r j p", r=128), writes=[bH])
                    S.dma("sp", hn2[:, :, 64:128], I["s5r"].rearrange("(j r) p -> r j p", r=128), writes=[bH])
                    for (src_, dst_, bank) in ((hn, H0, 0), (hn2, H0s, 1)):
                        for j in range(4):
                            T(lambda e, src_=src_, j=j, bank=bank: e.transpose(
                                out=PS[bank][:, j * 128:(j + 1) * 128], in_=src_[:, j, :], identity=identf[:]),
                              r=[bH, b_const], w=[bPS[bank]])
                        V(lambda e, dst_=dst_, bank=bank: e.tensor_copy(out=dst_[:], in_=PS[bank][:]),
                          r=[bPS[bank]], w=[bH])
                    V(lambda e: e.tensor_scalar(out=H0s[0:64, :], in0=H0s[0:64, :], scalar1=-1.0, scalar2=None,
                                                op0=ALU.mult), r=[bH], w=[bH])
                    shs = [128, G, 16]
                    h0v = H0[:].rearrange("p (b g) -> p g b", g=G)
                    h0sv = H0s[:].rearrange("p (b g) -> p g b", g=G)

                    def abc(tab, idx):
                        return tab[:, :, idx].unsqueeze(2).to_broadcast(shs)
                    V(lambda e: e.tensor_tensor(out=Xf[:], in0=h0v, in1=abc(AR, I_AM4), op=ALU.mult), r=[bH, b_tab], w=[bxo])
                    V(lambda e: e.tensor_tensor(out=Hp[:], in0=h0sv, in1=abc(AI, I_AM4), op=ALU.mult), r=[bH, b_tab], w=[bxo])
                    V(lambda e: e.tensor_tensor(out=Xb[:, :, 0:16], in0=Xf[:], in1=Hp[:], op=ALU.add), r=[bxo], w=[bXb])
                    V(lambda e: e.tensor_tensor(out=Xf[:], in0=h0v, in1=abc(AR, I_A4), op=ALU.mult), r=[bH, b_tab], w=[bxo])
                    V(lambda e: e.tensor_tensor(out=Hp[:], in0=h0sv, in1=abc(AI, I_A4), op=ALU.mult), r=[bH, b_tab], w=[bxo])
                    V(lambda e: e.tensor_tensor(out=Xf[:], in0=Xf[:], in1=Hp[:], op=ALU.add), r=[bxo], w=[bxo])
                    for q in range(4):
                        bank = q % 2
                        for j in range(8):
                            g = q * 8 + j
                            T(lambda e, g=g, j=j, bank=bank: e.matmul(
                                PS[bank][:, j * 64:j * 64 + 16], lhsT=Wt[:, g, :], rhs=U[:, g, 0:16],
                                start=True, stop=True), r=[b_tab, bU], w=[bPS[bank]])
                        V(lambda e, q=q, bank=bank: e.tensor_tensor(
                            out=Xf[:, q * 8:q * 8 + 8, :], in0=Xf[:, q * 8:q * 8 + 8, :],
                            in1=PS[bank][:].rearrange("p (j c) -> p j c", j=8)[:, :, 0:16], op=ALU.add),
                          r=[bxo, bPS[bank]], w=[bxo])
                    Xf2 = Xf[:].rearrange("p g b -> p (g b)")
                    for j in range(4):
                        T(lambda e, j=j: e.transpose(out=PS[2][:, j * 128:(j + 1) * 128],
                                                     in_=Xf2[:, j * 128:(j + 1) * 128], identity=identf[:]),
                          r=[bxo, b_const], w=[bPS[2]])
                    V(lambda e: e.tensor_copy(out=xo[:], in_=PS[2][:].rearrange("p (j c) -> p j c", j=4)),
                      r=[bPS[2]], w=[bxo])
                    for j in range(4):
                        for gl in range(8):
                            for (nm, c0) in (("o_s5r_s", 0), ("o_s5i_s", 64)):
                                S.dma("sp", O[nm].rearrange("(b g) p -> g b p", g=G)[8 * j + gl],
                                      xo[gl * 16:gl * 16 + 16, j, c0:c0 + 64], reads=[bxo])
                    xprev = lambda g: Xb[:, g, 0:16]
                    bXprev = bXb
                for gq in range(4):
                    bank = 6 + (gq % 2)
                    for j in range(8):
                        g = gq * 8 + j
                        T(lambda e, g=g, j=j, bank=bank: e.matmul(
                            PS[bank][:, j * 64:j * 64 + nch], lhsT=Tt[:, g, :], rhs=U[:, g, 0:nch],
                            start=True, stop=False), r=[b_tab, bU], w=[bPS[bank]])
                        T(lambda e, g=g, j=j, bank=bank: e.matmul(
                            PS[bank][:, j * 64:j * 64 + nch], lhsT=Vt[:, g, :], rhs=xprev(g)[:, 0:nch],
                            start=False, stop=True), r=[b_tab, bXprev], w=[bPS[bank]])
                    gs = slice(gq * 8, gq * 8 + 8)
                    yv = PS[bank][:].rearrange("p (j c) -> p j c", j=8)[:, :, 0:nch]
                    V(lambda e, gs=gs: e.tensor_tensor(out=ytmp[:, :, 0:nch], in0=U[:, gs, 0:nch],
                                                       in1=DS[:, gs].unsqueeze(2).to_broadcast([128, 8, nch]),
                                                       op=ALU.mult), r=[bU, b_tab], w=[bytmp])
                    V(lambda e, yv=yv: e.tensor_tensor(out=ytmp[:, :, 0:nch], in0=yv, in1=ytmp[:, :, 0:nch],
                                                       op=ALU.add), r=[bPS[bank], bytmp], w=[bytmp])
                    A(lambda e, gs=gs: e.activation(out=Zt[:, gs, 0:nch], in_=ytmp[:, :, 0:nch],
                                                    func=AF.Gelu_apprx_tanh), r=[bytmp], w=[bZ])
                for ct in range(4):
                    bank = ct % 2
                    taus = list(range(8)) if not is_s else [4, 5, 6, 7]
                    for ti_, tau in enumerate(taus):
                        for gl in range(8):
                            g = ct * 8 + gl
                            T(lambda e, g=g, gl=gl, tau=tau, ti_=ti_, bank=bank: e.matmul(
                                PS[bank][:, ti_ * 64:ti_ * 64 + nch],
                                lhsT=masters[:, tau, 112 - 16 * gl:240 - 16 * gl], rhs=Zt[:, g, 0:nch],
                                start=(gl == 0), stop=(gl == 7)), r=[b_tab, bZ], w=[bPS[bank]])
                    if not is_s:
                        A(lambda e, ct=ct, bank=bank: e.copy(
                            out=zT[:, ct, 0:n].rearrange("p (c t) -> p t c", t=8),
                            in_=PS[bank][:].rearrange("p (t c) -> p t c", t=8)), r=[bPS[bank]], w=[bzT])
                    else:
                        A(lambda e, ct=ct, bank=bank: e.copy(
                            out=zT[:, ct, 0:n].rearrange("p (b t) -> p t b", t=4),
                            in_=PS[bank][:].rearrange("p (t c) -> p t c", t=8)[:, 0:4, 0:16]),
                          r=[bPS[bank]], w=[bzT])
                for ct in range(4):
                    bank = 2 + (ct % 2)
                    for kt in range(4):
                        T(lambda e, ct=ct, kt=kt, bank=bank: e.matmul(
                            PS[bank][:, 0:n], lhsT=wglu[:, kt, ct * 128:(ct + 1) * 128], rhs=zT[:, kt, 0:n],
                            start=(kt == 0), stop=(kt == 3)), r=[b_wglu, bzT], w=[bPS[bank]])
                    A(lambda e, ct=ct, bank=bank: e.activation(out=sig[:, ct, 0:n], in_=PS[bank][:, 0:n],
                                                               func=AF.Sigmoid), r=[bPS[bank]], w=[bsig])
                V(lambda e: e.tensor_tensor(out=ssmT[:, :, t0:t0 + n], in0=zT[:, :, 0:n], in1=sig[:, :, 0:n],
                                            op=ALU.mult), r=[bzT, bsig], w=[b_ssmT[bi]])
            S.barrier()
        if dbg:
            with ExitStack() as sd:
                dtmp = alloc(sd, "dtmp", [128, 4, NTOK])
                bd = Buf("dtmp", S.GS[2])
                V(lambda e: e.tensor_copy(out=dtmp[:], in_=ssmT[:]), r=b_ssmT, w=[bd])
                S.dma("sp", O["dbg_ssm"][:, :, :], dtmp[:], reads=[bd])
                S.barrier()
        if stage <= 1:
            S.barrier()
            S.run_block()
            nck.__exit__(None, None, None)
            return nc

        with ExitStack() as sbx:
            x = alloc(sbx, "x", [128, NT, D])
            bx = [Buf("x%d" % n, S.GX) for n in range(NT)]
            for n in range(NTP):
                S.dma("sp", x[:, n, :], I["xp"][n * 128:(n + 1) * 128, :], writes=[bx[n]])
            S.dma("sp", x[0:TS, 16, :], I["xs"][:, :], writes=[bx[16]])
            scrB = make_scr(sbx, "B", [7])
            hT1 = alloc(sbx, "hT1", [128, 8, 128], BF16)
            bhT1 = Buf("hT1")

            def resid_add(n, npart, half, bank):
                V(lambda e: e.tensor_tensor(out=x[:npart, n, half * 512:(half + 1) * 512], in0=PS[bank][:npart, :],
                                            in1=x[:npart, n, half * 512:(half + 1) * 512], op=ALU.add),
                  r=[bPS[bank], bx[n]], w=[bx[n]])

            with ExitStack() as s1:
                wq = alloc(s1, "wqkvg", [128, 8, 2048], BF16)
                wout = alloc(s1, "wout", [128, 8, D], BF16)
                b_wqc = [Buf("wq%d" % c, S.GW[c]) for c in range(4)]
                b_wout = Buf("wout", S.GW[0])
                for c in range(4):
                    for kt in range(8):
                        S.dma("pool", wq[:, kt, c * 512:(c + 1) * 512],
                              I["w_in"][kt * 128:(kt + 1) * 128, 512 + c * 512:512 + (c + 1) * 512], writes=[b_wqc[c]])
                wout_loaded = [False]
                gm2 = alloc(s1, "gm2", [128, 8])
                gn = alloc(s1, "gn", [128, 4])
                rope = alloc(s1, "rope", [128, 3, NT, 64])
                dmp = alloc(s1, "dmp", [128, 512])
                dms = alloc(s1, "dms", [64, 256])
                xi = alloc(s1, "xi", [128, 768])
                zetap = alloc(s1, "zetap", [128, 4])
                zs = alloc(s1, "zs", [64, 64])
                cmask = alloc(s1, "cmask", [128, 16 * 64])
                b_t1 = Buf("tab1")
                S.dma("sp", gm2[:], I["g_mix"].rearrange("(k p) -> p k", p=128), writes=[b_t1])
                S.dma("sp", gn[:], I["ret_gn"].rearrange("(k p) -> p k", p=128), writes=[b_t1])
                for a_ in range(3):
                    S.dma("sp", rope[:, a_, :, :], I["c_rope"][a_], writes=[b_t1])
                S.dma("sp", dmp[:], I["c_dmask_p"][:, :], writes=[b_t1])
                S.dma("sp", dms[:], I["c_dmask_s"][:, :], writes=[b_t1])
                S.dma("sp", xi[:], I["c_xi"][0:1, :].partition_broadcast(128), writes=[b_t1])
                S.dma("sp", zetap[:], I["c_zeta_p"][:, :], writes=[b_t1])
                S.dma("sp", zs[:], I["c_zs"][:, :], writes=[b_t1])
                S.dma("sp", cmask[:], I["c_cmask"][0:1, :].partition_broadcast(128), writes=[b_t1])
                def load_wout():
                    load_w_bf16(wout, b_wout, I["w_out"], 8, D, 0)
                    for k in range(4):
                        V(lambda e: e.tensor_scalar(out=wout[:, 4 + k, :], in0=wout[:, 4 + k, :], scalar1=gn[:, k:k + 1],
                                                    scalar2=None, op0=ALU.mult), r=[b_wout, b_t1], w=[b_wout])
                    wout_loaded[0] = True
                t1q = alloc(s1, "t1q", [128, 512])
                t2q = alloc(s1, "t2q", [128, 512])
                t1k = alloc(s1, "t1k", [128, 512])
                t2k = alloc(s1, "t2k", [128, 512])
                qr = alloc(s1, "qr", [128, 512], BF16)
                kr = alloc(s1, "kr", [128, 512], BF16)
                qT = alloc(s1, "qT", [128, 4, 128], BF16)
                qxT = alloc(s1, "qxT", [128, 4, 128], BF16)
                kT = alloc(s1, "kT", [128, 4, 128], BF16)
                vb = alloc(s1, "vb", [128, 512], BF16)
                vz = alloc(s1, "vz", [128, 512], BF16)
                sg_ = alloc(s1, "sgl", [128, 512])
                sT = alloc(s1, "sT", [128, 4, 128], BF16)
                Sst = alloc(s1, "Sst", [128, 4, 128])
                Sbf = alloc(s1, "Sbf", [128, 4, 128], BF16)
                stats = alloc(s1, "stats", [128, 4, 6])
                mv = alloc(s1, "mv", [128, 4, 2])
                rs4 = alloc(s1, "rs4", [128, 4])
                nb4 = alloc(s1, "nb4", [128, 4])
                on = alloc(s1, "on", [128, 512])
                ret = alloc(s1, "ret", [128, 512], BF16)
                retT = alloc(s1, "retT", [128, 4, 128], BF16)
                S0 = [alloc(s1, "S0_%d" % i, [128, 4, 128]) for i in range(2)]
                S0b = [alloc(s1, "S0b_%d" % i, [128, 4, 128], BF16) for i in range(2)]
                qxm = [alloc(s1, "qxm_%d" % i, [128, 4, 64], BF16) for i in range(2)]
                vzb = [alloc(s1, "vzb_%d" % i, [64, 512], BF16) for i in range(2)]
                Sn = [alloc(s1, "Sn_%d" % i, [128, 4, 128]) for i in range(2)]
                bS0 = [Buf("S0_%d" % i, S.GL[i]) for i in range(2)]
                bS0b = [Buf("S0b_%d" % i) for i in range(2)]
                bqxm = [Buf("qxm%d" % i) for i in range(2)]
                bvzb = [Buf("vzb%d" % i) for i in range(2)]
                bSn = [Buf("Sn%d" % i, S.GS[i]) for i in range(2)]
                (b_t1q, b_t2q, b_t1k, b_t2k, b_qr, b_kr, b_qT, b_qxT, b_kT, b_vb, b_vz, b_sg, b_sT, b_Sst, b_Sbf,
                 b_st, b_on, b_ret, b_retT) = [Buf("p1b%d" % i) for i in range(19)]
                b_Sst.grp = S.GS[2]
                V(lambda e: e.memset(Sst[:], 0.0), w=[b_Sst])
                GC_P = [float(g ** 128) for g in GAM]
                GC_S = [float(g ** 4) for g in GAM]

                import os as _os
                _tl = _os.environ.get("K_TILES")
                _tiles = [int(v) for v in _tl.split(",") if int(v) >= 0] if _tl else list(range(NT))
                _step = int(_os.environ.get("K_STEP", "99"))
                hT1s = [hT1, alloc(s1, "hT1c", [128, 8, 128], BF16)]
                bhT1s = [bhT1, Buf("hT1c")]

                def p1b_norm(n):
                    npt_ = TS if n == 16 else 128
                    rmsnorm_hT(x[:npt_, n, :], bx[n], npt_, gm2[:], hT1s[n % 2], bhT1s[n % 2], scrB, 0, None, bg=b_t1)
                if _tiles:
                    p1b_norm(_tiles[0])
                for ti_, n in enumerate(_tiles):
                    is_s = (n == 16)
                    npt = TS if is_s else 128
                    tok0 = n * 128
                    hT1, bhT1 = hT1s[n % 2], bhT1s[n % 2]
                    pob = [4, 6, 7, 1] if is_s else [4, 4, 4, 4]

                    def po(h):
                        if is_s:
                            return PS[pob[h]][:npt, 0:128]
                        return PS[4][:npt, h * 128:(h + 1) * 128]
                    for c in range(4):
                        for kt in range(8):
                            T(lambda e: e.matmul(PS[c][:npt, :], lhsT=hT1[:, kt, 0:npt],
                                                 rhs=wq[:, kt, c * 512:(c + 1) * 512], start=(kt == 0), stop=(kt == 7)),
                              r=[bhT1, b_wqc[c]], w=[bPS[c]])
                    if not wout_loaded[0]:
                        load_wout()
                    if _step <= 1:
                        continue
                    for (bank, t1_, t2_, out_, bt1, bt2, bo) in ((0, t1q, t2q, qr, b_t1q, b_t2q, b_qr),
                                                               (1, t1k, t2k, kr, b_t1k, b_t2k, b_kr)):
                        pv4 = PS[bank][:npt, :].rearrange("p (h a j) -> p h a j", h=4, a=2)
                        t1v = t1_[:npt, :].rearrange("p (h a j) -> p h a j", h=4, a=2)
                        t2v = t2_[:npt, :].rearrange("p (h a j) -> p h a j", h=4, a=2)
                        cosb = rope[:npt, 0, n, :].unsqueeze(1).unsqueeze(1).to_broadcast([npt, 4, 2, 64])
                        sinb = rope[:npt, 1, n, :].unsqueeze(1).to_broadcast([npt, 4, 64])
                        nsinb = rope[:npt, 2, n, :].unsqueeze(1).to_broadcast([npt, 4, 64])
                        V(lambda e: e.tensor_tensor(out=t1v, in0=pv4, in1=cosb, op=ALU.mult), r=[bPS[bank], b_t1], w=[bt1])
                        V(lambda e: e.tensor_tensor(out=t2v[:, :, 0, :], in0=pv4[:, :, 1, :], in1=nsinb, op=ALU.mult),
                          r=[bPS[bank], b_t1], w=[bt2])
                        V(lambda e: e.tensor_tensor(out=t2v[:, :, 1, :], in0=pv4[:, :, 0, :], in1=sinb, op=ALU.mult),
                          r=[bPS[bank], b_t1], w=[bt2])
                        V(lambda e: e.tensor_tensor(out=out_[:npt, :], in0=t1_[:npt, :], in1=t2_[:npt, :], op=ALU.add),
                           r=[bt1, bt2], w=[bo])
                    if _step <= 2:
                        continue
                    A(lambda e: e.copy(out=vb[:npt, :], in_=PS[2][:npt, :]), r=[bPS[2]], w=[b_vb])
                    if not is_s:
                        V(lambda e: e.tensor_tensor(
                            out=vz[:, :].rearrange("p (h e) -> p h e", h=4),
                            in0=PS[2][:, :].rearrange("p (h e) -> p h e", h=4),
                            in1=zetap[:, :].unsqueeze(2).to_broadcast([128, 4, 128]), op=ALU.mult),
                          r=[bPS[2], b_t1], w=[b_vz])
                    A(lambda e: e.activation(out=sg_[:npt, :], in_=PS[3][:npt, :], func=AF.Silu), r=[bPS[3]], w=[b_sg])
                    pv4b = ps_bf(4)
                    pv5b = ps_bf(5)
                    for h in range(4):
                        T(lambda e: e.transpose(out=pv4b[:, h * 128:h * 128 + npt], in_=qr[:npt, h * 128:(h + 1) * 128],
                                                identity=identb[:npt, :npt]), r=[b_qr, b_const], w=[bPS[4]])
                    for h in range(4):
                        T(lambda e: e.transpose(out=pv5b[:, h * 128:h * 128 + npt], in_=kr[:npt, h * 128:(h + 1) * 128],
                                                identity=identb[:npt, :npt]), r=[b_kr, b_const], w=[bPS[5]])
                    q4 = pv4b[:, 0:512].rearrange("p (h t) -> p h t", h=4)[:, :, 0:npt]
                    k4 = pv5b[:, 0:512].rearrange("p (h t) -> p h t", h=4)[:, :, 0:npt]
                    A(lambda e: e.copy(out=qT[:, :, 0:npt], in_=q4), r=[bPS[4]], w=[b_qT])
                    xiv = (xi[:, 0:512].rearrange("p (h t) -> p h t", h=4) if not is_s
                           else xi[:, 512:768].rearrange("p (h t) -> p h t", h=4))
                    V(lambda e: e.tensor_tensor(out=qxT[:, :, 0:npt], in0=q4, in1=xiv, op=ALU.mult),
                      r=[bPS[4], b_t1], w=[b_qxT])
                    A(lambda e: e.copy(out=kT[:, :, 0:npt], in_=k4), r=[bPS[5]], w=[b_kT])
                    if _step <= 3:
                        continue
                    for h in range(4):
                        T(lambda e: e.matmul(PS[6][:npt, h * 128:h * 128 + npt], lhsT=kT[:, h, 0:npt], rhs=qT[:, h, 0:npt],
                                             start=True, stop=True), r=[b_kT, b_qT], w=[bPS[6]])
                    dmv = (dmp[:, :].rearrange("p (h t) -> p h t", h=4) if not is_s
                           else dms[:, :].rearrange("p (h t) -> p h t", h=4))
                    V(lambda e: e.tensor_tensor(out=sT[:npt, :, 0:npt],
                                                in0=PS[6][:npt, :].rearrange("p (h t) -> p h t", h=4)[:, :, 0:npt],
                                                in1=dmv, op=ALU.mult), r=[bPS[6], b_t1], w=[b_sT])
                    if _step <= 4:
                        continue
                    if ti_ + 1 < len(_tiles):
                        p1b_norm(_tiles[ti_ + 1])
                    for h in range(4):
                        only = (n == 0)
                        T(lambda e: e.matmul(po(h), lhsT=sT[:npt, h, 0:npt],
                                             rhs=vb[:npt, h * 128:(h + 1) * 128], start=True, stop=only),
                          r=[b_sT, b_vb], w=[bPS[pob[h]]])
                        if (not is_s) and n > 0:
                            T(lambda e: e.matmul(po(h), lhsT=qxT[:, h, 0:npt],
                                                 rhs=Sbf[:, h, :], start=False, stop=True),
                              r=[b_qxT, b_Sbf], w=[bPS[4]])
                    if not is_s:
                        for h in range(4):
                            T(lambda e: e.matmul(PS[5][:, h * 128:(h + 1) * 128], lhsT=kr[:, h * 128:(h + 1) * 128],
                                                 rhs=vz[:, h * 128:(h + 1) * 128], start=True, stop=True),
                              r=[b_kr, b_vz], w=[bPS[5]])
                        for h in range(4):
                            V(lambda e: e.scalar_tensor_tensor(out=Sst[:, h, :], in0=Sst[:, h, :], scalar=GC_P[h],
                                                               op0=ALU.mult, in1=PS[5][:, h * 128:(h + 1) * 128],
                                                               op1=ALU.add), r=[b_Sst, bPS[5]], w=[b_Sst])
                        A(lambda e: e.copy(out=Sbf[:], in_=Sst[:]), r=[b_Sst], w=[b_Sbf])
                        if n == NTP - 1:
                            S.dma("sp", O["o_ret_p"].rearrange("h d e -> d h e"), Sst[:], reads=[b_Sst])
                    else:
                        for b in range(16):
                            sl = b % 2
                            S.dma("sp", S0[sl][:], I["sret"][b].rearrange("h d e -> d h e"), writes=[bS0[sl]])
                            A(lambda e: e.copy(out=S0b[sl][:], in_=S0[sl][:]), r=[bS0[sl]], w=[bS0b[sl]])
                            V(lambda e: e.tensor_tensor(
                                out=qxm[sl][:], in0=qxT[:, :, 0:64],
                                in1=cmask[:, b * 64:(b + 1) * 64].unsqueeze(1).to_broadcast([128, 4, 64]), op=ALU.mult),
                              r=[b_qxT, b_t1], w=[bqxm[sl]])
                            for h in range(4):
                                T(lambda e: e.matmul(po(h), lhsT=qxm[sl][:, h, :],
                                                     rhs=S0b[sl][:, h, :], start=False, stop=(b == 15)),
                                  r=[bqxm[sl], bS0b[sl]], w=[bPS[pob[h]]])
                            V(lambda e: e.tensor_tensor(
                                out=vzb[sl][:, :].rearrange("p (h e) -> p h e", h=4),
                                in0=PS[2][:64, :].rearrange("p (h e) -> p h e", h=4),
                                in1=zs[:, b * 4:(b + 1) * 4].unsqueeze(2).to_broadcast([64, 4, 128]), op=ALU.mult),
                              r=[bPS[2], b_t1], w=[bvzb[sl]])
                            kvb = 5 if sl == 0 else 0
                            for h in range(4):
                                T(lambda e: e.matmul(PS[kvb][:, h * 128:(h + 1) * 128], lhsT=kr[:64, h * 128:(h + 1) * 128],
                                                     rhs=vzb[sl][:, h * 128:(h + 1) * 128], start=True, stop=True),
                                  r=[b_kr, bvzb[sl]], w=[bPS[kvb]])
                            for h in range(4):
                                V(lambda e: e.scalar_tensor_tensor(out=Sn[sl][:, h, :], in0=S0[sl][:, h, :], scalar=GC_S[h],
                                                                   op0=ALU.mult, in1=PS[kvb][:, h * 128:(h + 1) * 128],
                                                                   op1=ALU.add), r=[bS0[sl], bPS[kvb]], w=[bSn[sl]])
                            S.dma("sp", O["o_ret_s"][b].rearrange("h d e -> d h e"), Sn[sl][:], reads=[bSn[sl]])
                    if _step <= 5:
                        continue
                    for h in range(4):
                        V(lambda e: e.bn_stats(out=stats[:npt, h, :], in_=po(h)),
                          r=[bPS[pob[h]]], w=[b_st])
                    for h in range(4):
                        V(lambda e: e.bn_aggr(out=mv[:npt, h, :], in_=stats[:npt, h, :]), r=[b_st], w=[b_st])
                    A(lambda e: e.activation(out=rs4[:npt, :], in_=mv[:npt, :, 1], func=AF.Sqrt, scale=1.0,
                                             bias=epsc[:npt, :]), r=[b_st, b_const], w=[b_st])
                    V(lambda e: e.reciprocal(out=rs4[:npt, :], in_=rs4[:npt, :]), r=[b_st], w=[b_st])
                    V(lambda e: e.scalar_tensor_tensor(out=nb4[:npt, :], in0=mv[:npt, :, 0], scalar=-1.0, op0=ALU.mult,
                                                       in1=rs4[:npt, :], op1=ALU.mult), r=[b_st], w=[b_st])
                    for h in range(4):
                        A(lambda e: e.activation(out=on[:npt, h * 128:(h + 1) * 128], in_=po(h),
                                                 func=AF.Identity, scale=rs4[:npt, h:h + 1], bias=nb4[:npt, h:h + 1]),
                          r=[bPS[pob[h]], b_st], w=[b_on])
                    V(lambda e: e.tensor_tensor(out=ret[:npt, :], in0=on[:npt, :], in1=sg_[:npt, :], op=ALU.mult),
                       r=[b_on, b_sg], w=[b_ret])
                    if _step <= 6:
                        continue
                    pv6b = ps_bf(6)
                    for h in range(4):
                        T(lambda e: e.transpose(out=pv6b[:, h * 128:h * 128 + npt], in_=ret[:npt, h * 128:(h + 1) * 128],
                                                identity=identb[:npt, :npt]), r=[b_ret, b_const], w=[bPS[6]])
                    A(lambda e: e.copy(out=retT[:, :, 0:npt],
                                       in_=pv6b[:, 0:512].rearrange("p (h t) -> p h t", h=4)[:, :, 0:npt]),
                      r=[bPS[6]], w=[b_retT])
                    if _step <= 7:
                        continue
                    bi_ = min(n // 4, 4)
                    for half in range(2):
                        bank = 2 + half
                        for kt in range(8):
                            lh = ssmT[:, kt, tok0:tok0 + npt] if kt < 4 else retT[:, kt - 4, 0:npt]
                            T(lambda e: e.matmul(PS[bank][:npt, :], lhsT=lh, rhs=wout[:, kt, half * 512:(half + 1) * 512],
                                                 start=(kt == 0), stop=(kt == 7)),
                              r=[b_ssmT[bi_], b_retT, b_wout], w=[bPS[bank]])
                        resid_add(n, npt, half, bank)
                S.barrier()
            if dbg:
                for n in range(NT):
                    S.dma("sp", O["dbg_x"][:, n, :], x[:, n, :], reads=[bx[n]])
            if stage <= 2:
                S.barrier()
                S.run_block()
                nck.__exit__(None, None, None)
                return nc

            with ExitStack() as s2:
                gx = alloc(s2, "gx", [128, 8])
                gmem = alloc(s2, "gmem", [128, 8])
                ones = alloc(s2, "ones", [128, 128], BF16)
                b_t2 = Buf("tab2")
                S.dma("sp", gx[:], I["g_xattn"].rearrange("(k p) -> p k", p=128), writes=[b_t2])
                S.dma("sp", gmem[:], I["g_mem"].rearrange("(k p) -> p k", p=128), writes=[b_t2])
                V(lambda e: e.memset(ones[:], 1.0), w=[b_t2])
                KT = alloc(s2, "KT", [128, 8, MEM], BF16)
                Vm = alloc(s2, "Vm", [128, 2, D], BF16)
                b_KT, b_Vm = Buf("KT"), Buf("Vm")
                wmq = alloc(s2, "wmq", [128, 8, D], BF16)
                b_wmq, b_wmo = Buf("wmq", S.GW[2]), Buf("wmo", S.GW[3])
                with ExitStack() as s2a:
                    wmk = alloc(s2a, "wmk", [128, 8, D], BF16)
                    wmv = alloc(s2a, "wmv", [128, 8, D], BF16)
                    b_wmk, b_wmv = Buf("wmk", S.GW[0]), Buf("wmv", S.GW[1])
                    load_w_bf16(wmk, b_wmk, I["w_mk"], 8, D, 0)
                    load_w_bf16(wmv, b_wmv, I["w_mv"], 8, D, 0)
                    load_w_bf16(wmq, b_wmq, I["w_mq"], 8, D, 0)
                    mx = [alloc(s2a, "mx%d" % i, [128, D]) for i in range(2)]
                    bmx = [Buf("mx%d" % i, S.GL[i]) for i in range(2)]
                    mhT = alloc(s2a, "mhT", [128, 8, MEM], BF16)
                    b_mhT = Buf("mhT")
                    mo = [alloc(s2a, "mo%d" % i, [128, D]) for i in range(2)]
                    bmo = [Buf("mo%d" % i, S.GS[i]) for i in range(2)]
                    _k2a = int(_os.environ.get("K2A", "9"))
                    for mt in range(2):
                        S.dma("sp", mx[mt][:], I["memp"][mt * 128:(mt + 1) * 128, :], writes=[bmx[mt]])
                        if _k2a >= 1:
                            rmsnorm_hT(mx[mt][:, :], bmx[mt], 128, gmem[:], mhT, b_mhT, scrB, mt * 128, None,
                                       ln=True, bg=b_t2)
                    oi = 0
                    for (wm, bwm, oname, isv) in ((wmk, b_wmk, "o_mk", False), (wmv, b_wmv, "o_mv", True)) if _k2a >= 2 else ():
                        for mt in range(2):
                            sl = oi % 2
                            oi += 1
                            for half in range(2):
                                bank = half
                                for kt in range(8):
                                    T(lambda e: e.matmul(PS[bank][:, :], lhsT=mhT[:, kt, mt * 128:(mt + 1) * 128],
                                                         rhs=wm[:, kt, half * 512:(half + 1) * 512], start=(kt == 0),
                                                         stop=(kt == 7)), r=[b_mhT, bwm], w=[bPS[bank]])
                                A(lambda e: e.copy(out=mo[sl][:, half * 512:(half + 1) * 512], in_=PS[bank][:, :]),
                                  r=[bPS[bank]], w=[bmo[sl]])
                                if isv:
                                    V(lambda e: e.tensor_copy(out=Vm[:, mt, half * 512:(half + 1) * 512], in_=PS[bank][:, :]),
                                      r=[bPS[bank]], w=[b_Vm])
                            S.dma("sp", O[oname][mt * 128:(mt + 1) * 128, :], mo[sl][:], reads=[bmo[sl]])
                    for j in range(8 if _k2a >= 3 else 0):
                        bank = 2 + (j % 2)
                        for kt in range(8):
                            T(lambda e: e.matmul(PS[bank][:, 0:MEM], lhsT=wmk[:, kt, j * 128:(j + 1) * 128],
                                                 rhs=mhT[:, kt, :], start=(kt == 0), stop=(kt == 7)),
                              r=[b_mhT, b_wmk], w=[bPS[bank]])
                        A(lambda e: e.copy(out=KT[:, j, :], in_=PS[bank][:, 0:MEM]), r=[bPS[bank]], w=[b_KT])
                    S.barrier()
                wmo = alloc(s2, "wmo", [128, 8, D], BF16)
                load_w_bf16(wmo, b_wmo, I["w_mo"], 8, D, 0)
                hT4 = alloc(s2, "hT4", [128, 8, 512], BF16)
                qm4 = alloc(s2, "qm4", [128, 8, 512], BF16)
                oT4 = alloc(s2, "oT4", [128, 8, 512], BF16)
                eT4 = [alloc(s2, "eT4_%d" % i, [128, 2, 512], BF16) for i in range(2)]
                rdn4 = [alloc(s2, "rdn4_%d" % i, [128, 512]) for i in range(2)]
                b_hT4, b_qm4, b_oT4 = Buf("hT4"), Buf("qm4"), Buf("oT4")
                b_eT4 = [Buf("eT4_%d" % i) for i in range(2)]
                b_rdn4 = [Buf("rdn4_%d" % i) for i in range(2)]
                Kb = [alloc(s2, "Kb%d" % i, [128, 2, D]) for i in range(2)]
                bKb = [Buf("Kb%d" % i, S.GL[i]) for i in range(2)]
                KbT = [alloc(s2, "KbT%d" % i, [128, 8, MEM], BF16) for i in range(2)]
                bKbT = [Buf("KbT%d" % i) for i in range(2)]
                Vb = [alloc(s2, "Vb%d" % i, [128, 2, D], BF16) for i in range(2)]
                bVb = [Buf("Vb%d" % i, S.GW[i]) for i in range(2)]
                eTs = alloc(s2, "eTs", [128, 2, 4, 64], BF16)
                b_eTs = Buf("eTs")
                qrot = [0]

                def q_proj(nc_):
                    for j in range(8):
                        bank = 5 + (qrot[0] % 3)
                        qrot[0] += 1
                        for kt in range(8):
                            T(lambda e: e.matmul(PS[bank][:, 0:nc_], lhsT=wmq[:, kt, j * 128:(j + 1) * 128],
                                                 rhs=hT4[:, kt, 0:nc_], start=(kt == 0), stop=(kt == 7)),
                              r=[b_wmq, b_hT4], w=[bPS[bank]])
                        A(lambda e: e.activation(out=qm4[:, j, 0:nc_], in_=PS[bank][:, 0:nc_], func=AF.Copy,
                                                 scale=1.0 / 16.0), r=[bPS[bank]], w=[b_qm4])

                def w_mo_resid(n, npt, c0):
                    for half in range(2):
                        bank = 5 + (qrot[0] % 3)
                        qrot[0] += 1
                        for j in range(8):
                            T(lambda e: e.matmul(PS[bank][:npt, :], lhsT=oT4[:, j, c0:c0 + npt],
                                                 rhs=wmo[:, j, half * 512:(half + 1) * 512], start=(j == 0), stop=(j == 7)),
                              r=[b_oT4, b_wmo], w=[bPS[bank]])
                        resid_add(n, npt, half, bank)

                for bi in range(4):
                    for ti in range(4):
                        n = bi * 4 + ti
                        rmsnorm_hT(x[:, n, :], bx[n], 128, gx[:], hT4, b_hT4, scrB, ti * 128, None, ln=True, bg=b_t2)
                    q_proj(512)
                    for h in range(4):
                        par = h % 2
                        for mt in range(2):
                            bank = mt
                            for dt_ in range(2):
                                T(lambda e: e.matmul(PS[bank][:, :], lhsT=KT[:, h * 2 + dt_, mt * 128:(mt + 1) * 128],
                                                     rhs=qm4[:, h * 2 + dt_, :], start=(dt_ == 0), stop=(dt_ == 1)),
                                  r=[b_KT, b_qm4], w=[bPS[bank]])
                            A(lambda e: e.activation(out=eT4[par][:, mt, :], in_=PS[bank][:, :], func=AF.Exp),
                              r=[bPS[bank]], w=[b_eT4[par]])
                        for mt in range(2):
                            T(lambda e: e.matmul(PS[2][:, :], lhsT=ones[:, :], rhs=eT4[par][:, mt, :], start=(mt == 0),
                                                 stop=(mt == 1)), r=[b_t2, b_eT4[par]], w=[bPS[2]])
                        A(lambda e: e.activation(out=rdn4[par][:, :], in_=PS[2][:, :], func=AF.Ln), r=[bPS[2]], w=[b_rdn4[par]])
                        A(lambda e: e.activation(out=rdn4[par][:, :], in_=rdn4[par][:, :], func=AF.Exp, scale=-1.0),
                          r=[b_rdn4[par]], w=[b_rdn4[par]])
                        for dt_ in range(2):
                            bank = 3 + dt_
                            j = h * 2 + dt_
                            for mt in range(2):
                                T(lambda e: e.matmul(PS[bank][:, :], lhsT=Vm[:, mt, j * 128:(j + 1) * 128],
                                                     rhs=eT4[par][:, mt, :], start=(mt == 0), stop=(mt == 1)),
                                  r=[b_Vm, b_eT4[par]], w=[bPS[bank]])
                            V(lambda e: e.tensor_tensor(out=oT4[:, j, :], in0=PS[bank][:, :], in1=rdn4[par][:, :], op=ALU.mult),
                              r=[bPS[bank], b_rdn4[par]], w=[b_oT4])
                    for ti in range(4):
                        w_mo_resid(bi * 4 + ti, 128, ti * 128)
                n = 16
                rmsnorm_hT(x[:TS, n, :], bx[n], TS, gx[:], hT4, b_hT4, scrB, 0, None, ln=True, bg=b_t2)
                q_proj(TS)
                rden_s = rdn4[0][:, 0:256].rearrange("p (h t) -> p h t", h=4)
                for b in range(16):
                    sl = b % 2
                    S.dma("sp", Kb[sl][:], I["ck"][b].rearrange("(mt p) d -> p mt d", p=128), writes=[bKb[sl]])
                    for q4 in range(4):
                        bank = 2 + (q4 % 2)
                        for i4 in range(4):
                            idx = q4 * 4 + i4
                            j, mt = idx // 2, idx % 2
                            T(lambda e: e.transpose(out=PS[bank][:, i4 * 128:(i4 + 1) * 128],
                                                    in_=Kb[sl][:, mt, j * 128:(j + 1) * 128], identity=identf[:]),
                              r=[bKb[sl], b_const], w=[bPS[bank]])
                        A(lambda e: e.copy(
                            out=KbT[sl][:, 2 * q4:2 * q4 + 2, :].rearrange("p j (m t) -> p j m t", m=2),
                            in_=PS[bank][:, :].rearrange("p (j m t) -> p j m t", j=2, m=2)),
                          r=[bPS[bank]], w=[bKbT[sl]])
                    for h in range(4):
                        for mt in range(2):
                            c0 = mt * 256 + h * 64 + 4 * b
                            for dt_ in range(2):
                                T(lambda e: e.matmul(PS[4][:, c0:c0 + 4],
                                                     lhsT=KbT[sl][:, h * 2 + dt_, mt * 128:(mt + 1) * 128],
                                                     rhs=qm4[:, h * 2 + dt_, 4 * b:4 * b + 4], start=(dt_ == 0),
                                                     stop=(dt_ == 1)), r=[bKbT[sl], b_qm4], w=[bPS[4]])
                A(lambda e: e.activation(out=eTs[:].rearrange("p m h t -> p (m h t)"), in_=PS[4][:, :], func=AF.Exp),
                  r=[bPS[4]], w=[b_eTs])
                for h in range(4):
                    for mt in range(2):
                        T(lambda e: e.matmul(PS[0][:, h * 64:(h + 1) * 64], lhsT=ones[:, :], rhs=eTs[:, mt, h, :],
                                             start=(mt == 0), stop=(mt == 1)), r=[b_t2, b_eTs], w=[bPS[0]])
                V(lambda e: e.reciprocal(out=rden_s, in_=PS[0][:, 0:256].rearrange("p (h t) -> p h t", h=4)),
                  r=[bPS[0]], w=[b_rdn4[0]])
                for b in range(16):
                    sl = b % 2
                    for mt in range(2):
                        S.dma("pool", Vb[sl][:, mt, :], I["cv"][b, mt * 128:(mt + 1) * 128, :], writes=[bVb[sl]])
                    for j in range(8):
                        h = j // 2
                        for mt in range(2):
                            T(lambda e: e.matmul(PS[1][:, j * 64 + 4 * b:j * 64 + 4 * b + 4],
                                                 lhsT=Vb[sl][:, mt, j * 128:(j + 1) * 128],
                                                 rhs=eTs[:, mt, h, 4 * b:4 * b + 4], start=(mt == 0), stop=(mt == 1)),
                              r=[bVb[sl], b_eTs], w=[bPS[1]])
                V(lambda e: e.tensor_tensor(
                    out=oT4[:, :, 0:64].rearrange("p (h a) t -> p h a t", a=2),
                    in0=PS[1][:, :].rearrange("p (h a t) -> p h a t", h=4, a=2),
                    in1=rden_s.unsqueeze(2).to_broadcast([128, 4, 2, 64]), op=ALU.mult),
                  r=[bPS[1], b_rdn4[0]], w=[b_oT4])
                w_mo_resid(16, TS, 0)
                S.barrier()
            if stage <= 3:
                if dbg:
                    for n in range(NT):
                        S.dma("sp", O["dbg_x"][:, n, :], x[:, n, :], reads=[bx[n]])
                S.barrier()
                S.run_block()
                nck.__exit__(None, None, None)
                return nc

            with ExitStack() as s3:
                gml = alloc(s3, "gml", [128, 8])
                b_t3 = Buf("tab3")
                S.dma("sp", gml[:], I["g_mlp"].rearrange("(k p) -> p k", p=128), writes=[b_t3])
                hTa = alloc(s3, "hTa", [128, 8, NTOK], BF16)
                b_hTa = [Buf("hTa%d" % n) for n in range(NT)]
                wup = [alloc(s3, "wup%d" % i, [128, 8, 512], BF16) for i in range(2)]
                wdn = [alloc(s3, "wdn%d" % i, [128, 4, D], BF16) for i in range(2)]
                bwup = [Buf("wup%d" % i, S.GW[i]) for i in range(2)]
                bwdn = [Buf("wdn%d" % i, S.GW[2 + i]) for i in range(2)]
                rl = [alloc(s3, "rl%d" % i, [128, 512]) for i in range(2)]
                brl = [Buf("rl%d" % i) for i in range(2)]
                aT = [alloc(s3, "aT%d" % i, [128, 4, 512], BF16) for i in range(2)]
                baT = [Buf("aT%d" % i) for i in range(2)]

                def load_fc(fc):
                    sl = fc % 2
                    for kt in range(8):
                        S.dma("pool", wup[sl][:, kt, :], I["w_up"][kt * 128:(kt + 1) * 128, fc * 512:(fc + 1) * 512],
                              writes=[bwup[sl]])
                    for ft in range(4):
                        S.dma("pool", wdn[sl][:, ft, :], I["w_down"][fc * 512 + ft * 128:fc * 512 + (ft + 1) * 128, :],
                              writes=[bwdn[sl]])
                load_fc(0)
                scrB["pb"] = [7, 6]
                for n in range(NT):
                    npt = TS if n == 16 else 128
                    rmsnorm_hT(x[:npt, n, :], bx[n], npt, gml[:], hTa, b_hTa[n], scrB, n * 128, None, ln=True, bg=b_t3)
                blocks3 = [(i * 512, 512) for i in range(4)] + [(SEQ, TS)]
                items = [(fc, blk) for fc in range(8) for blk in blocks3]
                ctr = {"ri": 0, "di": 0}
                load_fc(1)

                def mlp_up(i):
                    fc, (t0, nn) = items[i]
                    sl, asl = fc % 2, i % 2
                    tiles = list(range(t0 // 128, t0 // 128 + (nn + 127) // 128))
                    for ft in range(4):
                        bank = ft
                        for kt in range(8):
                            T(lambda e: e.matmul(PS[bank][:, 0:nn], lhsT=wup[sl][:, kt, ft * 128:(ft + 1) * 128],
                                                 rhs=hTa[:, kt, t0:t0 + nn], start=(kt == 0), stop=(kt == 7)),
                              r=[bwup[sl]] + [b_hTa[t] for t in tiles], w=[bPS[bank]])
                        rsl = ctr["ri"] % 2
                        ctr["ri"] += 1
                        A(lambda e: e.activation(out=rl[rsl][:, 0:nn], in_=PS[bank][:, 0:nn], func=AF.Relu),
                          r=[bPS[bank]], w=[brl[rsl]])
                        V(lambda e: e.tensor_tensor(out=aT[asl][:, ft, 0:nn], in0=rl[rsl][:, 0:nn], in1=rl[rsl][:, 0:nn],
                                                    op=ALU.mult), r=[brl[rsl]], w=[baT[asl]])

                def mlp_down(i):
                    fc, (t0, nn) = items[i]
                    sl, asl = fc % 2, i % 2
                    tiles = list(range(t0 // 128, t0 // 128 + (nn + 127) // 128))
                    for ti, tl in enumerate(tiles):
                        npt = TS if tl == 16 else 128
                        for half in range(2):
                            bank = 4 + (ctr["di"] % 4)
                            ctr["di"] += 1
                            for ft in range(4):
                                T(lambda e: e.matmul(PS[bank][:npt, :], lhsT=aT[asl][:, ft, ti * 128:ti * 128 + npt],
                                                     rhs=wdn[sl][:, ft, half * 512:(half + 1) * 512], start=(ft == 0),
                                                     stop=(ft == 3)), r=[baT[asl], bwdn[sl]], w=[bPS[bank]])
                            resid_add(tl, npt, half, bank)

                mlp_up(0)
                for i in range(len(items)):
                    if i + 1 < len(items):
                        mlp_up(i + 1)
                    mlp_down(i)
                    fc = items[i][0]
                    if (i + 1 == len(items) or items[i + 1][0] != fc) and fc + 2 < 8:
                        load_fc(fc + 2)
                S.barrier()
            if dbg:
                for n in range(NT):
                    S.dma("sp", O["dbg_x"][:, n, :], x[:, n, :], reads=[bx[n]])
            with ExitStack() as s4:
                gf = alloc(s4, "gf", [128, D])
                b_gf = Buf("gf")
                S.dma("sp", gf[:], I["g_final"].rearrange("(o d) -> o d", o=1).partition_broadcast(128), writes=[b_gf])
                yst = [alloc(s4, "yst%d" % i, [128, D]) for i in range(3)]
                byst = [Buf("yst%d" % i, S.GS[i]) for i in range(3)]
                for n in range(NT):
                    npt = TS if n == 16 else 128
                    sl = n % 3
                    k4 = n % 2
                    sq, ss, rstd, bscr = scrB["sq"][k4], scrB["ss"][k4], scrB["rstd"][k4], scrB["ba"][k4]
                    A(lambda e: e.activation(out=sq[:npt, :], in_=x[:npt, n, :], func=AF.Square, accum_out=ss[:npt, :]),
                      r=[bx[n]], w=[bscr])
                    A(lambda e: e.activation(out=rstd[:npt, :], in_=ss[:npt, :], func=AF.Ln, scale=1.0 / D,
                                             bias=epsc[:npt, :]), r=[bscr, b_const], w=[bscr])
                    A(lambda e: e.activation(out=rstd[:npt, :], in_=rstd[:npt, :], func=AF.Exp, scale=-0.5),
                      r=[bscr], w=[bscr])
                    V(lambda e: e.scalar_tensor_tensor(out=yst[sl][:npt, :], in0=x[:npt, n, :], scalar=rstd[:npt, :],
                                                       op0=ALU.mult, in1=gf[:npt, :], op1=ALU.mult),
                      r=[bx[n], bscr, b_gf], w=[byst[sl]])
                    if n < 16:
                        S.dma("sp", O["yp"][n * 128:(n + 1) * 128, :], yst[sl][:, :], reads=[byst[sl]])
                    else:
                        S.dma("sp", O["ys"][:, :], yst[sl][:TS, :], reads=[byst[sl]])
                S.barrier()
            S.barrier()
            S.run_block()
            nck.__exit__(None, None, None)
    return nc


_NC = None


def kernel(**inputs):
    global _NC
    if _NC is None:
        _NC = build()
    maps = _in_maps(inputs)
    res = run_bass_kernel_spmd(_NC, maps, core_ids=list(range(8)))
    R = res.results
    f = np.float32

    def cat(name, shape=None):
        return np.stack([np.asarray(R[c][name], f) for c in range(8)])
    y_prompt = cat("yp")
    y_sample = cat("ys").reshape(128, 4, D)
    s5r_p = cat("o_s5r_p")[None]
    s5i_p = cat("o_s5i_p")[None]
    ret_p = cat("o_ret_p")[None]
    mk_p = cat("o_mk").reshape(8, MEM, 4, 256)[None]
    mv_p = cat("o_mv").reshape(8, MEM, 4, 256)[None]
    s5r_s = cat("o_s5r_s").reshape(128, G, 64)[None]
    s5i_s = cat("o_s5i_s").reshape(128, G, 64)[None]
    ret_s = cat("o_ret_s").reshape(128, 4, 128, 128)[None]
    return (y_prompt, y_sample, s5r_p, s5i_p, ret_p, mk_p, mv_p, s5r_s, s5i_s, ret_s)


def _in_maps(inputs):
    cst = _consts()
    f = np.float32
    maps = []
    w = {}
    for k in W_NAMES:
        a = np.asarray(inputs[k], f)
        if k != "g_final":
            a = a[0]
        w[k] = np.ascontiguousarray(a.reshape(W_SHAPES[k]))
    for c in range(8):
        m = dict(w)
        m.update(cst)
        b0 = 16 * c
        m["xp"] = np.ascontiguousarray(np.asarray(inputs["x_prompt"], f)[c])
        m["xs"] = np.ascontiguousarray(np.asarray(inputs["x_sample"], f)[b0:b0 + 16].reshape(TS, D))
        m["memp"] = np.ascontiguousarray(np.asarray(inputs["mem_prompt"], f)[c])
        m["s5r"] = np.ascontiguousarray(np.asarray(inputs["state_s5_re"], f)[0, b0:b0 + 16].reshape(512, 64))
        m["s5i"] = np.ascontiguousarray(np.asarray(inputs["state_s5_im"], f)[0, b0:b0 + 16].reshape(512, 64))
        m["sret"] = np.ascontiguousarray(np.asarray(inputs["state_ret"], f)[0, b0:b0 + 16])
        m["ck"] = np.ascontiguousarray(np.asarray(inputs["cache_mem_k"], f)[0, b0:b0 + 16].reshape(16, MEM, D))
        m["cv"] = np.ascontiguousarray(np.asarray(inputs["cache_mem_v"], f)[0, b0:b0 + 16].reshape(16, MEM, D))
        maps.append(m)
    return maps
```

```python
import numpy as np
import concourse.bass as bass
import concourse.mybir as mybir
from concourse.bass_utils import run_bass_kernel_spmd
from contextlib import ExitStack

F32 = mybir.dt.float32
BF16 = mybir.dt.bfloat16
AF = mybir.ActivationFunctionType
ALU = mybir.AluOpType

D = 1024
SEQ = 2048
NTP = 16
TS = 64
NT = 17
NTOK = SEQ + TS
G = 32
DFF = 4096
MEM = 256
EPS = 1e-6
PAST = 16384.0
MAGIC = 12582912.0
TWO_PI = float(2.0 * np.pi)
ML = [7, 6, 5, 4, 3, 2, 1, 0, 1, 2, 3, 4, 5, 6, 7, 8, -4, 0.5]
K1 = len(ML)
I_A1, I_A8, I_A4, I_AM4, I_HALF = 8, 15, 3, 16, 17
GAM = [1.0 - 2.0 ** (-5.0 - h) for h in range(4)]


class Grp:
    __slots__ = ("sem", "cnt", "sealed")


class Buf:
    __slots__ = ("w", "r", "name", "grp", "ps")

    def __init__(self, name="", grp=None, ps=False):
        self.w = None
        self.r = []
        self.name = name
        self.grp = grp
        self.ps = ps


class _Rec:
    def __init__(self):
        self.call = None

    def __getattr__(self, name):
        def f(*a, **kw):
            self.call = (name, a, kw)
            return self
        return f


class Sched:
    ENG = ("pe", "dve", "act", "pool", "sp")

    def __init__(self, nc, stack, self_sync=("dve", "act", "pool")):
        self.nc = nc
        self.stack = stack
        self.prog = {k: [] for k in self.ENG}
        self.cnt = {k: 0 for k in self.ENG}
        self.waited = {k: {} for k in self.ENG}
        self.sem = {}
        self.nsem = 0
        for k in ("pe", "dve", "act", "pool"):
            self.sem[k] = self.new_sem("c_" + k)
        self.self_sync = set(self_sync)
        self.groups = []
        self.GC = self.group("gc")
        self.GP = self.group("gp")
        self.GW = [self.group("gw%d" % i) for i in range(4)]
        self.GX = self.group("gx")
        self.GL = [self.group("gl%d" % i) for i in range(2)]
        self.GS = [self.group("gs%d" % i) for i in range(3)]

    def group(self, name):
        g = Grp()
        g.sem = self.new_sem(name)
        g.cnt = 0
        g.sealed = False
        self.groups.append(g)
        return g

    def new_sem(self, name):
        self.nsem += 1
        assert self.nsem < 98, "too many semaphores"
        return self.stack.enter_context(self.nc.semaphore(name + "_%d" % self.nsem))

    def _waits(self, eng, deps):
        w = self.waited[eng]
        need = {}
        dd = []
        for d in deps:
            if isinstance(d, Grp):
                d.sealed = True
                dd.append((d.sem, d.cnt))
            else:
                dd.append(d)
        deps = dd
        for (s, v) in deps:
            if eng in self.sem and s is self.sem[eng] and eng not in self.self_sync:
                continue
            k = id(s)
            if w.get(k, 0) >= v:
                continue
            if k not in need or need[k][1] < v:
                need[k] = (s, v)
        for k, (s, v) in need.items():
            w[k] = v
            self.prog[eng].append(lambda e, s=s, v=v: e.wait_ge(s, v))

    def op(self, eng, fn, reads=(), writes=()):
        deps = []
        for b in reads:
            if b.w is not None:
                deps.append(b.w)
            if b.ps:
                mys = self.sem[eng]
                deps.extend(d for d in b.r if not (isinstance(d, tuple) and d[0] is mys))
        for b in writes:
            if b.w is not None:
                deps.append(b.w)
            deps.extend(b.r)
        self._waits(eng, deps)
        self.cnt[eng] += 1
        c = self.cnt[eng]
        s = self.sem[eng]
        rec = _Rec()
        fn(rec)
        name, a, kw = rec.call
        self.prog[eng].append(lambda e, name=name, a=a, kw=kw, s=s: getattr(e, name)(*a, **kw).then_inc(s, 1))
        for b in reads:
            b.r.append((s, c))
        for b in writes:
            b.w = (s, c)
            b.r = []

    def dma(self, q, out, in_, reads=(), writes=(), **kw):
        tb = writes[0] if writes else reads[0]
        g = tb.grp
        if g is None:
            g = self.GP if q == "pool" else (self.GC if writes else self.GS[0])
        deps = []
        for b in reads:
            if b.w is not None:
                deps.append(b.w)
        for b in writes:
            if b.w is not None and b.w is not g:
                deps.append(b.w)
            deps.extend(b.r)
        self._waits(q, deps)
        if g.sealed and g.cnt > 0:
            self._waits(q, [(g.sem, g.cnt)])
        g.sealed = False
        g.cnt += 16
        s = g.sem
        self.prog[q].append(
            lambda e, out=out, in_=in_, s=s, kw=kw: e.dma_start(out=out, in_=in_, **kw).then_inc(s, 16))
        for b in reads:
            b.r.append(g)
        for b in writes:
            b.w = g
            b.r = []

    def barrier(self, engines=None):
        deps = [(self.sem[k], self.cnt[k]) for k in ("pe", "dve", "act", "pool") if self.cnt[k] > 0]
        deps += [g for g in self.groups if g.cnt > 0]
        for e in (engines or self.ENG):
            self._waits(e, deps)

    def run_block(self):
        nc = self.nc
        with nc.Block() as block:
            @block.sync
            def _(e):
                for t in self.prog["sp"]:
                    t(e)

            @block.tensor
            def _(e):
                for t in self.prog["pe"]:
                    t(e)

            @block.vector
            def _(e):
                for t in self.prog["dve"]:
                    t(e)

            @block.scalar
            def _(e):
                for t in self.prog["act"]:
                    t(e)

            @block.gpsimd
            def _(e):
                for t in self.prog["pool"]:
                    t(e)


_CONSTS = None


def _consts():
    global _CONSTS
    if _CONSTS is not None:
        return _CONSTS
    f = np.float32
    c = {}
    c["c_ident"] = np.eye(128, dtype=f)
    m = np.zeros((8, 128, 240), f)
    for a in range(8):
        for i in range(16):
            m[a, 16 * a + i, 112 + i] = 1.0
    c["c_masters"] = m
    ml = np.array(ML, np.float64)
    rows = np.concatenate([ml / (2 * np.pi), ml, 8.0 * (np.arange(64) + 1) / (2 * np.pi)])
    c["c_rows"] = rows.astype(f)[None, :]
    sg = np.zeros((128, 2), f)
    sg[:64, 0] = 1.0
    sg[64:, 0] = -1.0
    sg[:64, 1] = -1.0
    sg[64:, 1] = 1.0
    c["c_sgn"] = sg
    inv = (f(10000.0) ** (-(np.arange(64, dtype=f) / f(64.0)))).astype(f)
    pos = np.zeros((128, NT), f)
    for n in range(NTP):
        pos[:, n] = 128 * n + np.arange(128)
    pos[:64, 16] = PAST + (np.arange(64) % 4)
    ang = (pos[:, :, None] * inv[None, None, :]).astype(f).astype(np.float64)
    c["c_rope"] = np.stack([np.cos(ang), np.sin(ang), -np.sin(ang)]).astype(f)
    lg = np.log(np.array(GAM, np.float64))
    sc = 128.0 ** -0.5
    idx = np.arange(128)
    dm = np.zeros((128, 4, 128), np.float64)
    diff = idx[None, :] - idx[:, None]
    for h in range(4):
        dm[:, h, :] = np.where(diff >= 0, np.exp(np.maximum(diff, 0) * lg[h]), 0.0) * sc
    c["c_dmask_p"] = dm.reshape(128, 512).astype(f)
    ds_ = np.zeros((64, 4, 64), np.float64)
    r = np.arange(64)
    bb = r // 4
    tt = r % 4
    same = bb[:, None] == bb[None, :]
    dts = tt[None, :] - tt[:, None]
    for h in range(4):
        ds_[:, h, :] = np.where(same & (dts >= 0), np.exp(np.maximum(dts, 0) * lg[h]), 0.0) * sc
    c["c_dmask_s"] = ds_.reshape(64, 256).astype(f)
    xi_p = np.stack([np.exp((idx + 1.0) * lg[h]) * sc for h in range(4)])
    xi_s = np.stack([np.exp((tt + 1.0) * lg[h]) * sc for h in range(4)])
    c["c_xi"] = np.concatenate([xi_p.reshape(-1), xi_s.reshape(-1)]).astype(f)[None, :]
    zp = np.stack([np.exp((127.0 - idx) * lg[h]) for h in range(4)], axis=1)
    c["c_zeta_p"] = zp.astype(f)
    zs = np.zeros((64, 16, 4), np.float64)
    for h in range(4):
        for b in range(16):
            zs[:, b, h] = np.where(bb == b, np.exp((3.0 - tt) * lg[h]), 0.0)
    c["c_zs"] = zs.reshape(64, 64).astype(f)
    cm = np.zeros((16, 64), f)
    for b in range(16):
        cm[b, 4 * b:4 * b + 4] = 1.0
    c["c_cmask"] = cm.reshape(1, -1)
    _CONSTS = c
    return c


W_NAMES = ["g_mix", "w_in", "lam_re", "lam_im", "log_dt", "b_re", "b_im", "c_re", "c_im", "d_skip", "w_glu",
           "ret_gn", "w_out", "g_xattn", "g_mem", "w_mq", "w_mk", "w_mv", "w_mo", "g_mlp", "w_up", "w_down",
           "g_final"]
W_SHAPES = {"g_mix": [D], "w_in": [D, 2560], "lam_re": [G, 64], "lam_im": [G, 64], "log_dt": [G],
            "b_re": [G, 64, 16], "b_im": [G, 64, 16], "c_re": [G * 16, 64], "c_im": [G * 16, 64], "d_skip": [512],
            "w_glu": [512, 512], "ret_gn": [512], "w_out": [D, D], "g_xattn": [D], "g_mem": [D], "w_mq": [D, D],
            "w_mk": [D, D], "w_mv": [D, D], "w_mo": [D, D], "g_mlp": [D], "w_up": [D, DFF], "w_down": [DFF, D],
            "g_final": [D]}
IN_SHAPES = {"xp": [SEQ, D], "xs": [TS, D], "memp": [MEM, D], "s5r": [512, 64], "s5i": [512, 64],
             "sret": [16, 4, 128, 128], "ck": [16, MEM, D], "cv": [16, MEM, D]}
OUT_SHAPES = {"yp": [SEQ, D], "ys": [TS, D], "o_s5r_p": [G, 64], "o_s5i_p": [G, 64], "o_ret_p": [4, 128, 128],
              "o_mk": [MEM, D], "o_mv": [MEM, D], "o_s5r_s": [512, 64], "o_s5i_s": [512, 64],
              "o_ret_s": [16, 4, 128, 128]}


def build(stage=99, dbg=False):
    nc = bass.Bass("TRN2", target_bir_lowering=False)
    cst = _consts()
    I = {}
    for k, shp in list(IN_SHAPES.items()) + list(W_SHAPES.items()):
        I[k] = nc.dram_tensor(k, shp, F32, kind="ExternalInput").ap()
    for k, v in cst.items():
        I[k] = nc.dram_tensor(k, list(v.shape), F32, kind="ExternalInput").ap()
    O = {}
    for k, shp in OUT_SHAPES.items():
        O[k] = nc.dram_tensor(k, shp, F32, kind="ExternalOutput").ap()
    if dbg:
        O["dbg_ssm"] = nc.dram_tensor("dbg_ssm", [128, 4, NTOK], F32, kind="ExternalOutput").ap()
        O["dbg_x"] = nc.dram_tensor("dbg_x", [128, NT, D], F32, kind="ExternalOutput").ap()

    with ExitStack() as st:
        S = Sched(nc, st)

        def alloc(stack, name, shape, dt=F32):
            return stack.enter_context(nc.sbuf_tensor(name, shape, dt))

        def palloc(stack, name, shape, dt=F32):
            return stack.enter_context(nc.psum_tensor(name, shape, dt))

        def V(fn, r=(), w=()):
            S.op("dve", fn, reads=r, writes=w)

        def A(fn, r=(), w=()):
            S.op("act", fn, reads=r, writes=w)

        import os as _os0
        _nopool = _os0.environ.get("K_NOPOOL") == "1"

        def PL(fn, r=(), w=()):
            S.op("dve" if _nopool else "pool", fn, reads=r, writes=w)

        def T(fn, r=(), w=()):
            S.op("pe", fn, reads=r, writes=w)

        nck = nc.allow_non_contiguous_dma(reason="small param layout loads")
        nck.__enter__()

        identb = alloc(st, "identb", [128, 128], BF16)
        identf = alloc(st, "identf", [128, 128], F32)
        sgn = alloc(st, "sgn", [128, 2])
        epsc = alloc(st, "epsc", [128, 1])
        ssmT = alloc(st, "ssmT", [128, 4, NTOK], BF16)
        b_const = Buf("const")
        b_ssmT = [Buf("ssmT%d" % i) for i in range(5)]
        b_constp = Buf("constp")
        S.dma("pool", identb[:], I["c_ident"][:, :], writes=[b_constp])
        S.dma("sp", identf[:], I["c_ident"][:, :], writes=[b_const])
        S.dma("sp", sgn[:], I["c_sgn"][:, :], writes=[b_const])
        V(lambda e: e.memset(epsc[:], EPS), r=[b_constp], w=[b_const])
        PS = [palloc(st, "ps%d" % i, [128, 512], F32) for i in range(8)]
        bPS = [Buf("ps%d" % i, ps=True) for i in range(8)]

        def ps_bf(i):
            return PS[i][:].bitcast(BF16)

        def make_scr(stack, tag, pbanks):
            d = {"i": 0, "pb": list(pbanks)}
            d["sq"] = [alloc(stack, "sq%s%d" % (tag, i), [128, D], BF16) for i in range(2)]
            d["ss"] = [alloc(stack, "ss%s%d" % (tag, i), [128, 1]) for i in range(2)]
            d["rstd"] = [alloc(stack, "rstd%s%d" % (tag, i), [128, 1]) for i in range(2)]
            d["hb"] = [alloc(stack, "hb%s%d" % (tag, i), [128, D], BF16) for i in range(2)]
            d["ba"] = [Buf("ba%s%d" % (tag, i)) for i in range(2)]
            d["bh"] = [Buf("bh%s%d" % (tag, i)) for i in range(2)]
            return d

        def rmsnorm_hT(xt_ap, bx, npart, gcol, hT_ap, bhT, scr, col0, ph, ln=False, bg=None, out4=None):
            k = scr["i"] % 2
            pbank = scr["pb"][scr["i"] % len(scr["pb"])]
            scr["i"] += 1
            sq, ss, rstd, hb = scr["sq"][k], scr["ss"][k], scr["rstd"][k], scr["hb"][k]
            ba, bh = scr["ba"][k], scr["bh"][k]
            A(lambda e: e.activation(out=sq[:npart, :], in_=xt_ap, func=AF.Square, accum_out=ss[:npart, :]),
              r=[bx], w=[ba])
            if ln:
                A(lambda e: e.activation(out=rstd[:npart, :], in_=ss[:npart, :], func=AF.Ln, scale=1.0 / D,
                                         bias=epsc[:npart, :]), r=[ba, b_const], w=[ba])
                A(lambda e: e.activation(out=rstd[:npart, :], in_=rstd[:npart, :], func=AF.Exp, scale=-0.5),
                  r=[ba], w=[ba])
            else:
                A(lambda e: e.activation(out=rstd[:npart, :], in_=ss[:npart, :], func=AF.Sqrt, scale=1.0 / D,
                                         bias=epsc[:npart, :]), r=[ba, b_const], w=[ba])
                V(lambda e: e.reciprocal(out=rstd[:npart, :], in_=rstd[:npart, :]), r=[ba], w=[ba])
            V(lambda e: e.tensor_scalar(out=hb[:npart, :], in0=xt_ap, scalar1=rstd[:npart, :], scalar2=None,
                                        op0=ALU.mult), r=[bx, ba], w=[bh])
            pv = ps_bf(pbank)
            for kt in range(8):
                T(lambda e, kt=kt: e.transpose(out=pv[:, kt * 128:kt * 128 + npart],
                                               in_=hb[:npart, kt * 128:(kt + 1) * 128],
                                               identity=identb[:npart, :npart]),
                  r=[bh, b_const], w=[bPS[pbank]])
            if out4 is not None:
                V(lambda e: e.tensor_tensor(
                    out=out4, in0=pv.rearrange("p (k c s) -> p k c s", k=8, s=8),
                    in1=gcol.unsqueeze(2).unsqueeze(3).to_broadcast([128, 8, 16, 8]), op=ALU.mult),
                  r=[bPS[pbank], b_const] + ([bg] if bg is not None else []), w=[bhT])
                return
            V(lambda e: e.tensor_tensor(
                out=hT_ap[:, :, col0:col0 + npart],
                in0=pv.rearrange("p (k t) -> p k t", k=8)[:, :, 0:npart],
                in1=gcol.unsqueeze(2).to_broadcast([128, 8, npart]), op=ALU.mult),
              r=[bPS[pbank], b_const] + ([bg] if bg is not None else []), w=[bhT])

        def load_w_bf16(dst, bdst, src, kt_n, ncols, c0=0):
            for kt in range(kt_n):
                for cc in range(0, ncols, 1024):
                    w_ = min(1024, ncols - cc)
                    S.dma("pool", dst[:, kt, cc:cc + w_], src[kt * 128:(kt + 1) * 128, c0 + cc:c0 + cc + w_],
                          writes=[bdst])

        with ExitStack() as sa:
            Wt = alloc(sa, "Wt", [128, G, 128], BF16)
            Wst = alloc(sa, "Wst", [128, G, 128], BF16)
            Tt = alloc(sa, "Tt", [128, G, 128], BF16)
            Vt = alloc(sa, "Vt", [128, G, 128], BF16)
            COSR = alloc(sa, "COSR", [128, G, 64])
            SINR = alloc(sa, "SINR", [128, G, 64])
            masters = alloc(sa, "masters", [128, 8, 240], BF16)
            AR = alloc(sa, "AR", [128, G, K1])
            AI = alloc(sa, "AI", [128, G, K1])
            MAGJ = alloc(sa, "MAGJ", [128, G, K1])
            DS = alloc(sa, "DS", [128, G])
            gm = alloc(sa, "gm", [128, 8])
            winu = alloc(sa, "winu", [128, 8, 512], BF16)
            wglu = alloc(sa, "wglu", [128, 4, 512], BF16)
            b_tab = Buf("s5tab")
            b_winu = Buf("winu", S.GW[0])
            b_wglu = Buf("wglu", S.GW[1])
            b_tabp = Buf("s5tabp")
            S.dma("pool", masters[:], I["c_masters"].rearrange("a k j -> k a j"), writes=[b_tabp])
            S.dma("sp", gm[:], I["g_mix"].rearrange("(k p) -> p k", p=128), writes=[b_tab])
            for tau in range(8):
                S.dma("sp", DS[16 * tau:16 * tau + 16, :], I["d_skip"].rearrange("(g h) -> h g", h=16),
                      writes=[b_tab])
            load_w_bf16(winu, b_winu, I["w_in"], 8, 512, 0)
            load_w_bf16(wglu, b_wglu, I["w_glu"], 4, 512, 0)

            with ExitStack() as s0:
                rows = alloc(s0, "rows", [128, 2 * K1 + 64])
                LR = alloc(s0, "LR", [128, G])
                LI = alloc(s0, "LI", [128, G])
                DT = alloc(s0, "DT", [128, G])
                LRDT = alloc(s0, "LRDT", [128, G])
                LIDT = alloc(s0, "LIDT", [128, G])
                tA = alloc(s0, "tA", [128, G, 64])
                tB = alloc(s0, "tB", [128, G, 64])
                tC = alloc(s0, "tC", [128, G, 64])
                COSJ = alloc(s0, "COSJ", [128, G, K1])
                SINJ = alloc(s0, "SINJ", [128, G, K1])
                sm = alloc(s0, "sm", [128, 12, G])
                Br1 = alloc(s0, "Br1", [128, G, 16])
                Br2 = alloc(s0, "Br2", [128, G, 16])
                BB1 = alloc(s0, "BB1", [128, G, 16])
                BB2 = alloc(s0, "BB2", [128, G, 16])
                tb1 = alloc(s0, "tb1", [128, G, 16])
                big1 = alloc(s0, "big1", [128, G, 128])
                big2 = alloc(s0, "big2", [128, G, 128])
                WTpad = alloc(s0, "WTpad", [128, G, 256], BF16)
                WTs = alloc(s0, "WTs", [128, G, 128], BF16)
                CN1 = alloc(s0, "CN1", [128, 4, 128])
                CN2 = alloc(s0, "CN2", [128, 4, 128])
                CMa = alloc(s0, "CMa", [128, G, 16])
                CMb = alloc(s0, "CMb", [128, G, 16])
                CMab = alloc(s0, "CMab", [128, G, 16], BF16)
                b0 = Buf("p0in")
                bt = Buf("p0tmp")
                S.dma("sp", rows[:], I["c_rows"][0:1, :].partition_broadcast(128), writes=[b0])
                for hf in range(2):
                    S.dma("sp", LR[64 * hf:64 * hf + 64, :], I["lam_re"].rearrange("g p -> p g"), writes=[b0])
                    S.dma("sp", LI[64 * hf:64 * hf + 64, :], I["lam_im"].rearrange("g p -> p g"), writes=[b0])
                S.dma("sp", DT[:], I["log_dt"].rearrange("(o g) -> o g", o=1).partition_broadcast(128), writes=[b0])
                S.dma("sp", Br1[0:64], I["b_re"].rearrange("g p h -> p g h"), writes=[b0])
                S.dma("sp", Br1[64:128], I["b_im"].rearrange("g p h -> p g h"), writes=[b0])
                S.dma("sp", Br2[0:64], I["b_im"].rearrange("g p h -> p g h"), writes=[b0])
                S.dma("sp", Br2[64:128], I["b_re"].rearrange("g p h -> p g h"), writes=[b0])
                S.dma("sp", CN1[:, :, 0:64], I["c_re"].rearrange("(c r) p -> r c p", r=128), writes=[b0])
                S.dma("sp", CN1[:, :, 64:128], I["c_im"].rearrange("(c r) p -> r c p", r=128), writes=[b0])
                S.dma("sp", CN2[:, :, 0:64], I["c_im"].rearrange("(c r) p -> r c p", r=128), writes=[b0])
                S.dma("sp", CN2[:, :, 64:128], I["c_re"].rearrange("(c r) p -> r c p", r=128), writes=[b0])
                MT1 = rows[:, 0:K1]
                MLr = rows[:, K1:2 * K1]
                MRT = rows[:, 2 * K1:2 * K1 + 64]
                A(lambda e: e.activation(out=DT[:], in_=DT[:], func=AF.Exp), r=[b0], w=[b0])
                V(lambda e: e.tensor_tensor(out=LRDT[:], in0=LR[:], in1=DT[:], op=ALU.mult), r=[b0], w=[bt])
                V(lambda e: e.tensor_tensor(out=LIDT[:], in0=LI[:], in1=DT[:], op=ALU.mult), r=[b0], w=[bt])

                def trig(mt_ap, K, cos_out, sin_out):
                    shp = [128, G, K]
                    a_, b_, c_ = tA[:, :, 0:K], tB[:, :, 0:K], tC[:, :, 0:K]
                    V(lambda e: e.tensor_tensor(out=a_, in0=LIDT[:].unsqueeze(2).to_broadcast(shp),
                                                in1=mt_ap.unsqueeze(1).to_broadcast(shp), op=ALU.mult),
                      r=[bt, b0], w=[bt])
                    for (outp, off) in ((sin_out, 0.0), (cos_out, 0.25)):
                        if outp is None:
                            continue
                        V(lambda e, off=off: e.tensor_scalar(out=c_, in0=a_, scalar1=off, scalar2=None,
                                                             op0=ALU.add), r=[bt], w=[bt])
                        V(lambda e: e.tensor_scalar(out=b_, in0=c_, scalar1=MAGIC, scalar2=None, op0=ALU.add),
                          r=[bt], w=[bt])
                        V(lambda e: e.tensor_scalar(out=b_, in0=b_, scalar1=MAGIC, scalar2=None, op0=ALU.subtract),
                          r=[bt], w=[bt])
                        V(lambda e: e.tensor_tensor(out=c_, in0=c_, in1=b_, op=ALU.subtract), r=[bt], w=[bt])
                        A(lambda e, outp=outp: e.activation(out=outp, in_=c_, func=AF.Sin, scale=TWO_PI),
                          r=[bt], w=[b_tab])

                trig(MT1, K1, COSJ[:], SINJ[:])
                trig(MRT, 64, COSR[:], SINR[:])
                shpj = [128, G, K1]
                V(lambda e: e.tensor_tensor(out=MAGJ[:], in0=LRDT[:].unsqueeze(2).to_broadcast(shpj),
                                            in1=MLr.unsqueeze(1).to_broadcast(shpj), op=ALU.mult),
                  r=[bt, b0], w=[b_tab])
                A(lambda e: e.activation(out=MAGJ[:], in_=MAGJ[:], func=AF.Exp), r=[b_tab], w=[b_tab])
                V(lambda e: e.tensor_tensor(out=AR[:], in0=MAGJ[:], in1=COSJ[:], op=ALU.mult), r=[b_tab], w=[b_tab])
                V(lambda e: e.tensor_tensor(out=AI[:], in0=MAGJ[:], in1=SINJ[:], op=ALU.mult), r=[b_tab], w=[b_tab])
                em1, shalf, cm1, am1r, ai1, den, fr, fi, t0_, t1_ = [sm[:, i, :] for i in range(10)]
                x_ = LRDT[:]
                V(lambda e: e.tensor_scalar(out=em1, in0=x_, scalar1=0.2, scalar2=1.0, op0=ALU.mult, op1=ALU.add),
                  r=[bt], w=[bt])
                for cf in (0.25, 1.0 / 3.0, 0.5):
                    V(lambda e: e.tensor_tensor(out=em1, in0=em1, in1=x_, op=ALU.mult), r=[bt], w=[bt])
                    V(lambda e, cf=cf: e.tensor_scalar(out=em1, in0=em1, scalar1=cf, scalar2=1.0, op0=ALU.mult,
                                                       op1=ALU.add), r=[bt], w=[bt])
                V(lambda e: e.tensor_tensor(out=em1, in0=em1, in1=x_, op=ALU.mult), r=[bt], w=[bt])
                V(lambda e: e.tensor_copy(out=shalf, in_=SINJ[:, :, I_HALF]), r=[b_tab], w=[bt])
                V(lambda e: e.scalar_tensor_tensor(out=cm1, in0=shalf, scalar=-2.0, op0=ALU.mult, in1=shalf,
                                                   op1=ALU.mult), r=[bt], w=[bt])
                V(lambda e: e.tensor_tensor(out=am1r, in0=em1, in1=COSJ[:, :, I_A1], op=ALU.mult), r=[bt, b_tab], w=[bt])
                V(lambda e: e.tensor_tensor(out=am1r, in0=am1r, in1=cm1, op=ALU.add), r=[bt], w=[bt])
                V(lambda e: e.tensor_copy(out=ai1, in_=AI[:, :, I_A1]), r=[b_tab], w=[bt])
                V(lambda e: e.tensor_tensor(out=den, in0=LR[:], in1=LR[:], op=ALU.mult), r=[b0], w=[bt])
                V(lambda e: e.tensor_tensor(out=t0_, in0=LI[:], in1=LI[:], op=ALU.mult), r=[b0], w=[bt])
                V(lambda e: e.tensor_tensor(out=den, in0=den, in1=t0_, op=ALU.add), r=[bt], w=[bt])
                V(lambda e: e.reciprocal(out=den, in_=den), r=[bt], w=[bt])
                V(lambda e: e.tensor_tensor(out=fr, in0=am1r, in1=LR[:], op=ALU.mult), r=[bt, b0], w=[bt])
                V(lambda e: e.tensor_tensor(out=t0_, in0=ai1, in1=LI[:], op=ALU.mult), r=[bt, b0], w=[bt])
                V(lambda e: e.tensor_tensor(out=fr, in0=fr, in1=t0_, op=ALU.add), r=[bt], w=[bt])
                V(lambda e: e.tensor_tensor(out=fr, in0=fr, in1=den, op=ALU.mult), r=[bt], w=[bt])
                V(lambda e: e.tensor_tensor(out=fi, in0=ai1, in1=LR[:], op=ALU.mult), r=[bt, b0], w=[bt])
                V(lambda e: e.tensor_tensor(out=t0_, in0=am1r, in1=LI[:], op=ALU.mult), r=[bt, b0], w=[bt])
                V(lambda e: e.tensor_tensor(out=fi, in0=fi, in1=t0_, op=ALU.subtract), r=[bt], w=[bt])
                V(lambda e: e.tensor_tensor(out=fi, in0=fi, in1=den, op=ALU.mult), r=[bt], w=[bt])
                V(lambda e: e.tensor_scalar(out=Br2[:], in0=Br2[:], scalar1=sgn[:, 1:2], scalar2=None, op0=ALU.mult),
                  r=[b0, b_const], w=[b0])
                shb = [128, G, 16]
                frb = fr.unsqueeze(2).to_broadcast(shb)
                fib = fi.unsqueeze(2).to_broadcast(shb)
                V(lambda e: e.tensor_tensor(out=BB1[:], in0=Br1[:], in1=frb, op=ALU.mult), r=[b0, bt], w=[bt])
                V(lambda e: e.tensor_tensor(out=tb1[:], in0=Br2[:], in1=fib, op=ALU.mult), r=[b0, bt], w=[bt])
                V(lambda e: e.tensor_tensor(out=BB1[:], in0=BB1[:], in1=tb1[:], op=ALU.add), r=[bt], w=[bt])
                V(lambda e: e.tensor_tensor(out=BB2[:], in0=Br2[:], in1=frb, op=ALU.mult), r=[b0, bt], w=[bt])
                V(lambda e: e.tensor_tensor(out=tb1[:], in0=Br1[:], in1=fib, op=ALU.mult), r=[b0, bt], w=[bt])
                V(lambda e: e.tensor_tensor(out=BB2[:], in0=BB2[:], in1=tb1[:], op=ALU.subtract), r=[bt], w=[bt])
                sh4 = [128, G, 8, 16]
                arv = AR[:, :, 0:8].unsqueeze(3).to_broadcast(sh4)
                aiv = AI[:, :, 0:8].unsqueeze(3).to_broadcast(sh4)
                bb1 = BB1[:].unsqueeze(2).to_broadcast(sh4)
                bb2 = BB2[:].unsqueeze(2).to_broadcast(sh4)
                g1 = big1[:].rearrange("p g (s h) -> p g s h", s=8)
                g2 = big2[:].rearrange("p g (s h) -> p g s h", s=8)
                V(lambda e: e.memset(WTpad[:], 0.0), w=[bt])
                V(lambda e: e.tensor_tensor(out=g1, in0=arv, in1=bb1, op=ALU.mult), r=[b_tab, bt], w=[bt])
                V(lambda e: e.tensor_tensor(out=g2, in0=aiv, in1=bb2, op=ALU.mult), r=[b_tab, bt], w=[bt])
                V(lambda e: e.tensor_tensor(out=WTpad[:, :, 0:128], in0=big1[:], in1=big2[:], op=ALU.add),
                  r=[bt], w=[bt])
                V(lambda e: e.tensor_tensor(out=g1, in0=arv, in1=bb2, op=ALU.mult), r=[b_tab, bt], w=[bt])
                V(lambda e: e.tensor_tensor(out=g2, in0=aiv, in1=bb1, op=ALU.mult), r=[b_tab, bt], w=[bt])
                V(lambda e: e.tensor_tensor(out=WTs[:], in0=big1[:], in1=big2[:], op=ALU.subtract), r=[bt], w=[bt])
                for (src_fn, dstt) in ((lambda g: WTpad[:, g, 0:128], Wt), (lambda g: WTs[:, g, :], Wst)):
                    for gq in range(8):
                        bank = gq % 2
                        pv = ps_bf(bank)
                        for j in range(4):
                            g = gq * 4 + j
                            T(lambda e, g=g, j=j, pv=pv, src_fn=src_fn: e.transpose(
                                out=pv[:, j * 128:(j + 1) * 128], in_=src_fn(g), identity=identb[:]),
                              r=[bt, b_const], w=[bPS[bank]])
                        A(lambda e, gq=gq, pv=pv, dstt=dstt: e.copy(
                            out=dstt[:, gq * 4:gq * 4 + 4, :], in_=pv[:, 0:512].rearrange("p (j c) -> p j c", j=4)),
                          r=[bPS[bank]], w=[b_tab])
                for (CN, CM, col) in ((CN1, CMa, 0), (CN2, CMb, None)):
                    for c4 in range(4):
                        bank = 2 + (c4 % 2)
                        T(lambda e, CN=CN, c4=c4, bank=bank: e.transpose(out=PS[bank][:, 0:128], in_=CN[:, c4, :],
                                                                         identity=identf[:]),
                          r=[b0, b_const], w=[bPS[bank]])
                        if col is not None:
                            V(lambda e, CM=CM, c4=c4, bank=bank: e.tensor_scalar(
                                out=CM[:, c4 * 8:(c4 + 1) * 8, :],
                                in0=PS[bank][:, 0:128].rearrange("p (g h) -> p g h", g=8),
                                scalar1=sgn[:, 0:1], scalar2=None, op0=ALU.mult),
                              r=[bPS[bank], b_const], w=[bt])
                        else:
                            V(lambda e, CM=CM, c4=c4, bank=bank: e.tensor_scalar(
                                out=CM[:, c4 * 8:(c4 + 1) * 8, :],
                                in0=PS[bank][:, 0:128].rearrange("p (g h) -> p g h", g=8),
                                scalar1=-1.0, scalar2=None, op0=ALU.mult),
                              r=[bPS[bank]], w=[bt])
                V(lambda e: e.tensor_copy(out=CMab[:], in_=CMa[:]), r=[bt], w=[bt])
                afw = AR[:, :, 8:16].unsqueeze(3).to_broadcast(sh4)
                aifw = AI[:, :, 8:16].unsqueeze(3).to_broadcast(sh4)
                cma = CMa[:].unsqueeze(2).to_broadcast(sh4)
                cmb = CMb[:].unsqueeze(2).to_broadcast(sh4)
                V(lambda e: e.tensor_tensor(out=g1, in0=afw, in1=cma, op=ALU.mult), r=[b_tab, bt], w=[bt])
                V(lambda e: e.tensor_tensor(out=g2, in0=aifw, in1=cmb, op=ALU.mult), r=[b_tab, bt], w=[bt])
                V(lambda e: e.tensor_tensor(out=Vt[:], in0=big1[:], in1=big2[:], op=ALU.add), r=[bt], w=[b_tab])
                for gq in range(8):
                    bank = 4 + (gq % 2)
                    for j in range(4):
                        g = gq * 4 + j
                        for tau in range(8):
                            c0 = (7 - tau) * 16
                            T(lambda e, g=g, j=j, tau=tau, c0=c0, bank=bank: e.matmul(
                                PS[bank][:, j * 128 + tau * 16:j * 128 + tau * 16 + 16],
                                lhsT=WTpad[:, g, c0:c0 + 128], rhs=CMab[:, g, :], start=True, stop=True),
                              r=[bt], w=[bPS[bank]])
                    A(lambda e, gq=gq, bank=bank: e.copy(
                        out=Tt[:, gq * 4:gq * 4 + 4, :], in_=PS[bank][:].rearrange("p (j c) -> p j c", j=4)),
                      r=[bPS[bank]], w=[b_tab])
                S.barrier()
            xst = [alloc(sa, "xst%d" % i, [128, D]) for i in range(2)]
            bxst = [Buf("xst%d" % i, S.GL[i]) for i in range(2)]
            scrA = make_scr(sa, "A", [7])
            bscr = Buf("scrA")
            hT2 = [alloc(sa, "hT_%d" % i, [128, 8, 512], BF16) for i in range(2)]
            bhT2 = [Buf("hT_%d" % i) for i in range(2)]
            uT2 = [alloc(sa, "uT_%d" % i, [128, 4, 512], BF16) for i in range(2)]
            buT2 = [Buf("uT_%d" % i) for i in range(2)]
            U = alloc(sa, "U", [128, G, 64], BF16)
            bU = Buf("U")
            rr = alloc(sa, "rr", [128, G, 64])
            rs = alloc(sa, "rs", [128, G, 64])
            ww = alloc(sa, "ww", [128, G, 64])
            ws = alloc(sa, "ws", [128, G, 64])
            tmpr = alloc(sa, "tmpr", [128, 16, 64])
            b_r, b_rs, b_w, b_ws, b_tmpr = Buf("r"), Buf("rs"), Buf("w"), Buf("ws"), Buf("tmpr")
            Xb = alloc(sa, "Xb", [128, G, 65], BF16)
            bXb = Buf("Xb")
            Xc = alloc(sa, "Xc", [128, G])
            Xsc = alloc(sa, "Xsc", [128, G])
            ctmp = alloc(sa, "ctmp", [128, 2, G])
            bXc = Buf("Xc", S.GS[0])
            ytmp = alloc(sa, "ytmp", [128, 8, 64])
            bytmp = Buf("ytmp")
            Zt = alloc(sa, "Zt", [128, G, 64], BF16)
            bZ = Buf("Z")
            zT = alloc(sa, "zT", [128, 4, 512], BF16)
            bzT = Buf("zT")
            sig = alloc(sa, "sig", [128, 4, 512])
            bsig = Buf("sig")
            H0 = alloc(sa, "H0", [128, 512])
            H0s = alloc(sa, "H0s", [128, 512])
            hn = alloc(sa, "hn", [128, 4, 128])
            hn2 = alloc(sa, "hn2", [128, 4, 128])
            Hp = alloc(sa, "Hp", [128, G, 16])
            Xf = alloc(sa, "Xf", [128, G, 16])
            xo = alloc(sa, "xo", [128, 4, 128])
            bH = Buf("H0")
            bxo = Buf("xo", S.GS[1])
            V(lambda e: e.memset(Xc[:], 0.0), r=[b_tabp], w=[bXc, b_tab])
            V(lambda e: e.memset(Xsc[:], 0.0), w=[bXc])
            V(lambda e: e.memset(Xb[:], 0.0), w=[bXb])

            blocks = [(i * 512, 512, False) for i in range(4)] + [(SEQ, TS, True)]
            if _os0.environ.get("K1A") == "0":
                blocks = []
            def p1a_stageA(bi):
                t0, n, is_s = blocks[bi]
                hT, bhT = hT2[bi % 2], bhT2[bi % 2]
                uT, buT = uT2[bi % 2], buT2[bi % 2]
                ntile = (n + 127) // 128
                for ti in range(ntile):
                    npart = min(128, n - ti * 128)
                    slot = (bi * 4 + ti) % 2
                    src = I["xs"][:, :] if is_s else I["xp"][t0 + ti * 128:t0 + ti * 128 + 128, :]
                    S.dma("sp", xst[slot][:npart, :], src, writes=[bxst[slot]])
                    o4 = None if is_s else hT[:, :, :].rearrange("p k (s c) -> p k c s", s=8)[:, :, ti * 16:(ti + 1) * 16, :]
                    rmsnorm_hT(xst[slot][:npart, :], bxst[slot], npart, gm[:], hT, bhT,
                               scrA, ti * 128, None, bg=b_tab, out4=o4)
                for ct in range(4):
                    bank = ct
                    for kt in range(8):
                        T(lambda e, ct=ct, kt=kt, bank=bank: e.matmul(
                            PS[bank][:, 0:n], lhsT=winu[:, kt, ct * 128:(ct + 1) * 128], rhs=hT[:, kt, 0:n],
                            start=(kt == 0), stop=(kt == 7)), r=[b_winu, bhT], w=[bPS[bank]])
                    A(lambda e, ct=ct, bank=bank: e.copy(out=uT[:, ct, 0:n], in_=PS[bank][:, 0:n]),
                      r=[bPS[bank]], w=[buT])

            if blocks:
                p1a_stageA(0)
            for bi, (t0, n, is_s) in enumerate(blocks):
                nch = n // 8 if not is_s else 16
                uT, buT = uT2[bi % 2], buT2[bi % 2]
                for gq in range(4):
                    bank = 4 + (gq % 2)
                    for j in range(8):
                        g = gq * 8 + j
                        ct, gl = g // 8, g % 8
                        if not is_s:
                            uv = uT[:, ct, 0:n].rearrange("p (s c) -> p s c", s=8)
                            sig_list = list(range(8))
                        else:
                            uv = uT[:, ct, 0:n].rearrange("p (b t) -> p t b", t=4)
                            sig_list = [4, 5, 6, 7]
                        for si, sg_ in enumerate(sig_list):
                            rhs = uv[:, sg_ if not is_s else si, :]
                            T(lambda e, j=j, gl=gl, sg_=sg_, rhs=rhs, si=si, bank=bank, L=len(sig_list): e.matmul(
                                PS[bank][:, j * 64:j * 64 + nch],
                                lhsT=masters[:, gl, 112 - 16 * sg_:240 - 16 * sg_], rhs=rhs,
                                start=(si == 0), stop=(si == L - 1)),
                              r=[b_tab, buT], w=[bPS[bank]])
                    A(lambda e, gq=gq, bank=bank: e.copy(
                        out=U[:, gq * 8:gq * 8 + 8, 0:nch],
                        in_=PS[bank][:].rearrange("p (j c) -> p j c", j=8)[:, :, 0:nch]),
                      r=[bPS[bank]], w=[bU])
                if not is_s:
                    for hf in range(2):
                        for j in range(16):
                            g = hf * 16 + j
                            for (wt, bk) in ((Wt, 0), (Wst, 2)):
                                bank = bk + j // 8
                                T(lambda e, g=g, j=j, wt=wt, bank=bank: e.matmul(
                                    PS[bank][:, (j % 8) * 64:(j % 8) * 64 + 64], lhsT=wt[:, g, :], rhs=U[:, g, :],
                                    start=True, stop=True), r=[b_tab, bU], w=[bPS[bank]])
                        for q in range(2):
                            gs = slice(hf * 16 + q * 8, hf * 16 + q * 8 + 8)
                            Sv = PS[q][:].rearrange("p (j c) -> p j c", j=8)
                            Ssv = PS[2 + q][:].rearrange("p (j c) -> p j c", j=8)
                            tm = tmpr[:, q * 8:q * 8 + 8, :]
                            V(lambda e, gs=gs, Sv=Sv: e.tensor_tensor(out=rr[:, gs, :], in0=Sv, in1=COSR[:, gs, :],
                                                                     op=ALU.mult), r=[bPS[q], b_tab], w=[b_r])
                            V(lambda e, gs=gs, Ssv=Ssv, tm=tm: e.tensor_tensor(out=tm, in0=Ssv, in1=SINR[:, gs, :],
                                                                              op=ALU.mult),
                              r=[bPS[2 + q], b_tab], w=[b_tmpr])
                            V(lambda e, gs=gs, tm=tm: e.tensor_tensor(out=rr[:, gs, :], in0=rr[:, gs, :], in1=tm,
                                                                     op=ALU.subtract), r=[b_r, b_tmpr], w=[b_r])
                            V(lambda e, gs=gs, Ssv=Ssv: e.tensor_tensor(out=rs[:, gs, :], in0=Ssv, in1=COSR[:, gs, :],
                                                                       op=ALU.mult), r=[bPS[2 + q], b_tab], w=[b_rs])
                            V(lambda e, gs=gs, Sv=Sv, tm=tm: e.tensor_tensor(out=tm, in0=Sv, in1=SINR[:, gs, :],
                                                                            op=ALU.mult),
                              r=[bPS[q], b_tab], w=[b_tmpr])
                            V(lambda e, gs=gs, tm=tm: e.tensor_tensor(out=rs[:, gs, :], in0=rs[:, gs, :], in1=tm,
                                                                     op=ALU.add), r=[b_rs, b_tmpr], w=[b_rs])
                    for g in range(G):
                        rho = MAGJ[:, g, I_A8:I_A8 + 1].to_broadcast([128, 64])
                        V(lambda e, g=g, rho=rho: e.tensor_tensor_scan(
                            out=ww[:, g, :], data0=rho, data1=rr[:, g, :], initial=Xc[:, g:g + 1], op0=ALU.mult,
                            op1=ALU.add), r=[b_r, b_tab, bXc], w=[b_w])
                        V(lambda e, g=g, rho=rho: e.tensor_tensor_scan(
                            out=ws[:, g, :], data0=rho, data1=rs[:, g, :], initial=Xsc[:, g:g + 1], op0=ALU.mult,
                            op1=ALU.add), r=[b_rs, b_tab, bXc], w=[b_ws])
                    if bi + 1 < len(blocks):
                        p1a_stageA(bi + 1)
                    ce, se_ = COSR[:, :, 63], SINR[:, :, 63]
                    we, wse = ww[:, :, 63], ws[:, :, 63]
                    V(lambda e: e.tensor_tensor(out=ctmp[:, 0, :], in0=ce, in1=we, op=ALU.mult), r=[b_w, b_tab], w=[bscr])
                    V(lambda e: e.tensor_tensor(out=ctmp[:, 1, :], in0=se_, in1=wse, op=ALU.mult), r=[b_ws, b_tab], w=[bscr])
                    V(lambda e: e.tensor_tensor(out=Xc[:], in0=ctmp[:, 0, :], in1=ctmp[:, 1, :], op=ALU.add),
                      r=[bscr], w=[bXc])
                    V(lambda e: e.tensor_tensor(out=ctmp[:, 0, :], in0=ce, in1=wse, op=ALU.mult), r=[b_ws, b_tab], w=[bscr])
                    V(lambda e: e.tensor_tensor(out=ctmp[:, 1, :], in0=se_, in1=we, op=ALU.mult), r=[b_w, b_tab], w=[bscr])
                    V(lambda e: e.tensor_tensor(out=Xsc[:], in0=ctmp[:, 0, :], in1=ctmp[:, 1, :], op=ALU.subtract),
                      r=[bscr], w=[bXc])
                    if bi > 0:
                        V(lambda e: e.tensor_copy(out=Xb[:, :, 0], in_=Xb[:, :, 64]), r=[bXb], w=[bXb])
                    V(lambda e: e.tensor_tensor(out=ww[:], in0=ww[:], in1=COSR[:], op=ALU.mult), r=[b_w, b_tab, bXc],
                      w=[b_w])
                    PL(lambda e: e.tensor_tensor(out=ws[:], in0=ws[:], in1=SINR[:], op=ALU.mult), r=[b_ws, b_tab, bXc],
                       w=[b_ws])
                    V(lambda e: e.tensor_tensor(out=Xb[:, :, 1:65], in0=ww[:], in1=ws[:], op=ALU.add),
                      r=[b_w, b_ws], w=[bXb])
                    xprev = lambda g: Xb[:, g, 0:64]
                    bXprev = bXb
                    if bi == 3:
                        S.dma("sp", O["o_s5r_p"].rearrange("g p -> p g"), Xc[0:64, :], reads=[bXc])
                        S.dma("sp", O["o_s5i_p"].rearrange("g p -> p g"), Xc[64:128, :], reads=[bXc])
                else:
                    S.dma("sp", hn[:, :, 0:64], I["s5r"].rearrange("(j r) p -> r j p", r=128), writes=[bH])
                    S.dma("sp", hn[:, :, 64:128], I["s5i"].rearrange("(j r) p -> r j p", r=128), writes=[bH])
                    S.dma("sp", hn2[:, :, 0:64], I["s5i"].rearrange("(j r) p -> r j p", r=128), writes=[bH])
                    S.dma("sp", hn2[:, :, 64:128], I["s5r"].rearrange("(j r) p -> r j p", r=128), writes=[bH])
                    for (src_, dst_, bank) in ((hn, H0, 0), (hn2, H0s, 1)):
                        for j in range(4):
                            T(lambda e, src_=src_, j=j, bank=bank: e.transpose(
                                out=PS[bank][:, j * 128:(j + 1) * 128], in_=src_[:, j, :], identity=identf[:]),
                              r=[bH, b_const], w=[bPS[bank]])
                        V(lambda e, dst_=dst_, bank=bank: e.tensor_copy(out=dst_[:], in_=PS[bank][:]),
                          r=[bPS[bank]], w=[bH])
                    V(lambda e: e.tensor_scalar(out=H0s[0:64, :], in0=H0s[0:64, :], scalar1=-1.0, scalar2=None,
                                                op0=ALU.mult), r=[bH], w=[bH])
                    shs = [128, G, 16]
                    h0v = H0[:].rearrange("p (b g) -> p g b", g=G)
                    h0sv = H0s[:].rearrange("p (b g) -> p g b", g=G)

                    def abc(tab, idx):
                        return tab[:, :, idx].unsqueeze(2).to_broadcast(shs)
                    V(lambda e: e.tensor_tensor(out=Xf[:], in0=h0v, in1=abc(AR, I_AM4), op=ALU.mult), r=[bH, b_tab], w=[bxo])
                    V(lambda e: e.tensor_tensor(out=Hp[:], in0=h0sv, in1=abc(AI, I_AM4), op=ALU.mult), r=[bH, b_tab], w=[bxo])
                    V(lambda e: e.tensor_tensor(out=Xb[:, :, 0:16], in0=Xf[:], in1=Hp[:], op=ALU.add), r=[bxo], w=[bXb])
                    V(lambda e: e.tensor_tensor(out=Xf[:], in0=h0v, in1=abc(AR, I_A4), op=ALU.mult), r=[bH, b_tab], w=[bxo])
                    V(lambda e: e.tensor_tensor(out=Hp[:], in0=h0sv, in1=abc(AI, I_A4), op=ALU.mult), r=[bH, b_tab], w=[bxo])
                    V(lambda e: e.tensor_tensor(out=Xf[:], in0=Xf[:], in1=Hp[:], op=ALU.add), r=[bxo], w=[bxo])
                    for q in range(4):
                        bank = q % 2
                        for j in range(8):
                            g = q * 8 + j
                            T(lambda e, g=g, j=j, bank=bank: e.matmul(
                                PS[bank][:, j * 64:j * 64 + 16], lhsT=Wt[:, g, :], rhs=U[:, g, 0:16],
                                start=True, stop=True), r=[b_tab, bU], w=[bPS[bank]])
                        V(lambda e, q=q, bank=bank: e.tensor_tensor(
                            out=Xf[:, q * 8:q * 8 + 8, :], in0=Xf[:, q * 8:q * 8 + 8, :],
                            in1=PS[bank][:].rearrange("p (j c) -> p j c", j=8)[:, :, 0:16], op=ALU.add),
                          r=[bxo, bPS[bank]], w=[bxo])
                    Xf2 = Xf[:].rearrange("p g b -> p (g b)")
                    for j in range(4):
                        T(lambda e, j=j: e.transpose(out=PS[2][:, j * 128:(j + 1) * 128],
                                                     in_=Xf2[:, j * 128:(j + 1) * 128], identity=identf[:]),
                          r=[bxo, b_const], w=[bPS[2]])
                    V(lambda e: e.tensor_copy(out=xo[:], in_=PS[2][:].rearrange("p (j c) -> p j c", j=4)),
                      r=[bPS[2]], w=[bxo])
                    for j in range(4):
                        for gl in range(8):
                            for (nm, c0) in (("o_s5r_s", 0), ("o_s5i_s", 64)):
                                S.dma("sp", O[nm].rearrange("(b g) p -> g b p", g=G)[8 * j + gl],
                                      xo[gl * 16:gl * 16 + 16, j, c0:c0 + 64], reads=[bxo])
                    xprev = lambda g: Xb[:, g, 0:16]
                    bXprev = bXb
                for gq in range(4):
                    bank = 6 + (gq % 2)
                    for j in range(8):
                        g = gq * 8 + j
                        T(lambda e, g=g, j=j, bank=bank: e.matmul(
                            PS[bank][:, j * 64:j * 64 + nch], lhsT=Tt[:, g, :], rhs=U[:, g, 0:nch],
                            start=True, stop=False), r=[b_tab, bU], w=[bPS[bank]])
                        T(lambda e, g=g, j=j, bank=bank: e.matmul(
                            PS[bank][:, j * 64:j * 64 + nch], lhsT=Vt[:, g, :], rhs=xprev(g)[:, 0:nch],
                            start=False, stop=True), r=[b_tab, bXprev], w=[bPS[bank]])
                    gs = slice(gq * 8, gq * 8 + 8)
                    yv = PS[bank][:].rearrange("p (j c) -> p j c", j=8)[:, :, 0:nch]
                    V(lambda e, gs=gs: e.tensor_tensor(out=ytmp[:, :, 0:nch], in0=U[:, gs, 0:nch],
                                                       in1=DS[:, gs].unsqueeze(2).to_broadcast([128, 8, nch]),
                                                       op=ALU.mult), r=[bU, b_tab], w=[bytmp])
                    V(lambda e, yv=yv: e.tensor_tensor(out=ytmp[:, :, 0:nch], in0=yv, in1=ytmp[:, :, 0:nch],
                                                       op=ALU.add), r=[bPS[bank], bytmp], w=[bytmp])
                    A(lambda e, gs=gs: e.activation(out=Zt[:, gs, 0:nch], in_=ytmp[:, :, 0:nch],
                                                    func=AF.Gelu_apprx_tanh), r=[bytmp], w=[bZ])
                for ct in range(4):
                    bank = ct % 2
                    taus = list(range(8)) if not is_s else [4, 5, 6, 7]
                    for ti_, tau in enumerate(taus):
                        for gl in range(8):
                            g = ct * 8 + gl
                            T(lambda e, g=g, gl=gl, tau=tau, ti_=ti_, bank=bank: e.matmul(
                                PS[bank][:, ti_ * 64:ti_ * 64 + nch],
                                lhsT=masters[:, tau, 112 - 16 * gl:240 - 16 * gl], rhs=Zt[:, g, 0:nch],
                                start=(gl == 0), stop=(gl == 7)), r=[b_tab, bZ], w=[bPS[bank]])
                    if not is_s:
                        A(lambda e, ct=ct, bank=bank: e.copy(
                            out=zT[:, ct, 0:n].rearrange("p (c t) -> p t c", t=8),
                            in_=PS[bank][:].rearrange("p (t c) -> p t c", t=8)), r=[bPS[bank]], w=[bzT])
                    else:
                        A(lambda e, ct=ct, bank=bank: e.copy(
                            out=zT[:, ct, 0:n].rearrange("p (b t) -> p t b", t=4),
                            in_=PS[bank][:].rearrange("p (t c) -> p t c", t=8)[:, 0:4, 0:16]),
                          r=[bPS[bank]], w=[bzT])
                for ct in range(4):
                    bank = 2 + (ct % 2)
                    for kt in range(4):
                        T(lambda e, ct=ct, kt=kt, bank=bank: e.matmul(
                            PS[bank][:, 0:n], lhsT=wglu[:, kt, ct * 128:(ct + 1) * 128], rhs=zT[:, kt, 0:n],
                            start=(kt == 0), stop=(kt == 3)), r=[b_wglu, bzT], w=[bPS[bank]])
                    A(lambda e, ct=ct, bank=bank: e.activation(out=sig[:, ct, 0:n], in_=PS[bank][:, 0:n],
                                                               func=AF.Sigmoid), r=[bPS[bank]], w=[bsig])
                V(lambda e: e.tensor_tensor(out=ssmT[:, :, t0:t0 + n], in0=zT[:, :, 0:n], in1=sig[:, :, 0:n],
                                            op=ALU.mult), r=[bzT, bsig], w=[b_ssmT[bi]])
            S.barrier()
        if dbg:
            with ExitStack() as sd:
                dtmp = alloc(sd, "dtmp", [128, 4, NTOK])
                bd = Buf("dtmp", S.GS[2])
                V(lambda e: e.tensor_copy(out=dtmp[:], in_=ssmT[:]), r=b_ssmT, w=[bd])
                S.dma("sp", O["dbg_ssm"][:, :, :], dtmp[:], reads=[bd])
                S.barrier()
        if stage <= 1:
            S.barrier()
            S.run_block()
            nck.__exit__(None, None, None)
            return nc

        with ExitStack() as sbx:
            x = alloc(sbx, "x", [128, NT, D])
            bx = [Buf("x%d" % n, S.GX) for n in range(NT)]
            for n in range(NTP):
                S.dma("sp", x[:, n, :], I["xp"][n * 128:(n + 1) * 128, :], writes=[bx[n]])
            S.dma("sp", x[0:TS, 16, :], I["xs"][:, :], writes=[bx[16]])
            scrB = make_scr(sbx, "B", [7])
            hT1 = alloc(sbx, "hT1", [128, 8, 128], BF16)
            bhT1 = Buf("hT1")

            def resid_add(n, npart, half, bank):
                V(lambda e: e.tensor_tensor(out=x[:npart, n, half * 512:(half + 1) * 512], in0=PS[bank][:npart, :],
                                            in1=x[:npart, n, half * 512:(half + 1) * 512], op=ALU.add),
                  r=[bPS[bank], bx[n]], w=[bx[n]])

            with ExitStack() as s1:
                wq = alloc(s1, "wqkvg", [128, 8, 2048], BF16)
                wout = alloc(s1, "wout", [128, 8, D], BF16)
                b_wqc = [Buf("wq%d" % c, S.GW[c]) for c in range(4)]
                b_wout = Buf("wout", S.GW[0])
                for c in range(4):
                    for kt in range(8):
                        S.dma("pool", wq[:, kt, c * 512:(c + 1) * 512],
                              I["w_in"][kt * 128:(kt + 1) * 128, 512 + c * 512:512 + (c + 1) * 512], writes=[b_wqc[c]])
                wout_loaded = [False]
                gm2 = alloc(s1, "gm2", [128, 8])
                gn = alloc(s1, "gn", [128, 4])
                rope = alloc(s1, "rope", [128, 3, NT, 64])
                dmp = alloc(s1, "dmp", [128, 512])
                dms = alloc(s1, "dms", [64, 256])
                xi = alloc(s1, "xi", [128, 768])
                zetap = alloc(s1, "zetap", [128, 4])
                zs = alloc(s1, "zs", [64, 64])
                cmask = alloc(s1, "cmask", [128, 16 * 64])
                b_t1 = Buf("tab1")
                S.dma("sp", gm2[:], I["g_mix"].rearrange("(k p) -> p k", p=128), writes=[b_t1])
                S.dma("sp", gn[:], I["ret_gn"].rearrange("(k p) -> p k", p=128), writes=[b_t1])
                for a_ in range(3):
                    S.dma("sp", rope[:, a_, :, :], I["c_rope"][a_], writes=[b_t1])
                S.dma("sp", dmp[:], I["c_dmask_p"][:, :], writes=[b_t1])
                S.dma("sp", dms[:], I["c_dmask_s"][:, :], writes=[b_t1])
                S.dma("sp", xi[:], I["c_xi"][0:1, :].partition_broadcast(128), writes=[b_t1])
                S.dma("sp", zetap[:], I["c_zeta_p"][:, :], writes=[b_t1])
                S.dma("sp", zs[:], I["c_zs"][:, :], writes=[b_t1])
                S.dma("sp", cmask[:], I["c_cmask"][0:1, :].partition_broadcast(128), writes=[b_t1])
                def load_wout():
                    load_w_bf16(wout, b_wout, I["w_out"], 8, D, 0)
                    for k in range(4):
                        V(lambda e: e.tensor_scalar(out=wout[:, 4 + k, :], in0=wout[:, 4 + k, :], scalar1=gn[:, k:k + 1],
                                                    scalar2=None, op0=ALU.mult), r=[b_wout, b_t1], w=[b_wout])
                    wout_loaded[0] = True
                t1q = alloc(s1, "t1q", [128, 512])
                t2q = alloc(s1, "t2q", [128, 512])
                t1k = alloc(s1, "t1k", [128, 512])
                t2k = alloc(s1, "t2k", [128, 512])
                qr = alloc(s1, "qr", [128, 512], BF16)
                kr = alloc(s1, "kr", [128, 512], BF16)
                qT = alloc(s1, "qT", [128, 4, 128], BF16)
                qxT = alloc(s1, "qxT", [128, 4, 128], BF16)
                kT = alloc(s1, "kT", [128, 4, 128], BF16)
                vb = alloc(s1, "vb", [128, 512], BF16)
                vz = alloc(s1, "vz", [128, 512], BF16)
                sg_ = alloc(s1, "sgl", [128, 512])
                sT = alloc(s1, "sT", [128, 4, 128], BF16)
                Sst = alloc(s1, "Sst", [128, 4, 128])
                Sbf = alloc(s1, "Sbf", [128, 4, 128], BF16)
                stats = alloc(s1, "stats", [128, 4, 6])
                mv = alloc(s1, "mv", [128, 4, 2])
                rs4 = alloc(s1, "rs4", [128, 4])
                nb4 = alloc(s1, "nb4", [128, 4])
                on = alloc(s1, "on", [128, 512])
                ret = alloc(s1, "ret", [128, 512], BF16)
                retT = alloc(s1, "retT", [128, 4, 128], BF16)
                S0 = [alloc(s1, "S0_%d" % i, [128, 4, 128]) for i in range(2)]
                S0b = [alloc(s1, "S0b_%d" % i, [128, 4, 128], BF16) for i in range(2)]
                qxm = [alloc(s1, "qxm_%d" % i, [128, 4, 64], BF16) for i in range(2)]
                vzb = [alloc(s1, "vzb_%d" % i, [64, 512], BF16) for i in range(2)]
                Sn = [alloc(s1, "Sn_%d" % i, [128, 4, 128]) for i in range(2)]
                bS0 = [Buf("S0_%d" % i, S.GL[i]) for i in range(2)]
                bS0b = [Buf("S0b_%d" % i) for i in range(2)]
                bqxm = [Buf("qxm%d" % i) for i in range(2)]
                bvzb = [Buf("vzb%d" % i) for i in range(2)]
                bSn = [Buf("Sn%d" % i, S.GS[i]) for i in range(2)]
                (b_t1q, b_t2q, b_t1k, b_t2k, b_qr, b_kr, b_qT, b_qxT, b_kT, b_vb, b_vz, b_sg, b_sT, b_Sst, b_Sbf,
                 b_st, b_on, b_ret, b_retT) = [Buf("p1b%d" % i) for i in range(19)]
                b_Sst.grp = S.GS[2]
                V(lambda e: e.memset(Sst[:], 0.0), w=[b_Sst])
                GC_P = [float(g ** 128) for g in GAM]
                GC_S = [float(g ** 4) for g in GAM]

                import os as _os
                _tl = _os.environ.get("K_TILES")
                _tiles = [int(v) for v in _tl.split(",") if int(v) >= 0] if _tl else list(range(NT))
                _step = int(_os.environ.get("K_STEP", "99"))
                hT1s = [hT1, alloc(s1, "hT1c", [128, 8, 128], BF16)]
                bhT1s = [bhT1, Buf("hT1c")]

                def p1b_norm(n):
                    npt_ = TS if n == 16 else 128
                    rmsnorm_hT(x[:npt_, n, :], bx[n], npt_, gm2[:], hT1s[n % 2], bhT1s[n % 2], scrB, 0, None, bg=b_t1)
                def p1b_proj(n):
                    npt_ = TS if n == 16 else 128
                    hTn, bhTn = hT1s[n % 2], bhT1s[n % 2]
                    for c in range(4):
                        for kt in range(8):
                            T(lambda e: e.matmul(PS[c][:npt_, :], lhsT=hTn[:, kt, 0:npt_],
                                                 rhs=wq[:, kt, c * 512:(c + 1) * 512], start=(kt == 0), stop=(kt == 7)),
                              r=[bhTn, b_wqc[c]], w=[bPS[c]])
                if _tiles:
                    p1b_norm(_tiles[0])
                    p1b_proj(_tiles[0])
                    load_wout()
                for ti_, n in enumerate(_tiles):
                    is_s = (n == 16)
                    npt = TS if is_s else 128
                    tok0 = n * 128
                    hT1, bhT1 = hT1s[n % 2], bhT1s[n % 2]
                    pob = [4, 6, 7, 1] if is_s else [4, 4, 4, 4]

                    def po(h):
                        if is_s:
                            return PS[pob[h]][:npt, 0:128]
                        return PS[4][:npt, h * 128:(h + 1) * 128]
                    if _step <= 1:
                        continue
                    for (bank, t1_, t2_, out_, bt1, bt2, bo) in ((0, t1q, t2q, qr, b_t1q, b_t2q, b_qr),
                                                               (1, t1k, t2k, kr, b_t1k, b_t2k, b_kr)):
                        pv4 = PS[bank][:npt, :].rearrange("p (h a j) -> p h a j", h=4, a=2)
                        t1v = t1_[:npt, :].rearrange("p (h a j) -> p h a j", h=4, a=2)
                        t2v = t2_[:npt, :].rearrange("p (h a j) -> p h a j", h=4, a=2)
                        cosb = rope[:npt, 0, n, :].unsqueeze(1).unsqueeze(1).to_broadcast([npt, 4, 2, 64])
                        sinb = rope[:npt, 1, n, :].unsqueeze(1).to_broadcast([npt, 4, 64])
                        nsinb = rope[:npt, 2, n, :].unsqueeze(1).to_broadcast([npt, 4, 64])
                        V(lambda e: e.tensor_tensor(out=t1v, in0=pv4, in1=cosb, op=ALU.mult), r=[bPS[bank], b_t1], w=[bt1])
                        V(lambda e: e.tensor_tensor(out=t2v[:, :, 0, :], in0=pv4[:, :, 1, :], in1=nsinb, op=ALU.mult),
                          r=[bPS[bank], b_t1], w=[bt2])
                        V(lambda e: e.tensor_tensor(out=t2v[:, :, 1, :], in0=pv4[:, :, 0, :], in1=sinb, op=ALU.mult),
                          r=[bPS[bank], b_t1], w=[bt2])
                        V(lambda e: e.tensor_tensor(out=out_[:npt, :], in0=t1_[:npt, :], in1=t2_[:npt, :], op=ALU.add),
                           r=[bt1, bt2], w=[bo])
                    if _step <= 2:
                        continue
                    A(lambda e: e.copy(out=vb[:npt, :], in_=PS[2][:npt, :]), r=[bPS[2]], w=[b_vb])
                    if not is_s:
                        V(lambda e: e.tensor_tensor(
                            out=vz[:, :].rearrange("p (h e) -> p h e", h=4),
                            in0=PS[2][:, :].rearrange("p (h e) -> p h e", h=4),
                            in1=zetap[:, :].unsqueeze(2).to_broadcast([128, 4, 128]), op=ALU.mult),
                          r=[bPS[2], b_t1], w=[b_vz])
                    A(lambda e: e.activation(out=sg_[:npt, :], in_=PS[3][:npt, :], func=AF.Silu), r=[bPS[3]], w=[b_sg])
                    pv4b = ps_bf(4)
                    pv5b = ps_bf(5)
                    for h in range(4):
                        T(lambda e: e.transpose(out=pv4b[:, h * 128:h * 128 + npt], in_=qr[:npt, h * 128:(h + 1) * 128],
                                                identity=identb[:npt, :npt]), r=[b_qr, b_const], w=[bPS[4]])
                    for h in range(4):
                        T(lambda e: e.transpose(out=pv5b[:, h * 128:h * 128 + npt], in_=kr[:npt, h * 128:(h + 1) * 128],
                                                identity=identb[:npt, :npt]), r=[b_kr, b_const], w=[bPS[5]])
                    q4 = pv4b[:, 0:512].rearrange("p (h t) -> p h t", h=4)[:, :, 0:npt]
                    k4 = pv5b[:, 0:512].rearrange("p (h t) -> p h t", h=4)[:, :, 0:npt]
                    A(lambda e: e.copy(out=qT[:, :, 0:npt], in_=q4), r=[bPS[4]], w=[b_qT])
                    xiv = (xi[:, 0:512].rearrange("p (h t) -> p h t", h=4) if not is_s
                           else xi[:, 512:768].rearrange("p (h t) -> p h t", h=4))
                    V(lambda e: e.tensor_tensor(out=qxT[:, :, 0:npt], in0=q4, in1=xiv, op=ALU.mult),
                      r=[bPS[4], b_t1], w=[b_qxT])
                    A(lambda e: e.copy(out=kT[:, :, 0:npt], in_=k4), r=[bPS[5]], w=[b_kT])
                    if _step <= 3:
                        continue
                    for h in range(4):
                        T(lambda e: e.matmul(PS[6][:npt, h * 128:h * 128 + npt], lhsT=kT[:, h, 0:npt], rhs=qT[:, h, 0:npt],
                                             start=True, stop=True), r=[b_kT, b_qT], w=[bPS[6]])
                    dmv = (dmp[:, :].rearrange("p (h t) -> p h t", h=4) if not is_s
                           else dms[:, :].rearrange("p (h t) -> p h t", h=4))
                    V(lambda e: e.tensor_tensor(out=sT[:npt, :, 0:npt],
                                                in0=PS[6][:npt, :].rearrange("p (h t) -> p h t", h=4)[:, :, 0:npt],
                                                in1=dmv, op=ALU.mult), r=[bPS[6], b_t1], w=[b_sT])
                    if _step <= 4:
                        continue
                    if ti_ + 1 < len(_tiles):
                        p1b_norm(_tiles[ti_ + 1])
                    for h in range(4):
                        only = (n == 0)
                        T(lambda e: e.matmul(po(h), lhsT=sT[:npt, h, 0:npt],
                                             rhs=vb[:npt, h * 128:(h + 1) * 128], start=True, stop=only),
                          r=[b_sT, b_vb], w=[bPS[pob[h]]])
                        if (not is_s) and n > 0:
                            T(lambda e: e.matmul(po(h), lhsT=qxT[:, h, 0:npt],
                                                 rhs=Sbf[:, h, :], start=False, stop=True),
                              r=[b_qxT, b_Sbf], w=[bPS[4]])
                    if not is_s:
                        for h in range(4):
                            T(lambda e: e.matmul(PS[5][:, h * 128:(h + 1) * 128], lhsT=kr[:, h * 128:(h + 1) * 128],
                                                 rhs=vz[:, h * 128:(h + 1) * 128], start=True, stop=True),
                              r=[b_kr, b_vz], w=[bPS[5]])
                        for h in range(4):
                            V(lambda e: e.scalar_tensor_tensor(out=Sst[:, h, :], in0=Sst[:, h, :], scalar=GC_P[h],
                                                               op0=ALU.mult, in1=PS[5][:, h * 128:(h + 1) * 128],
                                                               op1=ALU.add), r=[b_Sst, bPS[5]], w=[b_Sst])
                        A(lambda e: e.copy(out=Sbf[:], in_=Sst[:]), r=[b_Sst], w=[b_Sbf])
                        if n == NTP - 1:
                            S.dma("sp", O["o_ret_p"].rearrange("h d e -> d h e"), Sst[:], reads=[b_Sst])
                    else:
                        for b in range(16):
                            sl = b % 2
                            S.dma("sp", S0[sl][:], I["sret"][b].rearrange("h d e -> d h e"), writes=[bS0[sl]])
                            A(lambda e: e.copy(out=S0b[sl][:], in_=S0[sl][:]), r=[bS0[sl]], w=[bS0b[sl]])
                            V(lambda e: e.tensor_tensor(
                                out=qxm[sl][:], in0=qxT[:, :, 0:64],
                                in1=cmask[:, b * 64:(b + 1) * 64].unsqueeze(1).to_broadcast([128, 4, 64]), op=ALU.mult),
                              r=[b_qxT, b_t1], w=[bqxm[sl]])
                            for h in range(4):
                                T(lambda e: e.matmul(po(h), lhsT=qxm[sl][:, h, :],
                                                     rhs=S0b[sl][:, h, :], start=False, stop=(b == 15)),
                                  r=[bqxm[sl], bS0b[sl]], w=[bPS[pob[h]]])
                            V(lambda e: e.tensor_tensor(
                                out=vzb[sl][:, :].rearrange("p (h e) -> p h e", h=4),
                                in0=PS[2][:64, :].rearrange("p (h e) -> p h e", h=4),
                                in1=zs[:, b * 4:(b + 1) * 4].unsqueeze(2).to_broadcast([64, 4, 128]), op=ALU.mult),
                              r=[bPS[2], b_t1], w=[bvzb[sl]])
                            kvb = 5 if sl == 0 else 0
                            for h in range(4):
                                T(lambda e: e.matmul(PS[kvb][:, h * 128:(h + 1) * 128], lhsT=kr[:64, h * 128:(h + 1) * 128],
                                                     rhs=vzb[sl][:, h * 128:(h + 1) * 128], start=True, stop=True),
                                  r=[b_kr, bvzb[sl]], w=[bPS[kvb]])
                            for h in range(4):
                                V(lambda e: e.scalar_tensor_tensor(out=Sn[sl][:, h, :], in0=S0[sl][:, h, :], scalar=GC_S[h],
                                                                   op0=ALU.mult, in1=PS[kvb][:, h * 128:(h + 1) * 128],
                                                                   op1=ALU.add), r=[bS0[sl], bPS[kvb]], w=[bSn[sl]])
                            S.dma("sp", O["o_ret_s"][b].rearrange("h d e -> d h e"), Sn[sl][:], reads=[bSn[sl]])
                    if _step <= 5:
                        continue
                    if ti_ + 1 < len(_tiles):
                        p1b_proj(_tiles[ti_ + 1])
                    for h in range(4):
                        V(lambda e: e.bn_stats(out=stats[:npt, h, :], in_=po(h)),
                          r=[bPS[pob[h]]], w=[b_st])
                    for h in range(4):
                        V(lambda e: e.bn_aggr(out=mv[:npt, h, :], in_=stats[:npt, h, :]), r=[b_st], w=[b_st])
                    A(lambda e: e.activation(out=rs4[:npt, :], in_=mv[:npt, :, 1], func=AF.Sqrt, scale=1.0,
                                             bias=epsc[:npt, :]), r=[b_st, b_const], w=[b_st])
                    V(lambda e: e.reciprocal(out=rs4[:npt, :], in_=rs4[:npt, :]), r=[b_st], w=[b_st])
                    V(lambda e: e.scalar_tensor_tensor(out=nb4[:npt, :], in0=mv[:npt, :, 0], scalar=-1.0, op0=ALU.mult,
                                                       in1=rs4[:npt, :], op1=ALU.mult), r=[b_st], w=[b_st])
                    for h in range(4):
                        A(lambda e: e.activation(out=on[:npt, h * 128:(h + 1) * 128], in_=po(h),
                                                 func=AF.Identity, scale=rs4[:npt, h:h + 1], bias=nb4[:npt, h:h + 1]),
                          r=[bPS[pob[h]], b_st], w=[b_on])
                    V(lambda e: e.tensor_tensor(out=ret[:npt, :], in0=on[:npt, :], in1=sg_[:npt, :], op=ALU.mult),
                       r=[b_on, b_sg], w=[b_ret])
                    if _step <= 6:
                        continue
                    pv6b = ps_bf(6)
                    for h in range(4):
                        T(lambda e: e.transpose(out=pv6b[:, h * 128:h * 128 + npt], in_=ret[:npt, h * 128:(h + 1) * 128],
                                                identity=identb[:npt, :npt]), r=[b_ret, b_const], w=[bPS[6]])
                    A(lambda e: e.copy(out=retT[:, :, 0:npt],
                                       in_=pv6b[:, 0:512].rearrange("p (h t) -> p h t", h=4)[:, :, 0:npt]),
                      r=[bPS[6]], w=[b_retT])
                    if _step <= 7:
                        continue
                    bi_ = min(n // 4, 4)
                    for half in range(2):
                        bank = 6 + half
                        for kt in range(8):
                            lh = ssmT[:, kt, tok0:tok0 + npt] if kt < 4 else retT[:, kt - 4, 0:npt]
                            T(lambda e: e.matmul(PS[bank][:npt, :], lhsT=lh, rhs=wout[:, kt, half * 512:(half + 1) * 512],
                                                 start=(kt == 0), stop=(kt == 7)),
                              r=[b_ssmT[bi_], b_retT, b_wout], w=[bPS[bank]])
                        resid_add(n, npt, half, bank)
                S.barrier()
            if dbg:
                for n in range(NT):
                    S.dma("sp", O["dbg_x"][:, n, :], x[:, n, :], reads=[bx[n]])
            if stage <= 2:
                S.barrier()
                S.run_block()
                nck.__exit__(None, None, None)
                return nc

            with ExitStack() as s2:
                gx = alloc(s2, "gx", [128, 8])
                gmem = alloc(s2, "gmem", [128, 8])
                ones = alloc(s2, "ones", [128, 128], BF16)
                b_t2 = Buf("tab2")
                S.dma("sp", gx[:], I["g_xattn"].rearrange("(k p) -> p k", p=128), writes=[b_t2])
                S.dma("sp", gmem[:], I["g_mem"].rearrange("(k p) -> p k", p=128), writes=[b_t2])
                V(lambda e: e.memset(ones[:], 1.0), w=[b_t2])
                KT = alloc(s2, "KT", [128, 8, MEM], BF16)
                Vm = alloc(s2, "Vm", [128, 2, D], BF16)
                b_KT, b_Vm = Buf("KT"), Buf("Vm")
                wmq = alloc(s2, "wmq", [128, 8, D], BF16)
                b_wmq, b_wmo = Buf("wmq", S.GW[2]), Buf("wmo", S.GW[3])
                with ExitStack() as s2a:
                    wmk = alloc(s2a, "wmk", [128, 8, D], BF16)
                    wmv = alloc(s2a, "wmv", [128, 8, D], BF16)
                    b_wmk, b_wmv = Buf("wmk", S.GW[0]), Buf("wmv", S.GW[1])
                    load_w_bf16(wmk, b_wmk, I["w_mk"], 8, D, 0)
                    load_w_bf16(wmv, b_wmv, I["w_mv"], 8, D, 0)
                    load_w_bf16(wmq, b_wmq, I["w_mq"], 8, D, 0)
                    mx = [alloc(s2a, "mx%d" % i, [128, D]) for i in range(2)]
                    bmx = [Buf("mx%d" % i, S.GL[i]) for i in range(2)]
                    mhT = alloc(s2a, "mhT", [128, 8, MEM], BF16)
                    b_mhT = Buf("mhT")
                    mo = [alloc(s2a, "mo%d" % i, [128, D]) for i in range(2)]
                    bmo = [Buf("mo%d" % i, S.GS[i]) for i in range(2)]
                    _k2a = int(_os.environ.get("K2A", "9"))
                    for mt in range(2):
                        S.dma("sp", mx[mt][:], I["memp"][mt * 128:(mt + 1) * 128, :], writes=[bmx[mt]])
                        if _k2a >= 1:
                            rmsnorm_hT(mx[mt][:, :], bmx[mt], 128, gmem[:], mhT, b_mhT, scrB, mt * 128, None,
                                       ln=True, bg=b_t2)
                    oi = 0
                    for (wm, bwm, oname, isv) in ((wmk, b_wmk, "o_mk", False), (wmv, b_wmv, "o_mv", True)) if _k2a >= 2 else ():
                        for mt in range(2):
                            sl = oi % 2
                            oi += 1
                            for half in range(2):
                                bank = half
                                for kt in range(8):
                                    T(lambda e: e.matmul(PS[bank][:, :], lhsT=mhT[:, kt, mt * 128:(mt + 1) * 128],
                                                         rhs=wm[:, kt, half * 512:(half + 1) * 512], start=(kt == 0),
                                                         stop=(kt == 7)), r=[b_mhT, bwm], w=[bPS[bank]])
                                A(lambda e: e.copy(out=mo[sl][:, half * 512:(half + 1) * 512], in_=PS[bank][:, :]),
                                  r=[bPS[bank]], w=[bmo[sl]])
                                if isv:
                                    V(lambda e: e.tensor_copy(out=Vm[:, mt, half * 512:(half + 1) * 512], in_=PS[bank][:, :]),
                                      r=[bPS[bank]], w=[b_Vm])
                            S.dma("sp", O[oname][mt * 128:(mt + 1) * 128, :], mo[sl][:], reads=[bmo[sl]])
                    for j in range(8 if _k2a >= 3 else 0):
                        bank = 2 + (j % 2)
                        for kt in range(8):
                            T(lambda e: e.matmul(PS[bank][:, 0:MEM], lhsT=wmk[:, kt, j * 128:(j + 1) * 128],
                                                 rhs=mhT[:, kt, :], start=(kt == 0), stop=(kt == 7)),
                              r=[b_mhT, b_wmk], w=[bPS[bank]])
                        A(lambda e: e.copy(out=KT[:, j, :], in_=PS[bank][:, 0:MEM]), r=[bPS[bank]], w=[b_KT])
                    S.barrier()
                wmo = alloc(s2, "wmo", [128, 8, D], BF16)
                load_w_bf16(wmo, b_wmo, I["w_mo"], 8, D, 0)
                hT4 = alloc(s2, "hT4", [128, 8, 512], BF16)
                qm4 = alloc(s2, "qm4", [128, 8, 512], BF16)
                oT4 = alloc(s2, "oT4", [128, 8, 512], BF16)
                eT4 = [alloc(s2, "eT4_%d" % i, [128, 2, 512], BF16) for i in range(2)]
                rdn4 = [alloc(s2, "rdn4_%d" % i, [128, 512]) for i in range(2)]
                b_hT4, b_qm4, b_oT4 = Buf("hT4"), Buf("qm4"), Buf("oT4")
                b_eT4 = [Buf("eT4_%d" % i) for i in range(2)]
                b_rdn4 = [Buf("rdn4_%d" % i) for i in range(2)]
                Kb = [alloc(s2, "Kb%d" % i, [128, 2, D]) for i in range(2)]
                bKb = [Buf("Kb%d" % i, S.GL[i]) for i in range(2)]
                KbT = [alloc(s2, "KbT%d" % i, [128, 8, MEM], BF16) for i in range(2)]
                bKbT = [Buf("KbT%d" % i) for i in range(2)]
                Vb = [alloc(s2, "Vb%d" % i, [128, 2, D], BF16) for i in range(2)]
                bVb = [Buf("Vb%d" % i, S.GW[i]) for i in range(2)]
                eTs = alloc(s2, "eTs", [128, 2, 4, 64], BF16)
                b_eTs = Buf("eTs")
                qrot = [0]

                def q_proj(nc_):
                    for j in range(8):
                        bank = 5 + (qrot[0] % 3)
                        qrot[0] += 1
                        for kt in range(8):
                            T(lambda e: e.matmul(PS[bank][:, 0:nc_], lhsT=wmq[:, kt, j * 128:(j + 1) * 128],
                                                 rhs=hT4[:, kt, 0:nc_], start=(kt == 0), stop=(kt == 7)),
                              r=[b_wmq, b_hT4], w=[bPS[bank]])
                        A(lambda e: e.activation(out=qm4[:, j, 0:nc_], in_=PS[bank][:, 0:nc_], func=AF.Copy,
                                                 scale=1.0 / 16.0), r=[bPS[bank]], w=[b_qm4])

                def w_mo_resid(n, npt, c0):
                    for half in range(2):
                        bank = 5 + (qrot[0] % 3)
                        qrot[0] += 1
                        for j in range(8):
                            T(lambda e: e.matmul(PS[bank][:npt, :], lhsT=oT4[:, j, c0:c0 + npt],
                                                 rhs=wmo[:, j, half * 512:(half + 1) * 512], start=(j == 0), stop=(j == 7)),
                              r=[b_oT4, b_wmo], w=[bPS[bank]])
                        resid_add(n, npt, half, bank)

                for bi in range(4):
                    for ti in range(4):
                        n = bi * 4 + ti
                        rmsnorm_hT(x[:, n, :], bx[n], 128, gx[:], hT4, b_hT4, scrB, ti * 128, None, ln=True, bg=b_t2)
                    q_proj(512)
                    for h in range(4):
                        par = h % 2
                        for mt in range(2):
                            bank = mt
                            for dt_ in range(2):
                                T(lambda e: e.matmul(PS[bank][:, :], lhsT=KT[:, h * 2 + dt_, mt * 128:(mt + 1) * 128],
                                                     rhs=qm4[:, h * 2 + dt_, :], start=(dt_ == 0), stop=(dt_ == 1)),
                                  r=[b_KT, b_qm4], w=[bPS[bank]])
                            A(lambda e: e.activation(out=eT4[par][:, mt, :], in_=PS[bank][:, :], func=AF.Exp),
                              r=[bPS[bank]], w=[b_eT4[par]])
                        for mt in range(2):
                            T(lambda e: e.matmul(PS[2][:, :], lhsT=ones[:, :], rhs=eT4[par][:, mt, :], start=(mt == 0),
                                                 stop=(mt == 1)), r=[b_t2, b_eT4[par]], w=[bPS[2]])
                        A(lambda e: e.activation(out=rdn4[par][:, :], in_=PS[2][:, :], func=AF.Ln), r=[bPS[2]], w=[b_rdn4[par]])
                        A(lambda e: e.activation(out=rdn4[par][:, :], in_=rdn4[par][:, :], func=AF.Exp, scale=-1.0),
                          r=[b_rdn4[par]], w=[b_rdn4[par]])
                        for dt_ in range(2):
                            bank = 3 + dt_
                            j = h * 2 + dt_
                            for mt in range(2):
                                T(lambda e: e.matmul(PS[bank][:, :], lhsT=Vm[:, mt, j * 128:(j + 1) * 128],
                                                     rhs=eT4[par][:, mt, :], start=(mt == 0), stop=(mt == 1)),
                                  r=[b_Vm, b_eT4[par]], w=[bPS[bank]])
                            V(lambda e: e.tensor_tensor(out=oT4[:, j, :], in0=PS[bank][:, :], in1=rdn4[par][:, :], op=ALU.mult),
                              r=[bPS[bank], b_rdn4[par]], w=[b_oT4])
                    for ti in range(4):
                        w_mo_resid(bi * 4 + ti, 128, ti * 128)
                n = 16
                rmsnorm_hT(x[:TS, n, :], bx[n], TS, gx[:], hT4, b_hT4, scrB, 0, None, ln=True, bg=b_t2)
                q_proj(TS)
                rden_s = rdn4[0][:, 0:256].rearrange("p (h t) -> p h t", h=4)
                for b in range(16):
                    sl = b % 2
                    S.dma("sp", Kb[sl][:], I["ck"][b].rearrange("(mt p) d -> p mt d", p=128), writes=[bKb[sl]])
                    for q4 in range(4):
                        bank = 2 + (q4 % 2)
                        for i4 in range(4):
                            idx = q4 * 4 + i4
                            j, mt = idx // 2, idx % 2
                            T(lambda e: e.transpose(out=PS[bank][:, i4 * 128:(i4 + 1) * 128],
                                                    in_=Kb[sl][:, mt, j * 128:(j + 1) * 128], identity=identf[:]),
                              r=[bKb[sl], b_const], w=[bPS[bank]])
                        A(lambda e: e.copy(
                            out=KbT[sl][:, 2 * q4:2 * q4 + 2, :].rearrange("p j (m t) -> p j m t", m=2),
                            in_=PS[bank][:, :].rearrange("p (j m t) -> p j m t", j=2, m=2)),
                          r=[bPS[bank]], w=[bKbT[sl]])
                    for h in range(4):
                        for mt in range(2):
                            c0 = mt * 256 + h * 64 + 4 * b
                            for dt_ in range(2):
                                T(lambda e: e.matmul(PS[4][:, c0:c0 + 4],
                                                     lhsT=KbT[sl][:, h * 2 + dt_, mt * 128:(mt + 1) * 128],
                                                     rhs=qm4[:, h * 2 + dt_, 4 * b:4 * b + 4], start=(dt_ == 0),
                                                     stop=(dt_ == 1)), r=[bKbT[sl], b_qm4], w=[bPS[4]])
                A(lambda e: e.activation(out=eTs[:].rearrange("p m h t -> p (m h t)"), in_=PS[4][:, :], func=AF.Exp),
                  r=[bPS[4]], w=[b_eTs])
                for h in range(4):
                    for mt in range(2):
                        T(lambda e: e.matmul(PS[0][:, h * 64:(h + 1) * 64], lhsT=ones[:, :], rhs=eTs[:, mt, h, :],
                                             start=(mt == 0), stop=(mt == 1)), r=[b_t2, b_eTs], w=[bPS[0]])
                V(lambda e: e.reciprocal(out=rden_s, in_=PS[0][:, 0:256].rearrange("p (h t) -> p h t", h=4)),
                  r=[bPS[0]], w=[b_rdn4[0]])
                for b in range(16):
                    sl = b % 2
                    for mt in range(2):
                        S.dma("pool", Vb[sl][:, mt, :], I["cv"][b, mt * 128:(mt + 1) * 128, :], writes=[bVb[sl]])
                    for j in range(8):
                        h = j // 2
                        for mt in range(2):
                            T(lambda e: e.matmul(PS[1][:, j * 64 + 4 * b:j * 64 + 4 * b + 4],
                                                 lhsT=Vb[sl][:, mt, j * 128:(j + 1) * 128],
                                                 rhs=eTs[:, mt, h, 4 * b:4 * b + 4], start=(mt == 0), stop=(mt == 1)),
                              r=[bVb[sl], b_eTs], w=[bPS[1]])
                V(lambda e: e.tensor_tensor(
                    out=oT4[:, :, 0:64].rearrange("p (h a) t -> p h a t", a=2),
                    in0=PS[1][:, :].rearrange("p (h a t) -> p h a t", h=4, a=2),
                    in1=rden_s.unsqueeze(2).to_broadcast([128, 4, 2, 64]), op=ALU.mult),
                  r=[bPS[1], b_rdn4[0]], w=[b_oT4])
                w_mo_resid(16, TS, 0)
                S.barrier()
            if stage <= 3:
                if dbg:
                    for n in range(NT):
                        S.dma("sp", O["dbg_x"][:, n, :], x[:, n, :], reads=[bx[n]])
                S.barrier()
                S.run_block()
                nck.__exit__(None, None, None)
                return nc

            with ExitStack() as s3:
                gml = alloc(s3, "gml", [128, 8])
                b_t3 = Buf("tab3")
                S.dma("sp", gml[:], I["g_mlp"].rearrange("(k p) -> p k", p=128), writes=[b_t3])
                hTa = alloc(s3, "hTa", [128, 8, NTOK], BF16)
                b_hTa = [Buf("hTa%d" % n) for n in range(NT)]
                wup = [alloc(s3, "wup%d" % i, [128, 8, 512], BF16) for i in range(2)]
                wdn = [alloc(s3, "wdn%d" % i, [128, 4, D], BF16) for i in range(2)]
                bwup = [Buf("wup%d" % i, S.GW[i]) for i in range(2)]
                bwdn = [Buf("wdn%d" % i, S.GW[2 + i]) for i in range(2)]
                rl = [alloc(s3, "rl%d" % i, [128, 512]) for i in range(2)]
                brl = [Buf("rl%d" % i) for i in range(2)]
                aT = [alloc(s3, "aT%d" % i, [128, 4, 512], BF16) for i in range(2)]
                baT = [Buf("aT%d" % i) for i in range(2)]

                def load_fc(fc):
                    sl = fc % 2
                    for kt in range(8):
                        S.dma("pool", wup[sl][:, kt, :], I["w_up"][kt * 128:(kt + 1) * 128, fc * 512:(fc + 1) * 512],
                              writes=[bwup[sl]])
                    for ft in range(4):
                        S.dma("pool", wdn[sl][:, ft, :], I["w_down"][fc * 512 + ft * 128:fc * 512 + (ft + 1) * 128, :],
                              writes=[bwdn[sl]])
                load_fc(0)
                scrB["pb"] = [7, 6]
                for n in range(NT):
                    npt = TS if n == 16 else 128
                    rmsnorm_hT(x[:npt, n, :], bx[n], npt, gml[:], hTa, b_hTa[n], scrB, n * 128, None, ln=True, bg=b_t3)
                blocks3 = [(i * 512, 512) for i in range(4)] + [(SEQ, TS)]
                items = [(fc, blk) for fc in range(8) for blk in blocks3]
                ctr = {"ri": 0, "di": 0}
                load_fc(1)

                def mlp_up(i):
                    fc, (t0, nn) = items[i]
                    sl, asl = fc % 2, i % 2
                    tiles = list(range(t0 // 128, t0 // 128 + (nn + 127) // 128))
                    for ft in range(4):
                        bank = ft
                        for kt in range(8):
                            T(lambda e: e.matmul(PS[bank][:, 0:nn], lhsT=wup[sl][:, kt, ft * 128:(ft + 1) * 128],
                                                 rhs=hTa[:, kt, t0:t0 + nn], start=(kt == 0), stop=(kt == 7)),
                              r=[bwup[sl]] + [b_hTa[t] for t in tiles], w=[bPS[bank]])
                        rsl = ctr["ri"] % 2
                        ctr["ri"] += 1
                        A(lambda e: e.activation(out=rl[rsl][:, 0:nn], in_=PS[bank][:, 0:nn], func=AF.Relu),
                          r=[bPS[bank]], w=[brl[rsl]])
                        V(lambda e: e.tensor_tensor(out=aT[asl][:, ft, 0:nn], in0=rl[rsl][:, 0:nn], in1=rl[rsl][:, 0:nn],
                                                    op=ALU.mult), r=[brl[rsl]], w=[baT[asl]])

                def mlp_down(i):
                    fc, (t0, nn) = items[i]
                    sl, asl = fc % 2, i % 2
                    tiles = list(range(t0 // 128, t0 // 128 + (nn + 127) // 128))
                    for ti, tl in enumerate(tiles):
                        npt = TS if tl == 16 else 128
                        for half in range(2):
                            bank = 4 + (ctr["di"] % 4)
                            ctr["di"] += 1
                            for ft in range(4):
                                T(lambda e: e.matmul(PS[bank][:npt, :], lhsT=aT[asl][:, ft, ti * 128:ti * 128 + npt],
                                                     rhs=wdn[sl][:, ft, half * 512:(half + 1) * 512], start=(ft == 0),
                                                     stop=(ft == 3)), r=[baT[asl], bwdn[sl]], w=[bPS[bank]])
                            resid_add(tl, npt, half, bank)

                mlp_up(0)
                for i in range(len(items)):
                    if i + 1 < len(items):
                        mlp_up(i + 1)
                    mlp_down(i)
                    fc = items[i][0]
                    if (i + 1 == len(items) or items[i + 1][0] != fc) and fc + 2 < 8:
                        load_fc(fc + 2)
                S.barrier()
            if dbg:
                for n in range(NT):
                    S.dma("sp", O["dbg_x"][:, n, :], x[:, n, :], reads=[bx[n]])
            with ExitStack() as s4:
                gf = alloc(s4, "gf", [128, D])
                b_gf = Buf("gf")
                S.dma("sp", gf[:], I["g_final"].rearrange("(o d) -> o d", o=1).partition_broadcast(128), writes=[b_gf])
                yst = [alloc(s4, "yst%d" % i, [128, D]) for i in range(3)]
                byst = [Buf("yst%d" % i, S.GS[i]) for i in range(3)]
                for n in range(NT):
                    npt = TS if n == 16 else 128
                    sl = n % 3
                    k4 = n % 2
                    sq, ss, rstd, bscr = scrB["sq"][k4], scrB["ss"][k4], scrB["rstd"][k4], scrB["ba"][k4]
                    A(lambda e: e.activation(out=sq[:npt, :], in_=x[:npt, n, :], func=AF.Square, accum_out=ss[:npt, :]),
                      r=[bx[n]], w=[bscr])
                    A(lambda e: e.activation(out=rstd[:npt, :], in_=ss[:npt, :], func=AF.Ln, scale=1.0 / D,
                                             bias=epsc[:npt, :]), r=[bscr, b_const], w=[bscr])
                    A(lambda e: e.activation(out=rstd[:npt, :], in_=rstd[:npt, :], func=AF.Exp, scale=-0.5),
                      r=[bscr], w=[bscr])
                    V(lambda e: e.scalar_tensor_tensor(out=yst[sl][:npt, :], in0=x[:npt, n, :], scalar=rstd[:npt, :],
                                                       op0=ALU.mult, in1=gf[:npt, :], op1=ALU.mult),
                      r=[bx[n], bscr, b_gf], w=[byst[sl]])
                    if n < 16:
                        S.dma("sp", O["yp"][n * 128:(n + 1) * 128, :], yst[sl][:, :], reads=[byst[sl]])
                    else:
                        S.dma("sp", O["ys"][:, :], yst[sl][:TS, :], reads=[byst[sl]])
                S.barrier()
            S.barrier()
            S.run_block()
            nck.__exit__(None, None, None)
    return nc


_NC = None


def kernel(**inputs):
    global _NC
    if _NC is None:
        _NC = build()
    maps = _in_maps(inputs)
    res = run_bass_kernel_spmd(_NC, maps, core_ids=list(range(8)))
    R = res.results
    f = np.float32

    def cat(name, shape=None):
        return np.stack([np.asarray(R[c][name], f) for c in range(8)])
    y_prompt = cat("yp")
    y_sample = cat("ys").reshape(128, 4, D)
    s5r_p = cat("o_s5r_p")[None]
    s5i_p = cat("o_s5i_p")[None]
    ret_p = cat("o_ret_p")[None]
    mk_p = cat("o_mk").reshape(8, MEM, 4, 256)[None]
    mv_p = cat("o_mv").reshape(8, MEM, 4, 256)[None]
    s5r_s = cat("o_s5r_s").reshape(128, G, 64)[None]
    s5i_s = cat("o_s5i_s").reshape(128, G, 64)[None]
    ret_s = cat("o_ret_s").reshape(128, 4, 128, 128)[None]
    return (y_prompt, y_sample, s5r_p, s5i_p, ret_p, mk_p, mv_p, s5r_s, s5i_s, ret_s)


def _in_maps(inputs):
    cst = _consts()
    f = np.float32
    maps = []
    w = {}
    for k in W_NAMES:
        a = np.asarray(inputs[k], f)
        if k != "g_final":
            a = a[0]
        w[k] = np.ascontiguousarray(a.reshape(W_SHAPES[k]))
    for c in range(8):
        m = dict(w)
        m.update(cst)
        b0 = 16 * c
        m["xp"] = np.ascontiguousarray(np.asarray(inputs["x_prompt"], f)[c])
        m["xs"] = np.ascontiguousarray(np.asarray(inputs["x_sample"], f)[b0:b0 + 16].reshape(TS, D))
        m["memp"] = np.ascontiguousarray(np.asarray(inputs["mem_prompt"], f)[c])
        m["s5r"] = np.ascontiguousarray(np.asarray(inputs["state_s5_re"], f)[0, b0:b0 + 16].reshape(512, 64))
        m["s5i"] = np.ascontiguousarray(np.asarray(inputs["state_s5_im"], f)[0, b0:b0 + 16].reshape(512, 64))
        m["sret"] = np.ascontiguousarray(np.asarray(inputs["state_ret"], f)[0, b0:b0 + 16])
        m["ck"] = np.ascontiguousarray(np.asarray(inputs["cache_mem_k"], f)[0, b0:b0 + 16].reshape(16, MEM, D))
        m["cv"] = np.ascontiguousarray(np.asarray(inputs["cache_mem_v"], f)[0, b0:b0 + 16].reshape(16, MEM, D))
        maps.append(m)
    return maps
```

```python
import numpy as np
import concourse.bass as bass
import concourse.mybir as mybir
from concourse.bass_utils import run_bass_kernel_spmd
from contextlib import ExitStack

F32 = mybir.dt.float32
BF16 = mybir.dt.bfloat16
AF = mybir.ActivationFunctionType
ALU = mybir.AluOpType

D = 1024
SEQ = 2048
NTP = 16
TS = 64
NT = 17
NTOK = SEQ + TS
G = 32
DFF = 4096
MEM = 256
EPS = 1e-6
PAST = 16384.0
MAGIC = 12582912.0
TWO_PI = float(2.0 * np.pi)
ML = [7, 6, 5, 4, 3, 2, 1, 0, 1, 2, 3, 4, 5, 6, 7, 8, -4, 0.5]
K1 = len(ML)
I_A1, I_A8, I_A4, I_AM4, I_HALF = 8, 15, 3, 16, 17
GAM = [1.0 - 2.0 ** (-5.0 - h) for h in range(4)]


class Grp:
    __slots__ = ("sem", "cnt", "sealed")


class Buf:
    __slots__ = ("w", "r", "name", "grp", "ps")

    def __init__(self, name="", grp=None, ps=False):
        self.w = None
        self.r = []
        self.name = name
        self.grp = grp
        self.ps = ps


class _Rec:
    def __init__(self):
        self.call = None

    def __getattr__(self, name):
        def f(*a, **kw):
            self.call = (name, a, kw)
            return self
        return f


class Sched:
    ENG = ("pe", "dve", "act", "pool", "sp")

    def __init__(self, nc, stack, self_sync=("dve", "act", "pool")):
        self.nc = nc
        self.stack = stack
        self.prog = {k: [] for k in self.ENG}
        self.cnt = {k: 0 for k in self.ENG}
        self.waited = {k: {} for k in self.ENG}
        self.sem = {}
        self.nsem = 0
        for k in ("pe", "dve", "act", "pool"):
            self.sem[k] = self.new_sem("c_" + k)
        self.self_sync = set(self_sync)
        self.groups = []
        self.GC = self.group("gc")
        self.GP = self.group("gp")
        self.GW = [self.group("gw%d" % i) for i in range(4)]
        self.GX = self.group("gx")
        self.GL = [self.group("gl%d" % i) for i in range(2)]
        self.GS = [self.group("gs%d" % i) for i in range(3)]

    def group(self, name):
        g = Grp()
        g.sem = self.new_sem(name)
        g.cnt = 0
        g.sealed = False
        self.groups.append(g)
        return g

    def new_sem(self, name):
        self.nsem += 1
        assert self.nsem < 98, "too many semaphores"
        return self.stack.enter_context(self.nc.semaphore(name + "_%d" % self.nsem))

    def _waits(self, eng, deps):
        w = self.waited[eng]
        need = {}
        dd = []
        for d in deps:
            if isinstance(d, Grp):
                d.sealed = True
                dd.append((d.sem, d.cnt))
            else:
                dd.append(d)
        deps = dd
        for (s, v) in deps:
            if eng in self.sem and s is self.sem[eng] and eng not in self.self_sync:
                continue
            k = id(s)
            if w.get(k, 0) >= v:
                continue
            if k not in need or need[k][1] < v:
                need[k] = (s, v)
        for k, (s, v) in need.items():
            w[k] = v
            self.prog[eng].append(lambda e, s=s, v=v: e.wait_ge(s, v))

    def op(self, eng, fn, reads=(), writes=()):
        deps = []
        for b in reads:
            if b.w is not None:
                deps.append(b.w)
            if b.ps:
                mys = self.sem[eng]
                deps.extend(d for d in b.r if not (isinstance(d, tuple) and d[0] is mys))
        for b in writes:
            if b.w is not None:
                deps.append(b.w)
            deps.extend(b.r)
        self._waits(eng, deps)
        self.cnt[eng] += 1
        c = self.cnt[eng]
        s = self.sem[eng]
        rec = _Rec()
        fn(rec)
        name, a, kw = rec.call
        self.prog[eng].append(lambda e, name=name, a=a, kw=kw, s=s: getattr(e, name)(*a, **kw).then_inc(s, 1))
        for b in reads:
            b.r.append((s, c))
        for b in writes:
            b.w = (s, c)
            b.r = []

    def dma(self, q, out, in_, reads=(), writes=(), **kw):
        tb = writes[0] if writes else reads[0]
        g = tb.grp
        if g is None:
            g = self.GP if q == "pool" else (self.GC if writes else self.GS[0])
        deps = []
        for b in reads:
            if b.w is not None:
                deps.append(b.w)
        for b in writes:
            if b.w is not None and b.w is not g:
                deps.append(b.w)
            deps.extend(b.r)
        self._waits(q, deps)
        if g.sealed and g.cnt > 0:
            self._waits(q, [(g.sem, g.cnt)])
        g.sealed = False
        g.cnt += 16
        s = g.sem
        self.prog[q].append(
            lambda e, out=out, in_=in_, s=s, kw=kw: e.dma_start(out=out, in_=in_, **kw).then_inc(s, 16))
        for b in reads:
            b.r.append(g)
        for b in writes:
            b.w = g
            b.r = []

    def barrier(self, engines=None):
        deps = [(self.sem[k], self.cnt[k]) for k in ("pe", "dve", "act", "pool") if self.cnt[k] > 0]
        deps += [g for g in self.groups if g.cnt > 0]
        for e in (engines or self.ENG):
            self._waits(e, deps)

    def run_block(self):
        nc = self.nc
        with nc.Block() as block:
            @block.sync
            def _(e):
                for t in self.prog["sp"]:
                    t(e)

            @block.tensor
            def _(e):
                for t in self.prog["pe"]:
                    t(e)

            @block.vector
            def _(e):
                for t in self.prog["dve"]:
                    t(e)

            @block.scalar
            def _(e):
                for t in self.prog["act"]:
                    t(e)

            @block.gpsimd
            def _(e):
                for t in self.prog["pool"]:
                    t(e)


_CONSTS = None


def _consts():
    global _CONSTS
    if _CONSTS is not None:
        return _CONSTS
    f = np.float32
    c = {}
    c["c_ident"] = np.eye(128, dtype=f)
    m = np.zeros((8, 128, 240), f)
    for a in range(8):
        for i in range(16):
            m[a, 16 * a + i, 112 + i] = 1.0
    c["c_masters"] = m
    ml = np.array(ML, np.float64)
    rows = np.concatenate([ml / (2 * np.pi), ml, 8.0 * (np.arange(64) + 1) / (2 * np.pi)])
    c["c_rows"] = rows.astype(f)[None, :]
    sg = np.zeros((128, 2), f)
    sg[:64, 0] = 1.0
    sg[64:, 0] = -1.0
    sg[:64, 1] = -1.0
    sg[64:, 1] = 1.0
    c["c_sgn"] = sg
    inv = (f(10000.0) ** (-(np.arange(64, dtype=f) / f(64.0)))).astype(f)
    pos = np.zeros((128, NT), f)
    for n in range(NTP):
        pos[:, n] = 128 * n + np.arange(128)
    pos[:64, 16] = PAST + (np.arange(64) % 4)
    ang = (pos[:, :, None] * inv[None, None, :]).astype(f).astype(np.float64)
    c["c_rope"] = np.stack([np.cos(ang), np.sin(ang), -np.sin(ang)]).astype(f)
    lg = np.log(np.array(GAM, np.float64))
    sc = 128.0 ** -0.5
    idx = np.arange(128)
    dm = np.zeros((128, 4, 128), np.float64)
    diff = idx[None, :] - idx[:, None]
    for h in range(4):
        dm[:, h, :] = np.where(diff >= 0, np.exp(np.maximum(diff, 0) * lg[h]), 0.0) * sc
    c["c_dmask_p"] = dm.reshape(128, 512).astype(f)
    ds_ = np.zeros((64, 4, 64), np.float64)
    r = np.arange(64)
    bb = r // 4
    tt = r % 4
    same = bb[:, None] == bb[None, :]
    dts = tt[None, :] - tt[:, None]
    for h in range(4):
        ds_[:, h, :] = np.where(same & (dts >= 0), np.exp(np.maximum(dts, 0) * lg[h]), 0.0) * sc
    c["c_dmask_s"] = ds_.reshape(64, 256).astype(f)
    xi_p = np.stack([np.exp((idx + 1.0) * lg[h]) * sc for h in range(4)])
    xi_s = np.stack([np.exp((tt + 1.0) * lg[h]) * sc for h in range(4)])
    c["c_xi"] = np.concatenate([xi_p.reshape(-1), xi_s.reshape(-1)]).astype(f)[None, :]
    zp = np.stack([np.exp((127.0 - idx) * lg[h]) for h in range(4)], axis=1)
    c["c_zeta_p"] = zp.astype(f)
    zs = np.zeros((64, 16, 4), np.float64)
    for h in range(4):
        for b in range(16):
            zs[:, b, h] = np.where(bb == b, np.exp((3.0 - tt) * lg[h]), 0.0)
    c["c_zs"] = zs.reshape(64, 64).astype(f)
    cm = np.zeros((16, 64), f)
    for b in range(16):
        cm[b, 4 * b:4 * b + 4] = 1.0
    c["c_cmask"] = cm.reshape(1, -1)
    _CONSTS = c
    return c


W_NAMES = ["g_mix", "w_in", "lam_re", "lam_im", "log_dt", "b_re", "b_im", "c_re", "c_im", "d_skip", "w_glu",
           "ret_gn", "w_out", "g_xattn", "g_mem", "w_mq", "w_mk", "w_mv", "w_mo", "g_mlp", "w_up", "w_down",
           "g_final"]
W_SHAPES = {"g_mix": [D], "w_in": [D, 2560], "lam_re": [G, 64], "lam_im": [G, 64], "log_dt": [G],
            "b_re": [G, 64, 16], "b_im": [G, 64, 16], "c_re": [G * 16, 64], "c_im": [G * 16, 64], "d_skip": [512],
            "w_glu": [512, 512], "ret_gn": [512], "w_out": [D, D], "g_xattn": [D], "g_mem": [D], "w_mq": [D, D],
            "w_mk": [D, D], "w_mv": [D, D], "w_mo": [D, D], "g_mlp": [D], "w_up": [D, DFF], "w_down": [DFF, D],
            "g_final": [D]}
IN_SHAPES = {"xp": [SEQ, D], "xs": [TS, D], "memp": [MEM, D], "s5r": [512, 64], "s5i": [512, 64],
             "sret": [16, 4, 128, 128], "ck": [16, MEM, D], "cv": [16, MEM, D]}
OUT_SHAPES = {"yp": [SEQ, D], "ys": [TS, D], "o_s5r_p": [G, 64], "o_s5i_p": [G, 64], "o_ret_p": [4, 128, 128],
              "o_mk": [MEM, D], "o_mv": [MEM, D], "o_s5r_s": [512, 64], "o_s5i_s": [512, 64],
              "o_ret_s": [16, 4, 128, 128]}


def build(stage=99, dbg=False):
    nc = bass.Bass("TRN2", target_bir_lowering=False)
    cst = _consts()
    I = {}
    for k, shp in list(IN_SHAPES.items()) + list(W_SHAPES.items()):
        I[k] = nc.dram_tensor(k, shp, F32, kind="ExternalInput").ap()
    for k, v in cst.items():
        I[k] = nc.dram_tensor(k, list(v.shape), F32, kind="ExternalInput").ap()
    O = {}
    for k, shp in OUT_SHAPES.items():
        O[k] = nc.dram_tensor(k, shp, F32, kind="ExternalOutput").ap()
    if dbg:
        O["dbg_ssm"] = nc.dram_tensor("dbg_ssm", [128, 4, NTOK], F32, kind="ExternalOutput").ap()
        O["dbg_x"] = nc.dram_tensor("dbg_x", [128, NT, D], F32, kind="ExternalOutput").ap()

    with ExitStack() as st:
        S = Sched(nc, st)

        def alloc(stack, name, shape, dt=F32):
            return stack.enter_context(nc.sbuf_tensor(name, shape, dt))

        def palloc(stack, name, shape, dt=F32):
            return stack.enter_context(nc.psum_tensor(name, shape, dt))

        def V(fn, r=(), w=()):
            S.op("dve", fn, reads=r, writes=w)

        def A(fn, r=(), w=()):
            S.op("act", fn, reads=r, writes=w)

        import os as _os0
        _nopool = _os0.environ.get("K_NOPOOL") == "1"

        def PL(fn, r=(), w=()):
            S.op("dve" if _nopool else "pool", fn, reads=r, writes=w)

        def T(fn, r=(), w=()):
            S.op("pe", fn, reads=r, writes=w)

        nck = nc.allow_non_contiguous_dma(reason="small param layout loads")
        nck.__enter__()

        identb = alloc(st, "identb", [128, 128], BF16)
        identf = alloc(st, "identf", [128, 128], F32)
        sgn = alloc(st, "sgn", [128, 2])
        epsc = alloc(st, "epsc", [128, 1])
        ssmT = alloc(st, "ssmT", [128, 4, NTOK], BF16)
        b_const = Buf("const")
        b_ssmT = [Buf("ssmT%d" % i) for i in range(5)]
        b_constp = Buf("constp")
        S.dma("pool", identb[:], I["c_ident"][:, :], writes=[b_constp])
        S.dma("sp", identf[:], I["c_ident"][:, :], writes=[b_const])
        S.dma("sp", sgn[:], I["c_sgn"][:, :], writes=[b_const])
        V(lambda e: e.memset(epsc[:], EPS), r=[b_constp], w=[b_const])
        PS = [palloc(st, "ps%d" % i, [128, 512], F32) for i in range(8)]
        bPS = [Buf("ps%d" % i, ps=True) for i in range(8)]

        def ps_bf(i):
            return PS[i][:].bitcast(BF16)

        def make_scr(stack, tag, pbanks):
            d = {"i": 0, "pb": list(pbanks)}
            d["sq"] = [alloc(stack, "sq%s%d" % (tag, i), [128, D], BF16) for i in range(2)]
            d["ss"] = [alloc(stack, "ss%s%d" % (tag, i), [128, 1]) for i in range(2)]
            d["rstd"] = [alloc(stack, "rstd%s%d" % (tag, i), [128, 1]) for i in range(2)]
            d["hb"] = [alloc(stack, "hb%s%d" % (tag, i), [128, D], BF16) for i in range(2)]
            d["ba"] = [Buf("ba%s%d" % (tag, i)) for i in range(2)]
            d["bh"] = [Buf("bh%s%d" % (tag, i)) for i in range(2)]
            return d

        def rmsnorm_hT(xt_ap, bx, npart, gcol, hT_ap, bhT, scr, col0, ph, ln=False, bg=None, out4=None):
            k = scr["i"] % 2
            pbank = scr["pb"][scr["i"] % len(scr["pb"])]
            scr["i"] += 1
            sq, ss, rstd, hb = scr["sq"][k], scr["ss"][k], scr["rstd"][k], scr["hb"][k]
            ba, bh = scr["ba"][k], scr["bh"][k]
            A(lambda e: e.activation(out=sq[:npart, :], in_=xt_ap, func=AF.Square, accum_out=ss[:npart, :]),
              r=[bx], w=[ba])
            if ln:
                A(lambda e: e.activation(out=rstd[:npart, :], in_=ss[:npart, :], func=AF.Ln, scale=1.0 / D,
                                         bias=epsc[:npart, :]), r=[ba, b_const], w=[ba])
                A(lambda e: e.activation(out=rstd[:npart, :], in_=rstd[:npart, :], func=AF.Exp, scale=-0.5),
                  r=[ba], w=[ba])
            else:
                A(lambda e: e.activation(out=rstd[:npart, :], in_=ss[:npart, :], func=AF.Sqrt, scale=1.0 / D,
                                         bias=epsc[:npart, :]), r=[ba, b_const], w=[ba])
                V(lambda e: e.reciprocal(out=rstd[:npart, :], in_=rstd[:npart, :]), r=[ba], w=[ba])
            V(lambda e: e.tensor_scalar(out=hb[:npart, :], in0=xt_ap, scalar1=rstd[:npart, :], scalar2=None,
                                        op0=ALU.mult), r=[bx, ba], w=[bh])
            pv = ps_bf(pbank)
            for kt in range(8):
                T(lambda e, kt=kt: e.transpose(out=pv[:, kt * 128:kt * 128 + npart],
                                               in_=hb[:npart, kt * 128:(kt + 1) * 128],
                                               identity=identb[:npart, :npart]),
                  r=[bh, b_const], w=[bPS[pbank]])
            if out4 is not None:
                V(lambda e: e.tensor_tensor(
                    out=out4, in0=pv.rearrange("p (k c s) -> p k c s", k=8, s=8),
                    in1=gcol.unsqueeze(2).unsqueeze(3).to_broadcast([128, 8, 16, 8]), op=ALU.mult),
                  r=[bPS[pbank], b_const] + ([bg] if bg is not None else []), w=[bhT])
                return
            V(lambda e: e.tensor_tensor(
                out=hT_ap[:, :, col0:col0 + npart],
                in0=pv.rearrange("p (k t) -> p k t", k=8)[:, :, 0:npart],
                in1=gcol.unsqueeze(2).to_broadcast([128, 8, npart]), op=ALU.mult),
              r=[bPS[pbank], b_const] + ([bg] if bg is not None else []), w=[bhT])

        def load_w_bf16(dst, bdst, src, kt_n, ncols, c0=0):
            for kt in range(kt_n):
                for cc in range(0, ncols, 1024):
                    w_ = min(1024, ncols - cc)
                    S.dma("pool", dst[:, kt, cc:cc + w_], src[kt * 128:(kt + 1) * 128, c0 + cc:c0 + cc + w_],
                          writes=[bdst])

        with ExitStack() as sa:
            Wt = alloc(sa, "Wt", [128, G, 128], BF16)
            Wst = alloc(sa, "Wst", [128, G, 128], BF16)
            Tt = alloc(sa, "Tt", [128, G, 128], BF16)
            Vt = alloc(sa, "Vt", [128, G, 128], BF16)
            COSR = alloc(sa, "COSR", [128, G, 64])
            SINR = alloc(sa, "SINR", [128, G, 64])
            masters = alloc(sa, "masters", [128, 8, 240], BF16)
            AR = alloc(sa, "AR", [128, G, K1])
            AI = alloc(sa, "AI", [128, G, K1])
            MAGJ = alloc(sa, "MAGJ", [128, G, K1])
            DS = alloc(sa, "DS", [128, G])
            gm = alloc(sa, "gm", [128, 8])
            winu = alloc(sa, "winu", [128, 8, 512], BF16)
            wglu = alloc(sa, "wglu", [128, 4, 512], BF16)
            b_tab = Buf("s5tab")
            b_winu = Buf("winu", S.GW[0])
            b_wglu = Buf("wglu", S.GW[1])
            b_tabp = Buf("s5tabp")
            S.dma("pool", masters[:], I["c_masters"].rearrange("a k j -> k a j"), writes=[b_tabp])
            S.dma("sp", gm[:], I["g_mix"].rearrange("(k p) -> p k", p=128), writes=[b_tab])
            for tau in range(8):
                S.dma("sp", DS[16 * tau:16 * tau + 16, :], I["d_skip"].rearrange("(g h) -> h g", h=16),
                      writes=[b_tab])
            load_w_bf16(winu, b_winu, I["w_in"], 8, 512, 0)
            load_w_bf16(wglu, b_wglu, I["w_glu"], 4, 512, 0)

            with ExitStack() as s0:
                rows = alloc(s0, "rows", [128, 2 * K1 + 64])
                LR = alloc(s0, "LR", [128, G])
                LI = alloc(s0, "LI", [128, G])
                DT = alloc(s0, "DT", [128, G])
                LRDT = alloc(s0, "LRDT", [128, G])
                LIDT = alloc(s0, "LIDT", [128, G])
                tA = alloc(s0, "tA", [128, G, 64])
                tB = alloc(s0, "tB", [128, G, 64])
                tC = alloc(s0, "tC", [128, G, 64])
                COSJ = alloc(s0, "COSJ", [128, G, K1])
                SINJ = alloc(s0, "SINJ", [128, G, K1])
                sm = alloc(s0, "sm", [128, 12, G])
                Br1 = alloc(s0, "Br1", [128, G, 16])
                Br2 = alloc(s0, "Br2", [128, G, 16])
                BB1 = alloc(s0, "BB1", [128, G, 16])
                BB2 = alloc(s0, "BB2", [128, G, 16])
                tb1 = alloc(s0, "tb1", [128, G, 16])
                big1 = alloc(s0, "big1", [128, G, 128])
                big2 = alloc(s0, "big2", [128, G, 128])
                WTpad = alloc(s0, "WTpad", [128, G, 256], BF16)
                WTs = alloc(s0, "WTs", [128, G, 128], BF16)
                CN1 = alloc(s0, "CN1", [128, 4, 128])
                CN2 = alloc(s0, "CN2", [128, 4, 128])
                CMa = alloc(s0, "CMa", [128, G, 16])
                CMb = alloc(s0, "CMb", [128, G, 16])
                CMab = alloc(s0, "CMab", [128, G, 16], BF16)
                b0 = Buf("p0in")
                bt = Buf("p0tmp")
                S.dma("sp", rows[:], I["c_rows"][0:1, :].partition_broadcast(128), writes=[b0])
                for hf in range(2):
                    S.dma("sp", LR[64 * hf:64 * hf + 64, :], I["lam_re"].rearrange("g p -> p g"), writes=[b0])
                    S.dma("sp", LI[64 * hf:64 * hf + 64, :], I["lam_im"].rearrange("g p -> p g"), writes=[b0])
                S.dma("sp", DT[:], I["log_dt"].rearrange("(o g) -> o g", o=1).partition_broadcast(128), writes=[b0])
                S.dma("sp", Br1[0:64], I["b_re"].rearrange("g p h -> p g h"), writes=[b0])
                S.dma("sp", Br1[64:128], I["b_im"].rearrange("g p h -> p g h"), writes=[b0])
                S.dma("sp", Br2[0:64], I["b_im"].rearrange("g p h -> p g h"), writes=[b0])
                S.dma("sp", Br2[64:128], I["b_re"].rearrange("g p h -> p g h"), writes=[b0])
                S.dma("sp", CN1[:, :, 0:64], I["c_re"].rearrange("(c r) p -> r c p", r=128), writes=[b0])
                S.dma("sp", CN1[:, :, 64:128], I["c_im"].rearrange("(c r) p -> r c p", r=128), writes=[b0])
                S.dma("sp", CN2[:, :, 0:64], I["c_im"].rearrange("(c r) p -> r c p", r=128), writes=[b0])
                S.dma("sp", CN2[:, :, 64:128], I["c_re"].rearrange("(c r) p -> r c p", r=128), writes=[b0])
                MT1 = rows[:, 0:K1]
                MLr = rows[:, K1:2 * K1]
                MRT = rows[:, 2 * K1:2 * K1 + 64]
                A(lambda e: e.activation(out=DT[:], in_=DT[:], func=AF.Exp), r=[b0], w=[b0])
                V(lambda e: e.tensor_tensor(out=LRDT[:], in0=LR[:], in1=DT[:], op=ALU.mult), r=[b0], w=[bt])
                V(lambda e: e.tensor_tensor(out=LIDT[:], in0=LI[:], in1=DT[:], op=ALU.mult), r=[b0], w=[bt])

                def trig(mt_ap, K, cos_out, sin_out):
                    shp = [128, G, K]
                    a_, b_, c_ = tA[:, :, 0:K], tB[:, :, 0:K], tC[:, :, 0:K]
                    V(lambda e: e.tensor_tensor(out=a_, in0=LIDT[:].unsqueeze(2).to_broadcast(shp),
                                                in1=mt_ap.unsqueeze(1).to_broadcast(shp), op=ALU.mult),
                      r=[bt, b0], w=[bt])
                    for (outp, off) in ((sin_out, 0.0), (cos_out, 0.25)):
                        if outp is None:
                            continue
                        V(lambda e, off=off: e.tensor_scalar(out=c_, in0=a_, scalar1=off, scalar2=None,
                                                             op0=ALU.add), r=[bt], w=[bt])
                        V(lambda e: e.tensor_scalar(out=b_, in0=c_, scalar1=MAGIC, scalar2=None, op0=ALU.add),
                          r=[bt], w=[bt])
                        V(lambda e: e.tensor_scalar(out=b_, in0=b_, scalar1=MAGIC, scalar2=None, op0=ALU.subtract),
                          r=[bt], w=[bt])
                        V(lambda e: e.tensor_tensor(out=c_, in0=c_, in1=b_, op=ALU.subtract), r=[bt], w=[bt])
                        A(lambda e, outp=outp: e.activation(out=outp, in_=c_, func=AF.Sin, scale=TWO_PI),
                          r=[bt], w=[b_tab])

                trig(MT1, K1, COSJ[:], SINJ[:])
                trig(MRT, 64, COSR[:], SINR[:])
                shpj = [128, G, K1]
                V(lambda e: e.tensor_tensor(out=MAGJ[:], in0=LRDT[:].unsqueeze(2).to_broadcast(shpj),
                                            in1=MLr.unsqueeze(1).to_broadcast(shpj), op=ALU.mult),
                  r=[bt, b0], w=[b_tab])
                A(lambda e: e.activation(out=MAGJ[:], in_=MAGJ[:], func=AF.Exp), r=[b_tab], w=[b_tab])
                V(lambda e: e.tensor_tensor(out=AR[:], in0=MAGJ[:], in1=COSJ[:], op=ALU.mult), r=[b_tab], w=[b_tab])
                V(lambda e: e.tensor_tensor(out=AI[:], in0=MAGJ[:], in1=SINJ[:], op=ALU.mult), r=[b_tab], w=[b_tab])
                em1, shalf, cm1, am1r, ai1, den, fr, fi, t0_, t1_ = [sm[:, i, :] for i in range(10)]
                x_ = LRDT[:]
                V(lambda e: e.tensor_scalar(out=em1, in0=x_, scalar1=0.2, scalar2=1.0, op0=ALU.mult, op1=ALU.add),
                  r=[bt], w=[bt])
                for cf in (0.25, 1.0 / 3.0, 0.5):
                    V(lambda e: e.tensor_tensor(out=em1, in0=em1, in1=x_, op=ALU.mult), r=[bt], w=[bt])
                    V(lambda e, cf=cf: e.tensor_scalar(out=em1, in0=em1, scalar1=cf, scalar2=1.0, op0=ALU.mult,
                                                       op1=ALU.add), r=[bt], w=[bt])
                V(lambda e: e.tensor_tensor(out=em1, in0=em1, in1=x_, op=ALU.mult), r=[bt], w=[bt])
                V(lambda e: e.tensor_copy(out=shalf, in_=SINJ[:, :, I_HALF]), r=[b_tab], w=[bt])
                V(lambda e: e.scalar_tensor_tensor(out=cm1, in0=shalf, scalar=-2.0, op0=ALU.mult, in1=shalf,
                                                   op1=ALU.mult), r=[bt], w=[bt])
                V(lambda e: e.tensor_tensor(out=am1r, in0=em1, in1=COSJ[:, :, I_A1], op=ALU.mult), r=[bt, b_tab], w=[bt])
                V(lambda e: e.tensor_tensor(out=am1r, in0=am1r, in1=cm1, op=ALU.add), r=[bt], w=[bt])
                V(lambda e: e.tensor_copy(out=ai1, in_=AI[:, :, I_A1]), r=[b_tab], w=[bt])
                V(lambda e: e.tensor_tensor(out=den, in0=LR[:], in1=LR[:], op=ALU.mult), r=[b0], w=[bt])
                V(lambda e: e.tensor_tensor(out=t0_, in0=LI[:], in1=LI[:], op=ALU.mult), r=[b0], w=[bt])
                V(lambda e: e.tensor_tensor(out=den, in0=den, in1=t0_, op=ALU.add), r=[bt], w=[bt])
                V(lambda e: e.reciprocal(out=den, in_=den), r=[bt], w=[bt])
                V(lambda e: e.tensor_tensor(out=fr, in0=am1r, in1=LR[:], op=ALU.mult), r=[bt, b0], w=[bt])
                V(lambda e: e.tensor_tensor(out=t0_, in0=ai1, in1=LI[:], op=ALU.mult), r=[bt, b0], w=[bt])
                V(lambda e: e.tensor_tensor(out=fr, in0=fr, in1=t0_, op=ALU.add), r=[bt], w=[bt])
                V(lambda e: e.tensor_tensor(out=fr, in0=fr, in1=den, op=ALU.mult), r=[bt], w=[bt])
                V(lambda e: e.tensor_tensor(out=fi, in0=ai1, in1=LR[:], op=ALU.mult), r=[bt, b0], w=[bt])
                V(lambda e: e.tensor_tensor(out=t0_, in0=am1r, in1=LI[:], op=ALU.mult), r=[bt, b0], w=[bt])
                V(lambda e: e.tensor_tensor(out=fi, in0=fi, in1=t0_, op=ALU.subtract), r=[bt], w=[bt])
                V(lambda e: e.tensor_tensor(out=fi, in0=fi, in1=den, op=ALU.mult), r=[bt], w=[bt])
                V(lambda e: e.tensor_scalar(out=Br2[:], in0=Br2[:], scalar1=sgn[:, 1:2], scalar2=None, op0=ALU.mult),
                  r=[b0, b_const], w=[b0])
                shb = [128, G, 16]
                frb = fr.unsqueeze(2).to_broadcast(shb)
                fib = fi.unsqueeze(2).to_broadcast(shb)
                V(lambda e: e.tensor_tensor(out=BB1[:], in0=Br1[:], in1=frb, op=ALU.mult), r=[b0, bt], w=[bt])
                V(lambda e: e.tensor_tensor(out=tb1[:], in0=Br2[:], in1=fib, op=ALU.mult), r=[b0, bt], w=[bt])
                V(lambda e: e.tensor_tensor(out=BB1[:], in0=BB1[:], in1=tb1[:], op=ALU.add), r=[bt], w=[bt])
                V(lambda e: e.tensor_tensor(out=BB2[:], in0=Br2[:], in1=frb, op=ALU.mult), r=[b0, bt], w=[bt])
                V(lambda e: e.tensor_tensor(out=tb1[:], in0=Br1[:], in1=fib, op=ALU.mult), r=[b0, bt], w=[bt])
                V(lambda e: e.tensor_tensor(out=BB2[:], in0=BB2[:], in1=tb1[:], op=ALU.subtract), r=[bt], w=[bt])
                sh4 = [128, G, 8, 16]
                arv = AR[:, :, 0:8].unsqueeze(3).to_broadcast(sh4)
                aiv = AI[:, :, 0:8].unsqueeze(3).to_broadcast(sh4)
                bb1 = BB1[:].unsqueeze(2).to_broadcast(sh4)
                bb2 = BB2[:].unsqueeze(2).to_broadcast(sh4)
                g1 = big1[:].rearrange("p g (s h) -> p g s h", s=8)
                g2 = big2[:].rearrange("p g (s h) -> p g s h", s=8)
                V(lambda e: e.memset(WTpad[:], 0.0), w=[bt])
                V(lambda e: e.tensor_tensor(out=g1, in0=arv, in1=bb1, op=ALU.mult), r=[b_tab, bt], w=[bt])
                V(lambda e: e.tensor_tensor(out=g2, in0=aiv, in1=bb2, op=ALU.mult), r=[b_tab, bt], w=[bt])
                V(lambda e: e.tensor_tensor(out=WTpad[:, :, 0:128], in0=big1[:], in1=big2[:], op=ALU.add),
                  r=[bt], w=[bt])
                V(lambda e: e.tensor_tensor(out=g1, in0=arv, in1=bb2, op=ALU.mult), r=[b_tab, bt], w=[bt])
                V(lambda e: e.tensor_tensor(out=g2, in0=aiv, in1=bb1, op=ALU.mult), r=[b_tab, bt], w=[bt])
                V(lambda e: e.tensor_tensor(out=WTs[:], in0=big1[:], in1=big2[:], op=ALU.subtract), r=[bt], w=[bt])
                for (src_fn, dstt) in ((lambda g: WTpad[:, g, 0:128], Wt), (lambda g: WTs[:, g, :], Wst)):
                    for gq in range(8):
                        bank = gq % 2
                        pv = ps_bf(bank)
                        for j in range(4):
                            g = gq * 4 + j
                            T(lambda e, g=g, j=j, pv=pv, src_fn=src_fn: e.transpose(
                                out=pv[:, j * 128:(j + 1) * 128], in_=src_fn(g), identity=identb[:]),
                              r=[bt, b_const], w=[bPS[bank]])
                        A(lambda e, gq=gq, pv=pv, dstt=dstt: e.copy(
                            out=dstt[:, gq * 4:gq * 4 + 4, :], in_=pv[:, 0:512].rearrange("p (j c) -> p j c", j=4)),
                          r=[bPS[bank]], w=[b_tab])
                for (CN, CM, col) in ((CN1, CMa, 0), (CN2, CMb, None)):
                    for c4 in range(4):
                        bank = 2 + (c4 % 2)
                        T(lambda e, CN=CN, c4=c4, bank=bank: e.transpose(out=PS[bank][:, 0:128], in_=CN[:, c4, :],
                                                                         identity=identf[:]),
                          r=[b0, b_const], w=[bPS[bank]])
                        if col is not None:
                            V(lambda e, CM=CM, c4=c4, bank=bank: e.tensor_scalar(
                                out=CM[:, c4 * 8:(c4 + 1) * 8, :],
                                in0=PS[bank][:, 0:128].rearrange("p (g h) -> p g h", g=8),
                                scalar1=sgn[:, 0:1], scalar2=None, op0=ALU.mult),
                              r=[bPS[bank], b_const], w=[bt])
                        else:
                            V(lambda e, CM=CM, c4=c4, bank=bank: e.tensor_scalar(
                                out=CM[:, c4 * 8:(c4 + 1) * 8, :],
                                in0=PS[bank][:, 0:128].rearrange("p (g h) -> p g h", g=8),
                                scalar1=-1.0, scalar2=None, op0=ALU.mult),
                              r=[bPS[bank]], w=[bt])
                V(lambda e: e.tensor_copy(out=CMab[:], in_=CMa[:]), r=[bt], w=[bt])
                afw = AR[:, :, 8:16].unsqueeze(3).to_broadcast(sh4)
                aifw = AI[:, :, 8:16].unsqueeze(3).to_broadcast(sh4)
                cma = CMa[:].unsqueeze(2).to_broadcast(sh4)
                cmb = CMb[:].unsqueeze(2).to_broadcast(sh4)
                V(lambda e: e.tensor_tensor(out=g1, in0=afw, in1=cma, op=ALU.mult), r=[b_tab, bt], w=[bt])
                V(lambda e: e.tensor_tensor(out=g2, in0=aifw, in1=cmb, op=ALU.mult), r=[b_tab, bt], w=[bt])
                V(lambda e: e.tensor_tensor(out=Vt[:], in0=big1[:], in1=big2[:], op=ALU.add), r=[bt], w=[b_tab])
                for gq in range(8):
                    bank = 4 + (gq % 2)
                    for j in range(4):
                        g = gq * 4 + j
                        for tau in range(8):
                            c0 = (7 - tau) * 16
                            T(lambda e, g=g, j=j, tau=tau, c0=c0, bank=bank: e.matmul(
                                PS[bank][:, j * 128 + tau * 16:j * 128 + tau * 16 + 16],
                                lhsT=WTpad[:, g, c0:c0 + 128], rhs=CMab[:, g, :], start=True, stop=True),
                              r=[bt], w=[bPS[bank]])
                    A(lambda e, gq=gq, bank=bank: e.copy(
                        out=Tt[:, gq * 4:gq * 4 + 4, :], in_=PS[bank][:].rearrange("p (j c) -> p j c", j=4)),
                      r=[bPS[bank]], w=[b_tab])
                S.barrier()
            xst = [alloc(sa, "xst%d" % i, [128, D]) for i in range(2)]
            bxst = [Buf("xst%d" % i, S.GL[i]) for i in range(2)]
            scrA = make_scr(sa, "A", [7])
            bscr = Buf("scrA")
            hT2 = [alloc(sa, "hT_%d" % i, [128, 8, 512], BF16) for i in range(2)]
            bhT2 = [Buf("hT_%d" % i) for i in range(2)]
            uT2 = [alloc(sa, "uT_%d" % i, [128, 4, 512], BF16) for i in range(2)]
            buT2 = [Buf("uT_%d" % i) for i in range(2)]
            U = alloc(sa, "U", [128, G, 64], BF16)
            bU = Buf("U")
            rr = alloc(sa, "rr", [128, G, 64])
            rs = alloc(sa, "rs", [128, G, 64])
            ww = alloc(sa, "ww", [128, G, 64])
            ws = alloc(sa, "ws", [128, G, 64])
            tmpr = alloc(sa, "tmpr", [128, 16, 64])
            b_r, b_rs, b_w, b_ws, b_tmpr = Buf("r"), Buf("rs"), Buf("w"), Buf("ws"), Buf("tmpr")
            Xb = alloc(sa, "Xb", [128, G, 65], BF16)
            bXb = Buf("Xb")
            Xc = alloc(sa, "Xc", [128, G])
            Xsc = alloc(sa, "Xsc", [128, G])
            ctmp = alloc(sa, "ctmp", [128, 2, G])
            bXc = Buf("Xc", S.GS[0])
            ytmp = alloc(sa, "ytmp", [128, 8, 64])
            bytmp = Buf("ytmp")
            Zt = alloc(sa, "Zt", [128, G, 64], BF16)
            bZ = Buf("Z")
            zT = alloc(sa, "zT", [128, 4, 512], BF16)
            bzT = Buf("zT")
            sig = alloc(sa, "sig", [128, 4, 512])
            bsig = Buf("sig")
            H0 = alloc(sa, "H0", [128, 512])
            H0s = alloc(sa, "H0s", [128, 512])
            hn = alloc(sa, "hn", [128, 4, 128])
            hn2 = alloc(sa, "hn2", [128, 4, 128])
            Hp = alloc(sa, "Hp", [128, G, 16])
            Xf = alloc(sa, "Xf", [128, G, 16])
            xo = alloc(sa, "xo", [128, 4, 128])
            bH = Buf("H0")
            bxo = Buf("xo", S.GS[1])
            V(lambda e: e.memset(Xc[:], 0.0), r=[b_tabp], w=[bXc, b_tab])
            V(lambda e: e.memset(Xsc[:], 0.0), w=[bXc])
            V(lambda e: e.memset(Xb[:], 0.0), w=[bXb])

            blocks = [(i * 512, 512, False) for i in range(4)] + [(SEQ, TS, True)]
            if _os0.environ.get("K1A") == "0":
                blocks = []
            def p1a_stageA(bi):
                t0, n, is_s = blocks[bi]
                hT, bhT = hT2[bi % 2], bhT2[bi % 2]
                uT, buT = uT2[bi % 2], buT2[bi % 2]
                ntile = (n + 127) // 128
                for ti in range(ntile):
                    npart = min(128, n - ti * 128)
                    slot = (bi * 4 + ti) % 2
                    src = I["xs"][:, :] if is_s else I["xp"][t0 + ti * 128:t0 + ti * 128 + 128, :]
                    S.dma("sp", xst[slot][:npart, :], src, writes=[bxst[slot]])
                    o4 = None if is_s else hT[:, :, :].rearrange("p k (s c) -> p k c s", s=8)[:, :, ti * 16:(ti + 1) * 16, :]
                    rmsnorm_hT(xst[slot][:npart, :], bxst[slot], npart, gm[:], hT, bhT,
                               scrA, ti * 128, None, bg=b_tab, out4=o4)
                for ct in range(4):
                    bank = ct
                    for kt in range(8):
                        T(lambda e, ct=ct, kt=kt, bank=bank: e.matmul(
                            PS[bank][:, 0:n], lhsT=winu[:, kt, ct * 128:(ct + 1) * 128], rhs=hT[:, kt, 0:n],
                            start=(kt == 0), stop=(kt == 7)), r=[b_winu, bhT], w=[bPS[bank]])
                    A(lambda e, ct=ct, bank=bank: e.copy(out=uT[:, ct, 0:n], in_=PS[bank][:, 0:n]),
                      r=[bPS[bank]], w=[buT])

            if blocks:
                p1a_stageA(0)
            for bi, (t0, n, is_s) in enumerate(blocks):
                nch = n // 8 if not is_s else 16
                uT, buT = uT2[bi % 2], buT2[bi % 2]
                for gq in range(4):
                    bank = 4 + (gq % 2)
                    for j in range(8):
                        g = gq * 8 + j
                        ct, gl = g // 8, g % 8
                        if not is_s:
                            uv = uT[:, ct, 0:n].rearrange("p (s c) -> p s c", s=8)
                            sig_list = list(range(8))
                        else:
                            uv = uT[:, ct, 0:n].rearrange("p (b t) -> p t b", t=4)
                            sig_list = [4, 5, 6, 7]
                        for si, sg_ in enumerate(sig_list):
                            rhs = uv[:, sg_ if not is_s else si, :]
                            T(lambda e, j=j, gl=gl, sg_=sg_, rhs=rhs, si=si, bank=bank, L=len(sig_list): e.matmul(
                                PS[bank][:, j * 64:j * 64 + nch],
                                lhsT=masters[:, gl, 112 - 16 * sg_:240 - 16 * sg_], rhs=rhs,
                                start=(si == 0), stop=(si == L - 1)),
                              r=[b_tab, buT], w=[bPS[bank]])
                    A(lambda e, gq=gq, bank=bank: e.copy(
                        out=U[:, gq * 8:gq * 8 + 8, 0:nch],
                        in_=PS[bank][:].rearrange("p (j c) -> p j c", j=8)[:, :, 0:nch]),
                      r=[bPS[bank]], w=[bU])
                if not is_s:
                    for hf in range(2):
                        for j in range(16):
                            g = hf * 16 + j
                            for (wt, bk) in ((Wt, 0), (Wst, 2)):
                                bank = bk + j // 8
                                T(lambda e, g=g, j=j, wt=wt, bank=bank: e.matmul(
                                    PS[bank][:, (j % 8) * 64:(j % 8) * 64 + 64], lhsT=wt[:, g, :], rhs=U[:, g, :],
                                    start=True, stop=True), r=[b_tab, bU], w=[bPS[bank]])
                        for q in range(2):
                            gs = slice(hf * 16 + q * 8, hf * 16 + q * 8 + 8)
                            Sv = PS[q][:].rearrange("p (j c) -> p j c", j=8)
                            Ssv = PS[2 + q][:].rearrange("p (j c) -> p j c", j=8)
                            tm = tmpr[:, q * 8:q * 8 + 8, :]
                            V(lambda e, gs=gs, Sv=Sv: e.tensor_tensor(out=rr[:, gs, :], in0=Sv, in1=COSR[:, gs, :],
                                                                     op=ALU.mult), r=[bPS[q], b_tab], w=[b_r])
                            V(lambda e, gs=gs, Ssv=Ssv, tm=tm: e.tensor_tensor(out=tm, in0=Ssv, in1=SINR[:, gs, :],
                                                                              op=ALU.mult),
                              r=[bPS[2 + q], b_tab], w=[b_tmpr])
                            V(lambda e, gs=gs, tm=tm: e.tensor_tensor(out=rr[:, gs, :], in0=rr[:, gs, :], in1=tm,
                                                                     op=ALU.subtract), r=[b_r, b_tmpr], w=[b_r])
                            V(lambda e, gs=gs, Ssv=Ssv: e.tensor_tensor(out=rs[:, gs, :], in0=Ssv, in1=COSR[:, gs, :],
                                                                       op=ALU.mult), r=[bPS[2 + q], b_tab], w=[b_rs])
                            V(lambda e, gs=gs, Sv=Sv, tm=tm: e.tensor_tensor(out=tm, in0=Sv, in1=SINR[:, gs, :],
                                                                            op=ALU.mult),
                              r=[bPS[q], b_tab], w=[b_tmpr])
                            V(lambda e, gs=gs, tm=tm: e.tensor_tensor(out=rs[:, gs, :], in0=rs[:, gs, :], in1=tm,
                                                                     op=ALU.add), r=[b_rs, b_tmpr], w=[b_rs])
                    for g in range(G):
                        rho = MAGJ[:, g, I_A8:I_A8 + 1].to_broadcast([128, 64])
                        V(lambda e, g=g, rho=rho: e.tensor_tensor_scan(
                            out=ww[:, g, :], data0=rho, data1=rr[:, g, :], initial=Xc[:, g:g + 1], op0=ALU.mult,
                            op1=ALU.add), r=[b_r, b_tab, bXc], w=[b_w])
                        V(lambda e, g=g, rho=rho: e.tensor_tensor_scan(
                            out=ws[:, g, :], data0=rho, data1=rs[:, g, :], initial=Xsc[:, g:g + 1], op0=ALU.mult,
                            op1=ALU.add), r=[b_rs, b_tab, bXc], w=[b_ws])
                    if bi + 1 < len(blocks):
                        p1a_stageA(bi + 1)
                    ce, se_ = COSR[:, :, 63], SINR[:, :, 63]
                    we, wse = ww[:, :, 63], ws[:, :, 63]
                    V(lambda e: e.tensor_tensor(out=ctmp[:, 0, :], in0=ce, in1=we, op=ALU.mult), r=[b_w, b_tab], w=[bscr])
                    V(lambda e: e.tensor_tensor(out=ctmp[:, 1, :], in0=se_, in1=wse, op=ALU.mult), r=[b_ws, b_tab], w=[bscr])
                    V(lambda e: e.tensor_tensor(out=Xc[:], in0=ctmp[:, 0, :], in1=ctmp[:, 1, :], op=ALU.add),
                      r=[bscr], w=[bXc])
                    V(lambda e: e.tensor_tensor(out=ctmp[:, 0, :], in0=ce, in1=wse, op=ALU.mult), r=[b_ws, b_tab], w=[bscr])
                    V(lambda e: e.tensor_tensor(out=ctmp[:, 1, :], in0=se_, in1=we, op=ALU.mult), r=[b_w, b_tab], w=[bscr])
                    V(lambda e: e.tensor_tensor(out=Xsc[:], in0=ctmp[:, 0, :], in1=ctmp[:, 1, :], op=ALU.subtract),
                      r=[bscr], w=[bXc])
                    if bi > 0:
                        V(lambda e: e.tensor_copy(out=Xb[:, :, 0], in_=Xb[:, :, 64]), r=[bXb], w=[bXb])
                    V(lambda e: e.tensor_tensor(out=ww[:], in0=ww[:], in1=COSR[:], op=ALU.mult), r=[b_w, b_tab, bXc],
                      w=[b_w])
                    PL(lambda e: e.tensor_tensor(out=ws[:], in0=ws[:], in1=SINR[:], op=ALU.mult), r=[b_ws, b_tab, bXc],
                       w=[b_ws])
                    V(lambda e: e.tensor_tensor(out=Xb[:, :, 1:65], in0=ww[:], in1=ws[:], op=ALU.add),
                      r=[b_w, b_ws], w=[bXb])
                    xprev = lambda g: Xb[:, g, 0:64]
                    bXprev = bXb
                    if bi == 3:
                        S.dma("sp", O["o_s5r_p"].rearrange("g p -> p g"), Xc[0:64, :], reads=[bXc])
                        S.dma("sp", O["o_s5i_p"].rearrange("g p -> p g"), Xc[64:128, :], reads=[bXc])
                else:
                    S.dma("sp", hn[:, :, 0:64], I["s5r"].rearrange("(j r) p -> r j p", r=128), writes=[bH])
                    S.dma("sp", hn[:, :, 64:128], I["s5i"].rearrange("(j r) p -> r j p", r=128), writes=[bH])
                    S.dma("sp", hn2[:, :, 0:64], I["s5i"].rearrange("(j r) p -> r j p", r=128), writes=[bH])
                    S.dma("sp", hn2[:, :, 64:128], I["s5r"].rearrange("(j r) p -> r j p", r=128), writes=[bH])
                    for (src_, dst_, bank) in ((hn, H0, 0), (hn2, H0s, 1)):
                        for j in range(4):
                            T(lambda e, src_=src_, j=j, bank=bank: e.transpose(
                                out=PS[bank][:, j * 128:(j + 1) * 128], in_=src_[:, j, :], identity=identf[:]),
                              r=[bH, b_const], w=[bPS[bank]])
                        V(lambda e, dst_=dst_, bank=bank: e.tensor_copy(out=dst_[:], in_=PS[bank][:]),
                          r=[bPS[bank]], w=[bH])
                    V(lambda e: e.tensor_scalar(out=H0s[0:64, :], in0=H0s[0:64, :], scalar1=-1.0, scalar2=None,
                                                op0=ALU.mult), r=[bH], w=[bH])
                    shs = [128, G, 16]
                    h0v = H0[:].rearrange("p (b g) -> p g b", g=G)
                    h0sv = H0s[:].rearrange("p (b g) -> p g b", g=G)

                    def abc(tab, idx):
                        return tab[:, :, idx].unsqueeze(2).to_broadcast(shs)
                    V(lambda e: e.tensor_tensor(out=Xf[:], in0=h0v, in1=abc(AR, I_AM4), op=ALU.mult), r=[bH, b_tab], w=[bxo])
                    V(lambda e: e.tensor_tensor(out=Hp[:], in0=h0sv, in1=abc(AI, I_AM4), op=ALU.mult), r=[bH, b_tab], w=[bxo])
                    V(lambda e: e.tensor_tensor(out=Xb[:, :, 0:16], in0=Xf[:], in1=Hp[:], op=ALU.add), r=[bxo], w=[bXb])
                    V(lambda e: e.tensor_tensor(out=Xf[:], in0=h0v, in1=abc(AR, I_A4), op=ALU.mult), r=[bH, b_tab], w=[bxo])
                    V(lambda e: e.tensor_tensor(out=Hp[:], in0=h0sv, in1=abc(AI, I_A4), op=ALU.mult), r=[bH, b_tab], w=[bxo])
                    V(lambda e: e.tensor_tensor(out=Xf[:], in0=Xf[:], in1=Hp[:], op=ALU.add), r=[bxo], w=[bxo])
                    for q in range(4):
                        bank = q % 2
                        for j in range(8):
                            g = q * 8 + j
                            T(lambda e, g=g, j=j, bank=bank: e.matmul(
                                PS[bank][:, j * 64:j * 64 + 16], lhsT=Wt[:, g, :], rhs=U[:, g, 0:16],
                                start=True, stop=True), r=[b_tab, bU], w=[bPS[bank]])
                        V(lambda e, q=q, bank=bank: e.tensor_tensor(
                            out=Xf[:, q * 8:q * 8 + 8, :], in0=Xf[:, q * 8:q * 8 + 8, :],
                            in1=PS[bank][:].rearrange("p (j c) -> p j c", j=8)[:, :, 0:16], op=ALU.add),
                          r=[bxo, bPS[bank]], w=[bxo])
                    Xf2 = Xf[:].rearrange("p g b -> p (g b)")
                    for j in range(4):
                        T(lambda e, j=j: e.transpose(out=PS[2][:, j * 128:(j + 1) * 128],
                                                     in_=Xf2[:, j * 128:(j + 1) * 128], identity=identf[:]),
                          r=[bxo, b_const], w=[bPS[2]])
                    V(lambda e: e.tensor_copy(out=xo[:], in_=PS[2][:].rearrange("p (j c) -> p j c", j=4)),
                      r=[bPS[2]], w=[bxo])
                    for j in range(4):
                        for gl in range(8):
                            for (nm, c0) in (("o_s5r_s", 0), ("o_s5i_s", 64)):
                                S.dma("sp", O[nm].rearrange("(b g) p -> g b p", g=G)[8 * j + gl],
                                      xo[gl * 16:gl * 16 + 16, j, c0:c0 + 64], reads=[bxo])
                    xprev = lambda g: Xb[:, g, 0:16]
                    bXprev = bXb
                for gq in range(4):
                    bank = 6 + (gq % 2)
                    for j in range(8):
                        g = gq * 8 + j
                        T(lambda e, g=g, j=j, bank=bank: e.matmul(
                            PS[bank][:, j * 64:j * 64 + nch], lhsT=Tt[:, g, :], rhs=U[:, g, 0:nch],
                            start=True, stop=False), r=[b_tab, bU], w=[bPS[bank]])
                        T(lambda e, g=g, j=j, bank=bank: e.matmul(
                            PS[bank][:, j * 64:j * 64 + nch], lhsT=Vt[:, g, :], rhs=xprev(g)[:, 0:nch],
                            start=False, stop=True), r=[b_tab, bXprev], w=[bPS[bank]])
                    gs = slice(gq * 8, gq * 8 + 8)
                    yv = PS[bank][:].rearrange("p (j c) -> p j c", j=8)[:, :, 0:nch]
                    V(lambda e, gs=gs: e.tensor_tensor(out=ytmp[:, :, 0:nch], in0=U[:, gs, 0:nch],
                                                       in1=DS[:, gs].unsqueeze(2).to_broadcast([128, 8, nch]),
                                                       op=ALU.mult), r=[bU, b_tab], w=[bytmp])
                    V(lambda e, yv=yv: e.tensor_tensor(out=ytmp[:, :, 0:nch], in0=yv, in1=ytmp[:, :, 0:nch],
                                                       op=ALU.add), r=[bPS[bank], bytmp], w=[bytmp])
                    A(lambda e, gs=gs: e.activation(out=Zt[:, gs, 0:nch], in_=ytmp[:, :, 0:nch],
                                                    func=AF.Gelu_apprx_tanh), r=[bytmp], w=[bZ])
                for ct in range(4):
                    bank = ct % 2
                    taus = list(range(8)) if not is_s else [4, 5, 6, 7]
                    for ti_, tau in enumerate(taus):
                        for gl in range(8):
                            g = ct * 8 + gl
                            T(lambda e, g=g, gl=gl, tau=tau, ti_=ti_, bank=bank: e.matmul(
                                PS[bank][:, ti_ * 64:ti_ * 64 + nch],
                                lhsT=masters[:, tau, 112 - 16 * gl:240 - 16 * gl], rhs=Zt[:, g, 0:nch],
                                start=(gl == 0), stop=(gl == 7)), r=[b_tab, bZ], w=[bPS[bank]])
                    if not is_s:
                        A(lambda e, ct=ct, bank=bank: e.copy(
                            out=zT[:, ct, 0:n].rearrange("p (c t) -> p t c", t=8),
                            in_=PS[bank][:].rearrange("p (t c) -> p t c", t=8)), r=[bPS[bank]], w=[bzT])
                    else:
                        A(lambda e, ct=ct, bank=bank: e.copy(
                            out=zT[:, ct, 0:n].rearrange("p (b t) -> p t b", t=4),
                            in_=PS[bank][:].rearrange("p (t c) -> p t c", t=8)[:, 0:4, 0:16]),
                          r=[bPS[bank]], w=[bzT])
                for ct in range(4):
                    bank = 2 + (ct % 2)
                    for kt in range(4):
                        T(lambda e, ct=ct, kt=kt, bank=bank: e.matmul(
                            PS[bank][:, 0:n], lhsT=wglu[:, kt, ct * 128:(ct + 1) * 128], rhs=zT[:, kt, 0:n],
                            start=(kt == 0), stop=(kt == 3)), r=[b_wglu, bzT], w=[bPS[bank]])
                    A(lambda e, ct=ct, bank=bank: e.activation(out=sig[:, ct, 0:n], in_=PS[bank][:, 0:n],
                                                               func=AF.Sigmoid), r=[bPS[bank]], w=[bsig])
                V(lambda e: e.tensor_tensor(out=ssmT[:, :, t0:t0 + n], in0=zT[:, :, 0:n], in1=sig[:, :, 0:n],
                                            op=ALU.mult), r=[bzT, bsig], w=[b_ssmT[bi]])
            S.barrier()
        if dbg:
            with ExitStack() as sd:
                dtmp = alloc(sd, "dtmp", [128, 4, NTOK])
                bd = Buf("dtmp", S.GS[2])
                V(lambda e: e.tensor_copy(out=dtmp[:], in_=ssmT[:]), r=b_ssmT, w=[bd])
                S.dma("sp", O["dbg_ssm"][:, :, :], dtmp[:], reads=[bd])
                S.barrier()
        if stage <= 1:
            S.barrier()
            S.run_block()
            nck.__exit__(None, None, None)
            return nc

        with ExitStack() as sbx:
            x = alloc(sbx, "x", [128, NT, D])
            bx = [Buf("x%d" % n, S.GX) for n in range(NT)]
            for n in range(NTP):
                S.dma("sp", x[:, n, :], I["xp"][n * 128:(n + 1) * 128, :], writes=[bx[n]])
            S.dma("sp", x[0:TS, 16, :], I["xs"][:, :], writes=[bx[16]])
            scrB = make_scr(sbx, "B", [7])
            hT1 = alloc(sbx, "hT1", [128, 8, 128], BF16)
            bhT1 = Buf("hT1")

            def resid_add(n, npart, half, bank):
                V(lambda e: e.tensor_tensor(out=x[:npart, n, half * 512:(half + 1) * 512], in0=PS[bank][:npart, :],
                                            in1=x[:npart, n, half * 512:(half + 1) * 512], op=ALU.add),
                  r=[bPS[bank], bx[n]], w=[bx[n]])

            with ExitStack() as s1:
                wq = alloc(s1, "wqkvg", [128, 8, 2048], BF16)
                wout = alloc(s1, "wout", [128, 8, D], BF16)
                b_wqc = [Buf("wq%d" % c, S.GW[c]) for c in range(4)]
                b_wout = Buf("wout", S.GW[0])
                for c in range(4):
                    for kt in range(8):
                        S.dma("pool", wq[:, kt, c * 512:(c + 1) * 512],
                              I["w_in"][kt * 128:(kt + 1) * 128, 512 + c * 512:512 + (c + 1) * 512], writes=[b_wqc[c]])
                wout_loaded = [False]
                gm2 = alloc(s1, "gm2", [128, 8])
                gn = alloc(s1, "gn", [128, 4])
                rope = alloc(s1, "rope", [128, 3, NT, 64])
                dmp = alloc(s1, "dmp", [128, 512])
                dms = alloc(s1, "dms", [64, 256])
                xi = alloc(s1, "xi", [128, 768])
                zetap = alloc(s1, "zetap", [128, 4])
                zs = alloc(s1, "zs", [64, 64])
                cmask = alloc(s1, "cmask", [128, 16 * 64])
                b_t1 = Buf("tab1")
                S.dma("sp", gm2[:], I["g_mix"].rearrange("(k p) -> p k", p=128), writes=[b_t1])
                S.dma("sp", gn[:], I["ret_gn"].rearrange("(k p) -> p k", p=128), writes=[b_t1])
                for a_ in range(3):
                    S.dma("sp", rope[:, a_, :, :], I["c_rope"][a_], writes=[b_t1])
                S.dma("sp", dmp[:], I["c_dmask_p"][:, :], writes=[b_t1])
                S.dma("sp", dms[:], I["c_dmask_s"][:, :], writes=[b_t1])
                S.dma("sp", xi[:], I["c_xi"][0:1, :].partition_broadcast(128), writes=[b_t1])
                S.dma("sp", zetap[:], I["c_zeta_p"][:, :], writes=[b_t1])
                S.dma("sp", zs[:], I["c_zs"][:, :], writes=[b_t1])
                S.dma("sp", cmask[:], I["c_cmask"][0:1, :].partition_broadcast(128), writes=[b_t1])
                def load_wout():
                    load_w_bf16(wout, b_wout, I["w_out"], 8, D, 0)
                    for k in range(4):
                        V(lambda e: e.tensor_scalar(out=wout[:, 4 + k, :], in0=wout[:, 4 + k, :], scalar1=gn[:, k:k + 1],
                                                    scalar2=None, op0=ALU.mult), r=[b_wout, b_t1], w=[b_wout])
                    wout_loaded[0] = True
                t1q = alloc(s1, "t1q", [128, 512])
                t2q = alloc(s1, "t2q", [128, 512])
                t1k = alloc(s1, "t1k", [128, 512])
                t2k = alloc(s1, "t2k", [128, 512])
                qr = alloc(s1, "qr", [128, 512], BF16)
                kr = alloc(s1, "kr", [128, 512], BF16)
                qT = alloc(s1, "qT", [128, 4, 128], BF16)
                qxT = alloc(s1, "qxT", [128, 4, 128], BF16)
                kT = alloc(s1, "kT", [128, 4, 128], BF16)
                vb = alloc(s1, "vb", [128, 512], BF16)
                vz = alloc(s1, "vz", [128, 512], BF16)
                sg_ = alloc(s1, "sgl", [128, 512])
                sT = alloc(s1, "sT", [128, 4, 128], BF16)
                Sst = alloc(s1, "Sst", [128, 4, 128])
                Sbf = alloc(s1, "Sbf", [128, 4, 128], BF16)
                stats = alloc(s1, "stats", [128, 4, 6])
                mv = alloc(s1, "mv", [128, 4, 2])
                rs4 = alloc(s1, "rs4", [128, 4])
                nb4 = alloc(s1, "nb4", [128, 4])
                on = alloc(s1, "on", [128, 512])
                ret = alloc(s1, "ret", [128, 512], BF16)
                retT = alloc(s1, "retT", [128, 4, 128], BF16)
                S0 = [alloc(s1, "S0_%d" % i, [128, 4, 128]) for i in range(2)]
                S0b = [alloc(s1, "S0b_%d" % i, [128, 4, 128], BF16) for i in range(2)]
                qxm = [alloc(s1, "qxm_%d" % i, [128, 4, 64], BF16) for i in range(2)]
                vzb = [alloc(s1, "vzb_%d" % i, [64, 512], BF16) for i in range(2)]
                Sn = [alloc(s1, "Sn_%d" % i, [128, 4, 128]) for i in range(2)]
                bS0 = [Buf("S0_%d" % i, S.GL[i]) for i in range(2)]
                bS0b = [Buf("S0b_%d" % i) for i in range(2)]
                bqxm = [Buf("qxm%d" % i) for i in range(2)]
                bvzb = [Buf("vzb%d" % i) for i in range(2)]
                bSn = [Buf("Sn%d" % i, S.GS[i]) for i in range(2)]
                (b_t1q, b_t2q, b_t1k, b_t2k, b_qr, b_kr, b_qT, b_qxT, b_kT, b_vb, b_vz, b_sg, b_sT, b_Sst, b_Sbf,
                 b_st, b_on, b_ret, b_retT) = [Buf("p1b%d" % i) for i in range(19)]
                b_Sst.grp = S.GS[2]
                V(lambda e: e.memset(Sst[:], 0.0), w=[b_Sst])
                GC_P = [float(g ** 128) for g in GAM]
                GC_S = [float(g ** 4) for g in GAM]

                import os as _os
                _tl = _os.environ.get("K_TILES")
                _tiles = [int(v) for v in _tl.split(",") if int(v) >= 0] if _tl else list(range(NT))
                _step = int(_os.environ.get("K_STEP", "99"))
                hT1s = [hT1, alloc(s1, "hT1c", [128, 8, 128], BF16)]
                bhT1s = [bhT1, Buf("hT1c")]

                def p1b_norm(n):
                    npt_ = TS if n == 16 else 128
                    rmsnorm_hT(x[:npt_, n, :], bx[n], npt_, gm2[:], hT1s[n % 2], bhT1s[n % 2], scrB, 0, None, bg=b_t1)
                def p1b_proj(n):
                    npt_ = TS if n == 16 else 128
                    hTn, bhTn = hT1s[n % 2], bhT1s[n % 2]
                    for c in range(4):
                        for kt in range(8):
                            T(lambda e: e.matmul(PS[c][:npt_, :], lhsT=hTn[:, kt, 0:npt_],
                                                 rhs=wq[:, kt, c * 512:(c + 1) * 512], start=(kt == 0), stop=(kt == 7)),
                              r=[bhTn, b_wqc[c]], w=[bPS[c]])
                if _tiles:
                    p1b_norm(_tiles[0])
                    p1b_proj(_tiles[0])
                    load_wout()
                for ti_, n in enumerate(_tiles):
                    is_s = (n == 16)
                    npt = TS if is_s else 128
                    tok0 = n * 128
                    hT1, bhT1 = hT1s[n % 2], bhT1s[n % 2]
                    pob = [4, 6, 7, 1] if is_s else [4, 4, 4, 4]

                    def po(h):
                        if is_s:
                            return PS[pob[h]][:npt, 0:128]
                        return PS[4][:npt, h * 128:(h + 1) * 128]
                    if _step <= 1:
                        continue
                    for (bank, t1_, t2_, out_, bt1, bt2, bo) in ((0, t1q, t2q, qr, b_t1q, b_t2q, b_qr),
                                                               (1, t1k, t2k, kr, b_t1k, b_t2k, b_kr)):
                        pv4 = PS[bank][:npt, :].rearrange("p (h a j) -> p h a j", h=4, a=2)
                        t1v = t1_[:npt, :].rearrange("p (h a j) -> p h a j", h=4, a=2)
                        t2v = t2_[:npt, :].rearrange("p (h a j) -> p h a j", h=4, a=2)
                        cosb = rope[:npt, 0, n, :].unsqueeze(1).unsqueeze(1).to_broadcast([npt, 4, 2, 64])
                        sinb = rope[:npt, 1, n, :].unsqueeze(1).to_broadcast([npt, 4, 64])
                        nsinb = rope[:npt, 2, n, :].unsqueeze(1).to_broadcast([npt, 4, 64])
                        V(lambda e: e.tensor_tensor(out=t1v, in0=pv4, in1=cosb, op=ALU.mult), r=[bPS[bank], b_t1], w=[bt1])
                        V(lambda e: e.tensor_tensor(out=t2v[:, :, 0, :], in0=pv4[:, :, 1, :], in1=nsinb, op=ALU.mult),
                          r=[bPS[bank], b_t1], w=[bt2])
                        V(lambda e: e.tensor_tensor(out=t2v[:, :, 1, :], in0=pv4[:, :, 0, :], in1=sinb, op=ALU.mult),
                          r=[bPS[bank], b_t1], w=[bt2])
                        V(lambda e: e.tensor_tensor(out=out_[:npt, :], in0=t1_[:npt, :], in1=t2_[:npt, :], op=ALU.add),
                           r=[bt1, bt2], w=[bo])
                    if _step <= 2:
                        continue
                    A(lambda e: e.copy(out=vb[:npt, :], in_=PS[2][:npt, :]), r=[bPS[2]], w=[b_vb])
                    if not is_s:
                        V(lambda e: e.tensor_tensor(
                            out=vz[:, :].rearrange("p (h e) -> p h e", h=4),
                            in0=PS[2][:, :].rearrange("p (h e) -> p h e", h=4),
                            in1=zetap[:, :].unsqueeze(2).to_broadcast([128, 4, 128]), op=ALU.mult),
                          r=[bPS[2], b_t1], w=[b_vz])
                    A(lambda e: e.activation(out=sg_[:npt, :], in_=PS[3][:npt, :], func=AF.Silu), r=[bPS[3]], w=[b_sg])
                    pv4b = ps_bf(4)
                    pv5b = ps_bf(5)
                    for h in range(4):
                        T(lambda e: e.transpose(out=pv4b[:, h * 128:h * 128 + npt], in_=qr[:npt, h * 128:(h + 1) * 128],
                                                identity=identb[:npt, :npt]), r=[b_qr, b_const], w=[bPS[4]])
                    for h in range(4):
                        T(lambda e: e.transpose(out=pv5b[:, h * 128:h * 128 + npt], in_=kr[:npt, h * 128:(h + 1) * 128],
                                                identity=identb[:npt, :npt]), r=[b_kr, b_const], w=[bPS[5]])
                    q4 = pv4b[:, 0:512].rearrange("p (h t) -> p h t", h=4)[:, :, 0:npt]
                    k4 = pv5b[:, 0:512].rearrange("p (h t) -> p h t", h=4)[:, :, 0:npt]
                    A(lambda e: e.copy(out=qT[:, :, 0:npt], in_=q4), r=[bPS[4]], w=[b_qT])
                    xiv = (xi[:, 0:512].rearrange("p (h t) -> p h t", h=4) if not is_s
                           else xi[:, 512:768].rearrange("p (h t) -> p h t", h=4))
                    V(lambda e: e.tensor_tensor(out=qxT[:, :, 0:npt], in0=q4, in1=xiv, op=ALU.mult),
                      r=[bPS[4], b_t1], w=[b_qxT])
                    A(lambda e: e.copy(out=kT[:, :, 0:npt], in_=k4), r=[bPS[5]], w=[b_kT])
                    if _step <= 3:
                        continue
                    for h in range(4):
                        T(lambda e: e.matmul(PS[6][:npt, h * 128:h * 128 + npt], lhsT=kT[:, h, 0:npt], rhs=qT[:, h, 0:npt],
                                             start=True, stop=True), r=[b_kT, b_qT], w=[bPS[6]])
                    dmv = (dmp[:, :].rearrange("p (h t) -> p h t", h=4) if not is_s
                           else dms[:, :].rearrange("p (h t) -> p h t", h=4))
                    V(lambda e: e.tensor_tensor(out=sT[:npt, :, 0:npt],
                                                in0=PS[6][:npt, :].rearrange("p (h t) -> p h t", h=4)[:, :, 0:npt],
                                                in1=dmv, op=ALU.mult), r=[bPS[6], b_t1], w=[b_sT])
                    if _step <= 4:
                        continue
                    if ti_ + 1 < len(_tiles):
                        p1b_norm(_tiles[ti_ + 1])
                    for h in range(4):
                        only = (n == 0)
                        T(lambda e: e.matmul(po(h), lhsT=sT[:npt, h, 0:npt],
                                             rhs=vb[:npt, h * 128:(h + 1) * 128], start=True, stop=only),
                          r=[b_sT, b_vb], w=[bPS[pob[h]]])
                        if (not is_s) and n > 0:
                            T(lambda e: e.matmul(po(h), lhsT=qxT[:, h, 0:npt],
                                                 rhs=Sbf[:, h, :], start=False, stop=True),
                              r=[b_qxT, b_Sbf], w=[bPS[4]])
                    if not is_s:
                        for h in range(4):
                            T(lambda e: e.matmul(PS[5][:, h * 128:(h + 1) * 128], lhsT=kr[:, h * 128:(h + 1) * 128],
                                                 rhs=vz[:, h * 128:(h + 1) * 128], start=True, stop=True),
                              r=[b_kr, b_vz], w=[bPS[5]])
                        for h in range(4):
                            V(lambda e: e.scalar_tensor_tensor(out=Sst[:, h, :], in0=Sst[:, h, :], scalar=GC_P[h],
                                                               op0=ALU.mult, in1=PS[5][:, h * 128:(h + 1) * 128],
                                                               op1=ALU.add), r=[b_Sst, bPS[5]], w=[b_Sst])
                        A(lambda e: e.copy(out=Sbf[:], in_=Sst[:]), r=[b_Sst], w=[b_Sbf])
                        if n == NTP - 1:
                            S.dma("sp", O["o_ret_p"].rearrange("h d e -> d h e"), Sst[:], reads=[b_Sst])
                    else:
                        for b in range(16):
                            sl = b % 2
                            S.dma("sp", S0[sl][:], I["sret"][b].rearrange("h d e -> d h e"), writes=[bS0[sl]])
                            A(lambda e: e.copy(out=S0b[sl][:], in_=S0[sl][:]), r=[bS0[sl]], w=[bS0b[sl]])
                            V(lambda e: e.tensor_tensor(
                                out=qxm[sl][:], in0=qxT[:, :, 0:64],
                                in1=cmask[:, b * 64:(b + 1) * 64].unsqueeze(1).to_broadcast([128, 4, 64]), op=ALU.mult),
                              r=[b_qxT, b_t1], w=[bqxm[sl]])
                            for h in range(4):
                                T(lambda e: e.matmul(po(h), lhsT=qxm[sl][:, h, :],
                                                     rhs=S0b[sl][:, h, :], start=False, stop=(b == 15)),
                                  r=[bqxm[sl], bS0b[sl]], w=[bPS[pob[h]]])
                            V(lambda e: e.tensor_tensor(
                                out=vzb[sl][:, :].rearrange("p (h e) -> p h e", h=4),
                                in0=PS[2][:64, :].rearrange("p (h e) -> p h e", h=4),
                                in1=zs[:, b * 4:(b + 1) * 4].unsqueeze(2).to_broadcast([64, 4, 128]), op=ALU.mult),
                              r=[bPS[2], b_t1], w=[bvzb[sl]])
                            kvb = 5 if sl == 0 else 0
                            for h in range(4):
                                T(lambda e: e.matmul(PS[kvb][:, h * 128:(h + 1) * 128], lhsT=kr[:64, h * 128:(h + 1) * 128],
                                                     rhs=vzb[sl][:, h * 128:(h + 1) * 128], start=True, stop=True),
                                  r=[b_kr, bvzb[sl]], w=[bPS[kvb]])
                            for h in range(4):
                                V(lambda e: e.scalar_tensor_tensor(out=Sn[sl][:, h, :], in0=S0[sl][:, h, :], scalar=GC_S[h],
                                                                   op0=ALU.mult, in1=PS[kvb][:, h * 128:(h + 1) * 128],
                                                                   op1=ALU.add), r=[bS0[sl], bPS[kvb]], w=[bSn[sl]])
                            S.dma("sp", O["o_ret_s"][b].rearrange("h d e -> d h e"), Sn[sl][:], reads=[bSn[sl]])
                    if _step <= 5:
                        continue
                    if ti_ + 1 < len(_tiles):
                        p1b_proj(_tiles[ti_ + 1])
                    for h in range(4):
                        V(lambda e: e.bn_stats(out=stats[:npt, h, :], in_=po(h)),
                          r=[bPS[pob[h]]], w=[b_st])
                    for h in range(4):
                        V(lambda e: e.bn_aggr(out=mv[:npt, h, :], in_=stats[:npt, h, :]), r=[b_st], w=[b_st])
                    A(lambda e: e.activation(out=rs4[:npt, :], in_=mv[:npt, :, 1], func=AF.Sqrt, scale=1.0,
                                             bias=epsc[:npt, :]), r=[b_st, b_const], w=[b_st])
                    V(lambda e: e.reciprocal(out=rs4[:npt, :], in_=rs4[:npt, :]), r=[b_st], w=[b_st])
                    V(lambda e: e.scalar_tensor_tensor(out=nb4[:npt, :], in0=mv[:npt, :, 0], scalar=-1.0, op0=ALU.mult,
                                                       in1=rs4[:npt, :], op1=ALU.mult), r=[b_st], w=[b_st])
                    for h in range(4):
                        A(lambda e: e.activation(out=on[:npt, h * 128:(h + 1) * 128], in_=po(h),
                                                 func=AF.Identity, scale=rs4[:npt, h:h + 1], bias=nb4[:npt, h:h + 1]),
                          r=[bPS[pob[h]], b_st], w=[b_on])
                    V(lambda e: e.tensor_tensor(out=ret[:npt, :], in0=on[:npt, :], in1=sg_[:npt, :], op=ALU.mult),
                       r=[b_on, b_sg], w=[b_ret])
                    if _step <= 6:
                        continue
                    pv6b = ps_bf(6)
                    for h in range(4):
                        T(lambda e: e.transpose(out=pv6b[:, h * 128:h * 128 + npt], in_=ret[:npt, h * 128:(h + 1) * 128],
                                                identity=identb[:npt, :npt]), r=[b_ret, b_const], w=[bPS[6]])
                    A(lambda e: e.copy(out=retT[:, :, 0:npt],
                                       in_=pv6b[:, 0:512].rearrange("p (h t) -> p h t", h=4)[:, :, 0:npt]),
                      r=[bPS[6]], w=[b_retT])
                    if _step <= 7:
                        continue
                    bi_ = min(n // 4, 4)
                    for half in range(2):
                        bank = 6 + half
                        for kt in range(8):
                            lh = ssmT[:, kt, tok0:tok0 + npt] if kt < 4 else retT[:, kt - 4, 0:npt]
                            T(lambda e: e.matmul(PS[bank][:npt, :], lhsT=lh, rhs=wout[:, kt, half * 512:(half + 1) * 512],
                                                 start=(kt == 0), stop=(kt == 7)),
                              r=[b_ssmT[bi_], b_retT, b_wout], w=[bPS[bank]])
                        resid_add(n, npt, half, bank)
                S.barrier()
            if dbg:
                for n in range(NT):
                    S.dma("sp", O["dbg_x"][:, n, :], x[:, n, :], reads=[bx[n]])
            if stage <= 2:
                S.barrier()
                S.run_block()
                nck.__exit__(None, None, None)
                return nc

            with ExitStack() as s2:
                gx = alloc(s2, "gx", [128, 8])
                gmem = alloc(s2, "gmem", [128, 8])
                ones = alloc(s2, "ones", [128, 128], BF16)
                b_t2 = Buf("tab2")
                S.dma("sp", gx[:], I["g_xattn"].rearrange("(k p) -> p k", p=128), writes=[b_t2])
                S.dma("sp", gmem[:], I["g_mem"].rearrange("(k p) -> p k", p=128), writes=[b_t2])
                V(lambda e: e.memset(ones[:], 1.0), w=[b_t2])
                KT = alloc(s2, "KT", [128, 8, MEM], BF16)
                Vm = alloc(s2, "Vm", [128, 2, D], BF16)
                b_KT, b_Vm = Buf("KT"), Buf("Vm")
                wmq = alloc(s2, "wmq", [128, 8, D], BF16)
                b_wmq, b_wmo = Buf("wmq", S.GW[2]), Buf("wmo", S.GW[3])
                with ExitStack() as s2a:
                    wmk = alloc(s2a, "wmk", [128, 8, D], BF16)
                    wmv = alloc(s2a, "wmv", [128, 8, D], BF16)
                    b_wmk, b_wmv = Buf("wmk", S.GW[0]), Buf("wmv", S.GW[1])
                    load_w_bf16(wmk, b_wmk, I["w_mk"], 8, D, 0)
                    load_w_bf16(wmv, b_wmv, I["w_mv"], 8, D, 0)
                    load_w_bf16(wmq, b_wmq, I["w_mq"], 8, D, 0)
                    mx = [alloc(s2a, "mx%d" % i, [128, D]) for i in range(2)]
                    bmx = [Buf("mx%d" % i, S.GL[i]) for i in range(2)]
                    mhT = alloc(s2a, "mhT", [128, 8, MEM], BF16)
                    b_mhT = Buf("mhT")
                    mo = [alloc(s2a, "mo%d" % i, [128, D]) for i in range(2)]
                    bmo = [Buf("mo%d" % i, S.GS[i]) for i in range(2)]
                    _k2a = int(_os.environ.get("K2A", "9"))
                    for mt in range(2):
                        S.dma("sp", mx[mt][:], I["memp"][mt * 128:(mt + 1) * 128, :], writes=[bmx[mt]])
                        if _k2a >= 1:
                            rmsnorm_hT(mx[mt][:, :], bmx[mt], 128, gmem[:], mhT, b_mhT, scrB, mt * 128, None,
                                       ln=True, bg=b_t2)
                    oi = 0
                    for (wm, bwm, oname, isv) in ((wmk, b_wmk, "o_mk", False), (wmv, b_wmv, "o_mv", True)) if _k2a >= 2 else ():
                        for mt in range(2):
                            sl = oi % 2
                            oi += 1
                            for half in range(2):
                                bank = half
                                for kt in range(8):
                                    T(lambda e: e.matmul(PS[bank][:, :], lhsT=mhT[:, kt, mt * 128:(mt + 1) * 128],
                                                         rhs=wm[:, kt, half * 512:(half + 1) * 512], start=(kt == 0),
                                                         stop=(kt == 7)), r=[b_mhT, bwm], w=[bPS[bank]])
                                A(lambda e: e.copy(out=mo[sl][:, half * 512:(half + 1) * 512], in_=PS[bank][:, :]),
                                  r=[bPS[bank]], w=[bmo[sl]])
                                if isv:
                                    V(lambda e: e.tensor_copy(out=Vm[:, mt, half * 512:(half + 1) * 512], in_=PS[bank][:, :]),
                                      r=[bPS[bank]], w=[b_Vm])
                            S.dma("sp", O[oname][mt * 128:(mt + 1) * 128, :], mo[sl][:], reads=[bmo[sl]])
                    for j in range(8 if _k2a >= 3 else 0):
                        bank = 2 + (j % 2)
                        for kt in range(8):
                            T(lambda e: e.matmul(PS[bank][:, 0:MEM], lhsT=wmk[:, kt, j * 128:(j + 1) * 128],
                                                 rhs=mhT[:, kt, :], start=(kt == 0), stop=(kt == 7)),
                              r=[b_mhT, b_wmk], w=[bPS[bank]])
                        A(lambda e: e.copy(out=KT[:, j, :], in_=PS[bank][:, 0:MEM]), r=[bPS[bank]], w=[b_KT])
                    S.barrier()
                wmo = alloc(s2, "wmo", [128, 8, D], BF16)
                load_w_bf16(wmo, b_wmo, I["w_mo"], 8, D, 0)
                hT4 = alloc(s2, "hT4", [128, 8, 512], BF16)
                qm4 = alloc(s2, "qm4", [128, 8, 512], BF16)
                oT4 = alloc(s2, "oT4", [128, 8, 512], BF16)
                eT4 = [alloc(s2, "eT4_%d" % i, [128, 2, 512], BF16) for i in range(2)]
                rdn4 = [alloc(s2, "rdn4_%d" % i, [128, 512]) for i in range(2)]
                b_hT4, b_qm4, b_oT4 = Buf("hT4"), Buf("qm4"), Buf("oT4")
                b_eT4 = [Buf("eT4_%d" % i) for i in range(2)]
                b_rdn4 = [Buf("rdn4_%d" % i) for i in range(2)]
                Kb = [alloc(s2, "Kb%d" % i, [128, 2, D]) for i in range(2)]
                bKb = [Buf("Kb%d" % i, S.GL[i]) for i in range(2)]
                KbT = [alloc(s2, "KbT%d" % i, [128, 8, MEM], BF16) for i in range(2)]
                bKbT = [Buf("KbT%d" % i) for i in range(2)]
                Vb = [alloc(s2, "Vb%d" % i, [128, 2, D], BF16) for i in range(2)]
                bVb = [Buf("Vb%d" % i, S.GW[i]) for i in range(2)]
                eTs = alloc(s2, "eTs", [128, 2, 4, 64], BF16)
                b_eTs = Buf("eTs")
                qrot = [0]

                def q_proj(nc_):
                    for j in range(8):
                        bank = 5 + (qrot[0] % 3)
                        qrot[0] += 1
                        for kt in range(8):
                            T(lambda e: e.matmul(PS[bank][:, 0:nc_], lhsT=wmq[:, kt, j * 128:(j + 1) * 128],
                                                 rhs=hT4[:, kt, 0:nc_], start=(kt == 0), stop=(kt == 7)),
                              r=[b_wmq, b_hT4], w=[bPS[bank]])
                        A(lambda e: e.activation(out=qm4[:, j, 0:nc_], in_=PS[bank][:, 0:nc_], func=AF.Copy,
                                                 scale=1.0 / 16.0), r=[bPS[bank]], w=[b_qm4])

                def w_mo_resid(n, npt, c0):
                    for half in range(2):
                        bank = 5 + (qrot[0] % 3)
                        qrot[0] += 1
                        for j in range(8):
                            T(lambda e: e.matmul(PS[bank][:npt, :], lhsT=oT4[:, j, c0:c0 + npt],
                                                 rhs=wmo[:, j, half * 512:(half + 1) * 512], start=(j == 0), stop=(j == 7)),
                              r=[b_oT4, b_wmo], w=[bPS[bank]])
                        resid_add(n, npt, half, bank)

                for bi in range(4):
                    for ti in range(4):
                        n = bi * 4 + ti
                        rmsnorm_hT(x[:, n, :], bx[n], 128, gx[:], hT4, b_hT4, scrB, ti * 128, None, ln=True, bg=b_t2)
                    q_proj(512)
                    for h in range(4):
                        par = h % 2
                        for mt in range(2):
                            bank = mt
                            for dt_ in range(2):
                                T(lambda e: e.matmul(PS[bank][:, :], lhsT=KT[:, h * 2 + dt_, mt * 128:(mt + 1) * 128],
                                                     rhs=qm4[:, h * 2 + dt_, :], start=(dt_ == 0), stop=(dt_ == 1)),
                                  r=[b_KT, b_qm4], w=[bPS[bank]])
                            A(lambda e: e.activation(out=eT4[par][:, mt, :], in_=PS[bank][:, :], func=AF.Exp),
                              r=[bPS[bank]], w=[b_eT4[par]])
                        for mt in range(2):
                            T(lambda e: e.matmul(PS[2][:, :], lhsT=ones[:, :], rhs=eT4[par][:, mt, :], start=(mt == 0),
                                                 stop=(mt == 1)), r=[b_t2, b_eT4[par]], w=[bPS[2]])
                        A(lambda e: e.activation(out=rdn4[par][:, :], in_=PS[2][:, :], func=AF.Ln), r=[bPS[2]], w=[b_rdn4[par]])
                        A(lambda e: e.activation(out=rdn4[par][:, :], in_=rdn4[par][:, :], func=AF.Exp, scale=-1.0),
                          r=[b_rdn4[par]], w=[b_rdn4[par]])
                        for dt_ in range(2):
                            bank = 3 + dt_
                            j = h * 2 + dt_
                            for mt in range(2):
                                T(lambda e: e.matmul(PS[bank][:, :], lhsT=Vm[:, mt, j * 128:(j + 1) * 128],
                                                     rhs=eT4[par][:, mt, :], start=(mt == 0), stop=(mt == 1)),
                                  r=[b_Vm, b_eT4[par]], w=[bPS[bank]])
                            V(lambda e: e.tensor_tensor(out=oT4[:, j, :], in0=PS[bank][:, :], in1=rdn4[par][:, :], op=ALU.mult),
                              r=[bPS[bank], b_rdn4[par]], w=[b_oT4])
                    for ti in range(4):
                        w_mo_resid(bi * 4 + ti, 128, ti * 128)
                n = 16
                rmsnorm_hT(x[:TS, n, :], bx[n], TS, gx[:], hT4, b_hT4, scrB, 0, None, ln=True, bg=b_t2)
                q_proj(TS)
                rden_s = rdn4[0][:, 0:256].rearrange("p (h t) -> p h t", h=4)
                for b in range(16):
                    sl = b % 2
                    S.dma("sp", Kb[sl][:], I["ck"][b].rearrange("(mt p) d -> p mt d", p=128), writes=[bKb[sl]])
                    for q4 in range(4):
                        bank = 2 + (q4 % 2)
                        for i4 in range(4):
                            idx = q4 * 4 + i4
                            j, mt = idx // 2, idx % 2
                            T(lambda e: e.transpose(out=PS[bank][:, i4 * 128:(i4 + 1) * 128],
                                                    in_=Kb[sl][:, mt, j * 128:(j + 1) * 128], identity=identf[:]),
                              r=[bKb[sl], b_const], w=[bPS[bank]])
                        A(lambda e: e.copy(
                            out=KbT[sl][:, 2 * q4:2 * q4 + 2, :].rearrange("p j (m t) -> p j m t", m=2),
                            in_=PS[bank][:, :].rearrange("p (j m t) -> p j m t", j=2, m=2)),
                          r=[bPS[bank]], w=[bKbT[sl]])
                    for h in range(4):
                        for mt in range(2):
                            c0 = mt * 256 + h * 64 + 4 * b
                            for dt_ in range(2):
                                T(lambda e: e.matmul(PS[4][:, c0:c0 + 4],
                                                     lhsT=KbT[sl][:, h * 2 + dt_, mt * 128:(mt + 1) * 128],
                                                     rhs=qm4[:, h * 2 + dt_, 4 * b:4 * b + 4], start=(dt_ == 0),
                                                     stop=(dt_ == 1)), r=[bKbT[sl], b_qm4], w=[bPS[4]])
                A(lambda e: e.activation(out=eTs[:].rearrange("p m h t -> p (m h t)"), in_=PS[4][:, :], func=AF.Exp),
                  r=[bPS[4]], w=[b_eTs])
                for h in range(4):
                    for mt in range(2):
                        T(lambda e: e.matmul(PS[0][:, h * 64:(h + 1) * 64], lhsT=ones[:, :], rhs=eTs[:, mt, h, :],
                                             start=(mt == 0), stop=(mt == 1)), r=[b_t2, b_eTs], w=[bPS[0]])
                V(lambda e: e.reciprocal(out=rden_s, in_=PS[0][:, 0:256].rearrange("p (h t) -> p h t", h=4)),
                  r=[bPS[0]], w=[b_rdn4[0]])
                for b in range(16):
                    sl = b % 2
                    for mt in range(2):
                        S.dma("pool", Vb[sl][:, mt, :], I["cv"][b, mt * 128:(mt + 1) * 128, :], writes=[bVb[sl]])
                    for j in range(8):
                        h = j // 2
                        for mt in range(2):
                            T(lambda e: e.matmul(PS[1][:, j * 64 + 4 * b:j * 64 + 4 * b + 4],
                                                 lhsT=Vb[sl][:, mt, j * 128:(j + 1) * 128],
                                                 rhs=eTs[:, mt, h, 4 * b:4 * b + 4], start=(mt == 0), stop=(mt == 1)),
                              r=[bVb[sl], b_eTs], w=[bPS[1]])
                V(lambda e: e.tensor_tensor(
                    out=oT4[:, :, 0:64].rearrange("p (h a) t -> p h a t", a=2),
                    in0=PS[1][:, :].rearrange("p (h a t) -> p h a t", h=4, a=2),
                    in1=rden_s.unsqueeze(2).to_broadcast([128, 4, 2, 64]), op=ALU.mult),
                  r=[bPS[1], b_rdn4[0]], w=[b_oT4])
                w_mo_resid(16, TS, 0)
                S.barrier()
            if stage <= 3:
                if dbg:
                    for n in range(NT):
                        S.dma("sp", O["dbg_x"][:, n, :], x[:, n, :], reads=[bx[n]])
                S.barrier()
                S.run_block()
                nck.__exit__(None, None, None)
                return nc

            with ExitStack() as s3:
                gml = alloc(s3, "gml", [128, 8])
                b_t3 = Buf("tab3")
                S.dma("sp", gml[:], I["g_mlp"].rearrange("(k p) -> p k", p=128), writes=[b_t3])
                hTa = alloc(s3, "hTa", [128, 8, NTOK], BF16)
                b_hTa = [Buf("hTa%d" % n) for n in range(NT)]
                wup = [alloc(s3, "wup%d" % i, [128, 8, 512], BF16) for i in range(2)]
                wdn = [alloc(s3, "wdn%d" % i, [128, 4, D], BF16) for i in range(2)]
                bwup = [Buf("wup%d" % i, S.GW[i]) for i in range(2)]
                bwdn = [Buf("wdn%d" % i, S.GW[2 + i]) for i in range(2)]
                rl = [alloc(s3, "rl%d" % i, [128, 512]) for i in range(2)]
                brl = [Buf("rl%d" % i) for i in range(2)]
                aT = [alloc(s3, "aT%d" % i, [128, 4, 512], BF16) for i in range(2)]
                baT = [Buf("aT%d" % i) for i in range(2)]

                def load_fc(fc):
                    sl = fc % 2
                    for kt in range(8):
                        S.dma("pool", wup[sl][:, kt, :], I["w_up"][kt * 128:(kt + 1) * 128, fc * 512:(fc + 1) * 512],
                              writes=[bwup[sl]])
                    for ft in range(4):
                        S.dma("pool", wdn[sl][:, ft, :], I["w_down"][fc * 512 + ft * 128:fc * 512 + (ft + 1) * 128, :],
                              writes=[bwdn[sl]])
                load_fc(0)
                scrB["pb"] = [7, 6]
                for n in range(NT):
                    npt = TS if n == 16 else 128
                    rmsnorm_hT(x[:npt, n, :], bx[n], npt, gml[:], hTa, b_hTa[n], scrB, n * 128, None, ln=True, bg=b_t3)
                gf = alloc(s3, "gf", [128, D])
                b_gf = Buf("gf")
                S.dma("sp", gf[:], I["g_final"].rearrange("(o d) -> o d", o=1).partition_broadcast(128), writes=[b_gf])
                yst = [alloc(s3, "yst%d" % i, [128, D]) for i in range(3)]
                byst = [Buf("yst%d" % i, S.GS[i]) for i in range(3)]

                def final_norm(n):
                    npt = TS if n == 16 else 128
                    sl = n % 3
                    k4 = n % 2
                    sq, ss, rstd, bscr = scrB["sq"][k4], scrB["ss"][k4], scrB["rstd"][k4], scrB["ba"][k4]
                    A(lambda e: e.activation(out=sq[:npt, :], in_=x[:npt, n, :], func=AF.Square, accum_out=ss[:npt, :]),
                      r=[bx[n]], w=[bscr])
                    A(lambda e: e.activation(out=rstd[:npt, :], in_=ss[:npt, :], func=AF.Ln, scale=1.0 / D,
                                             bias=epsc[:npt, :]), r=[bscr, b_const], w=[bscr])
                    A(lambda e: e.activation(out=rstd[:npt, :], in_=rstd[:npt, :], func=AF.Exp, scale=-0.5),
                      r=[bscr], w=[bscr])
                    V(lambda e: e.scalar_tensor_tensor(out=yst[sl][:npt, :], in0=x[:npt, n, :], scalar=rstd[:npt, :],
                                                       op0=ALU.mult, in1=gf[:npt, :], op1=ALU.mult),
                      r=[bx[n], bscr, b_gf], w=[byst[sl]])
                    if n < 16:
                        S.dma("sp", O["yp"][n * 128:(n + 1) * 128, :], yst[sl][:, :], reads=[byst[sl]])
                    else:
                        S.dma("sp", O["ys"][:, :], yst[sl][:TS, :], reads=[byst[sl]])

                blocks3 = [(i * 512, 512) for i in range(4)] + [(SEQ, TS)]
                items = [(fc, blk) for fc in range(8) for blk in blocks3]
                ctr = {"ri": 0, "di": 0}
                load_fc(1)

                def mlp_up(i):
                    fc, (t0, nn) = items[i]
                    sl, asl = fc % 2, i % 2
                    tiles = list(range(t0 // 128, t0 // 128 + (nn + 127) // 128))
                    for ft in range(4):
                        bank = ft
                        for kt in range(8):
                            T(lambda e: e.matmul(PS[bank][:, 0:nn], lhsT=wup[sl][:, kt, ft * 128:(ft + 1) * 128],
                                                 rhs=hTa[:, kt, t0:t0 + nn], start=(kt == 0), stop=(kt == 7)),
                              r=[bwup[sl]] + [b_hTa[t] for t in tiles], w=[bPS[bank]])
                        rsl = ctr["ri"] % 2
                        ctr["ri"] += 1
                        A(lambda e: e.activation(out=rl[rsl][:, 0:nn], in_=PS[bank][:, 0:nn], func=AF.Relu),
                          r=[bPS[bank]], w=[brl[rsl]])
                        V(lambda e: e.tensor_tensor(out=aT[asl][:, ft, 0:nn], in0=rl[rsl][:, 0:nn], in1=rl[rsl][:, 0:nn],
                                                    op=ALU.mult), r=[brl[rsl]], w=[baT[asl]])

                def mlp_down(i):
                    fc, (t0, nn) = items[i]
                    sl, asl = fc % 2, i % 2
                    tiles = list(range(t0 // 128, t0 // 128 + (nn + 127) // 128))
                    for ti, tl in enumerate(tiles):
                        npt = TS if tl == 16 else 128
                        for half in range(2):
                            bank = 4 + (ctr["di"] % 4)
                            ctr["di"] += 1
                            for ft in range(4):
                                T(lambda e: e.matmul(PS[bank][:npt, :], lhsT=aT[asl][:, ft, ti * 128:ti * 128 + npt],
                                                     rhs=wdn[sl][:, ft, half * 512:(half + 1) * 512], start=(ft == 0),
                                                     stop=(ft == 3)), r=[baT[asl], bwdn[sl]], w=[bPS[bank]])
                            resid_add(tl, npt, half, bank)
                        if fc == 7:
                            final_norm(tl)

                mlp_up(0)
                for i in range(len(items)):
                    if i + 1 < len(items):
                        mlp_up(i + 1)
                    mlp_down(i)
                    fc = items[i][0]
                    if (i + 1 == len(items) or items[i + 1][0] != fc) and fc + 2 < 8:
                        load_fc(fc + 2)
                S.barrier()
            if dbg:
                for n in range(NT):
                    S.dma("sp", O["dbg_x"][:, n, :], x[:, n, :], reads=[bx[n]])
            S.barrier()
            S.run_block()
            nck.__exit__(None, None, None)
    return nc


_NC = None


def kernel(**inputs):
    global _NC
    if _NC is None:
        _NC = build()
    maps = _in_maps(inputs)
    res = run_bass_kernel_spmd(_NC, maps, core_ids=list(range(8)))
    R = res.results
    f = np.float32

    def cat(name, shape=None):
        return np.stack([np.asarray(R[c][name], f) for c in range(8)])
    y_prompt = cat("yp")
    y_sample = cat("ys").reshape(128, 4, D)
    s5r_p = cat("o_s5r_p")[None]
    s5i_p = cat("o_s5i_p")[None]
    ret_p = cat("o_ret_p")[None]
    mk_p = cat("o_mk").reshape(8, MEM, 4, 256)[None]
    mv_p = cat("o_mv").reshape(8, MEM, 4, 256)[None]
    s5r_s = cat("o_s5r_s").reshape(128, G, 64)[None]
    s5i_s = cat("o_s5i_s").reshape(128, G, 64)[None]
    ret_s = cat("o_ret_s").reshape(128, 4, 128, 128)[None]
    return (y_prompt, y_sample, s5r_p, s5i_p, ret_p, mk_p, mv_p, s5r_s, s5i_s, ret_s)


def _in_maps(inputs):
    cst = _consts()
    f = np.float32
    maps = []
    w = {}
    for k in W_NAMES:
        a = np.asarray(inputs[k], f)
        if k != "g_final":
            a = a[0]
        w[k] = np.ascontiguousarray(a.reshape(W_SHAPES[k]))
    for c in range(8):
        m = dict(w)
        m.update(cst)
        b0 = 16 * c
        m["xp"] = np.ascontiguousarray(np.asarray(inputs["x_prompt"], f)[c])
        m["xs"] = np.ascontiguousarray(np.asarray(inputs["x_sample"], f)[b0:b0 + 16].reshape(TS, D))
        m["memp"] = np.ascontiguousarray(np.asarray(inputs["mem_prompt"], f)[c])
        m["s5r"] = np.ascontiguousarray(np.asarray(inputs["state_s5_re"], f)[0, b0:b0 + 16].reshape(512, 64))
        m["s5i"] = np.ascontiguousarray(np.asarray(inputs["state_s5_im"], f)[0, b0:b0 + 16].reshape(512, 64))
        m["sret"] = np.ascontiguousarray(np.asarray(inputs["state_ret"], f)[0, b0:b0 + 16])
        m["ck"] = np.ascontiguousarray(np.asarray(inputs["cache_mem_k"], f)[0, b0:b0 + 16].reshape(16, MEM, D))
        m["cv"] = np.ascontiguousarray(np.asarray(inputs["cache_mem_v"], f)[0, b0:b0 + 16].reshape(16, MEM, D))
        maps.append(m)
    return maps
```

```python
import numpy as np
import concourse.bass as bass
import concourse.mybir as mybir
from concourse.bass_utils import run_bass_kernel_spmd
from contextlib import ExitStack

F32 = mybir.dt.float32
BF16 = mybir.dt.bfloat16
AF = mybir.ActivationFunctionType
ALU = mybir.AluOpType

D = 1024
SEQ = 2048
NTP = 16
TS = 64
NT = 17
NTOK = SEQ + TS
G = 32
DFF = 4096
MEM = 256
EPS = 1e-6
PAST = 16384.0
MAGIC = 12582912.0
TWO_PI = float(2.0 * np.pi)
ML = [7, 6, 5, 4, 3, 2, 1, 0, 1, 2, 3, 4, 5, 6, 7, 8, -4, 0.5]
K1 = len(ML)
I_A1, I_A8, I_A4, I_AM4, I_HALF = 8, 15, 3, 16, 17
GAM = [1.0 - 2.0 ** (-5.0 - h) for h in range(4)]


class Grp:
    __slots__ = ("sem", "cnt", "sealed")


class Buf:
    __slots__ = ("w", "r", "name", "grp", "ps")

    def __init__(self, name="", grp=None, ps=False):
        self.w = None
        self.r = []
        self.name = name
        self.grp = grp
        self.ps = ps


class _Rec:
    def __init__(self):
        self.call = None

    def __getattr__(self, name):
        def f(*a, **kw):
            self.call = (name, a, kw)
            return self
        return f


class Sched:
    ENG = ("pe", "dve", "act", "pool", "sp")

    def __init__(self, nc, stack, self_sync=("dve", "act", "pool")):
        self.nc = nc
        self.stack = stack
        self.prog = {k: [] for k in self.ENG}
        self.cnt = {k: 0 for k in self.ENG}
        self.waited = {k: {} for k in self.ENG}
        self.sem = {}
        self.nsem = 0
        for k in ("pe", "dve", "act", "pool"):
            self.sem[k] = self.new_sem("c_" + k)
        self.self_sync = set(self_sync)
        self.groups = []
        self.GC = self.group("gc")
        self.GP = self.group("gp")
        self.GW = [self.group("gw%d" % i) for i in range(4)]
        self.GX = self.group("gx")
        self.GL = [self.group("gl%d" % i) for i in range(2)]
        self.GS = [self.group("gs%d" % i) for i in range(3)]

    def group(self, name):
        g = Grp()
        g.sem = self.new_sem(name)
        g.cnt = 0
        g.sealed = False
        self.groups.append(g)
        return g

    def new_sem(self, name):
        self.nsem += 1
        assert self.nsem < 98, "too many semaphores"
        return self.stack.enter_context(self.nc.semaphore(name + "_%d" % self.nsem))

    def _waits(self, eng, deps):
        w = self.waited[eng]
        need = {}
        dd = []
        for d in deps:
            if isinstance(d, Grp):
                d.sealed = True
                dd.append((d.sem, d.cnt))
            else:
                dd.append(d)
        deps = dd
        for (s, v) in deps:
            if eng in self.sem and s is self.sem[eng] and eng not in self.self_sync:
                continue
            k = id(s)
            if w.get(k, 0) >= v:
                continue
            if k not in need or need[k][1] < v:
                need[k] = (s, v)
        for k, (s, v) in need.items():
            w[k] = v
            self.prog[eng].append(lambda e, s=s, v=v: e.wait_ge(s, v))

    def op(self, eng, fn, reads=(), writes=()):
        deps = []
        for b in reads:
            if b.w is not None:
                deps.append(b.w)
            if b.ps:
                mys = self.sem[eng]
                deps.extend(d for d in b.r if not (isinstance(d, tuple) and d[0] is mys))
        for b in writes:
            if b.w is not None:
                deps.append(b.w)
            deps.extend(b.r)
        self._waits(eng, deps)
        self.cnt[eng] += 1
        c = self.cnt[eng]
        s = self.sem[eng]
        rec = _Rec()
        fn(rec)
        name, a, kw = rec.call
        self.prog[eng].append(lambda e, name=name, a=a, kw=kw, s=s: getattr(e, name)(*a, **kw).then_inc(s, 1))
        for b in reads:
            b.r.append((s, c))
        for b in writes:
            b.w = (s, c)
            b.r = []

    def dma(self, q, out, in_, reads=(), writes=(), **kw):
        tb = writes[0] if writes else reads[0]
        g = tb.grp
        if g is None:
            g = self.GP if q == "pool" else (self.GC if writes else self.GS[0])
        deps = []
        for b in reads:
            if b.w is not None:
                deps.append(b.w)
        for b in writes:
            if b.w is not None and b.w is not g:
                deps.append(b.w)
            deps.extend(b.r)
        self._waits(q, deps)
        if g.sealed and g.cnt > 0:
            self._waits(q, [(g.sem, g.cnt)])
        g.sealed = False
        g.cnt += 16
        s = g.sem
        self.prog[q].append(
            lambda e, out=out, in_=in_, s=s, kw=kw: e.dma_start(out=out, in_=in_, **kw).then_inc(s, 16))
        for b in reads:
            b.r.append(g)
        for b in writes:
            b.w = g
            b.r = []

    def barrier(self, engines=None):
        deps = [(self.sem[k], self.cnt[k]) for k in ("pe", "dve", "act", "pool") if self.cnt[k] > 0]
        deps += [g for g in self.groups if g.cnt > 0]
        for e in (engines or self.ENG):
            self._waits(e, deps)

    def run_block(self):
        nc = self.nc
        with nc.Block() as block:
            @block.sync
            def _(e):
                for t in self.prog["sp"]:
                    t(e)

            @block.tensor
            def _(e):
                for t in self.prog["pe"]:
                    t(e)

            @block.vector
            def _(e):
                for t in self.prog["dve"]:
                    t(e)

            @block.scalar
            def _(e):
                for t in self.prog["act"]:
                    t(e)

            @block.gpsimd
            def _(e):
                for t in self.prog["pool"]:
                    t(e)


_CONSTS = None


def _consts():
    global _CONSTS
    if _CONSTS is not None:
        return _CONSTS
    f = np.float32
    c = {}
    c["c_ident"] = np.eye(128, dtype=f)
    m = np.zeros((8, 128, 240), f)
    for a in range(8):
        for i in range(16):
            m[a, 16 * a + i, 112 + i] = 1.0
    c["c_masters"] = m
    ml = np.array(ML, np.float64)
    rows = np.concatenate([ml / (2 * np.pi), ml, 8.0 * (np.arange(64) + 1) / (2 * np.pi)])
    c["c_rows"] = rows.astype(f)[None, :]
    sg = np.zeros((128, 2), f)
    sg[:64, 0] = 1.0
    sg[64:, 0] = -1.0
    sg[:64, 1] = -1.0
    sg[64:, 1] = 1.0
    c["c_sgn"] = sg
    inv = (f(10000.0) ** (-(np.arange(64, dtype=f) / f(64.0)))).astype(f)
    pos = np.zeros((128, NT), f)
    for n in range(NTP):
        pos[:, n] = 128 * n + np.arange(128)
    pos[:64, 16] = PAST + (np.arange(64) % 4)
    ang = (pos[:, :, None] * inv[None, None, :]).astype(f).astype(np.float64)
    c["c_rope"] = np.stack([np.cos(ang), np.sin(ang), -np.sin(ang)]).astype(f)
    lg = np.log(np.array(GAM, np.float64))
    sc = 128.0 ** -0.5
    idx = np.arange(128)
    dm = np.zeros((128, 4, 128), np.float64)
    diff = idx[None, :] - idx[:, None]
    for h in range(4):
        dm[:, h, :] = np.where(diff >= 0, np.exp(np.maximum(diff, 0) * lg[h]), 0.0) * sc
    c["c_dmask_p"] = dm.reshape(128, 512).astype(f)
    ds_ = np.zeros((64, 4, 64), np.float64)
    r = np.arange(64)
    bb = r // 4
    tt = r % 4
    same = bb[:, None] == bb[None, :]
    dts = tt[None, :] - tt[:, None]
    for h in range(4):
        ds_[:, h, :] = np.where(same & (dts >= 0), np.exp(np.maximum(dts, 0) * lg[h]), 0.0) * sc
    c["c_dmask_s"] = ds_.reshape(64, 256).astype(f)
    xi_p = np.stack([np.exp((idx + 1.0) * lg[h]) * sc for h in range(4)])
    xi_s = np.stack([np.exp((tt + 1.0) * lg[h]) * sc for h in range(4)])
    c["c_xi"] = np.concatenate([xi_p.reshape(-1), xi_s.reshape(-1)]).astype(f)[None, :]
    zp = np.stack([np.exp((127.0 - idx) * lg[h]) for h in range(4)], axis=1)
    c["c_zeta_p"] = zp.astype(f)
    zs = np.zeros((64, 16, 4), np.float64)
    for h in range(4):
        for b in range(16):
            zs[:, b, h] = np.where(bb == b, np.exp((3.0 - tt) * lg[h]), 0.0)
    c["c_zs"] = zs.reshape(64, 64).astype(f)
    cm = np.zeros((16, 64), f)
    for b in range(16):
        cm[b, 4 * b:4 * b + 4] = 1.0
    c["c_cmask"] = cm.reshape(1, -1)
    _CONSTS = c
    return c


W_NAMES = ["g_mix", "w_in", "lam_re", "lam_im", "log_dt", "b_re", "b_im", "c_re", "c_im", "d_skip", "w_glu",
           "ret_gn", "w_out", "g_xattn", "g_mem", "w_mq", "w_mk", "w_mv", "w_mo", "g_mlp", "w_up", "w_down",
           "g_final"]
W_SHAPES = {"g_mix": [D], "w_in": [D, 2560], "lam_re": [G, 64], "lam_im": [G, 64], "log_dt": [G],
            "b_re": [G, 64, 16], "b_im": [G, 64, 16], "c_re": [G * 16, 64], "c_im": [G * 16, 64], "d_skip": [512],
            "w_glu": [512, 512], "ret_gn": [512], "w_out": [D, D], "g_xattn": [D], "g_mem": [D], "w_mq": [D, D],
            "w_mk": [D, D], "w_mv": [D, D], "w_mo": [D, D], "g_mlp": [D], "w_up": [D, DFF], "w_down": [DFF, D],
            "g_final": [D]}
IN_SHAPES = {"xp": [SEQ, D], "xs": [TS, D], "memp": [MEM, D], "s5r": [512, 64], "s5i": [512, 64],
             "sret": [16, 4, 128, 128], "ck": [16, MEM, D], "cv": [16, MEM, D]}
OUT_SHAPES = {"yp": [SEQ, D], "ys": [TS, D], "o_s5r_p": [G, 64], "o_s5i_p": [G, 64], "o_ret_p": [4, 128, 128],
              "o_mk": [MEM, D], "o_mv": [MEM, D], "o_s5r_s": [512, 64], "o_s5i_s": [512, 64],
              "o_ret_s": [16, 4, 128, 128]}


def build(stage=99, dbg=False):
    nc = bass.Bass("TRN2", target_bir_lowering=False)
    cst = _consts()
    I = {}
    for k, shp in list(IN_SHAPES.items()) + list(W_SHAPES.items()):
        I[k] = nc.dram_tensor(k, shp, F32, kind="ExternalInput").ap()
    for k, v in cst.items():
        I[k] = nc.dram_tensor(k, list(v.shape), F32, kind="ExternalInput").ap()
    O = {}
    for k, shp in OUT_SHAPES.items():
        O[k] = nc.dram_tensor(k, shp, F32, kind="ExternalOutput").ap()
    if dbg:
        O["dbg_ssm"] = nc.dram_tensor("dbg_ssm", [128, 4, NTOK], F32, kind="ExternalOutput").ap()
        O["dbg_x"] = nc.dram_tensor("dbg_x", [128, NT, D], F32, kind="ExternalOutput").ap()

    with ExitStack() as st:
        S = Sched(nc, st)

        def alloc(stack, name, shape, dt=F32):
            return stack.enter_context(nc.sbuf_tensor(name, shape, dt))

        def palloc(stack, name, shape, dt=F32):
            return stack.enter_context(nc.psum_tensor(name, shape, dt))

        def V(fn, r=(), w=()):
            S.op("dve", fn, reads=r, writes=w)

        def A(fn, r=(), w=()):
            S.op("act", fn, reads=r, writes=w)

        import os as _os0
        _nopool = _os0.environ.get("K_NOPOOL") == "1"

        def PL(fn, r=(), w=()):
            S.op("dve" if _nopool else "pool", fn, reads=r, writes=w)

        def T(fn, r=(), w=()):
            S.op("pe", fn, reads=r, writes=w)

        nck = nc.allow_non_contiguous_dma(reason="small param layout loads")
        nck.__enter__()

        identb = alloc(st, "identb", [128, 128], BF16)
        identf = alloc(st, "identf", [128, 128], F32)
        sgn = alloc(st, "sgn", [128, 2])
        epsc = alloc(st, "epsc", [128, 1])
        ssmT = alloc(st, "ssmT", [128, 4, NTOK], BF16)
        b_const = Buf("const")
        b_ssmT = [Buf("ssmT%d" % i) for i in range(5)]
        b_constp = Buf("constp")
        S.dma("pool", identb[:], I["c_ident"][:, :], writes=[b_constp])
        S.dma("sp", identf[:], I["c_ident"][:, :], writes=[b_const])
        S.dma("sp", sgn[:], I["c_sgn"][:, :], writes=[b_const])
        V(lambda e: e.memset(epsc[:], EPS), r=[b_constp], w=[b_const])
        PS = [palloc(st, "ps%d" % i, [128, 512], F32) for i in range(8)]
        bPS = [Buf("ps%d" % i, ps=True) for i in range(8)]

        def ps_bf(i):
            return PS[i][:].bitcast(BF16)

        def make_scr(stack, tag, pbanks):
            d = {"i": 0, "pb": list(pbanks)}
            d["sq"] = [alloc(stack, "sq%s%d" % (tag, i), [128, D], BF16) for i in range(2)]
            d["ss"] = [alloc(stack, "ss%s%d" % (tag, i), [128, 1]) for i in range(2)]
            d["rstd"] = [alloc(stack, "rstd%s%d" % (tag, i), [128, 1]) for i in range(2)]
            d["hb"] = [alloc(stack, "hb%s%d" % (tag, i), [128, D], BF16) for i in range(2)]
            d["ba"] = [Buf("ba%s%d" % (tag, i)) for i in range(2)]
            d["bh"] = [Buf("bh%s%d" % (tag, i)) for i in range(2)]
            return d

        def rmsnorm_hT(xt_ap, bx, npart, gcol, hT_ap, bhT, scr, col0, ph, ln=False, bg=None, out4=None):
            k = scr["i"] % 2
            pbank = scr["pb"][scr["i"] % len(scr["pb"])]
            scr["i"] += 1
            sq, ss, rstd, hb = scr["sq"][k], scr["ss"][k], scr["rstd"][k], scr["hb"][k]
            ba, bh = scr["ba"][k], scr["bh"][k]
            A(lambda e: e.activation(out=sq[:npart, :], in_=xt_ap, func=AF.Square, accum_out=ss[:npart, :]),
              r=[bx], w=[ba])
            if ln:
                A(lambda e: e.activation(out=rstd[:npart, :], in_=ss[:npart, :], func=AF.Ln, scale=1.0 / D,
                                         bias=epsc[:npart, :]), r=[ba, b_const], w=[ba])
                A(lambda e: e.activation(out=rstd[:npart, :], in_=rstd[:npart, :], func=AF.Exp, scale=-0.5),
                  r=[ba], w=[ba])
            else:
                A(lambda e: e.activation(out=rstd[:npart, :], in_=ss[:npart, :], func=AF.Sqrt, scale=1.0 / D,
                                         bias=epsc[:npart, :]), r=[ba, b_const], w=[ba])
                V(lambda e: e.reciprocal(out=rstd[:npart, :], in_=rstd[:npart, :]), r=[ba], w=[ba])
            V(lambda e: e.tensor_scalar(out=hb[:npart, :], in0=xt_ap, scalar1=rstd[:npart, :], scalar2=None,
                                        op0=ALU.mult), r=[bx, ba], w=[bh])
            pv = ps_bf(pbank)
            for kt in range(8):
                T(lambda e, kt=kt: e.transpose(out=pv[:, kt * 128:kt * 128 + npart],
                                               in_=hb[:npart, kt * 128:(kt + 1) * 128],
                                               identity=identb[:npart, :npart]),
                  r=[bh, b_const], w=[bPS[pbank]])
            if out4 is not None:
                V(lambda e: e.tensor_tensor(
                    out=out4, in0=pv.rearrange("p (k c s) -> p k c s", k=8, s=8),
                    in1=gcol.unsqueeze(2).unsqueeze(3).to_broadcast([128, 8, 16, 8]), op=ALU.mult),
                  r=[bPS[pbank], b_const] + ([bg] if bg is not None else []), w=[bhT])
                return
            V(lambda e: e.tensor_tensor(
                out=hT_ap[:, :, col0:col0 + npart],
                in0=pv.rearrange("p (k t) -> p k t", k=8)[:, :, 0:npart],
                in1=gcol.unsqueeze(2).to_broadcast([128, 8, npart]), op=ALU.mult),
              r=[bPS[pbank], b_const] + ([bg] if bg is not None else []), w=[bhT])

        def load_w_bf16(dst, bdst, src, kt_n, ncols, c0=0):
            for kt in range(kt_n):
                for cc in range(0, ncols, 1024):
                    w_ = min(1024, ncols - cc)
                    S.dma("pool", dst[:, kt, cc:cc + w_], src[kt * 128:(kt + 1) * 128, c0 + cc:c0 + cc + w_],
                          writes=[bdst])

        with ExitStack() as sa:
            Wt = alloc(sa, "Wt", [128, G, 128], BF16)
            Wst = alloc(sa, "Wst", [128, G, 128], BF16)
            Tt = alloc(sa, "Tt", [128, G, 128], BF16)
            Vt = alloc(sa, "Vt", [128, G, 128], BF16)
            COSR = alloc(sa, "COSR", [128, G, 64])
            SINR = alloc(sa, "SINR", [128, G, 64])
            masters = alloc(sa, "masters", [128, 8, 240], BF16)
            AR = alloc(sa, "AR", [128, G, K1])
            AI = alloc(sa, "AI", [128, G, K1])
            MAGJ = alloc(sa, "MAGJ", [128, G, K1])
            DS = alloc(sa, "DS", [128, G])
            gm = alloc(sa, "gm", [128, 8])
            winu = alloc(sa, "winu", [128, 8, 512], BF16)
            wglu = alloc(sa, "wglu", [128, 4, 512], BF16)
            b_tab = Buf("s5tab")
            b_winu = Buf("winu", S.GW[0])
            b_wglu = Buf("wglu", S.GW[1])
            b_tabp = Buf("s5tabp")
            S.dma("pool", masters[:], I["c_masters"].rearrange("a k j -> k a j"), writes=[b_tabp])
            S.dma("sp", gm[:], I["g_mix"].rearrange("(k p) -> p k", p=128), writes=[b_tab])
            for tau in range(8):
                S.dma("sp", DS[16 * tau:16 * tau + 16, :], I["d_skip"].rearrange("(g h) -> h g", h=16),
                      writes=[b_tab])
            load_w_bf16(winu, b_winu, I["w_in"], 8, 512, 0)
            load_w_bf16(wglu, b_wglu, I["w_glu"], 4, 512, 0)

            with ExitStack() as s0:
                rows = alloc(s0, "rows", [128, 2 * K1 + 64])
                LR = alloc(s0, "LR", [128, G])
                LI = alloc(s0, "LI", [128, G])
                DT = alloc(s0, "DT", [128, G])
                LRDT = alloc(s0, "LRDT", [128, G])
                LIDT = alloc(s0, "LIDT", [128, G])
                tA = alloc(s0, "tA", [128, G, 64])
                tB = alloc(s0, "tB", [128, G, 64])
                tC = alloc(s0, "tC", [128, G, 64])
                COSJ = alloc(s0, "COSJ", [128, G, K1])
                SINJ = alloc(s0, "SINJ", [128, G, K1])
                sm = alloc(s0, "sm", [128, 12, G])
                Br1 = alloc(s0, "Br1", [128, G, 16])
                Br2 = alloc(s0, "Br2", [128, G, 16])
                BB1 = alloc(s0, "BB1", [128, G, 16])
                BB2 = alloc(s0, "BB2", [128, G, 16])
                tb1 = alloc(s0, "tb1", [128, G, 16])
                big1 = alloc(s0, "big1", [128, G, 128])
                big2 = alloc(s0, "big2", [128, G, 128])
                WTpad = alloc(s0, "WTpad", [128, G, 256], BF16)
                WTs = alloc(s0, "WTs", [128, G, 128], BF16)
                CN1 = alloc(s0, "CN1", [128, 4, 128])
                CN2 = alloc(s0, "CN2", [128, 4, 128])
                CMa = alloc(s0, "CMa", [128, G, 16])
                CMb = alloc(s0, "CMb", [128, G, 16])
                CMab = alloc(s0, "CMab", [128, G, 16], BF16)
                b0 = Buf("p0in")
                bt = Buf("p0tmp")
                S.dma("sp", rows[:], I["c_rows"][0:1, :].partition_broadcast(128), writes=[b0])
                for hf in range(2):
                    S.dma("sp", LR[64 * hf:64 * hf + 64, :], I["lam_re"].rearrange("g p -> p g"), writes=[b0])
                    S.dma("sp", LI[64 * hf:64 * hf + 64, :], I["lam_im"].rearrange("g p -> p g"), writes=[b0])
                S.dma("sp", DT[:], I["log_dt"].rearrange("(o g) -> o g", o=1).partition_broadcast(128), writes=[b0])
                S.dma("sp", Br1[0:64], I["b_re"].rearrange("g p h -> p g h"), writes=[b0])
                S.dma("sp", Br1[64:128], I["b_im"].rearrange("g p h -> p g h"), writes=[b0])
                S.dma("sp", Br2[0:64], I["b_im"].rearrange("g p h -> p g h"), writes=[b0])
                S.dma("sp", Br2[64:128], I["b_re"].rearrange("g p h -> p g h"), writes=[b0])
                S.dma("sp", CN1[:, :, 0:64], I["c_re"].rearrange("(c r) p -> r c p", r=128), writes=[b0])
                S.dma("sp", CN1[:, :, 64:128], I["c_im"].rearrange("(c r) p -> r c p", r=128), writes=[b0])
                S.dma("sp", CN2[:, :, 0:64], I["c_im"].rearrange("(c r) p -> r c p", r=128), writes=[b0])
                S.dma("sp", CN2[:, :, 64:128], I["c_re"].rearrange("(c r) p -> r c p", r=128), writes=[b0])
                MT1 = rows[:, 0:K1]
                MLr = rows[:, K1:2 * K1]
                MRT = rows[:, 2 * K1:2 * K1 + 64]
                A(lambda e: e.activation(out=DT[:], in_=DT[:], func=AF.Exp), r=[b0], w=[b0])
                V(lambda e: e.tensor_tensor(out=LRDT[:], in0=LR[:], in1=DT[:], op=ALU.mult), r=[b0], w=[bt])
                V(lambda e: e.tensor_tensor(out=LIDT[:], in0=LI[:], in1=DT[:], op=ALU.mult), r=[b0], w=[bt])

                def trig(mt_ap, K, cos_out, sin_out):
                    shp = [128, G, K]
                    a_, b_, c_ = tA[:, :, 0:K], tB[:, :, 0:K], tC[:, :, 0:K]
                    V(lambda e: e.tensor_tensor(out=a_, in0=LIDT[:].unsqueeze(2).to_broadcast(shp),
                                                in1=mt_ap.unsqueeze(1).to_broadcast(shp), op=ALU.mult),
                      r=[bt, b0], w=[bt])
                    for (outp, off) in ((sin_out, 0.0), (cos_out, 0.25)):
                        if outp is None:
                            continue
                        V(lambda e, off=off: e.tensor_scalar(out=c_, in0=a_, scalar1=off, scalar2=None,
                                                             op0=ALU.add), r=[bt], w=[bt])
                        V(lambda e: e.tensor_scalar(out=b_, in0=c_, scalar1=MAGIC, scalar2=None, op0=ALU.add),
                          r=[bt], w=[bt])
                        V(lambda e: e.tensor_scalar(out=b_, in0=b_, scalar1=MAGIC, scalar2=None, op0=ALU.subtract),
                          r=[bt], w=[bt])
                        V(lambda e: e.tensor_tensor(out=c_, in0=c_, in1=b_, op=ALU.subtract), r=[bt], w=[bt])
                        A(lambda e, outp=outp: e.activation(out=outp, in_=c_, func=AF.Sin, scale=TWO_PI),
                          r=[bt], w=[b_tab])

                trig(MT1, K1, COSJ[:], SINJ[:])
                trig(MRT, 64, COSR[:], SINR[:])
                shpj = [128, G, K1]
                V(lambda e: e.tensor_tensor(out=MAGJ[:], in0=LRDT[:].unsqueeze(2).to_broadcast(shpj),
                                            in1=MLr.unsqueeze(1).to_broadcast(shpj), op=ALU.mult),
                  r=[bt, b0], w=[b_tab])
                A(lambda e: e.activation(out=MAGJ[:], in_=MAGJ[:], func=AF.Exp), r=[b_tab], w=[b_tab])
                V(lambda e: e.tensor_tensor(out=AR[:], in0=MAGJ[:], in1=COSJ[:], op=ALU.mult), r=[b_tab], w=[b_tab])
                V(lambda e: e.tensor_tensor(out=AI[:], in0=MAGJ[:], in1=SINJ[:], op=ALU.mult), r=[b_tab], w=[b_tab])
                em1, shalf, cm1, am1r, ai1, den, fr, fi, t0_, t1_ = [sm[:, i, :] for i in range(10)]
                x_ = LRDT[:]
                V(lambda e: e.tensor_scalar(out=em1, in0=x_, scalar1=0.2, scalar2=1.0, op0=ALU.mult, op1=ALU.add),
                  r=[bt], w=[bt])
                for cf in (0.25, 1.0 / 3.0, 0.5):
                    V(lambda e: e.tensor_tensor(out=em1, in0=em1, in1=x_, op=ALU.mult), r=[bt], w=[bt])
                    V(lambda e, cf=cf: e.tensor_scalar(out=em1, in0=em1, scalar1=cf, scalar2=1.0, op0=ALU.mult,
                                                       op1=ALU.add), r=[bt], w=[bt])
                V(lambda e: e.tensor_tensor(out=em1, in0=em1, in1=x_, op=ALU.mult), r=[bt], w=[bt])
                V(lambda e: e.tensor_copy(out=shalf, in_=SINJ[:, :, I_HALF]), r=[b_tab], w=[bt])
                V(lambda e: e.scalar_tensor_tensor(out=cm1, in0=shalf, scalar=-2.0, op0=ALU.mult, in1=shalf,
                                                   op1=ALU.mult), r=[bt], w=[bt])
                V(lambda e: e.tensor_tensor(out=am1r, in0=em1, in1=COSJ[:, :, I_A1], op=ALU.mult), r=[bt, b_tab], w=[bt])
                V(lambda e: e.tensor_tensor(out=am1r, in0=am1r, in1=cm1, op=ALU.add), r=[bt], w=[bt])
                V(lambda e: e.tensor_copy(out=ai1, in_=AI[:, :, I_A1]), r=[b_tab], w=[bt])
                V(lambda e: e.tensor_tensor(out=den, in0=LR[:], in1=LR[:], op=ALU.mult), r=[b0], w=[bt])
                V(lambda e: e.tensor_tensor(out=t0_, in0=LI[:], in1=LI[:], op=ALU.mult), r=[b0], w=[bt])
                V(lambda e: e.tensor_tensor(out=den, in0=den, in1=t0_, op=ALU.add), r=[bt], w=[bt])
                V(lambda e: e.reciprocal(out=den, in_=den), r=[bt], w=[bt])
                V(lambda e: e.tensor_tensor(out=fr, in0=am1r, in1=LR[:], op=ALU.mult), r=[bt, b0], w=[bt])
                V(lambda e: e.tensor_tensor(out=t0_, in0=ai1, in1=LI[:], op=ALU.mult), r=[bt, b0], w=[bt])
                V(lambda e: e.tensor_tensor(out=fr, in0=fr, in1=t0_, op=ALU.add), r=[bt], w=[bt])
                V(lambda e: e.tensor_tensor(out=fr, in0=fr, in1=den, op=ALU.mult), r=[bt], w=[bt])
                V(lambda e: e.tensor_tensor(out=fi, in0=ai1, in1=LR[:], op=ALU.mult), r=[bt, b0], w=[bt])
                V(lambda e: e.tensor_tensor(out=t0_, in0=am1r, in1=LI[:], op=ALU.mult), r=[bt, b0], w=[bt])
                V(lambda e: e.tensor_tensor(out=fi, in0=fi, in1=t0_, op=ALU.subtract), r=[bt], w=[bt])
                V(lambda e: e.tensor_tensor(out=fi, in0=fi, in1=den, op=ALU.mult), r=[bt], w=[bt])
                V(lambda e: e.tensor_scalar(out=Br2[:], in0=Br2[:], scalar1=sgn[:, 1:2], scalar2=None, op0=ALU.mult),
                  r=[b0, b_const], w=[b0])
                shb = [128, G, 16]
                frb = fr.unsqueeze(2).to_broadcast(shb)
                fib = fi.unsqueeze(2).to_broadcast(shb)
                V(lambda e: e.tensor_tensor(out=BB1[:], in0=Br1[:], in1=frb, op=ALU.mult), r=[b0, bt], w=[bt])
                V(lambda e: e.tensor_tensor(out=tb1[:], in0=Br2[:], in1=fib, op=ALU.mult), r=[b0, bt], w=[bt])
                V(lambda e: e.tensor_tensor(out=BB1[:], in0=BB1[:], in1=tb1[:], op=ALU.add), r=[bt], w=[bt])
                V(lambda e: e.tensor_tensor(out=BB2[:], in0=Br2[:], in1=frb, op=ALU.mult), r=[b0, bt], w=[bt])
                V(lambda e: e.tensor_tensor(out=tb1[:], in0=Br1[:], in1=fib, op=ALU.mult), r=[b0, bt], w=[bt])
                V(lambda e: e.tensor_tensor(out=BB2[:], in0=BB2[:], in1=tb1[:], op=ALU.subtract), r=[bt], w=[bt])
                sh4 = [128, G, 8, 16]
                arv = AR[:, :, 0:8].unsqueeze(3).to_broadcast(sh4)
                aiv = AI[:, :, 0:8].unsqueeze(3).to_broadcast(sh4)
                bb1 = BB1[:].unsqueeze(2).to_broadcast(sh4)
                bb2 = BB2[:].unsqueeze(2).to_broadcast(sh4)
                g1 = big1[:].rearrange("p g (s h) -> p g s h", s=8)
                g2 = big2[:].rearrange("p g (s h) -> p g s h", s=8)
                V(lambda e: e.memset(WTpad[:], 0.0), w=[bt])
                V(lambda e: e.tensor_tensor(out=g1, in0=arv, in1=bb1, op=ALU.mult), r=[b_tab, bt], w=[bt])
                V(lambda e: e.tensor_tensor(out=g2, in0=aiv, in1=bb2, op=ALU.mult), r=[b_tab, bt], w=[bt])
                V(lambda e: e.tensor_tensor(out=WTpad[:, :, 0:128], in0=big1[:], in1=big2[:], op=ALU.add),
                  r=[bt], w=[bt])
                V(lambda e: e.tensor_tensor(out=g1, in0=arv, in1=bb2, op=ALU.mult), r=[b_tab, bt], w=[bt])
                V(lambda e: e.tensor_tensor(out=g2, in0=aiv, in1=bb1, op=ALU.mult), r=[b_tab, bt], w=[bt])
                V(lambda e: e.tensor_tensor(out=WTs[:], in0=big1[:], in1=big2[:], op=ALU.subtract), r=[bt], w=[bt])
                for (src_fn, dstt) in ((lambda g: WTpad[:, g, 0:128], Wt), (lambda g: WTs[:, g, :], Wst)):
                    for gq in range(8):
                        bank = gq % 2
                        pv = ps_bf(bank)
                        for j in range(4):
                            g = gq * 4 + j
                            T(lambda e, g=g, j=j, pv=pv, src_fn=src_fn: e.transpose(
                                out=pv[:, j * 128:(j + 1) * 128], in_=src_fn(g), identity=identb[:]),
                              r=[bt, b_const], w=[bPS[bank]])
                        A(lambda e, gq=gq, pv=pv, dstt=dstt: e.copy(
                            out=dstt[:, gq * 4:gq * 4 + 4, :], in_=pv[:, 0:512].rearrange("p (j c) -> p j c", j=4)),
                          r=[bPS[bank]], w=[b_tab])
                for (CN, CM, col) in ((CN1, CMa, 0), (CN2, CMb, None)):
                    for c4 in range(4):
                        bank = 2 + (c4 % 2)
                        T(lambda e, CN=CN, c4=c4, bank=bank: e.transpose(out=PS[bank][:, 0:128], in_=CN[:, c4, :],
                                                                         identity=identf[:]),
                          r=[b0, b_const], w=[bPS[bank]])
                        if col is not None:
                            V(lambda e, CM=CM, c4=c4, bank=bank: e.tensor_scalar(
                                out=CM[:, c4 * 8:(c4 + 1) * 8, :],
                                in0=PS[bank][:, 0:128].rearrange("p (g h) -> p g h", g=8),
                                scalar1=sgn[:, 0:1], scalar2=None, op0=ALU.mult),
                              r=[bPS[bank], b_const], w=[bt])
                        else:
                            V(lambda e, CM=CM, c4=c4, bank=bank: e.tensor_scalar(
                                out=CM[:, c4 * 8:(c4 + 1) * 8, :],
                                in0=PS[bank][:, 0:128].rearrange("p (g h) -> p g h", g=8),
                                scalar1=-1.0, scalar2=None, op0=ALU.mult),
                              r=[bPS[bank]], w=[bt])
                V(lambda e: e.tensor_copy(out=CMab[:], in_=CMa[:]), r=[bt], w=[bt])
                afw = AR[:, :, 8:16].unsqueeze(3).to_broadcast(sh4)
                aifw = AI[:, :, 8:16].unsqueeze(3).to_broadcast(sh4)
                cma = CMa[:].unsqueeze(2).to_broadcast(sh4)
                cmb = CMb[:].unsqueeze(2).to_broadcast(sh4)
                V(lambda e: e.tensor_tensor(out=g1, in0=afw, in1=cma, op=ALU.mult), r=[b_tab, bt], w=[bt])
                V(lambda e: e.tensor_tensor(out=g2, in0=aifw, in1=cmb, op=ALU.mult), r=[b_tab, bt], w=[bt])
                V(lambda e: e.tensor_tensor(out=Vt[:], in0=big1[:], in1=big2[:], op=ALU.add), r=[bt], w=[b_tab])
                for gq in range(8):
                    bank = 4 + (gq % 2)
                    for j in range(4):
                        g = gq * 4 + j
                        for tau in range(8):
                            c0 = (7 - tau) * 16
                            T(lambda e, g=g, j=j, tau=tau, c0=c0, bank=bank: e.matmul(
                                PS[bank][:, j * 128 + tau * 16:j * 128 + tau * 16 + 16],
                                lhsT=WTpad[:, g, c0:c0 + 128], rhs=CMab[:, g, :], start=True, stop=True),
                              r=[bt], w=[bPS[bank]])
                    A(lambda e, gq=gq, bank=bank: e.copy(
                        out=Tt[:, gq * 4:gq * 4 + 4, :], in_=PS[bank][:].rearrange("p (j c) -> p j c", j=4)),
                      r=[bPS[bank]], w=[b_tab])
                S.barrier()
            xst = [alloc(sa, "xst%d" % i, [128, D]) for i in range(2)]
            bxst = [Buf("xst%d" % i, S.GL[i]) for i in range(2)]
            scrA = make_scr(sa, "A", [7])
            bscr = Buf("scrA")
            hT2 = [alloc(sa, "hT_%d" % i, [128, 8, 512], BF16) for i in range(2)]
            bhT2 = [Buf("hT_%d" % i) for i in range(2)]
            uT2 = [alloc(sa, "uT_%d" % i, [128, 4, 512], BF16) for i in range(2)]
            buT2 = [Buf("uT_%d" % i) for i in range(2)]
            U = alloc(sa, "U", [128, G, 64], BF16)
            bU = Buf("U")
            rr = alloc(sa, "rr", [128, G, 64])
            rs = alloc(sa, "rs", [128, G, 64])
            ww = alloc(sa, "ww", [128, G, 64])
            ws = alloc(sa, "ws", [128, G, 64])
            tmpr = alloc(sa, "tmpr", [128, 16, 64])
            b_r, b_rs, b_w, b_ws, b_tmpr = Buf("r"), Buf("rs"), Buf("w"), Buf("ws"), Buf("tmpr")
            Xb = alloc(sa, "Xb", [128, G, 65], BF16)
            bXb = Buf("Xb")
            Xc = alloc(sa, "Xc", [128, G])
            Xsc = alloc(sa, "Xsc", [128, G])
            ctmp = alloc(sa, "ctmp", [128, 2, G])
            bXc = Buf("Xc", S.GS[0])
            ytmp = alloc(sa, "ytmp", [128, 8, 64])
            bytmp = Buf("ytmp")
            Zt = alloc(sa, "Zt", [128, G, 64], BF16)
            bZ = Buf("Z")
            zT = alloc(sa, "zT", [128, 4, 512], BF16)
            bzT = Buf("zT")
            sig = alloc(sa, "sig", [128, 4, 512])
            bsig = Buf("sig")
            H0 = alloc(sa, "H0", [128, 512])
            H0s = alloc(sa, "H0s", [128, 512])
            hn = alloc(sa, "hn", [128, 4, 128])
            hn2 = alloc(sa, "hn2", [128, 4, 128])
            Hp = alloc(sa, "Hp", [128, G, 16])
            Xf = alloc(sa, "Xf", [128, G, 16])
            xo = alloc(sa, "xo", [128, 4, 128])
            bH = Buf("H0")
            bxo = Buf("xo", S.GS[1])
            V(lambda e: e.memset(Xc[:], 0.0), r=[b_tabp], w=[bXc, b_tab])
            V(lambda e: e.memset(Xsc[:], 0.0), w=[bXc])
            V(lambda e: e.memset(Xb[:], 0.0), w=[bXb])

            blocks = [(i * 512, 512, False) for i in range(4)] + [(SEQ, TS, True)]
            if _os0.environ.get("K1A") == "0":
                blocks = []
            def p1a_stageA(bi):
                t0, n, is_s = blocks[bi]
                hT, bhT = hT2[bi % 2], bhT2[bi % 2]
                uT, buT = uT2[bi % 2], buT2[bi % 2]
                ntile = (n + 127) // 128
                for ti in range(ntile):
                    npart = min(128, n - ti * 128)
                    slot = (bi * 4 + ti) % 2
                    src = I["xs"][:, :] if is_s else I["xp"][t0 + ti * 128:t0 + ti * 128 + 128, :]
                    S.dma("sp", xst[slot][:npart, :], src, writes=[bxst[slot]])
                    o4 = None if is_s else hT[:, :, :].rearrange("p k (s c) -> p k c s", s=8)[:, :, ti * 16:(ti + 1) * 16, :]
                    rmsnorm_hT(xst[slot][:npart, :], bxst[slot], npart, gm[:], hT, bhT,
                               scrA, ti * 128, None, bg=b_tab, out4=o4)
                for ct in range(4):
                    bank = ct
                    for kt in range(8):
                        T(lambda e, ct=ct, kt=kt, bank=bank: e.matmul(
                            PS[bank][:, 0:n], lhsT=winu[:, kt, ct * 128:(ct + 1) * 128], rhs=hT[:, kt, 0:n],
                            start=(kt == 0), stop=(kt == 7)), r=[b_winu, bhT], w=[bPS[bank]])
                    A(lambda e, ct=ct, bank=bank: e.copy(out=uT[:, ct, 0:n], in_=PS[bank][:, 0:n]),
                      r=[bPS[bank]], w=[buT])

            if blocks:
                p1a_stageA(0)
            for bi, (t0, n, is_s) in enumerate(blocks):
                nch = n // 8 if not is_s else 16
                uT, buT = uT2[bi % 2], buT2[bi % 2]
                for gq in range(4):
                    bank = 4 + (gq % 2)
                    for j in range(8):
                        g = gq * 8 + j
                        ct, gl = g // 8, g % 8
                        if not is_s:
                            uv = uT[:, ct, 0:n].rearrange("p (s c) -> p s c", s=8)
                            sig_list = list(range(8))
                        else:
                            uv = uT[:, ct, 0:n].rearrange("p (b t) -> p t b", t=4)
                            sig_list = [4, 5, 6, 7]
                        for si, sg_ in enumerate(sig_list):
                            rhs = uv[:, sg_ if not is_s else si, :]
                            T(lambda e, j=j, gl=gl, sg_=sg_, rhs=rhs, si=si, bank=bank, L=len(sig_list): e.matmul(
                                PS[bank][:, j * 64:j * 64 + nch],
                                lhsT=masters[:, gl, 112 - 16 * sg_:240 - 16 * sg_], rhs=rhs,
                                start=(si == 0), stop=(si == L - 1)),
                              r=[b_tab, buT], w=[bPS[bank]])
                    A(lambda e, gq=gq, bank=bank: e.copy(
                        out=U[:, gq * 8:gq * 8 + 8, 0:nch],
                        in_=PS[bank][:].rearrange("p (j c) -> p j c", j=8)[:, :, 0:nch]),
                      r=[bPS[bank]], w=[bU])
                if not is_s:
                    for hf in range(2):
                        for j in range(16):
                            g = hf * 16 + j
                            for (wt, bk) in ((Wt, 0), (Wst, 2)):
                                bank = bk + j // 8
                                T(lambda e, g=g, j=j, wt=wt, bank=bank: e.matmul(
                                    PS[bank][:, (j % 8) * 64:(j % 8) * 64 + 64], lhsT=wt[:, g, :], rhs=U[:, g, :],
                                    start=True, stop=True), r=[b_tab, bU], w=[bPS[bank]])
                        for q in range(2):
                            gs = slice(hf * 16 + q * 8, hf * 16 + q * 8 + 8)
                            Sv = PS[q][:].rearrange("p (j c) -> p j c", j=8)
                            Ssv = PS[2 + q][:].rearrange("p (j c) -> p j c", j=8)
                            tm = tmpr[:, q * 8:q * 8 + 8, :]
                            V(lambda e, gs=gs, Sv=Sv: e.tensor_tensor(out=rr[:, gs, :], in0=Sv, in1=COSR[:, gs, :],
                                                                     op=ALU.mult), r=[bPS[q], b_tab], w=[b_r])
                            V(lambda e, gs=gs, Ssv=Ssv, tm=tm: e.tensor_tensor(out=tm, in0=Ssv, in1=SINR[:, gs, :],
                                                                              op=ALU.mult),
                              r=[bPS[2 + q], b_tab], w=[b_tmpr])
                            V(lambda e, gs=gs, tm=tm: e.tensor_tensor(out=rr[:, gs, :], in0=rr[:, gs, :], in1=tm,
                                                                     op=ALU.subtract), r=[b_r, b_tmpr], w=[b_r])
                            V(lambda e, gs=gs, Ssv=Ssv: e.tensor_tensor(out=rs[:, gs, :], in0=Ssv, in1=COSR[:, gs, :],
                                                                       op=ALU.mult), r=[bPS[2 + q], b_tab], w=[b_rs])
                            V(lambda e, gs=gs, Sv=Sv, tm=tm: e.tensor_tensor(out=tm, in0=Sv, in1=SINR[:, gs, :],
                                                                            op=ALU.mult),
                              r=[bPS[q], b_tab], w=[b_tmpr])
                            V(lambda e, gs=gs, tm=tm: e.tensor_tensor(out=rs[:, gs, :], in0=rs[:, gs, :], in1=tm,
                                                                     op=ALU.add), r=[b_rs, b_tmpr], w=[b_rs])
                    for g in range(G):
                        rho = MAGJ[:, g, I_A8:I_A8 + 1].to_broadcast([128, 64])
                        V(lambda e, g=g, rho=rho: e.tensor_tensor_scan(
                            out=ww[:, g, :], data0=rho, data1=rr[:, g, :], initial=Xc[:, g:g + 1], op0=ALU.mult,
                            op1=ALU.add), r=[b_r, b_tab, bXc], w=[b_w])
                        V(lambda e, g=g, rho=rho: e.tensor_tensor_scan(
                            out=ws[:, g, :], data0=rho, data1=rs[:, g, :], initial=Xsc[:, g:g + 1], op0=ALU.mult,
                            op1=ALU.add), r=[b_rs, b_tab, bXc], w=[b_ws])
                    if bi + 1 < len(blocks):
                        p1a_stageA(bi + 1)
                    ce, se_ = COSR[:, :, 63], SINR[:, :, 63]
                    we, wse = ww[:, :, 63], ws[:, :, 63]
                    V(lambda e: e.tensor_tensor(out=ctmp[:, 0, :], in0=ce, in1=we, op=ALU.mult), r=[b_w, b_tab], w=[bscr])
                    V(lambda e: e.tensor_tensor(out=ctmp[:, 1, :], in0=se_, in1=wse, op=ALU.mult), r=[b_ws, b_tab], w=[bscr])
                    V(lambda e: e.tensor_tensor(out=Xc[:], in0=ctmp[:, 0, :], in1=ctmp[:, 1, :], op=ALU.add),
                      r=[bscr], w=[bXc])
                    V(lambda e: e.tensor_tensor(out=ctmp[:, 0, :], in0=ce, in1=wse, op=ALU.mult), r=[b_ws, b_tab], w=[bscr])
                    V(lambda e: e.tensor_tensor(out=ctmp[:, 1, :], in0=se_, in1=we, op=ALU.mult), r=[b_w, b_tab], w=[bscr])
                    V(lambda e: e.tensor_tensor(out=Xsc[:], in0=ctmp[:, 0, :], in1=ctmp[:, 1, :], op=ALU.subtract),
                      r=[bscr], w=[bXc])
                    if bi > 0:
                        V(lambda e: e.tensor_copy(out=Xb[:, :, 0], in_=Xb[:, :, 64]), r=[bXb], w=[bXb])
                    V(lambda e: e.tensor_tensor(out=ww[:], in0=ww[:], in1=COSR[:], op=ALU.mult), r=[b_w, b_tab, bXc],
                      w=[b_w])
                    PL(lambda e: e.tensor_tensor(out=ws[:], in0=ws[:], in1=SINR[:], op=ALU.mult), r=[b_ws, b_tab, bXc],
                       w=[b_ws])
                    V(lambda e: e.tensor_tensor(out=Xb[:, :, 1:65], in0=ww[:], in1=ws[:], op=ALU.add),
                      r=[b_w, b_ws], w=[bXb])
                    xprev = lambda g: Xb[:, g, 0:64]
                    bXprev = bXb
                    if bi == 3:
                        S.dma("sp", O["o_s5r_p"].rearrange("g p -> p g"), Xc[0:64, :], reads=[bXc])
                        S.dma("sp", O["o_s5i_p"].rearrange("g p -> p g"), Xc[64:128, :], reads=[bXc])
                else:
                    S.dma("sp", hn[:, :, 0:64], I["s5r"].rearrange("(j r) p -> r j p", r=128), writes=[bH])
                    S.dma("sp", hn[:, :, 64:128], I["s5i"].rearrange("(j r) p -> r j p", r=128), writes=[bH])
                    S.dma("sp", hn2[:, :, 0:64], I["s5i"].rearrange("(j r) p -> r j p", r=128), writes=[bH])
                    S.dma("sp", hn2[:, :, 64:128], I["s5r"].rearrange("(j r) p -> r j p", r=128), writes=[bH])
                    for (src_, dst_, bank) in ((hn, H0, 0), (hn2, H0s, 1)):
                        for j in range(4):
                            T(lambda e, src_=src_, j=j, bank=bank: e.transpose(
                                out=PS[bank][:, j * 128:(j + 1) * 128], in_=src_[:, j, :], identity=identf[:]),
                              r=[bH, b_const], w=[bPS[bank]])
                        V(lambda e, dst_=dst_, bank=bank: e.tensor_copy(out=dst_[:], in_=PS[bank][:]),
                          r=[bPS[bank]], w=[bH])
                    V(lambda e: e.tensor_scalar(out=H0s[0:64, :], in0=H0s[0:64, :], scalar1=-1.0, scalar2=None,
                                                op0=ALU.mult), r=[bH], w=[bH])
                    shs = [128, G, 16]
                    h0v = H0[:].rearrange("p (b g) -> p g b", g=G)
                    h0sv = H0s[:].rearrange("p (b g) -> p g b", g=G)

                    def abc(tab, idx):
                        return tab[:, :, idx].unsqueeze(2).to_broadcast(shs)
                    V(lambda e: e.tensor_tensor(out=Xf[:], in0=h0v, in1=abc(AR, I_AM4), op=ALU.mult), r=[bH, b_tab], w=[bxo])
                    V(lambda e: e.tensor_tensor(out=Hp[:], in0=h0sv, in1=abc(AI, I_AM4), op=ALU.mult), r=[bH, b_tab], w=[bxo])
                    V(lambda e: e.tensor_tensor(out=Xb[:, :, 0:16], in0=Xf[:], in1=Hp[:], op=ALU.add), r=[bxo], w=[bXb])
                    V(lambda e: e.tensor_tensor(out=Xf[:], in0=h0v, in1=abc(AR, I_A4), op=ALU.mult), r=[bH, b_tab], w=[bxo])
                    V(lambda e: e.tensor_tensor(out=Hp[:], in0=h0sv, in1=abc(AI, I_A4), op=ALU.mult), r=[bH, b_tab], w=[bxo])
                    V(lambda e: e.tensor_tensor(out=Xf[:], in0=Xf[:], in1=Hp[:], op=ALU.add), r=[bxo], w=[bxo])
                    for q in range(4):
                        bank = q % 2
                        for j in range(8):
                            g = q * 8 + j
                            T(lambda e, g=g, j=j, bank=bank: e.matmul(
                                PS[bank][:, j * 64:j * 64 + 16], lhsT=Wt[:, g, :], rhs=U[:, g, 0:16],
                                start=True, stop=True), r=[b_tab, bU], w=[bPS[bank]])
                        V(lambda e, q=q, bank=bank: e.tensor_tensor(
                            out=Xf[:, q * 8:q * 8 + 8, :], in0=Xf[:, q * 8:q * 8 + 8, :],
                            in1=PS[bank][:].rearrange("p (j c) -> p j c", j=8)[:, :, 0:16], op=ALU.add),
                          r=[bxo, bPS[bank]], w=[bxo])
                    Xf2 = Xf[:].rearrange("p g b -> p (g b)")
                    for j in range(4):
                        T(lambda e, j=j: e.transpose(out=PS[2][:, j * 128:(j + 1) * 128],
                                                     in_=Xf2[:, j * 128:(j + 1) * 128], identity=identf[:]),
                          r=[bxo, b_const], w=[bPS[2]])
                    V(lambda e: e.tensor_copy(out=xo[:], in_=PS[2][:].rearrange("p (j c) -> p j c", j=4)),
                      r=[bPS[2]], w=[bxo])
                    for j in range(4):
                        for gl in range(8):
                            for (nm, c0) in (("o_s5r_s", 0), ("o_s5i_s", 64)):
                                S.dma("sp", O[nm].rearrange("(b g) p -> g b p", g=G)[8 * j + gl],
                                      xo[gl * 16:gl * 16 + 16, j, c0:c0 + 64], reads=[bxo])
                    xprev = lambda g: Xb[:, g, 0:16]
                    bXprev = bXb
                for gq in range(4):
                    bank = 6 + (gq % 2)
                    for j in range(8):
                        g = gq * 8 + j
                        T(lambda e, g=g, j=j, bank=bank: e.matmul(
                            PS[bank][:, j * 64:j * 64 + nch], lhsT=Tt[:, g, :], rhs=U[:, g, 0:nch],
                            start=True, stop=False), r=[b_tab, bU], w=[bPS[bank]])
                        T(lambda e, g=g, j=j, bank=bank: e.matmul(
                            PS[bank][:, j * 64:j * 64 + nch], lhsT=Vt[:, g, :], rhs=xprev(g)[:, 0:nch],
                            start=False, stop=True), r=[b_tab, bXprev], w=[bPS[bank]])
                    gs = slice(gq * 8, gq * 8 + 8)
                    yv = PS[bank][:].rearrange("p (j c) -> p j c", j=8)[:, :, 0:nch]
                    V(lambda e, gs=gs: e.tensor_tensor(out=ytmp[:, :, 0:nch], in0=U[:, gs, 0:nch],
                                                       in1=DS[:, gs].unsqueeze(2).to_broadcast([128, 8, nch]),
                                                       op=ALU.mult), r=[bU, b_tab], w=[bytmp])
                    V(lambda e, yv=yv: e.tensor_tensor(out=ytmp[:, :, 0:nch], in0=yv, in1=ytmp[:, :, 0:nch],
                                                       op=ALU.add), r=[bPS[bank], bytmp], w=[bytmp])
                    A(lambda e, gs=gs: e.activation(out=Zt[:, gs, 0:nch], in_=ytmp[:, :, 0:nch],
                                                    func=AF.Gelu_apprx_tanh), r=[bytmp], w=[bZ])
                for ct in range(4):
                    bank = ct % 2
                    taus = list(range(8)) if not is_s else [4, 5, 6, 7]
                    for ti_, tau in enumerate(taus):
                        for gl in range(8):
                            g = ct * 8 + gl
                            T(lambda e, g=g, gl=gl, tau=tau, ti_=ti_, bank=bank: e.matmul(
                                PS[bank][:, ti_ * 64:ti_ * 64 + nch],
                                lhsT=masters[:, tau, 112 - 16 * gl:240 - 16 * gl], rhs=Zt[:, g, 0:nch],
                                start=(gl == 0), stop=(gl == 7)), r=[b_tab, bZ], w=[bPS[bank]])
                    if not is_s:
                        A(lambda e, ct=ct, bank=bank: e.copy(
                            out=zT[:, ct, 0:n].rearrange("p (c t) -> p t c", t=8),
                            in_=PS[bank][:].rearrange("p (t c) -> p t c", t=8)), r=[bPS[bank]], w=[bzT])
                    else:
                        A(lambda e, ct=ct, bank=bank: e.copy(
                            out=zT[:, ct, 0:n].rearrange("p (b t) -> p t b", t=4),
                            in_=PS[bank][:].rearrange("p (t c) -> p t c", t=8)[:, 0:4, 0:16]),
                          r=[bPS[bank]], w=[bzT])
                for ct in range(4):
                    bank = 2 + (ct % 2)
                    for kt in range(4):
                        T(lambda e, ct=ct, kt=kt, bank=bank: e.matmul(
                            PS[bank][:, 0:n], lhsT=wglu[:, kt, ct * 128:(ct + 1) * 128], rhs=zT[:, kt, 0:n],
                            start=(kt == 0), stop=(kt == 3)), r=[b_wglu, bzT], w=[bPS[bank]])
                    A(lambda e, ct=ct, bank=bank: e.activation(out=sig[:, ct, 0:n], in_=PS[bank][:, 0:n],
                                                               func=AF.Sigmoid), r=[bPS[bank]], w=[bsig])
                V(lambda e: e.tensor_tensor(out=ssmT[:, :, t0:t0 + n], in0=zT[:, :, 0:n], in1=sig[:, :, 0:n],
                                            op=ALU.mult), r=[bzT, bsig], w=[b_ssmT[bi]])
            S.barrier()
        if dbg:
            with ExitStack() as sd:
                dtmp = alloc(sd, "dtmp", [128, 4, NTOK])
                bd = Buf("dtmp", S.GS[2])
                V(lambda e: e.tensor_copy(out=dtmp[:], in_=ssmT[:]), r=b_ssmT, w=[bd])
                S.dma("sp", O["dbg_ssm"][:, :, :], dtmp[:], reads=[bd])
                S.barrier()
        if stage <= 1:
            S.barrier()
            S.run_block()
            nck.__exit__(None, None, None)
            return nc

        with ExitStack() as sbx:
            x = alloc(sbx, "x", [128, NT, D])
            bx = [Buf("x%d" % n, S.GX) for n in range(NT)]
            for n in range(NTP):
                S.dma("sp", x[:, n, :], I["xp"][n * 128:(n + 1) * 128, :], writes=[bx[n]])
            S.dma("sp", x[0:TS, 16, :], I["xs"][:, :], writes=[bx[16]])
            scrB = make_scr(sbx, "B", [7])
            hT1 = alloc(sbx, "hT1", [128, 8, 128], BF16)
            bhT1 = Buf("hT1")

            def resid_add(n, npart, half, bank):
                V(lambda e: e.tensor_tensor(out=x[:npart, n, half * 512:(half + 1) * 512], in0=PS[bank][:npart, :],
                                            in1=x[:npart, n, half * 512:(half + 1) * 512], op=ALU.add),
                  r=[bPS[bank], bx[n]], w=[bx[n]])

            with ExitStack() as s1:
                wq = alloc(s1, "wqkvg", [128, 8, 2048], BF16)
                wout = alloc(s1, "wout", [128, 8, D], BF16)
                b_wqc = [Buf("wq%d" % c, S.GW[c]) for c in range(4)]
                b_wout = Buf("wout", S.GW[0])
                for c in range(4):
                    for kt in range(8):
                        S.dma("pool", wq[:, kt, c * 512:(c + 1) * 512],
                              I["w_in"][kt * 128:(kt + 1) * 128, 512 + c * 512:512 + (c + 1) * 512], writes=[b_wqc[c]])
                wout_loaded = [False]
                gm2 = alloc(s1, "gm2", [128, 8])
                gn = alloc(s1, "gn", [128, 4])
                rope = alloc(s1, "rope", [128, 3, NT, 64])
                dmp = alloc(s1, "dmp", [128, 512])
                dms = alloc(s1, "dms", [64, 256])
                xi = alloc(s1, "xi", [128, 768])
                zetap = alloc(s1, "zetap", [128, 4])
                zs = alloc(s1, "zs", [64, 64])
                cmask = alloc(s1, "cmask", [128, 16 * 64])
                b_t1 = Buf("tab1")
                S.dma("sp", gm2[:], I["g_mix"].rearrange("(k p) -> p k", p=128), writes=[b_t1])
                S.dma("sp", gn[:], I["ret_gn"].rearrange("(k p) -> p k", p=128), writes=[b_t1])
                for a_ in range(3):
                    S.dma("sp", rope[:, a_, :, :], I["c_rope"][a_], writes=[b_t1])
                S.dma("sp", dmp[:], I["c_dmask_p"][:, :], writes=[b_t1])
                S.dma("sp", dms[:], I["c_dmask_s"][:, :], writes=[b_t1])
                S.dma("sp", xi[:], I["c_xi"][0:1, :].partition_broadcast(128), writes=[b_t1])
                S.dma("sp", zetap[:], I["c_zeta_p"][:, :], writes=[b_t1])
                S.dma("sp", zs[:], I["c_zs"][:, :], writes=[b_t1])
                S.dma("sp", cmask[:], I["c_cmask"][0:1, :].partition_broadcast(128), writes=[b_t1])
                def load_wout():
                    load_w_bf16(wout, b_wout, I["w_out"], 8, D, 0)
                    for k in range(4):
                        V(lambda e: e.tensor_scalar(out=wout[:, 4 + k, :], in0=wout[:, 4 + k, :], scalar1=gn[:, k:k + 1],
                                                    scalar2=None, op0=ALU.mult), r=[b_wout, b_t1], w=[b_wout])
                    wout_loaded[0] = True
                t1q = alloc(s1, "t1q", [128, 512])
                t2q = alloc(s1, "t2q", [128, 512])
                t1k = alloc(s1, "t1k", [128, 512])
                t2k = alloc(s1, "t2k", [128, 512])
                qr = alloc(s1, "qr", [128, 512], BF16)
                kr = alloc(s1, "kr", [128, 512], BF16)
                qT = alloc(s1, "qT", [128, 4, 128], BF16)
                qxT = alloc(s1, "qxT", [128, 4, 128], BF16)
                kT = alloc(s1, "kT", [128, 4, 128], BF16)
                vb = alloc(s1, "vb", [128, 512], BF16)
                vz = alloc(s1, "vz", [128, 512], BF16)
                sg_ = alloc(s1, "sgl", [128, 512])
                sT = alloc(s1, "sT", [128, 4, 128], BF16)
                Sst = alloc(s1, "Sst", [128, 4, 128])
                Sbf = alloc(s1, "Sbf", [128, 4, 128], BF16)
                stats = alloc(s1, "stats", [128, 4, 6])
                mv = alloc(s1, "mv", [128, 4, 2])
                rs4 = alloc(s1, "rs4", [128, 4])
                nb4 = alloc(s1, "nb4", [128, 4])
                on = alloc(s1, "on", [128, 512])
                ret = alloc(s1, "ret", [128, 512], BF16)
                retT = alloc(s1, "retT", [128, 4, 128], BF16)
                S0 = [alloc(s1, "S0_%d" % i, [128, 4, 128]) for i in range(2)]
                S0b = [alloc(s1, "S0b_%d" % i, [128, 4, 128], BF16) for i in range(2)]
                qxm = [alloc(s1, "qxm_%d" % i, [128, 4, 64], BF16) for i in range(2)]
                vzb = [alloc(s1, "vzb_%d" % i, [64, 512], BF16) for i in range(2)]
                Sn = [alloc(s1, "Sn_%d" % i, [128, 4, 128]) for i in range(2)]
                bS0 = [Buf("S0_%d" % i, S.GL[i]) for i in range(2)]
                bS0b = [Buf("S0b_%d" % i) for i in range(2)]
                bqxm = [Buf("qxm%d" % i) for i in range(2)]
                bvzb = [Buf("vzb%d" % i) for i in range(2)]
                bSn = [Buf("Sn%d" % i, S.GS[i]) for i in range(2)]
                (b_t1q, b_t2q, b_t1k, b_t2k, b_qr, b_kr, b_qT, b_qxT, b_kT, b_vb, b_vz, b_sg, b_sT, b_Sst, b_Sbf,
                 b_st, b_on, b_ret, b_retT) = [Buf("p1b%d" % i) for i in range(19)]
                b_Sst.grp = S.GS[2]
                V(lambda e: e.memset(Sst[:], 0.0), w=[b_Sst])
                GC_P = [float(g ** 128) for g in GAM]
                GC_S = [float(g ** 4) for g in GAM]

                import os as _os
                _tl = _os.environ.get("K_TILES")
                _tiles = [int(v) for v in _tl.split(",") if int(v) >= 0] if _tl else list(range(NT))
                _step = int(_os.environ.get("K_STEP", "99"))
                hT1s = [hT1, alloc(s1, "hT1c", [128, 8, 128], BF16)]
                bhT1s = [bhT1, Buf("hT1c")]

                def p1b_norm(n):
                    npt_ = TS if n == 16 else 128
                    rmsnorm_hT(x[:npt_, n, :], bx[n], npt_, gm2[:], hT1s[n % 2], bhT1s[n % 2], scrB, 0, None, bg=b_t1)
                def p1b_proj(n):
                    npt_ = TS if n == 16 else 128
                    hTn, bhTn = hT1s[n % 2], bhT1s[n % 2]
                    for c in range(4):
                        for kt in range(8):
                            T(lambda e: e.matmul(PS[c][:npt_, :], lhsT=hTn[:, kt, 0:npt_],
                                                 rhs=wq[:, kt, c * 512:(c + 1) * 512], start=(kt == 0), stop=(kt == 7)),
                              r=[bhTn, b_wqc[c]], w=[bPS[c]])
                if _tiles:
                    p1b_norm(_tiles[0])
                    p1b_proj(_tiles[0])
                    load_wout()
                for ti_, n in enumerate(_tiles):
                    is_s = (n == 16)
                    npt = TS if is_s else 128
                    tok0 = n * 128
                    hT1, bhT1 = hT1s[n % 2], bhT1s[n % 2]
                    pob = [4, 6, 7, 1] if is_s else [4, 4, 4, 4]

                    def po(h):
                        if is_s:
                            return PS[pob[h]][:npt, 0:128]
                        return PS[4][:npt, h * 128:(h + 1) * 128]
                    if _step <= 1:
                        continue
                    for (bank, t1_, t2_, out_, bt1, bt2, bo) in ((0, t1q, t2q, qr, b_t1q, b_t2q, b_qr),
                                                               (1, t1k, t2k, kr, b_t1k, b_t2k, b_kr)):
                        pv4 = PS[bank][:npt, :].rearrange("p (h a j) -> p h a j", h=4, a=2)
                        t1v = t1_[:npt, :].rearrange("p (h a j) -> p h a j", h=4, a=2)
                        t2v = t2_[:npt, :].rearrange("p (h a j) -> p h a j", h=4, a=2)
                        cosb = rope[:npt, 0, n, :].unsqueeze(1).unsqueeze(1).to_broadcast([npt, 4, 2, 64])
                        sinb = rope[:npt, 1, n, :].unsqueeze(1).to_broadcast([npt, 4, 64])
                        nsinb = rope[:npt, 2, n, :].unsqueeze(1).to_broadcast([npt, 4, 64])
                        V(lambda e: e.tensor_tensor(out=t1v, in0=pv4, in1=cosb, op=ALU.mult), r=[bPS[bank], b_t1], w=[bt1])
                        V(lambda e: e.tensor_tensor(out=t2v[:, :, 0, :], in0=pv4[:, :, 1, :], in1=nsinb, op=ALU.mult),
                          r=[bPS[bank], b_t1], w=[bt2])
                        V(lambda e: e.tensor_tensor(out=t2v[:, :, 1, :], in0=pv4[:, :, 0, :], in1=sinb, op=ALU.mult),
                          r=[bPS[bank], b_t1], w=[bt2])
                        V(lambda e: e.tensor_tensor(out=out_[:npt, :], in0=t1_[:npt, :], in1=t2_[:npt, :], op=ALU.add),
                           r=[bt1, bt2], w=[bo])
                    if _step <= 2:
                        continue
                    A(lambda e: e.copy(out=vb[:npt, :], in_=PS[2][:npt, :]), r=[bPS[2]], w=[b_vb])
                    if not is_s:
                        V(lambda e: e.tensor_tensor(
                            out=vz[:, :].rearrange("p (h e) -> p h e", h=4),
                            in0=PS[2][:, :].rearrange("p (h e) -> p h e", h=4),
                            in1=zetap[:, :].unsqueeze(2).to_broadcast([128, 4, 128]), op=ALU.mult),
                          r=[bPS[2], b_t1], w=[b_vz])
                    A(lambda e: e.activation(out=sg_[:npt, :], in_=PS[3][:npt, :], func=AF.Silu), r=[bPS[3]], w=[b_sg])
                    pv4b = ps_bf(4)
                    pv5b = ps_bf(5)
                    for h in range(4):
                        T(lambda e: e.transpose(out=pv4b[:, h * 128:h * 128 + npt], in_=qr[:npt, h * 128:(h + 1) * 128],
                                                identity=identb[:npt, :npt]), r=[b_qr, b_const], w=[bPS[4]])
                    for h in range(4):
                        T(lambda e: e.transpose(out=pv5b[:, h * 128:h * 128 + npt], in_=kr[:npt, h * 128:(h + 1) * 128],
                                                identity=identb[:npt, :npt]), r=[b_kr, b_const], w=[bPS[5]])
                    q4 = pv4b[:, 0:512].rearrange("p (h t) -> p h t", h=4)[:, :, 0:npt]
                    k4 = pv5b[:, 0:512].rearrange("p (h t) -> p h t", h=4)[:, :, 0:npt]
                    A(lambda e: e.copy(out=qT[:, :, 0:npt], in_=q4), r=[bPS[4]], w=[b_qT])
                    xiv = (xi[:, 0:512].rearrange("p (h t) -> p h t", h=4) if not is_s
                           else xi[:, 512:768].rearrange("p (h t) -> p h t", h=4))
                    V(lambda e: e.tensor_tensor(out=qxT[:, :, 0:npt], in0=q4, in1=xiv, op=ALU.mult),
                      r=[bPS[4], b_t1], w=[b_qxT])
                    A(lambda e: e.copy(out=kT[:, :, 0:npt], in_=k4), r=[bPS[5]], w=[b_kT])
                    if _step <= 3:
                        continue
                    for h in range(4):
                        T(lambda e: e.matmul(PS[6][:npt, h * 128:h * 128 + npt], lhsT=kT[:, h, 0:npt], rhs=qT[:, h, 0:npt],
                                             start=True, stop=True), r=[b_kT, b_qT], w=[bPS[6]])
                    dmv = (dmp[:, :].rearrange("p (h t) -> p h t", h=4) if not is_s
                           else dms[:, :].rearrange("p (h t) -> p h t", h=4))
                    V(lambda e: e.tensor_tensor(out=sT[:npt, :, 0:npt],
                                                in0=PS[6][:npt, :].rearrange("p (h t) -> p h t", h=4)[:, :, 0:npt],
                                                in1=dmv, op=ALU.mult), r=[bPS[6], b_t1], w=[b_sT])
                    if _step <= 4:
                        continue
                    if ti_ + 1 < len(_tiles):
                        p1b_norm(_tiles[ti_ + 1])
                    for h in range(4):
                        only = (n == 0)
                        T(lambda e: e.matmul(po(h), lhsT=sT[:npt, h, 0:npt],
                                             rhs=vb[:npt, h * 128:(h + 1) * 128], start=True, stop=only),
                          r=[b_sT, b_vb], w=[bPS[pob[h]]])
                        if (not is_s) and n > 0:
                            T(lambda e: e.matmul(po(h), lhsT=qxT[:, h, 0:npt],
                                                 rhs=Sbf[:, h, :], start=False, stop=True),
                              r=[b_qxT, b_Sbf], w=[bPS[4]])
                    if not is_s:
                        for h in range(4):
                            T(lambda e: e.matmul(PS[5][:, h * 128:(h + 1) * 128], lhsT=kr[:, h * 128:(h + 1) * 128],
                                                 rhs=vz[:, h * 128:(h + 1) * 128], start=True, stop=True),
                              r=[b_kr, b_vz], w=[bPS[5]])
                        for h in range(4):
                            V(lambda e: e.scalar_tensor_tensor(out=Sst[:, h, :], in0=Sst[:, h, :], scalar=GC_P[h],
                                                               op0=ALU.mult, in1=PS[5][:, h * 128:(h + 1) * 128],
                                                               op1=ALU.add), r=[b_Sst, bPS[5]], w=[b_Sst])
                        A(lambda e: e.copy(out=Sbf[:], in_=Sst[:]), r=[b_Sst], w=[b_Sbf])
                        if n == NTP - 1:
                            S.dma("sp", O["o_ret_p"].rearrange("h d e -> d h e"), Sst[:], reads=[b_Sst])
                    else:
                        S.dma("sp", S0[0][:], I["sret"][0].rearrange("h d e -> d h e"), writes=[bS0[0]])
                        for b in range(16):
                            sl = b % 2
                            if b + 1 < 16:
                                S.dma("sp", S0[1 - sl][:], I["sret"][b + 1].rearrange("h d e -> d h e"), writes=[bS0[1 - sl]])
                            A(lambda e: e.copy(out=S0b[sl][:], in_=S0[sl][:]), r=[bS0[sl]], w=[bS0b[sl]])
                            V(lambda e: e.tensor_tensor(
                                out=qxm[sl][:], in0=qxT[:, :, 0:64],
                                in1=cmask[:, b * 64:(b + 1) * 64].unsqueeze(1).to_broadcast([128, 4, 64]), op=ALU.mult),
                              r=[b_qxT, b_t1], w=[bqxm[sl]])
                            for h in range(4):
                                T(lambda e: e.matmul(po(h), lhsT=qxm[sl][:, h, :],
                                                     rhs=S0b[sl][:, h, :], start=False, stop=(b == 15)),
                                  r=[bqxm[sl], bS0b[sl]], w=[bPS[pob[h]]])
                            V(lambda e: e.tensor_tensor(
                                out=vzb[sl][:, :].rearrange("p (h e) -> p h e", h=4),
                                in0=PS[2][:64, :].rearrange("p (h e) -> p h e", h=4),
                                in1=zs[:, b * 4:(b + 1) * 4].unsqueeze(2).to_broadcast([64, 4, 128]), op=ALU.mult),
                              r=[bPS[2], b_t1], w=[bvzb[sl]])
                            kvb = 5 if sl == 0 else 0
                            for h in range(4):
                                T(lambda e: e.matmul(PS[kvb][:, h * 128:(h + 1) * 128], lhsT=kr[:64, h * 128:(h + 1) * 128],
                                                     rhs=vzb[sl][:, h * 128:(h + 1) * 128], start=True, stop=True),
                                  r=[b_kr, bvzb[sl]], w=[bPS[kvb]])
                            for h in range(4):
                                V(lambda e: e.scalar_tensor_tensor(out=Sn[sl][:, h, :], in0=S0[sl][:, h, :], scalar=GC_S[h],
                                                                   op0=ALU.mult, in1=PS[kvb][:, h * 128:(h + 1) * 128],
                                                                   op1=ALU.add), r=[bS0[sl], bPS[kvb]], w=[bSn[sl]])
                            S.dma("sp", O["o_ret_s"][b].rearrange("h d e -> d h e"), Sn[sl][:], reads=[bSn[sl]])
                    if _step <= 5:
                        continue
                    if ti_ + 1 < len(_tiles):
                        p1b_proj(_tiles[ti_ + 1])
                    for h in range(4):
                        V(lambda e: e.bn_stats(out=stats[:npt, h, :], in_=po(h)),
                          r=[bPS[pob[h]]], w=[b_st])
                    for h in range(4):
                        V(lambda e: e.bn_aggr(out=mv[:npt, h, :], in_=stats[:npt, h, :]), r=[b_st], w=[b_st])
                    A(lambda e: e.activation(out=rs4[:npt, :], in_=mv[:npt, :, 1], func=AF.Sqrt, scale=1.0,
                                             bias=epsc[:npt, :]), r=[b_st, b_const], w=[b_st])
                    V(lambda e: e.reciprocal(out=rs4[:npt, :], in_=rs4[:npt, :]), r=[b_st], w=[b_st])
                    V(lambda e: e.scalar_tensor_tensor(out=nb4[:npt, :], in0=mv[:npt, :, 0], scalar=-1.0, op0=ALU.mult,
                                                       in1=rs4[:npt, :], op1=ALU.mult), r=[b_st], w=[b_st])
                    for h in range(4):
                        A(lambda e: e.activation(out=on[:npt, h * 128:(h + 1) * 128], in_=po(h),
                                                 func=AF.Identity, scale=rs4[:npt, h:h + 1], bias=nb4[:npt, h:h + 1]),
                          r=[bPS[pob[h]], b_st], w=[b_on])
                    V(lambda e: e.tensor_tensor(out=ret[:npt, :], in0=on[:npt, :], in1=sg_[:npt, :], op=ALU.mult),
                       r=[b_on, b_sg], w=[b_ret])
                    if _step <= 6:
                        continue
                    pv6b = ps_bf(6)
                    for h in range(4):
                        T(lambda e: e.transpose(out=pv6b[:, h * 128:h * 128 + npt], in_=ret[:npt, h * 128:(h + 1) * 128],
                                                identity=identb[:npt, :npt]), r=[b_ret, b_const], w=[bPS[6]])
                    A(lambda e: e.copy(out=retT[:, :, 0:npt],
                                       in_=pv6b[:, 0:512].rearrange("p (h t) -> p h t", h=4)[:, :, 0:npt]),
                      r=[bPS[6]], w=[b_retT])
                    if _step <= 7:
                        continue
                    bi_ = min(n // 4, 4)
                    for half in range(2):
                        bank = 6 + half
                        for kt in range(8):
                            lh = ssmT[:, kt, tok0:tok0 + npt] if kt < 4 else retT[:, kt - 4, 0:npt]
                            T(lambda e: e.matmul(PS[bank][:npt, :], lhsT=lh, rhs=wout[:, kt, half * 512:(half + 1) * 512],
                                                 start=(kt == 0), stop=(kt == 7)),
                              r=[b_ssmT[bi_], b_retT, b_wout], w=[bPS[bank]])
                        resid_add(n, npt, half, bank)
                S.barrier()
            if dbg:
                for n in range(NT):
                    S.dma("sp", O["dbg_x"][:, n, :], x[:, n, :], reads=[bx[n]])
            if stage <= 2:
                S.barrier()
                S.run_block()
                nck.__exit__(None, None, None)
                return nc

            with ExitStack() as s2:
                gx = alloc(s2, "gx", [128, 8])
                gmem = alloc(s2, "gmem", [128, 8])
                ones = alloc(s2, "ones", [128, 128], BF16)
                b_t2 = Buf("tab2")
                S.dma("sp", gx[:], I["g_xattn"].rearrange("(k p) -> p k", p=128), writes=[b_t2])
                S.dma("sp", gmem[:], I["g_mem"].rearrange("(k p) -> p k", p=128), writes=[b_t2])
                V(lambda e: e.memset(ones[:], 1.0), w=[b_t2])
                KT = alloc(s2, "KT", [128, 8, MEM], BF16)
                Vm = alloc(s2, "Vm", [128, 2, D], BF16)
                b_KT, b_Vm = Buf("KT"), Buf("Vm")
                wmq = alloc(s2, "wmq", [128, 8, D], BF16)
                b_wmq, b_wmo = Buf("wmq", S.GW[2]), Buf("wmo", S.GW[3])
                with ExitStack() as s2a:
                    wmk = alloc(s2a, "wmk", [128, 8, D], BF16)
                    wmv = alloc(s2a, "wmv", [128, 8, D], BF16)
                    b_wmk, b_wmv = Buf("wmk", S.GW[0]), Buf("wmv", S.GW[1])
                    load_w_bf16(wmk, b_wmk, I["w_mk"], 8, D, 0)
                    load_w_bf16(wmv, b_wmv, I["w_mv"], 8, D, 0)
                    load_w_bf16(wmq, b_wmq, I["w_mq"], 8, D, 0)
                    mx = [alloc(s2a, "mx%d" % i, [128, D]) for i in range(2)]
                    bmx = [Buf("mx%d" % i, S.GL[i]) for i in range(2)]
                    mhT = alloc(s2a, "mhT", [128, 8, MEM], BF16)
                    b_mhT = Buf("mhT")
                    mo = [alloc(s2a, "mo%d" % i, [128, D]) for i in range(2)]
                    bmo = [Buf("mo%d" % i, S.GS[i]) for i in range(2)]
                    _k2a = int(_os.environ.get("K2A", "9"))
                    for mt in range(2):
                        S.dma("sp", mx[mt][:], I["memp"][mt * 128:(mt + 1) * 128, :], writes=[bmx[mt]])
                        if _k2a >= 1:
                            rmsnorm_hT(mx[mt][:, :], bmx[mt], 128, gmem[:], mhT, b_mhT, scrB, mt * 128, None,
                                       ln=True, bg=b_t2)
                    oi = 0
                    for (wm, bwm, oname, isv) in ((wmk, b_wmk, "o_mk", False), (wmv, b_wmv, "o_mv", True)) if _k2a >= 2 else ():
                        for mt in range(2):
                            sl = oi % 2
                            oi += 1
                            for half in range(2):
                                bank = half
                                for kt in range(8):
                                    T(lambda e: e.matmul(PS[bank][:, :], lhsT=mhT[:, kt, mt * 128:(mt + 1) * 128],
                                                         rhs=wm[:, kt, half * 512:(half + 1) * 512], start=(kt == 0),
                                                         stop=(kt == 7)), r=[b_mhT, bwm], w=[bPS[bank]])
                                A(lambda e: e.copy(out=mo[sl][:, half * 512:(half + 1) * 512], in_=PS[bank][:, :]),
                                  r=[bPS[bank]], w=[bmo[sl]])
                                if isv:
                                    V(lambda e: e.tensor_copy(out=Vm[:, mt, half * 512:(half + 1) * 512], in_=PS[bank][:, :]),
                                      r=[bPS[bank]], w=[b_Vm])
                            S.dma("sp", O[oname][mt * 128:(mt + 1) * 128, :], mo[sl][:], reads=[bmo[sl]])
                    for j in range(8 if _k2a >= 3 else 0):
                        bank = 2 + (j % 2)
                        for kt in range(8):
                            T(lambda e: e.matmul(PS[bank][:, 0:MEM], lhsT=wmk[:, kt, j * 128:(j + 1) * 128],
                                                 rhs=mhT[:, kt, :], start=(kt == 0), stop=(kt == 7)),
                              r=[b_mhT, b_wmk], w=[bPS[bank]])
                        A(lambda e: e.copy(out=KT[:, j, :], in_=PS[bank][:, 0:MEM]), r=[bPS[bank]], w=[b_KT])
                    S.barrier()
                wmo = alloc(s2, "wmo", [128, 8, D], BF16)
                load_w_bf16(wmo, b_wmo, I["w_mo"], 8, D, 0)
                hT4 = alloc(s2, "hT4", [128, 8, 512], BF16)
                qm4 = alloc(s2, "qm4", [128, 8, 512], BF16)
                oT4 = alloc(s2, "oT4", [128, 8, 512], BF16)
                eT4 = [alloc(s2, "eT4_%d" % i, [128, 2, 512], BF16) for i in range(2)]
                rdn4 = [alloc(s2, "rdn4_%d" % i, [128, 512]) for i in range(2)]
                b_hT4, b_qm4, b_oT4 = Buf("hT4"), Buf("qm4"), Buf("oT4")
                b_eT4 = [Buf("eT4_%d" % i) for i in range(2)]
                b_rdn4 = [Buf("rdn4_%d" % i) for i in range(2)]
                Kb = [alloc(s2, "Kb%d" % i, [128, 2, D]) for i in range(2)]
                bKb = [Buf("Kb%d" % i, S.GL[i]) for i in range(2)]
                KbT = [alloc(s2, "KbT%d" % i, [128, 8, MEM], BF16) for i in range(2)]
                bKbT = [Buf("KbT%d" % i) for i in range(2)]
                Vb = [alloc(s2, "Vb%d" % i, [128, 2, D], BF16) for i in range(2)]
                bVb = [Buf("Vb%d" % i, S.GW[i]) for i in range(2)]
                eTs = alloc(s2, "eTs", [128, 2, 4, 64], BF16)
                b_eTs = Buf("eTs")
                qrot = [0]

                def q_proj(nc_):
                    for j in range(8):
                        bank = 5 + (qrot[0] % 3)
                        qrot[0] += 1
                        for kt in range(8):
                            T(lambda e: e.matmul(PS[bank][:, 0:nc_], lhsT=wmq[:, kt, j * 128:(j + 1) * 128],
                                                 rhs=hT4[:, kt, 0:nc_], start=(kt == 0), stop=(kt == 7)),
                              r=[b_wmq, b_hT4], w=[bPS[bank]])
                        A(lambda e: e.activation(out=qm4[:, j, 0:nc_], in_=PS[bank][:, 0:nc_], func=AF.Copy,
                                                 scale=1.0 / 16.0), r=[bPS[bank]], w=[b_qm4])

                def w_mo_resid(n, npt, c0):
                    for half in range(2):
                        bank = 5 + (qrot[0] % 3)
                        qrot[0] += 1
                        for j in range(8):
                            T(lambda e: e.matmul(PS[bank][:npt, :], lhsT=oT4[:, j, c0:c0 + npt],
                                                 rhs=wmo[:, j, half * 512:(half + 1) * 512], start=(j == 0), stop=(j == 7)),
                              r=[b_oT4, b_wmo], w=[bPS[bank]])
                        resid_add(n, npt, half, bank)

                for bi in range(4):
                    for ti in range(4):
                        n = bi * 4 + ti
                        rmsnorm_hT(x[:, n, :], bx[n], 128, gx[:], hT4, b_hT4, scrB, ti * 128, None, ln=True, bg=b_t2)
                    q_proj(512)
                    for h in range(4):
                        par = h % 2
                        for mt in range(2):
                            bank = mt
                            for dt_ in range(2):
                                T(lambda e: e.matmul(PS[bank][:, :], lhsT=KT[:, h * 2 + dt_, mt * 128:(mt + 1) * 128],
                                                     rhs=qm4[:, h * 2 + dt_, :], start=(dt_ == 0), stop=(dt_ == 1)),
                                  r=[b_KT, b_qm4], w=[bPS[bank]])
                            A(lambda e: e.activation(out=eT4[par][:, mt, :], in_=PS[bank][:, :], func=AF.Exp),
                              r=[bPS[bank]], w=[b_eT4[par]])
                        for mt in range(2):
                            T(lambda e: e.matmul(PS[2][:, :], lhsT=ones[:, :], rhs=eT4[par][:, mt, :], start=(mt == 0),
                                                 stop=(mt == 1)), r=[b_t2, b_eT4[par]], w=[bPS[2]])
                        A(lambda e: e.activation(out=rdn4[par][:, :], in_=PS[2][:, :], func=AF.Ln), r=[bPS[2]], w=[b_rdn4[par]])
                        A(lambda e: e.activation(out=rdn4[par][:, :], in_=rdn4[par][:, :], func=AF.Exp, scale=-1.0),
                          r=[b_rdn4[par]], w=[b_rdn4[par]])
                        for dt_ in range(2):
                            bank = 3 + dt_
                            j = h * 2 + dt_
                            for mt in range(2):
                                T(lambda e: e.matmul(PS[bank][:, :], lhsT=Vm[:, mt, j * 128:(j + 1) * 128],
                                                     rhs=eT4[par][:, mt, :], start=(mt == 0), stop=(mt == 1)),
                                  r=[b_Vm, b_eT4[par]], w=[bPS[bank]])
                            V(lambda e: e.tensor_tensor(out=oT4[:, j, :], in0=PS[bank][:, :], in1=rdn4[par][:, :], op=ALU.mult),
                              r=[bPS[bank], b_rdn4[par]], w=[b_oT4])
                    for ti in range(4):
                        w_mo_resid(bi * 4 + ti, 128, ti * 128)
                n = 16
                rmsnorm_hT(x[:TS, n, :], bx[n], TS, gx[:], hT4, b_hT4, scrB, 0, None, ln=True, bg=b_t2)
                q_proj(TS)
                rden_s = rdn4[0][:, 0:256].rearrange("p (h t) -> p h t", h=4)
                for b in range(16):
                    sl = b % 2
                    S.dma("sp", Kb[sl][:], I["ck"][b].rearrange("(mt p) d -> p mt d", p=128), writes=[bKb[sl]])
                    for q4 in range(4):
                        bank = 2 + (q4 % 2)
                        for i4 in range(4):
                            idx = q4 * 4 + i4
                            j, mt = idx // 2, idx % 2
                            T(lambda e: e.transpose(out=PS[bank][:, i4 * 128:(i4 + 1) * 128],
                                                    in_=Kb[sl][:, mt, j * 128:(j + 1) * 128], identity=identf[:]),
                              r=[bKb[sl], b_const], w=[bPS[bank]])
                        A(lambda e: e.copy(
                            out=KbT[sl][:, 2 * q4:2 * q4 + 2, :].rearrange("p j (m t) -> p j m t", m=2),
                            in_=PS[bank][:, :].rearrange("p (j m t) -> p j m t", j=2, m=2)),
                          r=[bPS[bank]], w=[bKbT[sl]])
                    for h in range(4):
                        for mt in range(2):
                            c0 = mt * 256 + h * 64 + 4 * b
                            for dt_ in range(2):
                                T(lambda e: e.matmul(PS[4][:, c0:c0 + 4],
                                                     lhsT=KbT[sl][:, h * 2 + dt_, mt * 128:(mt + 1) * 128],
                                                     rhs=qm4[:, h * 2 + dt_, 4 * b:4 * b + 4], start=(dt_ == 0),
                                                     stop=(dt_ == 1)), r=[bKbT[sl], b_qm4], w=[bPS[4]])
                A(lambda e: e.activation(out=eTs[:].rearrange("p m h t -> p (m h t)"), in_=PS[4][:, :], func=AF.Exp),
                  r=[bPS[4]], w=[b_eTs])
                for h in range(4):
                    for mt in range(2):
                        T(lambda e: e.matmul(PS[0][:, h * 64:(h + 1) * 64], lhsT=ones[:, :], rhs=eTs[:, mt, h, :],
                                             start=(mt == 0), stop=(mt == 1)), r=[b_t2, b_eTs], w=[bPS[0]])
                V(lambda e: e.reciprocal(out=rden_s, in_=PS[0][:, 0:256].rearrange("p (h t) -> p h t", h=4)),
                  r=[bPS[0]], w=[b_rdn4[0]])
                for b in range(16):
                    sl = b % 2
                    for mt in range(2):
                        S.dma("pool", Vb[sl][:, mt, :], I["cv"][b, mt * 128:(mt + 1) * 128, :], writes=[bVb[sl]])
                    for j in range(8):
                        h = j // 2
                        for mt in range(2):
                            T(lambda e: e.matmul(PS[1][:, j * 64 + 4 * b:j * 64 + 4 * b + 4],
                                                 lhsT=Vb[sl][:, mt, j * 128:(j + 1) * 128],
                                                 rhs=eTs[:, mt, h, 4 * b:4 * b + 4], start=(mt == 0), stop=(mt == 1)),
                              r=[bVb[sl], b_eTs], w=[bPS[1]])
                V(lambda e: e.tensor_tensor(
                    out=oT4[:, :, 0:64].rearrange("p (h a) t -> p h a t", a=2),
                    in0=PS[1][:, :].rearrange("p (h a t) -> p h a t", h=4, a=2),
                    in1=rden_s.unsqueeze(2).to_broadcast([128, 4, 2, 64]), op=ALU.mult),
                  r=[bPS[1], b_rdn4[0]], w=[b_oT4])
                w_mo_resid(16, TS, 0)
                S.barrier()
            if stage <= 3:
                if dbg:
                    for n in range(NT):
                        S.dma("sp", O["dbg_x"][:, n, :], x[:, n, :], reads=[bx[n]])
                S.barrier()
                S.run_block()
                nck.__exit__(None, None, None)
                return nc

            with ExitStack() as s3:
                gml = alloc(s3, "gml", [128, 8])
                b_t3 = Buf("tab3")
                S.dma("sp", gml[:], I["g_mlp"].rearrange("(k p) -> p k", p=128), writes=[b_t3])
                hTa = alloc(s3, "hTa", [128, 8, NTOK], BF16)
                b_hTa = [Buf("hTa%d" % n) for n in range(NT)]
                wup = [alloc(s3, "wup%d" % i, [128, 8, 512], BF16) for i in range(2)]
                wdn = [alloc(s3, "wdn%d" % i, [128, 4, D], BF16) for i in range(2)]
                bwup = [Buf("wup%d" % i, S.GW[i]) for i in range(2)]
                bwdn = [Buf("wdn%d" % i, S.GW[2 + i]) for i in range(2)]
                rl = [alloc(s3, "rl%d" % i, [128, 512]) for i in range(2)]
                brl = [Buf("rl%d" % i) for i in range(2)]
                aT = [alloc(s3, "aT%d" % i, [128, 4, 512], BF16) for i in range(2)]
                baT = [Buf("aT%d" % i) for i in range(2)]

                def load_fc(fc):
                    sl = fc % 2
                    for kt in range(8):
                        S.dma("pool", wup[sl][:, kt, :], I["w_up"][kt * 128:(kt + 1) * 128, fc * 512:(fc + 1) * 512],
                              writes=[bwup[sl]])
                    for ft in range(4):
                        S.dma("pool", wdn[sl][:, ft, :], I["w_down"][fc * 512 + ft * 128:fc * 512 + (ft + 1) * 128, :],
                              writes=[bwdn[sl]])
                load_fc(0)
                scrB["pb"] = [7, 6]
                for n in range(NT):
                    npt = TS if n == 16 else 128
                    rmsnorm_hT(x[:npt, n, :], bx[n], npt, gml[:], hTa, b_hTa[n], scrB, n * 128, None, ln=True, bg=b_t3)
                gf = alloc(s3, "gf", [128, D])
                b_gf = Buf("gf")
                S.dma("sp", gf[:], I["g_final"].rearrange("(o d) -> o d", o=1).partition_broadcast(128), writes=[b_gf])
                yst = [alloc(s3, "yst%d" % i, [128, D]) for i in range(3)]
                byst = [Buf("yst%d" % i, S.GS[i]) for i in range(3)]

                def final_norm(n):
                    npt = TS if n == 16 else 128
                    sl = n % 3
                    k4 = n % 2
                    sq, ss, rstd, bscr = scrB["sq"][k4], scrB["ss"][k4], scrB["rstd"][k4], scrB["ba"][k4]
                    A(lambda e: e.activation(out=sq[:npt, :], in_=x[:npt, n, :], func=AF.Square, accum_out=ss[:npt, :]),
                      r=[bx[n]], w=[bscr])
                    A(lambda e: e.activation(out=rstd[:npt, :], in_=ss[:npt, :], func=AF.Ln, scale=1.0 / D,
                                             bias=epsc[:npt, :]), r=[bscr, b_const], w=[bscr])
                    A(lambda e: e.activation(out=rstd[:npt, :], in_=rstd[:npt, :], func=AF.Exp, scale=-0.5),
                      r=[bscr], w=[bscr])
                    V(lambda e: e.scalar_tensor_tensor(out=yst[sl][:npt, :], in0=x[:npt, n, :], scalar=rstd[:npt, :],
                                                       op0=ALU.mult, in1=gf[:npt, :], op1=ALU.mult),
                      r=[bx[n], bscr, b_gf], w=[byst[sl]])
                    if n < 16:
                        S.dma("sp", O["yp"][n * 128:(n + 1) * 128, :], yst[sl][:, :], reads=[byst[sl]])
                    else:
                        S.dma("sp", O["ys"][:, :], yst[sl][:TS, :], reads=[byst[sl]])

                blocks3 = [(i * 512, 512) for i in range(4)] + [(SEQ, TS)]
                items = [(fc, blk) for fc in range(8) for blk in blocks3]
                ctr = {"ri": 0, "di": 0}
                load_fc(1)

                def mlp_up(i):
                    fc, (t0, nn) = items[i]
                    sl, asl = fc % 2, i % 2
                    tiles = list(range(t0 // 128, t0 // 128 + (nn + 127) // 128))
                    for ft in range(4):
                        bank = ft
                        for kt in range(8):
                            T(lambda e: e.matmul(PS[bank][:, 0:nn], lhsT=wup[sl][:, kt, ft * 128:(ft + 1) * 128],
                                                 rhs=hTa[:, kt, t0:t0 + nn], start=(kt == 0), stop=(kt == 7)),
                              r=[bwup[sl]] + [b_hTa[t] for t in tiles], w=[bPS[bank]])
                        rsl = ctr["ri"] % 2
                        ctr["ri"] += 1
                        A(lambda e: e.activation(out=rl[rsl][:, 0:nn], in_=PS[bank][:, 0:nn], func=AF.Relu),
                          r=[bPS[bank]], w=[brl[rsl]])
                        V(lambda e: e.tensor_tensor(out=aT[asl][:, ft, 0:nn], in0=rl[rsl][:, 0:nn], in1=rl[rsl][:, 0:nn],
                                                    op=ALU.mult), r=[brl[rsl]], w=[baT[asl]])

                def mlp_down(i):
                    fc, (t0, nn) = items[i]
                    sl, asl = fc % 2, i % 2
                    tiles = list(range(t0 // 128, t0 // 128 + (nn + 127) // 128))
                    for ti, tl in enumerate(tiles):
                        npt = TS if tl == 16 else 128
                        for half in range(2):
                            bank = 4 + (ctr["di"] % 4)
                            ctr["di"] += 1
                            for ft in range(4):
                                T(lambda e: e.matmul(PS[bank][:npt, :], lhsT=aT[asl][:, ft, ti * 128:ti * 128 + npt],
                                                     rhs=wdn[sl][:, ft, half * 512:(half + 1) * 512], start=(ft == 0),
                                                     stop=(ft == 3)), r=[baT[asl], bwdn[sl]], w=[bPS[bank]])
                            resid_add(tl, npt, half, bank)
                        if fc == 7:
                            final_norm(tl)

                mlp_up(0)
                for i in range(len(items)):
                    if i + 1 < len(items):
                        mlp_up(i + 1)
                    mlp_down(i)
                    fc = items[i][0]
                    if (i + 1 == len(items) or items[i + 1][0] != fc) and fc + 2 < 8:
                        load_fc(fc + 2)
                S.barrier()
            if dbg:
                for n in range(NT):
                    S.dma("sp", O["dbg_x"][:, n, :], x[:, n, :], reads=[bx[n]])
            S.barrier()
            S.run_block()
            nck.__exit__(None, None, None)
    return nc


_NC = None


def kernel(**inputs):
    global _NC
    if _NC is None:
        _NC = build()
    maps = _in_maps(inputs)
    res = run_bass_kernel_spmd(_NC, maps, core_ids=list(range(8)))
    R = res.results
    f = np.float32

    def cat(name, shape=None):
        return np.stack([np.asarray(R[c][name], f) for c in range(8)])
    y_prompt = cat("yp")
    y_sample = cat("ys").reshape(128, 4, D)
    s5r_p = cat("o_s5r_p")[None]
    s5i_p = cat("o_s5i_p")[None]
    ret_p = cat("o_ret_p")[None]
    mk_p = cat("o_mk").reshape(8, MEM, 4, 256)[None]
    mv_p = cat("o_mv").reshape(8, MEM, 4, 256)[None]
    s5r_s = cat("o_s5r_s").reshape(128, G, 64)[None]
    s5i_s = cat("o_s5i_s").reshape(128, G, 64)[None]
    ret_s = cat("o_ret_s").reshape(128, 4, 128, 128)[None]
    return (y_prompt, y_sample, s5r_p, s5i_p, ret_p, mk_p, mv_p, s5r_s, s5i_s, ret_s)


def _in_maps(inputs):
    cst = _consts()
    f = np.float32
    maps = []
    w = {}
    for k in W_NAMES:
        a = np.asarray(inputs[k], f)
        if k != "g_final":
            a = a[0]
        w[k] = np.ascontiguousarray(a.reshape(W_SHAPES[k]))
    for c in range(8):
        m = dict(w)
        m.update(cst)
        b0 = 16 * c
        m["xp"] = np.ascontiguousarray(np.asarray(inputs["x_prompt"], f)[c])
        m["xs"] = np.ascontiguousarray(np.asarray(inputs["x_sample"], f)[b0:b0 + 16].reshape(TS, D))
        m["memp"] = np.ascontiguousarray(np.asarray(inputs["mem_prompt"], f)[c])
        m["s5r"] = np.ascontiguousarray(np.asarray(inputs["state_s5_re"], f)[0, b0:b0 + 16].reshape(512, 64))
        m["s5i"] = np.ascontiguousarray(np.asarray(inputs["state_s5_im"], f)[0, b0:b0 + 16].reshape(512, 64))
        m["sret"] = np.ascontiguousarray(np.asarray(inputs["state_ret"], f)[0, b0:b0 + 16])
        m["ck"] = np.ascontiguousarray(np.asarray(inputs["cache_mem_k"], f)[0, b0:b0 + 16].reshape(16, MEM, D))
        m["cv"] = np.ascontiguousarray(np.asarray(inputs["cache_mem_v"], f)[0, b0:b0 + 16].reshape(16, MEM, D))
        maps.append(m)
    return maps
```

```python
import numpy as np
import concourse.bass as bass
import concourse.mybir as mybir
from concourse.bass_utils import run_bass_kernel_spmd
from contextlib import ExitStack

F32 = mybir.dt.float32
BF16 = mybir.dt.bfloat16
AF = mybir.ActivationFunctionType
ALU = mybir.AluOpType

D = 1024
SEQ = 2048
NTP = 16
TS = 64
NT = 17
NTOK = SEQ + TS
G = 32
DFF = 4096
MEM = 256
EPS = 1e-6
PAST = 16384.0
MAGIC = 12582912.0
TWO_PI = float(2.0 * np.pi)
ML = [7, 6, 5, 4, 3, 2, 1, 0, 1, 2, 3, 4, 5, 6, 7, 8, -4, 0.5]
K1 = len(ML)
I_A1, I_A8, I_A4, I_AM4, I_HALF = 8, 15, 3, 16, 17
GAM = [1.0 - 2.0 ** (-5.0 - h) for h in range(4)]


class Grp:
    __slots__ = ("sem", "cnt", "sealed")


class Buf:
    __slots__ = ("w", "r", "name", "grp", "ps")

    def __init__(self, name="", grp=None, ps=False):
        self.w = None
        self.r = []
        self.name = name
        self.grp = grp
        self.ps = ps


class _Rec:
    def __init__(self):
        self.call = None

    def __getattr__(self, name):
        def f(*a, **kw):
            self.call = (name, a, kw)
            return self
        return f


class Sched:
    ENG = ("pe", "dve", "act", "pool", "sp")

    def __init__(self, nc, stack, self_sync=("dve", "act", "pool")):
        self.nc = nc
        self.stack = stack
        self.prog = {k: [] for k in self.ENG}
        self.cnt = {k: 0 for k in self.ENG}
        self.waited = {k: {} for k in self.ENG}
        self.sem = {}
        self.nsem = 0
        for k in ("pe", "dve", "act", "pool"):
            self.sem[k] = self.new_sem("c_" + k)
        self.self_sync = set(self_sync)
        self.groups = []
        self.GC = self.group("gc")
        self.GP = self.group("gp")
        self.GW = [self.group("gw%d" % i) for i in range(4)]
        self.GX = self.group("gx")
        self.GL = [self.group("gl%d" % i) for i in range(2)]
        self.GS = [self.group("gs%d" % i) for i in range(3)]

    def group(self, name):
        g = Grp()
        g.sem = self.new_sem(name)
        g.cnt = 0
        g.sealed = False
        self.groups.append(g)
        return g

    def new_sem(self, name):
        self.nsem += 1
        assert self.nsem < 98, "too many semaphores"
        return self.stack.enter_context(self.nc.semaphore(name + "_%d" % self.nsem))

    def _waits(self, eng, deps):
        w = self.waited[eng]
        need = {}
        dd = []
        for d in deps:
            if isinstance(d, Grp):
                d.sealed = True
                dd.append((d.sem, d.cnt))
            else:
                dd.append(d)
        deps = dd
        for (s, v) in deps:
            if eng in self.sem and s is self.sem[eng] and eng not in self.self_sync:
                continue
            k = id(s)
            if w.get(k, 0) >= v:
                continue
            if k not in need or need[k][1] < v:
                need[k] = (s, v)
        for k, (s, v) in need.items():
            w[k] = v
            self.prog[eng].append(lambda e, s=s, v=v: e.wait_ge(s, v))

    def op(self, eng, fn, reads=(), writes=()):
        deps = []
        for b in reads:
            if b.w is not None:
                deps.append(b.w)
            if b.ps:
                mys = self.sem[eng]
                deps.extend(d for d in b.r if not (isinstance(d, tuple) and d[0] is mys))
        for b in writes:
            if b.w is not None:
                deps.append(b.w)
            deps.extend(b.r)
        self._waits(eng, deps)
        self.cnt[eng] += 1
        c = self.cnt[eng]
        s = self.sem[eng]
        rec = _Rec()
        fn(rec)
        name, a, kw = rec.call
        self.prog[eng].append(lambda e, name=name, a=a, kw=kw, s=s: getattr(e, name)(*a, **kw).then_inc(s, 1))
        for b in reads:
            b.r.append((s, c))
        for b in writes:
            b.w = (s, c)
            b.r = []

    def dma(self, q, out, in_, reads=(), writes=(), **kw):
        tb = writes[0] if writes else reads[0]
        g = tb.grp
        if g is None:
            g = self.GP if q == "pool" else (self.GC if writes else self.GS[0])
        deps = []
        for b in reads:
            if b.w is not None:
                deps.append(b.w)
        for b in writes:
            if b.w is not None and b.w is not g:
                deps.append(b.w)
            deps.extend(b.r)
        self._waits(q, deps)
        if g.sealed and g.cnt > 0:
            self._waits(q, [(g.sem, g.cnt)])
        g.sealed = False
        g.cnt += 16
        s = g.sem
        self.prog[q].append(
            lambda e, out=out, in_=in_, s=s, kw=kw: e.dma_start(out=out, in_=in_, **kw).then_inc(s, 16))
        for b in reads:
            b.r.append(g)
        for b in writes:
            b.w = g
            b.r = []

    def barrier(self, engines=None):
        deps = [(self.sem[k], self.cnt[k]) for k in ("pe", "dve", "act", "pool") if self.cnt[k] > 0]
        deps += [g for g in self.groups if g.cnt > 0]
        for e in (engines or self.ENG):
            self._waits(e, deps)

    def run_block(self):
        nc = self.nc
        with nc.Block() as block:
            @block.sync
            def _(e):
                for t in self.prog["sp"]:
                    t(e)

            @block.tensor
            def _(e):
                for t in self.prog["pe"]:
                    t(e)

            @block.vector
            def _(e):
                for t in self.prog["dve"]:
                    t(e)

            @block.scalar
            def _(e):
                for t in self.prog["act"]:
                    t(e)

            @block.gpsimd
            def _(e):
                for t in self.prog["pool"]:
                    t(e)


_CONSTS = None


def _consts():
    global _CONSTS
    if _CONSTS is not None:
        return _CONSTS
    f = np.float32
    c = {}
    c["c_ident"] = np.eye(128, dtype=f)
    m = np.zeros((8, 128, 240), f)
    for a in range(8):
        for i in range(16):
            m[a, 16 * a + i, 112 + i] = 1.0
    c["c_masters"] = m
    ml = np.array(ML, np.float64)
    rows = np.concatenate([ml / (2 * np.pi), ml, 8.0 * (np.arange(64) + 1) / (2 * np.pi)])
    c["c_rows"] = rows.astype(f)[None, :]
    sg = np.zeros((128, 2), f)
    sg[:64, 0] = 1.0
    sg[64:, 0] = -1.0
    sg[:64, 1] = -1.0
    sg[64:, 1] = 1.0
    c["c_sgn"] = sg
    inv = (f(10000.0) ** (-(np.arange(64, dtype=f) / f(64.0)))).astype(f)
    pos = np.zeros((128, NT), f)
    for n in range(NTP):
        pos[:, n] = 128 * n + np.arange(128)
    pos[:64, 16] = PAST + (np.arange(64) % 4)
    ang = (pos[:, :, None] * inv[None, None, :]).astype(f).astype(np.float64)
    c["c_rope"] = np.stack([np.cos(ang), np.sin(ang), -np.sin(ang)]).astype(f)
    lg = np.log(np.array(GAM, np.float64))
    sc = 128.0 ** -0.5
    idx = np.arange(128)
    dm = np.zeros((128, 4, 128), np.float64)
    diff = idx[None, :] - idx[:, None]
    for h in range(4):
        dm[:, h, :] = np.where(diff >= 0, np.exp(np.maximum(diff, 0) * lg[h]), 0.0) * sc
    c["c_dmask_p"] = dm.reshape(128, 512).astype(f)
    ds_ = np.zeros((64, 4, 64), np.float64)
    r = np.arange(64)
    bb = r // 4
    tt = r % 4
    same = bb[:, None] == bb[None, :]
    dts = tt[None, :] - tt[:, None]
    for h in range(4):
        ds_[:, h, :] = np.where(same & (dts >= 0), np.exp(np.maximum(dts, 0) * lg[h]), 0.0) * sc
    c["c_dmask_s"] = ds_.reshape(64, 256).astype(f)
    xi_p = np.stack([np.exp((idx + 1.0) * lg[h]) * sc for h in range(4)])
    xi_s = np.stack([np.exp((tt + 1.0) * lg[h]) * sc for h in range(4)])
    c["c_xi"] = np.concatenate([xi_p.reshape(-1), xi_s.reshape(-1)]).astype(f)[None, :]
    zp = np.stack([np.exp((127.0 - idx) * lg[h]) for h in range(4)], axis=1)
    c["c_zeta_p"] = zp.astype(f)
    zs = np.zeros((64, 16, 4), np.float64)
    for h in range(4):
        for b in range(16):
            zs[:, b, h] = np.where(bb == b, np.exp((3.0 - tt) * lg[h]), 0.0)
    c["c_zs"] = zs.reshape(64, 64).astype(f)
    cm = np.zeros((16, 64), f)
    for b in range(16):
        cm[b, 4 * b:4 * b + 4] = 1.0
    c["c_cmask"] = cm.reshape(1, -1)
    _CONSTS = c
    return c


W_NAMES = ["g_mix", "w_in", "lam_re", "lam_im", "log_dt", "b_re", "b_im", "c_re", "c_im", "d_skip", "w_glu",
           "ret_gn", "w_out", "g_xattn", "g_mem", "w_mq", "w_mk", "w_mv", "w_mo", "g_mlp", "w_up", "w_down",
           "g_final"]
W_SHAPES = {"g_mix": [D], "w_in": [D, 2560], "lam_re": [G, 64], "lam_im": [G, 64], "log_dt": [G],
            "b_re": [G, 64, 16], "b_im": [G, 64, 16], "c_re": [G * 16, 64], "c_im": [G * 16, 64], "d_skip": [512],
            "w_glu": [512, 512], "ret_gn": [512], "w_out": [D, D], "g_xattn": [D], "g_mem": [D], "w_mq": [D, D],
            "w_mk": [D, D], "w_mv": [D, D], "w_mo": [D, D], "g_mlp": [D], "w_up": [D, DFF], "w_down": [DFF, D],
            "g_final": [D]}
IN_SHAPES = {"xp": [SEQ, D], "xs": [TS, D], "memp": [MEM, D], "s5r": [512, 64], "s5i": [512, 64],
             "sret": [16, 4, 128, 128], "ck": [16, MEM, D], "cv": [16, MEM, D]}
OUT_SHAPES = {"yp": [SEQ, D], "ys": [TS, D], "o_s5r_p": [G, 64], "o_s5i_p": [G, 64], "o_ret_p": [4, 128, 128],
              "o_mk": [MEM, D], "o_mv": [MEM, D], "o_s5r_s": [512, 64], "o_s5i_s": [512, 64],
              "o_ret_s": [16, 4, 128, 128]}


def build(stage=99, dbg=False):
    nc = bass.Bass("TRN2", target_bir_lowering=False)
    cst = _consts()
    I = {}
    for k, shp in list(IN_SHAPES.items()) + list(W_SHAPES.items()):
        I[k] = nc.dram_tensor(k, shp, F32, kind="ExternalInput").ap()
    for k, v in cst.items():
        I[k] = nc.dram_tensor(k, list(v.shape), F32, kind="ExternalInput").ap()
    O = {}
    for k, shp in OUT_SHAPES.items():
        O[k] = nc.dram_tensor(k, shp, F32, kind="ExternalOutput").ap()
    if dbg:
        O["dbg_ssm"] = nc.dram_tensor("dbg_ssm", [128, 4, NTOK], F32, kind="ExternalOutput").ap()
        O["dbg_x"] = nc.dram_tensor("dbg_x", [128, NT, D], F32, kind="ExternalOutput").ap()

    with ExitStack() as st:
        S = Sched(nc, st)

        def alloc(stack, name, shape, dt=F32):
            return stack.enter_context(nc.sbuf_tensor(name, shape, dt))

        def palloc(stack, name, shape, dt=F32):
            return stack.enter_context(nc.psum_tensor(name, shape, dt))

        def V(fn, r=(), w=()):
            S.op("dve", fn, reads=r, writes=w)

        def A(fn, r=(), w=()):
            S.op("act", fn, reads=r, writes=w)

        import os as _os0
        _nopool = _os0.environ.get("K_NOPOOL") == "1"

        def PL(fn, r=(), w=()):
            S.op("dve" if _nopool else "pool", fn, reads=r, writes=w)

        def T(fn, r=(), w=()):
            S.op("pe", fn, reads=r, writes=w)

        nck = nc.allow_non_contiguous_dma(reason="small param layout loads")
        nck.__enter__()

        identb = alloc(st, "identb", [128, 128], BF16)
        identf = alloc(st, "identf", [128, 128], F32)
        sgn = alloc(st, "sgn", [128, 2])
        epsc = alloc(st, "epsc", [128, 1])
        ssmT = alloc(st, "ssmT", [128, 4, NTOK], BF16)
        b_const = Buf("const")
        b_ssmT = [Buf("ssmT%d" % i) for i in range(5)]
        b_constp = Buf("constp")
        S.dma("pool", identb[:], I["c_ident"][:, :], writes=[b_constp])
        S.dma("sp", identf[:], I["c_ident"][:, :], writes=[b_const])
        S.dma("sp", sgn[:], I["c_sgn"][:, :], writes=[b_const])
        V(lambda e: e.memset(epsc[:], EPS), r=[b_constp], w=[b_const])
        PS = [palloc(st, "ps%d" % i, [128, 512], F32) for i in range(8)]
        bPS = [Buf("ps%d" % i, ps=True) for i in range(8)]

        def ps_bf(i):
            return PS[i][:].bitcast(BF16)

        def make_scr(stack, tag, pbanks):
            d = {"i": 0, "pb": list(pbanks)}
            d["sq"] = [alloc(stack, "sq%s" % tag, [128, D], BF16)] * 2
            d["ss"] = [alloc(stack, "ss%s%d" % (tag, i), [128, 1]) for i in range(2)]
            d["rstd"] = [alloc(stack, "rstd%s%d" % (tag, i), [128, 1]) for i in range(2)]
            d["hb"] = [alloc(stack, "hb%s%d" % (tag, i), [128, D], BF16) for i in range(2)]
            d["ba"] = [Buf("ba%s%d" % (tag, i)) for i in range(2)]
            d["bh"] = [Buf("bh%s%d" % (tag, i)) for i in range(2)]
            return d

        def rmsnorm_hT(xt_ap, bx, npart, gcol, hT_ap, bhT, scr, col0, ph, ln=False, bg=None, out4=None):
            k = scr["i"] % 2
            pbank = scr["pb"][scr["i"] % len(scr["pb"])]
            scr["i"] += 1
            sq, ss, rstd, hb = scr["sq"][k], scr["ss"][k], scr["rstd"][k], scr["hb"][k]
            ba, bh = scr["ba"][k], scr["bh"][k]
            A(lambda e: e.activation(out=sq[:npart, :], in_=xt_ap, func=AF.Square, accum_out=ss[:npart, :]),
              r=[bx], w=[ba])
            if ln:
                A(lambda e: e.activation(out=rstd[:npart, :], in_=ss[:npart, :], func=AF.Ln, scale=1.0 / D,
                                         bias=epsc[:npart, :]), r=[ba, b_const], w=[ba])
                A(lambda e: e.activation(out=rstd[:npart, :], in_=rstd[:npart, :], func=AF.Exp, scale=-0.5),
                  r=[ba], w=[ba])
            else:
                A(lambda e: e.activation(out=rstd[:npart, :], in_=ss[:npart, :], func=AF.Sqrt, scale=1.0 / D,
                                         bias=epsc[:npart, :]), r=[ba, b_const], w=[ba])
                V(lambda e: e.reciprocal(out=rstd[:npart, :], in_=rstd[:npart, :]), r=[ba], w=[ba])
            V(lambda e: e.tensor_scalar(out=hb[:npart, :], in0=xt_ap, scalar1=rstd[:npart, :], scalar2=None,
                                        op0=ALU.mult), r=[bx, ba], w=[bh])
            pv = ps_bf(pbank)
            for kt in range(8):
                T(lambda e, kt=kt: e.transpose(out=pv[:, kt * 128:kt * 128 + npart],
                                               in_=hb[:npart, kt * 128:(kt + 1) * 128],
                                               identity=identb[:npart, :npart]),
                  r=[bh, b_const], w=[bPS[pbank]])
            if out4 is not None:
                V(lambda e: e.tensor_tensor(
                    out=out4, in0=pv.rearrange("p (k c s) -> p k c s", k=8, s=8),
                    in1=gcol.unsqueeze(2).unsqueeze(3).to_broadcast([128, 8, 16, 8]), op=ALU.mult),
                  r=[bPS[pbank], b_const] + ([bg] if bg is not None else []), w=[bhT])
                return
            V(lambda e: e.tensor_tensor(
                out=hT_ap[:, :, col0:col0 + npart],
                in0=pv.rearrange("p (k t) -> p k t", k=8)[:, :, 0:npart],
                in1=gcol.unsqueeze(2).to_broadcast([128, 8, npart]), op=ALU.mult),
              r=[bPS[pbank], b_const] + ([bg] if bg is not None else []), w=[bhT])

        def load_w_bf16(dst, bdst, src, kt_n, ncols, c0=0):
            for kt in range(kt_n):
                for cc in range(0, ncols, 1024):
                    w_ = min(1024, ncols - cc)
                    S.dma("pool", dst[:, kt, cc:cc + w_], src[kt * 128:(kt + 1) * 128, c0 + cc:c0 + cc + w_],
                          writes=[bdst])

        with ExitStack() as sa:
            Wt = alloc(sa, "Wt", [128, G, 128], BF16)
            Wst = alloc(sa, "Wst", [128, G, 128], BF16)
            Tt = alloc(sa, "Tt", [128, G, 128], BF16)
            Vt = alloc(sa, "Vt", [128, G, 128], BF16)
            COSR = alloc(sa, "COSR", [128, G, 64])
            SINR = alloc(sa, "SINR", [128, G, 64])
            masters = alloc(sa, "masters", [128, 8, 240], BF16)
            AR = alloc(sa, "AR", [128, G, K1])
            AI = alloc(sa, "AI", [128, G, K1])
            MAGJ = alloc(sa, "MAGJ", [128, G, K1])
            DS = alloc(sa, "DS", [128, G])
            gm = alloc(sa, "gm", [128, 8])
            winu = alloc(sa, "winu", [128, 8, 512], BF16)
            wglu = alloc(sa, "wglu", [128, 4, 512], BF16)
            b_tab = Buf("s5tab")
            b_winu = Buf("winu", S.GW[0])
            b_wglu = Buf("wglu", S.GW[1])
            b_tabp = Buf("s5tabp")
            S.dma("pool", masters[:], I["c_masters"].rearrange("a k j -> k a j"), writes=[b_tabp])
            S.dma("sp", gm[:], I["g_mix"].rearrange("(k p) -> p k", p=128), writes=[b_tab])
            for tau in range(8):
                S.dma("sp", DS[16 * tau:16 * tau + 16, :], I["d_skip"].rearrange("(g h) -> h g", h=16),
                      writes=[b_tab])
            load_w_bf16(winu, b_winu, I["w_in"], 8, 512, 0)
            load_w_bf16(wglu, b_wglu, I["w_glu"], 4, 512, 0)

            with ExitStack() as s0:
                rows = alloc(s0, "rows", [128, 2 * K1 + 64])
                LR = alloc(s0, "LR", [128, G])
                LI = alloc(s0, "LI", [128, G])
                DT = alloc(s0, "DT", [128, G])
                LRDT = alloc(s0, "LRDT", [128, G])
                LIDT = alloc(s0, "LIDT", [128, G])
                tA = alloc(s0, "tA", [128, G, 64])
                tB = alloc(s0, "tB", [128, G, 64])
                tC = alloc(s0, "tC", [128, G, 64])
                COSJ = alloc(s0, "COSJ", [128, G, K1])
                SINJ = alloc(s0, "SINJ", [128, G, K1])
                sm = alloc(s0, "sm", [128, 12, G])
                Br1 = alloc(s0, "Br1", [128, G, 16])
                Br2 = alloc(s0, "Br2", [128, G, 16])
                BB1 = alloc(s0, "BB1", [128, G, 16])
                BB2 = alloc(s0, "BB2", [128, G, 16])
                tb1 = alloc(s0, "tb1", [128, G, 16])
                big1 = alloc(s0, "big1", [128, G, 128])
                big2 = alloc(s0, "big2", [128, G, 128])
                WTpad = alloc(s0, "WTpad", [128, G, 256], BF16)
                WTs = alloc(s0, "WTs", [128, G, 128], BF16)
                CN1 = alloc(s0, "CN1", [128, 4, 128])
                CN2 = alloc(s0, "CN2", [128, 4, 128])
                CMa = alloc(s0, "CMa", [128, G, 16])
                CMb = alloc(s0, "CMb", [128, G, 16])
                CMab = alloc(s0, "CMab", [128, G, 16], BF16)
                b0 = Buf("p0in")
                bt = Buf("p0tmp")
                S.dma("sp", rows[:], I["c_rows"][0:1, :].partition_broadcast(128), writes=[b0])
                for hf in range(2):
                    S.dma("sp", LR[64 * hf:64 * hf + 64, :], I["lam_re"].rearrange("g p -> p g"), writes=[b0])
                    S.dma("sp", LI[64 * hf:64 * hf + 64, :], I["lam_im"].rearrange("g p -> p g"), writes=[b0])
                S.dma("sp", DT[:], I["log_dt"].rearrange("(o g) -> o g", o=1).partition_broadcast(128), writes=[b0])
                S.dma("sp", Br1[0:64], I["b_re"].rearrange("g p h -> p g h"), writes=[b0])
                S.dma("sp", Br1[64:128], I["b_im"].rearrange("g p h -> p g h"), writes=[b0])
                S.dma("sp", Br2[0:64], I["b_im"].rearrange("g p h -> p g h"), writes=[b0])
                S.dma("sp", Br2[64:128], I["b_re"].rearrange("g p h -> p g h"), writes=[b0])
                S.dma("sp", CN1[:, :, 0:64], I["c_re"].rearrange("(c r) p -> r c p", r=128), writes=[b0])
                S.dma("sp", CN1[:, :, 64:128], I["c_im"].rearrange("(c r) p -> r c p", r=128), writes=[b0])
                S.dma("sp", CN2[:, :, 0:64], I["c_im"].rearrange("(c r) p -> r c p", r=128), writes=[b0])
                S.dma("sp", CN2[:, :, 64:128], I["c_re"].rearrange("(c r) p -> r c p", r=128), writes=[b0])
                MT1 = rows[:, 0:K1]
                MLr = rows[:, K1:2 * K1]
                MRT = rows[:, 2 * K1:2 * K1 + 64]
                A(lambda e: e.activation(out=DT[:], in_=DT[:], func=AF.Exp), r=[b0], w=[b0])
                V(lambda e: e.tensor_tensor(out=LRDT[:], in0=LR[:], in1=DT[:], op=ALU.mult), r=[b0], w=[bt])
                V(lambda e: e.tensor_tensor(out=LIDT[:], in0=LI[:], in1=DT[:], op=ALU.mult), r=[b0], w=[bt])

                def trig(mt_ap, K, cos_out, sin_out):
                    shp = [128, G, K]
                    a_, b_, c_ = tA[:, :, 0:K], tB[:, :, 0:K], tC[:, :, 0:K]
                    V(lambda e: e.tensor_tensor(out=a_, in0=LIDT[:].unsqueeze(2).to_broadcast(shp),
                                                in1=mt_ap.unsqueeze(1).to_broadcast(shp), op=ALU.mult),
                      r=[bt, b0], w=[bt])
                    for (outp, off) in ((sin_out, 0.0), (cos_out, 0.25)):
                        if outp is None:
                            continue
                        V(lambda e, off=off: e.tensor_scalar(out=c_, in0=a_, scalar1=off, scalar2=None,
                                                             op0=ALU.add), r=[bt], w=[bt])
                        V(lambda e: e.tensor_scalar(out=b_, in0=c_, scalar1=MAGIC, scalar2=None, op0=ALU.add),
                          r=[bt], w=[bt])
                        V(lambda e: e.tensor_scalar(out=b_, in0=b_, scalar1=MAGIC, scalar2=None, op0=ALU.subtract),
                          r=[bt], w=[bt])
                        V(lambda e: e.tensor_tensor(out=c_, in0=c_, in1=b_, op=ALU.subtract), r=[bt], w=[bt])
                        A(lambda e, outp=outp: e.activation(out=outp, in_=c_, func=AF.Sin, scale=TWO_PI),
                          r=[bt], w=[b_tab])

                trig(MT1, K1, COSJ[:], SINJ[:])
                trig(MRT, 64, COSR[:], SINR[:])
                shpj = [128, G, K1]
                V(lambda e: e.tensor_tensor(out=MAGJ[:], in0=LRDT[:].unsqueeze(2).to_broadcast(shpj),
                                            in1=MLr.unsqueeze(1).to_broadcast(shpj), op=ALU.mult),
                  r=[bt, b0], w=[b_tab])
                A(lambda e: e.activation(out=MAGJ[:], in_=MAGJ[:], func=AF.Exp), r=[b_tab], w=[b_tab])
                V(lambda e: e.tensor_tensor(out=AR[:], in0=MAGJ[:], in1=COSJ[:], op=ALU.mult), r=[b_tab], w=[b_tab])
                V(lambda e: e.tensor_tensor(out=AI[:], in0=MAGJ[:], in1=SINJ[:], op=ALU.mult), r=[b_tab], w=[b_tab])
                em1, shalf, cm1, am1r, ai1, den, fr, fi, t0_, t1_ = [sm[:, i, :] for i in range(10)]
                x_ = LRDT[:]
                V(lambda e: e.tensor_scalar(out=em1, in0=x_, scalar1=0.2, scalar2=1.0, op0=ALU.mult, op1=ALU.add),
                  r=[bt], w=[bt])
                for cf in (0.25, 1.0 / 3.0, 0.5):
                    V(lambda e: e.tensor_tensor(out=em1, in0=em1, in1=x_, op=ALU.mult), r=[bt], w=[bt])
                    V(lambda e, cf=cf: e.tensor_scalar(out=em1, in0=em1, scalar1=cf, scalar2=1.0, op0=ALU.mult,
                                                       op1=ALU.add), r=[bt], w=[bt])
                V(lambda e: e.tensor_tensor(out=em1, in0=em1, in1=x_, op=ALU.mult), r=[bt], w=[bt])
                V(lambda e: e.tensor_copy(out=shalf, in_=SINJ[:, :, I_HALF]), r=[b_tab], w=[bt])
                V(lambda e: e.scalar_tensor_tensor(out=cm1, in0=shalf, scalar=-2.0, op0=ALU.mult, in1=shalf,
                                                   op1=ALU.mult), r=[bt], w=[bt])
                V(lambda e: e.tensor_tensor(out=am1r, in0=em1, in1=COSJ[:, :, I_A1], op=ALU.mult), r=[bt, b_tab], w=[bt])
                V(lambda e: e.tensor_tensor(out=am1r, in0=am1r, in1=cm1, op=ALU.add), r=[bt], w=[bt])
                V(lambda e: e.tensor_copy(out=ai1, in_=AI[:, :, I_A1]), r=[b_tab], w=[bt])
                V(lambda e: e.tensor_tensor(out=den, in0=LR[:], in1=LR[:], op=ALU.mult), r=[b0], w=[bt])
                V(lambda e: e.tensor_tensor(out=t0_, in0=LI[:], in1=LI[:], op=ALU.mult), r=[b0], w=[bt])
                V(lambda e: e.tensor_tensor(out=den, in0=den, in1=t0_, op=ALU.add), r=[bt], w=[bt])
                V(lambda e: e.reciprocal(out=den, in_=den), r=[bt], w=[bt])
                V(lambda e: e.tensor_tensor(out=fr, in0=am1r, in1=LR[:], op=ALU.mult), r=[bt, b0], w=[bt])
                V(lambda e: e.tensor_tensor(out=t0_, in0=ai1, in1=LI[:], op=ALU.mult), r=[bt, b0], w=[bt])
                V(lambda e: e.tensor_tensor(out=fr, in0=fr, in1=t0_, op=ALU.add), r=[bt], w=[bt])
                V(lambda e: e.tensor_tensor(out=fr, in0=fr, in1=den, op=ALU.mult), r=[bt], w=[bt])
                V(lambda e: e.tensor_tensor(out=fi, in0=ai1, in1=LR[:], op=ALU.mult), r=[bt, b0], w=[bt])
                V(lambda e: e.tensor_tensor(out=t0_, in0=am1r, in1=LI[:], op=ALU.mult), r=[bt, b0], w=[bt])
                V(lambda e: e.tensor_tensor(out=fi, in0=fi, in1=t0_, op=ALU.subtract), r=[bt], w=[bt])
                V(lambda e: e.tensor_tensor(out=fi, in0=fi, in1=den, op=ALU.mult), r=[bt], w=[bt])
                V(lambda e: e.tensor_scalar(out=Br2[:], in0=Br2[:], scalar1=sgn[:, 1:2], scalar2=None, op0=ALU.mult),
                  r=[b0, b_const], w=[b0])
                shb = [128, G, 16]
                frb = fr.unsqueeze(2).to_broadcast(shb)
                fib = fi.unsqueeze(2).to_broadcast(shb)
                V(lambda e: e.tensor_tensor(out=BB1[:], in0=Br1[:], in1=frb, op=ALU.mult), r=[b0, bt], w=[bt])
                V(lambda e: e.tensor_tensor(out=tb1[:], in0=Br2[:], in1=fib, op=ALU.mult), r=[b0, bt], w=[bt])
                V(lambda e: e.tensor_tensor(out=BB1[:], in0=BB1[:], in1=tb1[:], op=ALU.add), r=[bt], w=[bt])
                V(lambda e: e.tensor_tensor(out=BB2[:], in0=Br2[:], in1=frb, op=ALU.mult), r=[b0, bt], w=[bt])
                V(lambda e: e.tensor_tensor(out=tb1[:], in0=Br1[:], in1=fib, op=ALU.mult), r=[b0, bt], w=[bt])
                V(lambda e: e.tensor_tensor(out=BB2[:], in0=BB2[:], in1=tb1[:], op=ALU.subtract), r=[bt], w=[bt])
                sh4 = [128, G, 8, 16]
                arv = AR[:, :, 0:8].unsqueeze(3).to_broadcast(sh4)
                aiv = AI[:, :, 0:8].unsqueeze(3).to_broadcast(sh4)
                bb1 = BB1[:].unsqueeze(2).to_broadcast(sh4)
                bb2 = BB2[:].unsqueeze(2).to_broadcast(sh4)
                g1 = big1[:].rearrange("p g (s h) -> p g s h", s=8)
                g2 = big2[:].rearrange("p g (s h) -> p g s h", s=8)
                V(lambda e: e.memset(WTpad[:], 0.0), w=[bt])
                V(lambda e: e.tensor_tensor(out=g1, in0=arv, in1=bb1, op=ALU.mult), r=[b_tab, bt], w=[bt])
                V(lambda e: e.tensor_tensor(out=g2, in0=aiv, in1=bb2, op=ALU.mult), r=[b_tab, bt], w=[bt])
                V(lambda e: e.tensor_tensor(out=WTpad[:, :, 0:128], in0=big1[:], in1=big2[:], op=ALU.add),
                  r=[bt], w=[bt])
                V(lambda e: e.tensor_tensor(out=g1, in0=arv, in1=bb2, op=ALU.mult), r=[b_tab, bt], w=[bt])
                V(lambda e: e.tensor_tensor(out=g2, in0=aiv, in1=bb1, op=ALU.mult), r=[b_tab, bt], w=[bt])
                V(lambda e: e.tensor_tensor(out=WTs[:], in0=big1[:], in1=big2[:], op=ALU.subtract), r=[bt], w=[bt])
                for (src_fn, dstt) in ((lambda g: WTpad[:, g, 0:128], Wt), (lambda g: WTs[:, g, :], Wst)):
                    for gq in range(8):
                        bank = gq % 2
                        pv = ps_bf(bank)
                        for j in range(4):
                            g = gq * 4 + j
                            T(lambda e, g=g, j=j, pv=pv, src_fn=src_fn: e.transpose(
                                out=pv[:, j * 128:(j + 1) * 128], in_=src_fn(g), identity=identb[:]),
                              r=[bt, b_const], w=[bPS[bank]])
                        A(lambda e, gq=gq, pv=pv, dstt=dstt: e.copy(
                            out=dstt[:, gq * 4:gq * 4 + 4, :], in_=pv[:, 0:512].rearrange("p (j c) -> p j c", j=4)),
                          r=[bPS[bank]], w=[b_tab])
                for (CN, CM, col) in ((CN1, CMa, 0), (CN2, CMb, None)):
                    for c4 in range(4):
                        bank = 2 + (c4 % 2)
                        T(lambda e, CN=CN, c4=c4, bank=bank: e.transpose(out=PS[bank][:, 0:128], in_=CN[:, c4, :],
                                                                         identity=identf[:]),
                          r=[b0, b_const], w=[bPS[bank]])
                        if col is not None:
                            V(lambda e, CM=CM, c4=c4, bank=bank: e.tensor_scalar(
                                out=CM[:, c4 * 8:(c4 + 1) * 8, :],
                                in0=PS[bank][:, 0:128].rearrange("p (g h) -> p g h", g=8),
                                scalar1=sgn[:, 0:1], scalar2=None, op0=ALU.mult),
                              r=[bPS[bank], b_const], w=[bt])
                        else:
                            V(lambda e, CM=CM, c4=c4, bank=bank: e.tensor_scalar(
                                out=CM[:, c4 * 8:(c4 + 1) * 8, :],
                                in0=PS[bank][:, 0:128].rearrange("p (g h) -> p g h", g=8),
                                scalar1=-1.0, scalar2=None, op0=ALU.mult),
                              r=[bPS[bank]], w=[bt])
                V(lambda e: e.tensor_copy(out=CMab[:], in_=CMa[:]), r=[bt], w=[bt])
                afw = AR[:, :, 8:16].unsqueeze(3).to_broadcast(sh4)
                aifw = AI[:, :, 8:16].unsqueeze(3).to_broadcast(sh4)
                cma = CMa[:].unsqueeze(2).to_broadcast(sh4)
                cmb = CMb[:].unsqueeze(2).to_broadcast(sh4)
                V(lambda e: e.tensor_tensor(out=g1, in0=afw, in1=cma, op=ALU.mult), r=[b_tab, bt], w=[bt])
                V(lambda e: e.tensor_tensor(out=g2, in0=aifw, in1=cmb, op=ALU.mult), r=[b_tab, bt], w=[bt])
                V(lambda e: e.tensor_tensor(out=Vt[:], in0=big1[:], in1=big2[:], op=ALU.add), r=[bt], w=[b_tab])
                for gq in range(8):
                    bank = 4 + (gq % 2)
                    for j in range(4):
                        g = gq * 4 + j
                        for tau in range(8):
                            c0 = (7 - tau) * 16
                            T(lambda e, g=g, j=j, tau=tau, c0=c0, bank=bank: e.matmul(
                                PS[bank][:, j * 128 + tau * 16:j * 128 + tau * 16 + 16],
                                lhsT=WTpad[:, g, c0:c0 + 128], rhs=CMab[:, g, :], start=True, stop=True),
                              r=[bt], w=[bPS[bank]])
                    A(lambda e, gq=gq, bank=bank: e.copy(
                        out=Tt[:, gq * 4:gq * 4 + 4, :], in_=PS[bank][:].rearrange("p (j c) -> p j c", j=4)),
                      r=[bPS[bank]], w=[b_tab])
                S.barrier()
            xst = [alloc(sa, "xst%d" % i, [128, D]) for i in range(2)]
            bxst = [Buf("xst%d" % i, S.GL[i]) for i in range(2)]
            scrA = make_scr(sa, "A", [7])
            bscr = Buf("scrA")
            hT2 = [alloc(sa, "hT_%d" % i, [128, 8, 512], BF16) for i in range(2)]
            bhT2 = [Buf("hT_%d" % i) for i in range(2)]
            uT2 = [alloc(sa, "uT_%d" % i, [128, 4, 512], BF16) for i in range(2)]
            buT2 = [Buf("uT_%d" % i) for i in range(2)]
            U = alloc(sa, "U", [128, G, 64], BF16)
            bU = Buf("U")
            rr = alloc(sa, "rr", [128, G, 64])
            rs = alloc(sa, "rs", [128, G, 64])
            ww = alloc(sa, "ww", [128, G, 64])
            ws = alloc(sa, "ws", [128, G, 64])
            tmpr = alloc(sa, "tmpr", [128, 16, 64])
            b_r, b_rs, b_w, b_ws, b_tmpr = Buf("r"), Buf("rs"), Buf("w"), Buf("ws"), Buf("tmpr")
            Xb = alloc(sa, "Xb", [128, G, 65], BF16)
            bXb = Buf("Xb")
            Xc = alloc(sa, "Xc", [128, G])
            Xsc = alloc(sa, "Xsc", [128, G])
            ctmp = alloc(sa, "ctmp", [128, 2, G])
            bXc = Buf("Xc", S.GS[0])
            ytmp = alloc(sa, "ytmp", [128, 8, 64])
            bytmp = Buf("ytmp")
            Zt = alloc(sa, "Zt", [128, G, 64], BF16)
            bZ = Buf("Z")
            zT = alloc(sa, "zT", [128, 4, 512], BF16)
            bzT = Buf("zT")
            sig = alloc(sa, "sig", [128, 4, 512])
            bsig = Buf("sig")
            H0 = alloc(sa, "H0", [128, 512])
            H0s = alloc(sa, "H0s", [128, 512])
            hn = alloc(sa, "hn", [128, 4, 128])
            hn2 = alloc(sa, "hn2", [128, 4, 128])
            Hp = alloc(sa, "Hp", [128, G, 16])
            Xf = alloc(sa, "Xf", [128, G, 16])
            xo = alloc(sa, "xo", [128, 4, 128])
            bH = Buf("H0")
            bxo = Buf("xo", S.GS[1])
            V(lambda e: e.memset(Xc[:], 0.0), r=[b_tabp], w=[bXc, b_tab])
            V(lambda e: e.memset(Xsc[:], 0.0), w=[bXc])
            V(lambda e: e.memset(Xb[:], 0.0), w=[bXb])

            blocks = [(i * 512, 512, False) for i in range(4)] + [(SEQ, TS, True)]
            if _os0.environ.get("K1A") == "0":
                blocks = []
            def p1a_stageA(bi):
                t0, n, is_s = blocks[bi]
                hT, bhT = hT2[bi % 2], bhT2[bi % 2]
                uT, buT = uT2[bi % 2], buT2[bi % 2]
                ntile = (n + 127) // 128
                for ti in range(ntile):
                    npart = min(128, n - ti * 128)
                    slot = (bi * 4 + ti) % 2
                    src = I["xs"][:, :] if is_s else I["xp"][t0 + ti * 128:t0 + ti * 128 + 128, :]
                    S.dma("sp", xst[slot][:npart, :], src, writes=[bxst[slot]])
                    o4 = None if is_s else hT[:, :, :].rearrange("p k (s c) -> p k c s", s=8)[:, :, ti * 16:(ti + 1) * 16, :]
                    rmsnorm_hT(xst[slot][:npart, :], bxst[slot], npart, gm[:], hT, bhT,
                               scrA, ti * 128, None, bg=b_tab, out4=o4)
                for ct in range(4):
                    bank = ct
                    for kt in range(8):
                        T(lambda e, ct=ct, kt=kt, bank=bank: e.matmul(
                            PS[bank][:, 0:n], lhsT=winu[:, kt, ct * 128:(ct + 1) * 128], rhs=hT[:, kt, 0:n],
                            start=(kt == 0), stop=(kt == 7)), r=[b_winu, bhT], w=[bPS[bank]])
                    A(lambda e, ct=ct, bank=bank: e.copy(out=uT[:, ct, 0:n], in_=PS[bank][:, 0:n]),
                      r=[bPS[bank]], w=[buT])

            if blocks:
                p1a_stageA(0)
            for bi, (t0, n, is_s) in enumerate(blocks):
                nch = n // 8 if not is_s else 16
                uT, buT = uT2[bi % 2], buT2[bi % 2]
                for gq in range(4):
                    bank = 4 + (gq % 2)
                    for j in range(8):
                        g = gq * 8 + j
                        ct, gl = g // 8, g % 8
                        if not is_s:
                            uv = uT[:, ct, 0:n].rearrange("p (s c) -> p s c", s=8)
                            sig_list = list(range(8))
                        else:
                            uv = uT[:, ct, 0:n].rearrange("p (b t) -> p t b", t=4)
                            sig_list = [4, 5, 6, 7]
                        for si, sg_ in enumerate(sig_list):
                            rhs = uv[:, sg_ if not is_s else si, :]
                            T(lambda e, j=j, gl=gl, sg_=sg_, rhs=rhs, si=si, bank=bank, L=len(sig_list): e.matmul(
                                PS[bank][:, j * 64:j * 64 + nch],
                                lhsT=masters[:, gl, 112 - 16 * sg_:240 - 16 * sg_], rhs=rhs,
                                start=(si == 0), stop=(si == L - 1)),
                              r=[b_tab, buT], w=[bPS[bank]])
                    A(lambda e, gq=gq, bank=bank: e.copy(
                        out=U[:, gq * 8:gq * 8 + 8, 0:nch],
                        in_=PS[bank][:].rearrange("p (j c) -> p j c", j=8)[:, :, 0:nch]),
                      r=[bPS[bank]], w=[bU])
                if not is_s:
                    for hf in range(2):
                        for j in range(16):
                            g = hf * 16 + j
                            for (wt, bk) in ((Wt, 0), (Wst, 2)):
                                bank = bk + j // 8
                                T(lambda e, g=g, j=j, wt=wt, bank=bank: e.matmul(
                                    PS[bank][:, (j % 8) * 64:(j % 8) * 64 + 64], lhsT=wt[:, g, :], rhs=U[:, g, :],
                                    start=True, stop=True), r=[b_tab, bU], w=[bPS[bank]])
                        for q in range(2):
                            gs = slice(hf * 16 + q * 8, hf * 16 + q * 8 + 8)
                            Sv = PS[q][:].rearrange("p (j c) -> p j c", j=8)
                            Ssv = PS[2 + q][:].rearrange("p (j c) -> p j c", j=8)
                            tm = tmpr[:, q * 8:q * 8 + 8, :]
                            V(lambda e, gs=gs, Sv=Sv: e.tensor_tensor(out=rr[:, gs, :], in0=Sv, in1=COSR[:, gs, :],
                                                                     op=ALU.mult), r=[bPS[q], b_tab], w=[b_r])
                            V(lambda e, gs=gs, Ssv=Ssv, tm=tm: e.tensor_tensor(out=tm, in0=Ssv, in1=SINR[:, gs, :],
                                                                              op=ALU.mult),
                              r=[bPS[2 + q], b_tab], w=[b_tmpr])
                            V(lambda e, gs=gs, tm=tm: e.tensor_tensor(out=rr[:, gs, :], in0=rr[:, gs, :], in1=tm,
                                                                     op=ALU.subtract), r=[b_r, b_tmpr], w=[b_r])
                            V(lambda e, gs=gs, Ssv=Ssv: e.tensor_tensor(out=rs[:, gs, :], in0=Ssv, in1=COSR[:, gs, :],
                                                                       op=ALU.mult), r=[bPS[2 + q], b_tab], w=[b_rs])
                            V(lambda e, gs=gs, Sv=Sv, tm=tm: e.tensor_tensor(out=tm, in0=Sv, in1=SINR[:, gs, :],
                                                                            op=ALU.mult),
                              r=[bPS[q], b_tab], w=[b_tmpr])
                            V(lambda e, gs=gs, tm=tm: e.tensor_tensor(out=rs[:, gs, :], in0=rs[:, gs, :], in1=tm,
                                                                     op=ALU.add), r=[b_rs, b_tmpr], w=[b_rs])
                    for g in range(G):
                        rho = MAGJ[:, g, I_A8:I_A8 + 1].to_broadcast([128, 64])
                        V(lambda e, g=g, rho=rho: e.tensor_tensor_scan(
                            out=ww[:, g, :], data0=rho, data1=rr[:, g, :], initial=Xc[:, g:g + 1], op0=ALU.mult,
                            op1=ALU.add), r=[b_r, b_tab, bXc], w=[b_w])
                        V(lambda e, g=g, rho=rho: e.tensor_tensor_scan(
                            out=ws[:, g, :], data0=rho, data1=rs[:, g, :], initial=Xsc[:, g:g + 1], op0=ALU.mult,
                            op1=ALU.add), r=[b_rs, b_tab, bXc], w=[b_ws])
                    if bi + 1 < len(blocks):
                        p1a_stageA(bi + 1)
                    ce, se_ = COSR[:, :, 63], SINR[:, :, 63]
                    we, wse = ww[:, :, 63], ws[:, :, 63]
                    V(lambda e: e.tensor_tensor(out=ctmp[:, 0, :], in0=ce, in1=we, op=ALU.mult), r=[b_w, b_tab], w=[bscr])
                    V(lambda e: e.tensor_tensor(out=ctmp[:, 1, :], in0=se_, in1=wse, op=ALU.mult), r=[b_ws, b_tab], w=[bscr])
                    V(lambda e: e.tensor_tensor(out=Xc[:], in0=ctmp[:, 0, :], in1=ctmp[:, 1, :], op=ALU.add),
                      r=[bscr], w=[bXc])
                    V(lambda e: e.tensor_tensor(out=ctmp[:, 0, :], in0=ce, in1=wse, op=ALU.mult), r=[b_ws, b_tab], w=[bscr])
                    V(lambda e: e.tensor_tensor(out=ctmp[:, 1, :], in0=se_, in1=we, op=ALU.mult), r=[b_w, b_tab], w=[bscr])
                    V(lambda e: e.tensor_tensor(out=Xsc[:], in0=ctmp[:, 0, :], in1=ctmp[:, 1, :], op=ALU.subtract),
                      r=[bscr], w=[bXc])
                    if bi > 0:
                        V(lambda e: e.tensor_copy(out=Xb[:, :, 0], in_=Xb[:, :, 64]), r=[bXb], w=[bXb])
                    V(lambda e: e.tensor_tensor(out=ww[:], in0=ww[:], in1=COSR[:], op=ALU.mult), r=[b_w, b_tab, bXc],
                      w=[b_w])
                    PL(lambda e: e.tensor_tensor(out=ws[:], in0=ws[:], in1=SINR[:], op=ALU.mult), r=[b_ws, b_tab, bXc],
                       w=[b_ws])
                    V(lambda e: e.tensor_tensor(out=Xb[:, :, 1:65], in0=ww[:], in1=ws[:], op=ALU.add),
                      r=[b_w, b_ws], w=[bXb])
                    xprev = lambda g: Xb[:, g, 0:64]
                    bXprev = bXb
                    if bi == 3:
                        S.dma("sp", O["o_s5r_p"].rearrange("g p -> p g"), Xc[0:64, :], reads=[bXc])
                        S.dma("sp", O["o_s5i_p"].rearrange("g p -> p g"), Xc[64:128, :], reads=[bXc])
                else:
                    S.dma("sp", hn[:, :, 0:64], I["s5r"].rearrange("(j r) p -> r j p", r=128), writes=[bH])
                    S.dma("sp", hn[:, :, 64:128], I["s5i"].rearrange("(j r) p -> r j p", r=128), writes=[bH])
                    S.dma("sp", hn2[:, :, 0:64], I["s5i"].rearrange("(j r) p -> r j p", r=128), writes=[bH])
                    S.dma("sp", hn2[:, :, 64:128], I["s5r"].rearrange("(j r) p -> r j p", r=128), writes=[bH])
                    for (src_, dst_, bank) in ((hn, H0, 0), (hn2, H0s, 1)):
                        for j in range(4):
                            T(lambda e, src_=src_, j=j, bank=bank: e.transpose(
                                out=PS[bank][:, j * 128:(j + 1) * 128], in_=src_[:, j, :], identity=identf[:]),
                              r=[bH, b_const], w=[bPS[bank]])
                        V(lambda e, dst_=dst_, bank=bank: e.tensor_copy(out=dst_[:], in_=PS[bank][:]),
                          r=[bPS[bank]], w=[bH])
                    V(lambda e: e.tensor_scalar(out=H0s[0:64, :], in0=H0s[0:64, :], scalar1=-1.0, scalar2=None,
                                                op0=ALU.mult), r=[bH], w=[bH])
                    shs = [128, G, 16]
                    h0v = H0[:].rearrange("p (b g) -> p g b", g=G)
                    h0sv = H0s[:].rearrange("p (b g) -> p g b", g=G)

                    def abc(tab, idx):
                        return tab[:, :, idx].unsqueeze(2).to_broadcast(shs)
                    V(lambda e: e.tensor_tensor(out=Xf[:], in0=h0v, in1=abc(AR, I_AM4), op=ALU.mult), r=[bH, b_tab], w=[bxo])
                    V(lambda e: e.tensor_tensor(out=Hp[:], in0=h0sv, in1=abc(AI, I_AM4), op=ALU.mult), r=[bH, b_tab], w=[bxo])
                    V(lambda e: e.tensor_tensor(out=Xb[:, :, 0:16], in0=Xf[:], in1=Hp[:], op=ALU.add), r=[bxo], w=[bXb])
                    V(lambda e: e.tensor_tensor(out=Xf[:], in0=h0v, in1=abc(AR, I_A4), op=ALU.mult), r=[bH, b_tab], w=[bxo])
                    V(lambda e: e.tensor_tensor(out=Hp[:], in0=h0sv, in1=abc(AI, I_A4), op=ALU.mult), r=[bH, b_tab], w=[bxo])
                    V(lambda e: e.tensor_tensor(out=Xf[:], in0=Xf[:], in1=Hp[:], op=ALU.add), r=[bxo], w=[bxo])
                    for q in range(4):
                        bank = q % 2
                        for j in range(8):
                            g = q * 8 + j
                            T(lambda e, g=g, j=j, bank=bank: e.matmul(
                                PS[bank][:, j * 64:j * 64 + 16], lhsT=Wt[:, g, :], rhs=U[:, g, 0:16],
                                start=True, stop=True), r=[b_tab, bU], w=[bPS[bank]])
                        V(lambda e, q=q, bank=bank: e.tensor_tensor(
                            out=Xf[:, q * 8:q * 8 + 8, :], in0=Xf[:, q * 8:q * 8 + 8, :],
                            in1=PS[bank][:].rearrange("p (j c) -> p j c", j=8)[:, :, 0:16], op=ALU.add),
                          r=[bxo, bPS[bank]], w=[bxo])
                    Xf2 = Xf[:].rearrange("p g b -> p (g b)")
                    for j in range(4):
                        T(lambda e, j=j: e.transpose(out=PS[2][:, j * 128:(j + 1) * 128],
                                                     in_=Xf2[:, j * 128:(j + 1) * 128], identity=identf[:]),
                          r=[bxo, b_const], w=[bPS[2]])
                    V(lambda e: e.tensor_copy(out=xo[:], in_=PS[2][:].rearrange("p (j c) -> p j c", j=4)),
                      r=[bPS[2]], w=[bxo])
                    for j in range(4):
                        for gl in range(8):
                            for (nm, c0) in (("o_s5r_s", 0), ("o_s5i_s", 64)):
                                S.dma("sp", O[nm].rearrange("(b g) p -> g b p", g=G)[8 * j + gl],
                                      xo[gl * 16:gl * 16 + 16, j, c0:c0 + 64], reads=[bxo])
                    xprev = lambda g: Xb[:, g, 0:16]
                    bXprev = bXb
                for gq in range(4):
                    bank = 6 + (gq % 2)
                    for j in range(8):
                        g = gq * 8 + j
                        T(lambda e, g=g, j=j, bank=bank: e.matmul(
                            PS[bank][:, j * 64:j * 64 + nch], lhsT=Tt[:, g, :], rhs=U[:, g, 0:nch],
                            start=True, stop=False), r=[b_tab, bU], w=[bPS[bank]])
                        T(lambda e, g=g, j=j, bank=bank: e.matmul(
                            PS[bank][:, j * 64:j * 64 + nch], lhsT=Vt[:, g, :], rhs=xprev(g)[:, 0:nch],
                            start=False, stop=True), r=[b_tab, bXprev], w=[bPS[bank]])
                    gs = slice(gq * 8, gq * 8 + 8)
                    yv = PS[bank][:].rearrange("p (j c) -> p j c", j=8)[:, :, 0:nch]
                    V(lambda e, gs=gs: e.tensor_tensor(out=ytmp[:, :, 0:nch], in0=U[:, gs, 0:nch],
                                                       in1=DS[:, gs].unsqueeze(2).to_broadcast([128, 8, nch]),
                                                       op=ALU.mult), r=[bU, b_tab], w=[bytmp])
                    V(lambda e, yv=yv: e.tensor_tensor(out=ytmp[:, :, 0:nch], in0=yv, in1=ytmp[:, :, 0:nch],
                                                       op=ALU.add), r=[bPS[bank], bytmp], w=[bytmp])
                    A(lambda e, gs=gs: e.activation(out=Zt[:, gs, 0:nch], in_=ytmp[:, :, 0:nch],
                                                    func=AF.Gelu_apprx_tanh), r=[bytmp], w=[bZ])
                for ct in range(4):
                    bank = ct % 2
                    taus = list(range(8)) if not is_s else [4, 5, 6, 7]
                    for ti_, tau in enumerate(taus):
                        for gl in range(8):
                            g = ct * 8 + gl
                            T(lambda e, g=g, gl=gl, tau=tau, ti_=ti_, bank=bank: e.matmul(
                                PS[bank][:, ti_ * 64:ti_ * 64 + nch],
                                lhsT=masters[:, tau, 112 - 16 * gl:240 - 16 * gl], rhs=Zt[:, g, 0:nch],
                                start=(gl == 0), stop=(gl == 7)), r=[b_tab, bZ], w=[bPS[bank]])
                    if not is_s:
                        A(lambda e, ct=ct, bank=bank: e.copy(
                            out=zT[:, ct, 0:n].rearrange("p (c t) -> p t c", t=8),
                            in_=PS[bank][:].rearrange("p (t c) -> p t c", t=8)), r=[bPS[bank]], w=[bzT])
                    else:
                        A(lambda e, ct=ct, bank=bank: e.copy(
                            out=zT[:, ct, 0:n].rearrange("p (b t) -> p t b", t=4),
                            in_=PS[bank][:].rearrange("p (t c) -> p t c", t=8)[:, 0:4, 0:16]),
                          r=[bPS[bank]], w=[bzT])
                for ct in range(4):
                    bank = 2 + (ct % 2)
                    for kt in range(4):
                        T(lambda e, ct=ct, kt=kt, bank=bank: e.matmul(
                            PS[bank][:, 0:n], lhsT=wglu[:, kt, ct * 128:(ct + 1) * 128], rhs=zT[:, kt, 0:n],
                            start=(kt == 0), stop=(kt == 3)), r=[b_wglu, bzT], w=[bPS[bank]])
                    A(lambda e, ct=ct, bank=bank: e.activation(out=sig[:, ct, 0:n], in_=PS[bank][:, 0:n],
                                                               func=AF.Sigmoid), r=[bPS[bank]], w=[bsig])
                V(lambda e: e.tensor_tensor(out=ssmT[:, :, t0:t0 + n], in0=zT[:, :, 0:n], in1=sig[:, :, 0:n],
                                            op=ALU.mult), r=[bzT, bsig], w=[b_ssmT[bi]])
            S.barrier()
        if dbg:
            with ExitStack() as sd:
                dtmp = alloc(sd, "dtmp", [128, 4, NTOK])
                bd = Buf("dtmp", S.GS[2])
                V(lambda e: e.tensor_copy(out=dtmp[:], in_=ssmT[:]), r=b_ssmT, w=[bd])
                S.dma("sp", O["dbg_ssm"][:, :, :], dtmp[:], reads=[bd])
                S.barrier()
        if stage <= 1:
            S.barrier()
            S.run_block()
            nck.__exit__(None, None, None)
            return nc

        with ExitStack() as sbx:
            x = alloc(sbx, "x", [128, NT, D])
            bx = [Buf("x%d" % n, S.GX) for n in range(NT)]
            for n in range(NTP):
                S.dma("sp", x[:, n, :], I["xp"][n * 128:(n + 1) * 128, :], writes=[bx[n]])
            S.dma("sp", x[0:TS, 16, :], I["xs"][:, :], writes=[bx[16]])
            scrB = make_scr(sbx, "B", [7])
            hT1 = alloc(sbx, "hT1", [128, 8, 128], BF16)
            bhT1 = Buf("hT1")

            def resid_add(n, npart, half, bank):
                V(lambda e: e.tensor_tensor(out=x[:npart, n, half * 512:(half + 1) * 512], in0=PS[bank][:npart, :],
                                            in1=x[:npart, n, half * 512:(half + 1) * 512], op=ALU.add),
                  r=[bPS[bank], bx[n]], w=[bx[n]])

            with ExitStack() as s1:
                wq = alloc(s1, "wqkvg", [128, 8, 2048], BF16)
                wout = alloc(s1, "wout", [128, 8, D], BF16)
                b_wqc = [Buf("wq%d" % c, S.GW[c]) for c in range(4)]
                b_wout = Buf("wout", S.GW[0])
                for c in range(4):
                    for kt in range(8):
                        S.dma("pool", wq[:, kt, c * 512:(c + 1) * 512],
                              I["w_in"][kt * 128:(kt + 1) * 128, 512 + c * 512:512 + (c + 1) * 512], writes=[b_wqc[c]])
                wout_loaded = [False]
                gm2 = alloc(s1, "gm2", [128, 8])
                gn = alloc(s1, "gn", [128, 4])
                rope = alloc(s1, "rope", [128, 3, NT, 64])
                dmp = alloc(s1, "dmp", [128, 512])
                dms = alloc(s1, "dms", [64, 256])
                xi = alloc(s1, "xi", [128, 768])
                zetap = alloc(s1, "zetap", [128, 4])
                zs = alloc(s1, "zs", [64, 64])
                cmask = alloc(s1, "cmask", [128, 16 * 64])
                b_t1 = Buf("tab1")
                S.dma("sp", gm2[:], I["g_mix"].rearrange("(k p) -> p k", p=128), writes=[b_t1])
                S.dma("sp", gn[:], I["ret_gn"].rearrange("(k p) -> p k", p=128), writes=[b_t1])
                for a_ in range(3):
                    S.dma("sp", rope[:, a_, :, :], I["c_rope"][a_], writes=[b_t1])
                S.dma("sp", dmp[:], I["c_dmask_p"][:, :], writes=[b_t1])
                S.dma("sp", dms[:], I["c_dmask_s"][:, :], writes=[b_t1])
                S.dma("sp", xi[:], I["c_xi"][0:1, :].partition_broadcast(128), writes=[b_t1])
                S.dma("sp", zetap[:], I["c_zeta_p"][:, :], writes=[b_t1])
                S.dma("sp", zs[:], I["c_zs"][:, :], writes=[b_t1])
                S.dma("sp", cmask[:], I["c_cmask"][0:1, :].partition_broadcast(128), writes=[b_t1])
                def load_wout():
                    load_w_bf16(wout, b_wout, I["w_out"], 8, D, 0)
                    for k in range(4):
                        V(lambda e: e.tensor_scalar(out=wout[:, 4 + k, :], in0=wout[:, 4 + k, :], scalar1=gn[:, k:k + 1],
                                                    scalar2=None, op0=ALU.mult), r=[b_wout, b_t1], w=[b_wout])
                    wout_loaded[0] = True
                t1q = alloc(s1, "t1q", [128, 512])
                t2q = alloc(s1, "t2q", [128, 512])
                t1k = alloc(s1, "t1k", [128, 512])
                t2k = alloc(s1, "t2k", [128, 512])
                qr = alloc(s1, "qr", [128, 512], BF16)
                kr = alloc(s1, "kr", [128, 512], BF16)
                qT = alloc(s1, "qT", [128, 4, 128], BF16)
                qxT = alloc(s1, "qxT", [128, 4, 128], BF16)
                kT = alloc(s1, "kT", [128, 4, 128], BF16)
                vb = alloc(s1, "vb", [128, 512], BF16)
                vz = alloc(s1, "vz", [128, 512], BF16)
                sg_ = alloc(s1, "sgl", [128, 512])
                sT = alloc(s1, "sT", [128, 4, 128], BF16)
                Sst = alloc(s1, "Sst", [128, 4, 128])
                Sbf = alloc(s1, "Sbf", [128, 4, 128], BF16)
                stats = alloc(s1, "stats", [128, 4, 6])
                mv = alloc(s1, "mv", [128, 4, 2])
                rs4 = alloc(s1, "rs4", [128, 4])
                nb4 = alloc(s1, "nb4", [128, 4])
                on = alloc(s1, "on", [128, 512])
                ret = alloc(s1, "ret", [128, 512], BF16)
                retT = alloc(s1, "retT", [128, 4, 128], BF16)
                S0 = [alloc(s1, "S0_%d" % i, [128, 4, 128]) for i in range(2)]
                S0b = [alloc(s1, "S0b_%d" % i, [128, 4, 128], BF16) for i in range(2)]
                qxm = [alloc(s1, "qxm_%d" % i, [128, 4, 64], BF16) for i in range(2)]
                vzb = [alloc(s1, "vzb_%d" % i, [64, 512], BF16) for i in range(2)]
                Sn = [alloc(s1, "Sn_%d" % i, [128, 4, 128]) for i in range(2)]
                bS0 = [Buf("S0_%d" % i, S.GL[i]) for i in range(2)]
                bS0b = [Buf("S0b_%d" % i) for i in range(2)]
                bqxm = [Buf("qxm%d" % i) for i in range(2)]
                bvzb = [Buf("vzb%d" % i) for i in range(2)]
                bSn = [Buf("Sn%d" % i, S.GS[i]) for i in range(2)]
                (b_t1q, b_t2q, b_t1k, b_t2k, b_qr, b_kr, b_qT, b_qxT, b_kT, b_vb, b_vz, b_sg, b_sT, b_Sst, b_Sbf,
                 b_st, b_on, b_ret, b_retT) = [Buf("p1b%d" % i) for i in range(19)]
                b_Sst.grp = S.GS[2]
                V(lambda e: e.memset(Sst[:], 0.0), w=[b_Sst])
                GC_P = [float(g ** 128) for g in GAM]
                GC_S = [float(g ** 4) for g in GAM]

                import os as _os
                _tl = _os.environ.get("K_TILES")
                _tiles = [int(v) for v in _tl.split(",") if int(v) >= 0] if _tl else list(range(NT))
                _step = int(_os.environ.get("K_STEP", "99"))
                hT1s = [hT1, alloc(s1, "hT1c", [128, 8, 128], BF16)]
                bhT1s = [bhT1, Buf("hT1c")]

                def p1b_norm(n):
                    npt_ = TS if n == 16 else 128
                    rmsnorm_hT(x[:npt_, n, :], bx[n], npt_, gm2[:], hT1s[n % 2], bhT1s[n % 2], scrB, 0, None, bg=b_t1)
                def p1b_proj(n):
                    npt_ = TS if n == 16 else 128
                    hTn, bhTn = hT1s[n % 2], bhT1s[n % 2]
                    for c in range(4):
                        for kt in range(8):
                            T(lambda e: e.matmul(PS[c][:npt_, :], lhsT=hTn[:, kt, 0:npt_],
                                                 rhs=wq[:, kt, c * 512:(c + 1) * 512], start=(kt == 0), stop=(kt == 7)),
                              r=[bhTn, b_wqc[c]], w=[bPS[c]])
                if _tiles:
                    p1b_norm(_tiles[0])
                    p1b_proj(_tiles[0])
                    load_wout()
                for ti_, n in enumerate(_tiles):
                    is_s = (n == 16)
                    npt = TS if is_s else 128
                    tok0 = n * 128
                    hT1, bhT1 = hT1s[n % 2], bhT1s[n % 2]
                    pob = [4, 6, 7, 1] if is_s else [4, 4, 4, 4]

                    def po(h):
                        if is_s:
                            return PS[pob[h]][:npt, 0:128]
                        return PS[4][:npt, h * 128:(h + 1) * 128]
                    if _step <= 1:
                        continue
                    for (bank, t1_, t2_, out_, bt1, bt2, bo) in ((0, t1q, t2q, qr, b_t1q, b_t2q, b_qr),
                                                               (1, t1k, t2k, kr, b_t1k, b_t2k, b_kr)):
                        pv4 = PS[bank][:npt, :].rearrange("p (h a j) -> p h a j", h=4, a=2)
                        t1v = t1_[:npt, :].rearrange("p (h a j) -> p h a j", h=4, a=2)
                        t2v = t2_[:npt, :].rearrange("p (h a j) -> p h a j", h=4, a=2)
                        cosb = rope[:npt, 0, n, :].unsqueeze(1).unsqueeze(1).to_broadcast([npt, 4, 2, 64])
                        sinb = rope[:npt, 1, n, :].unsqueeze(1).to_broadcast([npt, 4, 64])
                        nsinb = rope[:npt, 2, n, :].unsqueeze(1).to_broadcast([npt, 4, 64])
                        V(lambda e: e.tensor_tensor(out=t1v, in0=pv4, in1=cosb, op=ALU.mult), r=[bPS[bank], b_t1], w=[bt1])
                        V(lambda e: e.tensor_tensor(out=t2v[:, :, 0, :], in0=pv4[:, :, 1, :], in1=nsinb, op=ALU.mult),
                          r=[bPS[bank], b_t1], w=[bt2])
                        V(lambda e: e.tensor_tensor(out=t2v[:, :, 1, :], in0=pv4[:, :, 0, :], in1=sinb, op=ALU.mult),
                          r=[bPS[bank], b_t1], w=[bt2])
                        V(lambda e: e.tensor_tensor(out=out_[:npt, :], in0=t1_[:npt, :], in1=t2_[:npt, :], op=ALU.add),
                           r=[bt1, bt2], w=[bo])
                    if _step <= 2:
                        continue
                    A(lambda e: e.copy(out=vb[:npt, :], in_=PS[2][:npt, :]), r=[bPS[2]], w=[b_vb])
                    if not is_s:
                        V(lambda e: e.tensor_tensor(
                            out=vz[:, :].rearrange("p (h e) -> p h e", h=4),
                            in0=PS[2][:, :].rearrange("p (h e) -> p h e", h=4),
                            in1=zetap[:, :].unsqueeze(2).to_broadcast([128, 4, 128]), op=ALU.mult),
                          r=[bPS[2], b_t1], w=[b_vz])
                    A(lambda e: e.activation(out=sg_[:npt, :], in_=PS[3][:npt, :], func=AF.Silu), r=[bPS[3]], w=[b_sg])
                    pv4b = ps_bf(4)
                    pv5b = ps_bf(5)
                    for h in range(4):
                        T(lambda e: e.transpose(out=pv4b[:, h * 128:h * 128 + npt], in_=qr[:npt, h * 128:(h + 1) * 128],
                                                identity=identb[:npt, :npt]), r=[b_qr, b_const], w=[bPS[4]])
                    for h in range(4):
                        T(lambda e: e.transpose(out=pv5b[:, h * 128:h * 128 + npt], in_=kr[:npt, h * 128:(h + 1) * 128],
                                                identity=identb[:npt, :npt]), r=[b_kr, b_const], w=[bPS[5]])
                    q4 = pv4b[:, 0:512].rearrange("p (h t) -> p h t", h=4)[:, :, 0:npt]
                    k4 = pv5b[:, 0:512].rearrange("p (h t) -> p h t", h=4)[:, :, 0:npt]
                    A(lambda e: e.copy(out=qT[:, :, 0:npt], in_=q4), r=[bPS[4]], w=[b_qT])
                    xiv = (xi[:, 0:512].rearrange("p (h t) -> p h t", h=4) if not is_s
                           else xi[:, 512:768].rearrange("p (h t) -> p h t", h=4))
                    V(lambda e: e.tensor_tensor(out=qxT[:, :, 0:npt], in0=q4, in1=xiv, op=ALU.mult),
                      r=[bPS[4], b_t1], w=[b_qxT])
                    A(lambda e: e.copy(out=kT[:, :, 0:npt], in_=k4), r=[bPS[5]], w=[b_kT])
                    if _step <= 3:
                        continue
                    for h in range(4):
                        T(lambda e: e.matmul(PS[6][:npt, h * 128:h * 128 + npt], lhsT=kT[:, h, 0:npt], rhs=qT[:, h, 0:npt],
                                             start=True, stop=True), r=[b_kT, b_qT], w=[bPS[6]])
                    dmv = (dmp[:, :].rearrange("p (h t) -> p h t", h=4) if not is_s
                           else dms[:, :].rearrange("p (h t) -> p h t", h=4))
                    V(lambda e: e.tensor_tensor(out=sT[:npt, :, 0:npt],
                                                in0=PS[6][:npt, :].rearrange("p (h t) -> p h t", h=4)[:, :, 0:npt],
                                                in1=dmv, op=ALU.mult), r=[bPS[6], b_t1], w=[b_sT])
                    if _step <= 4:
                        continue
                    if ti_ + 1 < len(_tiles):
                        p1b_norm(_tiles[ti_ + 1])
                    for h in range(4):
                        only = (n == 0)
                        T(lambda e: e.matmul(po(h), lhsT=sT[:npt, h, 0:npt],
                                             rhs=vb[:npt, h * 128:(h + 1) * 128], start=True, stop=only),
                          r=[b_sT, b_vb], w=[bPS[pob[h]]])
                        if (not is_s) and n > 0:
                            T(lambda e: e.matmul(po(h), lhsT=qxT[:, h, 0:npt],
                                                 rhs=Sbf[:, h, :], start=False, stop=True),
                              r=[b_qxT, b_Sbf], w=[bPS[4]])
                    if not is_s:
                        for h in range(4):
                            T(lambda e: e.matmul(PS[5][:, h * 128:(h + 1) * 128], lhsT=kr[:, h * 128:(h + 1) * 128],
                                                 rhs=vz[:, h * 128:(h + 1) * 128], start=True, stop=True),
                              r=[b_kr, b_vz], w=[bPS[5]])
                        for h in range(4):
                            V(lambda e: e.scalar_tensor_tensor(out=Sst[:, h, :], in0=Sst[:, h, :], scalar=GC_P[h],
                                                               op0=ALU.mult, in1=PS[5][:, h * 128:(h + 1) * 128],
                                                               op1=ALU.add), r=[b_Sst, bPS[5]], w=[b_Sst])
                        A(lambda e: e.copy(out=Sbf[:], in_=Sst[:]), r=[b_Sst], w=[b_Sbf])
                        if n == NTP - 1:
                            S.dma("sp", O["o_ret_p"].rearrange("h d e -> d h e"), Sst[:], reads=[b_Sst])
                    else:
                        S.dma("sp", S0[0][:], I["sret"][0].rearrange("h d e -> d h e"), writes=[bS0[0]])
                        for b in range(16):
                            sl = b % 2
                            if b + 1 < 16:
                                S.dma("sp", S0[1 - sl][:], I["sret"][b + 1].rearrange("h d e -> d h e"), writes=[bS0[1 - sl]])
                            A(lambda e: e.copy(out=S0b[sl][:], in_=S0[sl][:]), r=[bS0[sl]], w=[bS0b[sl]])
                            V(lambda e: e.tensor_tensor(
                                out=qxm[sl][:], in0=qxT[:, :, 0:64],
                                in1=cmask[:, b * 64:(b + 1) * 64].unsqueeze(1).to_broadcast([128, 4, 64]), op=ALU.mult),
                              r=[b_qxT, b_t1], w=[bqxm[sl]])
                            for h in range(4):
                                T(lambda e: e.matmul(po(h), lhsT=qxm[sl][:, h, :],
                                                     rhs=S0b[sl][:, h, :], start=False, stop=(b == 15)),
                                  r=[bqxm[sl], bS0b[sl]], w=[bPS[pob[h]]])
                            V(lambda e: e.tensor_tensor(
                                out=vzb[sl][:, :].rearrange("p (h e) -> p h e", h=4),
                                in0=PS[2][:64, :].rearrange("p (h e) -> p h e", h=4),
                                in1=zs[:, b * 4:(b + 1) * 4].unsqueeze(2).to_broadcast([64, 4, 128]), op=ALU.mult),
                              r=[bPS[2], b_t1], w=[bvzb[sl]])
                            kvb = 5 if sl == 0 else 0
                            for h in range(4):
                                T(lambda e: e.matmul(PS[kvb][:, h * 128:(h + 1) * 128], lhsT=kr[:64, h * 128:(h + 1) * 128],
                                                     rhs=vzb[sl][:, h * 128:(h + 1) * 128], start=True, stop=True),
                                  r=[b_kr, bvzb[sl]], w=[bPS[kvb]])
                            for h in range(4):
                                V(lambda e: e.scalar_tensor_tensor(out=Sn[sl][:, h, :], in0=S0[sl][:, h, :], scalar=GC_S[h],
                                                                   op0=ALU.mult, in1=PS[kvb][:, h * 128:(h + 1) * 128],
                                                                   op1=ALU.add), r=[bS0[sl], bPS[kvb]], w=[bSn[sl]])
                            S.dma("sp", O["o_ret_s"][b].rearrange("h d e -> d h e"), Sn[sl][:], reads=[bSn[sl]])
                    if _step <= 5:
                        continue
                    if ti_ + 1 < len(_tiles):
                        p1b_proj(_tiles[ti_ + 1])
                    for h in range(4):
                        V(lambda e: e.bn_stats(out=stats[:npt, h, :], in_=po(h)),
                          r=[bPS[pob[h]]], w=[b_st])
                    for h in range(4):
                        V(lambda e: e.bn_aggr(out=mv[:npt, h, :], in_=stats[:npt, h, :]), r=[b_st], w=[b_st])
                    A(lambda e: e.activation(out=rs4[:npt, :], in_=mv[:npt, :, 1], func=AF.Sqrt, scale=1.0,
                                             bias=epsc[:npt, :]), r=[b_st, b_const], w=[b_st])
                    V(lambda e: e.reciprocal(out=rs4[:npt, :], in_=rs4[:npt, :]), r=[b_st], w=[b_st])
                    V(lambda e: e.scalar_tensor_tensor(out=nb4[:npt, :], in0=mv[:npt, :, 0], scalar=-1.0, op0=ALU.mult,
                                                       in1=rs4[:npt, :], op1=ALU.mult), r=[b_st], w=[b_st])
                    for h in range(4):
                        A(lambda e: e.activation(out=on[:npt, h * 128:(h + 1) * 128], in_=po(h),
                                                 func=AF.Identity, scale=rs4[:npt, h:h + 1], bias=nb4[:npt, h:h + 1]),
                          r=[bPS[pob[h]], b_st], w=[b_on])
                    V(lambda e: e.tensor_tensor(out=ret[:npt, :], in0=on[:npt, :], in1=sg_[:npt, :], op=ALU.mult),
                       r=[b_on, b_sg], w=[b_ret])
                    if _step <= 6:
                        continue
                    pv6b = ps_bf(6)
                    for h in range(4):
                        T(lambda e: e.transpose(out=pv6b[:, h * 128:h * 128 + npt], in_=ret[:npt, h * 128:(h + 1) * 128],
                                                identity=identb[:npt, :npt]), r=[b_ret, b_const], w=[bPS[6]])
                    A(lambda e: e.copy(out=retT[:, :, 0:npt],
                                       in_=pv6b[:, 0:512].rearrange("p (h t) -> p h t", h=4)[:, :, 0:npt]),
                      r=[bPS[6]], w=[b_retT])
                    if _step <= 7:
                        continue
                    bi_ = min(n // 4, 4)
                    for half in range(2):
                        bank = 6 + half
                        for kt in range(8):
                            lh = ssmT[:, kt, tok0:tok0 + npt] if kt < 4 else retT[:, kt - 4, 0:npt]
                            T(lambda e: e.matmul(PS[bank][:npt, :], lhsT=lh, rhs=wout[:, kt, half * 512:(half + 1) * 512],
                                                 start=(kt == 0), stop=(kt == 7)),
                              r=[b_ssmT[bi_], b_retT, b_wout], w=[bPS[bank]])
                        resid_add(n, npt, half, bank)
                S.barrier()
            if dbg:
                for n in range(NT):
                    S.dma("sp", O["dbg_x"][:, n, :], x[:, n, :], reads=[bx[n]])
            if stage <= 2:
                S.barrier()
                S.run_block()
                nck.__exit__(None, None, None)
                return nc

            with ExitStack() as s2:
                gx = alloc(s2, "gx", [128, 8])
                gmem = alloc(s2, "gmem", [128, 8])
                ones = alloc(s2, "ones", [128, 128], BF16)
                b_t2 = Buf("tab2")
                S.dma("sp", gx[:], I["g_xattn"].rearrange("(k p) -> p k", p=128), writes=[b_t2])
                S.dma("sp", gmem[:], I["g_mem"].rearrange("(k p) -> p k", p=128), writes=[b_t2])
                V(lambda e: e.memset(ones[:], 1.0), w=[b_t2])
                KT = alloc(s2, "KT", [128, 8, MEM], BF16)
                Vm = alloc(s2, "Vm", [128, 2, D], BF16)
                b_KT, b_Vm = Buf("KT"), Buf("Vm")
                wmq = alloc(s2, "wmq", [128, 8, D], BF16)
                b_wmq, b_wmo = Buf("wmq", S.GW[2]), Buf("wmo", S.GW[3])
                with ExitStack() as s2a:
                    wmk = alloc(s2a, "wmk", [128, 8, D], BF16)
                    wmv = alloc(s2a, "wmv", [128, 8, D], BF16)
                    b_wmk, b_wmv = Buf("wmk", S.GW[0]), Buf("wmv", S.GW[1])
                    load_w_bf16(wmk, b_wmk, I["w_mk"], 8, D, 0)
                    load_w_bf16(wmv, b_wmv, I["w_mv"], 8, D, 0)
                    load_w_bf16(wmq, b_wmq, I["w_mq"], 8, D, 0)
                    mx = [alloc(s2a, "mx%d" % i, [128, D]) for i in range(2)]
                    bmx = [Buf("mx%d" % i, S.GL[i]) for i in range(2)]
                    mhT = alloc(s2a, "mhT", [128, 8, MEM], BF16)
                    b_mhT = Buf("mhT")
                    mo = [alloc(s2a, "mo%d" % i, [128, D]) for i in range(2)]
                    bmo = [Buf("mo%d" % i, S.GS[i]) for i in range(2)]
                    _k2a = int(_os.environ.get("K2A", "9"))
                    for mt in range(2):
                        S.dma("sp", mx[mt][:], I["memp"][mt * 128:(mt + 1) * 128, :], writes=[bmx[mt]])
                        if _k2a >= 1:
                            rmsnorm_hT(mx[mt][:, :], bmx[mt], 128, gmem[:], mhT, b_mhT, scrB, mt * 128, None,
                                       ln=True, bg=b_t2)
                    oi = 0
                    for (wm, bwm, oname, isv) in ((wmk, b_wmk, "o_mk", False), (wmv, b_wmv, "o_mv", True)) if _k2a >= 2 else ():
                        for mt in range(2):
                            sl = oi % 2
                            oi += 1
                            for half in range(2):
                                bank = half
                                for kt in range(8):
                                    T(lambda e: e.matmul(PS[bank][:, :], lhsT=mhT[:, kt, mt * 128:(mt + 1) * 128],
                                                         rhs=wm[:, kt, half * 512:(half + 1) * 512], start=(kt == 0),
                                                         stop=(kt == 7)), r=[b_mhT, bwm], w=[bPS[bank]])
                                A(lambda e: e.copy(out=mo[sl][:, half * 512:(half + 1) * 512], in_=PS[bank][:, :]),
                                  r=[bPS[bank]], w=[bmo[sl]])
                                if isv:
                                    V(lambda e: e.tensor_copy(out=Vm[:, mt, half * 512:(half + 1) * 512], in_=PS[bank][:, :]),
                                      r=[bPS[bank]], w=[b_Vm])
                            S.dma("sp", O[oname][mt * 128:(mt + 1) * 128, :], mo[sl][:], reads=[bmo[sl]])
                    for j in range(8 if _k2a >= 3 else 0):
                        bank = 2 + (j % 2)
                        for kt in range(8):
                            T(lambda e: e.matmul(PS[bank][:, 0:MEM], lhsT=wmk[:, kt, j * 128:(j + 1) * 128],
                                                 rhs=mhT[:, kt, :], start=(kt == 0), stop=(kt == 7)),
                              r=[b_mhT, b_wmk], w=[bPS[bank]])
                        A(lambda e: e.copy(out=KT[:, j, :], in_=PS[bank][:, 0:MEM]), r=[bPS[bank]], w=[b_KT])
                    S.barrier()
                wmo = alloc(s2, "wmo", [128, 8, D], BF16)
                load_w_bf16(wmo, b_wmo, I["w_mo"], 8, D, 0)
                hT4s = [alloc(s2, "hT4_%d" % i, [128, 8, 512], BF16) for i in range(2)]
                b_hT4s = [Buf("hT4_%d" % i) for i in range(2)]
                qm4 = alloc(s2, "qm4", [128, 8, 512], BF16)
                oT4 = alloc(s2, "oT4", [128, 8, 512], BF16)
                eT4 = [alloc(s2, "eT4_%d" % i, [128, 2, 512], BF16) for i in range(2)]
                rdn4 = [alloc(s2, "rdn4_%d" % i, [128, 512]) for i in range(2)]
                b_qm4, b_oT4 = Buf("qm4"), Buf("oT4")
                b_eT4 = [Buf("eT4_%d" % i) for i in range(2)]
                b_rdn4 = [Buf("rdn4_%d" % i) for i in range(2)]
                Kb = [alloc(s2, "Kb%d" % i, [128, 2, D]) for i in range(2)]
                bKb = [Buf("Kb%d" % i, S.GL[i]) for i in range(2)]
                KbT = [alloc(s2, "KbT%d" % i, [128, 8, MEM], BF16) for i in range(2)]
                bKbT = [Buf("KbT%d" % i) for i in range(2)]
                Vb = [alloc(s2, "Vb%d" % i, [128, 2, D], BF16) for i in range(2)]
                bVb = [Buf("Vb%d" % i, S.GW[i]) for i in range(2)]
                eTs = alloc(s2, "eTs", [128, 2, 4, 64], BF16)
                b_eTs = Buf("eTs")
                qrot = [0]

                def q_proj(nc_, hT4, b_hT4):
                    for j in range(8):
                        bank = 5 + (qrot[0] % 3)
                        qrot[0] += 1
                        for kt in range(8):
                            T(lambda e: e.matmul(PS[bank][:, 0:nc_], lhsT=wmq[:, kt, j * 128:(j + 1) * 128],
                                                 rhs=hT4[:, kt, 0:nc_], start=(kt == 0), stop=(kt == 7)),
                              r=[b_wmq, b_hT4], w=[bPS[bank]])
                        A(lambda e: e.activation(out=qm4[:, j, 0:nc_], in_=PS[bank][:, 0:nc_], func=AF.Copy,
                                                 scale=1.0 / 16.0), r=[bPS[bank]], w=[b_qm4])

                def w_mo_resid(n, npt, c0):
                    for half in range(2):
                        bank = 5 + (qrot[0] % 3)
                        qrot[0] += 1
                        for j in range(8):
                            T(lambda e: e.matmul(PS[bank][:npt, :], lhsT=oT4[:, j, c0:c0 + npt],
                                                 rhs=wmo[:, j, half * 512:(half + 1) * 512], start=(j == 0), stop=(j == 7)),
                              r=[b_oT4, b_wmo], w=[bPS[bank]])
                        resid_add(n, npt, half, bank)

                def p2_norms(bi):
                    hT4, b_hT4 = hT4s[bi % 2], b_hT4s[bi % 2]
                    if bi < 4:
                        for ti in range(4):
                            n = bi * 4 + ti
                            rmsnorm_hT(x[:, n, :], bx[n], 128, gx[:], hT4, b_hT4, scrB, ti * 128, None, ln=True, bg=b_t2)
                    else:
                        rmsnorm_hT(x[:TS, 16, :], bx[16], TS, gx[:], hT4, b_hT4, scrB, 0, None, ln=True, bg=b_t2)

                p2_norms(0)
                for bi in range(4):
                    q_proj(512, hT4s[bi % 2], b_hT4s[bi % 2])
                    for h in range(4):
                        par = h % 2
                        for mt in range(2):
                            bank = mt
                            for dt_ in range(2):
                                T(lambda e: e.matmul(PS[bank][:, :], lhsT=KT[:, h * 2 + dt_, mt * 128:(mt + 1) * 128],
                                                     rhs=qm4[:, h * 2 + dt_, :], start=(dt_ == 0), stop=(dt_ == 1)),
                                  r=[b_KT, b_qm4], w=[bPS[bank]])
                            A(lambda e: e.activation(out=eT4[par][:, mt, :], in_=PS[bank][:, :], func=AF.Exp),
                              r=[bPS[bank]], w=[b_eT4[par]])
                        for mt in range(2):
                            T(lambda e: e.matmul(PS[2][:, :], lhsT=ones[:, :], rhs=eT4[par][:, mt, :], start=(mt == 0),
                                                 stop=(mt == 1)), r=[b_t2, b_eT4[par]], w=[bPS[2]])
                        A(lambda e: e.activation(out=rdn4[par][:, :], in_=PS[2][:, :], func=AF.Ln), r=[bPS[2]], w=[b_rdn4[par]])
                        A(lambda e: e.activation(out=rdn4[par][:, :], in_=rdn4[par][:, :], func=AF.Exp, scale=-1.0),
                          r=[b_rdn4[par]], w=[b_rdn4[par]])
                        for dt_ in range(2):
                            bank = 3 + dt_
                            j = h * 2 + dt_
                            for mt in range(2):
                                T(lambda e: e.matmul(PS[bank][:, :], lhsT=Vm[:, mt, j * 128:(j + 1) * 128],
                                                     rhs=eT4[par][:, mt, :], start=(mt == 0), stop=(mt == 1)),
                                  r=[b_Vm, b_eT4[par]], w=[bPS[bank]])
                            V(lambda e: e.tensor_tensor(out=oT4[:, j, :], in0=PS[bank][:, :], in1=rdn4[par][:, :], op=ALU.mult),
                              r=[bPS[bank], b_rdn4[par]], w=[b_oT4])
                    p2_norms(bi + 1)
                    for ti in range(4):
                        w_mo_resid(bi * 4 + ti, 128, ti * 128)
                n = 16
                q_proj(TS, hT4s[0], b_hT4s[0])
                rden_s = rdn4[0][:, 0:256].rearrange("p (h t) -> p h t", h=4)
                for b in range(16):
                    sl = b % 2
                    S.dma("sp", Kb[sl][:], I["ck"][b].rearrange("(mt p) d -> p mt d", p=128), writes=[bKb[sl]])
                    for q4 in range(4):
                        bank = 2 + (q4 % 2)
                        for i4 in range(4):
                            idx = q4 * 4 + i4
                            j, mt = idx // 2, idx % 2
                            T(lambda e: e.transpose(out=PS[bank][:, i4 * 128:(i4 + 1) * 128],
                                                    in_=Kb[sl][:, mt, j * 128:(j + 1) * 128], identity=identf[:]),
                              r=[bKb[sl], b_const], w=[bPS[bank]])
                        A(lambda e: e.copy(
                            out=KbT[sl][:, 2 * q4:2 * q4 + 2, :].rearrange("p j (m t) -> p j m t", m=2),
                            in_=PS[bank][:, :].rearrange("p (j m t) -> p j m t", j=2, m=2)),
                          r=[bPS[bank]], w=[bKbT[sl]])
                    for h in range(4):
                        for mt in range(2):
                            c0 = mt * 256 + h * 64 + 4 * b
                            for dt_ in range(2):
                                T(lambda e: e.matmul(PS[4][:, c0:c0 + 4],
                                                     lhsT=KbT[sl][:, h * 2 + dt_, mt * 128:(mt + 1) * 128],
                                                     rhs=qm4[:, h * 2 + dt_, 4 * b:4 * b + 4], start=(dt_ == 0),
                                                     stop=(dt_ == 1)), r=[bKbT[sl], b_qm4], w=[bPS[4]])
                A(lambda e: e.activation(out=eTs[:].rearrange("p m h t -> p (m h t)"), in_=PS[4][:, :], func=AF.Exp),
                  r=[bPS[4]], w=[b_eTs])
                for h in range(4):
                    for mt in range(2):
                        T(lambda e: e.matmul(PS[0][:, h * 64:(h + 1) * 64], lhsT=ones[:, :], rhs=eTs[:, mt, h, :],
                                             start=(mt == 0), stop=(mt == 1)), r=[b_t2, b_eTs], w=[bPS[0]])
                V(lambda e: e.reciprocal(out=rden_s, in_=PS[0][:, 0:256].rearrange("p (h t) -> p h t", h=4)),
                  r=[bPS[0]], w=[b_rdn4[0]])
                for b in range(16):
                    sl = b % 2
                    for mt in range(2):
                        S.dma("pool", Vb[sl][:, mt, :], I["cv"][b, mt * 128:(mt + 1) * 128, :], writes=[bVb[sl]])
                    for j in range(8):
                        h = j // 2
                        for mt in range(2):
                            T(lambda e: e.matmul(PS[1][:, j * 64 + 4 * b:j * 64 + 4 * b + 4],
                                                 lhsT=Vb[sl][:, mt, j * 128:(j + 1) * 128],
                                                 rhs=eTs[:, mt, h, 4 * b:4 * b + 4], start=(mt == 0), stop=(mt == 1)),
                              r=[bVb[sl], b_eTs], w=[bPS[1]])
                V(lambda e: e.tensor_tensor(
                    out=oT4[:, :, 0:64].rearrange("p (h a) t -> p h a t", a=2),
                    in0=PS[1][:, :].rearrange("p (h a t) -> p h a t", h=4, a=2),
                    in1=rden_s.unsqueeze(2).to_broadcast([128, 4, 2, 64]), op=ALU.mult),
                  r=[bPS[1], b_rdn4[0]], w=[b_oT4])
                w_mo_resid(16, TS, 0)
                S.barrier()
            if stage <= 3:
                if dbg:
                    for n in range(NT):
                        S.dma("sp", O["dbg_x"][:, n, :], x[:, n, :], reads=[bx[n]])
                S.barrier()
                S.run_block()
                nck.__exit__(None, None, None)
                return nc

            with ExitStack() as s3:
                gml = alloc(s3, "gml", [128, 8])
                b_t3 = Buf("tab3")
                S.dma("sp", gml[:], I["g_mlp"].rearrange("(k p) -> p k", p=128), writes=[b_t3])
                hTa = alloc(s3, "hTa", [128, 8, NTOK], BF16)
                b_hTa = [Buf("hTa%d" % n) for n in range(NT)]
                wup = [alloc(s3, "wup%d" % i, [128, 8, 512], BF16) for i in range(2)]
                wdn = [alloc(s3, "wdn%d" % i, [128, 4, D], BF16) for i in range(2)]
                bwup = [Buf("wup%d" % i, S.GW[i]) for i in range(2)]
                bwdn = [Buf("wdn%d" % i, S.GW[2 + i]) for i in range(2)]
                rl = [alloc(s3, "rl%d" % i, [128, 512]) for i in range(2)]
                brl = [Buf("rl%d" % i) for i in range(2)]
                aT = [alloc(s3, "aT%d" % i, [128, 4, 512], BF16) for i in range(2)]
                baT = [Buf("aT%d" % i) for i in range(2)]

                def load_fc(fc):
                    sl = fc % 2
                    for kt in range(8):
                        S.dma("pool", wup[sl][:, kt, :], I["w_up"][kt * 128:(kt + 1) * 128, fc * 512:(fc + 1) * 512],
                              writes=[bwup[sl]])
                    for ft in range(4):
                        S.dma("pool", wdn[sl][:, ft, :], I["w_down"][fc * 512 + ft * 128:fc * 512 + (ft + 1) * 128, :],
                              writes=[bwdn[sl]])
                load_fc(0)
                scrB["pb"] = [7, 6]
                for n in range(NT):
                    npt = TS if n == 16 else 128
                    rmsnorm_hT(x[:npt, n, :], bx[n], npt, gml[:], hTa, b_hTa[n], scrB, n * 128, None, ln=True, bg=b_t3)
                gf = alloc(s3, "gf", [128, D])
                b_gf = Buf("gf")
                S.dma("sp", gf[:], I["g_final"].rearrange("(o d) -> o d", o=1).partition_broadcast(128), writes=[b_gf])
                yst = [alloc(s3, "yst%d" % i, [128, D]) for i in range(3)]
                byst = [Buf("yst%d" % i, S.GS[i]) for i in range(3)]

                def final_norm(n):
                    npt = TS if n == 16 else 128
                    sl = n % 3
                    k4 = n % 2
                    sq, ss, rstd, bscr = scrB["sq"][k4], scrB["ss"][k4], scrB["rstd"][k4], scrB["ba"][k4]
                    A(lambda e: e.activation(out=sq[:npt, :], in_=x[:npt, n, :], func=AF.Square, accum_out=ss[:npt, :]),
                      r=[bx[n]], w=[bscr])
                    A(lambda e: e.activation(out=rstd[:npt, :], in_=ss[:npt, :], func=AF.Ln, scale=1.0 / D,
                                             bias=epsc[:npt, :]), r=[bscr, b_const], w=[bscr])
                    A(lambda e: e.activation(out=rstd[:npt, :], in_=rstd[:npt, :], func=AF.Exp, scale=-0.5),
                      r=[bscr], w=[bscr])
                    V(lambda e: e.scalar_tensor_tensor(out=yst[sl][:npt, :], in0=x[:npt, n, :], scalar=rstd[:npt, :],
                                                       op0=ALU.mult, in1=gf[:npt, :], op1=ALU.mult),
                      r=[bx[n], bscr, b_gf], w=[byst[sl]])
                    if n < 16:
                        S.dma("sp", O["yp"][n * 128:(n + 1) * 128, :], yst[sl][:, :], reads=[byst[sl]])
                    else:
                        S.dma("sp", O["ys"][:, :], yst[sl][:TS, :], reads=[byst[sl]])

                blocks3 = [(i * 512, 512) for i in range(4)] + [(SEQ, TS)]
                items = [(fc, blk) for fc in range(8) for blk in blocks3]
                ctr = {"ri": 0, "di": 0}
                load_fc(1)

                def mlp_up(i):
                    fc, (t0, nn) = items[i]
                    sl, asl = fc % 2, i % 2
                    tiles = list(range(t0 // 128, t0 // 128 + (nn + 127) // 128))
                    for ft in range(4):
                        bank = ft
                        for kt in range(8):
                            T(lambda e: e.matmul(PS[bank][:, 0:nn], lhsT=wup[sl][:, kt, ft * 128:(ft + 1) * 128],
                                                 rhs=hTa[:, kt, t0:t0 + nn], start=(kt == 0), stop=(kt == 7)),
                              r=[bwup[sl]] + [b_hTa[t] for t in tiles], w=[bPS[bank]])
                        rsl = ctr["ri"] % 2
                        ctr["ri"] += 1
                        A(lambda e: e.activation(out=rl[rsl][:, 0:nn], in_=PS[bank][:, 0:nn], func=AF.Relu),
                          r=[bPS[bank]], w=[brl[rsl]])
                        V(lambda e: e.tensor_tensor(out=aT[asl][:, ft, 0:nn], in0=rl[rsl][:, 0:nn], in1=rl[rsl][:, 0:nn],
                                                    op=ALU.mult), r=[brl[rsl]], w=[baT[asl]])

                def mlp_down(i):
                    fc, (t0, nn) = items[i]
                    sl, asl = fc % 2, i % 2
                    tiles = list(range(t0 // 128, t0 // 128 + (nn + 127) // 128))
                    for ti, tl in enumerate(tiles):
                        npt = TS if tl == 16 else 128
                        for half in range(2):
                            bank = 4 + (ctr["di"] % 4)
                            ctr["di"] += 1
                            for ft in range(4):
                                T(lambda e: e.matmul(PS[bank][:npt, :], lhsT=aT[asl][:, ft, ti * 128:ti * 128 + npt],
                                                     rhs=wdn[sl][:, ft, half * 512:(half + 1) * 512], start=(ft == 0),
                                                     stop=(ft == 3)), r=[baT[asl], bwdn[sl]], w=[bPS[bank]])
                            resid_add(tl, npt, half, bank)
                        if fc == 7:
                            final_norm(tl)

                mlp_up(0)
                for i in range(len(items)):
                    if i + 1 < len(items):
                        mlp_up(i + 1)
                    mlp_down(i)
                    fc = items[i][0]
                    if (i + 1 == len(items) or items[i + 1][0] != fc) and fc + 2 < 8:
                        load_fc(fc + 2)
                S.barrier()
            if dbg:
                for n in range(NT):
                    S.dma("sp", O["dbg_x"][:, n, :], x[:, n, :], reads=[bx[n]])
            S.barrier()
            S.run_block()
            nck.__exit__(None, None, None)
    return nc


_NC = None


def kernel(**inputs):
    global _NC
    if _NC is None:
        _NC = build()
    maps = _in_maps(inputs)
    res = run_bass_kernel_spmd(_NC, maps, core_ids=list(range(8)))
    R = res.results
    f = np.float32

    def cat(name, shape=None):
        return np.stack([np.asarray(R[c][name], f) for c in range(8)])
    y_prompt = cat("yp")
    y_sample = cat("ys").reshape(128, 4, D)
    s5r_p = cat("o_s5r_p")[None]
    s5i_p = cat("o_s5i_p")[None]
    ret_p = cat("o_ret_p")[None]
    mk_p = cat("o_mk").reshape(8, MEM, 4, 256)[None]
    mv_p = cat("o_mv").reshape(8, MEM, 4, 256)[None]
    s5r_s = cat("o_s5r_s").reshape(128, G, 64)[None]
    s5i_s = cat("o_s5i_s").reshape(128, G, 64)[None]
    ret_s = cat("o_ret_s").reshape(128, 4, 128, 128)[None]
    return (y_prompt, y_sample, s5r_p, s5i_p, ret_p, mk_p, mv_p, s5r_s, s5i_s, ret_s)


def _in_maps(inputs):
    cst = _consts()
    f = np.float32
    maps = []
    w = {}
    for k in W_NAMES:
        a = np.asarray(inputs[k], f)
        if k != "g_final":
            a = a[0]
        w[k] = np.ascontiguousarray(a.reshape(W_SHAPES[k]))
    for c in range(8):
        m = dict(w)
        m.update(cst)
        b0 = 16 * c
        m["xp"] = np.ascontiguousarray(np.asarray(inputs["x_prompt"], f)[c])
        m["xs"] = np.ascontiguousarray(np.asarray(inputs["x_sample"], f)[b0:b0 + 16].reshape(TS, D))
        m["memp"] = np.ascontiguousarray(np.asarray(inputs["mem_prompt"], f)[c])
        m["s5r"] = np.ascontiguousarray(np.asarray(inputs["state_s5_re"], f)[0, b0:b0 + 16].reshape(512, 64))
        m["s5i"] = np.ascontiguousarray(np.asarray(inputs["state_s5_im"], f)[0, b0:b0 + 16].reshape(512, 64))
        m["sret"] = np.ascontiguousarray(np.asarray(inputs["state_ret"], f)[0, b0:b0 + 16])
        m["ck"] = np.ascontiguousarray(np.asarray(inputs["cache_mem_k"], f)[0, b0:b0 + 16].reshape(16, MEM, D))
        m["cv"] = np.ascontiguousarray(np.asarray(inputs["cache_mem_v"], f)[0, b0:b0 + 16].reshape(16, MEM, D))
        maps.append(m)
    return maps
```

```python
import numpy as np
import concourse.bass as bass
import concourse.mybir as mybir
from concourse.bass_utils import run_bass_kernel_spmd
from contextlib import ExitStack

F32 = mybir.dt.float32
BF16 = mybir.dt.bfloat16
AF = mybir.ActivationFunctionType
ALU = mybir.AluOpType

D = 1024
SEQ = 2048
NTP = 16
TS = 64
NT = 17
NTOK = SEQ + TS
G = 32
DFF = 4096
MEM = 256
EPS = 1e-6
PAST = 16384.0
MAGIC = 12582912.0
TWO_PI = float(2.0 * np.pi)
ML = [7, 6, 5, 4, 3, 2, 1, 0, 1, 2, 3, 4, 5, 6, 7, 8, -4, 0.5]
K1 = len(ML)
I_A1, I_A8, I_A4, I_AM4, I_HALF = 8, 15, 3, 16, 17
GAM = [1.0 - 2.0 ** (-5.0 - h) for h in range(4)]


class Grp:
    __slots__ = ("sem", "cnt", "sealed")


class Buf:
    __slots__ = ("w", "r", "name", "grp", "ps")

    def __init__(self, name="", grp=None, ps=False):
        self.w = None
        self.r = []
        self.name = name
        self.grp = grp
        self.ps = ps


class _Rec:
    def __init__(self):
        self.call = None

    def __getattr__(self, name):
        def f(*a, **kw):
            self.call = (name, a, kw)
            return self
        return f


class Sched:
    ENG = ("pe", "dve", "act", "pool", "sp")

    def __init__(self, nc, stack, self_sync=("dve", "act", "pool")):
        self.nc = nc
        self.stack = stack
        self.prog = {k: [] for k in self.ENG}
        self.cnt = {k: 0 for k in self.ENG}
        self.waited = {k: {} for k in self.ENG}
        self.sem = {}
        self.nsem = 0
        for k in ("pe", "dve", "act", "pool"):
            self.sem[k] = self.new_sem("c_" + k)
        self.self_sync = set(self_sync)
        self.groups = []
        self.GC = self.group("gc")
        self.GP = self.group("gp")
        self.GW = [self.group("gw%d" % i) for i in range(4)]
        self.GX = self.group("gx")
        self.GL = [self.group("gl%d" % i) for i in range(2)]
        self.GS = [self.group("gs%d" % i) for i in range(3)]

    def group(self, name):
        g = Grp()
        g.sem = self.new_sem(name)
        g.cnt = 0
        g.sealed = False
        self.groups.append(g)
        return g

    def new_sem(self, name):
        self.nsem += 1
        assert self.nsem < 98, "too many semaphores"
        return self.stack.enter_context(self.nc.semaphore(name + "_%d" % self.nsem))

    def _waits(self, eng, deps):
        w = self.waited[eng]
        need = {}
        dd = []
        for d in deps:
            if isinstance(d, Grp):
                d.sealed = True
                dd.append((d.sem, d.cnt))
            else:
                dd.append(d)
        deps = dd
        for (s, v) in deps:
            if eng in self.sem and s is self.sem[eng] and eng not in self.self_sync:
                continue
            k = id(s)
            if w.get(k, 0) >= v:
                continue
            if k not in need or need[k][1] < v:
                need[k] = (s, v)
        for k, (s, v) in need.items():
            w[k] = v
            self.prog[eng].append(lambda e, s=s, v=v: e.wait_ge(s, v))

    def op(self, eng, fn, reads=(), writes=()):
        deps = []
        for b in reads:
            if b.w is not None:
                deps.append(b.w)
            if b.ps:
                mys = self.sem[eng]
                deps.extend(d for d in b.r if not (isinstance(d, tuple) and d[0] is mys))
        for b in writes:
            if b.w is not None:
                deps.append(b.w)
            deps.extend(b.r)
        self._waits(eng, deps)
        self.cnt[eng] += 1
        c = self.cnt[eng]
        s = self.sem[eng]
        rec = _Rec()
        fn(rec)
        name, a, kw = rec.call
        self.prog[eng].append(lambda e, name=name, a=a, kw=kw, s=s: getattr(e, name)(*a, **kw).then_inc(s, 1))
        for b in reads:
            b.r.append((s, c))
        for b in writes:
            b.w = (s, c)
            b.r = []

    def dma(self, q, out, in_, reads=(), writes=(), **kw):
        tb = writes[0] if writes else reads[0]
        g = tb.grp
        if g is None:
            g = self.GP if q == "pool" else (self.GC if writes else self.GS[0])
        deps = []
        for b in reads:
            if b.w is not None:
                deps.append(b.w)
        for b in writes:
            if b.w is not None and b.w is not g:
                deps.append(b.w)
            deps.extend(b.r)
        self._waits(q, deps)
        if g.sealed and g.cnt > 0:
            self._waits(q, [(g.sem, g.cnt)])
        g.sealed = False
        g.cnt += 16
        s = g.sem
        self.prog[q].append(
            lambda e, out=out, in_=in_, s=s, kw=kw: e.dma_start(out=out, in_=in_, **kw).then_inc(s, 16))
        for b in reads:
            b.r.append(g)
        for b in writes:
            b.w = g
            b.r = []

    def barrier(self, engines=None):
        deps = [(self.sem[k], self.cnt[k]) for k in ("pe", "dve", "act", "pool") if self.cnt[k] > 0]
        deps += [g for g in self.groups if g.cnt > 0]
        for e in (engines or self.ENG):
            self._waits(e, deps)

    def run_block(self):
        nc = self.nc
        with nc.Block() as block:
            @block.sync
            def _(e):
                for t in self.prog["sp"]:
                    t(e)

            @block.tensor
            def _(e):
                for t in self.prog["pe"]:
                    t(e)

            @block.vector
            def _(e):
                for t in self.prog["dve"]:
                    t(e)

            @block.scalar
            def _(e):
                for t in self.prog["act"]:
                    t(e)

            @block.gpsimd
            def _(e):
                for t in self.prog["pool"]:
                    t(e)


_CONSTS = None


def _consts():
    global _CONSTS
    if _CONSTS is not None:
        return _CONSTS
    f = np.float32
    c = {}
    c["c_ident"] = np.eye(128, dtype=f)
    m = np.zeros((8, 128, 240), f)
    for a in range(8):
        for i in range(16):
            m[a, 16 * a + i, 112 + i] = 1.0
    c["c_masters"] = m
    ml = np.array(ML, np.float64)
    rows = np.concatenate([ml / (2 * np.pi), ml, 8.0 * (np.arange(64) + 1) / (2 * np.pi)])
    c["c_rows"] = rows.astype(f)[None, :]
    sg = np.zeros((128, 2), f)
    sg[:64, 0] = 1.0
    sg[64:, 0] = -1.0
    sg[:64, 1] = -1.0
    sg[64:, 1] = 1.0
    c["c_sgn"] = sg
    inv = (f(10000.0) ** (-(np.arange(64, dtype=f) / f(64.0)))).astype(f)
    pos = np.zeros((128, NT), f)
    for n in range(NTP):
        pos[:, n] = 128 * n + np.arange(128)
    pos[:64, 16] = PAST + (np.arange(64) % 4)
    ang = (pos[:, :, None] * inv[None, None, :]).astype(f).astype(np.float64)
    c["c_rope"] = np.stack([np.cos(ang), np.sin(ang), -np.sin(ang)]).astype(f)
    lg = np.log(np.array(GAM, np.float64))
    sc = 128.0 ** -0.5
    idx = np.arange(128)
    dm = np.zeros((128, 4, 128), np.float64)
    diff = idx[None, :] - idx[:, None]
    for h in range(4):
        dm[:, h, :] = np.where(diff >= 0, np.exp(np.maximum(diff, 0) * lg[h]), 0.0) * sc
    c["c_dmask_p"] = dm.reshape(128, 512).astype(f)
    ds_ = np.zeros((64, 4, 64), np.float64)
    r = np.arange(64)
    bb = r // 4
    tt = r % 4
    same = bb[:, None] == bb[None, :]
    dts = tt[None, :] - tt[:, None]
    for h in range(4):
        ds_[:, h, :] = np.where(same & (dts >= 0), np.exp(np.maximum(dts, 0) * lg[h]), 0.0) * sc
    c["c_dmask_s"] = ds_.reshape(64, 256).astype(f)
    xi_p = np.stack([np.exp((idx + 1.0) * lg[h]) * sc for h in range(4)])
    xi_s = np.stack([np.exp((tt + 1.0) * lg[h]) * sc for h in range(4)])
    c["c_xi"] = np.concatenate([xi_p.reshape(-1), xi_s.reshape(-1)]).astype(f)[None, :]
    zp = np.stack([np.exp((127.0 - idx) * lg[h]) for h in range(4)], axis=1)
    c["c_zeta_p"] = zp.astype(f)
    zs = np.zeros((64, 16, 4), np.float64)
    for h in range(4):
        for b in range(16):
            zs[:, b, h] = np.where(bb == b, np.exp((3.0 - tt) * lg[h]), 0.0)
    c["c_zs"] = zs.reshape(64, 64).astype(f)
    cm = np.zeros((16, 64), f)
    for b in range(16):
        cm[b, 4 * b:4 * b + 4] = 1.0
    c["c_cmask"] = cm.reshape(1, -1)
    _CONSTS = c
    return c


W_NAMES = ["g_mix", "w_in", "lam_re", "lam_im", "log_dt", "b_re", "b_im", "c_re", "c_im", "d_skip", "w_glu",
           "ret_gn", "w_out", "g_xattn", "g_mem", "w_mq", "w_mk", "w_mv", "w_mo", "g_mlp", "w_up", "w_down",
           "g_final"]
W_SHAPES = {"g_mix": [D], "w_in": [D, 2560], "lam_re": [G, 64], "lam_im": [G, 64], "log_dt": [G],
            "b_re": [G, 64, 16], "b_im": [G, 64, 16], "c_re": [G * 16, 64], "c_im": [G * 16, 64], "d_skip": [512],
            "w_glu": [512, 512], "ret_gn": [512], "w_out": [D, D], "g_xattn": [D], "g_mem": [D], "w_mq": [D, D],
            "w_mk": [D, D], "w_mv": [D, D], "w_mo": [D, D], "g_mlp": [D], "w_up": [D, DFF], "w_down": [DFF, D],
            "g_final": [D]}
IN_SHAPES = {"xp": [SEQ, D], "xs": [TS, D], "memp": [MEM, D], "s5r": [512, 64], "s5i": [512, 64],
             "sret": [16, 4, 128, 128], "ck": [16, MEM, D], "cv": [16, MEM, D]}
OUT_SHAPES = {"yp": [SEQ, D], "ys": [TS, D], "o_s5r_p": [G, 64], "o_s5i_p": [G, 64], "o_ret_p": [4, 128, 128],
              "o_mk": [MEM, D], "o_mv": [MEM, D], "o_s5r_s": [512, 64], "o_s5i_s": [512, 64],
              "o_ret_s": [16, 4, 128, 128]}


def build(stage=99, dbg=False):
    nc = bass.Bass("TRN2", target_bir_lowering=False)
    cst = _consts()
    I = {}
    for k, shp in list(IN_SHAPES.items()) + list(W_SHAPES.items()):
        I[k] = nc.dram_tensor(k, shp, F32, kind="ExternalInput").ap()
    for k, v in cst.items():
        I[k] = nc.dram_tensor(k, list(v.shape), F32, kind="ExternalInput").ap()
    O = {}
    for k, shp in OUT_SHAPES.items():
        O[k] = nc.dram_tensor(k, shp, F32, kind="ExternalOutput").ap()
    if dbg:
        O["dbg_ssm"] = nc.dram_tensor("dbg_ssm", [128, 4, NTOK], F32, kind="ExternalOutput").ap()
        O["dbg_x"] = nc.dram_tensor("dbg_x", [128, NT, D], F32, kind="ExternalOutput").ap()

    with ExitStack() as st:
        S = Sched(nc, st)

        def alloc(stack, name, shape, dt=F32):
            return stack.enter_context(nc.sbuf_tensor(name, shape, dt))

        def palloc(stack, name, shape, dt=F32):
            return stack.enter_context(nc.psum_tensor(name, shape, dt))

        def V(fn, r=(), w=()):
            S.op("dve", fn, reads=r, writes=w)

        def A(fn, r=(), w=()):
            S.op("act", fn, reads=r, writes=w)

        import os as _os0
        _nopool = _os0.environ.get("K_NOPOOL") == "1"

        def PL(fn, r=(), w=()):
            S.op("dve" if _nopool else "pool", fn, reads=r, writes=w)

        def T(fn, r=(), w=()):
            S.op("pe", fn, reads=r, writes=w)

        nck = nc.allow_non_contiguous_dma(reason="small param layout loads")
        nck.__enter__()

        identb = alloc(st, "identb", [128, 128], BF16)
        identf = alloc(st, "identf", [128, 128], F32)
        sgn = alloc(st, "sgn", [128, 2])
        epsc = alloc(st, "epsc", [128, 1])
        ssmT = alloc(st, "ssmT", [128, 4, NTOK], BF16)
        b_const = Buf("const")
        b_ssmT = [Buf("ssmT%d" % i) for i in range(5)]
        b_constp = Buf("constp")
        S.dma("pool", identb[:], I["c_ident"][:, :], writes=[b_constp])
        S.dma("sp", identf[:], I["c_ident"][:, :], writes=[b_const])
        S.dma("sp", sgn[:], I["c_sgn"][:, :], writes=[b_const])
        V(lambda e: e.memset(epsc[:], EPS), r=[b_constp], w=[b_const])
        PS = [palloc(st, "ps%d" % i, [128, 512], F32) for i in range(8)]
        bPS = [Buf("ps%d" % i, ps=True) for i in range(8)]

        def ps_bf(i):
            return PS[i][:].bitcast(BF16)

        def make_scr(stack, tag, pbanks):
            d = {"i": 0, "pb": list(pbanks)}
            d["sq"] = [alloc(stack, "sq%s" % tag, [128, D], BF16)] * 2
            d["ss"] = [alloc(stack, "ss%s%d" % (tag, i), [128, 1]) for i in range(2)]
            d["rstd"] = [alloc(stack, "rstd%s%d" % (tag, i), [128, 1]) for i in range(2)]
            d["hb"] = [alloc(stack, "hb%s%d" % (tag, i), [128, D], BF16) for i in range(2)]
            d["ba"] = [Buf("ba%s%d" % (tag, i)) for i in range(2)]
            d["bh"] = [Buf("bh%s%d" % (tag, i)) for i in range(2)]
            return d

        def rmsnorm_hT(xt_ap, bx, npart, gcol, hT_ap, bhT, scr, col0, ph, ln=False, bg=None, out4=None):
            k = scr["i"] % 2
            pbank = scr["pb"][scr["i"] % len(scr["pb"])]
            scr["i"] += 1
            sq, ss, rstd, hb = scr["sq"][k], scr["ss"][k], scr["rstd"][k], scr["hb"][k]
            ba, bh = scr["ba"][k], scr["bh"][k]
            A(lambda e: e.activation(out=sq[:npart, :], in_=xt_ap, func=AF.Square, accum_out=ss[:npart, :]),
              r=[bx], w=[ba])
            if ln:
                A(lambda e: e.activation(out=rstd[:npart, :], in_=ss[:npart, :], func=AF.Ln, scale=1.0 / D,
                                         bias=epsc[:npart, :]), r=[ba, b_const], w=[ba])
                A(lambda e: e.activation(out=rstd[:npart, :], in_=rstd[:npart, :], func=AF.Exp, scale=-0.5),
                  r=[ba], w=[ba])
            else:
                A(lambda e: e.activation(out=rstd[:npart, :], in_=ss[:npart, :], func=AF.Sqrt, scale=1.0 / D,
                                         bias=epsc[:npart, :]), r=[ba, b_const], w=[ba])
                V(lambda e: e.reciprocal(out=rstd[:npart, :], in_=rstd[:npart, :]), r=[ba], w=[ba])
            V(lambda e: e.tensor_scalar(out=hb[:npart, :], in0=xt_ap, scalar1=rstd[:npart, :], scalar2=None,
                                        op0=ALU.mult), r=[bx, ba], w=[bh])
            pv = ps_bf(pbank)
            for kt in range(8):
                T(lambda e, kt=kt: e.transpose(out=pv[:, kt * 128:kt * 128 + npart],
                                               in_=hb[:npart, kt * 128:(kt + 1) * 128],
                                               identity=identb[:npart, :npart]),
                  r=[bh, b_const], w=[bPS[pbank]])
            if out4 is not None:
                V(lambda e: e.tensor_tensor(
                    out=out4, in0=pv.rearrange("p (k c s) -> p k s c", k=8, s=8),
                    in1=gcol.unsqueeze(2).unsqueeze(3).to_broadcast([128, 8, 8, 16]), op=ALU.mult),
                  r=[bPS[pbank], b_const] + ([bg] if bg is not None else []), w=[bhT])
                return
            V(lambda e: e.tensor_tensor(
                out=hT_ap[:, :, col0:col0 + npart],
                in0=pv.rearrange("p (k t) -> p k t", k=8)[:, :, 0:npart],
                in1=gcol.unsqueeze(2).to_broadcast([128, 8, npart]), op=ALU.mult),
              r=[bPS[pbank], b_const] + ([bg] if bg is not None else []), w=[bhT])

        def load_w_bf16(dst, bdst, src, kt_n, ncols, c0=0):
            for kt in range(kt_n):
                for cc in range(0, ncols, 1024):
                    w_ = min(1024, ncols - cc)
                    S.dma("pool", dst[:, kt, cc:cc + w_], src[kt * 128:(kt + 1) * 128, c0 + cc:c0 + cc + w_],
                          writes=[bdst])

        with ExitStack() as sa:
            Wt = alloc(sa, "Wt", [128, G, 128], BF16)
            Wst = alloc(sa, "Wst", [128, G, 128], BF16)
            Tt = alloc(sa, "Tt", [128, G, 128], BF16)
            Vt = alloc(sa, "Vt", [128, G, 128], BF16)
            COSR = alloc(sa, "COSR", [128, G, 64])
            SINR = alloc(sa, "SINR", [128, G, 64])
            masters = alloc(sa, "masters", [128, 8, 240], BF16)
            AR = alloc(sa, "AR", [128, G, K1])
            AI = alloc(sa, "AI", [128, G, K1])
            MAGJ = alloc(sa, "MAGJ", [128, G, K1])
            DS = alloc(sa, "DS", [128, G])
            gm = alloc(sa, "gm", [128, 8])
            winu = alloc(sa, "winu", [128, 8, 512], BF16)
            wglu = alloc(sa, "wglu", [128, 4, 512], BF16)
            b_tab = Buf("s5tab")
            b_winu = Buf("winu", S.GW[0])
            b_wglu = Buf("wglu", S.GW[1])
            b_tabp = Buf("s5tabp")
            S.dma("pool", masters[:], I["c_masters"].rearrange("a k j -> k a j"), writes=[b_tabp])
            S.dma("sp", gm[:], I["g_mix"].rearrange("(k p) -> p k", p=128), writes=[b_tab])
            for tau in range(8):
                S.dma("sp", DS[16 * tau:16 * tau + 16, :], I["d_skip"].rearrange("(g h) -> h g", h=16),
                      writes=[b_tab])
            load_w_bf16(winu, b_winu, I["w_in"], 8, 512, 0)
            load_w_bf16(wglu, b_wglu, I["w_glu"], 4, 512, 0)

            with ExitStack() as s0:
                rows = alloc(s0, "rows", [128, 2 * K1 + 64])
                LR = alloc(s0, "LR", [128, G])
                LI = alloc(s0, "LI", [128, G])
                DT = alloc(s0, "DT", [128, G])
                LRDT = alloc(s0, "LRDT", [128, G])
                LIDT = alloc(s0, "LIDT", [128, G])
                tA = alloc(s0, "tA", [128, G, 64])
                tB = alloc(s0, "tB", [128, G, 64])
                tC = alloc(s0, "tC", [128, G, 64])
                COSJ = alloc(s0, "COSJ", [128, G, K1])
                SINJ = alloc(s0, "SINJ", [128, G, K1])
                sm = alloc(s0, "sm", [128, 12, G])
                Br1 = alloc(s0, "Br1", [128, G, 16])
                Br2 = alloc(s0, "Br2", [128, G, 16])
                BB1 = alloc(s0, "BB1", [128, G, 16])
                BB2 = alloc(s0, "BB2", [128, G, 16])
                tb1 = alloc(s0, "tb1", [128, G, 16])
                big1 = alloc(s0, "big1", [128, G, 128])
                big2 = alloc(s0, "big2", [128, G, 128])
                WTpad = alloc(s0, "WTpad", [128, G, 256], BF16)
                WTs = alloc(s0, "WTs", [128, G, 128], BF16)
                CN1 = alloc(s0, "CN1", [128, 4, 128])
                CN2 = alloc(s0, "CN2", [128, 4, 128])
                CMa = alloc(s0, "CMa", [128, G, 16])
                CMb = alloc(s0, "CMb", [128, G, 16])
                CMab = alloc(s0, "CMab", [128, G, 16], BF16)
                b0 = Buf("p0in")
                bt = Buf("p0tmp")
                S.dma("sp", rows[:], I["c_rows"][0:1, :].partition_broadcast(128), writes=[b0])
                for hf in range(2):
                    S.dma("sp", LR[64 * hf:64 * hf + 64, :], I["lam_re"].rearrange("g p -> p g"), writes=[b0])
                    S.dma("sp", LI[64 * hf:64 * hf + 64, :], I["lam_im"].rearrange("g p -> p g"), writes=[b0])
                S.dma("sp", DT[:], I["log_dt"].rearrange("(o g) -> o g", o=1).partition_broadcast(128), writes=[b0])
                S.dma("sp", Br1[0:64], I["b_re"].rearrange("g p h -> p g h"), writes=[b0])
                S.dma("sp", Br1[64:128], I["b_im"].rearrange("g p h -> p g h"), writes=[b0])
                S.dma("sp", Br2[0:64], I["b_im"].rearrange("g p h -> p g h"), writes=[b0])
                S.dma("sp", Br2[64:128], I["b_re"].rearrange("g p h -> p g h"), writes=[b0])
                S.dma("sp", CN1[:, :, 0:64], I["c_re"].rearrange("(c r) p -> r c p", r=128), writes=[b0])
                S.dma("sp", CN1[:, :, 64:128], I["c_im"].rearrange("(c r) p -> r c p", r=128), writes=[b0])
                S.dma("sp", CN2[:, :, 0:64], I["c_im"].rearrange("(c r) p -> r c p", r=128), writes=[b0])
                S.dma("sp", CN2[:, :, 64:128], I["c_re"].rearrange("(c r) p -> r c p", r=128), writes=[b0])
                MT1 = rows[:, 0:K1]
                MLr = rows[:, K1:2 * K1]
                MRT = rows[:, 2 * K1:2 * K1 + 64]
                A(lambda e: e.activation(out=DT[:], in_=DT[:], func=AF.Exp), r=[b0], w=[b0])
                V(lambda e: e.tensor_tensor(out=LRDT[:], in0=LR[:], in1=DT[:], op=ALU.mult), r=[b0], w=[bt])
                V(lambda e: e.tensor_tensor(out=LIDT[:], in0=LI[:], in1=DT[:], op=ALU.mult), r=[b0], w=[bt])

                def trig(mt_ap, K, cos_out, sin_out):
                    shp = [128, G, K]
                    a_, b_, c_ = tA[:, :, 0:K], tB[:, :, 0:K], tC[:, :, 0:K]
                    V(lambda e: e.tensor_tensor(out=a_, in0=LIDT[:].unsqueeze(2).to_broadcast(shp),
                                                in1=mt_ap.unsqueeze(1).to_broadcast(shp), op=ALU.mult),
                      r=[bt, b0], w=[bt])
                    for (outp, off) in ((sin_out, 0.0), (cos_out, 0.25)):
                        if outp is None:
                            continue
                        V(lambda e, off=off: e.tensor_scalar(out=c_, in0=a_, scalar1=off, scalar2=None,
                                                             op0=ALU.add), r=[bt], w=[bt])
                        V(lambda e: e.tensor_scalar(out=b_, in0=c_, scalar1=MAGIC, scalar2=None, op0=ALU.add),
                          r=[bt], w=[bt])
                        V(lambda e: e.tensor_scalar(out=b_, in0=b_, scalar1=MAGIC, scalar2=None, op0=ALU.subtract),
                          r=[bt], w=[bt])
                        V(lambda e: e.tensor_tensor(out=c_, in0=c_, in1=b_, op=ALU.subtract), r=[bt], w=[bt])
                        A(lambda e, outp=outp: e.activation(out=outp, in_=c_, func=AF.Sin, scale=TWO_PI),
                          r=[bt], w=[b_tab])

                trig(MT1, K1, COSJ[:], SINJ[:])
                trig(MRT, 64, COSR[:], SINR[:])
                shpj = [128, G, K1]
                V(lambda e: e.tensor_tensor(out=MAGJ[:], in0=LRDT[:].unsqueeze(2).to_broadcast(shpj),
                                            in1=MLr.unsqueeze(1).to_broadcast(shpj), op=ALU.mult),
                  r=[bt, b0], w=[b_tab])
                A(lambda e: e.activation(out=MAGJ[:], in_=MAGJ[:], func=AF.Exp), r=[b_tab], w=[b_tab])
                V(lambda e: e.tensor_tensor(out=AR[:], in0=MAGJ[:], in1=COSJ[:], op=ALU.mult), r=[b_tab], w=[b_tab])
                V(lambda e: e.tensor_tensor(out=AI[:], in0=MAGJ[:], in1=SINJ[:], op=ALU.mult), r=[b_tab], w=[b_tab])
                em1, shalf, cm1, am1r, ai1, den, fr, fi, t0_, t1_ = [sm[:, i, :] for i in range(10)]
                x_ = LRDT[:]
                V(lambda e: e.tensor_scalar(out=em1, in0=x_, scalar1=0.2, scalar2=1.0, op0=ALU.mult, op1=ALU.add),
                  r=[bt], w=[bt])
                for cf in (0.25, 1.0 / 3.0, 0.5):
                    V(lambda e: e.tensor_tensor(out=em1, in0=em1, in1=x_, op=ALU.mult), r=[bt], w=[bt])
                    V(lambda e, cf=cf: e.tensor_scalar(out=em1, in0=em1, scalar1=cf, scalar2=1.0, op0=ALU.mult,
                                                       op1=ALU.add), r=[bt], w=[bt])
                V(lambda e: e.tensor_tensor(out=em1, in0=em1, in1=x_, op=ALU.mult), r=[bt], w=[bt])
                V(lambda e: e.tensor_copy(out=shalf, in_=SINJ[:, :, I_HALF]), r=[b_tab], w=[bt])
                V(lambda e: e.scalar_tensor_tensor(out=cm1, in0=shalf, scalar=-2.0, op0=ALU.mult, in1=shalf,
                                                   op1=ALU.mult), r=[bt], w=[bt])
                V(lambda e: e.tensor_tensor(out=am1r, in0=em1, in1=COSJ[:, :, I_A1], op=ALU.mult), r=[bt, b_tab], w=[bt])
                V(lambda e: e.tensor_tensor(out=am1r, in0=am1r, in1=cm1, op=ALU.add), r=[bt], w=[bt])
                V(lambda e: e.tensor_copy(out=ai1, in_=AI[:, :, I_A1]), r=[b_tab], w=[bt])
                V(lambda e: e.tensor_tensor(out=den, in0=LR[:], in1=LR[:], op=ALU.mult), r=[b0], w=[bt])
                V(lambda e: e.tensor_tensor(out=t0_, in0=LI[:], in1=LI[:], op=ALU.mult), r=[b0], w=[bt])
                V(lambda e: e.tensor_tensor(out=den, in0=den, in1=t0_, op=ALU.add), r=[bt], w=[bt])
                V(lambda e: e.reciprocal(out=den, in_=den), r=[bt], w=[bt])
                V(lambda e: e.tensor_tensor(out=fr, in0=am1r, in1=LR[:], op=ALU.mult), r=[bt, b0], w=[bt])
                V(lambda e: e.tensor_tensor(out=t0_, in0=ai1, in1=LI[:], op=ALU.mult), r=[bt, b0], w=[bt])
                V(lambda e: e.tensor_tensor(out=fr, in0=fr, in1=t0_, op=ALU.add), r=[bt], w=[bt])
                V(lambda e: e.tensor_tensor(out=fr, in0=fr, in1=den, op=ALU.mult), r=[bt], w=[bt])
                V(lambda e: e.tensor_tensor(out=fi, in0=ai1, in1=LR[:], op=ALU.mult), r=[bt, b0], w=[bt])
                V(lambda e: e.tensor_tensor(out=t0_, in0=am1r, in1=LI[:], op=ALU.mult), r=[bt, b0], w=[bt])
                V(lambda e: e.tensor_tensor(out=fi, in0=fi, in1=t0_, op=ALU.subtract), r=[bt], w=[bt])
                V(lambda e: e.tensor_tensor(out=fi, in0=fi, in1=den, op=ALU.mult), r=[bt], w=[bt])
                V(lambda e: e.tensor_scalar(out=Br2[:], in0=Br2[:], scalar1=sgn[:, 1:2], scalar2=None, op0=ALU.mult),
                  r=[b0, b_const], w=[b0])
                shb = [128, G, 16]
                frb = fr.unsqueeze(2).to_broadcast(shb)
                fib = fi.unsqueeze(2).to_broadcast(shb)
                V(lambda e: e.tensor_tensor(out=BB1[:], in0=Br1[:], in1=frb, op=ALU.mult), r=[b0, bt], w=[bt])
                V(lambda e: e.tensor_tensor(out=tb1[:], in0=Br2[:], in1=fib, op=ALU.mult), r=[b0, bt], w=[bt])
                V(lambda e: e.tensor_tensor(out=BB1[:], in0=BB1[:], in1=tb1[:], op=ALU.add), r=[bt], w=[bt])
                V(lambda e: e.tensor_tensor(out=BB2[:], in0=Br2[:], in1=frb, op=ALU.mult), r=[b0, bt], w=[bt])
                V(lambda e: e.tensor_tensor(out=tb1[:], in0=Br1[:], in1=fib, op=ALU.mult), r=[b0, bt], w=[bt])
                V(lambda e: e.tensor_tensor(out=BB2[:], in0=BB2[:], in1=tb1[:], op=ALU.subtract), r=[bt], w=[bt])
                sh4 = [128, G, 8, 16]
                arv = AR[:, :, 0:8].unsqueeze(3).to_broadcast(sh4)
                aiv = AI[:, :, 0:8].unsqueeze(3).to_broadcast(sh4)
                bb1 = BB1[:].unsqueeze(2).to_broadcast(sh4)
                bb2 = BB2[:].unsqueeze(2).to_broadcast(sh4)
                g1 = big1[:].rearrange("p g (s h) -> p g s h", s=8)
                g2 = big2[:].rearrange("p g (s h) -> p g s h", s=8)
                V(lambda e: e.memset(WTpad[:], 0.0), w=[bt])
                V(lambda e: e.tensor_tensor(out=g1, in0=arv, in1=bb1, op=ALU.mult), r=[b_tab, bt], w=[bt])
                V(lambda e: e.tensor_tensor(out=g2, in0=aiv, in1=bb2, op=ALU.mult), r=[b_tab, bt], w=[bt])
                V(lambda e: e.tensor_tensor(out=WTpad[:, :, 0:128], in0=big1[:], in1=big2[:], op=ALU.add),
                  r=[bt], w=[bt])
                V(lambda e: e.tensor_tensor(out=g1, in0=arv, in1=bb2, op=ALU.mult), r=[b_tab, bt], w=[bt])
                V(lambda e: e.tensor_tensor(out=g2, in0=aiv, in1=bb1, op=ALU.mult), r=[b_tab, bt], w=[bt])
                V(lambda e: e.tensor_tensor(out=WTs[:], in0=big1[:], in1=big2[:], op=ALU.subtract), r=[bt], w=[bt])
                for (src_fn, dstt) in ((lambda g: WTpad[:, g, 0:128], Wt), (lambda g: WTs[:, g, :], Wst)):
                    for gq in range(8):
                        bank = gq % 2
                        pv = ps_bf(bank)
                        for j in range(4):
                            g = gq * 4 + j
                            T(lambda e, g=g, j=j, pv=pv, src_fn=src_fn: e.transpose(
                                out=pv[:, j * 128:(j + 1) * 128], in_=src_fn(g), identity=identb[:]),
                              r=[bt, b_const], w=[bPS[bank]])
                        A(lambda e, gq=gq, pv=pv, dstt=dstt: e.copy(
                            out=dstt[:, gq * 4:gq * 4 + 4, :], in_=pv[:, 0:512].rearrange("p (j c) -> p j c", j=4)),
                          r=[bPS[bank]], w=[b_tab])
                for (CN, CM, col) in ((CN1, CMa, 0), (CN2, CMb, None)):
                    for c4 in range(4):
                        bank = 2 + (c4 % 2)
                        T(lambda e, CN=CN, c4=c4, bank=bank: e.transpose(out=PS[bank][:, 0:128], in_=CN[:, c4, :],
                                                                         identity=identf[:]),
                          r=[b0, b_const], w=[bPS[bank]])
                        if col is not None:
                            V(lambda e, CM=CM, c4=c4, bank=bank: e.tensor_scalar(
                                out=CM[:, c4 * 8:(c4 + 1) * 8, :],
                                in0=PS[bank][:, 0:128].rearrange("p (g h) -> p g h", g=8),
                                scalar1=sgn[:, 0:1], scalar2=None, op0=ALU.mult),
                              r=[bPS[bank], b_const], w=[bt])
                        else:
                            V(lambda e, CM=CM, c4=c4, bank=bank: e.tensor_scalar(
                                out=CM[:, c4 * 8:(c4 + 1) * 8, :],
                                in0=PS[bank][:, 0:128].rearrange("p (g h) -> p g h", g=8),
                                scalar1=-1.0, scalar2=None, op0=ALU.mult),
                              r=[bPS[bank]], w=[bt])
                V(lambda e: e.tensor_copy(out=CMab[:], in_=CMa[:]), r=[bt], w=[bt])
                afw = AR[:, :, 8:16].unsqueeze(3).to_broadcast(sh4)
                aifw = AI[:, :, 8:16].unsqueeze(3).to_broadcast(sh4)
                cma = CMa[:].unsqueeze(2).to_broadcast(sh4)
                cmb = CMb[:].unsqueeze(2).to_broadcast(sh4)
                V(lambda e: e.tensor_tensor(out=g1, in0=afw, in1=cma, op=ALU.mult), r=[b_tab, bt], w=[bt])
                V(lambda e: e.tensor_tensor(out=g2, in0=aifw, in1=cmb, op=ALU.mult), r=[b_tab, bt], w=[bt])
                V(lambda e: e.tensor_tensor(out=Vt[:], in0=big1[:], in1=big2[:], op=ALU.add), r=[bt], w=[b_tab])
                for gq in range(8):
                    bank = 4 + (gq % 2)
                    for j in range(4):
                        g = gq * 4 + j
                        for tau in range(8):
                            c0 = (7 - tau) * 16
                            T(lambda e, g=g, j=j, tau=tau, c0=c0, bank=bank: e.matmul(
                                PS[bank][:, j * 128 + tau * 16:j * 128 + tau * 16 + 16],
                                lhsT=WTpad[:, g, c0:c0 + 128], rhs=CMab[:, g, :], start=True, stop=True),
                              r=[bt], w=[bPS[bank]])
                    A(lambda e, gq=gq, bank=bank: e.copy(
                        out=Tt[:, gq * 4:gq * 4 + 4, :], in_=PS[bank][:].rearrange("p (j c) -> p j c", j=4)),
                      r=[bPS[bank]], w=[b_tab])
                S.barrier()
            xst = [alloc(sa, "xst%d" % i, [128, D]) for i in range(2)]
            bxst = [Buf("xst%d" % i, S.GL[i]) for i in range(2)]
            scrA = make_scr(sa, "A", [7])
            bscr = Buf("scrA")
            hT2 = [alloc(sa, "hT_%d" % i, [128, 8, 512], BF16) for i in range(2)]
            bhT2 = [Buf("hT_%d" % i) for i in range(2)]
            uT2 = [alloc(sa, "uT_%d" % i, [128, 4, 512], BF16) for i in range(2)]
            buT2 = [Buf("uT_%d" % i) for i in range(2)]
            U = alloc(sa, "U", [128, G, 64], BF16)
            bU = Buf("U")
            rr = alloc(sa, "rr", [128, G, 64])
            rs = alloc(sa, "rs", [128, G, 64])
            ww = alloc(sa, "ww", [128, G, 64])
            ws = alloc(sa, "ws", [128, G, 64])
            tmpr = alloc(sa, "tmpr", [128, 16, 64])
            b_r, b_rs, b_w, b_ws, b_tmpr = Buf("r"), Buf("rs"), Buf("w"), Buf("ws"), Buf("tmpr")
            Xb = alloc(sa, "Xb", [128, G, 65], BF16)
            bXb = Buf("Xb")
            Xc = alloc(sa, "Xc", [128, G])
            Xsc = alloc(sa, "Xsc", [128, G])
            ctmp = alloc(sa, "ctmp", [128, 2, G])
            bXc = Buf("Xc", S.GS[0])
            ytmp = alloc(sa, "ytmp", [128, 8, 64])
            bytmp = Buf("ytmp")
            Zt = alloc(sa, "Zt", [128, G, 64], BF16)
            bZ = Buf("Z")
            zT = alloc(sa, "zT", [128, 4, 512], BF16)
            bzT = Buf("zT")
            sig = alloc(sa, "sig", [128, 4, 512])
            bsig = Buf("sig")
            H0 = alloc(sa, "H0", [128, 512])
            H0s = alloc(sa, "H0s", [128, 512])
            hn = alloc(sa, "hn", [128, 4, 128])
            hn2 = alloc(sa, "hn2", [128, 4, 128])
            Hp = alloc(sa, "Hp", [128, G, 16])
            Xf = alloc(sa, "Xf", [128, G, 16])
            xo = alloc(sa, "xo", [128, 4, 128])
            bH = Buf("H0")
            bxo = Buf("xo", S.GS[1])
            V(lambda e: e.memset(Xc[:], 0.0), r=[b_tabp], w=[bXc, b_tab])
            V(lambda e: e.memset(Xsc[:], 0.0), w=[bXc])
            V(lambda e: e.memset(Xb[:], 0.0), w=[bXb])

            blocks = [(i * 512, 512, False) for i in range(4)] + [(SEQ, TS, True)]
            if _os0.environ.get("K1A") == "0":
                blocks = []
            def p1a_stageA(bi):
                t0, n, is_s = blocks[bi]
                hT, bhT = hT2[bi % 2], bhT2[bi % 2]
                uT, buT = uT2[bi % 2], buT2[bi % 2]
                ntile = (n + 127) // 128
                for ti in range(ntile):
                    npart = min(128, n - ti * 128)
                    slot = (bi * 4 + ti) % 2
                    src = I["xs"][:, :] if is_s else I["xp"][t0 + ti * 128:t0 + ti * 128 + 128, :]
                    S.dma("sp", xst[slot][:npart, :], src, writes=[bxst[slot]])
                    o4 = None if is_s else hT[:, :, :].rearrange("p k (s c) -> p k s c", s=8)[:, :, :, ti * 16:(ti + 1) * 16]
                    rmsnorm_hT(xst[slot][:npart, :], bxst[slot], npart, gm[:], hT, bhT,
                               scrA, ti * 128, None, bg=b_tab, out4=o4)
                for ct in range(4):
                    bank = ct
                    for kt in range(8):
                        T(lambda e, ct=ct, kt=kt, bank=bank: e.matmul(
                            PS[bank][:, 0:n], lhsT=winu[:, kt, ct * 128:(ct + 1) * 128], rhs=hT[:, kt, 0:n],
                            start=(kt == 0), stop=(kt == 7)), r=[b_winu, bhT], w=[bPS[bank]])
                    A(lambda e, ct=ct, bank=bank: e.copy(out=uT[:, ct, 0:n], in_=PS[bank][:, 0:n]),
                      r=[bPS[bank]], w=[buT])

            if blocks:
                p1a_stageA(0)
            for bi, (t0, n, is_s) in enumerate(blocks):
                nch = n // 8 if not is_s else 16
                uT, buT = uT2[bi % 2], buT2[bi % 2]
                for gq in range(4):
                    bank = 4 + (gq % 2)
                    for j in range(8):
                        g = gq * 8 + j
                        ct, gl = g // 8, g % 8
                        if not is_s:
                            uv = uT[:, ct, 0:n].rearrange("p (s c) -> p s c", s=8)
                            sig_list = list(range(8))
                        else:
                            uv = uT[:, ct, 0:n].rearrange("p (b t) -> p t b", t=4)
                            sig_list = [4, 5, 6, 7]
                        for si, sg_ in enumerate(sig_list):
                            rhs = uv[:, sg_ if not is_s else si, :]
                            T(lambda e, j=j, gl=gl, sg_=sg_, rhs=rhs, si=si, bank=bank, L=len(sig_list): e.matmul(
                                PS[bank][:, j * 64:j * 64 + nch],
                                lhsT=masters[:, gl, 112 - 16 * sg_:240 - 16 * sg_], rhs=rhs,
                                start=(si == 0), stop=(si == L - 1)),
                              r=[b_tab, buT], w=[bPS[bank]])
                    A(lambda e, gq=gq, bank=bank: e.copy(
                        out=U[:, gq * 8:gq * 8 + 8, 0:nch],
                        in_=PS[bank][:].rearrange("p (j c) -> p j c", j=8)[:, :, 0:nch]),
                      r=[bPS[bank]], w=[bU])
                if not is_s:
                    for hf in range(2):
                        for j in range(16):
                            g = hf * 16 + j
                            for (wt, bk) in ((Wt, 0), (Wst, 2)):
                                bank = bk + j // 8
                                T(lambda e, g=g, j=j, wt=wt, bank=bank: e.matmul(
                                    PS[bank][:, (j % 8) * 64:(j % 8) * 64 + 64], lhsT=wt[:, g, :], rhs=U[:, g, :],
                                    start=True, stop=True), r=[b_tab, bU], w=[bPS[bank]])
                        for q in range(2):
                            gs = slice(hf * 16 + q * 8, hf * 16 + q * 8 + 8)
                            Sv = PS[q][:].rearrange("p (j c) -> p j c", j=8)
                            Ssv = PS[2 + q][:].rearrange("p (j c) -> p j c", j=8)
                            tm = tmpr[:, q * 8:q * 8 + 8, :]
                            V(lambda e, gs=gs, Sv=Sv: e.tensor_tensor(out=rr[:, gs, :], in0=Sv, in1=COSR[:, gs, :],
                                                                     op=ALU.mult), r=[bPS[q], b_tab], w=[b_r])
                            V(lambda e, gs=gs, Ssv=Ssv, tm=tm: e.tensor_tensor(out=tm, in0=Ssv, in1=SINR[:, gs, :],
                                                                              op=ALU.mult),
                              r=[bPS[2 + q], b_tab], w=[b_tmpr])
                            V(lambda e, gs=gs, tm=tm: e.tensor_tensor(out=rr[:, gs, :], in0=rr[:, gs, :], in1=tm,
                                                                     op=ALU.subtract), r=[b_r, b_tmpr], w=[b_r])
                            V(lambda e, gs=gs, Ssv=Ssv: e.tensor_tensor(out=rs[:, gs, :], in0=Ssv, in1=COSR[:, gs, :],
                                                                       op=ALU.mult), r=[bPS[2 + q], b_tab], w=[b_rs])
                            V(lambda e, gs=gs, Sv=Sv, tm=tm: e.tensor_tensor(out=tm, in0=Sv, in1=SINR[:, gs, :],
                                                                            op=ALU.mult),
                              r=[bPS[q], b_tab], w=[b_tmpr])
                            V(lambda e, gs=gs, tm=tm: e.tensor_tensor(out=rs[:, gs, :], in0=rs[:, gs, :], in1=tm,
                                                                     op=ALU.add), r=[b_rs, b_tmpr], w=[b_rs])
                    for g in range(G):
                        rho = MAGJ[:, g, I_A8:I_A8 + 1].to_broadcast([128, 64])
                        V(lambda e, g=g, rho=rho: e.tensor_tensor_scan(
                            out=ww[:, g, :], data0=rho, data1=rr[:, g, :], initial=Xc[:, g:g + 1], op0=ALU.mult,
                            op1=ALU.add), r=[b_r, b_tab, bXc], w=[b_w])
                        V(lambda e, g=g, rho=rho: e.tensor_tensor_scan(
                            out=ws[:, g, :], data0=rho, data1=rs[:, g, :], initial=Xsc[:, g:g + 1], op0=ALU.mult,
                            op1=ALU.add), r=[b_rs, b_tab, bXc], w=[b_ws])
                    if bi + 1 < len(blocks):
                        p1a_stageA(bi + 1)
                    ce, se_ = COSR[:, :, 63], SINR[:, :, 63]
                    we, wse = ww[:, :, 63], ws[:, :, 63]
                    V(lambda e: e.tensor_tensor(out=ctmp[:, 0, :], in0=ce, in1=we, op=ALU.mult), r=[b_w, b_tab], w=[bscr])
                    V(lambda e: e.tensor_tensor(out=ctmp[:, 1, :], in0=se_, in1=wse, op=ALU.mult), r=[b_ws, b_tab], w=[bscr])
                    V(lambda e: e.tensor_tensor(out=Xc[:], in0=ctmp[:, 0, :], in1=ctmp[:, 1, :], op=ALU.add),
                      r=[bscr], w=[bXc])
                    V(lambda e: e.tensor_tensor(out=ctmp[:, 0, :], in0=ce, in1=wse, op=ALU.mult), r=[b_ws, b_tab], w=[bscr])
                    V(lambda e: e.tensor_tensor(out=ctmp[:, 1, :], in0=se_, in1=we, op=ALU.mult), r=[b_w, b_tab], w=[bscr])
                    V(lambda e: e.tensor_tensor(out=Xsc[:], in0=ctmp[:, 0, :], in1=ctmp[:, 1, :], op=ALU.subtract),
                      r=[bscr], w=[bXc])
                    if bi > 0:
                        V(lambda e: e.tensor_copy(out=Xb[:, :, 0], in_=Xb[:, :, 64]), r=[bXb], w=[bXb])
                    V(lambda e: e.tensor_tensor(out=ww[:], in0=ww[:], in1=COSR[:], op=ALU.mult), r=[b_w, b_tab, bXc],
                      w=[b_w])
                    V(lambda e: e.tensor_tensor(out=ws[:], in0=ws[:], in1=SINR[:], op=ALU.mult), r=[b_ws, b_tab, bXc],
                      w=[b_ws])
                    V(lambda e: e.tensor_tensor(out=Xb[:, :, 1:65], in0=ww[:], in1=ws[:], op=ALU.add),
                      r=[b_w, b_ws], w=[bXb])
                    xprev = lambda g: Xb[:, g, 0:64]
                    bXprev = bXb
                    if bi == 3:
                        S.dma("sp", O["o_s5r_p"].rearrange("g p -> p g"), Xc[0:64, :], reads=[bXc])
                        S.dma("sp", O["o_s5i_p"].rearrange("g p -> p g"), Xc[64:128, :], reads=[bXc])
                else:
                    S.dma("sp", hn[:, :, 0:64], I["s5r"].rearrange("(j r) p -> r j p", r=128), writes=[bH])
                    S.dma("sp", hn[:, :, 64:128], I["s5i"].rearrange("(j r) p -> r j p", r=128), writes=[bH])
                    S.dma("sp", hn2[:, :, 0:64], I["s5i"].rearrange("(j r) p -> r j p", r=128), writes=[bH])
                    S.dma("sp", hn2[:, :, 64:128], I["s5r"].rearrange("(j r) p -> r j p", r=128), writes=[bH])
                    for (src_, dst_, bank) in ((hn, H0, 0), (hn2, H0s, 1)):
                        for j in range(4):
                            T(lambda e, src_=src_, j=j, bank=bank: e.transpose(
                                out=PS[bank][:, j * 128:(j + 1) * 128], in_=src_[:, j, :], identity=identf[:]),
                              r=[bH, b_const], w=[bPS[bank]])
                        V(lambda e, dst_=dst_, bank=bank: e.tensor_copy(out=dst_[:], in_=PS[bank][:]),
                          r=[bPS[bank]], w=[bH])
                    V(lambda e: e.tensor_scalar(out=H0s[0:64, :], in0=H0s[0:64, :], scalar1=-1.0, scalar2=None,
                                                op0=ALU.mult), r=[bH], w=[bH])
                    shs = [128, G, 16]
                    h0v = H0[:].rearrange("p (b g) -> p g b", g=G)
                    h0sv = H0s[:].rearrange("p (b g) -> p g b", g=G)

                    def abc(tab, idx):
                        return tab[:, :, idx].unsqueeze(2).to_broadcast(shs)
                    V(lambda e: e.tensor_tensor(out=Xf[:], in0=h0v, in1=abc(AR, I_AM4), op=ALU.mult), r=[bH, b_tab], w=[bxo])
                    V(lambda e: e.tensor_tensor(out=Hp[:], in0=h0sv, in1=abc(AI, I_AM4), op=ALU.mult), r=[bH, b_tab], w=[bxo])
                    V(lambda e: e.tensor_tensor(out=Xb[:, :, 0:16], in0=Xf[:], in1=Hp[:], op=ALU.add), r=[bxo], w=[bXb])
                    V(lambda e: e.tensor_tensor(out=Xf[:], in0=h0v, in1=abc(AR, I_A4), op=ALU.mult), r=[bH, b_tab], w=[bxo])
                    V(lambda e: e.tensor_tensor(out=Hp[:], in0=h0sv, in1=abc(AI, I_A4), op=ALU.mult), r=[bH, b_tab], w=[bxo])
                    V(lambda e: e.tensor_tensor(out=Xf[:], in0=Xf[:], in1=Hp[:], op=ALU.add), r=[bxo], w=[bxo])
                    for q in range(4):
                        bank = q % 2
                        for j in range(8):
                            g = q * 8 + j
                            T(lambda e, g=g, j=j, bank=bank: e.matmul(
                                PS[bank][:, j * 64:j * 64 + 16], lhsT=Wt[:, g, :], rhs=U[:, g, 0:16],
                                start=True, stop=True), r=[b_tab, bU], w=[bPS[bank]])
                        V(lambda e, q=q, bank=bank: e.tensor_tensor(
                            out=Xf[:, q * 8:q * 8 + 8, :], in0=Xf[:, q * 8:q * 8 + 8, :],
                            in1=PS[bank][:].rearrange("p (j c) -> p j c", j=8)[:, :, 0:16], op=ALU.add),
                          r=[bxo, bPS[bank]], w=[bxo])
                    Xf2 = Xf[:].rearrange("p g b -> p (g b)")
                    for j in range(4):
                        T(lambda e, j=j: e.transpose(out=PS[2][:, j * 128:(j + 1) * 128],
                                                     in_=Xf2[:, j * 128:(j + 1) * 128], identity=identf[:]),
                          r=[bxo, b_const], w=[bPS[2]])
                    V(lambda e: e.tensor_copy(out=xo[:], in_=PS[2][:].rearrange("p (j c) -> p j c", j=4)),
                      r=[bPS[2]], w=[bxo])
                    for j in range(4):
                        for gl in range(8):
                            for (nm, c0) in (("o_s5r_s", 0), ("o_s5i_s", 64)):
                                S.dma("sp", O[nm].rearrange("(b g) p -> g b p", g=G)[8 * j + gl],
                                      xo[gl * 16:gl * 16 + 16, j, c0:c0 + 64], reads=[bxo])
                    xprev = lambda g: Xb[:, g, 0:16]
                    bXprev = bXb
                for gq in range(4):
                    bank = 6 + (gq % 2)
                    for j in range(8):
                        g = gq * 8 + j
                        T(lambda e, g=g, j=j, bank=bank: e.matmul(
                            PS[bank][:, j * 64:j * 64 + nch], lhsT=Tt[:, g, :], rhs=U[:, g, 0:nch],
                            start=True, stop=False), r=[b_tab, bU], w=[bPS[bank]])
                        T(lambda e, g=g, j=j, bank=bank: e.matmul(
                            PS[bank][:, j * 64:j * 64 + nch], lhsT=Vt[:, g, :], rhs=xprev(g)[:, 0:nch],
                            start=False, stop=True), r=[b_tab, bXprev], w=[bPS[bank]])
                    gs = slice(gq * 8, gq * 8 + 8)
                    yv = PS[bank][:].rearrange("p (j c) -> p j c", j=8)[:, :, 0:nch]
                    V(lambda e, gs=gs: e.tensor_tensor(out=ytmp[:, :, 0:nch], in0=U[:, gs, 0:nch],
                                                       in1=DS[:, gs].unsqueeze(2).to_broadcast([128, 8, nch]),
                                                       op=ALU.mult), r=[bU, b_tab], w=[bytmp])
                    V(lambda e, yv=yv: e.tensor_tensor(out=ytmp[:, :, 0:nch], in0=yv, in1=ytmp[:, :, 0:nch],
                                                       op=ALU.add), r=[bPS[bank], bytmp], w=[bytmp])
                    A(lambda e, gs=gs: e.activation(out=Zt[:, gs, 0:nch], in_=ytmp[:, :, 0:nch],
                                                    func=AF.Gelu_apprx_tanh), r=[bytmp], w=[bZ])
                for ct in range(4):
                    bank = ct % 2
                    taus = list(range(8)) if not is_s else [4, 5, 6, 7]
                    for ti_, tau in enumerate(taus):
                        for gl in range(8):
                            g = ct * 8 + gl
                            T(lambda e, g=g, gl=gl, tau=tau, ti_=ti_, bank=bank: e.matmul(
                                PS[bank][:, ti_ * 64:ti_ * 64 + nch],
                                lhsT=masters[:, tau, 112 - 16 * gl:240 - 16 * gl], rhs=Zt[:, g, 0:nch],
                                start=(gl == 0), stop=(gl == 7)), r=[b_tab, bZ], w=[bPS[bank]])
                    if not is_s:
                        A(lambda e, ct=ct, bank=bank: e.copy(
                            out=zT[:, ct, 0:n].rearrange("p (c t) -> p t c", t=8),
                            in_=PS[bank][:].rearrange("p (t c) -> p t c", t=8)), r=[bPS[bank]], w=[bzT])
                    else:
                        A(lambda e, ct=ct, bank=bank: e.copy(
                            out=zT[:, ct, 0:n].rearrange("p (b t) -> p t b", t=4),
                            in_=PS[bank][:].rearrange("p (t c) -> p t c", t=8)[:, 0:4, 0:16]),
                          r=[bPS[bank]], w=[bzT])
                for ct in range(4):
                    bank = 2 + (ct % 2)
                    for kt in range(4):
                        T(lambda e, ct=ct, kt=kt, bank=bank: e.matmul(
                            PS[bank][:, 0:n], lhsT=wglu[:, kt, ct * 128:(ct + 1) * 128], rhs=zT[:, kt, 0:n],
                            start=(kt == 0), stop=(kt == 3)), r=[b_wglu, bzT], w=[bPS[bank]])
                    A(lambda e, ct=ct, bank=bank: e.activation(out=sig[:, ct, 0:n], in_=PS[bank][:, 0:n],
                                                               func=AF.Sigmoid), r=[bPS[bank]], w=[bsig])
                V(lambda e: e.tensor_tensor(out=ssmT[:, :, t0:t0 + n], in0=zT[:, :, 0:n], in1=sig[:, :, 0:n],
                                            op=ALU.mult), r=[bzT, bsig], w=[b_ssmT[bi]])
            S.barrier()
        if dbg:
            with ExitStack() as sd:
                dtmp = alloc(sd, "dtmp", [128, 4, NTOK])
                bd = Buf("dtmp", S.GS[2])
                V(lambda e: e.tensor_copy(out=dtmp[:], in_=ssmT[:]), r=b_ssmT, w=[bd])
                S.dma("sp", O["dbg_ssm"][:, :, :], dtmp[:], reads=[bd])
                S.barrier()
        if stage <= 1:
            S.barrier()
            S.run_block()
            nck.__exit__(None, None, None)
            return nc

        with ExitStack() as sbx:
            x = alloc(sbx, "x", [128, NT, D])
            bx = [Buf("x%d" % n, S.GX) for n in range(NT)]
            for n in range(NTP):
                S.dma("sp", x[:, n, :], I["xp"][n * 128:(n + 1) * 128, :], writes=[bx[n]])
            S.dma("sp", x[0:TS, 16, :], I["xs"][:, :], writes=[bx[16]])
            scrB = make_scr(sbx, "B", [7])
            hT1 = alloc(sbx, "hT1", [128, 8, 128], BF16)
            bhT1 = Buf("hT1")

            def resid_add(n, npart, half, bank):
                V(lambda e: e.tensor_tensor(out=x[:npart, n, half * 512:(half + 1) * 512], in0=PS[bank][:npart, :],
                                            in1=x[:npart, n, half * 512:(half + 1) * 512], op=ALU.add),
                  r=[bPS[bank], bx[n]], w=[bx[n]])

            with ExitStack() as s1:
                wq = alloc(s1, "wqkvg", [128, 8, 2048], BF16)
                wout = alloc(s1, "wout", [128, 8, D], BF16)
                b_wqc = [Buf("wq%d" % c, S.GW[c]) for c in range(4)]
                b_wout = Buf("wout", S.GW[0])
                for c in range(4):
                    for kt in range(8):
                        S.dma("pool", wq[:, kt, c * 512:(c + 1) * 512],
                              I["w_in"][kt * 128:(kt + 1) * 128, 512 + c * 512:512 + (c + 1) * 512], writes=[b_wqc[c]])
                wout_loaded = [False]
                gm2 = alloc(s1, "gm2", [128, 8])
                gn = alloc(s1, "gn", [128, 4])
                rope = alloc(s1, "rope", [128, 3, NT, 64])
                dmp = alloc(s1, "dmp", [128, 512])
                dms = alloc(s1, "dms", [64, 256])
                xi = alloc(s1, "xi", [128, 768])
                zetap = alloc(s1, "zetap", [128, 4])
                zs = alloc(s1, "zs", [64, 64])
                cmask = alloc(s1, "cmask", [128, 16 * 64])
                b_t1 = Buf("tab1")
                S.dma("sp", gm2[:], I["g_mix"].rearrange("(k p) -> p k", p=128), writes=[b_t1])
                S.dma("sp", gn[:], I["ret_gn"].rearrange("(k p) -> p k", p=128), writes=[b_t1])
                for a_ in range(3):
                    S.dma("sp", rope[:, a_, :, :], I["c_rope"][a_], writes=[b_t1])
                S.dma("sp", dmp[:], I["c_dmask_p"][:, :], writes=[b_t1])
                S.dma("sp", dms[:], I["c_dmask_s"][:, :], writes=[b_t1])
                S.dma("sp", xi[:], I["c_xi"][0:1, :].partition_broadcast(128), writes=[b_t1])
                S.dma("sp", zetap[:], I["c_zeta_p"][:, :], writes=[b_t1])
                S.dma("sp", zs[:], I["c_zs"][:, :], writes=[b_t1])
                S.dma("sp", cmask[:], I["c_cmask"][0:1, :].partition_broadcast(128), writes=[b_t1])
                def load_wout():
                    load_w_bf16(wout, b_wout, I["w_out"], 8, D, 0)
                    for k in range(4):
                        V(lambda e: e.tensor_scalar(out=wout[:, 4 + k, :], in0=wout[:, 4 + k, :], scalar1=gn[:, k:k + 1],
                                                    scalar2=None, op0=ALU.mult), r=[b_wout, b_t1], w=[b_wout])
                    wout_loaded[0] = True
                t1q = alloc(s1, "t1q", [128, 512])
                t2q = alloc(s1, "t2q", [128, 512])
                t1k = alloc(s1, "t1k", [128, 512])
                t2k = alloc(s1, "t2k", [128, 512])
                qr = alloc(s1, "qr", [128, 512], BF16)
                kr = alloc(s1, "kr", [128, 512], BF16)
                qT = alloc(s1, "qT", [128, 4, 128], BF16)
                qxT = alloc(s1, "qxT", [128, 4, 128], BF16)
                kT = alloc(s1, "kT", [128, 4, 128], BF16)
                vb = alloc(s1, "vb", [128, 512], BF16)
                vz = alloc(s1, "vz", [128, 512], BF16)
                sg_ = alloc(s1, "sgl", [128, 512])
                sT = alloc(s1, "sT", [128, 4, 128], BF16)
                Sst = alloc(s1, "Sst", [128, 4, 128])
                Sbf = alloc(s1, "Sbf", [128, 4, 128], BF16)
                stats = alloc(s1, "stats", [128, 4, 6])
                mv = alloc(s1, "mv", [128, 4, 2])
                rs4 = alloc(s1, "rs4", [128, 4])
                nb4 = alloc(s1, "nb4", [128, 4])
                on = alloc(s1, "on", [128, 512])
                ret = alloc(s1, "ret", [128, 512], BF16)
                retT = alloc(s1, "retT", [128, 4, 128], BF16)
                S0 = [alloc(s1, "S0_%d" % i, [128, 4, 128]) for i in range(2)]
                S0b = [alloc(s1, "S0b_%d" % i, [128, 4, 128], BF16) for i in range(2)]
                qxm = [alloc(s1, "qxm_%d" % i, [128, 4, 64], BF16) for i in range(2)]
                vzb = [alloc(s1, "vzb_%d" % i, [64, 512], BF16) for i in range(2)]
                Sn = [alloc(s1, "Sn_%d" % i, [128, 4, 128]) for i in range(2)]
                bS0 = [Buf("S0_%d" % i, S.GL[i]) for i in range(2)]
                bS0b = [Buf("S0b_%d" % i) for i in range(2)]
                bqxm = [Buf("qxm%d" % i) for i in range(2)]
                bvzb = [Buf("vzb%d" % i) for i in range(2)]
                bSn = [Buf("Sn%d" % i, S.GS[i]) for i in range(2)]
                (b_t1q, b_t2q, b_t1k, b_t2k, b_qr, b_kr, b_qT, b_qxT, b_kT, b_vb, b_vz, b_sg, b_sT, b_Sst, b_Sbf,
                 b_st, b_on, b_ret, b_retT) = [Buf("p1b%d" % i) for i in range(19)]
                b_Sst.grp = S.GS[2]
                V(lambda e: e.memset(Sst[:], 0.0), w=[b_Sst])
                GC_P = [float(g ** 128) for g in GAM]
                GC_S = [float(g ** 4) for g in GAM]

                import os as _os
                _tl = _os.environ.get("K_TILES")
                _tiles = [int(v) for v in _tl.split(",") if int(v) >= 0] if _tl else list(range(NT))
                _step = int(_os.environ.get("K_STEP", "99"))
                hT1s = [hT1, alloc(s1, "hT1c", [128, 8, 128], BF16)]
                bhT1s = [bhT1, Buf("hT1c")]

                def p1b_norm(n):
                    npt_ = TS if n == 16 else 128
                    rmsnorm_hT(x[:npt_, n, :], bx[n], npt_, gm2[:], hT1s[n % 2], bhT1s[n % 2], scrB, 0, None, bg=b_t1)
                def p1b_proj(n):
                    npt_ = TS if n == 16 else 128
                    hTn, bhTn = hT1s[n % 2], bhT1s[n % 2]
                    for c in range(4):
                        for kt in range(8):
                            T(lambda e: e.matmul(PS[c][:npt_, :], lhsT=hTn[:, kt, 0:npt_],
                                                 rhs=wq[:, kt, c * 512:(c + 1) * 512], start=(kt == 0), stop=(kt == 7)),
                              r=[bhTn, b_wqc[c]], w=[bPS[c]])
                if _tiles:
                    p1b_norm(_tiles[0])
                    p1b_proj(_tiles[0])
                    load_wout()
                for ti_, n in enumerate(_tiles):
                    is_s = (n == 16)
                    npt = TS if is_s else 128
                    tok0 = n * 128
                    hT1, bhT1 = hT1s[n % 2], bhT1s[n % 2]
                    pob = [4, 6, 7, 1] if is_s else [4, 4, 4, 4]

                    def po(h):
                        if is_s:
                            return PS[pob[h]][:npt, 0:128]
                        return PS[4][:npt, h * 128:(h + 1) * 128]
                    if _step <= 1:
                        continue
                    for (bank, t1_, t2_, out_, bt1, bt2, bo) in ((0, t1q, t2q, qr, b_t1q, b_t2q, b_qr),
                                                               (1, t1k, t2k, kr, b_t1k, b_t2k, b_kr)):
                        pv4 = PS[bank][:npt, :].rearrange("p (h a j) -> p h a j", h=4, a=2)
                        t1v = t1_[:npt, :].rearrange("p (h a j) -> p h a j", h=4, a=2)
                        t2v = t2_[:npt, :].rearrange("p (h a j) -> p h a j", h=4, a=2)
                        cosb = rope[:npt, 0, n, :].unsqueeze(1).unsqueeze(1).to_broadcast([npt, 4, 2, 64])
                        sinb = rope[:npt, 1, n, :].unsqueeze(1).to_broadcast([npt, 4, 64])
                        nsinb = rope[:npt, 2, n, :].unsqueeze(1).to_broadcast([npt, 4, 64])
                        V(lambda e: e.tensor_tensor(out=t1v, in0=pv4, in1=cosb, op=ALU.mult), r=[bPS[bank], b_t1], w=[bt1])
                        V(lambda e: e.tensor_tensor(out=t2v[:, :, 0, :], in0=pv4[:, :, 1, :], in1=nsinb, op=ALU.mult),
                          r=[bPS[bank], b_t1], w=[bt2])
                        V(lambda e: e.tensor_tensor(out=t2v[:, :, 1, :], in0=pv4[:, :, 0, :], in1=sinb, op=ALU.mult),
                          r=[bPS[bank], b_t1], w=[bt2])
                        V(lambda e: e.tensor_tensor(out=out_[:npt, :], in0=t1_[:npt, :], in1=t2_[:npt, :], op=ALU.add),
                           r=[bt1, bt2], w=[bo])
                    if _step <= 2:
                        continue
                    A(lambda e: e.copy(out=vb[:npt, :], in_=PS[2][:npt, :]), r=[bPS[2]], w=[b_vb])
                    if not is_s:
                        V(lambda e: e.tensor_tensor(
                            out=vz[:, :].rearrange("p (h e) -> p h e", h=4),
                            in0=PS[2][:, :].rearrange("p (h e) -> p h e", h=4),
                            in1=zetap[:, :].unsqueeze(2).to_broadcast([128, 4, 128]), op=ALU.mult),
                          r=[bPS[2], b_t1], w=[b_vz])
                    A(lambda e: e.activation(out=sg_[:npt, :], in_=PS[3][:npt, :], func=AF.Silu), r=[bPS[3]], w=[b_sg])
                    pv4b = ps_bf(4)
                    pv5b = ps_bf(5)
                    for h in range(4):
                        T(lambda e: e.transpose(out=pv4b[:, h * 128:h * 128 + npt], in_=qr[:npt, h * 128:(h + 1) * 128],
                                                identity=identb[:npt, :npt]), r=[b_qr, b_const], w=[bPS[4]])
                    for h in range(4):
                        T(lambda e: e.transpose(out=pv5b[:, h * 128:h * 128 + npt], in_=kr[:npt, h * 128:(h + 1) * 128],
                                                identity=identb[:npt, :npt]), r=[b_kr, b_const], w=[bPS[5]])
                    q4 = pv4b[:, 0:512].rearrange("p (h t) -> p h t", h=4)[:, :, 0:npt]
                    k4 = pv5b[:, 0:512].rearrange("p (h t) -> p h t", h=4)[:, :, 0:npt]
                    A(lambda e: e.copy(out=qT[:, :, 0:npt], in_=q4), r=[bPS[4]], w=[b_qT])
                    xiv = (xi[:, 0:512].rearrange("p (h t) -> p h t", h=4) if not is_s
                           else xi[:, 512:768].rearrange("p (h t) -> p h t", h=4))
                    V(lambda e: e.tensor_tensor(out=qxT[:, :, 0:npt], in0=q4, in1=xiv, op=ALU.mult),
                      r=[bPS[4], b_t1], w=[b_qxT])
                    A(lambda e: e.copy(out=kT[:, :, 0:npt], in_=k4), r=[bPS[5]], w=[b_kT])
                    if _step <= 3:
                        continue
                    for h in range(4):
                        T(lambda e: e.matmul(PS[6][:npt, h * 128:h * 128 + npt], lhsT=kT[:, h, 0:npt], rhs=qT[:, h, 0:npt],
                                             start=True, stop=True), r=[b_kT, b_qT], w=[bPS[6]])
                    dmv = (dmp[:, :].rearrange("p (h t) -> p h t", h=4) if not is_s
                           else dms[:, :].rearrange("p (h t) -> p h t", h=4))
                    V(lambda e: e.tensor_tensor(out=sT[:npt, :, 0:npt],
                                                in0=PS[6][:npt, :].rearrange("p (h t) -> p h t", h=4)[:, :, 0:npt],
                                                in1=dmv, op=ALU.mult), r=[bPS[6], b_t1], w=[b_sT])
                    if _step <= 4:
                        continue
                    if ti_ + 1 < len(_tiles):
                        p1b_norm(_tiles[ti_ + 1])
                    for h in range(4):
                        only = (n == 0)
                        T(lambda e: e.matmul(po(h), lhsT=sT[:npt, h, 0:npt],
                                             rhs=vb[:npt, h * 128:(h + 1) * 128], start=True, stop=only),
                          r=[b_sT, b_vb], w=[bPS[pob[h]]])
                        if (not is_s) and n > 0:
                            T(lambda e: e.matmul(po(h), lhsT=qxT[:, h, 0:npt],
                                                 rhs=Sbf[:, h, :], start=False, stop=True),
                              r=[b_qxT, b_Sbf], w=[bPS[4]])
                    if not is_s:
                        for h in range(4):
                            T(lambda e: e.matmul(PS[5][:, h * 128:(h + 1) * 128], lhsT=kr[:, h * 128:(h + 1) * 128],
                                                 rhs=vz[:, h * 128:(h + 1) * 128], start=True, stop=True),
                              r=[b_kr, b_vz], w=[bPS[5]])
                        for h in range(4):
                            V(lambda e: e.scalar_tensor_tensor(out=Sst[:, h, :], in0=Sst[:, h, :], scalar=GC_P[h],
                                                               op0=ALU.mult, in1=PS[5][:, h * 128:(h + 1) * 128],
                                                               op1=ALU.add), r=[b_Sst, bPS[5]], w=[b_Sst])
                        A(lambda e: e.copy(out=Sbf[:], in_=Sst[:]), r=[b_Sst], w=[b_Sbf])
                        if n == NTP - 1:
                            S.dma("sp", O["o_ret_p"].rearrange("h d e -> d h e"), Sst[:], reads=[b_Sst])
                    else:
                        S.dma("sp", S0[0][:], I["sret"][0].rearrange("h d e -> d h e"), writes=[bS0[0]])
                        for b in range(16):
                            sl = b % 2
                            if b + 1 < 16:
                                S.dma("sp", S0[1 - sl][:], I["sret"][b + 1].rearrange("h d e -> d h e"), writes=[bS0[1 - sl]])
                            A(lambda e: e.copy(out=S0b[sl][:], in_=S0[sl][:]), r=[bS0[sl]], w=[bS0b[sl]])
                            V(lambda e: e.tensor_tensor(
                                out=qxm[sl][:], in0=qxT[:, :, 0:64],
                                in1=cmask[:, b * 64:(b + 1) * 64].unsqueeze(1).to_broadcast([128, 4, 64]), op=ALU.mult),
                              r=[b_qxT, b_t1], w=[bqxm[sl]])
                            for h in range(4):
                                T(lambda e: e.matmul(po(h), lhsT=qxm[sl][:, h, :],
                                                     rhs=S0b[sl][:, h, :], start=False, stop=(b == 15)),
                                  r=[bqxm[sl], bS0b[sl]], w=[bPS[pob[h]]])
                            V(lambda e: e.tensor_tensor(
                                out=vzb[sl][:, :].rearrange("p (h e) -> p h e", h=4),
                                in0=PS[2][:64, :].rearrange("p (h e) -> p h e", h=4),
                                in1=zs[:, b * 4:(b + 1) * 4].unsqueeze(2).to_broadcast([64, 4, 128]), op=ALU.mult),
                              r=[bPS[2], b_t1], w=[bvzb[sl]])
                            kvb = 5 if sl == 0 else 0
                            for h in range(4):
                                T(lambda e: e.matmul(PS[kvb][:, h * 128:(h + 1) * 128], lhsT=kr[:64, h * 128:(h + 1) * 128],
                                                     rhs=vzb[sl][:, h * 128:(h + 1) * 128], start=True, stop=True),
                                  r=[b_kr, bvzb[sl]], w=[bPS[kvb]])
                            for h in range(4):
                                V(lambda e: e.scalar_tensor_tensor(out=Sn[sl][:, h, :], in0=S0[sl][:, h, :], scalar=GC_S[h],
                                                                   op0=ALU.mult, in1=PS[kvb][:, h * 128:(h + 1) * 128],
                                                                   op1=ALU.add), r=[bS0[sl], bPS[kvb]], w=[bSn[sl]])
                            S.dma("sp", O["o_ret_s"][b].rearrange("h d e -> d h e"), Sn[sl][:], reads=[bSn[sl]])
                    if _step <= 5:
                        continue
                    if ti_ + 1 < len(_tiles):
                        p1b_proj(_tiles[ti_ + 1])
                    for h in range(4):
                        V(lambda e: e.bn_stats(out=stats[:npt, h, :], in_=po(h)),
                          r=[bPS[pob[h]]], w=[b_st])
                    for h in range(4):
                        V(lambda e: e.bn_aggr(out=mv[:npt, h, :], in_=stats[:npt, h, :]), r=[b_st], w=[b_st])
                    A(lambda e: e.activation(out=rs4[:npt, :], in_=mv[:npt, :, 1], func=AF.Sqrt, scale=1.0,
                                             bias=epsc[:npt, :]), r=[b_st, b_const], w=[b_st])
                    V(lambda e: e.reciprocal(out=rs4[:npt, :], in_=rs4[:npt, :]), r=[b_st], w=[b_st])
                    V(lambda e: e.scalar_tensor_tensor(out=nb4[:npt, :], in0=mv[:npt, :, 0], scalar=-1.0, op0=ALU.mult,
                                                       in1=rs4[:npt, :], op1=ALU.mult), r=[b_st], w=[b_st])
                    for h in range(4):
                        A(lambda e: e.activation(out=on[:npt, h * 128:(h + 1) * 128], in_=po(h),
                                                 func=AF.Identity, scale=rs4[:npt, h:h + 1], bias=nb4[:npt, h:h + 1]),
                          r=[bPS[pob[h]], b_st], w=[b_on])
                    V(lambda e: e.tensor_tensor(out=ret[:npt, :], in0=on[:npt, :], in1=sg_[:npt, :], op=ALU.mult),
                       r=[b_on, b_sg], w=[b_ret])
                    if _step <= 6:
                        continue
                    pv6b = ps_bf(6)
                    for h in range(4):
                        T(lambda e: e.transpose(out=pv6b[:, h * 128:h * 128 + npt], in_=ret[:npt, h * 128:(h + 1) * 128],
                                                identity=identb[:npt, :npt]), r=[b_ret, b_const], w=[bPS[6]])
                    A(lambda e: e.copy(out=retT[:, :, 0:npt],
                                       in_=pv6b[:, 0:512].rearrange("p (h t) -> p h t", h=4)[:, :, 0:npt]),
                      r=[bPS[6]], w=[b_retT])
                    if _step <= 7:
                        continue
                    bi_ = min(n // 4, 4)
                    for half in range(2):
                        bank = 6 + half
                        for kt in range(8):
                            lh = ssmT[:, kt, tok0:tok0 + npt] if kt < 4 else retT[:, kt - 4, 0:npt]
                            T(lambda e: e.matmul(PS[bank][:npt, :], lhsT=lh, rhs=wout[:, kt, half * 512:(half + 1) * 512],
                                                 start=(kt == 0), stop=(kt == 7)),
                              r=[b_ssmT[bi_], b_retT, b_wout], w=[bPS[bank]])
                        resid_add(n, npt, half, bank)
                S.barrier()
            if dbg:
                for n in range(NT):
                    S.dma("sp", O["dbg_x"][:, n, :], x[:, n, :], reads=[bx[n]])
            if stage <= 2:
                S.barrier()
                S.run_block()
                nck.__exit__(None, None, None)
                return nc

            with ExitStack() as s2:
                gx = alloc(s2, "gx", [128, 8])
                gmem = alloc(s2, "gmem", [128, 8])
                ones = alloc(s2, "ones", [128, 128], BF16)
                b_t2 = Buf("tab2")
                S.dma("sp", gx[:], I["g_xattn"].rearrange("(k p) -> p k", p=128), writes=[b_t2])
                S.dma("sp", gmem[:], I["g_mem"].rearrange("(k p) -> p k", p=128), writes=[b_t2])
                V(lambda e: e.memset(ones[:], 1.0), w=[b_t2])
                KT = alloc(s2, "KT", [128, 8, MEM], BF16)
                Vm = alloc(s2, "Vm", [128, 2, D], BF16)
                b_KT, b_Vm = Buf("KT"), Buf("Vm")
                wmq = alloc(s2, "wmq", [128, 8, D], BF16)
                b_wmq, b_wmo = Buf("wmq", S.GW[2]), Buf("wmo", S.GW[3])
                with ExitStack() as s2a:
                    wmk = alloc(s2a, "wmk", [128, 8, D], BF16)
                    wmv = alloc(s2a, "wmv", [128, 8, D], BF16)
                    b_wmk, b_wmv = Buf("wmk", S.GW[0]), Buf("wmv", S.GW[1])
                    load_w_bf16(wmk, b_wmk, I["w_mk"], 8, D, 0)
                    load_w_bf16(wmv, b_wmv, I["w_mv"], 8, D, 0)
                    load_w_bf16(wmq, b_wmq, I["w_mq"], 8, D, 0)
                    mx = [alloc(s2a, "mx%d" % i, [128, D]) for i in range(2)]
                    bmx = [Buf("mx%d" % i, S.GL[i]) for i in range(2)]
                    mhT = alloc(s2a, "mhT", [128, 8, MEM], BF16)
                    b_mhT = Buf("mhT")
                    mo = [alloc(s2a, "mo%d" % i, [128, D]) for i in range(2)]
                    bmo = [Buf("mo%d" % i, S.GS[i]) for i in range(2)]
                    _k2a = int(_os.environ.get("K2A", "9"))
                    for mt in range(2):
                        S.dma("sp", mx[mt][:], I["memp"][mt * 128:(mt + 1) * 128, :], writes=[bmx[mt]])
                        if _k2a >= 1:
                            rmsnorm_hT(mx[mt][:, :], bmx[mt], 128, gmem[:], mhT, b_mhT, scrB, mt * 128, None,
                                       ln=True, bg=b_t2)
                    oi = 0
                    for (wm, bwm, oname, isv) in ((wmk, b_wmk, "o_mk", False), (wmv, b_wmv, "o_mv", True)) if _k2a >= 2 else ():
                        for mt in range(2):
                            sl = oi % 2
                            oi += 1
                            for half in range(2):
                                bank = half
                                for kt in range(8):
                                    T(lambda e: e.matmul(PS[bank][:, :], lhsT=mhT[:, kt, mt * 128:(mt + 1) * 128],
                                                         rhs=wm[:, kt, half * 512:(half + 1) * 512], start=(kt == 0),
                                                         stop=(kt == 7)), r=[b_mhT, bwm], w=[bPS[bank]])
                                A(lambda e: e.copy(out=mo[sl][:, half * 512:(half + 1) * 512], in_=PS[bank][:, :]),
                                  r=[bPS[bank]], w=[bmo[sl]])
                                if isv:
                                    V(lambda e: e.tensor_copy(out=Vm[:, mt, half * 512:(half + 1) * 512], in_=PS[bank][:, :]),
                                      r=[bPS[bank]], w=[b_Vm])
                            S.dma("sp", O[oname][mt * 128:(mt + 1) * 128, :], mo[sl][:], reads=[bmo[sl]])
                    for j in range(8 if _k2a >= 3 else 0):
                        bank = 2 + (j % 2)
                        for kt in range(8):
                            T(lambda e: e.matmul(PS[bank][:, 0:MEM], lhsT=wmk[:, kt, j * 128:(j + 1) * 128],
                                                 rhs=mhT[:, kt, :], start=(kt == 0), stop=(kt == 7)),
                              r=[b_mhT, b_wmk], w=[bPS[bank]])
                        A(lambda e: e.copy(out=KT[:, j, :], in_=PS[bank][:, 0:MEM]), r=[bPS[bank]], w=[b_KT])
                    S.barrier()
                wmo = alloc(s2, "wmo", [128, 8, D], BF16)
                load_w_bf16(wmo, b_wmo, I["w_mo"], 8, D, 0)
                hT4s = [alloc(s2, "hT4_%d" % i, [128, 8, 512], BF16) for i in range(2)]
                b_hT4s = [Buf("hT4_%d" % i) for i in range(2)]
                qm4 = alloc(s2, "qm4", [128, 8, 512], BF16)
                oT4 = alloc(s2, "oT4", [128, 8, 512], BF16)
                eT4 = [alloc(s2, "eT4_%d" % i, [128, 2, 512], BF16) for i in range(2)]
                rdn4 = [alloc(s2, "rdn4_%d" % i, [128, 512]) for i in range(2)]
                b_qm4, b_oT4 = Buf("qm4"), Buf("oT4")
                b_eT4 = [Buf("eT4_%d" % i) for i in range(2)]
                b_rdn4 = [Buf("rdn4_%d" % i) for i in range(2)]
                Kb = [alloc(s2, "Kb%d" % i, [128, 2, D]) for i in range(2)]
                bKb = [Buf("Kb%d" % i, S.GL[i]) for i in range(2)]
                KbT = [alloc(s2, "KbT%d" % i, [128, 8, MEM], BF16) for i in range(2)]
                bKbT = [Buf("KbT%d" % i) for i in range(2)]
                Vb = [alloc(s2, "Vb%d" % i, [128, 2, D], BF16) for i in range(2)]
                bVb = [Buf("Vb%d" % i, S.GW[i]) for i in range(2)]
                eTs = alloc(s2, "eTs", [128, 2, 4, 64], BF16)
                b_eTs = Buf("eTs")
                qrot = [0]

                def q_proj(nc_, hT4, b_hT4):
                    for j in range(8):
                        bank = 5 + (qrot[0] % 3)
                        qrot[0] += 1
                        for kt in range(8):
                            T(lambda e: e.matmul(PS[bank][:, 0:nc_], lhsT=wmq[:, kt, j * 128:(j + 1) * 128],
                                                 rhs=hT4[:, kt, 0:nc_], start=(kt == 0), stop=(kt == 7)),
                              r=[b_wmq, b_hT4], w=[bPS[bank]])
                        A(lambda e: e.activation(out=qm4[:, j, 0:nc_], in_=PS[bank][:, 0:nc_], func=AF.Copy,
                                                 scale=1.0 / 16.0), r=[bPS[bank]], w=[b_qm4])

                def w_mo_resid(n, npt, c0):
                    for half in range(2):
                        bank = 5 + (qrot[0] % 3)
                        qrot[0] += 1
                        for j in range(8):
                            T(lambda e: e.matmul(PS[bank][:npt, :], lhsT=oT4[:, j, c0:c0 + npt],
                                                 rhs=wmo[:, j, half * 512:(half + 1) * 512], start=(j == 0), stop=(j == 7)),
                              r=[b_oT4, b_wmo], w=[bPS[bank]])
                        resid_add(n, npt, half, bank)

                def p2_norms(bi):
                    hT4, b_hT4 = hT4s[bi % 2], b_hT4s[bi % 2]
                    if bi < 4:
                        for ti in range(4):
                            n = bi * 4 + ti
                            rmsnorm_hT(x[:, n, :], bx[n], 128, gx[:], hT4, b_hT4, scrB, ti * 128, None, ln=True, bg=b_t2)
                    else:
                        rmsnorm_hT(x[:TS, 16, :], bx[16], TS, gx[:], hT4, b_hT4, scrB, 0, None, ln=True, bg=b_t2)

                p2_norms(0)
                for bi in range(4):
                    q_proj(512, hT4s[bi % 2], b_hT4s[bi % 2])
                    for h in range(4):
                        par = h % 2
                        for mt in range(2):
                            bank = mt
                            for dt_ in range(2):
                                T(lambda e: e.matmul(PS[bank][:, :], lhsT=KT[:, h * 2 + dt_, mt * 128:(mt + 1) * 128],
                                                     rhs=qm4[:, h * 2 + dt_, :], start=(dt_ == 0), stop=(dt_ == 1)),
                                  r=[b_KT, b_qm4], w=[bPS[bank]])
                            A(lambda e: e.activation(out=eT4[par][:, mt, :], in_=PS[bank][:, :], func=AF.Exp),
                              r=[bPS[bank]], w=[b_eT4[par]])
                        for mt in range(2):
                            T(lambda e: e.matmul(PS[2][:, :], lhsT=ones[:, :], rhs=eT4[par][:, mt, :], start=(mt == 0),
                                                 stop=(mt == 1)), r=[b_t2, b_eT4[par]], w=[bPS[2]])
                        A(lambda e: e.activation(out=rdn4[par][:, :], in_=PS[2][:, :], func=AF.Ln), r=[bPS[2]], w=[b_rdn4[par]])
                        A(lambda e: e.activation(out=rdn4[par][:, :], in_=rdn4[par][:, :], func=AF.Exp, scale=-1.0),
                          r=[b_rdn4[par]], w=[b_rdn4[par]])
                        for dt_ in range(2):
                            bank = 3 + dt_
                            j = h * 2 + dt_
                            for mt in range(2):
                                T(lambda e: e.matmul(PS[bank][:, :], lhsT=Vm[:, mt, j * 128:(j + 1) * 128],
                                                     rhs=eT4[par][:, mt, :], start=(mt == 0), stop=(mt == 1)),
                                  r=[b_Vm, b_eT4[par]], w=[bPS[bank]])
                            V(lambda e: e.tensor_tensor(out=oT4[:, j, :], in0=PS[bank][:, :], in1=rdn4[par][:, :], op=ALU.mult),
                              r=[bPS[bank], b_rdn4[par]], w=[b_oT4])
                    p2_norms(bi + 1)
                    for ti in range(4):
                        w_mo_resid(bi * 4 + ti, 128, ti * 128)
                n = 16
                q_proj(TS, hT4s[0], b_hT4s[0])
                rden_s = rdn4[0][:, 0:256].rearrange("p (h t) -> p h t", h=4)
                for b in range(16):
                    sl = b % 2
                    S.dma("sp", Kb[sl][:], I["ck"][b].rearrange("(mt p) d -> p mt d", p=128), writes=[bKb[sl]])
                    for q4 in range(4):
                        bank = 2 + (q4 % 2)
                        for i4 in range(4):
                            idx = q4 * 4 + i4
                            j, mt = idx // 2, idx % 2
                            T(lambda e: e.transpose(out=PS[bank][:, i4 * 128:(i4 + 1) * 128],
                                                    in_=Kb[sl][:, mt, j * 128:(j + 1) * 128], identity=identf[:]),
                              r=[bKb[sl], b_const], w=[bPS[bank]])
                        A(lambda e: e.copy(
                            out=KbT[sl][:, 2 * q4:2 * q4 + 2, :].rearrange("p j (m t) -> p j m t", m=2),
                            in_=PS[bank][:, :].rearrange("p (j m t) -> p j m t", j=2, m=2)),
                          r=[bPS[bank]], w=[bKbT[sl]])
                    for h in range(4):
                        for mt in range(2):
                            c0 = mt * 256 + h * 64 + 4 * b
                            for dt_ in range(2):
                                T(lambda e: e.matmul(PS[4][:, c0:c0 + 4],
                                                     lhsT=KbT[sl][:, h * 2 + dt_, mt * 128:(mt + 1) * 128],
                                                     rhs=qm4[:, h * 2 + dt_, 4 * b:4 * b + 4], start=(dt_ == 0),
                                                     stop=(dt_ == 1)), r=[bKbT[sl], b_qm4], w=[bPS[4]])
                A(lambda e: e.activation(out=eTs[:].rearrange("p m h t -> p (m h t)"), in_=PS[4][:, :], func=AF.Exp),
                  r=[bPS[4]], w=[b_eTs])
                for h in range(4):
                    for mt in range(2):
                        T(lambda e: e.matmul(PS[0][:, h * 64:(h + 1) * 64], lhsT=ones[:, :], rhs=eTs[:, mt, h, :],
                                             start=(mt == 0), stop=(mt == 1)), r=[b_t2, b_eTs], w=[bPS[0]])
                V(lambda e: e.reciprocal(out=rden_s, in_=PS[0][:, 0:256].rearrange("p (h t) -> p h t", h=4)),
                  r=[bPS[0]], w=[b_rdn4[0]])
                for b in range(16):
                    sl = b % 2
                    for mt in range(2):
                        S.dma("pool", Vb[sl][:, mt, :], I["cv"][b, mt * 128:(mt + 1) * 128, :], writes=[bVb[sl]])
                    for j in range(8):
                        h = j // 2
                        for mt in range(2):
                            T(lambda e: e.matmul(PS[1][:, j * 64 + 4 * b:j * 64 + 4 * b + 4],
                                                 lhsT=Vb[sl][:, mt, j * 128:(j + 1) * 128],
                                                 rhs=eTs[:, mt, h, 4 * b:4 * b + 4], start=(mt == 0), stop=(mt == 1)),
                              r=[bVb[sl], b_eTs], w=[bPS[1]])
                V(lambda e: e.tensor_tensor(
                    out=oT4[:, :, 0:64].rearrange("p (h a) t -> p h a t", a=2),
                    in0=PS[1][:, :].rearrange("p (h a t) -> p h a t", h=4, a=2),
                    in1=rden_s.unsqueeze(2).to_broadcast([128, 4, 2, 64]), op=ALU.mult),
                  r=[bPS[1], b_rdn4[0]], w=[b_oT4])
                w_mo_resid(16, TS, 0)
                S.barrier()
            if stage <= 3:
                if dbg:
                    for n in range(NT):
                        S.dma("sp", O["dbg_x"][:, n, :], x[:, n, :], reads=[bx[n]])
                S.barrier()
                S.run_block()
                nck.__exit__(None, None, None)
                return nc

            with ExitStack() as s3:
                gml = alloc(s3, "gml", [128, 8])
                b_t3 = Buf("tab3")
                S.dma("sp", gml[:], I["g_mlp"].rearrange("(k p) -> p k", p=128), writes=[b_t3])
                hTa = alloc(s3, "hTa", [128, 8, NTOK], BF16)
                b_hTa = [Buf("hTa%d" % n) for n in range(NT)]
                wup = [alloc(s3, "wup%d" % i, [128, 8, 512], BF16) for i in range(2)]
                wdn = [alloc(s3, "wdn%d" % i, [128, 4, D], BF16) for i in range(2)]
                bwup = [Buf("wup%d" % i, S.GW[i]) for i in range(2)]
                bwdn = [Buf("wdn%d" % i, S.GW[2 + i]) for i in range(2)]
                rl = [alloc(s3, "rl%d" % i, [128, 512]) for i in range(2)]
                brl = [Buf("rl%d" % i) for i in range(2)]
                aT = [alloc(s3, "aT%d" % i, [128, 4, 512], BF16) for i in range(2)]
                baT = [Buf("aT%d" % i) for i in range(2)]

                def load_fc(fc):
                    sl = fc % 2
                    for kt in range(8):
                        S.dma("pool", wup[sl][:, kt, :], I["w_up"][kt * 128:(kt + 1) * 128, fc * 512:(fc + 1) * 512],
                              writes=[bwup[sl]])
                    for ft in range(4):
                        S.dma("pool", wdn[sl][:, ft, :], I["w_down"][fc * 512 + ft * 128:fc * 512 + (ft + 1) * 128, :],
                              writes=[bwdn[sl]])
                load_fc(0)
                scrB["pb"] = [7, 6]
                for n in range(NT):
                    npt = TS if n == 16 else 128
                    rmsnorm_hT(x[:npt, n, :], bx[n], npt, gml[:], hTa, b_hTa[n], scrB, n * 128, None, ln=True, bg=b_t3)
                gf = alloc(s3, "gf", [128, D])
                b_gf = Buf("gf")
                S.dma("sp", gf[:], I["g_final"].rearrange("(o d) -> o d", o=1).partition_broadcast(128), writes=[b_gf])
                yst = [alloc(s3, "yst%d" % i, [128, D]) for i in range(3)]
                byst = [Buf("yst%d" % i, S.GS[i]) for i in range(3)]

                def final_norm(n):
                    npt = TS if n == 16 else 128
                    sl = n % 3
                    k4 = n % 2
                    sq, ss, rstd, bscr = scrB["sq"][k4], scrB["ss"][k4], scrB["rstd"][k4], scrB["ba"][k4]
                    A(lambda e: e.activation(out=sq[:npt, :], in_=x[:npt, n, :], func=AF.Square, accum_out=ss[:npt, :]),
                      r=[bx[n]], w=[bscr])
                    A(lambda e: e.activation(out=rstd[:npt, :], in_=ss[:npt, :], func=AF.Ln, scale=1.0 / D,
                                             bias=epsc[:npt, :]), r=[bscr, b_const], w=[bscr])
                    A(lambda e: e.activation(out=rstd[:npt, :], in_=rstd[:npt, :], func=AF.Exp, scale=-0.5),
                      r=[bscr], w=[bscr])
                    V(lambda e: e.scalar_tensor_tensor(out=yst[sl][:npt, :], in0=x[:npt, n, :], scalar=rstd[:npt, :],
                                                       op0=ALU.mult, in1=gf[:npt, :], op1=ALU.mult),
                      r=[bx[n], bscr, b_gf], w=[byst[sl]])
                    if n < 16:
                        S.dma("sp", O["yp"][n * 128:(n + 1) * 128, :], yst[sl][:, :], reads=[byst[sl]])
                    else:
                        S.dma("sp", O["ys"][:, :], yst[sl][:TS, :], reads=[byst[sl]])

                blocks3 = [(i * 512, 512) for i in range(4)] + [(SEQ, TS)]
                items = [(fc, blk) for fc in range(8) for blk in blocks3]
                ctr = {"ri": 0, "di": 0}
                load_fc(1)

                def mlp_up(i):
                    fc, (t0, nn) = items[i]
                    sl, asl = fc % 2, i % 2
                    tiles = list(range(t0 // 128, t0 // 128 + (nn + 127) // 128))
                    for ft in range(4):
                        bank = ft
                        for kt in range(8):
                            T(lambda e: e.matmul(PS[bank][:, 0:nn], lhsT=wup[sl][:, kt, ft * 128:(ft + 1) * 128],
                                                 rhs=hTa[:, kt, t0:t0 + nn], start=(kt == 0), stop=(kt == 7)),
                              r=[bwup[sl]] + [b_hTa[t] for t in tiles], w=[bPS[bank]])
                        rsl = ctr["ri"] % 2
                        ctr["ri"] += 1
                        A(lambda e: e.activation(out=rl[rsl][:, 0:nn], in_=PS[bank][:, 0:nn], func=AF.Relu),
                          r=[bPS[bank]], w=[brl[rsl]])
                        V(lambda e: e.tensor_tensor(out=aT[asl][:, ft, 0:nn], in0=rl[rsl][:, 0:nn], in1=rl[rsl][:, 0:nn],
                                                    op=ALU.mult), r=[brl[rsl]], w=[baT[asl]])

                def mlp_down(i):
                    fc, (t0, nn) = items[i]
                    sl, asl = fc % 2, i % 2
                    tiles = list(range(t0 // 128, t0 // 128 + (nn + 127) // 128))
                    for ti, tl in enumerate(tiles):
                        npt = TS if tl == 16 else 128
                        for half in range(2):
                            bank = 4 + (ctr["di"] % 4)
                            ctr["di"] += 1
                            for ft in range(4):
                                T(lambda e: e.matmul(PS[bank][:npt, :], lhsT=aT[asl][:, ft, ti * 128:ti * 128 + npt],
                                                     rhs=wdn[sl][:, ft, half * 512:(half + 1) * 512], start=(ft == 0),
                                                     stop=(ft == 3)), r=[baT[asl], bwdn[sl]], w=[bPS[bank]])
                            resid_add(tl, npt, half, bank)
                        if fc == 7:
                            final_norm(tl)

                mlp_up(0)
                for i in range(len(items)):
                    if i + 1 < len(items):
                        mlp_up(i + 1)
                    mlp_down(i)
                    fc = items[i][0]
                    if (i + 1 == len(items) or items[i + 1][0] != fc) and fc + 2 < 8:
                        load_fc(fc + 2)
                S.barrier()
            if dbg:
                for n in range(NT):
                    S.dma("sp", O["dbg_x"][:, n, :], x[:, n, :], reads=[bx[n]])
            S.barrier()
            S.run_block()
            nck.__exit__(None, None, None)
    return nc


_NC = None


def kernel(**inputs):
    global _NC
    if _NC is None:
        _NC = build()
    maps = _in_maps(inputs)
    res = run_bass_kernel_spmd(_NC, maps, core_ids=list(range(8)))
    R = res.results
    f = np.float32

    def cat(name, shape=None):
        return np.stack([np.asarray(R[c][name], f) for c in range(8)])
    y_prompt = cat("yp")
    y_sample = cat("ys").reshape(128, 4, D)
    s5r_p = cat("o_s5r_p")[None]
    s5i_p = cat("o_s5i_p")[None]
    ret_p = cat("o_ret_p")[None]
    mk_p = cat("o_mk").reshape(8, MEM, 4, 256)[None]
    mv_p = cat("o_mv").reshape(8, MEM, 4, 256)[None]
    s5r_s = cat("o_s5r_s").reshape(128, G, 64)[None]
    s5i_s = cat("o_s5i_s").reshape(128, G, 64)[None]
    ret_s = cat("o_ret_s").reshape(128, 4, 128, 128)[None]
    return (y_prompt, y_sample, s5r_p, s5i_p, ret_p, mk_p, mv_p, s5r_s, s5i_s, ret_s)


def _in_maps(inputs):
    cst = _consts()
    f = np.float32
    maps = []
    w = {}
    for k in W_NAMES:
        a = np.asarray(inputs[k], f)
        if k != "g_final":
            a = a[0]
        w[k] = np.ascontiguousarray(a.reshape(W_SHAPES[k]))
    for c in range(8):
        m = dict(w)
        m.update(cst)
        b0 = 16 * c
        m["xp"] = np.ascontiguousarray(np.asarray(inputs["x_prompt"], f)[c])
        m["xs"] = np.ascontiguousarray(np.asarray(inputs["x_sample"], f)[b0:b0 + 16].reshape(TS, D))
        m["memp"] = np.ascontiguousarray(np.asarray(inputs["mem_prompt"], f)[c])
        m["s5r"] = np.ascontiguousarray(np.asarray(inputs["state_s5_re"], f)[0, b0:b0 + 16].reshape(512, 64))
        m["s5i"] = np.ascontiguousarray(np.asarray(inputs["state_s5_im"], f)[0, b0:b0 + 16].reshape(512, 64))
        m["sret"] = np.ascontiguousarray(np.asarray(inputs["state_ret"], f)[0, b0:b0 + 16])
        m["ck"] = np.ascontiguousarray(np.asarray(inputs["cache_mem_k"], f)[0, b0:b0 + 16].reshape(16, MEM, D))
        m["cv"] = np.ascontiguousarray(np.asarray(inputs["cache_mem_v"], f)[0, b0:b0 + 16].reshape(16, MEM, D))
        maps.append(m)
    return maps
```

```python
import numpy as np
import concourse.bass as bass
import concourse.mybir as mybir
from concourse.bass_utils import run_bass_kernel_spmd
from contextlib import ExitStack

F32 = mybir.dt.float32
BF16 = mybir.dt.bfloat16
AF = mybir.ActivationFunctionType
ALU = mybir.AluOpType

D = 1024
SEQ = 2048
NTP = 16
TS = 64
NT = 17
NTOK = SEQ + TS
G = 32
DFF = 4096
MEM = 256
EPS = 1e-6
PAST = 16384.0
MAGIC = 12582912.0
TWO_PI = float(2.0 * np.pi)
ML = [7, 6, 5, 4, 3, 2, 1, 0, 1, 2, 3, 4, 5, 6, 7, 8, -4, 0.5]
K1 = len(ML)
I_A1, I_A8, I_A4, I_AM4, I_HALF = 8, 15, 3, 16, 17
GAM = [1.0 - 2.0 ** (-5.0 - h) for h in range(4)]


class Grp:
    __slots__ = ("sem", "cnt", "sealed")


class Buf:
    __slots__ = ("w", "r", "name", "grp", "ps")

    def __init__(self, name="", grp=None, ps=False):
        self.w = None
        self.r = []
        self.name = name
        self.grp = grp
        self.ps = ps


class _Rec:
    def __init__(self):
        self.call = None

    def __getattr__(self, name):
        def f(*a, **kw):
            self.call = (name, a, kw)
            return self
        return f


class Sched:
    ENG = ("pe", "dve", "act", "pool", "sp")

    def __init__(self, nc, stack, self_sync=("dve", "act", "pool")):
        self.nc = nc
        self.stack = stack
        self.prog = {k: [] for k in self.ENG}
        self.cnt = {k: 0 for k in self.ENG}
        self.waited = {k: {} for k in self.ENG}
        self.sem = {}
        self.nsem = 0
        for k in ("pe", "dve", "act", "pool"):
            self.sem[k] = self.new_sem("c_" + k)
        self.self_sync = set(self_sync)
        self.groups = []
        self.GC = self.group("gc")
        self.GP = self.group("gp")
        self.GW = [self.group("gw%d" % i) for i in range(4)]
        self.GX = self.group("gx")
        self.GL = [self.group("gl%d" % i) for i in range(2)]
        self.GS = [self.group("gs%d" % i) for i in range(3)]

    def group(self, name):
        g = Grp()
        g.sem = self.new_sem(name)
        g.cnt = 0
        g.sealed = False
        self.groups.append(g)
        return g

    def new_sem(self, name):
        self.nsem += 1
        assert self.nsem < 98, "too many semaphores"
        return self.stack.enter_context(self.nc.semaphore(name + "_%d" % self.nsem))

    def _waits(self, eng, deps):
        w = self.waited[eng]
        need = {}
        dd = []
        for d in deps:
            if isinstance(d, Grp):
                d.sealed = True
                dd.append((d.sem, d.cnt))
            else:
                dd.append(d)
        deps = dd
        for (s, v) in deps:
            if eng in self.sem and s is self.sem[eng] and eng not in self.self_sync:
                continue
            k = id(s)
            if w.get(k, 0) >= v:
                continue
            if k not in need or need[k][1] < v:
                need[k] = (s, v)
        for k, (s, v) in need.items():
            w[k] = v
            self.prog[eng].append(lambda e, s=s, v=v: e.wait_ge(s, v))

    def op(self, eng, fn, reads=(), writes=()):
        deps = []
        for b in reads:
            if b.w is not None:
                deps.append(b.w)
            if b.ps:
                mys = self.sem[eng]
                deps.extend(d for d in b.r if not (isinstance(d, tuple) and d[0] is mys))
        for b in writes:
            if b.w is not None:
                deps.append(b.w)
            deps.extend(b.r)
        self._waits(eng, deps)
        self.cnt[eng] += 1
        c = self.cnt[eng]
        s = self.sem[eng]
        rec = _Rec()
        fn(rec)
        name, a, kw = rec.call
        self.prog[eng].append(lambda e, name=name, a=a, kw=kw, s=s: getattr(e, name)(*a, **kw).then_inc(s, 1))
        for b in reads:
            b.r.append((s, c))
        for b in writes:
            b.w = (s, c)
            b.r = []

    def dma(self, q, out, in_, reads=(), writes=(), **kw):
        tb = writes[0] if writes else reads[0]
        g = tb.grp
        if g is None:
            g = self.GP if q == "pool" else (self.GC if writes else self.GS[0])
        deps = []
        for b in reads:
            if b.w is not None:
                deps.append(b.w)
        for b in writes:
            if b.w is not None and b.w is not g:
                deps.append(b.w)
            deps.extend(b.r)
        self._waits(q, deps)
        if g.sealed and g.cnt > 0:
            self._waits(q, [(g.sem, g.cnt)])
        g.sealed = False
        g.cnt += 16
        s = g.sem
        self.prog[q].append(
            lambda e, out=out, in_=in_, s=s, kw=kw: e.dma_start(out=out, in_=in_, **kw).then_inc(s, 16))
        for b in reads:
            b.r.append(g)
        for b in writes:
            b.w = g
            b.r = []

    def barrier(self, engines=None):
        deps = [(self.sem[k], self.cnt[k]) for k in ("pe", "dve", "act", "pool") if self.cnt[k] > 0]
        deps += [g for g in self.groups if g.cnt > 0]
        for e in (engines or self.ENG):
            self._waits(e, deps)

    def run_block(self):
        nc = self.nc
        with nc.Block() as block:
            @block.sync
            def _(e):
                for t in self.prog["sp"]:
                    t(e)

            @block.tensor
            def _(e):
                for t in self.prog["pe"]:
                    t(e)

            @block.vector
            def _(e):
                for t in self.prog["dve"]:
                    t(e)

            @block.scalar
            def _(e):
                for t in self.prog["act"]:
                    t(e)

            @block.gpsimd
            def _(e):
                for t in self.prog["pool"]:
                    t(e)


_CONSTS = None


def _consts():
    global _CONSTS
    if _CONSTS is not None:
        return _CONSTS
    f = np.float32
    c = {}
    c["c_ident"] = np.eye(128, dtype=f)
    m = np.zeros((8, 128, 240), f)
    for a in range(8):
        for i in range(16):
            m[a, 16 * a + i, 112 + i] = 1.0
    c["c_masters"] = m
    ml = np.array(ML, np.float64)
    rows = np.concatenate([ml / (2 * np.pi), ml, 8.0 * (np.arange(64) + 1) / (2 * np.pi)])
    c["c_rows"] = rows.astype(f)[None, :]
    sg = np.zeros((128, 2), f)
    sg[:64, 0] = 1.0
    sg[64:, 0] = -1.0
    sg[:64, 1] = -1.0
    sg[64:, 1] = 1.0
    c["c_sgn"] = sg
    inv = (f(10000.0) ** (-(np.arange(64, dtype=f) / f(64.0)))).astype(f)
    pos = np.zeros((128, NT), f)
    for n in range(NTP):
        pos[:, n] = 128 * n + np.arange(128)
    pos[:64, 16] = PAST + (np.arange(64) % 4)
    ang = (pos[:, :, None] * inv[None, None, :]).astype(f).astype(np.float64)
    c["c_rope"] = np.stack([np.cos(ang), np.sin(ang), -np.sin(ang)]).astype(f)
    lg = np.log(np.array(GAM, np.float64))
    sc = 128.0 ** -0.5
    idx = np.arange(128)
    dm = np.zeros((128, 4, 128), np.float64)
    diff = idx[None, :] - idx[:, None]
    for h in range(4):
        dm[:, h, :] = np.where(diff >= 0, np.exp(np.maximum(diff, 0) * lg[h]), 0.0) * sc
    c["c_dmask_p"] = dm.reshape(128, 512).astype(f)
    ds_ = np.zeros((64, 4, 64), np.float64)
    r = np.arange(64)
    bb = r // 4
    tt = r % 4
    same = bb[:, None] == bb[None, :]
    dts = tt[None, :] - tt[:, None]
    for h in range(4):
        ds_[:, h, :] = np.where(same & (dts >= 0), np.exp(np.maximum(dts, 0) * lg[h]), 0.0) * sc
    c["c_dmask_s"] = ds_.reshape(64, 256).astype(f)
    xi_p = np.stack([np.exp((idx + 1.0) * lg[h]) * sc for h in range(4)])
    xi_s = np.stack([np.exp((tt + 1.0) * lg[h]) * sc for h in range(4)])
    c["c_xi"] = np.concatenate([xi_p.reshape(-1), xi_s.reshape(-1)]).astype(f)[None, :]
    zp = np.stack([np.exp((127.0 - idx) * lg[h]) for h in range(4)], axis=1)
    c["c_zeta_p"] = zp.astype(f)
    zs = np.zeros((64, 16, 4), np.float64)
    for h in range(4):
        for b in range(16):
            zs[:, b, h] = np.where(bb == b, np.exp((3.0 - tt) * lg[h]), 0.0)
    c["c_zs"] = zs.reshape(64, 64).astype(f)
    cm = np.zeros((16, 64), f)
    for b in range(16):
        cm[b, 4 * b:4 * b + 4] = 1.0
    c["c_cmask"] = cm.reshape(1, -1)
    _CONSTS = c
    return c


W_NAMES = ["g_mix", "w_in", "lam_re", "lam_im", "log_dt", "b_re", "b_im", "c_re", "c_im", "d_skip", "w_glu",
           "ret_gn", "w_out", "g_xattn", "g_mem", "w_mq", "w_mk", "w_mv", "w_mo", "g_mlp", "w_up", "w_down",
           "g_final"]
W_SHAPES = {"g_mix": [D], "w_in": [D, 2560], "lam_re": [G, 64], "lam_im": [G, 64], "log_dt": [G],
            "b_re": [G, 64, 16], "b_im": [G, 64, 16], "c_re": [G * 16, 64], "c_im": [G * 16, 64], "d_skip": [512],
            "w_glu": [512, 512], "ret_gn": [512], "w_out": [D, D], "g_xattn": [D], "g_mem": [D], "w_mq": [D, D],
            "w_mk": [D, D], "w_mv": [D, D], "w_mo": [D, D], "g_mlp": [D], "w_up": [D, DFF], "w_down": [DFF, D],
            "g_final": [D]}
IN_SHAPES = {"xp": [SEQ, D], "xs": [TS, D], "memp": [MEM, D], "s5r": [512, 64], "s5i": [512, 64],
             "sret": [16, 4, 128, 128], "ck": [16, MEM, D], "cv": [16, MEM, D]}
OUT_SHAPES = {"yp": [SEQ, D], "ys": [TS, D], "o_s5r_p": [G, 64], "o_s5i_p": [G, 64], "o_ret_p": [4, 128, 128],
              "o_mk": [MEM, D], "o_mv": [MEM, D], "o_s5r_s": [512, 64], "o_s5i_s": [512, 64],
              "o_ret_s": [16, 4, 128, 128]}


def build(stage=99, dbg=False):
    nc = bass.Bass("TRN2", target_bir_lowering=False)
    cst = _consts()
    I = {}
    for k, shp in list(IN_SHAPES.items()) + list(W_SHAPES.items()):
        I[k] = nc.dram_tensor(k, shp, F32, kind="ExternalInput").ap()
    for k, v in cst.items():
        I[k] = nc.dram_tensor(k, list(v.shape), F32, kind="ExternalInput").ap()
    O = {}
    for k, shp in OUT_SHAPES.items():
        O[k] = nc.dram_tensor(k, shp, F32, kind="ExternalOutput").ap()
    if dbg:
        O["dbg_ssm"] = nc.dram_tensor("dbg_ssm", [128, 4, NTOK], F32, kind="ExternalOutput").ap()
        O["dbg_x"] = nc.dram_tensor("dbg_x", [128, NT, D], F32, kind="ExternalOutput").ap()

    with ExitStack() as st:
        S = Sched(nc, st)

        def alloc(stack, name, shape, dt=F32):
            return stack.enter_context(nc.sbuf_tensor(name, shape, dt))

        def palloc(stack, name, shape, dt=F32):
            return stack.enter_context(nc.psum_tensor(name, shape, dt))

        def V(fn, r=(), w=()):
            S.op("dve", fn, reads=r, writes=w)

        def A(fn, r=(), w=()):
            S.op("act", fn, reads=r, writes=w)

        import os as _os0
        _nopool = _os0.environ.get("K_NOPOOL") == "1"

        def PL(fn, r=(), w=()):
            S.op("dve" if _nopool else "pool", fn, reads=r, writes=w)

        def T(fn, r=(), w=()):
            S.op("pe", fn, reads=r, writes=w)

        nck = nc.allow_non_contiguous_dma(reason="small param layout loads")
        nck.__enter__()

        identb = alloc(st, "identb", [128, 128], BF16)
        identf = alloc(st, "identf", [128, 128], F32)
        sgn = alloc(st, "sgn", [128, 2])
        epsc = alloc(st, "epsc", [128, 1])
        ssmT = alloc(st, "ssmT", [128, 4, NTOK], BF16)
        b_const = Buf("const")
        b_ssmT = [Buf("ssmT%d" % i) for i in range(5)]
        b_constp = Buf("constp")
        S.dma("pool", identb[:], I["c_ident"][:, :], writes=[b_constp])
        S.dma("sp", identf[:], I["c_ident"][:, :], writes=[b_const])
        S.dma("sp", sgn[:], I["c_sgn"][:, :], writes=[b_const])
        V(lambda e: e.memset(epsc[:], EPS), r=[b_constp], w=[b_const])
        PS = [palloc(st, "ps%d" % i, [128, 512], F32) for i in range(8)]
        bPS = [Buf("ps%d" % i, ps=True) for i in range(8)]

        def ps_bf(i):
            return PS[i][:].bitcast(BF16)

        def make_scr(stack, tag, pbanks):
            d = {"i": 0, "pb": list(pbanks)}
            d["sq"] = [alloc(stack, "sq%s" % tag, [128, D], BF16)] * 2
            d["ss"] = [alloc(stack, "ss%s%d" % (tag, i), [128, 1]) for i in range(2)]
            d["rstd"] = [alloc(stack, "rstd%s%d" % (tag, i), [128, 1]) for i in range(2)]
            d["hb"] = [alloc(stack, "hb%s%d" % (tag, i), [128, D], BF16) for i in range(2)]
            d["ba"] = [Buf("ba%s%d" % (tag, i)) for i in range(2)]
            d["bh"] = [Buf("bh%s%d" % (tag, i)) for i in range(2)]
            return d

        def rmsnorm_hT(xt_ap, bx, npart, gcol, hT_ap, bhT, scr, col0, ph, ln=False, bg=None, out4=None):
            k = scr["i"] % 2
            pbank = scr["pb"][scr["i"] % len(scr["pb"])]
            scr["i"] += 1
            sq, ss, rstd, hb = scr["sq"][k], scr["ss"][k], scr["rstd"][k], scr["hb"][k]
            ba, bh = scr["ba"][k], scr["bh"][k]
            A(lambda e: e.activation(out=sq[:npart, :], in_=xt_ap, func=AF.Square, accum_out=ss[:npart, :]),
              r=[bx], w=[ba])
            if ln:
                A(lambda e: e.activation(out=rstd[:npart, :], in_=ss[:npart, :], func=AF.Ln, scale=1.0 / D,
                                         bias=epsc[:npart, :]), r=[ba, b_const], w=[ba])
                A(lambda e: e.activation(out=rstd[:npart, :], in_=rstd[:npart, :], func=AF.Exp, scale=-0.5),
                  r=[ba], w=[ba])
            else:
                A(lambda e: e.activation(out=rstd[:npart, :], in_=ss[:npart, :], func=AF.Sqrt, scale=1.0 / D,
                                         bias=epsc[:npart, :]), r=[ba, b_const], w=[ba])
                V(lambda e: e.reciprocal(out=rstd[:npart, :], in_=rstd[:npart, :]), r=[ba], w=[ba])
            V(lambda e: e.tensor_scalar(out=hb[:npart, :], in0=xt_ap, scalar1=rstd[:npart, :], scalar2=None,
                                        op0=ALU.mult), r=[bx, ba], w=[bh])
            pv = ps_bf(pbank)
            for kt in range(8):
                T(lambda e, kt=kt: e.transpose(out=pv[:, kt * 128:kt * 128 + npart],
                                               in_=hb[:npart, kt * 128:(kt + 1) * 128],
                                               identity=identb[:npart, :npart]),
                  r=[bh, b_const], w=[bPS[pbank]])
            if out4 is not None:
                V(lambda e: e.tensor_tensor(
                    out=out4, in0=pv.rearrange("p (k c s) -> p k s c", k=8, s=8),
                    in1=gcol.unsqueeze(2).unsqueeze(3).to_broadcast([128, 8, 8, 16]), op=ALU.mult),
                  r=[bPS[pbank], b_const] + ([bg] if bg is not None else []), w=[bhT])
                return
            V(lambda e: e.tensor_tensor(
                out=hT_ap[:, :, col0:col0 + npart],
                in0=pv.rearrange("p (k t) -> p k t", k=8)[:, :, 0:npart],
                in1=gcol.unsqueeze(2).to_broadcast([128, 8, npart]), op=ALU.mult),
              r=[bPS[pbank], b_const] + ([bg] if bg is not None else []), w=[bhT])

        def load_w_bf16(dst, bdst, src, kt_n, ncols, c0=0):
            for kt in range(kt_n):
                for cc in range(0, ncols, 1024):
                    w_ = min(1024, ncols - cc)
                    S.dma("pool", dst[:, kt, cc:cc + w_], src[kt * 128:(kt + 1) * 128, c0 + cc:c0 + cc + w_],
                          writes=[bdst])

        with ExitStack() as sa:
            Wt = alloc(sa, "Wt", [128, G, 128], BF16)
            Wst = alloc(sa, "Wst", [128, G, 128], BF16)
            Tt = alloc(sa, "Tt", [128, G, 128], BF16)
            Vt = alloc(sa, "Vt", [128, G, 128], BF16)
            COSR = alloc(sa, "COSR", [128, G, 64])
            SINR = alloc(sa, "SINR", [128, G, 64])
            masters = alloc(sa, "masters", [128, 8, 240], BF16)
            AR = alloc(sa, "AR", [128, G, K1])
            AI = alloc(sa, "AI", [128, G, K1])
            MAGJ = alloc(sa, "MAGJ", [128, G, K1])
            DS = alloc(sa, "DS", [128, G])
            gm = alloc(sa, "gm", [128, 8])
            winu = alloc(sa, "winu", [128, 8, 512], BF16)
            wglu = alloc(sa, "wglu", [128, 4, 512], BF16)
            b_tab = Buf("s5tab")
            b_winu = Buf("winu", S.GW[0])
            b_wglu = Buf("wglu", S.GW[1])
            b_tabp = Buf("s5tabp")
            S.dma("pool", masters[:], I["c_masters"].rearrange("a k j -> k a j"), writes=[b_tabp])
            S.dma("sp", gm[:], I["g_mix"].rearrange("(k p) -> p k", p=128), writes=[b_tab])
            for tau in range(8):
                S.dma("sp", DS[16 * tau:16 * tau + 16, :], I["d_skip"].rearrange("(g h) -> h g", h=16),
                      writes=[b_tab])
            load_w_bf16(winu, b_winu, I["w_in"], 8, 512, 0)
            load_w_bf16(wglu, b_wglu, I["w_glu"], 4, 512, 0)

            with ExitStack() as s0:
                rows = alloc(s0, "rows", [128, 2 * K1 + 64])
                LR = alloc(s0, "LR", [128, G])
                LI = alloc(s0, "LI", [128, G])
                DT = alloc(s0, "DT", [128, G])
                LRDT = alloc(s0, "LRDT", [128, G])
                LIDT = alloc(s0, "LIDT", [128, G])
                tA = alloc(s0, "tA", [128, G, 64])
                tB = alloc(s0, "tB", [128, G, 64])
                tC = alloc(s0, "tC", [128, G, 64])
                COSJ = alloc(s0, "COSJ", [128, G, K1])
                SINJ = alloc(s0, "SINJ", [128, G, K1])
                sm = alloc(s0, "sm", [128, 12, G])
                Br1 = alloc(s0, "Br1", [128, G, 16])
                Br2 = alloc(s0, "Br2", [128, G, 16])
                BB1 = alloc(s0, "BB1", [128, G, 16])
                BB2 = alloc(s0, "BB2", [128, G, 16])
                tb1 = alloc(s0, "tb1", [128, G, 16])
                big1 = alloc(s0, "big1", [128, G, 128])
                big2 = alloc(s0, "big2", [128, G, 128])
                WTpad = alloc(s0, "WTpad", [128, G, 256], BF16)
                WTs = alloc(s0, "WTs", [128, G, 128], BF16)
                CN1 = alloc(s0, "CN1", [128, 4, 128])
                CN2 = alloc(s0, "CN2", [128, 4, 128])
                CMa = alloc(s0, "CMa", [128, G, 16])
                CMb = alloc(s0, "CMb", [128, G, 16])
                CMab = alloc(s0, "CMab", [128, G, 16], BF16)
                b0 = Buf("p0in")
                bt = Buf("p0tmp")
                S.dma("sp", rows[:], I["c_rows"][0:1, :].partition_broadcast(128), writes=[b0])
                for hf in range(2):
                    S.dma("sp", LR[64 * hf:64 * hf + 64, :], I["lam_re"].rearrange("g p -> p g"), writes=[b0])
                    S.dma("sp", LI[64 * hf:64 * hf + 64, :], I["lam_im"].rearrange("g p -> p g"), writes=[b0])
                S.dma("sp", DT[:], I["log_dt"].rearrange("(o g) -> o g", o=1).partition_broadcast(128), writes=[b0])
                S.dma("sp", Br1[0:64], I["b_re"].rearrange("g p h -> p g h"), writes=[b0])
                S.dma("sp", Br1[64:128], I["b_im"].rearrange("g p h -> p g h"), writes=[b0])
                S.dma("sp", Br2[0:64], I["b_im"].rearrange("g p h -> p g h"), writes=[b0])
                S.dma("sp", Br2[64:128], I["b_re"].rearrange("g p h -> p g h"), writes=[b0])
                S.dma("sp", CN1[:, :, 0:64], I["c_re"].rearrange("(c r) p -> r c p", r=128), writes=[b0])
                S.dma("sp", CN1[:, :, 64:128], I["c_im"].rearrange("(c r) p -> r c p", r=128), writes=[b0])
                S.dma("sp", CN2[:, :, 0:64], I["c_im"].rearrange("(c r) p -> r c p", r=128), writes=[b0])
                S.dma("sp", CN2[:, :, 64:128], I["c_re"].rearrange("(c r) p -> r c p", r=128), writes=[b0])
                MT1 = rows[:, 0:K1]
                MLr = rows[:, K1:2 * K1]
                MRT = rows[:, 2 * K1:2 * K1 + 64]
                A(lambda e: e.activation(out=DT[:], in_=DT[:], func=AF.Exp), r=[b0], w=[b0])
                V(lambda e: e.tensor_tensor(out=LRDT[:], in0=LR[:], in1=DT[:], op=ALU.mult), r=[b0], w=[bt])
                V(lambda e: e.tensor_tensor(out=LIDT[:], in0=LI[:], in1=DT[:], op=ALU.mult), r=[b0], w=[bt])

                def trig(mt_ap, K, cos_out, sin_out):
                    shp = [128, G, K]
                    a_, b_, c_ = tA[:, :, 0:K], tB[:, :, 0:K], tC[:, :, 0:K]
                    V(lambda e: e.tensor_tensor(out=a_, in0=LIDT[:].unsqueeze(2).to_broadcast(shp),
                                                in1=mt_ap.unsqueeze(1).to_broadcast(shp), op=ALU.mult),
                      r=[bt, b0], w=[bt])
                    for (outp, off) in ((sin_out, 0.0), (cos_out, 0.25)):
                        if outp is None:
                            continue
                        V(lambda e, off=off: e.tensor_scalar(out=c_, in0=a_, scalar1=off, scalar2=None,
                                                             op0=ALU.add), r=[bt], w=[bt])
                        V(lambda e: e.tensor_scalar(out=b_, in0=c_, scalar1=MAGIC, scalar2=None, op0=ALU.add),
                          r=[bt], w=[bt])
                        V(lambda e: e.tensor_scalar(out=b_, in0=b_, scalar1=MAGIC, scalar2=None, op0=ALU.subtract),
                          r=[bt], w=[bt])
                        V(lambda e: e.tensor_tensor(out=c_, in0=c_, in1=b_, op=ALU.subtract), r=[bt], w=[bt])
                        A(lambda e, outp=outp: e.activation(out=outp, in_=c_, func=AF.Sin, scale=TWO_PI),
                          r=[bt], w=[b_tab])

                trig(MT1, K1, COSJ[:], SINJ[:])
                trig(MRT, 64, COSR[:], SINR[:])
                shpj = [128, G, K1]
                V(lambda e: e.tensor_tensor(out=MAGJ[:], in0=LRDT[:].unsqueeze(2).to_broadcast(shpj),
                                            in1=MLr.unsqueeze(1).to_broadcast(shpj), op=ALU.mult),
                  r=[bt, b0], w=[b_tab])
                A(lambda e: e.activation(out=MAGJ[:], in_=MAGJ[:], func=AF.Exp), r=[b_tab], w=[b_tab])
                V(lambda e: e.tensor_tensor(out=AR[:], in0=MAGJ[:], in1=COSJ[:], op=ALU.mult), r=[b_tab], w=[b_tab])
                V(lambda e: e.tensor_tensor(out=AI[:], in0=MAGJ[:], in1=SINJ[:], op=ALU.mult), r=[b_tab], w=[b_tab])
                em1, shalf, cm1, am1r, ai1, den, fr, fi, t0_, t1_ = [sm[:, i, :] for i in range(10)]
                x_ = LRDT[:]
                V(lambda e: e.tensor_scalar(out=em1, in0=x_, scalar1=0.2, scalar2=1.0, op0=ALU.mult, op1=ALU.add),
                  r=[bt], w=[bt])
                for cf in (0.25, 1.0 / 3.0, 0.5):
                    V(lambda e: e.tensor_tensor(out=em1, in0=em1, in1=x_, op=ALU.mult), r=[bt], w=[bt])
                    V(lambda e, cf=cf: e.tensor_scalar(out=em1, in0=em1, scalar1=cf, scalar2=1.0, op0=ALU.mult,
                                                       op1=ALU.add), r=[bt], w=[bt])
                V(lambda e: e.tensor_tensor(out=em1, in0=em1, in1=x_, op=ALU.mult), r=[bt], w=[bt])
                V(lambda e: e.tensor_copy(out=shalf, in_=SINJ[:, :, I_HALF]), r=[b_tab], w=[bt])
                V(lambda e: e.scalar_tensor_tensor(out=cm1, in0=shalf, scalar=-2.0, op0=ALU.mult, in1=shalf,
                                                   op1=ALU.mult), r=[bt], w=[bt])
                V(lambda e: e.tensor_tensor(out=am1r, in0=em1, in1=COSJ[:, :, I_A1], op=ALU.mult), r=[bt, b_tab], w=[bt])
                V(lambda e: e.tensor_tensor(out=am1r, in0=am1r, in1=cm1, op=ALU.add), r=[bt], w=[bt])
                V(lambda e: e.tensor_copy(out=ai1, in_=AI[:, :, I_A1]), r=[b_tab], w=[bt])
                V(lambda e: e.tensor_tensor(out=den, in0=LR[:], in1=LR[:], op=ALU.mult), r=[b0], w=[bt])
                V(lambda e: e.tensor_tensor(out=t0_, in0=LI[:], in1=LI[:], op=ALU.mult), r=[b0], w=[bt])
                V(lambda e: e.tensor_tensor(out=den, in0=den, in1=t0_, op=ALU.add), r=[bt], w=[bt])
                V(lambda e: e.reciprocal(out=den, in_=den), r=[bt], w=[bt])
                V(lambda e: e.tensor_tensor(out=fr, in0=am1r, in1=LR[:], op=ALU.mult), r=[bt, b0], w=[bt])
                V(lambda e: e.tensor_tensor(out=t0_, in0=ai1, in1=LI[:], op=ALU.mult), r=[bt, b0], w=[bt])
                V(lambda e: e.tensor_tensor(out=fr, in0=fr, in1=t0_, op=ALU.add), r=[bt], w=[bt])
                V(lambda e: e.tensor_tensor(out=fr, in0=fr, in1=den, op=ALU.mult), r=[bt], w=[bt])
                V(lambda e: e.tensor_tensor(out=fi, in0=ai1, in1=LR[:], op=ALU.mult), r=[bt, b0], w=[bt])
                V(lambda e: e.tensor_tensor(out=t0_, in0=am1r, in1=LI[:], op=ALU.mult), r=[bt, b0], w=[bt])
                V(lambda e: e.tensor_tensor(out=fi, in0=fi, in1=t0_, op=ALU.subtract), r=[bt], w=[bt])
                V(lambda e: e.tensor_tensor(out=fi, in0=fi, in1=den, op=ALU.mult), r=[bt], w=[bt])
                V(lambda e: e.tensor_scalar(out=Br2[:], in0=Br2[:], scalar1=sgn[:, 1:2], scalar2=None, op0=ALU.mult),
                  r=[b0, b_const], w=[b0])
                shb = [128, G, 16]
                frb = fr.unsqueeze(2).to_broadcast(shb)
                fib = fi.unsqueeze(2).to_broadcast(shb)
                V(lambda e: e.tensor_tensor(out=BB1[:], in0=Br1[:], in1=frb, op=ALU.mult), r=[b0, bt], w=[bt])
                V(lambda e: e.tensor_tensor(out=tb1[:], in0=Br2[:], in1=fib, op=ALU.mult), r=[b0, bt], w=[bt])
                V(lambda e: e.tensor_tensor(out=BB1[:], in0=BB1[:], in1=tb1[:], op=ALU.add), r=[bt], w=[bt])
                V(lambda e: e.tensor_tensor(out=BB2[:], in0=Br2[:], in1=frb, op=ALU.mult), r=[b0, bt], w=[bt])
                V(lambda e: e.tensor_tensor(out=tb1[:], in0=Br1[:], in1=fib, op=ALU.mult), r=[b0, bt], w=[bt])
                V(lambda e: e.tensor_tensor(out=BB2[:], in0=BB2[:], in1=tb1[:], op=ALU.subtract), r=[bt], w=[bt])
                sh4 = [128, G, 8, 16]
                arv = AR[:, :, 0:8].unsqueeze(3).to_broadcast(sh4)
                aiv = AI[:, :, 0:8].unsqueeze(3).to_broadcast(sh4)
                bb1 = BB1[:].unsqueeze(2).to_broadcast(sh4)
                bb2 = BB2[:].unsqueeze(2).to_broadcast(sh4)
                g1 = big1[:].rearrange("p g (s h) -> p g s h", s=8)
                g2 = big2[:].rearrange("p g (s h) -> p g s h", s=8)
                V(lambda e: e.memset(WTpad[:], 0.0), w=[bt])
                V(lambda e: e.tensor_tensor(out=g1, in0=arv, in1=bb1, op=ALU.mult), r=[b_tab, bt], w=[bt])
                V(lambda e: e.tensor_tensor(out=g2, in0=aiv, in1=bb2, op=ALU.mult), r=[b_tab, bt], w=[bt])
                V(lambda e: e.tensor_tensor(out=WTpad[:, :, 0:128], in0=big1[:], in1=big2[:], op=ALU.add),
                  r=[bt], w=[bt])
                V(lambda e: e.tensor_tensor(out=g1, in0=arv, in1=bb2, op=ALU.mult), r=[b_tab, bt], w=[bt])
                V(lambda e: e.tensor_tensor(out=g2, in0=aiv, in1=bb1, op=ALU.mult), r=[b_tab, bt], w=[bt])
                V(lambda e: e.tensor_tensor(out=WTs[:], in0=big1[:], in1=big2[:], op=ALU.subtract), r=[bt], w=[bt])
                for (src_fn, dstt) in ((lambda g: WTpad[:, g, 0:128], Wt), (lambda g: WTs[:, g, :], Wst)):
                    for gq in range(8):
                        bank = gq % 2
                        pv = ps_bf(bank)
                        for j in range(4):
                            g = gq * 4 + j
                            T(lambda e, g=g, j=j, pv=pv, src_fn=src_fn: e.transpose(
                                out=pv[:, j * 128:(j + 1) * 128], in_=src_fn(g), identity=identb[:]),
                              r=[bt, b_const], w=[bPS[bank]])
                        A(lambda e, gq=gq, pv=pv, dstt=dstt: e.copy(
                            out=dstt[:, gq * 4:gq * 4 + 4, :], in_=pv[:, 0:512].rearrange("p (j c) -> p j c", j=4)),
                          r=[bPS[bank]], w=[b_tab])
                for (CN, CM, col) in ((CN1, CMa, 0), (CN2, CMb, None)):
                    for c4 in range(4):
                        bank = 2 + (c4 % 2)
                        T(lambda e, CN=CN, c4=c4, bank=bank: e.transpose(out=PS[bank][:, 0:128], in_=CN[:, c4, :],
                                                                         identity=identf[:]),
                          r=[b0, b_const], w=[bPS[bank]])
                        if col is not None:
                            V(lambda e, CM=CM, c4=c4, bank=bank: e.tensor_scalar(
                                out=CM[:, c4 * 8:(c4 + 1) * 8, :],
                                in0=PS[bank][:, 0:128].rearrange("p (g h) -> p g h", g=8),
                                scalar1=sgn[:, 0:1], scalar2=None, op0=ALU.mult),
                              r=[bPS[bank], b_const], w=[bt])
                        else:
                            V(lambda e, CM=CM, c4=c4, bank=bank: e.tensor_scalar(
                                out=CM[:, c4 * 8:(c4 + 1) * 8, :],
                                in0=PS[bank][:, 0:128].rearrange("p (g h) -> p g h", g=8),
                                scalar1=-1.0, scalar2=None, op0=ALU.mult),
                              r=[bPS[bank]], w=[bt])
                V(lambda e: e.tensor_copy(out=CMab[:], in_=CMa[:]), r=[bt], w=[bt])
                afw = AR[:, :, 8:16].unsqueeze(3).to_broadcast(sh4)
                aifw = AI[:, :, 8:16].unsqueeze(3).to_broadcast(sh4)
                cma = CMa[:].unsqueeze(2).to_broadcast(sh4)
                cmb = CMb[:].unsqueeze(2).to_broadcast(sh4)
                V(lambda e: e.tensor_tensor(out=g1, in0=afw, in1=cma, op=ALU.mult), r=[b_tab, bt], w=[bt])
                V(lambda e: e.tensor_tensor(out=g2, in0=aifw, in1=cmb, op=ALU.mult), r=[b_tab, bt], w=[bt])
                V(lambda e: e.tensor_tensor(out=Vt[:], in0=big1[:], in1=big2[:], op=ALU.add), r=[bt], w=[b_tab])
                for gq in range(8):
                    bank = 4 + (gq % 2)
                    for j in range(4):
                        g = gq * 4 + j
                        for tau in range(8):
                            c0 = (7 - tau) * 16
                            T(lambda e, g=g, j=j, tau=tau, c0=c0, bank=bank: e.matmul(
                                PS[bank][:, j * 128 + tau * 16:j * 128 + tau * 16 + 16],
                                lhsT=WTpad[:, g, c0:c0 + 128], rhs=CMab[:, g, :], start=True, stop=True),
                              r=[bt], w=[bPS[bank]])
                    A(lambda e, gq=gq, bank=bank: e.copy(
                        out=Tt[:, gq * 4:gq * 4 + 4, :], in_=PS[bank][:].rearrange("p (j c) -> p j c", j=4)),
                      r=[bPS[bank]], w=[b_tab])
                S.barrier()
            xst = [alloc(sa, "xst%d" % i, [128, D]) for i in range(2)]
            bxst = [Buf("xst%d" % i, S.GL[i]) for i in range(2)]
            scrA = make_scr(sa, "A", [7])
            bscr = Buf("scrA")
            hT2 = [alloc(sa, "hT_%d" % i, [128, 8, 512], BF16) for i in range(2)]
            bhT2 = [Buf("hT_%d" % i) for i in range(2)]
            uT2 = [alloc(sa, "uT_%d" % i, [128, 4, 512], BF16) for i in range(2)]
            buT2 = [Buf("uT_%d" % i) for i in range(2)]
            U = alloc(sa, "U", [128, G, 64], BF16)
            bU = Buf("U")
            rr = alloc(sa, "rr", [128, G, 64])
            rs = alloc(sa, "rs", [128, G, 64])
            ww = alloc(sa, "ww", [128, G, 64])
            ws = alloc(sa, "ws", [128, G, 64])
            tmpr = alloc(sa, "tmpr", [128, 16, 64])
            b_r, b_rs, b_w, b_ws, b_tmpr = Buf("r"), Buf("rs"), Buf("w"), Buf("ws"), Buf("tmpr")
            Xb = alloc(sa, "Xb", [128, G, 65], BF16)
            bXb = Buf("Xb")
            Xc = alloc(sa, "Xc", [128, G])
            Xsc = alloc(sa, "Xsc", [128, G])
            ctmp = alloc(sa, "ctmp", [128, 2, G])
            bXc = Buf("Xc", S.GS[0])
            ytmp = alloc(sa, "ytmp", [128, 8, 64])
            bytmp = Buf("ytmp")
            Zt = alloc(sa, "Zt", [128, G, 64], BF16)
            bZ = Buf("Z")
            zT = alloc(sa, "zT", [128, 4, 512], BF16)
            bzT = Buf("zT")
            sig = alloc(sa, "sig", [128, 4, 512])
            bsig = Buf("sig")
            H0 = alloc(sa, "H0", [128, 512])
            H0s = alloc(sa, "H0s", [128, 512])
            hn = alloc(sa, "hn", [128, 4, 128])
            hn2 = alloc(sa, "hn2", [128, 4, 128])
            Hp = alloc(sa, "Hp", [128, G, 16])
            Xf = alloc(sa, "Xf", [128, G, 16])
            xo = alloc(sa, "xo", [128, 4, 128])
            bH = Buf("H0")
            bxo = Buf("xo", S.GS[1])
            V(lambda e: e.memset(Xc[:], 0.0), r=[b_tabp], w=[bXc, b_tab])
            V(lambda e: e.memset(Xsc[:], 0.0), w=[bXc])
            V(lambda e: e.memset(Xb[:], 0.0), w=[bXb])

            blocks = [(i * 512, 512, False) for i in range(4)] + [(SEQ, TS, True)]
            if _os0.environ.get("K1A") == "0":
                blocks = []
            def p1a_stageA(bi):
                t0, n, is_s = blocks[bi]
                hT, bhT = hT2[bi % 2], bhT2[bi % 2]
                uT, buT = uT2[bi % 2], buT2[bi % 2]
                ntile = (n + 127) // 128
                for ti in range(ntile):
                    npart = min(128, n - ti * 128)
                    slot = (bi * 4 + ti) % 2
                    src = I["xs"][:, :] if is_s else I["xp"][t0 + ti * 128:t0 + ti * 128 + 128, :]
                    S.dma("sp", xst[slot][:npart, :], src, writes=[bxst[slot]])
                    o4 = None if is_s else hT[:, :, :].rearrange("p k (s c) -> p k s c", s=8)[:, :, :, ti * 16:(ti + 1) * 16]
                    rmsnorm_hT(xst[slot][:npart, :], bxst[slot], npart, gm[:], hT, bhT,
                               scrA, ti * 128, None, bg=b_tab, out4=o4)
                for ct in range(4):
                    bank = ct
                    for kt in range(8):
                        T(lambda e, ct=ct, kt=kt, bank=bank: e.matmul(
                            PS[bank][:, 0:n], lhsT=winu[:, kt, ct * 128:(ct + 1) * 128], rhs=hT[:, kt, 0:n],
                            start=(kt == 0), stop=(kt == 7)), r=[b_winu, bhT], w=[bPS[bank]])
                    A(lambda e, ct=ct, bank=bank: e.copy(out=uT[:, ct, 0:n], in_=PS[bank][:, 0:n]),
                      r=[bPS[bank]], w=[buT])

            if blocks:
                p1a_stageA(0)
            for bi, (t0, n, is_s) in enumerate(blocks):
                nch = n // 8 if not is_s else 16
                uT, buT = uT2[bi % 2], buT2[bi % 2]
                for gq in range(4):
                    bank = 4 + (gq % 2)
                    for j in range(8):
                        g = gq * 8 + j
                        ct, gl = g // 8, g % 8
                        if not is_s:
                            uv = uT[:, ct, 0:n].rearrange("p (s c) -> p s c", s=8)
                            sig_list = list(range(8))
                        else:
                            uv = uT[:, ct, 0:n].rearrange("p (b t) -> p t b", t=4)
                            sig_list = [4, 5, 6, 7]
                        for si, sg_ in enumerate(sig_list):
                            rhs = uv[:, sg_ if not is_s else si, :]
                            T(lambda e, j=j, gl=gl, sg_=sg_, rhs=rhs, si=si, bank=bank, L=len(sig_list): e.matmul(
                                PS[bank][:, j * 64:j * 64 + nch],
                                lhsT=masters[:, gl, 112 - 16 * sg_:240 - 16 * sg_], rhs=rhs,
                                start=(si == 0), stop=(si == L - 1)),
                              r=[b_tab, buT], w=[bPS[bank]])
                    A(lambda e, gq=gq, bank=bank: e.copy(
                        out=U[:, gq * 8:gq * 8 + 8, 0:nch],
                        in_=PS[bank][:].rearrange("p (j c) -> p j c", j=8)[:, :, 0:nch]),
                      r=[bPS[bank]], w=[bU])
                if not is_s:
                    for hf in range(2):
                        for j in range(16):
                            g = hf * 16 + j
                            for (wt, bk) in ((Wt, 0), (Wst, 2)):
                                bank = bk + j // 8
                                T(lambda e, g=g, j=j, wt=wt, bank=bank: e.matmul(
                                    PS[bank][:, (j % 8) * 64:(j % 8) * 64 + 64], lhsT=wt[:, g, :], rhs=U[:, g, :],
                                    start=True, stop=True), r=[b_tab, bU], w=[bPS[bank]])
                        for q in range(2):
                            gs = slice(hf * 16 + q * 8, hf * 16 + q * 8 + 8)
                            Sv = PS[q][:].rearrange("p (j c) -> p j c", j=8)
                            Ssv = PS[2 + q][:].rearrange("p (j c) -> p j c", j=8)
                            tm = tmpr[:, q * 8:q * 8 + 8, :]
                            V(lambda e, gs=gs, Sv=Sv: e.tensor_tensor(out=rr[:, gs, :], in0=Sv, in1=COSR[:, gs, :],
                                                                     op=ALU.mult), r=[bPS[q], b_tab], w=[b_r])
                            V(lambda e, gs=gs, Ssv=Ssv, tm=tm: e.tensor_tensor(out=tm, in0=Ssv, in1=SINR[:, gs, :],
                                                                              op=ALU.mult),
                              r=[bPS[2 + q], b_tab], w=[b_tmpr])
                            V(lambda e, gs=gs, tm=tm: e.tensor_tensor(out=rr[:, gs, :], in0=rr[:, gs, :], in1=tm,
                                                                     op=ALU.subtract), r=[b_r, b_tmpr], w=[b_r])
                            V(lambda e, gs=gs, Ssv=Ssv: e.tensor_tensor(out=rs[:, gs, :], in0=Ssv, in1=COSR[:, gs, :],
                                                                       op=ALU.mult), r=[bPS[2 + q], b_tab], w=[b_rs])
                            V(lambda e, gs=gs, Sv=Sv, tm=tm: e.tensor_tensor(out=tm, in0=Sv, in1=SINR[:, gs, :],
                                                                            op=ALU.mult),
                              r=[bPS[q], b_tab], w=[b_tmpr])
                            V(lambda e, gs=gs, tm=tm: e.tensor_tensor(out=rs[:, gs, :], in0=rs[:, gs, :], in1=tm,
                                                                     op=ALU.add), r=[b_rs, b_tmpr], w=[b_rs])
                    for g in range(G):
                        rho = MAGJ[:, g, I_A8:I_A8 + 1].to_broadcast([128, 64])
                        V(lambda e, g=g, rho=rho: e.tensor_tensor_scan(
                            out=ww[:, g, :], data0=rho, data1=rr[:, g, :], initial=Xc[:, g:g + 1], op0=ALU.mult,
                            op1=ALU.add), r=[b_r, b_tab, bXc], w=[b_w])
                        V(lambda e, g=g, rho=rho: e.tensor_tensor_scan(
                            out=ws[:, g, :], data0=rho, data1=rs[:, g, :], initial=Xsc[:, g:g + 1], op0=ALU.mult,
                            op1=ALU.add), r=[b_rs, b_tab, bXc], w=[b_ws])
                    if bi + 1 < len(blocks):
                        p1a_stageA(bi + 1)
                    ce, se_ = COSR[:, :, 63], SINR[:, :, 63]
                    we, wse = ww[:, :, 63], ws[:, :, 63]
                    V(lambda e: e.tensor_tensor(out=ctmp[:, 0, :], in0=ce, in1=we, op=ALU.mult), r=[b_w, b_tab], w=[bscr])
                    V(lambda e: e.tensor_tensor(out=ctmp[:, 1, :], in0=se_, in1=wse, op=ALU.mult), r=[b_ws, b_tab], w=[bscr])
                    V(lambda e: e.tensor_tensor(out=Xc[:], in0=ctmp[:, 0, :], in1=ctmp[:, 1, :], op=ALU.add),
                      r=[bscr], w=[bXc])
                    V(lambda e: e.tensor_tensor(out=ctmp[:, 0, :], in0=ce, in1=wse, op=ALU.mult), r=[b_ws, b_tab], w=[bscr])
                    V(lambda e: e.tensor_tensor(out=ctmp[:, 1, :], in0=se_, in1=we, op=ALU.mult), r=[b_w, b_tab], w=[bscr])
                    V(lambda e: e.tensor_tensor(out=Xsc[:], in0=ctmp[:, 0, :], in1=ctmp[:, 1, :], op=ALU.subtract),
                      r=[bscr], w=[bXc])
                    if bi > 0:
                        V(lambda e: e.tensor_copy(out=Xb[:, :, 0], in_=Xb[:, :, 64]), r=[bXb], w=[bXb])
                    V(lambda e: e.tensor_tensor(out=ww[:], in0=ww[:], in1=COSR[:], op=ALU.mult), r=[b_w, b_tab, bXc],
                      w=[b_w])
                    V(lambda e: e.tensor_tensor(out=ws[:], in0=ws[:], in1=SINR[:], op=ALU.mult), r=[b_ws, b_tab, bXc],
                      w=[b_ws])
                    V(lambda e: e.tensor_tensor(out=Xb[:, :, 1:65], in0=ww[:], in1=ws[:], op=ALU.add),
                      r=[b_w, b_ws], w=[bXb])
                    xprev = lambda g: Xb[:, g, 0:64]
                    bXprev = bXb
                    if bi == 3:
                        S.dma("sp", O["o_s5r_p"].rearrange("g p -> p g"), Xc[0:64, :], reads=[bXc])
                        S.dma("sp", O["o_s5i_p"].rearrange("g p -> p g"), Xc[64:128, :], reads=[bXc])
                else:
                    S.dma("sp", hn[:, :, 0:64], I["s5r"].rearrange("(j r) p -> r j p", r=128), writes=[bH])
                    S.dma("sp", hn[:, :, 64:128], I["s5i"].rearrange("(j r) p -> r j p", r=128), writes=[bH])
                    S.dma("sp", hn2[:, :, 0:64], I["s5i"].rearrange("(j r) p -> r j p", r=128), writes=[bH])
                    S.dma("sp", hn2[:, :, 64:128], I["s5r"].rearrange("(j r) p -> r j p", r=128), writes=[bH])
                    for (src_, dst_, bank) in ((hn, H0, 0), (hn2, H0s, 1)):
                        for j in range(4):
                            T(lambda e, src_=src_, j=j, bank=bank: e.transpose(
                                out=PS[bank][:, j * 128:(j + 1) * 128], in_=src_[:, j, :], identity=identf[:]),
                              r=[bH, b_const], w=[bPS[bank]])
                        V(lambda e, dst_=dst_, bank=bank: e.tensor_copy(out=dst_[:], in_=PS[bank][:]),
                          r=[bPS[bank]], w=[bH])
                    V(lambda e: e.tensor_scalar(out=H0s[0:64, :], in0=H0s[0:64, :], scalar1=-1.0, scalar2=None,
                                                op0=ALU.mult), r=[bH], w=[bH])
                    shs = [128, G, 16]
                    h0v = H0[:].rearrange("p (b g) -> p g b", g=G)
                    h0sv = H0s[:].rearrange("p (b g) -> p g b", g=G)

                    def abc(tab, idx):
                        return tab[:, :, idx].unsqueeze(2).to_broadcast(shs)
                    V(lambda e: e.tensor_tensor(out=Xf[:], in0=h0v, in1=abc(AR, I_AM4), op=ALU.mult), r=[bH, b_tab], w=[bxo])
                    V(lambda e: e.tensor_tensor(out=Hp[:], in0=h0sv, in1=abc(AI, I_AM4), op=ALU.mult), r=[bH, b_tab], w=[bxo])
                    V(lambda e: e.tensor_tensor(out=Xb[:, :, 0:16], in0=Xf[:], in1=Hp[:], op=ALU.add), r=[bxo], w=[bXb])
                    V(lambda e: e.tensor_tensor(out=Xf[:], in0=h0v, in1=abc(AR, I_A4), op=ALU.mult), r=[bH, b_tab], w=[bxo])
                    V(lambda e: e.tensor_tensor(out=Hp[:], in0=h0sv, in1=abc(AI, I_A4), op=ALU.mult), r=[bH, b_tab], w=[bxo])
                    V(lambda e: e.tensor_tensor(out=Xf[:], in0=Xf[:], in1=Hp[:], op=ALU.add), r=[bxo], w=[bxo])
                    for q in range(4):
                        bank = q % 2
                        for j in range(8):
                            g = q * 8 + j
                            T(lambda e, g=g, j=j, bank=bank: e.matmul(
                                PS[bank][:, j * 64:j * 64 + 16], lhsT=Wt[:, g, :], rhs=U[:, g, 0:16],
                                start=True, stop=True), r=[b_tab, bU], w=[bPS[bank]])
                        V(lambda e, q=q, bank=bank: e.tensor_tensor(
                            out=Xf[:, q * 8:q * 8 + 8, :], in0=Xf[:, q * 8:q * 8 + 8, :],
                            in1=PS[bank][:].rearrange("p (j c) -> p j c", j=8)[:, :, 0:16], op=ALU.add),
                          r=[bxo, bPS[bank]], w=[bxo])
                    Xf2 = Xf[:].rearrange("p g b -> p (g b)")
                    for j in range(4):
                        T(lambda e, j=j: e.transpose(out=PS[2][:, j * 128:(j + 1) * 128],
                                                     in_=Xf2[:, j * 128:(j + 1) * 128], identity=identf[:]),
                          r=[bxo, b_const], w=[bPS[2]])
                    V(lambda e: e.tensor_copy(out=xo[:], in_=PS[2][:].rearrange("p (j c) -> p j c", j=4)),
                      r=[bPS[2]], w=[bxo])
                    for j in range(4):
                        for gl in range(8):
                            for (nm, c0) in (("o_s5r_s", 0), ("o_s5i_s", 64)):
                                S.dma("sp", O[nm].rearrange("(b g) p -> g b p", g=G)[8 * j + gl],
                                      xo[gl * 16:gl * 16 + 16, j, c0:c0 + 64], reads=[bxo])
                    xprev = lambda g: Xb[:, g, 0:16]
                    bXprev = bXb
                for gq in range(4):
                    bank = 6 + (gq % 2)
                    for j in range(8):
                        g = gq * 8 + j
                        T(lambda e, g=g, j=j, bank=bank: e.matmul(
                            PS[bank][:, j * 64:j * 64 + nch], lhsT=Tt[:, g, :], rhs=U[:, g, 0:nch],
                            start=True, stop=False), r=[b_tab, bU], w=[bPS[bank]])
                        T(lambda e, g=g, j=j, bank=bank: e.matmul(
                            PS[bank][:, j * 64:j * 64 + nch], lhsT=Vt[:, g, :], rhs=xprev(g)[:, 0:nch],
                            start=False, stop=True), r=[b_tab, bXprev], w=[bPS[bank]])
                    gs = slice(gq * 8, gq * 8 + 8)
                    yv = PS[bank][:].rearrange("p (j c) -> p j c", j=8)[:, :, 0:nch]
                    V(lambda e, gs=gs: e.tensor_tensor(out=ytmp[:, :, 0:nch], in0=U[:, gs, 0:nch],
                                                       in1=DS[:, gs].unsqueeze(2).to_broadcast([128, 8, nch]),
                                                       op=ALU.mult), r=[bU, b_tab], w=[bytmp])
                    V(lambda e, yv=yv: e.tensor_tensor(out=ytmp[:, :, 0:nch], in0=yv, in1=ytmp[:, :, 0:nch],
                                                       op=ALU.add), r=[bPS[bank], bytmp], w=[bytmp])
                    A(lambda e, gs=gs: e.activation(out=Zt[:, gs, 0:nch], in_=ytmp[:, :, 0:nch],
                                                    func=AF.Gelu_apprx_tanh), r=[bytmp], w=[bZ])
                for ct in range(4):
                    bank = ct % 2
                    taus = list(range(8)) if not is_s else [4, 5, 6, 7]
                    for ti_, tau in enumerate(taus):
                        for gl in range(8):
                            g = ct * 8 + gl
                            T(lambda e, g=g, gl=gl, tau=tau, ti_=ti_, bank=bank: e.matmul(
                                PS[bank][:, ti_ * 64:ti_ * 64 + nch],
                                lhsT=masters[:, tau, 112 - 16 * gl:240 - 16 * gl], rhs=Zt[:, g, 0:nch],
                                start=(gl == 0), stop=(gl == 7)), r=[b_tab, bZ], w=[bPS[bank]])
                    if not is_s:
                        A(lambda e, ct=ct, bank=bank: e.copy(
                            out=zT[:, ct, 0:n].rearrange("p (c t) -> p c t", t=8),
                            in_=PS[bank][:].rearrange("p (t c) -> p c t", t=8)), r=[bPS[bank]], w=[bzT])
                    else:
                        A(lambda e, ct=ct, bank=bank: e.copy(
                            out=zT[:, ct, 0:n].rearrange("p (b t) -> p t b", t=4),
                            in_=PS[bank][:].rearrange("p (t c) -> p t c", t=8)[:, 0:4, 0:16]),
                          r=[bPS[bank]], w=[bzT])
                for ct in range(4):
                    bank = 2 + (ct % 2)
                    for kt in range(4):
                        T(lambda e, ct=ct, kt=kt, bank=bank: e.matmul(
                            PS[bank][:, 0:n], lhsT=wglu[:, kt, ct * 128:(ct + 1) * 128], rhs=zT[:, kt, 0:n],
                            start=(kt == 0), stop=(kt == 3)), r=[b_wglu, bzT], w=[bPS[bank]])
                    A(lambda e, ct=ct, bank=bank: e.activation(out=sig[:, ct, 0:n], in_=PS[bank][:, 0:n],
                                                               func=AF.Sigmoid), r=[bPS[bank]], w=[bsig])
                V(lambda e: e.tensor_tensor(out=ssmT[:, :, t0:t0 + n], in0=zT[:, :, 0:n], in1=sig[:, :, 0:n],
                                            op=ALU.mult), r=[bzT, bsig], w=[b_ssmT[bi]])
            S.barrier()
        if dbg:
            with ExitStack() as sd:
                dtmp = alloc(sd, "dtmp", [128, 4, NTOK])
                bd = Buf("dtmp", S.GS[2])
                V(lambda e: e.tensor_copy(out=dtmp[:], in_=ssmT[:]), r=b_ssmT, w=[bd])
                S.dma("sp", O["dbg_ssm"][:, :, :], dtmp[:], reads=[bd])
                S.barrier()
        if stage <= 1:
            S.barrier()
            S.run_block()
            nck.__exit__(None, None, None)
            return nc

        with ExitStack() as sbx:
            x = alloc(sbx, "x", [128, NT, D])
            bx = [Buf("x%d" % n, S.GL[0] if n < 2 else S.GX) for n in range(NT)]
            for n in range(2):
                S.dma("sp", x[:, n, :], I["xp"][n * 128:(n + 1) * 128, :], writes=[bx[n]])
            scrB = make_scr(sbx, "B", [7])
            hT1 = alloc(sbx, "hT1", [128, 8, 128], BF16)
            bhT1 = Buf("hT1")

            def resid_add(n, npart, half, bank):
                V(lambda e: e.tensor_tensor(out=x[:npart, n, half * 512:(half + 1) * 512], in0=PS[bank][:npart, :],
                                            in1=x[:npart, n, half * 512:(half + 1) * 512], op=ALU.add),
                  r=[bPS[bank], bx[n]], w=[bx[n]])

            with ExitStack() as s1:
                wq = alloc(s1, "wqkvg", [128, 8, 2048], BF16)
                wout = alloc(s1, "wout", [128, 8, D], BF16)
                b_wqc = [Buf("wq%d" % c, S.GW[c]) for c in range(4)]
                b_wout = Buf("wout", S.GW[0])
                for c in range(4):
                    for kt in range(8):
                        S.dma("pool", wq[:, kt, c * 512:(c + 1) * 512],
                              I["w_in"][kt * 128:(kt + 1) * 128, 512 + c * 512:512 + (c + 1) * 512], writes=[b_wqc[c]])
                wout_loaded = [False]
                for n in range(2, NTP):
                    S.dma("sp", x[:, n, :], I["xp"][n * 128:(n + 1) * 128, :], reads=[b_wqc[1]] if n == 2 else [],
                          writes=[bx[n]])
                S.dma("sp", x[0:TS, 16, :], I["xs"][:, :], writes=[bx[16]])
                gm2 = alloc(s1, "gm2", [128, 8])
                gn = alloc(s1, "gn", [128, 4])
                rope = alloc(s1, "rope", [128, 3, NT, 64])
                dmp = alloc(s1, "dmp", [128, 512])
                dms = alloc(s1, "dms", [64, 256])
                xi = alloc(s1, "xi", [128, 768])
                zetap = alloc(s1, "zetap", [128, 4])
                zs = alloc(s1, "zs", [64, 64])
                cmask = alloc(s1, "cmask", [128, 16 * 64])
                b_t1 = Buf("tab1")
                S.dma("sp", gm2[:], I["g_mix"].rearrange("(k p) -> p k", p=128), writes=[b_t1])
                S.dma("sp", gn[:], I["ret_gn"].rearrange("(k p) -> p k", p=128), writes=[b_t1])
                for a_ in range(3):
                    S.dma("sp", rope[:, a_, :, :], I["c_rope"][a_], writes=[b_t1])
                S.dma("sp", dmp[:], I["c_dmask_p"][:, :], writes=[b_t1])
                S.dma("sp", dms[:], I["c_dmask_s"][:, :], writes=[b_t1])
                S.dma("sp", xi[:], I["c_xi"][0:1, :].partition_broadcast(128), writes=[b_t1])
                S.dma("sp", zetap[:], I["c_zeta_p"][:, :], writes=[b_t1])
                S.dma("sp", zs[:], I["c_zs"][:, :], writes=[b_t1])
                S.dma("sp", cmask[:], I["c_cmask"][0:1, :].partition_broadcast(128), writes=[b_t1])
                def load_wout():
                    load_w_bf16(wout, b_wout, I["w_out"], 8, D, 0)
                    for k in range(4):
                        V(lambda e: e.tensor_scalar(out=wout[:, 4 + k, :], in0=wout[:, 4 + k, :], scalar1=gn[:, k:k + 1],
                                                    scalar2=None, op0=ALU.mult), r=[b_wout, b_t1], w=[b_wout])
                    wout_loaded[0] = True
                t1q = alloc(s1, "t1q", [128, 512])
                t2q = alloc(s1, "t2q", [128, 512])
                t1k = alloc(s1, "t1k", [128, 512])
                t2k = alloc(s1, "t2k", [128, 512])
                qr = alloc(s1, "qr", [128, 512], BF16)
                kr = alloc(s1, "kr", [128, 512], BF16)
                qT = alloc(s1, "qT", [128, 4, 128], BF16)
                qxT = alloc(s1, "qxT", [128, 4, 128], BF16)
                kT = alloc(s1, "kT", [128, 4, 128], BF16)
                vb = alloc(s1, "vb", [128, 512], BF16)
                vz = alloc(s1, "vz", [128, 512], BF16)
                sg_ = alloc(s1, "sgl", [128, 512])
                sT = alloc(s1, "sT", [128, 4, 128], BF16)
                Sst = alloc(s1, "Sst", [128, 4, 128])
                Sbf = alloc(s1, "Sbf", [128, 4, 128], BF16)
                stats = alloc(s1, "stats", [128, 4, 6])
                mv = alloc(s1, "mv", [128, 4, 2])
                rs4 = alloc(s1, "rs4", [128, 4])
                nb4 = alloc(s1, "nb4", [128, 4])
                on = alloc(s1, "on", [128, 512])
                ret = alloc(s1, "ret", [128, 512], BF16)
                retT = alloc(s1, "retT", [128, 4, 128], BF16)
                S0 = [alloc(s1, "S0_%d" % i, [128, 4, 128]) for i in range(2)]
                S0b = [alloc(s1, "S0b_%d" % i, [128, 4, 128], BF16) for i in range(2)]
                qxm = [alloc(s1, "qxm_%d" % i, [128, 4, 64], BF16) for i in range(2)]
                vzb = [alloc(s1, "vzb_%d" % i, [64, 512], BF16) for i in range(2)]
                Sn = [alloc(s1, "Sn_%d" % i, [128, 4, 128]) for i in range(2)]
                bS0 = [Buf("S0_%d" % i, S.GL[i]) for i in range(2)]
                bS0b = [Buf("S0b_%d" % i) for i in range(2)]
                bqxm = [Buf("qxm%d" % i) for i in range(2)]
                bvzb = [Buf("vzb%d" % i) for i in range(2)]
                bSn = [Buf("Sn%d" % i, S.GS[i]) for i in range(2)]
                (b_t1q, b_t2q, b_t1k, b_t2k, b_qr, b_kr, b_qT, b_qxT, b_kT, b_vb, b_vz, b_sg, b_sT, b_Sst, b_Sbf,
                 b_st, b_on, b_ret, b_retT) = [Buf("p1b%d" % i) for i in range(19)]
                b_Sst.grp = S.GS[2]
                V(lambda e: e.memset(Sst[:], 0.0), w=[b_Sst])
                GC_P = [float(g ** 128) for g in GAM]
                GC_S = [float(g ** 4) for g in GAM]

                import os as _os
                _tl = _os.environ.get("K_TILES")
                _tiles = [int(v) for v in _tl.split(",") if int(v) >= 0] if _tl else list(range(NT))
                _step = int(_os.environ.get("K_STEP", "99"))
                hT1s = [hT1, alloc(s1, "hT1c", [128, 8, 128], BF16)]
                bhT1s = [bhT1, Buf("hT1c")]

                def p1b_norm(n):
                    npt_ = TS if n == 16 else 128
                    rmsnorm_hT(x[:npt_, n, :], bx[n], npt_, gm2[:], hT1s[n % 2], bhT1s[n % 2], scrB, 0, None, bg=b_t1)
                def p1b_proj(n):
                    npt_ = TS if n == 16 else 128
                    hTn, bhTn = hT1s[n % 2], bhT1s[n % 2]
                    for c in range(4):
                        for kt in range(8):
                            T(lambda e: e.matmul(PS[c][:npt_, :], lhsT=hTn[:, kt, 0:npt_],
                                                 rhs=wq[:, kt, c * 512:(c + 1) * 512], start=(kt == 0), stop=(kt == 7)),
                              r=[bhTn, b_wqc[c]], w=[bPS[c]])
                if _tiles:
                    p1b_norm(_tiles[0])
                    p1b_proj(_tiles[0])
                    load_wout()
                for ti_, n in enumerate(_tiles):
                    is_s = (n == 16)
                    npt = TS if is_s else 128
                    tok0 = n * 128
                    hT1, bhT1 = hT1s[n % 2], bhT1s[n % 2]
                    pob = [4, 6, 7, 1] if is_s else [4, 4, 4, 4]

                    def po(h):
                        if is_s:
                            return PS[pob[h]][:npt, 0:128]
                        return PS[4][:npt, h * 128:(h + 1) * 128]
                    if _step <= 1:
                        continue
                    for (bank, t1_, t2_, out_, bt1, bt2, bo) in ((0, t1q, t2q, qr, b_t1q, b_t2q, b_qr),
                                                               (1, t1k, t2k, kr, b_t1k, b_t2k, b_kr)):
                        pv4 = PS[bank][:npt, :].rearrange("p (h a j) -> p h a j", h=4, a=2)
                        t1v = t1_[:npt, :].rearrange("p (h a j) -> p h a j", h=4, a=2)
                        t2v = t2_[:npt, :].rearrange("p (h a j) -> p h a j", h=4, a=2)
                        cosb = rope[:npt, 0, n, :].unsqueeze(1).unsqueeze(1).to_broadcast([npt, 4, 2, 64])
                        sinb = rope[:npt, 1, n, :].unsqueeze(1).to_broadcast([npt, 4, 64])
                        nsinb = rope[:npt, 2, n, :].unsqueeze(1).to_broadcast([npt, 4, 64])
                        V(lambda e: e.tensor_tensor(out=t1v, in0=pv4, in1=cosb, op=ALU.mult), r=[bPS[bank], b_t1], w=[bt1])
                        V(lambda e: e.tensor_tensor(out=t2v[:, :, 0, :], in0=pv4[:, :, 1, :], in1=nsinb, op=ALU.mult),
                          r=[bPS[bank], b_t1], w=[bt2])
                        V(lambda e: e.tensor_tensor(out=t2v[:, :, 1, :], in0=pv4[:, :, 0, :], in1=sinb, op=ALU.mult),
                          r=[bPS[bank], b_t1], w=[bt2])
                        V(lambda e: e.tensor_tensor(out=out_[:npt, :], in0=t1_[:npt, :], in1=t2_[:npt, :], op=ALU.add),
                           r=[bt1, bt2], w=[bo])
                    if _step <= 2:
                        continue
                    A(lambda e: e.copy(out=vb[:npt, :], in_=PS[2][:npt, :]), r=[bPS[2]], w=[b_vb])
                    if not is_s:
                        V(lambda e: e.tensor_tensor(
                            out=vz[:, :].rearrange("p (h e) -> p h e", h=4),
                            in0=PS[2][:, :].rearrange("p (h e) -> p h e", h=4),
                            in1=zetap[:, :].unsqueeze(2).to_broadcast([128, 4, 128]), op=ALU.mult),
                          r=[bPS[2], b_t1], w=[b_vz])
                    A(lambda e: e.activation(out=sg_[:npt, :], in_=PS[3][:npt, :], func=AF.Silu), r=[bPS[3]], w=[b_sg])
                    pv4b = ps_bf(4)
                    pv5b = ps_bf(5)
                    for h in range(4):
                        T(lambda e: e.transpose(out=pv4b[:, h * 128:h * 128 + npt], in_=qr[:npt, h * 128:(h + 1) * 128],
                                                identity=identb[:npt, :npt]), r=[b_qr, b_const], w=[bPS[4]])
                    for h in range(4):
                        T(lambda e: e.transpose(out=pv5b[:, h * 128:h * 128 + npt], in_=kr[:npt, h * 128:(h + 1) * 128],
                                                identity=identb[:npt, :npt]), r=[b_kr, b_const], w=[bPS[5]])
                    q4 = pv4b[:, 0:512].rearrange("p (h t) -> p h t", h=4)[:, :, 0:npt]
                    k4 = pv5b[:, 0:512].rearrange("p (h t) -> p h t", h=4)[:, :, 0:npt]
                    A(lambda e: e.copy(out=qT[:, :, 0:npt], in_=q4), r=[bPS[4]], w=[b_qT])
                    xiv = (xi[:, 0:512].rearrange("p (h t) -> p h t", h=4) if not is_s
                           else xi[:, 512:768].rearrange("p (h t) -> p h t", h=4))
                    V(lambda e: e.tensor_tensor(out=qxT[:, :, 0:npt], in0=q4, in1=xiv, op=ALU.mult),
                      r=[bPS[4], b_t1], w=[b_qxT])
                    A(lambda e: e.copy(out=kT[:, :, 0:npt], in_=k4), r=[bPS[5]], w=[b_kT])
                    if _step <= 3:
                        continue
                    for h in range(4):
                        T(lambda e: e.matmul(PS[6][:npt, h * 128:h * 128 + npt], lhsT=kT[:, h, 0:npt], rhs=qT[:, h, 0:npt],
                                             start=True, stop=True), r=[b_kT, b_qT], w=[bPS[6]])
                    dmv = (dmp[:, :].rearrange("p (h t) -> p h t", h=4) if not is_s
                           else dms[:, :].rearrange("p (h t) -> p h t", h=4))
                    V(lambda e: e.tensor_tensor(out=sT[:npt, :, 0:npt],
                                                in0=PS[6][:npt, :].rearrange("p (h t) -> p h t", h=4)[:, :, 0:npt],
                                                in1=dmv, op=ALU.mult), r=[bPS[6], b_t1], w=[b_sT])
                    if _step <= 4:
                        continue
                    if ti_ + 1 < len(_tiles):
                        p1b_norm(_tiles[ti_ + 1])
                    for h in range(4):
                        only = (n == 0)
                        T(lambda e: e.matmul(po(h), lhsT=sT[:npt, h, 0:npt],
                                             rhs=vb[:npt, h * 128:(h + 1) * 128], start=True, stop=only),
                          r=[b_sT, b_vb], w=[bPS[pob[h]]])
                        if (not is_s) and n > 0:
                            T(lambda e: e.matmul(po(h), lhsT=qxT[:, h, 0:npt],
                                                 rhs=Sbf[:, h, :], start=False, stop=True),
                              r=[b_qxT, b_Sbf], w=[bPS[4]])
                    if not is_s:
                        for h in range(4):
                            T(lambda e: e.matmul(PS[5][:, h * 128:(h + 1) * 128], lhsT=kr[:, h * 128:(h + 1) * 128],
                                                 rhs=vz[:, h * 128:(h + 1) * 128], start=True, stop=True),
                              r=[b_kr, b_vz], w=[bPS[5]])
                        for h in range(4):
                            V(lambda e: e.scalar_tensor_tensor(out=Sst[:, h, :], in0=Sst[:, h, :], scalar=GC_P[h],
                                                               op0=ALU.mult, in1=PS[5][:, h * 128:(h + 1) * 128],
                                                               op1=ALU.add), r=[b_Sst, bPS[5]], w=[b_Sst])
                        A(lambda e: e.copy(out=Sbf[:], in_=Sst[:]), r=[b_Sst], w=[b_Sbf])
                        if n == NTP - 1:
                            S.dma("sp", O["o_ret_p"].rearrange("h d e -> d h e"), Sst[:], reads=[b_Sst])
                    else:
                        S.dma("sp", S0[0][:], I["sret"][0].rearrange("h d e -> d h e"), writes=[bS0[0]])
                        for b in range(16):
                            sl = b % 2
                            if b + 1 < 16:
                                S.dma("sp", S0[1 - sl][:], I["sret"][b + 1].rearrange("h d e -> d h e"), writes=[bS0[1 - sl]])
                            A(lambda e: e.copy(out=S0b[sl][:], in_=S0[sl][:]), r=[bS0[sl]], w=[bS0b[sl]])
                            V(lambda e: e.tensor_tensor(
                                out=qxm[sl][:], in0=qxT[:, :, 0:64],
                                in1=cmask[:, b * 64:(b + 1) * 64].unsqueeze(1).to_broadcast([128, 4, 64]), op=ALU.mult),
                              r=[b_qxT, b_t1], w=[bqxm[sl]])
                            for h in range(4):
                                T(lambda e: e.matmul(po(h), lhsT=qxm[sl][:, h, :],
                                                     rhs=S0b[sl][:, h, :], start=False, stop=(b == 15)),
                                  r=[bqxm[sl], bS0b[sl]], w=[bPS[pob[h]]])
                            V(lambda e: e.tensor_tensor(
                                out=vzb[sl][:, :].rearrange("p (h e) -> p h e", h=4),
                                in0=PS[2][:64, :].rearrange("p (h e) -> p h e", h=4),
                                in1=zs[:, b * 4:(b + 1) * 4].unsqueeze(2).to_broadcast([64, 4, 128]), op=ALU.mult),
                              r=[bPS[2], b_t1], w=[bvzb[sl]])
                            kvb = 5 if sl == 0 else 0
                            for h in range(4):
                                T(lambda e: e.matmul(PS[kvb][:, h * 128:(h + 1) * 128], lhsT=kr[:64, h * 128:(h + 1) * 128],
                                                     rhs=vzb[sl][:, h * 128:(h + 1) * 128], start=True, stop=True),
                                  r=[b_kr, bvzb[sl]], w=[bPS[kvb]])
                            for h in range(4):
                                V(lambda e: e.scalar_tensor_tensor(out=Sn[sl][:, h, :], in0=S0[sl][:, h, :], scalar=GC_S[h],
                                                                   op0=ALU.mult, in1=PS[kvb][:, h * 128:(h + 1) * 128],
                                                                   op1=ALU.add), r=[bS0[sl], bPS[kvb]], w=[bSn[sl]])
                            S.dma("sp", O["o_ret_s"][b].rearrange("h d e -> d h e"), Sn[sl][:], reads=[bSn[sl]])
                    if _step <= 5:
                        continue
                    if ti_ + 1 < len(_tiles):
                        p1b_proj(_tiles[ti_ + 1])
                    for h in range(4):
                        V(lambda e: e.bn_stats(out=stats[:npt, h, :], in_=po(h)),
                          r=[bPS[pob[h]]], w=[b_st])
                    for h in range(4):
                        V(lambda e: e.bn_aggr(out=mv[:npt, h, :], in_=stats[:npt, h, :]), r=[b_st], w=[b_st])
                    A(lambda e: e.activation(out=rs4[:npt, :], in_=mv[:npt, :, 1], func=AF.Sqrt, scale=1.0,
                                             bias=epsc[:npt, :]), r=[b_st, b_const], w=[b_st])
                    V(lambda e: e.reciprocal(out=rs4[:npt, :], in_=rs4[:npt, :]), r=[b_st], w=[b_st])
                    V(lambda e: e.scalar_tensor_tensor(out=nb4[:npt, :], in0=mv[:npt, :, 0], scalar=-1.0, op0=ALU.mult,
                                                       in1=rs4[:npt, :], op1=ALU.mult), r=[b_st], w=[b_st])
                    for h in range(4):
                        A(lambda e: e.activation(out=on[:npt, h * 128:(h + 1) * 128], in_=po(h),
                                                 func=AF.Identity, scale=rs4[:npt, h:h + 1], bias=nb4[:npt, h:h + 1]),
                          r=[bPS[pob[h]], b_st], w=[b_on])
                    V(lambda e: e.tensor_tensor(out=ret[:npt, :], in0=on[:npt, :], in1=sg_[:npt, :], op=ALU.mult),
                       r=[b_on, b_sg], w=[b_ret])
                    if _step <= 6:
                        continue
                    pv6b = ps_bf(6)
                    for h in range(4):
                        T(lambda e: e.transpose(out=pv6b[:, h * 128:h * 128 + npt], in_=ret[:npt, h * 128:(h + 1) * 128],
                                                identity=identb[:npt, :npt]), r=[b_ret, b_const], w=[bPS[6]])
                    A(lambda e: e.copy(out=retT[:, :, 0:npt],
                                       in_=pv6b[:, 0:512].rearrange("p (h t) -> p h t", h=4)[:, :, 0:npt]),
                      r=[bPS[6]], w=[b_retT])
                    if _step <= 7:
                        continue
                    bi_ = min(n // 4, 4)
                    for half in range(2):
                        bank = 6 + half
                        for kt in range(8):
                            lh = ssmT[:, kt, tok0:tok0 + npt] if kt < 4 else retT[:, kt - 4, 0:npt]
                            T(lambda e: e.matmul(PS[bank][:npt, :], lhsT=lh, rhs=wout[:, kt, half * 512:(half + 1) * 512],
                                                 start=(kt == 0), stop=(kt == 7)),
                              r=[b_ssmT[bi_], b_retT, b_wout], w=[bPS[bank]])
                        resid_add(n, npt, half, bank)
                S.barrier()
            if dbg:
                for n in range(NT):
                    S.dma("sp", O["dbg_x"][:, n, :], x[:, n, :], reads=[bx[n]])
            if stage <= 2:
                S.barrier()
                S.run_block()
                nck.__exit__(None, None, None)
                return nc

            with ExitStack() as s2:
                gx = alloc(s2, "gx", [128, 8])
                gmem = alloc(s2, "gmem", [128, 8])
                ones = alloc(s2, "ones", [128, 128], BF16)
                b_t2 = Buf("tab2")
                S.dma("sp", gx[:], I["g_xattn"].rearrange("(k p) -> p k", p=128), writes=[b_t2])
                S.dma("sp", gmem[:], I["g_mem"].rearrange("(k p) -> p k", p=128), writes=[b_t2])
                V(lambda e: e.memset(ones[:], 1.0), w=[b_t2])
                KT = alloc(s2, "KT", [128, 8, MEM], BF16)
                Vm = alloc(s2, "Vm", [128, 2, D], BF16)
                b_KT, b_Vm = Buf("KT"), Buf("Vm")
                wmq = alloc(s2, "wmq", [128, 8, D], BF16)
                b_wmq, b_wmo = Buf("wmq", S.GW[2]), Buf("wmo", S.GW[3])
                with ExitStack() as s2a:
                    wmk = alloc(s2a, "wmk", [128, 8, D], BF16)
                    wmv = alloc(s2a, "wmv", [128, 8, D], BF16)
                    b_wmk, b_wmv = Buf("wmk", S.GW[0]), Buf("wmv", S.GW[1])
                    load_w_bf16(wmk, b_wmk, I["w_mk"], 8, D, 0)
                    load_w_bf16(wmv, b_wmv, I["w_mv"], 8, D, 0)
                    load_w_bf16(wmq, b_wmq, I["w_mq"], 8, D, 0)
                    mx = [alloc(s2a, "mx%d" % i, [128, D]) for i in range(2)]
                    bmx = [Buf("mx%d" % i, S.GL[i]) for i in range(2)]
                    mhT = alloc(s2a, "mhT", [128, 8, MEM], BF16)
                    b_mhT = Buf("mhT")
                    mo = [alloc(s2a, "mo%d" % i, [128, D]) for i in range(2)]
                    bmo = [Buf("mo%d" % i, S.GS[i]) for i in range(2)]
                    _k2a = int(_os.environ.get("K2A", "9"))
                    for mt in range(2):
                        S.dma("sp", mx[mt][:], I["memp"][mt * 128:(mt + 1) * 128, :], writes=[bmx[mt]])
                        if _k2a >= 1:
                            rmsnorm_hT(mx[mt][:, :], bmx[mt], 128, gmem[:], mhT, b_mhT, scrB, mt * 128, None,
                                       ln=True, bg=b_t2)
                    oi = 0
                    for (wm, bwm, oname, isv) in ((wmk, b_wmk, "o_mk", False), (wmv, b_wmv, "o_mv", True)) if _k2a >= 2 else ():
                        for mt in range(2):
                            sl = oi % 2
                            oi += 1
                            for half in range(2):
                                bank = half
                                for kt in range(8):
                                    T(lambda e: e.matmul(PS[bank][:, :], lhsT=mhT[:, kt, mt * 128:(mt + 1) * 128],
                                                         rhs=wm[:, kt, half * 512:(half + 1) * 512], start=(kt == 0),
                                                         stop=(kt == 7)), r=[b_mhT, bwm], w=[bPS[bank]])
                                A(lambda e: e.copy(out=mo[sl][:, half * 512:(half + 1) * 512], in_=PS[bank][:, :]),
                                  r=[bPS[bank]], w=[bmo[sl]])
                                if isv:
                                    V(lambda e: e.tensor_copy(out=Vm[:, mt, half * 512:(half + 1) * 512], in_=PS[bank][:, :]),
                                      r=[bPS[bank]], w=[b_Vm])
                            S.dma("sp", O[oname][mt * 128:(mt + 1) * 128, :], mo[sl][:], reads=[bmo[sl]])
                    for j in range(8 if _k2a >= 3 else 0):
                        bank = 2 + (j % 2)
                        for kt in range(8):
                            T(lambda e: e.matmul(PS[bank][:, 0:MEM], lhsT=wmk[:, kt, j * 128:(j + 1) * 128],
                                                 rhs=mhT[:, kt, :], start=(kt == 0), stop=(kt == 7)),
                              r=[b_mhT, b_wmk], w=[bPS[bank]])
                        A(lambda e: e.copy(out=KT[:, j, :], in_=PS[bank][:, 0:MEM]), r=[bPS[bank]], w=[b_KT])
                    S.barrier()
                wmo = alloc(s2, "wmo", [128, 8, D], BF16)
                load_w_bf16(wmo, b_wmo, I["w_mo"], 8, D, 0)
                hT4s = [alloc(s2, "hT4_%d" % i, [128, 8, 512], BF16) for i in range(2)]
                b_hT4s = [Buf("hT4_%d" % i) for i in range(2)]
                qm4 = alloc(s2, "qm4", [128, 8, 512], BF16)
                oT4 = alloc(s2, "oT4", [128, 8, 512], BF16)
                eT4 = [alloc(s2, "eT4_%d" % i, [128, 2, 512], BF16) for i in range(2)]
                rdn4 = [alloc(s2, "rdn4_%d" % i, [128, 512]) for i in range(2)]
                b_qm4, b_oT4 = Buf("qm4"), Buf("oT4")
                b_eT4 = [Buf("eT4_%d" % i) for i in range(2)]
                b_rdn4 = [Buf("rdn4_%d" % i) for i in range(2)]
                Kb = [alloc(s2, "Kb%d" % i, [128, 2, D]) for i in range(2)]
                bKb = [Buf("Kb%d" % i, S.GL[i]) for i in range(2)]
                KbT = [alloc(s2, "KbT%d" % i, [128, 8, MEM], BF16) for i in range(2)]
                bKbT = [Buf("KbT%d" % i) for i in range(2)]
                Vb = [alloc(s2, "Vb%d" % i, [128, 2, D], BF16) for i in range(2)]
                bVb = [Buf("Vb%d" % i, S.GW[i]) for i in range(2)]
                eTs = alloc(s2, "eTs", [128, 2, 4, 64], BF16)
                b_eTs = Buf("eTs")
                qrot = [0]

                def q_proj(nc_, hT4, b_hT4):
                    for j in range(8):
                        bank = 5 + (qrot[0] % 3)
                        qrot[0] += 1
                        for kt in range(8):
                            T(lambda e: e.matmul(PS[bank][:, 0:nc_], lhsT=wmq[:, kt, j * 128:(j + 1) * 128],
                                                 rhs=hT4[:, kt, 0:nc_], start=(kt == 0), stop=(kt == 7)),
                              r=[b_wmq, b_hT4], w=[bPS[bank]])
                        A(lambda e: e.activation(out=qm4[:, j, 0:nc_], in_=PS[bank][:, 0:nc_], func=AF.Copy,
                                                 scale=1.0 / 16.0), r=[bPS[bank]], w=[b_qm4])

                def w_mo_resid(n, npt, c0):
                    for half in range(2):
                        bank = 5 + (qrot[0] % 3)
                        qrot[0] += 1
                        for j in range(8):
                            T(lambda e: e.matmul(PS[bank][:npt, :], lhsT=oT4[:, j, c0:c0 + npt],
                                                 rhs=wmo[:, j, half * 512:(half + 1) * 512], start=(j == 0), stop=(j == 7)),
                              r=[b_oT4, b_wmo], w=[bPS[bank]])
                        resid_add(n, npt, half, bank)

                def p2_norms(bi):
                    hT4, b_hT4 = hT4s[bi % 2], b_hT4s[bi % 2]
                    if bi < 4:
                        for ti in range(4):
                            n = bi * 4 + ti
                            rmsnorm_hT(x[:, n, :], bx[n], 128, gx[:], hT4, b_hT4, scrB, ti * 128, None, ln=True, bg=b_t2)
                    else:
                        rmsnorm_hT(x[:TS, 16, :], bx[16], TS, gx[:], hT4, b_hT4, scrB, 0, None, ln=True, bg=b_t2)

                p2_norms(0)
                for bi in range(4):
                    q_proj(512, hT4s[bi % 2], b_hT4s[bi % 2])
                    for h in range(4):
                        par = h % 2
                        for mt in range(2):
                            bank = mt
                            for dt_ in range(2):
                                T(lambda e: e.matmul(PS[bank][:, :], lhsT=KT[:, h * 2 + dt_, mt * 128:(mt + 1) * 128],
                                                     rhs=qm4[:, h * 2 + dt_, :], start=(dt_ == 0), stop=(dt_ == 1)),
                                  r=[b_KT, b_qm4], w=[bPS[bank]])
                            A(lambda e: e.activation(out=eT4[par][:, mt, :], in_=PS[bank][:, :], func=AF.Exp),
                              r=[bPS[bank]], w=[b_eT4[par]])
                        for mt in range(2):
                            T(lambda e: e.matmul(PS[2][:, :], lhsT=ones[:, :], rhs=eT4[par][:, mt, :], start=(mt == 0),
                                                 stop=(mt == 1)), r=[b_t2, b_eT4[par]], w=[bPS[2]])
                        A(lambda e: e.activation(out=rdn4[par][:, :], in_=PS[2][:, :], func=AF.Ln), r=[bPS[2]], w=[b_rdn4[par]])
                        A(lambda e: e.activation(out=rdn4[par][:, :], in_=rdn4[par][:, :], func=AF.Exp, scale=-1.0),
                          r=[b_rdn4[par]], w=[b_rdn4[par]])
                        for dt_ in range(2):
                            bank = 3 + dt_
                            j = h * 2 + dt_
                            for mt in range(2):
                                T(lambda e: e.matmul(PS[bank][:, :], lhsT=Vm[:, mt, j * 128:(j + 1) * 128],
                                                     rhs=eT4[par][:, mt, :], start=(mt == 0), stop=(mt == 1)),
                                  r=[b_Vm, b_eT4[par]], w=[bPS[bank]])
                            V(lambda e: e.tensor_tensor(out=oT4[:, j, :], in0=PS[bank][:, :], in1=rdn4[par][:, :], op=ALU.mult),
                              r=[bPS[bank], b_rdn4[par]], w=[b_oT4])
                    p2_norms(bi + 1)
                    for ti in range(4):
                        w_mo_resid(bi * 4 + ti, 128, ti * 128)
                n = 16
                q_proj(TS, hT4s[0], b_hT4s[0])
                rden_s = rdn4[0][:, 0:256].rearrange("p (h t) -> p h t", h=4)
                for b in range(16):
                    sl = b % 2
                    S.dma("sp", Kb[sl][:], I["ck"][b].rearrange("(mt p) d -> p mt d", p=128), writes=[bKb[sl]])
                    for q4 in range(4):
                        bank = 2 + (q4 % 2)
                        for i4 in range(4):
                            idx = q4 * 4 + i4
                            j, mt = idx // 2, idx % 2
                            T(lambda e: e.transpose(out=PS[bank][:, i4 * 128:(i4 + 1) * 128],
                                                    in_=Kb[sl][:, mt, j * 128:(j + 1) * 128], identity=identf[:]),
                              r=[bKb[sl], b_const], w=[bPS[bank]])
                        A(lambda e: e.copy(
                            out=KbT[sl][:, 2 * q4:2 * q4 + 2, :].rearrange("p j (m t) -> p j m t", m=2),
                            in_=PS[bank][:, :].rearrange("p (j m t) -> p j m t", j=2, m=2)),
                          r=[bPS[bank]], w=[bKbT[sl]])
                    for h in range(4):
                        for mt in range(2):
                            c0 = mt * 256 + h * 64 + 4 * b
                            for dt_ in range(2):
                                T(lambda e: e.matmul(PS[4][:, c0:c0 + 4],
                                                     lhsT=KbT[sl][:, h * 2 + dt_, mt * 128:(mt + 1) * 128],
                                                     rhs=qm4[:, h * 2 + dt_, 4 * b:4 * b + 4], start=(dt_ == 0),
                                                     stop=(dt_ == 1)), r=[bKbT[sl], b_qm4], w=[bPS[4]])
                A(lambda e: e.activation(out=eTs[:].rearrange("p m h t -> p (m h t)"), in_=PS[4][:, :], func=AF.Exp),
                  r=[bPS[4]], w=[b_eTs])
                for h in range(4):
                    for mt in range(2):
                        T(lambda e: e.matmul(PS[0][:, h * 64:(h + 1) * 64], lhsT=ones[:, :], rhs=eTs[:, mt, h, :],
                                             start=(mt == 0), stop=(mt == 1)), r=[b_t2, b_eTs], w=[bPS[0]])
                V(lambda e: e.reciprocal(out=rden_s, in_=PS[0][:, 0:256].rearrange("p (h t) -> p h t", h=4)),
                  r=[bPS[0]], w=[b_rdn4[0]])
                for b in range(16):
                    sl = b % 2
                    for mt in range(2):
                        S.dma("pool", Vb[sl][:, mt, :], I["cv"][b, mt * 128:(mt + 1) * 128, :], writes=[bVb[sl]])
                    for j in range(8):
                        h = j // 2
                        for mt in range(2):
                            T(lambda e: e.matmul(PS[1][:, j * 64 + 4 * b:j * 64 + 4 * b + 4],
                                                 lhsT=Vb[sl][:, mt, j * 128:(j + 1) * 128],
                                                 rhs=eTs[:, mt, h, 4 * b:4 * b + 4], start=(mt == 0), stop=(mt == 1)),
                              r=[bVb[sl], b_eTs], w=[bPS[1]])
                V(lambda e: e.tensor_tensor(
                    out=oT4[:, :, 0:64].rearrange("p (h a) t -> p h a t", a=2),
                    in0=PS[1][:, :].rearrange("p (h a t) -> p h a t", h=4, a=2),
                    in1=rden_s.unsqueeze(2).to_broadcast([128, 4, 2, 64]), op=ALU.mult),
                  r=[bPS[1], b_rdn4[0]], w=[b_oT4])
                w_mo_resid(16, TS, 0)
                S.barrier()
            if stage <= 3:
                if dbg:
                    for n in range(NT):
                        S.dma("sp", O["dbg_x"][:, n, :], x[:, n, :], reads=[bx[n]])
                S.barrier()
                S.run_block()
                nck.__exit__(None, None, None)
                return nc

            with ExitStack() as s3:
                gml = alloc(s3, "gml", [128, 8])
                b_t3 = Buf("tab3")
                S.dma("sp", gml[:], I["g_mlp"].rearrange("(k p) -> p k", p=128), writes=[b_t3])
                hTa = alloc(s3, "hTa", [128, 8, NTOK], BF16)
                b_hTa = [Buf("hTa%d" % n) for n in range(NT)]
                wup = [alloc(s3, "wup%d" % i, [128, 8, 512], BF16) for i in range(2)]
                wdn = [alloc(s3, "wdn%d" % i, [128, 4, D], BF16) for i in range(2)]
                bwup = [Buf("wup%d" % i, S.GW[i]) for i in range(2)]
                bwdn = [Buf("wdn%d" % i, S.GW[2 + i]) for i in range(2)]
                rl = [alloc(s3, "rl%d" % i, [128, 512]) for i in range(2)]
                brl = [Buf("rl%d" % i) for i in range(2)]
                aT = [alloc(s3, "aT%d" % i, [128, 4, 512], BF16) for i in range(2)]
                baT = [Buf("aT%d" % i) for i in range(2)]

                def load_fc(fc):
                    sl = fc % 2
                    for kt in range(8):
                        S.dma("pool", wup[sl][:, kt, :], I["w_up"][kt * 128:(kt + 1) * 128, fc * 512:(fc + 1) * 512],
                              writes=[bwup[sl]])
                    for ft in range(4):
                        S.dma("pool", wdn[sl][:, ft, :], I["w_down"][fc * 512 + ft * 128:fc * 512 + (ft + 1) * 128, :],
                              writes=[bwdn[sl]])
                load_fc(0)
                scrB["pb"] = [7, 6]
                for n in range(NT):
                    npt = TS if n == 16 else 128
                    rmsnorm_hT(x[:npt, n, :], bx[n], npt, gml[:], hTa, b_hTa[n], scrB, n * 128, None, ln=True, bg=b_t3)
                gf = alloc(s3, "gf", [128, D])
                b_gf = Buf("gf")
                S.dma("sp", gf[:], I["g_final"].rearrange("(o d) -> o d", o=1).partition_broadcast(128), writes=[b_gf])
                yst = [alloc(s3, "yst%d" % i, [128, D]) for i in range(3)]
                byst = [Buf("yst%d" % i, S.GS[i]) for i in range(3)]

                def final_norm(n):
                    npt = TS if n == 16 else 128
                    sl = n % 3
                    k4 = n % 2
                    sq, ss, rstd, bscr = scrB["sq"][k4], scrB["ss"][k4], scrB["rstd"][k4], scrB["ba"][k4]
                    A(lambda e: e.activation(out=sq[:npt, :], in_=x[:npt, n, :], func=AF.Square, accum_out=ss[:npt, :]),
                      r=[bx[n]], w=[bscr])
                    A(lambda e: e.activation(out=rstd[:npt, :], in_=ss[:npt, :], func=AF.Ln, scale=1.0 / D,
                                             bias=epsc[:npt, :]), r=[bscr, b_const], w=[bscr])
                    A(lambda e: e.activation(out=rstd[:npt, :], in_=rstd[:npt, :], func=AF.Exp, scale=-0.5),
                      r=[bscr], w=[bscr])
                    V(lambda e: e.scalar_tensor_tensor(out=yst[sl][:npt, :], in0=x[:npt, n, :], scalar=rstd[:npt, :],
                                                       op0=ALU.mult, in1=gf[:npt, :], op1=ALU.mult),
                      r=[bx[n], bscr, b_gf], w=[byst[sl]])
                    if n < 16:
                        S.dma("sp", O["yp"][n * 128:(n + 1) * 128, :], yst[sl][:, :], reads=[byst[sl]])
                    else:
                        S.dma("sp", O["ys"][:, :], yst[sl][:TS, :], reads=[byst[sl]])

                blocks3 = [(i * 512, 512) for i in range(4)] + [(SEQ, TS)]
                items = [(fc, blk) for fc in range(8) for blk in blocks3]
                ctr = {"ri": 0, "di": 0}
                load_fc(1)

                def mlp_up(i):
                    fc, (t0, nn) = items[i]
                    sl, asl = fc % 2, i % 2
                    tiles = list(range(t0 // 128, t0 // 128 + (nn + 127) // 128))
                    for ft in range(4):
                        bank = ft
                        for kt in range(8):
                            T(lambda e: e.matmul(PS[bank][:, 0:nn], lhsT=wup[sl][:, kt, ft * 128:(ft + 1) * 128],
                                                 rhs=hTa[:, kt, t0:t0 + nn], start=(kt == 0), stop=(kt == 7)),
                              r=[bwup[sl]] + [b_hTa[t] for t in tiles], w=[bPS[bank]])
                        rsl = ctr["ri"] % 2
                        ctr["ri"] += 1
                        A(lambda e: e.activation(out=rl[rsl][:, 0:nn], in_=PS[bank][:, 0:nn], func=AF.Relu),
                          r=[bPS[bank]], w=[brl[rsl]])
                        V(lambda e: e.tensor_tensor(out=aT[asl][:, ft, 0:nn], in0=rl[rsl][:, 0:nn], in1=rl[rsl][:, 0:nn],
                                                    op=ALU.mult), r=[brl[rsl]], w=[baT[asl]])

                def mlp_down(i):
                    fc, (t0, nn) = items[i]
                    sl, asl = fc % 2, i % 2
                    tiles = list(range(t0 // 128, t0 // 128 + (nn + 127) // 128))
                    for ti, tl in enumerate(tiles):
                        npt = TS if tl == 16 else 128
                        for half in range(2):
                            bank = 4 + (ctr["di"] % 4)
                            ctr["di"] += 1
                            for ft in range(4):
                                T(lambda e: e.matmul(PS[bank][:npt, :], lhsT=aT[asl][:, ft, ti * 128:ti * 128 + npt],
                                                     rhs=wdn[sl][:, ft, half * 512:(half + 1) * 512], start=(ft == 0),
                                                     stop=(ft == 3)), r=[baT[asl], bwdn[sl]], w=[bPS[bank]])
                            resid_add(tl, npt, half, bank)
                        if fc == 7:
                            final_norm(tl)

                mlp_up(0)
                for i in range(len(items)):
                    if i + 1 < len(items):
                        mlp_up(i + 1)
                    mlp_down(i)
                    fc = items[i][0]
                    if (i + 1 == len(items) or items[i + 1][0] != fc) and fc + 2 < 8:
                        load_fc(fc + 2)
                S.barrier()
            if dbg:
                for n in range(NT):
                    S.dma("sp", O["dbg_x"][:, n, :], x[:, n, :], reads=[bx[n]])
            S.barrier()
            S.run_block()
            nck.__exit__(None, None, None)
    return nc


_NC = None


def kernel(**inputs):
    global _NC
    if _NC is None:
        _NC = build()
    maps = _in_maps(inputs)
    res = run_bass_kernel_spmd(_NC, maps, core_ids=list(range(8)))
    R = res.results
    f = np.float32

    def cat(name, shape=None):
        return np.stack([np.asarray(R[c][name], f) for c in range(8)])
    y_prompt = cat("yp")
    y_sample = cat("ys").reshape(128, 4, D)
    s5r_p = cat("o_s5r_p")[None]
    s5i_p = cat("o_s5i_p")[None]
    ret_p = cat("o_ret_p")[None]
    mk_p = cat("o_mk").reshape(8, MEM, 4, 256)[None]
    mv_p = cat("o_mv").reshape(8, MEM, 4, 256)[None]
    s5r_s = cat("o_s5r_s").reshape(128, G, 64)[None]
    s5i_s = cat("o_s5i_s").reshape(128, G, 64)[None]
    ret_s = cat("o_ret_s").reshape(128, 4, 128, 128)[None]
    return (y_prompt, y_sample, s5r_p, s5i_p, ret_p, mk_p, mv_p, s5r_s, s5i_s, ret_s)


def _in_maps(inputs):
    cst = _consts()
    f = np.float32
    maps = []
    w = {}
    for k in W_NAMES:
        a = np.asarray(inputs[k], f)
        if k != "g_final":
            a = a[0]
        w[k] = np.ascontiguousarray(a.reshape(W_SHAPES[k]))
    for c in range(8):
        m = dict(w)
        m.update(cst)
        b0 = 16 * c
        m["xp"] = np.ascontiguousarray(np.asarray(inputs["x_prompt"], f)[c])
        m["xs"] = np.ascontiguousarray(np.asarray(inputs["x_sample"], f)[b0:b0 + 16].reshape(TS, D))
        m["memp"] = np.ascontiguousarray(np.asarray(inputs["mem_prompt"], f)[c])
        m["s5r"] = np.ascontiguousarray(np.asarray(inputs["state_s5_re"], f)[0, b0:b0 + 16].reshape(512, 64))
        m["s5i"] = np.ascontiguousarray(np.asarray(inputs["state_s5_im"], f)[0, b0:b0 + 16].reshape(512, 64))
        m["sret"] = np.ascontiguousarray(np.asarray(inputs["state_ret"], f)[0, b0:b0 + 16])
        m["ck"] = np.ascontiguousarray(np.asarray(inputs["cache_mem_k"], f)[0, b0:b0 + 16].reshape(16, MEM, D))
        m["cv"] = np.ascontiguousarray(np.asarray(inputs["cache_mem_v"], f)[0, b0:b0 + 16].reshape(16, MEM, D))
        maps.append(m)
    return maps
```

```python
import numpy as np
import concourse.bass as bass
import concourse.mybir as mybir
from concourse.bass_utils import run_bass_kernel_spmd
from contextlib import ExitStack

F32 = mybir.dt.float32
BF16 = mybir.dt.bfloat16
AF = mybir.ActivationFunctionType
ALU = mybir.AluOpType

D = 1024
SEQ = 2048
NTP = 16
TS = 64
NT = 17
NTOK = SEQ + TS
G = 32
DFF = 4096
MEM = 256
EPS = 1e-6
PAST = 16384.0
MAGIC = 12582912.0
TWO_PI = float(2.0 * np.pi)
ML = [7, 6, 5, 4, 3, 2, 1, 0, 1, 2, 3, 4, 5, 6, 7, 8, -4, 0.5]
K1 = len(ML)
I_A1, I_A8, I_A4, I_AM4, I_HALF = 8, 15, 3, 16, 17
GAM = [1.0 - 2.0 ** (-5.0 - h) for h in range(4)]


class Grp:
    __slots__ = ("sem", "cnt", "sealed")


class Buf:
    __slots__ = ("w", "r", "name", "grp", "ps")

    def __init__(self, name="", grp=None, ps=False):
        self.w = None
        self.r = []
        self.name = name
        self.grp = grp
        self.ps = ps


class _Rec:
    def __init__(self):
        self.call = None

    def __getattr__(self, name):
        def f(*a, **kw):
            self.call = (name, a, kw)
            return self
        return f


class Sched:
    ENG = ("pe", "dve", "act", "pool", "sp")

    def __init__(self, nc, stack, self_sync=("dve", "act", "pool")):
        self.nc = nc
        self.stack = stack
        self.prog = {k: [] for k in self.ENG}
        self.cnt = {k: 0 for k in self.ENG}
        self.waited = {k: {} for k in self.ENG}
        self.sem = {}
        self.nsem = 0
        for k in ("pe", "dve", "act", "pool"):
            self.sem[k] = self.new_sem("c_" + k)
        self.self_sync = set(self_sync)
        self.groups = []
        self.GC = self.group("gc")
        self.GP = self.group("gp")
        self.GW = [self.group("gw%d" % i) for i in range(4)]
        self.GX = self.group("gx")
        self.GL = [self.group("gl%d" % i) for i in range(2)]
        self.GS = [self.group("gs%d" % i) for i in range(3)]

    def group(self, name):
        g = Grp()
        g.sem = self.new_sem(name)
        g.cnt = 0
        g.sealed = False
        self.groups.append(g)
        return g

    def new_sem(self, name):
        self.nsem += 1
        assert self.nsem < 98, "too many semaphores"
        return self.stack.enter_context(self.nc.semaphore(name + "_%d" % self.nsem))

    def _waits(self, eng, deps):
        w = self.waited[eng]
        need = {}
        dd = []
        for d in deps:
            if isinstance(d, Grp):
                d.sealed = True
                dd.append((d.sem, d.cnt))
            else:
                dd.append(d)
        deps = dd
        for (s, v) in deps:
            if eng in self.sem and s is self.sem[eng] and eng not in self.self_sync:
                continue
            k = id(s)
            if w.get(k, 0) >= v:
                continue
            if k not in need or need[k][1] < v:
                need[k] = (s, v)
        for k, (s, v) in need.items():
            w[k] = v
            self.prog[eng].append(lambda e, s=s, v=v: e.wait_ge(s, v))

    def op(self, eng, fn, reads=(), writes=()):
        deps = []
        for b in reads:
            if b.w is not None:
                deps.append(b.w)
            if b.ps:
                mys = self.sem[eng]
                deps.extend(d for d in b.r if not (isinstance(d, tuple) and d[0] is mys))
        for b in writes:
            if b.w is not None:
                deps.append(b.w)
            deps.extend(b.r)
        self._waits(eng, deps)
        self.cnt[eng] += 1
        c = self.cnt[eng]
        s = self.sem[eng]
        rec = _Rec()
        fn(rec)
        name, a, kw = rec.call
        self.prog[eng].append(lambda e, name=name, a=a, kw=kw, s=s: getattr(e, name)(*a, **kw).then_inc(s, 1))
        for b in reads:
            b.r.append((s, c))
        for b in writes:
            b.w = (s, c)
            b.r = []

    def dma(self, q, out, in_, reads=(), writes=(), **kw):
        tb = writes[0] if writes else reads[0]
        g = tb.grp
        if g is None:
            g = self.GP if q == "pool" else (self.GC if writes else self.GS[0])
        deps = []
        for b in reads:
            if b.w is not None:
                deps.append(b.w)
        for b in writes:
            if b.w is not None and b.w is not g:
                deps.append(b.w)
            deps.extend(b.r)
        self._waits(q, deps)
        if g.sealed and g.cnt > 0:
            self._waits(q, [(g.sem, g.cnt)])
        g.sealed = False
        g.cnt += 16
        s = g.sem
        self.prog[q].append(
            lambda e, out=out, in_=in_, s=s, kw=kw: e.dma_start(out=out, in_=in_, **kw).then_inc(s, 16))
        for b in reads:
            b.r.append(g)
        for b in writes:
            b.w = g
            b.r = []

    def barrier(self, engines=None):
        deps = [(self.sem[k], self.cnt[k]) for k in ("pe", "dve", "act", "pool") if self.cnt[k] > 0]
        deps += [g for g in self.groups if g.cnt > 0]
        for e in (engines or self.ENG):
            self._waits(e, deps)

    def run_block(self):
        nc = self.nc
        with nc.Block() as block:
            @block.sync
            def _(e):
                for t in self.prog["sp"]:
                    t(e)

            @block.tensor
            def _(e):
                for t in self.prog["pe"]:
                    t(e)

            @block.vector
            def _(e):
                for t in self.prog["dve"]:
                    t(e)

            @block.scalar
            def _(e):
                for t in self.prog["act"]:
                    t(e)

            @block.gpsimd
            def _(e):
                for t in self.prog["pool"]:
                    t(e)


_CONSTS = None


def _consts():
    global _CONSTS
    if _CONSTS is not None:
        return _CONSTS
    f = np.float32
    c = {}
    c["c_ident"] = np.eye(128, dtype=f)
    m = np.zeros((8, 128, 240), f)
    for a in range(8):
        for i in range(16):
            m[a, 16 * a + i, 112 + i] = 1.0
    c["c_masters"] = m
    ml = np.array(ML, np.float64)
    rows = np.concatenate([ml / (2 * np.pi), ml, 8.0 * (np.arange(64) + 1) / (2 * np.pi)])
    c["c_rows"] = rows.astype(f)[None, :]
    sg = np.zeros((128, 2), f)
    sg[:64, 0] = 1.0
    sg[64:, 0] = -1.0
    sg[:64, 1] = -1.0
    sg[64:, 1] = 1.0
    c["c_sgn"] = sg
    inv = (f(10000.0) ** (-(np.arange(64, dtype=f) / f(64.0)))).astype(f)
    pos = np.zeros((128, NT), f)
    for n in range(NTP):
        pos[:, n] = 128 * n + np.arange(128)
    pos[:64, 16] = PAST + (np.arange(64) % 4)
    ang = (pos[:, :, None] * inv[None, None, :]).astype(f).astype(np.float64)
    c["c_rope"] = np.stack([np.cos(ang), np.sin(ang), -np.sin(ang)]).astype(f)
    lg = np.log(np.array(GAM, np.float64))
    sc = 128.0 ** -0.5
    idx = np.arange(128)
    dm = np.zeros((128, 4, 128), np.float64)
    diff = idx[None, :] - idx[:, None]
    for h in range(4):
        dm[:, h, :] = np.where(diff >= 0, np.exp(np.maximum(diff, 0) * lg[h]), 0.0) * sc
    c["c_dmask_p"] = dm.reshape(128, 512).astype(f)
    ds_ = np.zeros((64, 4, 64), np.float64)
    r = np.arange(64)
    bb = r // 4
    tt = r % 4
    same = bb[:, None] == bb[None, :]
    dts = tt[None, :] - tt[:, None]
    for h in range(4):
        ds_[:, h, :] = np.where(same & (dts >= 0), np.exp(np.maximum(dts, 0) * lg[h]), 0.0) * sc
    c["c_dmask_s"] = ds_.reshape(64, 256).astype(f)
    xi_p = np.stack([np.exp((idx + 1.0) * lg[h]) * sc for h in range(4)])
    xi_s = np.stack([np.exp((tt + 1.0) * lg[h]) * sc for h in range(4)])
    c["c_xi"] = np.concatenate([xi_p.reshape(-1), xi_s.reshape(-1)]).astype(f)[None, :]
    zp = np.stack([np.exp((127.0 - idx) * lg[h]) for h in range(4)], axis=1)
    c["c_zeta_p"] = zp.astype(f)
    zs = np.zeros((64, 16, 4), np.float64)
    for h in range(4):
        for b in range(16):
            zs[:, b, h] = np.where(bb == b, np.exp((3.0 - tt) * lg[h]), 0.0)
    c["c_zs"] = zs.reshape(64, 64).astype(f)
    cm = np.zeros((16, 64), f)
    for b in range(16):
        cm[b, 4 * b:4 * b + 4] = 1.0
    c["c_cmask"] = cm.reshape(1, -1)
    _CONSTS = c
    return c


W_NAMES = ["g_mix", "w_in", "lam_re", "lam_im", "log_dt", "b_re", "b_im", "c_re", "c_im", "d_skip", "w_glu",
           "ret_gn", "w_out", "g_xattn", "g_mem", "w_mq", "w_mk", "w_mv", "w_mo", "g_mlp", "w_up", "w_down",
           "g_final"]
W_SHAPES = {"g_mix": [D], "w_in": [D, 2560], "lam_re": [G, 64], "lam_im": [G, 64], "log_dt": [G],
            "b_re": [G, 64, 16], "b_im": [G, 64, 16], "c_re": [G * 16, 64], "c_im": [G * 16, 64], "d_skip": [512],
            "w_glu": [512, 512], "ret_gn": [512], "w_out": [D, D], "g_xattn": [D], "g_mem": [D], "w_mq": [D, D],
            "w_mk": [D, D], "w_mv": [D, D], "w_mo": [D, D], "g_mlp": [D], "w_up": [D, DFF], "w_down": [DFF, D],
            "g_final": [D]}
IN_SHAPES = {"xp": [SEQ, D], "xs": [TS, D], "memp": [MEM, D], "s5r": [512, 64], "s5i": [512, 64],
             "sret": [16, 4, 128, 128], "ck": [16, MEM, D], "cv": [16, MEM, D]}
OUT_SHAPES = {"yp": [SEQ, D], "ys": [TS, D], "o_s5r_p": [G, 64], "o_s5i_p": [G, 64], "o_ret_p": [4, 128, 128],
              "o_mk": [MEM, D], "o_mv": [MEM, D], "o_s5r_s": [512, 64], "o_s5i_s": [512, 64],
              "o_ret_s": [16, 4, 128, 128]}


def build(stage=99, dbg=False):
    nc = bass.Bass("TRN2", target_bir_lowering=False)
    cst = _consts()
    I = {}
    for k, shp in list(IN_SHAPES.items()) + list(W_SHAPES.items()):
        I[k] = nc.dram_tensor(k, shp, F32, kind="ExternalInput").ap()
    for k, v in cst.items():
        I[k] = nc.dram_tensor(k, list(v.shape), F32, kind="ExternalInput").ap()
    O = {}
    for k, shp in OUT_SHAPES.items():
        O[k] = nc.dram_tensor(k, shp, F32, kind="ExternalOutput").ap()
    if dbg:
        O["dbg_ssm"] = nc.dram_tensor("dbg_ssm", [128, 4, NTOK], F32, kind="ExternalOutput").ap()
        O["dbg_x"] = nc.dram_tensor("dbg_x", [128, NT, D], F32, kind="ExternalOutput").ap()

    with ExitStack() as st:
        S = Sched(nc, st)

        def alloc(stack, name, shape, dt=F32):
            return stack.enter_context(nc.sbuf_tensor(name, shape, dt))

        def palloc(stack, name, shape, dt=F32):
            return stack.enter_context(nc.psum_tensor(name, shape, dt))

        def V(fn, r=(), w=()):
            S.op("dve", fn, reads=r, writes=w)

        def A(fn, r=(), w=()):
            S.op("act", fn, reads=r, writes=w)

        import os as _os0
        _nopool = _os0.environ.get("K_NOPOOL") == "1"

        def PL(fn, r=(), w=()):
            S.op("dve" if _nopool else "pool", fn, reads=r, writes=w)

        def T(fn, r=(), w=()):
            S.op("pe", fn, reads=r, writes=w)

        nck = nc.allow_non_contiguous_dma(reason="small param layout loads")
        nck.__enter__()

        identb = alloc(st, "identb", [128, 128], BF16)
        identf = alloc(st, "identf", [128, 128], F32)
        sgn = alloc(st, "sgn", [128, 2])
        epsc = alloc(st, "epsc", [128, 1])
        ssmT = alloc(st, "ssmT", [128, 4, NTOK], BF16)
        b_const = Buf("const")
        b_ssmT = [Buf("ssmT%d" % i) for i in range(5)]
        b_constp = Buf("constp")
        S.dma("pool", identb[:], I["c_ident"][:, :], writes=[b_constp])
        S.dma("sp", identf[:], I["c_ident"][:, :], writes=[b_const])
        S.dma("sp", sgn[:], I["c_sgn"][:, :], writes=[b_const])
        V(lambda e: e.memset(epsc[:], EPS), r=[b_constp], w=[b_const])
        PS = [palloc(st, "ps%d" % i, [128, 512], F32) for i in range(8)]
        bPS = [Buf("ps%d" % i, ps=True) for i in range(8)]

        def ps_bf(i):
            return PS[i][:].bitcast(BF16)

        def make_scr(stack, tag, pbanks):
            d = {"i": 0, "pb": list(pbanks)}
            d["sq"] = [alloc(stack, "sq%s" % tag, [128, D], BF16)] * 2
            d["ss"] = [alloc(stack, "ss%s%d" % (tag, i), [128, 1]) for i in range(2)]
            d["rstd"] = [alloc(stack, "rstd%s%d" % (tag, i), [128, 1]) for i in range(2)]
            d["hb"] = [alloc(stack, "hb%s%d" % (tag, i), [128, D], BF16) for i in range(2)]
            d["ba"] = [Buf("ba%s%d" % (tag, i)) for i in range(2)]
            d["bh"] = [Buf("bh%s%d" % (tag, i)) for i in range(2)]
            return d

        def rmsnorm_hT(xt_ap, bx, npart, gcol, hT_ap, bhT, scr, col0, ph, ln=False, bg=None, out4=None):
            k = scr["i"] % 2
            pbank = scr["pb"][scr["i"] % len(scr["pb"])]
            scr["i"] += 1
            sq, ss, rstd, hb = scr["sq"][k], scr["ss"][k], scr["rstd"][k], scr["hb"][k]
            ba, bh = scr["ba"][k], scr["bh"][k]
            A(lambda e: e.activation(out=sq[:npart, :], in_=xt_ap, func=AF.Square, accum_out=ss[:npart, :]),
              r=[bx], w=[ba])
            if ln:
                A(lambda e: e.activation(out=rstd[:npart, :], in_=ss[:npart, :], func=AF.Ln, scale=1.0 / D,
                                         bias=epsc[:npart, :]), r=[ba, b_const], w=[ba])
                A(lambda e: e.activation(out=rstd[:npart, :], in_=rstd[:npart, :], func=AF.Exp, scale=-0.5),
                  r=[ba], w=[ba])
            else:
                A(lambda e: e.activation(out=rstd[:npart, :], in_=ss[:npart, :], func=AF.Sqrt, scale=1.0 / D,
                                         bias=epsc[:npart, :]), r=[ba, b_const], w=[ba])
                V(lambda e: e.reciprocal(out=rstd[:npart, :], in_=rstd[:npart, :]), r=[ba], w=[ba])
            V(lambda e: e.tensor_scalar(out=hb[:npart, :], in0=xt_ap, scalar1=rstd[:npart, :], scalar2=None,
                                        op0=ALU.mult), r=[bx, ba], w=[bh])
            pv = ps_bf(pbank)
            for kt in range(8):
                T(lambda e, kt=kt: e.transpose(out=pv[:, kt * 128:kt * 128 + npart],
                                               in_=hb[:npart, kt * 128:(kt + 1) * 128],
                                               identity=identb[:npart, :npart]),
                  r=[bh, b_const], w=[bPS[pbank]])
            if out4 is not None:
                V(lambda e: e.tensor_tensor(
                    out=out4, in0=pv.rearrange("p (k c s) -> p k s c", k=8, s=8),
                    in1=gcol.unsqueeze(2).unsqueeze(3).to_broadcast([128, 8, 8, 16]), op=ALU.mult),
                  r=[bPS[pbank], b_const] + ([bg] if bg is not None else []), w=[bhT])
                return
            V(lambda e: e.tensor_tensor(
                out=hT_ap[:, :, col0:col0 + npart],
                in0=pv.rearrange("p (k t) -> p k t", k=8)[:, :, 0:npart],
                in1=gcol.unsqueeze(2).to_broadcast([128, 8, npart]), op=ALU.mult),
              r=[bPS[pbank], b_const] + ([bg] if bg is not None else []), w=[bhT])

        def load_w_bf16(dst, bdst, src, kt_n, ncols, c0=0):
            for kt in range(kt_n):
                for cc in range(0, ncols, 1024):
                    w_ = min(1024, ncols - cc)
                    S.dma("pool", dst[:, kt, cc:cc + w_], src[kt * 128:(kt + 1) * 128, c0 + cc:c0 + cc + w_],
                          writes=[bdst])

        with ExitStack() as sa:
            Wt = alloc(sa, "Wt", [128, G, 128], BF16)
            Wst = alloc(sa, "Wst", [128, G, 128], BF16)
            Tt = alloc(sa, "Tt", [128, G, 128], BF16)
            Vt = alloc(sa, "Vt", [128, G, 128], BF16)
            COSR = alloc(sa, "COSR", [128, G, 64])
            SINR = alloc(sa, "SINR", [128, G, 64])
            masters = alloc(sa, "masters", [128, 8, 240], BF16)
            AR = alloc(sa, "AR", [128, G, K1])
            AI = alloc(sa, "AI", [128, G, K1])
            MAGJ = alloc(sa, "MAGJ", [128, G, K1])
            DS = alloc(sa, "DS", [128, G])
            gm = alloc(sa, "gm", [128, 8])
            winu = alloc(sa, "winu", [128, 8, 512], BF16)
            wglu = alloc(sa, "wglu", [128, 4, 512], BF16)
            b_tab = Buf("s5tab")
            b_winu = Buf("winu", S.GW[0])
            b_wglu = Buf("wglu", S.GW[1])
            b_tabp = Buf("s5tabp")
            S.dma("pool", masters[:], I["c_masters"].rearrange("a k j -> k a j"), writes=[b_tabp])
            S.dma("sp", gm[:], I["g_mix"].rearrange("(k p) -> p k", p=128), writes=[b_tab])
            for tau in range(8):
                S.dma("sp", DS[16 * tau:16 * tau + 16, :], I["d_skip"].rearrange("(g h) -> h g", h=16),
                      writes=[b_tab])
            load_w_bf16(winu, b_winu, I["w_in"], 8, 512, 0)
            load_w_bf16(wglu, b_wglu, I["w_glu"], 4, 512, 0)

            with ExitStack() as s0:
                rows = alloc(s0, "rows", [128, 2 * K1 + 64])
                LR = alloc(s0, "LR", [128, G])
                LI = alloc(s0, "LI", [128, G])
                DT = alloc(s0, "DT", [128, G])
                LRDT = alloc(s0, "LRDT", [128, G])
                LIDT = alloc(s0, "LIDT", [128, G])
                tA = alloc(s0, "tA", [128, G, 64])
                tB = alloc(s0, "tB", [128, G, 64])
                tC = alloc(s0, "tC", [128, G, 64])
                COSJ = alloc(s0, "COSJ", [128, G, K1])
                SINJ = alloc(s0, "SINJ", [128, G, K1])
                sm = alloc(s0, "sm", [128, 12, G])
                Br1 = alloc(s0, "Br1", [128, G, 16])
                Br2 = alloc(s0, "Br2", [128, G, 16])
                BB1 = alloc(s0, "BB1", [128, G, 16])
                BB2 = alloc(s0, "BB2", [128, G, 16])
                tb1 = alloc(s0, "tb1", [128, G, 16])
                big1 = alloc(s0, "big1", [128, G, 128])
                big2 = alloc(s0, "big2", [128, G, 128])
                WTpad = alloc(s0, "WTpad", [128, G, 256], BF16)
                WTs = alloc(s0, "WTs", [128, G, 128], BF16)
                CN1 = alloc(s0, "CN1", [128, 4, 128])
                CN2 = alloc(s0, "CN2", [128, 4, 128])
                CMa = alloc(s0, "CMa", [128, G, 16])
                CMb = alloc(s0, "CMb", [128, G, 16])
                CMab = alloc(s0, "CMab", [128, G, 16], BF16)
                b0 = Buf("p0in")
                bt = Buf("p0tmp")
                S.dma("sp", rows[:], I["c_rows"][0:1, :].partition_broadcast(128), writes=[b0])
                for hf in range(2):
                    S.dma("sp", LR[64 * hf:64 * hf + 64, :], I["lam_re"].rearrange("g p -> p g"), writes=[b0])
                    S.dma("sp", LI[64 * hf:64 * hf + 64, :], I["lam_im"].rearrange("g p -> p g"), writes=[b0])
                S.dma("sp", DT[:], I["log_dt"].rearrange("(o g) -> o g", o=1).partition_broadcast(128), writes=[b0])
                S.dma("sp", Br1[0:64], I["b_re"].rearrange("g p h -> p g h"), writes=[b0])
                S.dma("sp", Br1[64:128], I["b_im"].rearrange("g p h -> p g h"), writes=[b0])
                S.dma("sp", Br2[0:64], I["b_im"].rearrange("g p h -> p g h"), writes=[b0])
                S.dma("sp", Br2[64:128], I["b_re"].rearrange("g p h -> p g h"), writes=[b0])
                S.dma("sp", CN1[:, :, 0:64], I["c_re"].rearrange("(c r) p -> r c p", r=128), writes=[b0])
                S.dma("sp", CN1[:, :, 64:128], I["c_im"].rearrange("(c r) p -> r c p", r=128), writes=[b0])
                S.dma("sp", CN2[:, :, 0:64], I["c_im"].rearrange("(c r) p -> r c p", r=128), writes=[b0])
                S.dma("sp", CN2[:, :, 64:128], I["c_re"].rearrange("(c r) p -> r c p", r=128), writes=[b0])
                MT1 = rows[:, 0:K1]
                MLr = rows[:, K1:2 * K1]
                MRT = rows[:, 2 * K1:2 * K1 + 64]
                A(lambda e: e.activation(out=DT[:], in_=DT[:], func=AF.Exp), r=[b0], w=[b0])
                V(lambda e: e.tensor_tensor(out=LRDT[:], in0=LR[:], in1=DT[:], op=ALU.mult), r=[b0], w=[bt])
                V(lambda e: e.tensor_tensor(out=LIDT[:], in0=LI[:], in1=DT[:], op=ALU.mult), r=[b0], w=[bt])

                def trig(mt_ap, K, cos_out, sin_out):
                    shp = [128, G, K]
                    a_, b_, c_ = tA[:, :, 0:K], tB[:, :, 0:K], tC[:, :, 0:K]
                    V(lambda e: e.tensor_tensor(out=a_, in0=LIDT[:].unsqueeze(2).to_broadcast(shp),
                                                in1=mt_ap.unsqueeze(1).to_broadcast(shp), op=ALU.mult),
                      r=[bt, b0], w=[bt])
                    for (outp, off) in ((sin_out, 0.0), (cos_out, 0.25)):
                        if outp is None:
                            continue
                        V(lambda e, off=off: e.tensor_scalar(out=c_, in0=a_, scalar1=off, scalar2=None,
                                                             op0=ALU.add), r=[bt], w=[bt])
                        V(lambda e: e.tensor_scalar(out=b_, in0=c_, scalar1=MAGIC, scalar2=None, op0=ALU.add),
                          r=[bt], w=[bt])
                        V(lambda e: e.tensor_scalar(out=b_, in0=b_, scalar1=MAGIC, scalar2=None, op0=ALU.subtract),
                          r=[bt], w=[bt])
                        V(lambda e: e.tensor_tensor(out=c_, in0=c_, in1=b_, op=ALU.subtract), r=[bt], w=[bt])
                        A(lambda e, outp=outp: e.activation(out=outp, in_=c_, func=AF.Sin, scale=TWO_PI),
                          r=[bt], w=[b_tab])

                trig(MT1, K1, COSJ[:], SINJ[:])
                trig(MRT, 64, COSR[:], SINR[:])
                shpj = [128, G, K1]
                V(lambda e: e.tensor_tensor(out=MAGJ[:], in0=LRDT[:].unsqueeze(2).to_broadcast(shpj),
                                            in1=MLr.unsqueeze(1).to_broadcast(shpj), op=ALU.mult),
                  r=[bt, b0], w=[b_tab])
                A(lambda e: e.activation(out=MAGJ[:], in_=MAGJ[:], func=AF.Exp), r=[b_tab], w=[b_tab])
                V(lambda e: e.tensor_tensor(out=AR[:], in0=MAGJ[:], in1=COSJ[:], op=ALU.mult), r=[b_tab], w=[b_tab])
                V(lambda e: e.tensor_tensor(out=AI[:], in0=MAGJ[:], in1=SINJ[:], op=ALU.mult), r=[b_tab], w=[b_tab])
                em1, shalf, cm1, am1r, ai1, den, fr, fi, t0_, t1_ = [sm[:, i, :] for i in range(10)]
                x_ = LRDT[:]
                V(lambda e: e.tensor_scalar(out=em1, in0=x_, scalar1=0.2, scalar2=1.0, op0=ALU.mult, op1=ALU.add),
                  r=[bt], w=[bt])
                for cf in (0.25, 1.0 / 3.0, 0.5):
                    V(lambda e: e.tensor_tensor(out=em1, in0=em1, in1=x_, op=ALU.mult), r=[bt], w=[bt])
                    V(lambda e, cf=cf: e.tensor_scalar(out=em1, in0=em1, scalar1=cf, scalar2=1.0, op0=ALU.mult,
                                                       op1=ALU.add), r=[bt], w=[bt])
                V(lambda e: e.tensor_tensor(out=em1, in0=em1, in1=x_, op=ALU.mult), r=[bt], w=[bt])
                V(lambda e: e.tensor_copy(out=shalf, in_=SINJ[:, :, I_HALF]), r=[b_tab], w=[bt])
                V(lambda e: e.scalar_tensor_tensor(out=cm1, in0=shalf, scalar=-2.0, op0=ALU.mult, in1=shalf,
                                                   op1=ALU.mult), r=[bt], w=[bt])
                V(lambda e: e.tensor_tensor(out=am1r, in0=em1, in1=COSJ[:, :, I_A1], op=ALU.mult), r=[bt, b_tab], w=[bt])
                V(lambda e: e.tensor_tensor(out=am1r, in0=am1r, in1=cm1, op=ALU.add), r=[bt], w=[bt])
                V(lambda e: e.tensor_copy(out=ai1, in_=AI[:, :, I_A1]), r=[b_tab], w=[bt])
                V(lambda e: e.tensor_tensor(out=den, in0=LR[:], in1=LR[:], op=ALU.mult), r=[b0], w=[bt])
                V(lambda e: e.tensor_tensor(out=t0_, in0=LI[:], in1=LI[:], op=ALU.mult), r=[b0], w=[bt])
                V(lambda e: e.tensor_tensor(out=den, in0=den, in1=t0_, op=ALU.add), r=[bt], w=[bt])
                V(lambda e: e.reciprocal(out=den, in_=den), r=[bt], w=[bt])
                V(lambda e: e.tensor_tensor(out=fr, in0=am1r, in1=LR[:], op=ALU.mult), r=[bt, b0], w=[bt])
                V(lambda e: e.tensor_tensor(out=t0_, in0=ai1, in1=LI[:], op=ALU.mult), r=[bt, b0], w=[bt])
                V(lambda e: e.tensor_tensor(out=fr, in0=fr, in1=t0_, op=ALU.add), r=[bt], w=[bt])
                V(lambda e: e.tensor_tensor(out=fr, in0=fr, in1=den, op=ALU.mult), r=[bt], w=[bt])
                V(lambda e: e.tensor_tensor(out=fi, in0=ai1, in1=LR[:], op=ALU.mult), r=[bt, b0], w=[bt])
                V(lambda e: e.tensor_tensor(out=t0_, in0=am1r, in1=LI[:], op=ALU.mult), r=[bt, b0], w=[bt])
                V(lambda e: e.tensor_tensor(out=fi, in0=fi, in1=t0_, op=ALU.subtract), r=[bt], w=[bt])
                V(lambda e: e.tensor_tensor(out=fi, in0=fi, in1=den, op=ALU.mult), r=[bt], w=[bt])
                V(lambda e: e.tensor_scalar(out=Br2[:], in0=Br2[:], scalar1=sgn[:, 1:2], scalar2=None, op0=ALU.mult),
                  r=[b0, b_const], w=[b0])
                shb = [128, G, 16]
                frb = fr.unsqueeze(2).to_broadcast(shb)
                fib = fi.unsqueeze(2).to_broadcast(shb)
                V(lambda e: e.tensor_tensor(out=BB1[:], in0=Br1[:], in1=frb, op=ALU.mult), r=[b0, bt], w=[bt])
                V(lambda e: e.tensor_tensor(out=tb1[:], in0=Br2[:], in1=fib, op=ALU.mult), r=[b0, bt], w=[bt])
                V(lambda e: e.tensor_tensor(out=BB1[:], in0=BB1[:], in1=tb1[:], op=ALU.add), r=[bt], w=[bt])
                V(lambda e: e.tensor_tensor(out=BB2[:], in0=Br2[:], in1=frb, op=ALU.mult), r=[b0, bt], w=[bt])
                V(lambda e: e.tensor_tensor(out=tb1[:], in0=Br1[:], in1=fib, op=ALU.mult), r=[b0, bt], w=[bt])
                V(lambda e: e.tensor_tensor(out=BB2[:], in0=BB2[:], in1=tb1[:], op=ALU.subtract), r=[bt], w=[bt])
                sh4 = [128, G, 8, 16]
                arv = AR[:, :, 0:8].unsqueeze(3).to_broadcast(sh4)
                aiv = AI[:, :, 0:8].unsqueeze(3).to_broadcast(sh4)
                bb1 = BB1[:].unsqueeze(2).to_broadcast(sh4)
                bb2 = BB2[:].unsqueeze(2).to_broadcast(sh4)
                g1 = big1[:].rearrange("p g (s h) -> p g s h", s=8)
                g2 = big2[:].rearrange("p g (s h) -> p g s h", s=8)
                V(lambda e: e.memset(WTpad[:], 0.0), w=[bt])
                V(lambda e: e.tensor_tensor(out=g1, in0=arv, in1=bb1, op=ALU.mult), r=[b_tab, bt], w=[bt])
                V(lambda e: e.tensor_tensor(out=g2, in0=aiv, in1=bb2, op=ALU.mult), r=[b_tab, bt], w=[bt])
                V(lambda e: e.tensor_tensor(out=WTpad[:, :, 0:128], in0=big1[:], in1=big2[:], op=ALU.add),
                  r=[bt], w=[bt])
                V(lambda e: e.tensor_tensor(out=g1, in0=arv, in1=bb2, op=ALU.mult), r=[b_tab, bt], w=[bt])
                V(lambda e: e.tensor_tensor(out=g2, in0=aiv, in1=bb1, op=ALU.mult), r=[b_tab, bt], w=[bt])
                V(lambda e: e.tensor_tensor(out=WTs[:], in0=big1[:], in1=big2[:], op=ALU.subtract), r=[bt], w=[bt])
                for (src_fn, dstt) in ((lambda g: WTpad[:, g, 0:128], Wt), (lambda g: WTs[:, g, :], Wst)):
                    for gq in range(8):
                        bank = gq % 2
                        pv = ps_bf(bank)
                        for j in range(4):
                            g = gq * 4 + j
                            T(lambda e, g=g, j=j, pv=pv, src_fn=src_fn: e.transpose(
                                out=pv[:, j * 128:(j + 1) * 128], in_=src_fn(g), identity=identb[:]),
                              r=[bt, b_const], w=[bPS[bank]])
                        A(lambda e, gq=gq, pv=pv, dstt=dstt: e.copy(
                            out=dstt[:, gq * 4:gq * 4 + 4, :], in_=pv[:, 0:512].rearrange("p (j c) -> p j c", j=4)),
                          r=[bPS[bank]], w=[b_tab])
                for (CN, CM, col) in ((CN1, CMa, 0), (CN2, CMb, None)):
                    for c4 in range(4):
                        bank = 2 + (c4 % 2)
                        T(lambda e, CN=CN, c4=c4, bank=bank: e.transpose(out=PS[bank][:, 0:128], in_=CN[:, c4, :],
                                                                         identity=identf[:]),
                          r=[b0, b_const], w=[bPS[bank]])
                        if col is not None:
                            V(lambda e, CM=CM, c4=c4, bank=bank: e.tensor_scalar(
                                out=CM[:, c4 * 8:(c4 + 1) * 8, :],
                                in0=PS[bank][:, 0:128].rearrange("p (g h) -> p g h", g=8),
                                scalar1=sgn[:, 0:1], scalar2=None, op0=ALU.mult),
                              r=[bPS[bank], b_const], w=[bt])
                        else:
                            V(lambda e, CM=CM, c4=c4, bank=bank: e.tensor_scalar(
                                out=CM[:, c4 * 8:(c4 + 1) * 8, :],
                                in0=PS[bank][:, 0:128].rearrange("p (g h) -> p g h", g=8),
                                scalar1=-1.0, scalar2=None, op0=ALU.mult),
                              r=[bPS[bank]], w=[bt])
                V(lambda e: e.tensor_copy(out=CMab[:], in_=CMa[:]), r=[bt], w=[bt])
                afw = AR[:, :, 8:16].unsqueeze(3).to_broadcast(sh4)
                aifw = AI[:, :, 8:16].unsqueeze(3).to_broadcast(sh4)
                cma = CMa[:].unsqueeze(2).to_broadcast(sh4)
                cmb = CMb[:].unsqueeze(2).to_broadcast(sh4)
                V(lambda e: e.tensor_tensor(out=g1, in0=afw, in1=cma, op=ALU.mult), r=[b_tab, bt], w=[bt])
                V(lambda e: e.tensor_tensor(out=g2, in0=aifw, in1=cmb, op=ALU.mult), r=[b_tab, bt], w=[bt])
                V(lambda e: e.tensor_tensor(out=Vt[:], in0=big1[:], in1=big2[:], op=ALU.add), r=[bt], w=[b_tab])
                for gq in range(8):
                    bank = 4 + (gq % 2)
                    for j in range(4):
                        g = gq * 4 + j
                        for tau in range(8):
                            c0 = (7 - tau) * 16
                            T(lambda e, g=g, j=j, tau=tau, c0=c0, bank=bank: e.matmul(
                                PS[bank][:, j * 128 + tau * 16:j * 128 + tau * 16 + 16],
                                lhsT=WTpad[:, g, c0:c0 + 128], rhs=CMab[:, g, :], start=True, stop=True),
                              r=[bt], w=[bPS[bank]])
                    A(lambda e, gq=gq, bank=bank: e.copy(
                        out=Tt[:, gq * 4:gq * 4 + 4, :], in_=PS[bank][:].rearrange("p (j c) -> p j c", j=4)),
                      r=[bPS[bank]], w=[b_tab])
                S.barrier()
            xst = [alloc(sa, "xst%d" % i, [128, D]) for i in range(2)]
            bxst = [Buf("xst%d" % i, S.GL[i]) for i in range(2)]
            scrA = make_scr(sa, "A", [7])
            bscr = Buf("scrA")
            hT2 = [alloc(sa, "hT_%d" % i, [128, 8, 512], BF16) for i in range(2)]
            bhT2 = [Buf("hT_%d" % i) for i in range(2)]
            uT2 = [alloc(sa, "uT_%d" % i, [128, 4, 512], BF16) for i in range(2)]
            buT2 = [Buf("uT_%d" % i) for i in range(2)]
            U = alloc(sa, "U", [128, G, 64], BF16)
            bU = Buf("U")
            rr = alloc(sa, "rr", [128, G, 64])
            rs = alloc(sa, "rs", [128, G, 64])
            ww = alloc(sa, "ww", [128, G, 64])
            ws = alloc(sa, "ws", [128, G, 64])
            tmpr = alloc(sa, "tmpr", [128, 16, 64])
            b_r, b_rs, b_w, b_ws, b_tmpr = Buf("r"), Buf("rs"), Buf("w"), Buf("ws"), Buf("tmpr")
            Xb = alloc(sa, "Xb", [128, G, 65], BF16)
            bXb = Buf("Xb")
            Xc = alloc(sa, "Xc", [128, G])
            Xsc = alloc(sa, "Xsc", [128, G])
            ctmp = alloc(sa, "ctmp", [128, 2, G])
            bXc = Buf("Xc", S.GS[0])
            ytmp = alloc(sa, "ytmp", [128, 8, 64])
            bytmp = Buf("ytmp")
            Zt = alloc(sa, "Zt", [128, G, 64], BF16)
            bZ = Buf("Z")
            zT = alloc(sa, "zT", [128, 4, 512], BF16)
            bzT = Buf("zT")
            sig = alloc(sa, "sig", [128, 4, 512])
            bsig = Buf("sig")
            H0 = alloc(sa, "H0", [128, 512])
            H0s = alloc(sa, "H0s", [128, 512])
            hn = alloc(sa, "hn", [128, 4, 128])
            hn2 = alloc(sa, "hn2", [128, 4, 128])
            Hp = alloc(sa, "Hp", [128, G, 16])
            Xf = alloc(sa, "Xf", [128, G, 16])
            xo = alloc(sa, "xo", [128, 4, 128])
            bH = Buf("H0")
            bxo = Buf("xo", S.GS[1])
            V(lambda e: e.memset(Xc[:], 0.0), r=[b_tabp], w=[bXc, b_tab])
            V(lambda e: e.memset(Xsc[:], 0.0), w=[bXc])
            V(lambda e: e.memset(Xb[:], 0.0), w=[bXb])

            blocks = [(i * 512, 512, False) for i in range(4)] + [(SEQ, TS, True)]
            if _os0.environ.get("K1A") == "0":
                blocks = []
            def p1a_stageA(bi):
                t0, n, is_s = blocks[bi]
                hT, bhT = hT2[bi % 2], bhT2[bi % 2]
                uT, buT = uT2[bi % 2], buT2[bi % 2]
                ntile = (n + 127) // 128
                for ti in range(ntile):
                    npart = min(128, n - ti * 128)
                    slot = (bi * 4 + ti) % 2
                    src = I["xs"][:, :] if is_s else I["xp"][t0 + ti * 128:t0 + ti * 128 + 128, :]
                    S.dma("sp", xst[slot][:npart, :], src, writes=[bxst[slot]])
                    o4 = None if is_s else hT[:, :, :].rearrange("p k (s c) -> p k s c", s=8)[:, :, :, ti * 16:(ti + 1) * 16]
                    rmsnorm_hT(xst[slot][:npart, :], bxst[slot], npart, gm[:], hT, bhT,
                               scrA, ti * 128, None, bg=b_tab, out4=o4)
                for ct in range(4):
                    bank = ct
                    for kt in range(8):
                        T(lambda e, ct=ct, kt=kt, bank=bank: e.matmul(
                            PS[bank][:, 0:n], lhsT=winu[:, kt, ct * 128:(ct + 1) * 128], rhs=hT[:, kt, 0:n],
                            start=(kt == 0), stop=(kt == 7)), r=[b_winu, bhT], w=[bPS[bank]])
                    A(lambda e, ct=ct, bank=bank: e.copy(out=uT[:, ct, 0:n], in_=PS[bank][:, 0:n]),
                      r=[bPS[bank]], w=[buT])

            if blocks:
                p1a_stageA(0)
            for bi, (t0, n, is_s) in enumerate(blocks):
                nch = n // 8 if not is_s else 16
                uT, buT = uT2[bi % 2], buT2[bi % 2]
                for gq in range(4):
                    bank = 4 + (gq % 2)
                    for j in range(8):
                        g = gq * 8 + j
                        ct, gl = g // 8, g % 8
                        if not is_s:
                            uv = uT[:, ct, 0:n].rearrange("p (s c) -> p s c", s=8)
                            sig_list = list(range(8))
                        else:
                            uv = uT[:, ct, 0:n].rearrange("p (b t) -> p t b", t=4)
                            sig_list = [4, 5, 6, 7]
                        for si, sg_ in enumerate(sig_list):
                            rhs = uv[:, sg_ if not is_s else si, :]
                            T(lambda e, j=j, gl=gl, sg_=sg_, rhs=rhs, si=si, bank=bank, L=len(sig_list): e.matmul(
                                PS[bank][:, j * 64:j * 64 + nch],
                                lhsT=masters[:, gl, 112 - 16 * sg_:240 - 16 * sg_], rhs=rhs,
                                start=(si == 0), stop=(si == L - 1)),
                              r=[b_tab, buT], w=[bPS[bank]])
                    A(lambda e, gq=gq, bank=bank: e.copy(
                        out=U[:, gq * 8:gq * 8 + 8, 0:nch],
                        in_=PS[bank][:].rearrange("p (j c) -> p j c", j=8)[:, :, 0:nch]),
                      r=[bPS[bank]], w=[bU])
                if not is_s:
                    for hf in range(2):
                        for j in range(16):
                            g = hf * 16 + j
                            for (wt, bk) in ((Wt, 0), (Wst, 2)):
                                bank = bk + j // 8
                                T(lambda e, g=g, j=j, wt=wt, bank=bank: e.matmul(
                                    PS[bank][:, (j % 8) * 64:(j % 8) * 64 + 64], lhsT=wt[:, g, :], rhs=U[:, g, :],
                                    start=True, stop=True), r=[b_tab, bU], w=[bPS[bank]])
                        for q in range(2):
                            gs = slice(hf * 16 + q * 8, hf * 16 + q * 8 + 8)
                            Sv = PS[q][:].rearrange("p (j c) -> p j c", j=8)
                            Ssv = PS[2 + q][:].rearrange("p (j c) -> p j c", j=8)
                            tm = tmpr[:, q * 8:q * 8 + 8, :]
                            V(lambda e, gs=gs, Sv=Sv: e.tensor_tensor(out=rr[:, gs, :], in0=Sv, in1=COSR[:, gs, :],
                                                                     op=ALU.mult), r=[bPS[q], b_tab], w=[b_r])
                            V(lambda e, gs=gs, Ssv=Ssv, tm=tm: e.tensor_tensor(out=tm, in0=Ssv, in1=SINR[:, gs, :],
                                                                              op=ALU.mult),
                              r=[bPS[2 + q], b_tab], w=[b_tmpr])
                            V(lambda e, gs=gs, tm=tm: e.tensor_tensor(out=rr[:, gs, :], in0=rr[:, gs, :], in1=tm,
                                                                     op=ALU.subtract), r=[b_r, b_tmpr], w=[b_r])
                            V(lambda e, gs=gs, Ssv=Ssv: e.tensor_tensor(out=rs[:, gs, :], in0=Ssv, in1=COSR[:, gs, :],
                                                                       op=ALU.mult), r=[bPS[2 + q], b_tab], w=[b_rs])
                            V(lambda e, gs=gs, Sv=Sv, tm=tm: e.tensor_tensor(out=tm, in0=Sv, in1=SINR[:, gs, :],
                                                                            op=ALU.mult),
                              r=[bPS[q], b_tab], w=[b_tmpr])
                            V(lambda e, gs=gs, tm=tm: e.tensor_tensor(out=rs[:, gs, :], in0=rs[:, gs, :], in1=tm,
                                                                     op=ALU.add), r=[b_rs, b_tmpr], w=[b_rs])
                    for g in range(G):
                        rho = MAGJ[:, g, I_A8:I_A8 + 1].to_broadcast([128, 64])
                        V(lambda e, g=g, rho=rho: e.tensor_tensor_scan(
                            out=ww[:, g, :], data0=rho, data1=rr[:, g, :], initial=Xc[:, g:g + 1], op0=ALU.mult,
                            op1=ALU.add), r=[b_r, b_tab, bXc], w=[b_w])
                        V(lambda e, g=g, rho=rho: e.tensor_tensor_scan(
                            out=ws[:, g, :], data0=rho, data1=rs[:, g, :], initial=Xsc[:, g:g + 1], op0=ALU.mult,
                            op1=ALU.add), r=[b_rs, b_tab, bXc], w=[b_ws])
                    if bi + 1 < len(blocks):
                        p1a_stageA(bi + 1)
                    ce, se_ = COSR[:, :, 63], SINR[:, :, 63]
                    we, wse = ww[:, :, 63], ws[:, :, 63]
                    V(lambda e: e.tensor_tensor(out=ctmp[:, 0, :], in0=ce, in1=we, op=ALU.mult), r=[b_w, b_tab], w=[bscr])
                    V(lambda e: e.tensor_tensor(out=ctmp[:, 1, :], in0=se_, in1=wse, op=ALU.mult), r=[b_ws, b_tab], w=[bscr])
                    V(lambda e: e.tensor_tensor(out=Xc[:], in0=ctmp[:, 0, :], in1=ctmp[:, 1, :], op=ALU.add),
                      r=[bscr], w=[bXc])
                    V(lambda e: e.tensor_tensor(out=ctmp[:, 0, :], in0=ce, in1=wse, op=ALU.mult), r=[b_ws, b_tab], w=[bscr])
                    V(lambda e: e.tensor_tensor(out=ctmp[:, 1, :], in0=se_, in1=we, op=ALU.mult), r=[b_w, b_tab], w=[bscr])
                    V(lambda e: e.tensor_tensor(out=Xsc[:], in0=ctmp[:, 0, :], in1=ctmp[:, 1, :], op=ALU.subtract),
                      r=[bscr], w=[bXc])
                    if bi > 0:
                        V(lambda e: e.tensor_copy(out=Xb[:, :, 0], in_=Xb[:, :, 64]), r=[bXb], w=[bXb])
                    V(lambda e: e.tensor_tensor(out=ww[:], in0=ww[:], in1=COSR[:], op=ALU.mult), r=[b_w, b_tab, bXc],
                      w=[b_w])
                    V(lambda e: e.tensor_tensor(out=ws[:], in0=ws[:], in1=SINR[:], op=ALU.mult), r=[b_ws, b_tab, bXc],
                      w=[b_ws])
                    V(lambda e: e.tensor_tensor(out=Xb[:, :, 1:65], in0=ww[:], in1=ws[:], op=ALU.add),
                      r=[b_w, b_ws], w=[bXb])
                    xprev = lambda g: Xb[:, g, 0:64]
                    bXprev = bXb
                    if bi == 3:
                        S.dma("sp", O["o_s5r_p"].rearrange("g p -> p g"), Xc[0:64, :], reads=[bXc])
                        S.dma("sp", O["o_s5i_p"].rearrange("g p -> p g"), Xc[64:128, :], reads=[bXc])
                else:
                    S.dma("sp", hn[:, :, 0:64], I["s5r"].rearrange("(j r) p -> r j p", r=128), writes=[bH])
                    S.dma("sp", hn[:, :, 64:128], I["s5i"].rearrange("(j r) p -> r j p", r=128), writes=[bH])
                    S.dma("sp", hn2[:, :, 0:64], I["s5i"].rearrange("(j r) p -> r j p", r=128), writes=[bH])
                    S.dma("sp", hn2[:, :, 64:128], I["s5r"].rearrange("(j r) p -> r j p", r=128), writes=[bH])
                    for (src_, dst_, bank) in ((hn, H0, 0), (hn2, H0s, 1)):
                        for j in range(4):
                            T(lambda e, src_=src_, j=j, bank=bank: e.transpose(
                                out=PS[bank][:, j * 128:(j + 1) * 128], in_=src_[:, j, :], identity=identf[:]),
                              r=[bH, b_const], w=[bPS[bank]])
                        V(lambda e, dst_=dst_, bank=bank: e.tensor_copy(out=dst_[:], in_=PS[bank][:]),
                          r=[bPS[bank]], w=[bH])
                    V(lambda e: e.tensor_scalar(out=H0s[0:64, :], in0=H0s[0:64, :], scalar1=-1.0, scalar2=None,
                                                op0=ALU.mult), r=[bH], w=[bH])
                    shs = [128, G, 16]
                    h0v = H0[:].rearrange("p (b g) -> p g b", g=G)
                    h0sv = H0s[:].rearrange("p (b g) -> p g b", g=G)

                    def abc(tab, idx):
                        return tab[:, :, idx].unsqueeze(2).to_broadcast(shs)
                    V(lambda e: e.tensor_tensor(out=Xf[:], in0=h0v, in1=abc(AR, I_AM4), op=ALU.mult), r=[bH, b_tab], w=[bxo])
                    V(lambda e: e.tensor_tensor(out=Hp[:], in0=h0sv, in1=abc(AI, I_AM4), op=ALU.mult), r=[bH, b_tab], w=[bxo])
                    V(lambda e: e.tensor_tensor(out=Xb[:, :, 0:16], in0=Xf[:], in1=Hp[:], op=ALU.add), r=[bxo], w=[bXb])
                    V(lambda e: e.tensor_tensor(out=Xf[:], in0=h0v, in1=abc(AR, I_A4), op=ALU.mult), r=[bH, b_tab], w=[bxo])
                    V(lambda e: e.tensor_tensor(out=Hp[:], in0=h0sv, in1=abc(AI, I_A4), op=ALU.mult), r=[bH, b_tab], w=[bxo])
                    V(lambda e: e.tensor_tensor(out=Xf[:], in0=Xf[:], in1=Hp[:], op=ALU.add), r=[bxo], w=[bxo])
                    for q in range(4):
                        bank = q % 2
                        for j in range(8):
                            g = q * 8 + j
                            T(lambda e, g=g, j=j, bank=bank: e.matmul(
                                PS[bank][:, j * 64:j * 64 + 16], lhsT=Wt[:, g, :], rhs=U[:, g, 0:16],
                                start=True, stop=True), r=[b_tab, bU], w=[bPS[bank]])
                        V(lambda e, q=q, bank=bank: e.tensor_tensor(
                            out=Xf[:, q * 8:q * 8 + 8, :], in0=Xf[:, q * 8:q * 8 + 8, :],
                            in1=PS[bank][:].rearrange("p (j c) -> p j c", j=8)[:, :, 0:16], op=ALU.add),
                          r=[bxo, bPS[bank]], w=[bxo])
                    Xf2 = Xf[:].rearrange("p g b -> p (g b)")
                    for j in range(4):
                        T(lambda e, j=j: e.transpose(out=PS[2][:, j * 128:(j + 1) * 128],
                                                     in_=Xf2[:, j * 128:(j + 1) * 128], identity=identf[:]),
                          r=[bxo, b_const], w=[bPS[2]])
                    V(lambda e: e.tensor_copy(out=xo[:], in_=PS[2][:].rearrange("p (j c) -> p j c", j=4)),
                      r=[bPS[2]], w=[bxo])
                    for j in range(4):
                        for gl in range(8):
                            for (nm, c0) in (("o_s5r_s", 0), ("o_s5i_s", 64)):
                                S.dma("sp", O[nm].rearrange("(b g) p -> g b p", g=G)[8 * j + gl],
                                      xo[gl * 16:gl * 16 + 16, j, c0:c0 + 64], reads=[bxo])
                    xprev = lambda g: Xb[:, g, 0:16]
                    bXprev = bXb
                for gq in range(4):
                    bank = 6 + (gq % 2)
                    for j in range(8):
                        g = gq * 8 + j
                        T(lambda e, g=g, j=j, bank=bank: e.matmul(
                            PS[bank][:, j * 64:j * 64 + nch], lhsT=Tt[:, g, :], rhs=U[:, g, 0:nch],
                            start=True, stop=False), r=[b_tab, bU], w=[bPS[bank]])
                        T(lambda e, g=g, j=j, bank=bank: e.matmul(
                            PS[bank][:, j * 64:j * 64 + nch], lhsT=Vt[:, g, :], rhs=xprev(g)[:, 0:nch],
                            start=False, stop=True), r=[b_tab, bXprev], w=[bPS[bank]])
                    gs = slice(gq * 8, gq * 8 + 8)
                    yv = PS[bank][:].rearrange("p (j c) -> p j c", j=8)[:, :, 0:nch]
                    V(lambda e, gs=gs: e.tensor_tensor(out=ytmp[:, :, 0:nch], in0=U[:, gs, 0:nch],
                                                       in1=DS[:, gs].unsqueeze(2).to_broadcast([128, 8, nch]),
                                                       op=ALU.mult), r=[bU, b_tab], w=[bytmp])
                    V(lambda e, yv=yv: e.tensor_tensor(out=ytmp[:, :, 0:nch], in0=yv, in1=ytmp[:, :, 0:nch],
                                                       op=ALU.add), r=[bPS[bank], bytmp], w=[bytmp])
                    A(lambda e, gs=gs: e.activation(out=Zt[:, gs, 0:nch], in_=ytmp[:, :, 0:nch],
                                                    func=AF.Gelu_apprx_tanh), r=[bytmp], w=[bZ])
                for ct in range(4):
                    bank = ct % 2
                    taus = list(range(8)) if not is_s else [4, 5, 6, 7]
                    for ti_, tau in enumerate(taus):
                        for gl in range(8):
                            g = ct * 8 + gl
                            T(lambda e, g=g, gl=gl, tau=tau, ti_=ti_, bank=bank: e.matmul(
                                PS[bank][:, ti_ * 64:ti_ * 64 + nch],
                                lhsT=masters[:, tau, 112 - 16 * gl:240 - 16 * gl], rhs=Zt[:, g, 0:nch],
                                start=(gl == 0), stop=(gl == 7)), r=[b_tab, bZ], w=[bPS[bank]])
                    if not is_s:
                        A(lambda e, ct=ct, bank=bank: e.copy(
                            out=zT[:, ct, 0:n].rearrange("p (c t) -> p c t", t=8),
                            in_=PS[bank][:].rearrange("p (t c) -> p c t", t=8)), r=[bPS[bank]], w=[bzT])
                    else:
                        A(lambda e, ct=ct, bank=bank: e.copy(
                            out=zT[:, ct, 0:n].rearrange("p (b t) -> p t b", t=4),
                            in_=PS[bank][:].rearrange("p (t c) -> p t c", t=8)[:, 0:4, 0:16]),
                          r=[bPS[bank]], w=[bzT])
                for ct in range(4):
                    bank = 2 + (ct % 2)
                    for kt in range(4):
                        T(lambda e, ct=ct, kt=kt, bank=bank: e.matmul(
                            PS[bank][:, 0:n], lhsT=wglu[:, kt, ct * 128:(ct + 1) * 128], rhs=zT[:, kt, 0:n],
                            start=(kt == 0), stop=(kt == 3)), r=[b_wglu, bzT], w=[bPS[bank]])
                    A(lambda e, ct=ct, bank=bank: e.activation(out=sig[:, ct, 0:n], in_=PS[bank][:, 0:n],
                                                               func=AF.Sigmoid), r=[bPS[bank]], w=[bsig])
                PL(lambda e: e.tensor_tensor(out=ssmT[:, :, t0:t0 + n], in0=zT[:, :, 0:n], in1=sig[:, :, 0:n],
                                             op=ALU.mult), r=[bzT, bsig], w=[b_ssmT[bi]])
            S.barrier()
        if dbg:
            with ExitStack() as sd:
                dtmp = alloc(sd, "dtmp", [128, 4, NTOK])
                bd = Buf("dtmp", S.GS[2])
                V(lambda e: e.tensor_copy(out=dtmp[:], in_=ssmT[:]), r=b_ssmT, w=[bd])
                S.dma("sp", O["dbg_ssm"][:, :, :], dtmp[:], reads=[bd])
                S.barrier()
        if stage <= 1:
            S.barrier()
            S.run_block()
            nck.__exit__(None, None, None)
            return nc

        with ExitStack() as sbx:
            x = alloc(sbx, "x", [128, NT, D])
            bx = [Buf("x%d" % n, S.GL[0] if n < 2 else S.GX) for n in range(NT)]
            for n in range(2):
                S.dma("sp", x[:, n, :], I["xp"][n * 128:(n + 1) * 128, :], writes=[bx[n]])
            scrB = make_scr(sbx, "B", [7])
            hT1 = alloc(sbx, "hT1", [128, 8, 128], BF16)
            bhT1 = Buf("hT1")

            def resid_add(n, npart, half, bank):
                V(lambda e: e.tensor_tensor(out=x[:npart, n, half * 512:(half + 1) * 512], in0=PS[bank][:npart, :],
                                            in1=x[:npart, n, half * 512:(half + 1) * 512], op=ALU.add),
                  r=[bPS[bank], bx[n]], w=[bx[n]])

            with ExitStack() as s1:
                wq = alloc(s1, "wqkvg", [128, 8, 2048], BF16)
                wout = alloc(s1, "wout", [128, 8, D], BF16)
                b_wqc = [Buf("wq%d" % c, S.GW[c]) for c in range(4)]
                b_wout = Buf("wout", S.GW[0])
                for c in range(4):
                    for kt in range(8):
                        S.dma("pool", wq[:, kt, c * 512:(c + 1) * 512],
                              I["w_in"][kt * 128:(kt + 1) * 128, 512 + c * 512:512 + (c + 1) * 512], writes=[b_wqc[c]])
                wout_loaded = [False]
                for n in range(2, NTP):
                    S.dma("sp", x[:, n, :], I["xp"][n * 128:(n + 1) * 128, :], reads=[b_wqc[1]] if n == 2 else [],
                          writes=[bx[n]])
                S.dma("sp", x[0:TS, 16, :], I["xs"][:, :], writes=[bx[16]])
                gm2 = alloc(s1, "gm2", [128, 8])
                gn = alloc(s1, "gn", [128, 4])
                rope = alloc(s1, "rope", [128, 3, NT, 64])
                dmp = alloc(s1, "dmp", [128, 512])
                dms = alloc(s1, "dms", [64, 256])
                xi = alloc(s1, "xi", [128, 768])
                zetap = alloc(s1, "zetap", [128, 4])
                zs = alloc(s1, "zs", [64, 64])
                cmask = alloc(s1, "cmask", [128, 16 * 64])
                b_t1 = Buf("tab1")
                S.dma("sp", gm2[:], I["g_mix"].rearrange("(k p) -> p k", p=128), writes=[b_t1])
                S.dma("sp", gn[:], I["ret_gn"].rearrange("(k p) -> p k", p=128), writes=[b_t1])
                for a_ in range(3):
                    S.dma("sp", rope[:, a_, :, :], I["c_rope"][a_], writes=[b_t1])
                S.dma("sp", dmp[:], I["c_dmask_p"][:, :], writes=[b_t1])
                S.dma("sp", dms[:], I["c_dmask_s"][:, :], writes=[b_t1])
                S.dma("sp", xi[:], I["c_xi"][0:1, :].partition_broadcast(128), writes=[b_t1])
                S.dma("sp", zetap[:], I["c_zeta_p"][:, :], writes=[b_t1])
                S.dma("sp", zs[:], I["c_zs"][:, :], writes=[b_t1])
                S.dma("sp", cmask[:], I["c_cmask"][0:1, :].partition_broadcast(128), writes=[b_t1])
                def load_wout():
                    load_w_bf16(wout, b_wout, I["w_out"], 8, D, 0)
                    for k in range(4):
                        V(lambda e: e.tensor_scalar(out=wout[:, 4 + k, :], in0=wout[:, 4 + k, :], scalar1=gn[:, k:k + 1],
                                                    scalar2=None, op0=ALU.mult), r=[b_wout, b_t1], w=[b_wout])
                    wout_loaded[0] = True
                t1q = alloc(s1, "t1q", [128, 512])
                t2q = alloc(s1, "t2q", [128, 512])
                t1k = alloc(s1, "t1k", [128, 512])
                t2k = alloc(s1, "t2k", [128, 512])
                qr = alloc(s1, "qr", [128, 512], BF16)
                kr = alloc(s1, "kr", [128, 512], BF16)
                qT = alloc(s1, "qT", [128, 4, 128], BF16)
                qxT = alloc(s1, "qxT", [128, 4, 128], BF16)
                kT = alloc(s1, "kT", [128, 4, 128], BF16)
                vb = alloc(s1, "vb", [128, 512], BF16)
                vz = alloc(s1, "vz", [128, 512], BF16)
                sg_ = alloc(s1, "sgl", [128, 512])
                sT = alloc(s1, "sT", [128, 4, 128], BF16)
                Sst = alloc(s1, "Sst", [128, 4, 128])
                Sbf = alloc(s1, "Sbf", [128, 4, 128], BF16)
                stats = alloc(s1, "stats", [128, 4, 6])
                mv = alloc(s1, "mv", [128, 4, 2])
                rs4 = alloc(s1, "rs4", [128, 4])
                nb4 = alloc(s1, "nb4", [128, 4])
                on = alloc(s1, "on", [128, 512])
                ret = alloc(s1, "ret", [128, 512], BF16)
                retT = alloc(s1, "retT", [128, 4, 128], BF16)
                S0 = [alloc(s1, "S0_%d" % i, [128, 4, 128]) for i in range(2)]
                S0b = [alloc(s1, "S0b_%d" % i, [128, 4, 128], BF16) for i in range(2)]
                qxm = [alloc(s1, "qxm_%d" % i, [128, 4, 64], BF16) for i in range(2)]
                vzb = [alloc(s1, "vzb_%d" % i, [64, 512], BF16) for i in range(2)]
                Sn = [alloc(s1, "Sn_%d" % i, [128, 4, 128]) for i in range(2)]
                bS0 = [Buf("S0_%d" % i, S.GL[i]) for i in range(2)]
                bS0b = [Buf("S0b_%d" % i) for i in range(2)]
                bqxm = [Buf("qxm%d" % i) for i in range(2)]
                bvzb = [Buf("vzb%d" % i) for i in range(2)]
                bSn = [Buf("Sn%d" % i, S.GS[i]) for i in range(2)]
                (b_t1q, b_t2q, b_t1k, b_t2k, b_qr, b_kr, b_qT, b_qxT, b_kT, b_vb, b_vz, b_sg, b_sT, b_Sst, b_Sbf,
                 b_st, b_on, b_ret, b_retT) = [Buf("p1b%d" % i) for i in range(19)]
                b_Sst.grp = S.GS[2]
                V(lambda e: e.memset(Sst[:], 0.0), w=[b_Sst])
                GC_P = [float(g ** 128) for g in GAM]
                GC_S = [float(g ** 4) for g in GAM]

                import os as _os
                _tl = _os.environ.get("K_TILES")
                _tiles = [int(v) for v in _tl.split(",") if int(v) >= 0] if _tl else list(range(NT))
                _step = int(_os.environ.get("K_STEP", "99"))
                hT1s = [hT1, alloc(s1, "hT1c", [128, 8, 128], BF16)]
                bhT1s = [bhT1, Buf("hT1c")]

                def p1b_norm(n):
                    npt_ = TS if n == 16 else 128
                    rmsnorm_hT(x[:npt_, n, :], bx[n], npt_, gm2[:], hT1s[n % 2], bhT1s[n % 2], scrB, 0, None, bg=b_t1)
                def p1b_proj(n):
                    npt_ = TS if n == 16 else 128
                    hTn, bhTn = hT1s[n % 2], bhT1s[n % 2]
                    for c in range(4):
                        for kt in range(8):
                            T(lambda e: e.matmul(PS[c][:npt_, :], lhsT=hTn[:, kt, 0:npt_],
                                                 rhs=wq[:, kt, c * 512:(c + 1) * 512], start=(kt == 0), stop=(kt == 7)),
                              r=[bhTn, b_wqc[c]], w=[bPS[c]])
                if _tiles:
                    p1b_norm(_tiles[0])
                    p1b_proj(_tiles[0])
                    load_wout()
                for ti_, n in enumerate(_tiles):
                    is_s = (n == 16)
                    npt = TS if is_s else 128
                    tok0 = n * 128
                    hT1, bhT1 = hT1s[n % 2], bhT1s[n % 2]
                    pob = [4, 6, 7, 1] if is_s else [4, 4, 4, 4]

                    def po(h):
                        if is_s:
                            return PS[pob[h]][:npt, 0:128]
                        return PS[4][:npt, h * 128:(h + 1) * 128]
                    if _step <= 1:
                        continue
                    for (bank, t1_, t2_, out_, bt1, bt2, bo) in ((0, t1q, t2q, qr, b_t1q, b_t2q, b_qr),
                                                               (1, t1k, t2k, kr, b_t1k, b_t2k, b_kr)):
                        pv4 = PS[bank][:npt, :].rearrange("p (h a j) -> p h a j", h=4, a=2)
                        t1v = t1_[:npt, :].rearrange("p (h a j) -> p h a j", h=4, a=2)
                        t2v = t2_[:npt, :].rearrange("p (h a j) -> p h a j", h=4, a=2)
                        cosb = rope[:npt, 0, n, :].unsqueeze(1).unsqueeze(1).to_broadcast([npt, 4, 2, 64])
                        sinb = rope[:npt, 1, n, :].unsqueeze(1).to_broadcast([npt, 4, 64])
                        nsinb = rope[:npt, 2, n, :].unsqueeze(1).to_broadcast([npt, 4, 64])
                        V(lambda e: e.tensor_tensor(out=t1v, in0=pv4, in1=cosb, op=ALU.mult), r=[bPS[bank], b_t1], w=[bt1])
                        V(lambda e: e.tensor_tensor(out=t2v[:, :, 0, :], in0=pv4[:, :, 1, :], in1=nsinb, op=ALU.mult),
                          r=[bPS[bank], b_t1], w=[bt2])
                        V(lambda e: e.tensor_tensor(out=t2v[:, :, 1, :], in0=pv4[:, :, 0, :], in1=sinb, op=ALU.mult),
                          r=[bPS[bank], b_t1], w=[bt2])
                        V(lambda e: e.tensor_tensor(out=out_[:npt, :], in0=t1_[:npt, :], in1=t2_[:npt, :], op=ALU.add),
                           r=[bt1, bt2], w=[bo])
                    if _step <= 2:
                        continue
                    A(lambda e: e.copy(out=vb[:npt, :], in_=PS[2][:npt, :]), r=[bPS[2]], w=[b_vb])
                    if not is_s:
                        V(lambda e: e.tensor_tensor(
                            out=vz[:, :].rearrange("p (h e) -> p h e", h=4),
                            in0=PS[2][:, :].rearrange("p (h e) -> p h e", h=4),
                            in1=zetap[:, :].unsqueeze(2).to_broadcast([128, 4, 128]), op=ALU.mult),
                          r=[bPS[2], b_t1], w=[b_vz])
                    A(lambda e: e.activation(out=sg_[:npt, :], in_=PS[3][:npt, :], func=AF.Silu), r=[bPS[3]], w=[b_sg])
                    pv4b = ps_bf(4)
                    pv5b = ps_bf(5)
                    for h in range(4):
                        T(lambda e: e.transpose(out=pv4b[:, h * 128:h * 128 + npt], in_=qr[:npt, h * 128:(h + 1) * 128],
                                                identity=identb[:npt, :npt]), r=[b_qr, b_const], w=[bPS[4]])
                    for h in range(4):
                        T(lambda e: e.transpose(out=pv5b[:, h * 128:h * 128 + npt], in_=kr[:npt, h * 128:(h + 1) * 128],
                                                identity=identb[:npt, :npt]), r=[b_kr, b_const], w=[bPS[5]])
                    q4 = pv4b[:, 0:512].rearrange("p (h t) -> p h t", h=4)[:, :, 0:npt]
                    k4 = pv5b[:, 0:512].rearrange("p (h t) -> p h t", h=4)[:, :, 0:npt]
                    A(lambda e: e.copy(out=qT[:, :, 0:npt], in_=q4), r=[bPS[4]], w=[b_qT])
                    xiv = (xi[:, 0:512].rearrange("p (h t) -> p h t", h=4) if not is_s
                           else xi[:, 512:768].rearrange("p (h t) -> p h t", h=4))
                    V(lambda e: e.tensor_tensor(out=qxT[:, :, 0:npt], in0=q4, in1=xiv, op=ALU.mult),
                      r=[bPS[4], b_t1], w=[b_qxT])
                    A(lambda e: e.copy(out=kT[:, :, 0:npt], in_=k4), r=[bPS[5]], w=[b_kT])
                    if _step <= 3:
                        continue
                    for h in range(4):
                        T(lambda e: e.matmul(PS[6][:npt, h * 128:h * 128 + npt], lhsT=kT[:, h, 0:npt], rhs=qT[:, h, 0:npt],
                                             start=True, stop=True), r=[b_kT, b_qT], w=[bPS[6]])
                    dmv = (dmp[:, :].rearrange("p (h t) -> p h t", h=4) if not is_s
                           else dms[:, :].rearrange("p (h t) -> p h t", h=4))
                    V(lambda e: e.tensor_tensor(out=sT[:npt, :, 0:npt],
                                                in0=PS[6][:npt, :].rearrange("p (h t) -> p h t", h=4)[:, :, 0:npt],
                                                in1=dmv, op=ALU.mult), r=[bPS[6], b_t1], w=[b_sT])
                    if _step <= 4:
                        continue
                    if ti_ + 1 < len(_tiles):
                        p1b_norm(_tiles[ti_ + 1])
                    for h in range(4):
                        only = (n == 0)
                        T(lambda e: e.matmul(po(h), lhsT=sT[:npt, h, 0:npt],
                                             rhs=vb[:npt, h * 128:(h + 1) * 128], start=True, stop=only),
                          r=[b_sT, b_vb], w=[bPS[pob[h]]])
                        if (not is_s) and n > 0:
                            T(lambda e: e.matmul(po(h), lhsT=qxT[:, h, 0:npt],
                                                 rhs=Sbf[:, h, :], start=False, stop=True),
                              r=[b_qxT, b_Sbf], w=[bPS[4]])
                    if not is_s:
                        for h in range(4):
                            T(lambda e: e.matmul(PS[5][:, h * 128:(h + 1) * 128], lhsT=kr[:, h * 128:(h + 1) * 128],
                                                 rhs=vz[:, h * 128:(h + 1) * 128], start=True, stop=True),
                              r=[b_kr, b_vz], w=[bPS[5]])
                        for h in range(4):
                            V(lambda e: e.scalar_tensor_tensor(out=Sst[:, h, :], in0=Sst[:, h, :], scalar=GC_P[h],
                                                               op0=ALU.mult, in1=PS[5][:, h * 128:(h + 1) * 128],
                                                               op1=ALU.add), r=[b_Sst, bPS[5]], w=[b_Sst])
                        A(lambda e: e.copy(out=Sbf[:], in_=Sst[:]), r=[b_Sst], w=[b_Sbf])
                        if n == NTP - 1:
                            S.dma("sp", O["o_ret_p"].rearrange("h d e -> d h e"), Sst[:], reads=[b_Sst])
                    else:
                        S.dma("sp", S0[0][:], I["sret"][0].rearrange("h d e -> d h e"), writes=[bS0[0]])
                        for b in range(16):
                            sl = b % 2
                            if b + 1 < 16:
                                S.dma("sp", S0[1 - sl][:], I["sret"][b + 1].rearrange("h d e -> d h e"), writes=[bS0[1 - sl]])
                            A(lambda e: e.copy(out=S0b[sl][:], in_=S0[sl][:]), r=[bS0[sl]], w=[bS0b[sl]])
                            V(lambda e: e.tensor_tensor(
                                out=qxm[sl][:], in0=qxT[:, :, 0:64],
                                in1=cmask[:, b * 64:(b + 1) * 64].unsqueeze(1).to_broadcast([128, 4, 64]), op=ALU.mult),
                              r=[b_qxT, b_t1], w=[bqxm[sl]])
                            for h in range(4):
                                T(lambda e: e.matmul(po(h), lhsT=qxm[sl][:, h, :],
                                                     rhs=S0b[sl][:, h, :], start=False, stop=(b == 15)),
                                  r=[bqxm[sl], bS0b[sl]], w=[bPS[pob[h]]])
                            V(lambda e: e.tensor_tensor(
                                out=vzb[sl][:, :].rearrange("p (h e) -> p h e", h=4),
                                in0=PS[2][:64, :].rearrange("p (h e) -> p h e", h=4),
                                in1=zs[:, b * 4:(b + 1) * 4].unsqueeze(2).to_broadcast([64, 4, 128]), op=ALU.mult),
                              r=[bPS[2], b_t1], w=[bvzb[sl]])
                            kvb = 5 if sl == 0 else 0
                            for h in range(4):
                                T(lambda e: e.matmul(PS[kvb][:, h * 128:(h + 1) * 128], lhsT=kr[:64, h * 128:(h + 1) * 128],
                                                     rhs=vzb[sl][:, h * 128:(h + 1) * 128], start=True, stop=True),
                                  r=[b_kr, bvzb[sl]], w=[bPS[kvb]])
                            for h in range(4):
                                V(lambda e: e.scalar_tensor_tensor(out=Sn[sl][:, h, :], in0=S0[sl][:, h, :], scalar=GC_S[h],
                                                                   op0=ALU.mult, in1=PS[kvb][:, h * 128:(h + 1) * 128],
                                                                   op1=ALU.add), r=[bS0[sl], bPS[kvb]], w=[bSn[sl]])
                            S.dma("sp", O["o_ret_s"][b].rearrange("h d e -> d h e"), Sn[sl][:], reads=[bSn[sl]])
                    if _step <= 5:
                        continue
                    if ti_ + 1 < len(_tiles):
                        p1b_proj(_tiles[ti_ + 1])
                    for h in range(4):
                        V(lambda e: e.bn_stats(out=stats[:npt, h, :], in_=po(h)),
                          r=[bPS[pob[h]]], w=[b_st])
                    for h in range(4):
                        V(lambda e: e.bn_aggr(out=mv[:npt, h, :], in_=stats[:npt, h, :]), r=[b_st], w=[b_st])
                    A(lambda e: e.activation(out=rs4[:npt, :], in_=mv[:npt, :, 1], func=AF.Sqrt, scale=1.0,
                                             bias=epsc[:npt, :]), r=[b_st, b_const], w=[b_st])
                    V(lambda e: e.reciprocal(out=rs4[:npt, :], in_=rs4[:npt, :]), r=[b_st], w=[b_st])
                    V(lambda e: e.scalar_tensor_tensor(out=nb4[:npt, :], in0=mv[:npt, :, 0], scalar=-1.0, op0=ALU.mult,
                                                       in1=rs4[:npt, :], op1=ALU.mult), r=[b_st], w=[b_st])
                    for h in range(4):
                        A(lambda e: e.activation(out=on[:npt, h * 128:(h + 1) * 128], in_=po(h),
                                                 func=AF.Identity, scale=rs4[:npt, h:h + 1], bias=nb4[:npt, h:h + 1]),
                          r=[bPS[pob[h]], b_st], w=[b_on])
                    V(lambda e: e.tensor_tensor(out=ret[:npt, :], in0=on[:npt, :], in1=sg_[:npt, :], op=ALU.mult),
                       r=[b_on, b_sg], w=[b_ret])
                    if _step <= 6:
                        continue
                    pv6b = ps_bf(6)
                    for h in range(4):
                        T(lambda e: e.transpose(out=pv6b[:, h * 128:h * 128 + npt], in_=ret[:npt, h * 128:(h + 1) * 128],
                                                identity=identb[:npt, :npt]), r=[b_ret, b_const], w=[bPS[6]])
                    A(lambda e: e.copy(out=retT[:, :, 0:npt],
                                       in_=pv6b[:, 0:512].rearrange("p (h t) -> p h t", h=4)[:, :, 0:npt]),
                      r=[bPS[6]], w=[b_retT])
                    if _step <= 7:
                        continue
                    bi_ = min(n // 4, 4)
                    for half in range(2):
                        bank = 6 + half
                        for kt in range(8):
                            lh = ssmT[:, kt, tok0:tok0 + npt] if kt < 4 else retT[:, kt - 4, 0:npt]
                            T(lambda e: e.matmul(PS[bank][:npt, :], lhsT=lh, rhs=wout[:, kt, half * 512:(half + 1) * 512],
                                                 start=(kt == 0), stop=(kt == 7)),
                              r=[b_ssmT[bi_], b_retT, b_wout], w=[bPS[bank]])
                        resid_add(n, npt, half, bank)
                S.barrier()
            if dbg:
                for n in range(NT):
                    S.dma("sp", O["dbg_x"][:, n, :], x[:, n, :], reads=[bx[n]])
            if stage <= 2:
                S.barrier()
                S.run_block()
                nck.__exit__(None, None, None)
                return nc

            with ExitStack() as s2:
                gx = alloc(s2, "gx", [128, 8])
                gmem = alloc(s2, "gmem", [128, 8])
                ones = alloc(s2, "ones", [128, 128], BF16)
                b_t2 = Buf("tab2")
                S.dma("sp", gx[:], I["g_xattn"].rearrange("(k p) -> p k", p=128), writes=[b_t2])
                S.dma("sp", gmem[:], I["g_mem"].rearrange("(k p) -> p k", p=128), writes=[b_t2])
                V(lambda e: e.memset(ones[:], 1.0), w=[b_t2])
                KT = alloc(s2, "KT", [128, 8, MEM], BF16)
                Vm = alloc(s2, "Vm", [128, 2, D], BF16)
                b_KT, b_Vm = Buf("KT"), Buf("Vm")
                wmq = alloc(s2, "wmq", [128, 8, D], BF16)
                b_wmq, b_wmo = Buf("wmq", S.GW[2]), Buf("wmo", S.GW[3])
                with ExitStack() as s2a:
                    wmk = alloc(s2a, "wmk", [128, 8, D], BF16)
                    wmv = alloc(s2a, "wmv", [128, 8, D], BF16)
                    b_wmk, b_wmv = Buf("wmk", S.GW[0]), Buf("wmv", S.GW[1])
                    load_w_bf16(wmk, b_wmk, I["w_mk"], 8, D, 0)
                    load_w_bf16(wmv, b_wmv, I["w_mv"], 8, D, 0)
                    load_w_bf16(wmq, b_wmq, I["w_mq"], 8, D, 0)
                    mx = [alloc(s2a, "mx%d" % i, [128, D]) for i in range(2)]
                    bmx = [Buf("mx%d" % i, S.GL[i]) for i in range(2)]
                    mhT = alloc(s2a, "mhT", [128, 8, MEM], BF16)
                    b_mhT = Buf("mhT")
                    mo = [alloc(s2a, "mo%d" % i, [128, D]) for i in range(2)]
                    bmo = [Buf("mo%d" % i, S.GS[i]) for i in range(2)]
                    _k2a = int(_os.environ.get("K2A", "9"))
                    for mt in range(2):
                        S.dma("sp", mx[mt][:], I["memp"][mt * 128:(mt + 1) * 128, :], writes=[bmx[mt]])
                        if _k2a >= 1:
                            rmsnorm_hT(mx[mt][:, :], bmx[mt], 128, gmem[:], mhT, b_mhT, scrB, mt * 128, None,
                                       ln=True, bg=b_t2)
                    oi = 0
                    for (wm, bwm, oname, isv) in ((wmk, b_wmk, "o_mk", False), (wmv, b_wmv, "o_mv", True)) if _k2a >= 2 else ():
                        for mt in range(2):
                            sl = oi % 2
                            oi += 1
                            for half in range(2):
                                bank = half
                                for kt in range(8):
                                    T(lambda e: e.matmul(PS[bank][:, :], lhsT=mhT[:, kt, mt * 128:(mt + 1) * 128],
                                                         rhs=wm[:, kt, half * 512:(half + 1) * 512], start=(kt == 0),
                                                         stop=(kt == 7)), r=[b_mhT, bwm], w=[bPS[bank]])
                                A(lambda e: e.copy(out=mo[sl][:, half * 512:(half + 1) * 512], in_=PS[bank][:, :]),
                                  r=[bPS[bank]], w=[bmo[sl]])
                                if isv:
                                    V(lambda e: e.tensor_copy(out=Vm[:, mt, half * 512:(half + 1) * 512], in_=PS[bank][:, :]),
                                      r=[bPS[bank]], w=[b_Vm])
                            S.dma("sp", O[oname][mt * 128:(mt + 1) * 128, :], mo[sl][:], reads=[bmo[sl]])
                    for j in range(8 if _k2a >= 3 else 0):
                        bank = 2 + (j % 2)
                        for kt in range(8):
                            T(lambda e: e.matmul(PS[bank][:, 0:MEM], lhsT=wmk[:, kt, j * 128:(j + 1) * 128],
                                                 rhs=mhT[:, kt, :], start=(kt == 0), stop=(kt == 7)),
                              r=[b_mhT, b_wmk], w=[bPS[bank]])
                        A(lambda e: e.copy(out=KT[:, j, :], in_=PS[bank][:, 0:MEM]), r=[bPS[bank]], w=[b_KT])
                    S.barrier()
                wmo = alloc(s2, "wmo", [128, 8, D], BF16)
                load_w_bf16(wmo, b_wmo, I["w_mo"], 8, D, 0)
                hT4s = [alloc(s2, "hT4_%d" % i, [128, 8, 512], BF16) for i in range(2)]
                b_hT4s = [Buf("hT4_%d" % i) for i in range(2)]
                qm4 = alloc(s2, "qm4", [128, 8, 512], BF16)
                oT4 = alloc(s2, "oT4", [128, 8, 512], BF16)
                eT4 = [alloc(s2, "eT4_%d" % i, [128, 2, 512], BF16) for i in range(2)]
                rdn4 = [alloc(s2, "rdn4_%d" % i, [128, 512]) for i in range(2)]
                b_qm4, b_oT4 = Buf("qm4"), Buf("oT4")
                b_eT4 = [Buf("eT4_%d" % i) for i in range(2)]
                b_rdn4 = [Buf("rdn4_%d" % i) for i in range(2)]
                Kb = [alloc(s2, "Kb%d" % i, [128, 2, D]) for i in range(2)]
                bKb = [Buf("Kb%d" % i, S.GL[i]) for i in range(2)]
                KbT = [alloc(s2, "KbT%d" % i, [128, 8, MEM], BF16) for i in range(2)]
                bKbT = [Buf("KbT%d" % i) for i in range(2)]
                Vb = [alloc(s2, "Vb%d" % i, [128, 2, D], BF16) for i in range(2)]
                bVb = [Buf("Vb%d" % i, S.GW[i]) for i in range(2)]
                eTs = alloc(s2, "eTs", [128, 2, 4, 64], BF16)
                b_eTs = Buf("eTs")
                qrot = [0]

                def q_proj(nc_, hT4, b_hT4):
                    for j in range(8):
                        bank = 5 + (qrot[0] % 3)
                        qrot[0] += 1
                        for kt in range(8):
                            T(lambda e: e.matmul(PS[bank][:, 0:nc_], lhsT=wmq[:, kt, j * 128:(j + 1) * 128],
                                                 rhs=hT4[:, kt, 0:nc_], start=(kt == 0), stop=(kt == 7)),
                              r=[b_wmq, b_hT4], w=[bPS[bank]])
                        A(lambda e: e.activation(out=qm4[:, j, 0:nc_], in_=PS[bank][:, 0:nc_], func=AF.Copy,
                                                 scale=1.0 / 16.0), r=[bPS[bank]], w=[b_qm4])

                def w_mo_resid(n, npt, c0):
                    for half in range(2):
                        bank = 5 + (qrot[0] % 3)
                        qrot[0] += 1
                        for j in range(8):
                            T(lambda e: e.matmul(PS[bank][:npt, :], lhsT=oT4[:, j, c0:c0 + npt],
                                                 rhs=wmo[:, j, half * 512:(half + 1) * 512], start=(j == 0), stop=(j == 7)),
                              r=[b_oT4, b_wmo], w=[bPS[bank]])
                        resid_add(n, npt, half, bank)

                def p2_norms(bi):
                    hT4, b_hT4 = hT4s[bi % 2], b_hT4s[bi % 2]
                    if bi < 4:
                        for ti in range(4):
                            n = bi * 4 + ti
                            rmsnorm_hT(x[:, n, :], bx[n], 128, gx[:], hT4, b_hT4, scrB, ti * 128, None, ln=True, bg=b_t2)
                    else:
                        rmsnorm_hT(x[:TS, 16, :], bx[16], TS, gx[:], hT4, b_hT4, scrB, 0, None, ln=True, bg=b_t2)

                p2_norms(0)
                for bi in range(4):
                    q_proj(512, hT4s[bi % 2], b_hT4s[bi % 2])
                    for h in range(4):
                        par = h % 2
                        for mt in range(2):
                            bank = mt
                            for dt_ in range(2):
                                T(lambda e: e.matmul(PS[bank][:, :], lhsT=KT[:, h * 2 + dt_, mt * 128:(mt + 1) * 128],
                                                     rhs=qm4[:, h * 2 + dt_, :], start=(dt_ == 0), stop=(dt_ == 1)),
                                  r=[b_KT, b_qm4], w=[bPS[bank]])
                            A(lambda e: e.activation(out=eT4[par][:, mt, :], in_=PS[bank][:, :], func=AF.Exp),
                              r=[bPS[bank]], w=[b_eT4[par]])
                        for mt in range(2):
                            T(lambda e: e.matmul(PS[2][:, :], lhsT=ones[:, :], rhs=eT4[par][:, mt, :], start=(mt == 0),
                                                 stop=(mt == 1)), r=[b_t2, b_eT4[par]], w=[bPS[2]])
                        A(lambda e: e.activation(out=rdn4[par][:, :], in_=PS[2][:, :], func=AF.Ln), r=[bPS[2]], w=[b_rdn4[par]])
                        A(lambda e: e.activation(out=rdn4[par][:, :], in_=rdn4[par][:, :], func=AF.Exp, scale=-1.0),
                          r=[b_rdn4[par]], w=[b_rdn4[par]])
                        for dt_ in range(2):
                            bank = 3 + dt_
                            j = h * 2 + dt_
                            for mt in range(2):
                                T(lambda e: e.matmul(PS[bank][:, :], lhsT=Vm[:, mt, j * 128:(j + 1) * 128],
                                                     rhs=eT4[par][:, mt, :], start=(mt == 0), stop=(mt == 1)),
                                  r=[b_Vm, b_eT4[par]], w=[bPS[bank]])
                            V(lambda e: e.tensor_tensor(out=oT4[:, j, :], in0=PS[bank][:, :], in1=rdn4[par][:, :], op=ALU.mult),
                              r=[bPS[bank], b_rdn4[par]], w=[b_oT4])
                    p2_norms(bi + 1)
                    for ti in range(4):
                        w_mo_resid(bi * 4 + ti, 128, ti * 128)
                n = 16
                q_proj(TS, hT4s[0], b_hT4s[0])
                rden_s = rdn4[0][:, 0:256].rearrange("p (h t) -> p h t", h=4)
                for b in range(16):
                    sl = b % 2
                    S.dma("sp", Kb[sl][:], I["ck"][b].rearrange("(mt p) d -> p mt d", p=128), writes=[bKb[sl]])
                    for q4 in range(4):
                        bank = 2 + (q4 % 2)
                        for i4 in range(4):
                            idx = q4 * 4 + i4
                            j, mt = idx // 2, idx % 2
                            T(lambda e: e.transpose(out=PS[bank][:, i4 * 128:(i4 + 1) * 128],
                                                    in_=Kb[sl][:, mt, j * 128:(j + 1) * 128], identity=identf[:]),
                              r=[bKb[sl], b_const], w=[bPS[bank]])
                        A(lambda e: e.copy(
                            out=KbT[sl][:, 2 * q4:2 * q4 + 2, :].rearrange("p j (m t) -> p j m t", m=2),
                            in_=PS[bank][:, :].rearrange("p (j m t) -> p j m t", j=2, m=2)),
                          r=[bPS[bank]], w=[bKbT[sl]])
                    for h in range(4):
                        for mt in range(2):
                            c0 = mt * 256 + h * 64 + 4 * b
                            for dt_ in range(2):
                                T(lambda e: e.matmul(PS[4][:, c0:c0 + 4],
                                                     lhsT=KbT[sl][:, h * 2 + dt_, mt * 128:(mt + 1) * 128],
                                                     rhs=qm4[:, h * 2 + dt_, 4 * b:4 * b + 4], start=(dt_ == 0),
                                                     stop=(dt_ == 1)), r=[bKbT[sl], b_qm4], w=[bPS[4]])
                A(lambda e: e.activation(out=eTs[:].rearrange("p m h t -> p (m h t)"), in_=PS[4][:, :], func=AF.Exp),
                  r=[bPS[4]], w=[b_eTs])
                for h in range(4):
                    for mt in range(2):
                        T(lambda e: e.matmul(PS[0][:, h * 64:(h + 1) * 64], lhsT=ones[:, :], rhs=eTs[:, mt, h, :],
                                             start=(mt == 0), stop=(mt == 1)), r=[b_t2, b_eTs], w=[bPS[0]])
                V(lambda e: e.reciprocal(out=rden_s, in_=PS[0][:, 0:256].rearrange("p (h t) -> p h t", h=4)),
                  r=[bPS[0]], w=[b_rdn4[0]])
                for b in range(16):
                    sl = b % 2
                    for mt in range(2):
                        S.dma("pool", Vb[sl][:, mt, :], I["cv"][b, mt * 128:(mt + 1) * 128, :], writes=[bVb[sl]])
                    for j in range(8):
                        h = j // 2
                        for mt in range(2):
                            T(lambda e: e.matmul(PS[1][:, j * 64 + 4 * b:j * 64 + 4 * b + 4],
                                                 lhsT=Vb[sl][:, mt, j * 128:(j + 1) * 128],
                                                 rhs=eTs[:, mt, h, 4 * b:4 * b + 4], start=(mt == 0), stop=(mt == 1)),
                              r=[bVb[sl], b_eTs], w=[bPS[1]])
                V(lambda e: e.tensor_tensor(
                    out=oT4[:, :, 0:64].rearrange("p (h a) t -> p h a t", a=2),
                    in0=PS[1][:, :].rearrange("p (h a t) -> p h a t", h=4, a=2),
                    in1=rden_s.unsqueeze(2).to_broadcast([128, 4, 2, 64]), op=ALU.mult),
                  r=[bPS[1], b_rdn4[0]], w=[b_oT4])
                w_mo_resid(16, TS, 0)
                S.barrier()
            if stage <= 3:
                if dbg:
                    for n in range(NT):
                        S.dma("sp", O["dbg_x"][:, n, :], x[:, n, :], reads=[bx[n]])
                S.barrier()
                S.run_block()
                nck.__exit__(None, None, None)
                return nc

            with ExitStack() as s3:
                gml = alloc(s3, "gml", [128, 8])
                b_t3 = Buf("tab3")
                S.dma("sp", gml[:], I["g_mlp"].rearrange("(k p) -> p k", p=128), writes=[b_t3])
                hTa = alloc(s3, "hTa", [128, 8, NTOK], BF16)
                b_hTa = [Buf("hTa%d" % n) for n in range(NT)]
                wup = [alloc(s3, "wup%d" % i, [128, 8, 512], BF16) for i in range(2)]
                wdn = [alloc(s3, "wdn%d" % i, [128, 4, D], BF16) for i in range(2)]
                bwup = [Buf("wup%d" % i, S.GW[i]) for i in range(2)]
                bwdn = [Buf("wdn%d" % i, S.GW[2 + i]) for i in range(2)]
                rl = [alloc(s3, "rl%d" % i, [128, 512]) for i in range(2)]
                brl = [Buf("rl%d" % i) for i in range(2)]
                aT = [alloc(s3, "aT%d" % i, [128, 4, 512], BF16) for i in range(2)]
                baT = [Buf("aT%d" % i) for i in range(2)]

                def load_fc(fc):
                    sl = fc % 2
                    for kt in range(8):
                        S.dma("pool", wup[sl][:, kt, :], I["w_up"][kt * 128:(kt + 1) * 128, fc * 512:(fc + 1) * 512],
                              writes=[bwup[sl]])
                    for ft in range(4):
                        S.dma("pool", wdn[sl][:, ft, :], I["w_down"][fc * 512 + ft * 128:fc * 512 + (ft + 1) * 128, :],
                              writes=[bwdn[sl]])
                load_fc(0)
                scrB["pb"] = [7, 6]
                for n in range(NT):
                    npt = TS if n == 16 else 128
                    rmsnorm_hT(x[:npt, n, :], bx[n], npt, gml[:], hTa, b_hTa[n], scrB, n * 128, None, ln=True, bg=b_t3)
                gf = alloc(s3, "gf", [128, D])
                b_gf = Buf("gf")
                S.dma("sp", gf[:], I["g_final"].rearrange("(o d) -> o d", o=1).partition_broadcast(128), writes=[b_gf])
                yst = [alloc(s3, "yst%d" % i, [128, D]) for i in range(3)]
                byst = [Buf("yst%d" % i, S.GS[i]) for i in range(3)]

                def final_norm(n):
                    npt = TS if n == 16 else 128
                    sl = n % 3
                    k4 = n % 2
                    sq, ss, rstd, bscr = scrB["sq"][k4], scrB["ss"][k4], scrB["rstd"][k4], scrB["ba"][k4]
                    A(lambda e: e.activation(out=sq[:npt, :], in_=x[:npt, n, :], func=AF.Square, accum_out=ss[:npt, :]),
                      r=[bx[n]], w=[bscr])
                    A(lambda e: e.activation(out=rstd[:npt, :], in_=ss[:npt, :], func=AF.Ln, scale=1.0 / D,
                                             bias=epsc[:npt, :]), r=[bscr, b_const], w=[bscr])
                    A(lambda e: e.activation(out=rstd[:npt, :], in_=rstd[:npt, :], func=AF.Exp, scale=-0.5),
                      r=[bscr], w=[bscr])
                    V(lambda e: e.scalar_tensor_tensor(out=yst[sl][:npt, :], in0=x[:npt, n, :], scalar=rstd[:npt, :],
                                                       op0=ALU.mult, in1=gf[:npt, :], op1=ALU.mult),
                      r=[bx[n], bscr, b_gf], w=[byst[sl]])
                    if n < 16:
                        S.dma("sp", O["yp"][n * 128:(n + 1) * 128, :], yst[sl][:, :], reads=[byst[sl]])
                    else:
                        S.dma("sp", O["ys"][:, :], yst[sl][:TS, :], reads=[byst[sl]])

                blocks3 = [(i * 512, 512) for i in range(4)] + [(SEQ, TS)]
                items = [(fc, blk) for fc in range(8) for blk in blocks3]
                ctr = {"ri": 0, "di": 0}
                load_fc(1)

                def mlp_up(i):
                    fc, (t0, nn) = items[i]
                    sl, asl = fc % 2, i % 2
                    tiles = list(range(t0 // 128, t0 // 128 + (nn + 127) // 128))
                    for ft in range(4):
                        bank = ft
                        for kt in range(8):
                            T(lambda e: e.matmul(PS[bank][:, 0:nn], lhsT=wup[sl][:, kt, ft * 128:(ft + 1) * 128],
                                                 rhs=hTa[:, kt, t0:t0 + nn], start=(kt == 0), stop=(kt == 7)),
                              r=[bwup[sl]] + [b_hTa[t] for t in tiles], w=[bPS[bank]])
                        rsl = ctr["ri"] % 2
                        ctr["ri"] += 1
                        A(lambda e: e.activation(out=rl[rsl][:, 0:nn], in_=PS[bank][:, 0:nn], func=AF.Relu),
                          r=[bPS[bank]], w=[brl[rsl]])
                        V(lambda e: e.tensor_tensor(out=aT[asl][:, ft, 0:nn], in0=rl[rsl][:, 0:nn], in1=rl[rsl][:, 0:nn],
                                                    op=ALU.mult), r=[brl[rsl]], w=[baT[asl]])

                def mlp_down(i):
                    fc, (t0, nn) = items[i]
                    sl, asl = fc % 2, i % 2
                    tiles = list(range(t0 // 128, t0 // 128 + (nn + 127) // 128))
                    for ti, tl in enumerate(tiles):
                        npt = TS if tl == 16 else 128
                        for half in range(2):
                            bank = 4 + (ctr["di"] % 4)
                            ctr["di"] += 1
                            for ft in range(4):
                                T(lambda e: e.matmul(PS[bank][:npt, :], lhsT=aT[asl][:, ft, ti * 128:ti * 128 + npt],
                                                     rhs=wdn[sl][:, ft, half * 512:(half + 1) * 512], start=(ft == 0),
                                                     stop=(ft == 3)), r=[baT[asl], bwdn[sl]], w=[bPS[bank]])
                            resid_add(tl, npt, half, bank)
                        if fc == 7:
                            final_norm(tl)

                mlp_up(0)
                for i in range(len(items)):
                    if i + 1 < len(items):
                        mlp_up(i + 1)
                    mlp_down(i)
                    fc = items[i][0]
                    if (i + 1 == len(items) or items[i + 1][0] != fc) and fc + 2 < 8:
                        load_fc(fc + 2)
                S.barrier()
            if dbg:
                for n in range(NT):
                    S.dma("sp", O["dbg_x"][:, n, :], x[:, n, :], reads=[bx[n]])
            S.barrier()
            S.run_block()
            nck.__exit__(None, None, None)
    return nc


_NC = None


def kernel(**inputs):
    global _NC
    if _NC is None:
        _NC = build()
    maps = _in_maps(inputs)
    res = run_bass_kernel_spmd(_NC, maps, core_ids=list(range(8)))
    R = res.results
    f = np.float32

    def cat(name, shape=None):
        return np.stack([np.asarray(R[c][name], f) for c in range(8)])
    y_prompt = cat("yp")
    y_sample = cat("ys").reshape(128, 4, D)
    s5r_p = cat("o_s5r_p")[None]
    s5i_p = cat("o_s5i_p")[None]
    ret_p = cat("o_ret_p")[None]
    mk_p = cat("o_mk").reshape(8, MEM, 4, 256)[None]
    mv_p = cat("o_mv").reshape(8, MEM, 4, 256)[None]
    s5r_s = cat("o_s5r_s").reshape(128, G, 64)[None]
    s5i_s = cat("o_s5i_s").reshape(128, G, 64)[None]
    ret_s = cat("o_ret_s").reshape(128, 4, 128, 128)[None]
    return (y_prompt, y_sample, s5r_p, s5i_p, ret_p, mk_p, mv_p, s5r_s, s5i_s, ret_s)


def _in_maps(inputs):
    cst = _consts()
    f = np.float32
    maps = []
    w = {}
    for k in W_NAMES:
        a = np.asarray(inputs[k], f)
        if k != "g_final":
            a = a[0]
        w[k] = np.ascontiguousarray(a.reshape(W_SHAPES[k]))
    for c in range(8):
        m = dict(w)
        m.update(cst)
        b0 = 16 * c
        m["xp"] = np.ascontiguousarray(np.asarray(inputs["x_prompt"], f)[c])
        m["xs"] = np.ascontiguousarray(np.asarray(inputs["x_sample"], f)[b0:b0 + 16].reshape(TS, D))
        m["memp"] = np.ascontiguousarray(np.asarray(inputs["mem_prompt"], f)[c])
        m["s5r"] = np.ascontiguousarray(np.asarray(inputs["state_s5_re"], f)[0, b0:b0 + 16].reshape(512, 64))
        m["s5i"] = np.ascontiguousarray(np.asarray(inputs["state_s5_im"], f)[0, b0:b0 + 16].reshape(512, 64))
        m["sret"] = np.ascontiguousarray(np.asarray(inputs["state_ret"], f)[0, b0:b0 + 16])
        m["ck"] = np.ascontiguousarray(np.asarray(inputs["cache_mem_k"], f)[0, b0:b0 + 16].reshape(16, MEM, D))
        m["cv"] = np.ascontiguousarray(np.asarray(inputs["cache_mem_v"], f)[0, b0:b0 + 16].reshape(16, MEM, D))
        maps.append(m)
    return maps
```
